# Optimizing a Trainium2 kernel written in Bass

```python
import math
import jax, jax.numpy as jnp
from jax import lax
import numpy as np

D_MODEL = 1024
BATCH = 8
SEQ = 4096
DEPTH = 2

HEAD_DIM = 64
N_HEADS = D_MODEL // HEAD_DIM
SB_HEADS = N_HEADS // 4
DIL_HEADS = N_HEADS - SB_HEADS
DIL_PATTERNS = ((128, 1), (512, 4), (2048, 16))
DIL_GROUP_HEADS = DIL_HEADS // len(DIL_PATTERNS)
OUT_WIDTH = (SB_HEADS + DIL_GROUP_HEADS) * HEAD_DIM
BLOCK = 128
N_BUCKETS = 32
MAX_DISTANCE = 2048
D_FF = 2816
RWKV_HEAD = 64
RWKV_HEADS = D_MODEL // RWKV_HEAD
D_DECAY_LORA = 64
D_AAA_LORA = 64
D_GATE_LORA = 160
NORM_EPS = 1e-6
GN_EPS = 64e-5
NEG_INF = -1e30
N_EVEN = (DEPTH + 1) // 2
N_ODD = DEPTH // 2

kernel_name = "hybrid_stickbreak_dilated_rwkv7_macaron"


def rms_norm(x, g, eps=NORM_EPS):
    x32 = x.astype(jnp.float32)
    y = x32 * lax.rsqrt(jnp.mean(x32 * x32, axis=-1, keepdims=True) + eps)
    return (y * g.astype(jnp.float32)).astype(x.dtype)


def swiglu(h, w_gate, w_up, w_down):
    return (jax.nn.silu(h @ w_gate) * (h @ w_up)) @ w_down


def t5_bucket(dist):
    max_exact = N_BUCKETS // 2
    d = jnp.maximum(dist, 1).astype(jnp.float32)
    large = max_exact + (jnp.log(d / max_exact) / math.log(MAX_DISTANCE / max_exact)
                         * (N_BUCKETS - max_exact)).astype(jnp.int32)
    large = jnp.minimum(large, N_BUCKETS - 1)
    return jnp.where(dist < max_exact, dist, large)


def stick_breaking_attention(q, k, v):
    B, S, H, Dh = q.shape
    nb = S // BLOCK
    scale = Dh ** -0.5
    qb = q.reshape(B, nb, BLOCK, H, Dh).transpose(1, 0, 3, 2, 4)
    kpos = jnp.arange(S)

    def one_block(args):
        qi, n = args
        z = jnp.einsum('bhqd,bshd->bhqs', qi, k).astype(jnp.float32) * scale
        qpos = n * BLOCK + jnp.arange(BLOCK)
        strict = kpos[None, :] < qpos[:, None]
        log_keep = jnp.where(strict, jax.nn.log_sigmoid(-z), 0.0)
        after = lax.cumsum(log_keep, axis=3, reverse=True) - log_keep
        weight = jnp.where(strict, jnp.exp(jax.nn.log_sigmoid(z) + after), 0.0)
        return jnp.einsum('bhqs,bshd->bqhd', weight.astype(v.dtype), v)

    out = lax.map(one_block, (qb, jnp.arange(nb)))
    return out.transpose(1, 0, 2, 3, 4).reshape(B, S, H, Dh)


def dilated_window_attention(q, k, v, bias_table, window, dilation):
    B, S, G, Dh = q.shape
    r = dilation
    L = S // r
    span = window // r
    nb = -(-L // BLOCK)
    Lp = nb * BLOCK

    def to_sub(t):
        t = t.reshape(B, L, r, G, Dh).transpose(0, 2, 1, 3, 4)
        t = jnp.pad(t, ((0, 0), (0, 0), (0, Lp - L), (0, 0), (0, 0)))
        return t.reshape(B, r, nb, BLOCK, G, Dh)

    def with_prev(t):
        prev = jnp.pad(t, ((0, 0), (0, 0), (1, 0), (0, 0), (0, 0), (0, 0)))[:, :, :-1]
        return jnp.concatenate([prev, t], axis=3)

    qs = to_sub(q)
    kw, vw = with_prev(to_sub(k)), with_prev(to_sub(v))
    logits = jnp.einsum('brnqgd,brnkgd->brngqk', qs, kw).astype(jnp.float32) * (Dh ** -0.5)
    qi = jnp.arange(BLOCK)[:, None]
    kj = jnp.arange(2 * BLOCK)[None, :] - BLOCK
    dist = qi - kj
    in_window = (dist >= 0) & (dist <= span)
    has_key = (jnp.arange(nb)[:, None, None] > 0) | (kj[None] >= 0)
    mask = (in_window[None] & has_key)[:, None]
    bias = bias_table[t5_bucket(jnp.maximum(dist, 0) * r)].astype(jnp.float32).transpose(2, 0, 1)
    logits = jnp.where(mask, logits + bias, NEG_INF)
    m = jnp.max(logits, axis=-1, keepdims=True)
    p = jnp.exp(logits - m)
    den = jnp.sum(p, axis=-1, keepdims=True)
    o = jnp.einsum('brngqk,brnkgd->brnqgd', (p / den).astype(v.dtype), vw)
    lse = (m + jnp.log(den))[..., 0].transpose(0, 1, 2, 4, 3)

    def from_sub(t):
        t = t.reshape((B, r, Lp) + t.shape[4:])[:, :, :L]
        return jnp.swapaxes(t, 1, 2).reshape((B, S) + t.shape[3:])

    return from_sub(o), from_sub(lse)


def parallel_attention_mixer(h, w_in, q_norm, k_norm, w_out, rel_bias):
    B, S, _ = h.shape
    proj = h @ w_in
    a_cols = 3 * SB_HEADS * HEAD_DIM
    sb = proj[..., :a_cols].reshape(B, S, 3, SB_HEADS, HEAD_DIM)
    dl = proj[..., a_cols:].reshape(B, S, 3, DIL_HEADS, HEAD_DIM)
    out_a = stick_breaking_attention(sb[:, :, 0], sb[:, :, 1], sb[:, :, 2])
    q = rms_norm(dl[:, :, 0], q_norm)
    k = rms_norm(dl[:, :, 1], k_norm)
    v = dl[:, :, 2]
    outs, lses = [], []
    for g, (window, dilation) in enumerate(DIL_PATTERNS):
        sl = slice(g * DIL_GROUP_HEADS, (g + 1) * DIL_GROUP_HEADS)
        o, l = dilated_window_attention(q[:, :, sl], k[:, :, sl], v[:, :, sl],
                                        rel_bias[:, sl], window, dilation)
        outs.append(o)
        lses.append(l)
    alpha = jax.nn.softmax(jnp.stack(lses, axis=0), axis=0)
    out_b = jnp.sum(alpha[..., None].astype(v.dtype) * jnp.stack(outs, axis=0), axis=0)
    merged = jnp.concatenate([out_a.reshape(B, S, -1), out_b.reshape(B, S, -1)], axis=-1)
    return merged @ w_out


def rwkv7_step(state, inp):
    r_t, w_t, k_t, v_t, a_t, b_t = inp
    sa = jnp.einsum('bhvk,bhk->bhv', state, a_t)
    state = (state * w_t[:, :, None, :] + sa[..., None] * b_t[:, :, None, :]
             + v_t[..., None] * k_t[:, :, None, :])
    return state, jnp.einsum('bhvk,bhk->bhv', state, r_t)


def rwkv7_time_mix(h, mix, w0, w1, w2, a0, a1, a2, g1, g2, k_k, k_a, r_k,
                   w_r, w_k, w_v, w_o, lnx_g, lnx_b):
    B, S, D = h.shape
    H, N = RWKV_HEADS, RWKV_HEAD
    f32 = jnp.float32
    xx = jnp.pad(h, ((0, 0), (1, 0), (0, 0)))[:, :-1] - h
    xr, xw, xk, xv, xa, xg = [h + xx * mix[i] for i in range(6)]
    r = (xr @ w_r).astype(f32)
    k = (xk @ w_k).astype(f32)
    v = (xv @ w_v).astype(f32)
    w_log = -jax.nn.softplus(-(w0 + jnp.tanh(xw @ w1) @ w2).astype(f32)) - 0.5
    decay = jnp.exp(-jnp.exp(w_log))
    a = jax.nn.sigmoid((a0 + (xa @ a1) @ a2).astype(f32))
    g = jax.nn.sigmoid(xg @ g1) @ g2
    heads = lambda t: t.reshape(B, S, H, N)
    kk = heads(k * k_k.astype(f32))
    kk = kk / jnp.maximum(jnp.sqrt(jnp.sum(kk * kk, axis=-1, keepdims=True)), 1e-12)
    k = k * (1.0 + (a - 1.0) * k_a.astype(f32))
    r_h, k_h, v_h, a_h, w_h = heads(r), heads(k), heads(v), heads(a), heads(decay)
    tm = lambda t: jnp.swapaxes(t, 0, 1)
    seqs = (tm(r_h), tm(w_h), tm(k_h), tm(v_h), tm(-kk), tm(kk * a_h))
    _, y = lax.scan(rwkv7_step, jnp.zeros((B, H, N, N), f32), seqs)
    y = jnp.swapaxes(y, 0, 1)
    mu = jnp.mean(y, axis=-1, keepdims=True)
    var = jnp.mean(jnp.square(y - mu), axis=-1, keepdims=True)
    y = ((y - mu) * lax.rsqrt(var + GN_EPS)).reshape(B, S, D)
    y = y * lnx_g.astype(f32) + lnx_b.astype(f32)
    y = y + (jnp.sum(r_h * k_h * r_k.astype(f32), axis=-1, keepdims=True) * v_h).reshape(B, S, D)
    return (y.astype(h.dtype) * g) @ w_o


def setup_inputs(seed: int = 0) -> dict:
    key = jax.random.key(seed)
    ks = iter(jax.random.split(key, 48))
    D = D_MODEL
    nrm = lambda shape, scale: jax.random.normal(next(ks), shape, jnp.float32) * scale
    uni = lambda shape, lo, hi: jax.random.uniform(next(ks), shape, jnp.float32, lo, hi)
    return {
        "x": nrm((BATCH, SEQ, D), 1.0),
        "ffn_norm": 1.0 + nrm((DEPTH, 2, D), 0.05),
        "ffn_w_gate": nrm((DEPTH, 2, D, D_FF), D ** -0.5),
        "ffn_w_up": nrm((DEPTH, 2, D, D_FF), D ** -0.5),
        "ffn_w_down": nrm((DEPTH, 2, D_FF, D), D_FF ** -0.5),
        "mix_norm": 1.0 + nrm((DEPTH, D), 0.05),
        "rel_bias": nrm((N_BUCKETS, DIL_HEADS), 0.3),
        "attn_w_in": nrm((N_EVEN, D, 3 * N_HEADS * HEAD_DIM), D ** -0.5),
        "attn_q_norm": 1.0 + nrm((N_EVEN, HEAD_DIM), 0.05),
        "attn_k_norm": 1.0 + nrm((N_EVEN, HEAD_DIM), 0.05),
        "attn_w_out": nrm((N_EVEN, OUT_WIDTH, D), OUT_WIDTH ** -0.5),
        "rw_mix": uni((N_ODD, 6, D), 0.0, 1.0),
        "rw_w0": uni((N_ODD, D), -4.0, 0.0),
        "rw_w1": nrm((N_ODD, D, D_DECAY_LORA), D ** -0.5),
        "rw_w2": nrm((N_ODD, D_DECAY_LORA, D), 0.1 * D_DECAY_LORA ** -0.5),
        "rw_a0": nrm((N_ODD, D), 0.1),
        "rw_a1": nrm((N_ODD, D, D_AAA_LORA), D ** -0.5),
        "rw_a2": nrm((N_ODD, D_AAA_LORA, D), 0.1 * D_AAA_LORA ** -0.5),
        "rw_g1": nrm((N_ODD, D, D_GATE_LORA), D ** -0.5),
        "rw_g2": nrm((N_ODD, D_GATE_LORA, D), D_GATE_LORA ** -0.5),
        "rw_kk": 0.85 + nrm((N_ODD, D), 0.05),
        "rw_ka": 1.0 + nrm((N_ODD, D), 0.05),
        "rw_rk": nrm((N_ODD, RWKV_HEADS, RWKV_HEAD), 0.1),
        "rw_wr": nrm((N_ODD, D, D), D ** -0.5),
        "rw_wk": nrm((N_ODD, D, D), D ** -0.5),
        "rw_wv": nrm((N_ODD, D, D), D ** -0.5),
        "rw_wo": nrm((N_ODD, D, D), D ** -0.5),
        "rw_lnx_g": 1.0 + nrm((N_ODD, D), 0.05),
        "rw_lnx_b": nrm((N_ODD, D), 0.01),
    }


def reference(x, ffn_norm, ffn_w_gate, ffn_w_up, ffn_w_down, mix_norm, rel_bias,
              attn_w_in, attn_q_norm, attn_k_norm, attn_w_out,
              rw_mix, rw_w0, rw_w1, rw_w2, rw_a0, rw_a1, rw_a2, rw_g1, rw_g2,
              rw_kk, rw_ka, rw_rk, rw_wr, rw_wk, rw_wv, rw_wo, rw_lnx_g, rw_lnx_b):
    for layer in range(DEPTH):
        x = x + 0.5 * swiglu(rms_norm(x, ffn_norm[layer, 0]), ffn_w_gate[layer, 0],
                             ffn_w_up[layer, 0], ffn_w_down[layer, 0])
        h = rms_norm(x, mix_norm[layer])
        if layer % 2 == 0:
            e = layer // 2
            x = x + parallel_attention_mixer(h, attn_w_in[e], attn_q_norm[e], attn_k_norm[e],
                                             attn_w_out[e], rel_bias)
        else:
            o = layer // 2
            x = x + rwkv7_time_mix(h, rw_mix[o], rw_w0[o], rw_w1[o], rw_w2[o], rw_a0[o],
                                   rw_a1[o], rw_a2[o], rw_g1[o], rw_g2[o], rw_kk[o], rw_ka[o],
                                   rw_rk[o], rw_wr[o], rw_wk[o], rw_wv[o], rw_wo[o],
                                   rw_lnx_g[o], rw_lnx_b[o])
        x = x + 0.5 * swiglu(rms_norm(x, ffn_norm[layer, 1]), ffn_w_gate[layer, 1],
                             ffn_w_up[layer, 1], ffn_w_down[layer, 1])
    return x
```

```python
import numpy as np
from contextlib import ExitStack
import concourse.bass as bass
import concourse.mybir as mybir
from concourse.bass_utils import run_bass_kernel_spmd

F32 = mybir.dt.float32
BF16 = mybir.dt.bfloat16
AF = mybir.ActivationFunctionType
ALU = mybir.AluOpType
AX = mybir.AxisListType

D = 1024
DFF = 2816
NF = DFF // 128
SEQ = 4096


_UID = [0]


def _uniq(name):
    _UID[0] += 1
    return "%s_u%d" % (name, _UID[0])


def _merge(d, s):
    for k, v in s.items():
        if d.get(k, 0) < v:
            d[k] = v


class Buf:
    __slots__ = ("name", "wr", "rd", "acc", "dkey", "excl")

    def __init__(self, name, acc=False, excl=False):
        self.name = name
        self.wr = {}
        self.rd = {}
        self.acc = acc
        self.dkey = None
        self.excl = excl


class Sched:
    ENG = ("pe", "act", "dve", "pool", "sp")

    def __init__(self, nc, n_dma_sems=40):
        self.nc = nc
        self.eng = {"pe": nc.tensor, "act": nc.scalar, "dve": nc.vector, "pool": nc.gpsimd, "sp": nc.sync}
        self.sems = {}
        self.val = {}
        self.seen = {e: {} for e in self.ENG}
        self.epoch = 0
        self.ekey = {}
        self._new_engine_sems()
        self.dma_pool = []
        for i in range(n_dma_sems):
            k = "dma%d" % i
            self.sems[k] = nc.semaphore(k).__enter__()
            self.val[k] = 0
            self.dma_pool.append(k)
        self.nwait = 0

    def _new_engine_sems(self):
        for e in self.ENG:
            k = "%s_e%d" % (e, self.epoch)
            self.sems[k] = self.nc.semaphore(k).__enter__()
            self.val[k] = 0
            self.ekey[e] = k

    def buf(self, name, acc=False):
        return Buf(name, acc)

    def pbuf(self, name):
        return Buf(name, False, True)

    def dbuf(self, name, acc=False):
        b = Buf(name, acc)
        b.dkey = self.dma_pool.pop()
        return b

    def release(self, b):
        self.dma_pool.append(b.dkey)
        b.dkey = None

    def _wait(self, e, deps):
        for k, v in deps.items():
            if v <= 0:
                continue
            if e == "pe" and k == self.ekey["pe"]:
                continue
            if self.seen[e].get(k, 0) < v:
                self.eng[e].wait_ge(self.sems[k], v)
                self.seen[e][k] = v
                self.nwait += 1

    def _deps(self, reads, writes, e=None):
        deps = {}
        for b in reads:
            _merge(deps, b.wr)
            if b.excl:
                own = self.ekey.get(e)
                _merge(deps, {k: v for k, v in b.rd.items() if k != own})
        for b in writes:
            _merge(deps, b.wr)
            _merge(deps, b.rd)
        return deps

    def _record(self, ev, reads, writes):
        for b in reads:
            _merge(b.rd, ev)
        for b in writes:
            if b.acc:
                _merge(b.wr, ev)
            else:
                b.wr = dict(ev)
                b.rd = {}

    def op(self, e, fn, reads=(), writes=(), sig=True):
        self._wait(e, self._deps(reads, writes, e))
        ins = fn(self.eng[e])
        k = self.ekey[e]
        if sig:
            ins.then_inc(self.sems[k], 1)
            self.val[k] += 1
            v = self.val[k]
        else:
            v = self.val[k] + 1
        self._record({k: v}, reads, writes)
        return ins

    def dma(self, q, out, in_, track, reads=(), writes=(), slow=False):
        self._wait(q, self._deps(reads, writes))
        if slow:
            ins = self.eng[q].dma_start(out=out, in_=in_, allow_slow_non_contiguous=True)
        else:
            ins = self.eng[q].dma_start(out=out, in_=in_)
        k = track.dkey
        ins.then_inc(self.sems[k], 16)
        self.val[k] += 16
        self._record({k: self.val[k]}, reads, writes)
        return ins

    def barrier(self, new_epoch=True):
        allv = {k: v for k, v in self.val.items() if v > 0}
        for e in self.ENG:
            self._wait(e, allv)
        if new_epoch:
            self.epoch += 1
            self._new_engine_sems()

    def wait_for(self, e, bufs):
        deps = {}
        for b in bufs:
            _merge(deps, b.wr)
            _merge(deps, b.rd)
        self._wait(e, deps)


def ffn_phase(nc, S, x_in, x_out, xin_b, xout_b, wg, wu, wd, gain_row, ntok, eps=1e-6):
    G = 256
    ng = ntok // G
    with ExitStack() as es:
        sb = lambda name, shape, dt: es.enter_context(nc.sbuf_tensor(_uniq(name), shape, dt))
        ps = lambda name, shape, dt: es.enter_context(nc.psum_tensor(_uniq(name), shape, dt))
        Wg = sb("Wg", [128, 8, DFF], BF16)
        Wu = sb("Wu", [128, 8, DFF], BF16)
        Wd = sb("Wd", [128, NF, D], BF16)
        gB = sb("gB", [128, D], F32)
        ident = sb("ident", [128, 128], BF16)
        xt = [sb("xt%d" % i, [128, D], F32) for i in range(4)]
        ot = [sb("ot%d" % i, [128, D], F32) for i in range(2)]
        hb = [sb("hb%d" % i, [128, D], BF16) for i in range(2)]
        hT = [sb("hT%d" % i, [128, 8, G], BF16) for i in range(2)]
        aT = [sb("aT%d" % i, [128, G], BF16) for i in range(3)]
        sg = [sb("sg%d" % i, [128, G], F32) for i in range(2)]
        junk = sb("junk", [128, D], BF16)
        ss = [sb("ss%d" % i, [128, 1], F32) for i in range(2)]
        rs = [sb("rs%d" % i, [128, 1], F32) for i in range(2)]
        nh = sb("nh", [128, 1], F32)
        p_gu = [ps("p_gu%d" % i, [128, 2, G], F32) for i in range(2)]
        p_dn = [ps("p_dn%d" % i, [128, 512], F32) for i in range(4)]
        p_tr = [ps("p_tr%d" % i, [128, 8, 128], BF16) for i in range(2)]

        bWg, bWu, bWd, bgB = S.dbuf("Wg"), S.dbuf("Wu"), S.dbuf("Wd"), S.dbuf("gB")
        bxt = [S.dbuf("xt%d" % i) for i in range(4)]
        bot = [S.dbuf("ot%d" % i) for i in range(2)]
        bhb = [S.buf("hb") for _ in range(2)]
        bhT = [S.buf("hT") for _ in range(2)]
        baT = [S.buf("aT") for _ in range(3)]
        bsg = [S.buf("sg") for _ in range(2)]
        bjunk = S.buf("junk")
        bss = [S.buf("ss") for _ in range(2)]
        brs = [S.buf("rs") for _ in range(2)]
        bnh, bident = S.buf("nh"), S.buf("ident")
        bp_gu = [S.pbuf("pgu") for _ in range(2)]
        bp_dn = [S.pbuf("pdn") for _ in range(4)]
        bp_tr = [S.pbuf("ptr") for _ in range(2)]

        S.op("pool", lambda e: e.memset(nh[:], -0.5), writes=[bnh])
        S.op("pool", lambda e: e.memset(ident[:], 0.0), writes=[bident])
        S.op("pool", lambda e: e.affine_select(out=ident[:], in_=ident[:], pattern=[[-1, 128]],
                                               compare_op=ALU.not_equal, fill=1.0, base=0,
                                               channel_multiplier=1), reads=[bident], writes=[bident])
        S.dma("sp", gB[:], gain_row.to_broadcast([128, D]), bgB, writes=[bgB])
        wg_v = wg.rearrange("(c p) f -> p c f", p=128)
        wu_v = wu.rearrange("(c p) f -> p c f", p=128)
        wd_v = wd.rearrange("(c p) f -> p c f", p=128)
        for c in range(8):
            S.dma("pool", Wg[:, c, :], wg_v[:, c, :], bWg, writes=[bWg])
            S.dma("pool", Wu[:, c, :], wu_v[:, c, :], bWu, writes=[bWu])
        for c in range(NF):
            S.dma("pool", Wd[:, c, :], wd_v[:, c, :], bWd, writes=[bWd])

        def load(g):
            for s in range(2):
                i = (g % 2) * 2 + s
                t0 = g * G + s * 128
                S.dma("sp", xt[i][:], x_in[t0:t0 + 128, :], bxt[i], reads=[xin_b], writes=[bxt[i]])

        def prep(g):
            for s in range(2):
                i = (g % 2) * 2 + s
                S.op("dve", lambda e: e.scalar_tensor_tensor(out=junk[:], in0=xt[i][:], scalar=1.0, in1=xt[i][:],
                                                             op0=ALU.mult, op1=ALU.mult, accum_out=ss[s][:]),
                     reads=[bxt[i]], writes=[bjunk, bss[s]])
                S.op("dve", lambda e: e.tensor_scalar(out=ss[s][:], in0=ss[s][:], scalar1=1.0 / D, scalar2=eps,
                                                      op0=ALU.mult, op1=ALU.add), reads=[bss[s]], writes=[bss[s]])
                S.op("pool", lambda e: e.tensor_tensor(out=rs[s][:], in0=ss[s][:], in1=nh[:], op=ALU.pow),
                     reads=[bss[s], bnh], writes=[brs[s]])
                S.op("dve", lambda e: e.scalar_tensor_tensor(out=hb[s][:], in0=xt[i][:], scalar=rs[s][:], in1=gB[:],
                                                             op0=ALU.mult, op1=ALU.mult),
                     reads=[bxt[i], brs[s], bgB], writes=[bhb[s]])
                for c in range(8):
                    S.op("pe", lambda e: e.transpose(p_tr[s][:, c, :], hb[s][:, c * 128:(c + 1) * 128], ident[:]),
                         reads=[bhb[s], bident], writes=[bp_tr[s]], sig=(c == 7))
                S.op("act", lambda e: e.copy(out=hT[g % 2][:, :, s * 128:(s + 1) * 128], in_=p_tr[s][:]),
                     reads=[bp_tr[s]], writes=[bhT[g % 2]])

        load(0)
        if ng > 1:
            load(1)
        prep(0)
        for g in range(ng):
            h = hT[g % 2]
            for j in range(NF):
                pg = p_gu[j % 2]
                for c in range(8):
                    S.op("pe", lambda e: e.matmul(pg[:, 0, :], lhsT=Wg[:, c, j * 128:(j + 1) * 128], rhs=h[:, c, :],
                                                  start=(c == 0), stop=(c == 7)),
                         reads=[bWg, bhT[g % 2]], writes=[bp_gu[j % 2]], sig=False)
                for c in range(8):
                    S.op("pe", lambda e: e.matmul(pg[:, 1, :], lhsT=Wu[:, c, j * 128:(j + 1) * 128], rhs=h[:, c, :],
                                                  start=(c == 0), stop=(c == 7)),
                         reads=[bWu, bhT[g % 2]], writes=[bp_gu[j % 2]], sig=(c == 7))
                S.op("act", lambda e: e.activation(out=sg[j % 2][:], in_=pg[:, 0, :], func=AF.Silu),
                     reads=[bp_gu[j % 2]], writes=[bsg[j % 2]])
                S.op("dve", lambda e: e.tensor_tensor(out=aT[j % 3][:], in0=sg[j % 2][:], in1=pg[:, 1, :], op=ALU.mult),
                     reads=[bsg[j % 2], bp_gu[j % 2]], writes=[baT[j % 3]])
                for s in range(2):
                    for hh in range(2):
                        S.op("pe", lambda e: e.matmul(p_dn[s * 2 + hh][:], lhsT=aT[j % 3][:, s * 128:(s + 1) * 128],
                                                      rhs=Wd[:, j, hh * 512:(hh + 1) * 512],
                                                      start=(j == 0), stop=(j == NF - 1)),
                             reads=[baT[j % 3], bWd], writes=[bp_dn[s * 2 + hh]], sig=(j == NF - 1 or (s == 1 and hh == 1)))
                if j == 8 and g + 1 < ng:
                    prep(g + 1)
            for s in range(2):
                i = (g % 2) * 2 + s
                for hh in range(2):
                    S.op("dve", lambda e: e.scalar_tensor_tensor(out=ot[s][:, hh * 512:(hh + 1) * 512], in0=p_dn[s * 2 + hh][:],
                                                                 scalar=0.5, in1=xt[i][:, hh * 512:(hh + 1) * 512],
                                                                 op0=ALU.mult, op1=ALU.add),
                         reads=[bp_dn[s * 2 + hh], bxt[i]], writes=[bot[s]])
                t0 = g * G + s * 128
                S.dma("sp", x_out[t0:t0 + 128, :], ot[s][:], bot[s], reads=[bot[s]], writes=[xout_b])
            if g + 2 < ng:
                load(g + 2)
        S.barrier()
        for b in [bWg, bWu, bWd, bgB] + bxt + bot:
            S.release(b)


def make_ident(nc, S, ident, bident, dt_is_bf16=True):
    S.op("pool", lambda e: e.memset(ident[:], 0.0), writes=[bident])
    S.op("pool", lambda e: e.affine_select(out=ident[:], in_=ident[:], pattern=[[-1, 128]],
                                           compare_op=ALU.not_equal, fill=1.0, base=0,
                                           channel_multiplier=1), reads=[bident], writes=[bident])


def attn_in_phase(nc, S, x_in, xin_b, w_in, gain_row, qn, kn, QT, KT, V, bQT, bKT, bV, ntok, eps=1e-6):
    G = 512
    ng = ntok // G
    with ExitStack() as es:
        sb = lambda name, shape, dt: es.enter_context(nc.sbuf_tensor(_uniq(name), shape, dt))
        ps = lambda name, shape, dt: es.enter_context(nc.psum_tensor(_uniq(name), shape, dt))
        Win = sb("Win", [128, 8, 3072], BF16)
        gB = sb("gB", [128, D], F32)
        ident = sb("ident", [128, 128], BF16)
        bones = sb("bones", [128, 128], BF16)
        gq = sb("gq", [128, 1], F32)
        gk = sb("gk", [128, 1], F32)
        nh = sb("nh", [128, 1], F32)
        eps_ap = sb("eps_ap", [128, 1], F32)
        xt = [sb("xt%d" % i, [128, D], F32) for i in range(4)]
        hb = [sb("hb%d" % i, [128, D], BF16) for i in range(2)]
        hT = [sb("hT%d" % i, [128, 8, G], BF16) for i in range(2)]
        junk = sb("junk", [128, D], BF16)
        ss = [sb("ss%d" % i, [128, 1], F32) for i in range(2)]
        rs = [sb("rs%d" % i, [128, 1], F32) for i in range(2)]
        ob = [sb("ob%d" % i, [128, G], BF16) for i in range(3)]
        sq = [sb("sq%d" % i, [128, G], BF16) for i in range(2)]
        lt = [sb("lt%d" % i, [128, G], F32) for i in range(2)]
        rr = [sb("rr%d" % i, [128, G], F32) for i in range(2)]
        vb = [sb("vb%d" % i, [128, D], BF16) for i in range(2)]
        p_q = [ps("p_q%d" % i, [128, G], F32) for i in range(2)]
        p_s = [ps("p_s%d" % i, [128, G], F32) for i in range(2)]
        p_v = [ps("p_v%d" % i, [128, 512], F32) for i in range(2)]
        p_tr = [ps("p_tr%d" % i, [128, 8, 128], BF16) for i in range(2)]

        bWin, bgB, bgq, bgk = S.dbuf("Win"), S.dbuf("gB"), S.dbuf("gq"), S.dbuf("gk")
        bxt = [S.dbuf("xt") for _ in range(4)]
        bob = [S.dbuf("ob") for _ in range(3)]
        bvb = [S.dbuf("vb") for _ in range(2)]
        bhb = [S.buf("hb") for _ in range(2)]
        bhT = [S.buf("hT") for _ in range(2)]
        bjunk, bnh, bident, bbones = S.buf("junk"), S.buf("nh"), S.buf("ident"), S.buf("bones")
        bss = [S.buf("ss") for _ in range(2)]
        brs = [S.buf("rs") for _ in range(2)]
        bsq = [S.buf("sq") for _ in range(2)]
        blt = [S.buf("lt") for _ in range(2)]
        brr = [S.buf("rr") for _ in range(2)]
        bp_q = [S.pbuf("pq") for _ in range(2)]
        bp_s = [S.pbuf("ps") for _ in range(2)]
        bp_v = [S.pbuf("pv") for _ in range(2)]
        bp_tr = [S.pbuf("ptr") for _ in range(2)]

        S.op("pool", lambda e: e.memset(nh[:], -0.5), writes=[bnh])
        S.op("pool", lambda e: e.memset(eps_ap[:], eps), writes=[bnh])
        make_ident(nc, S, ident, bident)
        S.op("pool", lambda e: e.memset(bones[:], 0.0), writes=[bbones])
        S.op("pool", lambda e: e.memset(bones[0:64, 0:64], 1.0), writes=[bbones])
        S.op("pool", lambda e: e.memset(bones[64:128, 64:128], 1.0), writes=[bbones])
        S.dma("sp", gB[:], gain_row.to_broadcast([128, D]), bgB, writes=[bgB])
        for hh in range(2):
            S.dma("sp", gq[hh * 64:(hh + 1) * 64, :], qn, bgq, writes=[bgq])
            S.dma("sp", gk[hh * 64:(hh + 1) * 64, :], kn, bgk, writes=[bgk])
        S.op("dve", lambda e: e.tensor_scalar(out=gq[:], in0=gq[:], scalar1=0.125, scalar2=None, op0=ALU.mult),
             reads=[bgq], writes=[bgq])
        w_v = w_in.rearrange("(c p) f -> p c f", p=128)
        for c in range(8):
            S.dma("pool", Win[:, c, :], w_v[:, c, :], bWin, writes=[bWin])

        def load(g):
            for s in range(4):
                t0 = g * G + s * 128
                S.dma("sp", xt[s][:], x_in[t0:t0 + 128, :], bxt[s], reads=[xin_b], writes=[bxt[s]])

        def prep(g):
            for s in range(4):
                k = s % 2
                S.op("dve", lambda e: e.scalar_tensor_tensor(out=junk[:], in0=xt[s][:], scalar=1.0, in1=xt[s][:],
                                                             op0=ALU.mult, op1=ALU.mult, accum_out=ss[k][:]),
                     reads=[bxt[s]], writes=[bjunk, bss[k]])
                S.op("dve", lambda e: e.tensor_scalar(out=ss[k][:], in0=ss[k][:], scalar1=1.0 / D, scalar2=eps,
                                                      op0=ALU.mult, op1=ALU.add), reads=[bss[k]], writes=[bss[k]])
                S.op("pool", lambda e: e.tensor_tensor(out=rs[k][:], in0=ss[k][:], in1=nh[:], op=ALU.pow),
                     reads=[bss[k], bnh], writes=[brs[k]])
                S.op("dve", lambda e: e.scalar_tensor_tensor(out=hb[k][:], in0=xt[s][:], scalar=rs[k][:], in1=gB[:],
                                                             op0=ALU.mult, op1=ALU.mult),
                     reads=[bxt[s], brs[k], bgB], writes=[bhb[k]])
                for c in range(8):
                    S.op("pe", lambda e: e.transpose(p_tr[k][:, c, :], hb[k][:, c * 128:(c + 1) * 128], ident[:]),
                         reads=[bhb[k], bident], writes=[bp_tr[k]], sig=(c == 7))
                S.op("act", lambda e: e.copy(out=hT[g % 2][:, :, s * 128:(s + 1) * 128], in_=p_tr[k][:]),
                     reads=[bp_tr[k]], writes=[bhT[g % 2]])

        nob = 0
        for g in range(ng):
            load(g)
            prep(g)
            h = hT[g % 2]
            bh = bhT[g % 2]
            t0 = g * G
            for fc in range(16):
                isq = fc < 8
                ch = fc % 8
                if ch < 2:
                    col0 = (0 if isq else 256) + ch * 128
                else:
                    col0 = (768 if isq else 1536) + (ch - 2) * 128
                pq = p_q[fc % 2]
                for c in range(8):
                    S.op("pe", lambda e: e.matmul(pq[:], lhsT=Win[:, c, col0:col0 + 128], rhs=h[:, c, :],
                                                  start=(c == 0), stop=(c == 7)),
                         reads=[bWin, bh], writes=[bp_q[fc % 2]], sig=(c == 7))
                o = ob[nob % 3]
                bo = bob[nob % 3]
                nob += 1
                if ch < 2:
                    S.op("act", lambda e: e.activation(out=o[:], in_=pq[:], func=AF.Copy, scale=(0.125 if isq else 1.0)),
                         reads=[bp_q[fc % 2]], writes=[bo])
                else:
                    k = fc % 2
                    S.op("act", lambda e: e.activation(out=sq[k][:], in_=pq[:], func=AF.Square),
                         reads=[bp_q[k]], writes=[bsq[k]])
                    S.op("pe", lambda e: e.matmul(p_s[k][:], lhsT=bones[:], rhs=sq[k][:], start=True, stop=True),
                         reads=[bbones, bsq[k]], writes=[bp_s[k]])
                    S.op("act", lambda e: e.activation(out=lt[k][:], in_=p_s[k][:], func=AF.Ln, scale=1.0 / 64, bias=eps_ap[:]),
                         reads=[bp_s[k], bnh], writes=[blt[k]])
                    S.op("act", lambda e: e.activation(out=rr[k][:], in_=lt[k][:], func=AF.Exp, scale=-0.5),
                         reads=[blt[k]], writes=[brr[k]])
                    gcol = gq if isq else gk
                    S.op("dve", lambda e: e.scalar_tensor_tensor(out=o[:], in0=pq[:], scalar=gcol[:], in1=rr[k][:],
                                                                 op0=ALU.mult, op1=ALU.mult),
                         reads=[bp_q[k], brr[k], bgq, bgk], writes=[bo])
                dst, bdst = (QT, bQT) if isq else (KT, bKT)
                S.dma("sp", dst[ch * 128:(ch + 1) * 128, t0:t0 + G], o[:], bo, reads=[bo], writes=[bdst])
            for s in range(4):
                k = s % 2
                for (pv, cols, off) in ((p_v[0], (512, 768), 0), (p_v[0], (2304, 2560), 256), (p_v[1], (2560, 3072), 0)):
                    n = cols[1] - cols[0]
                    for c in range(8):
                        S.op("pe", lambda e: e.matmul(pv[:, off:off + n], lhsT=h[:, c, s * 128:(s + 1) * 128],
                                                      rhs=Win[:, c, cols[0]:cols[1]], start=(c == 0), stop=(c == 7)),
                             reads=[bWin, bh], writes=[bp_v[0], bp_v[1]], sig=(c == 7))
                S.op("act", lambda e: e.copy(out=vb[k][:, 0:512], in_=p_v[0][:]), reads=[bp_v[0]], writes=[bvb[k]])
                S.op("dve", lambda e: e.tensor_copy(out=vb[k][:, 512:1024], in_=p_v[1][:]), reads=[bp_v[1]], writes=[bvb[k]])
                S.dma("sp", V[t0 + s * 128:t0 + (s + 1) * 128, :], vb[k][:], bvb[k], reads=[bvb[k]], writes=[bV])
        S.barrier()
        for b in [bWin, bgB, bgq, bgk] + bxt + bob + bvb:
            S.release(b)


def sb_attn_phase(nc, S, QT, KT, V, MT, bQT, bKT, bV, bMT, ntok):
    nblk = ntok // 128
    with ExitStack() as es:
        sb = lambda name, shape, dt: es.enter_context(nc.sbuf_tensor(_uniq(name), shape, dt))
        ps = lambda name, shape, dt: es.enter_context(nc.psum_tensor(_uniq(name), shape, dt))
        qT = sb("qT", [128, 2, ntok], BF16)
        kT = sb("kT", [128, 2, ntok], BF16)
        v = sb("v", [128, nblk, 256], BF16)
        ones = sb("ones", [128, 512], F32)
        onec = sb("onec", [128, 1], F32)
        mneg = sb("mneg", [128, 128], BF16)
        ident = sb("ident", [128, 128], BF16)
        NB = 3
        e_ = [sb("e%d" % i, [128, 512], F32) for i in range(NB)]
        sp_ = [sb("sp%d" % i, [128, 512], F32) for i in range(NB)]
        cs_ = [sb("cs%d" % i, [128, 512], F32) for i in range(NB)]
        lw_ = [sb("lw%d" % i, [128, 512], F32) for i in range(NB)]
        w_ = [sb("w%d" % i, [128, 512], BF16) for i in range(NB)]
        wT_ = [sb("wT%d" % i, [128, 4, 128], BF16) for i in range(NB)]
        oT = [sb("oT%d" % i, [128, 512], BF16) for i in range(2)]
        p_z = [ps("p_z%d" % i, [128, 512], F32) for i in range(3)]
        p_w = [ps("p_w%d" % i, [128, 4, 128], BF16) for i in range(2)]
        p_o = [ps("p_o%d" % i, [128, 128], F32) for i in range(2)]

        bq, bk, bv = S.dbuf("qT"), S.dbuf("kT"), S.dbuf("v")
        boT = [S.dbuf("oT") for _ in range(2)]
        bones, bmneg, bident = S.buf("ones"), S.buf("mneg"), S.buf("ident")
        be = [S.buf("e") for _ in range(NB)]
        bsp = [S.buf("sp") for _ in range(NB)]
        bcs = [S.buf("cs") for _ in range(NB)]
        blw = [S.buf("lw") for _ in range(NB)]
        bw = [S.buf("w") for _ in range(NB)]
        bwT = [S.buf("wT") for _ in range(NB)]
        bp_z = [S.pbuf("pz") for _ in range(3)]
        bp_w = [S.pbuf("pw") for _ in range(2)]
        bp_o = [S.pbuf("po") for _ in range(2)]

        S.op("pool", lambda e: e.memset(ones[:], 1.0), writes=[bones])
        S.op("pool", lambda e: e.memset(onec[:], 1.0), writes=[bones])
        make_ident(nc, S, ident, bident)
        S.op("pool", lambda e: e.memset(mneg[:], 0.0), writes=[bmneg])
        S.op("pool", lambda e: e.affine_select(out=mneg[:], in_=mneg[:], pattern=[[-1, 128]], compare_op=ALU.is_gt,
                                               fill=-30000.0, base=0, channel_multiplier=1),
             reads=[bmneg], writes=[bmneg])
        for pr in range(2):
            S.dma("sp", qT[:, pr, :], QT[pr * 128:(pr + 1) * 128, 0:ntok], bq, reads=[bQT], writes=[bq])
            S.dma("sp", kT[:, pr, :], KT[pr * 128:(pr + 1) * 128, 0:ntok], bk, reads=[bKT], writes=[bk])
        S.dma("sp", v[:], V[0:ntok, 0:256].rearrange("(b p) c -> p b c", p=128), bv, reads=[bV], writes=[bv])

        u = 0
        for pr in range(2):
            for qb in range(nblk):
                for hh in range(2):
                    h = 2 * pr + hh
                    P = slice(64 * hh, 64 * hh + 64)
                    po, bpo = p_o[qb % 2], bp_o[qb % 2]
                    chunks = []
                    d0 = 4 * (qb // 4)
                    chunks.append((d0, qb + 1, True))
                    for c in range(qb // 4 - 1, -1, -1):
                        chunks.append((4 * c, 4 * c + 4, False))
                    prev = None
                    for ci, (b0, b1, diag) in enumerate(chunks):
                        W = (b1 - b0) * 128
                        nb = b1 - b0
                        i = u % NB
                        pz, bpz = p_z[u % 3], bp_z[u % 3]
                        pw, bpw = p_w[u % 2], bp_w[u % 2]
                        u += 1
                        S.op("pe", lambda e: e.matmul(pz[:, 0:W], lhsT=qT[P, pr, qb * 128:(qb + 1) * 128],
                                                      rhs=kT[P, pr, b0 * 128:b1 * 128], start=True, stop=(not diag)),
                             reads=[bq, bk], writes=[bpz], sig=(not diag))
                        if diag:
                            S.op("pe", lambda e: e.matmul(pz[:, W - 128:W], lhsT=ident[:], rhs=mneg[:], start=False, stop=True),
                                 reads=[bident, bmneg], writes=[bpz])
                        S.op("act", lambda e: e.activation(out=e_[i][:, 0:W], in_=pz[:, 0:W], func=AF.Exp),
                             reads=[bpz], writes=[be[i]])
                        S.op("act", lambda e: e.activation(out=sp_[i][:, 0:W], in_=e_[i][:, 0:W], func=AF.Ln, bias=onec[:]),
                             reads=[be[i], bones], writes=[bsp[i]])
                        if prev is None:
                            init = 0.0
                            rd = [bsp[i], bones]
                        else:
                            init = cs_[prev][:, 0:1]
                            rd = [bsp[i], bones, bcs[prev]]
                        S.op("dve", lambda e: e.tensor_tensor_scan(out=cs_[i][:, W - 1::-1] if W < 512 else cs_[i][:, ::-1],
                                                                   data0=ones[:, 0:W],
                                                                   data1=sp_[i][:, W - 1::-1] if W < 512 else sp_[i][:, ::-1],
                                                                   initial=init, op0=ALU.mult, op1=ALU.add),
                             reads=rd, writes=[bcs[i]])
                        S.op("dve", lambda e: e.tensor_tensor(out=lw_[i][:, 0:W], in0=pz[:, 0:W], in1=cs_[i][:, 0:W], op=ALU.subtract),
                             reads=[bpz, bcs[i]], writes=[blw[i]])
                        S.op("act", lambda e: e.activation(out=w_[i][:, 0:W], in_=lw_[i][:, 0:W], func=AF.Exp),
                             reads=[blw[i]], writes=[bw[i]])
                        for b in range(nb):
                            S.op("pe", lambda e: e.transpose(pw[:, b, :], w_[i][:, b * 128:(b + 1) * 128], ident[:]),
                                 reads=[bw[i], bident], writes=[bpw], sig=(b == nb - 1))
                        S.op("act", lambda e: e.copy(out=wT_[i][:, 0:nb, :], in_=pw[:, 0:nb, :]), reads=[bpw], writes=[bwT[i]])
                        for b in range(nb):
                            first = (ci == 0 and b == 0)
                            last = (ci == len(chunks) - 1 and b == nb - 1)
                            S.op("pe", lambda e: e.matmul(po[P, :], lhsT=v[:, b0 + b, h * 64:(h + 1) * 64], rhs=wT_[i][:, b, :],
                                                          start=first, stop=last),
                                 reads=[bv, bwT[i]], writes=[bpo], sig=(b == nb - 1))
                        prev = i
                    ob, bob = oT[(qb // 4) % 2], boT[(qb // 4) % 2]
                    S.op("dve", lambda e: e.tensor_copy(out=ob[P, (qb % 4) * 128:(qb % 4 + 1) * 128], in_=po[P, :]),
                         reads=[bpo], writes=[bob])
                if qb % 4 == 3 or qb == nblk - 1:
                    q0 = 4 * (qb // 4)
                    n = (qb - q0 + 1) * 128
                    ob, bob = oT[(qb // 4) % 2], boT[(qb // 4) % 2]
                    S.dma("sp", MT[pr * 128:(pr + 1) * 128, q0 * 128:q0 * 128 + n], ob[:, 0:n], bob, reads=[bob], writes=[bMT])
        S.barrier()
        for b in [bq, bk, bv] + boT:
            S.release(b)


def dil_bias_host(rel_bias):
    out = np.empty((12, 128, 2, 128), np.float32)
    kj = np.arange(128)[:, None]
    q = np.arange(128)[None, :]
    for g, r in enumerate((1, 4, 16)):
        for part, dist in ((1, q - kj), (0, q + 128 - kj)):
            valid = (dist >= 0) & (dist <= 128)
            dd = np.maximum(dist, 0) * r
            d = np.maximum(dd, 1).astype(np.float32)
            large = 16 + (np.log(d / np.float32(16)) / np.float32(np.log(2048 / 16)) * np.float32(16)).astype(np.int32)
            large = np.minimum(large, 31)
            bucket = np.where(dd < 16, dd, large)
            for j in range(4):
                hd = 4 * g + j
                out[hd, :, part, :] = np.where(valid, rel_bias[bucket, hd], np.float32(-30000.0))
    return out


def dil_attn_phase(nc, S, QT, KT, V, MT, dbias, bQT, bKT, bV, bMT, ntok):
    with ExitStack() as es:
        sb = lambda name, shape, dt: es.enter_context(nc.sbuf_tensor(_uniq(name), shape, dt))
        ps = lambda name, shape, dt: es.enter_context(nc.psum_tensor(_uniq(name), shape, dt))
        qT = sb("qT", [128, 2, ntok], BF16)
        kT = sb("kT", [128, 2, ntok], BF16)
        v = sb("v", [128, ntok // 128, 256], BF16)
        bias = sb("bias", [128, 4, 256], F32)
        onesb = sb("onesb", [128, 64], BF16)
        Nacc = sb("Nacc", [128, 2, ntok], F32)
        Dacc = sb("Dacc", [128, 2, ntok], F32)
        s_ = [sb("s%d" % i, [128, 256], F32) for i in range(3)]
        pT_ = [sb("pT%d" % i, [128, 2, 128], BF16) for i in range(3)]
        ob = [sb("ob%d" % i, [128, 1024], BF16) for i in range(2)]
        p_s = [ps("p_s%d" % i, [128, 2, 128], F32) for i in range(2)]
        p_n = [ps("p_n%d" % i, [128, 128], F32) for i in range(2)]
        p_d = [ps("p_d%d" % i, [128, 128], F32) for i in range(2)]
        p_pad = [ps("p_pad%d" % i, [128, 256], F32) for i in range(0)]

        bq, bk, bv, bbias = S.dbuf("qT"), S.dbuf("kT"), S.dbuf("v"), S.dbuf("bias")
        bob = [S.dbuf("ob") for _ in range(2)]
        bones, bN, bD = S.buf("ones"), S.buf("N"), S.buf("D")
        bs = [S.buf("s") for _ in range(3)]
        bpT = [S.buf("pT") for _ in range(3)]
        bp_s = [S.pbuf("ps") for _ in range(2)]
        bp_n = [S.pbuf("pn") for _ in range(2)]
        bp_d = [S.pbuf("pd") for _ in range(2)]

        S.op("pool", lambda e: e.memset(onesb[:], 1.0), writes=[bones])
        u = 0
        for g, r in enumerate((1, 4, 16)):
            L = ntok // r
            nb = L // 128
            for pr in range(2):
                r0 = 256 + (2 * g + pr) * 128
                S.dma("sp", qT[:, pr, :], QT[r0:r0 + 128, 0:ntok], bq, reads=[bQT], writes=[bq])
                S.dma("sp", kT[:, pr, :], KT[r0:r0 + 128, 0:ntok], bk, reads=[bKT], writes=[bk])
            vsrc = V[0:ntok, 256 + g * 256:256 + (g + 1) * 256].rearrange("(n i c) f -> c i n f", i=128, c=r)
            for c in range(r):
                S.dma("sp", v[:, c * nb:(c + 1) * nb, :], vsrc[c], bv, reads=[bV], writes=[bv])
            for j in range(4):
                S.dma("sp", bias[:, j, :], dbias[4 * g + j].rearrange("k a q -> k (a q)"), bbias, writes=[bbias])
            for j in range(4):
                pr, hh = j // 2, j % 2
                P = slice(64 * hh, 64 * hh + 64)
                for c in range(r):
                    for n in range(nb):
                        def tok(nn):
                            st = c + r * 128 * nn
                            return slice(st, st + r * 127 + 1, r)
                        i = u % 3
                        pss, bpss = p_s[u % 2], bp_s[u % 2]
                        pn, bpn = p_n[u % 2], bp_n[u % 2]
                        pd, bpd = p_d[u % 2], bp_d[u % 2]
                        u += 1
                        a0 = 0 if n > 0 else 1
                        if n > 0:
                            S.op("pe", lambda e: e.matmul(pss[:, 0, :], lhsT=kT[P, pr, tok(n - 1)], rhs=qT[P, pr, tok(n)],
                                                          start=True, stop=True), reads=[bq, bk], writes=[bpss], sig=False)
                        S.op("pe", lambda e: e.matmul(pss[:, 1, :], lhsT=kT[P, pr, tok(n)], rhs=qT[P, pr, tok(n)],
                                                      start=True, stop=True), reads=[bq, bk], writes=[bpss])
                        S.op("dve", lambda e: e.tensor_tensor(out=s_[i][:, a0 * 128:256], in0=pss[:, a0:2, :],
                                                              in1=bias[:, j, a0 * 128:256], op=ALU.add),
                             reads=[bpss, bbias], writes=[bs[i]])
                        S.op("act", lambda e: e.activation(out=pT_[i][:, a0:2, :], in_=s_[i][:, a0 * 128:256], func=AF.Exp),
                             reads=[bs[i]], writes=[bpT[i]])
                        for a in range(a0, 2):
                            S.op("pe", lambda e: e.matmul(pn[P, :], lhsT=v[:, c * nb + n - 1 + a, j * 64:(j + 1) * 64],
                                                          rhs=pT_[i][:, a, :], start=(a == a0), stop=(a == 1)),
                                 reads=[bv, bpT[i]], writes=[bpn], sig=(a == 1))
                        for a in range(a0, 2):
                            S.op("pe", lambda e: e.matmul(pd[P, :], lhsT=onesb[:, :], rhs=pT_[i][:, a, :],
                                                          start=(a == a0), stop=(a == 1)),
                                 reads=[bones, bpT[i]], writes=[bpd], sig=(a == 1))
                        if g == 0:
                            S.op("act", lambda e: e.copy(out=Nacc[P, pr, tok(n)], in_=pn[P, :]), reads=[bpn], writes=[bN])
                            S.op("dve", lambda e: e.tensor_copy(out=Dacc[P, pr, tok(n)], in_=pd[P, :]), reads=[bpd], writes=[bD])
                        else:
                            S.op("dve", lambda e: e.tensor_tensor(out=Nacc[P, pr, tok(n)], in0=pn[P, :], in1=Nacc[P, pr, tok(n)],
                                                                  op=ALU.add), reads=[bpn, bN], writes=[bN])
                            S.op("dve", lambda e: e.tensor_tensor(out=Dacc[P, pr, tok(n)], in0=pd[P, :], in1=Dacc[P, pr, tok(n)],
                                                                  op=ALU.add), reads=[bpd, bD], writes=[bD])
        k = 0
        for pr in range(2):
            for c0 in range(0, ntok, 1024):
                n = min(1024, ntok - c0)
                S.op("dve", lambda e: e.reciprocal(out=Dacc[:, pr, c0:c0 + n], in_=Dacc[:, pr, c0:c0 + n]), reads=[bD], writes=[bD])
                S.op("dve", lambda e: e.tensor_tensor(out=ob[k % 2][:, 0:n], in0=Nacc[:, pr, c0:c0 + n], in1=Dacc[:, pr, c0:c0 + n],
                                                      op=ALU.mult), reads=[bN, bD], writes=[bob[k % 2]])
                S.dma("sp", MT[256 + pr * 128:256 + (pr + 1) * 128, c0:c0 + n], ob[k % 2][:, 0:n], bob[k % 2],
                      reads=[bob[k % 2]], writes=[bMT])
                k += 1
        S.barrier()
        for b in [bq, bk, bv, bbias] + bob:
            S.release(b)


def out_proj_phase(nc, S, x_in, x_out, xin_b, xout_b, MT, bMT, w_out, kdim, ntok):
    G = 512
    ng = ntok // G
    kc = kdim // 128
    with ExitStack() as es:
        sb = lambda name, shape, dt: es.enter_context(nc.sbuf_tensor(_uniq(name), shape, dt))
        ps = lambda name, shape, dt: es.enter_context(nc.psum_tensor(_uniq(name), shape, dt))
        Wo = sb("Wo", [128, kc, D], BF16)
        mT = [sb("mT%d" % i, [128, kc, G], BF16) for i in range(2)]
        xt = [sb("xt%d" % i, [128, D], F32) for i in range(3)]
        ot = [sb("ot%d" % i, [128, D], F32) for i in range(2)]
        p_y = [ps("p_y%d" % i, [128, 512], F32) for i in range(4)]
        bWo = S.dbuf("Wo")
        bmT = [S.dbuf("mT") for _ in range(2)]
        bxt = [S.dbuf("xt") for _ in range(3)]
        bot = [S.dbuf("ot") for _ in range(2)]
        bp_y = [S.pbuf("py") for _ in range(4)]
        w_v = w_out.rearrange("(c p) f -> p c f", p=128)
        for c in range(kc):
            S.dma("pool", Wo[:, c, :], w_v[:, c, :], bWo, writes=[bWo])
        k = 0
        for g in range(ng):
            m, bm = mT[g % 2], bmT[g % 2]
            for c in range(kc):
                S.dma("sp", m[:, c, :], MT[c * 128:(c + 1) * 128, g * G:(g + 1) * G], bm, reads=[bMT], writes=[bm])
            for s in range(4):
                t0 = g * G + s * 128
                x_, bx = xt[k % 3], bxt[k % 3]
                o_, bo = ot[k % 2], bot[k % 2]
                S.dma("sp", x_[:], x_in[t0:t0 + 128, :], bx, reads=[xin_b], writes=[bx])
                for hh in range(2):
                    py, bpy = p_y[(2 * k + hh) % 4], bp_y[(2 * k + hh) % 4]
                    for c in range(kc):
                        S.op("pe", lambda e: e.matmul(py[:], lhsT=m[:, c, s * 128:(s + 1) * 128], rhs=Wo[:, c, hh * 512:(hh + 1) * 512],
                                                      start=(c == 0), stop=(c == kc - 1)),
                             reads=[bm, bWo], writes=[bpy], sig=(c == kc - 1))
                    S.op("dve", lambda e: e.tensor_tensor(out=o_[:, hh * 512:(hh + 1) * 512], in0=py[:], in1=x_[:, hh * 512:(hh + 1) * 512],
                                                          op=ALU.add), reads=[bpy, bx], writes=[bo])
                S.dma("sp", x_out[t0:t0 + 128, :], o_[:], bo, reads=[bo], writes=[xout_b])
                k += 1
        S.barrier()
        for b in [bWo] + bmT + bxt + bot:
            S.release(b)


C0 = float(np.exp(-0.5))


class RwPrep:
    def __init__(self, nc, S, es, G, mix_ids, x_in, xin_b, gain_row, mix, eps=1e-6):
        sb = lambda name, shape, dt: es.enter_context(nc.sbuf_tensor(_uniq(name), shape, dt))
        ps = lambda name, shape, dt: es.enter_context(nc.psum_tensor(_uniq(name), shape, dt))
        self.nc, self.S, self.G, self.mix_ids, self.x_in, self.xin_b, self.eps = nc, S, G, mix_ids, x_in, xin_b, eps
        self.gB = sb("gB", [128, D], F32)
        self.identf = sb("identf", [128, 128], F32)
        self.nh = sb("nh", [128, 1], F32)
        self.mixc = sb("mixc", [128, 6, 8], F32)
        self.xt = [sb("xt%d" % i, [128, D], F32) for i in range(2)]
        self.hn = [sb("hn%d" % i, [128, D], F32) for i in range(2)]
        self.junk = sb("junk", [128, D], BF16)
        self.ss = [sb("ss%d" % i, [128, 1], F32) for i in range(2)]
        self.rs = [sb("rs%d" % i, [128, 1], F32) for i in range(2)]
        self.hT = [sb("hT%d" % i, [128, 8, G + 1], F32) for i in range(2)]
        self.xx = [sb("xx%d" % i, [128, G], F32) for i in range(2)]
        self.xm = {i: sb("xm%d" % i, [128, 8, G], BF16) for i in mix_ids}
        self.p_tr = [ps("p_tr%d" % i, [128, 4, 128], F32) for i in range(2)]
        self.bgB, self.bmixc = S.dbuf("gB"), S.dbuf("mixc")
        self.bxt = [S.dbuf("xt") for _ in range(2)]
        self.bhn = [S.buf("hn") for _ in range(2)]
        self.bident, self.bnh, self.bjunk = S.buf("identf"), S.buf("nh"), S.buf("junk")
        self.bss = [S.buf("ss") for _ in range(2)]
        self.brs = [S.buf("rs") for _ in range(2)]
        self.bhT = [S.buf("hT") for _ in range(2)]
        self.bxx = [S.buf("xx") for _ in range(2)]
        self.bxm = {i: S.buf("xm") for i in mix_ids}
        self.bp_tr = [S.pbuf("ptr") for _ in range(2)]
        S.op("pool", lambda e: e.memset(self.nh[:], -0.5), writes=[self.bnh])
        make_ident(nc, S, self.identf, self.bident)
        S.dma("sp", self.gB[:], gain_row.to_broadcast([128, D]), self.bgB, writes=[self.bgB])
        for i in range(6):
            S.dma("sp", self.mixc[:, i, :], mix[i:i + 1, :].rearrange("o (c p) -> p (o c)", p=128), self.bmixc, writes=[self.bmixc], slow=True)
        S.op("dve", lambda e: e.memset(self.hT[1][:, :, G:G + 1], 0.0), writes=[self.bhT[1]])
        self.dsems = [self.bgB, self.bmixc] + self.bxt

    def group(self, g):
        nc, S, G = self.nc, self.S, self.G
        hT, bhT = self.hT[g % 2], self.bhT[g % 2]
        hTp, bhTp = self.hT[(g + 1) % 2], self.bhT[(g + 1) % 2]
        S.op("pool", lambda e: e.tensor_copy(out=hT[:, :, 0:1], in_=hTp[:, :, G:G + 1]), reads=[bhTp], writes=[bhT])
        k = 0
        for s in range(G // 128):
            t0 = g * G + s * 128
            x_, bx = self.xt[s % 2], self.bxt[s % 2]
            h_, bh = self.hn[s % 2], self.bhn[s % 2]
            ss, bss, rs, brs = self.ss[s % 2], self.bss[s % 2], self.rs[s % 2], self.brs[s % 2]
            S.dma("sp", x_[:], self.x_in[t0:t0 + 128, :], bx, reads=[self.xin_b], writes=[bx])
            S.op("dve", lambda e: e.scalar_tensor_tensor(out=self.junk[:], in0=x_[:], scalar=1.0, in1=x_[:], op0=ALU.mult, op1=ALU.mult,
                                                         accum_out=ss[:]), reads=[bx], writes=[self.bjunk, bss])
            S.op("dve", lambda e: e.tensor_scalar(out=ss[:], in0=ss[:], scalar1=1.0 / D, scalar2=self.eps, op0=ALU.mult, op1=ALU.add),
                 reads=[bss], writes=[bss])
            S.op("pool", lambda e: e.tensor_tensor(out=rs[:], in0=ss[:], in1=self.nh[:], op=ALU.pow), reads=[bss, self.bnh], writes=[brs])
            S.op("dve", lambda e: e.scalar_tensor_tensor(out=h_[:], in0=x_[:], scalar=rs[:], in1=self.gB[:], op0=ALU.mult, op1=ALU.mult),
                 reads=[bx, brs, self.bgB], writes=[bh])
            for half in range(2):
                pt, bpt = self.p_tr[k % 2], self.bp_tr[k % 2]
                k += 1
                for c4 in range(4):
                    c = half * 4 + c4
                    S.op("pe", lambda e: e.transpose(pt[:, c4, :], h_[:, c * 128:(c + 1) * 128], self.identf[:]),
                         reads=[bh, self.bident], writes=[bpt], sig=(c4 == 3))
                S.op("act", lambda e: e.copy(out=hT[:, half * 4:half * 4 + 4, 1 + s * 128:1 + (s + 1) * 128], in_=pt[:]),
                     reads=[bpt], writes=[bhT])
        for c in range(8):
            xx, bxx = self.xx[c % 2], self.bxx[c % 2]
            S.op("dve", lambda e: e.tensor_tensor(out=xx[:], in0=hT[:, c, 0:G], in1=hT[:, c, 1:G + 1], op=ALU.subtract),
                 reads=[bhT], writes=[bxx])
            for n, i in enumerate(self.mix_ids):
                eng = "dve" if n % 2 == 0 else "pool"
                if eng == "dve":
                    S.op("dve", lambda e: e.scalar_tensor_tensor(out=self.xm[i][:, c, :], in0=xx[:], scalar=self.mixc[:, i, c:c + 1],
                                                                 in1=hT[:, c, 1:G + 1], op0=ALU.mult, op1=ALU.add),
                         reads=[bxx, bhT, self.bmixc], writes=[self.bxm[i]])
                else:
                    S.op("pool", lambda e: e.tensor_scalar(out=self.junk[:, 0:G], in0=xx[:], scalar1=self.mixc[:, i, c:c + 1], scalar2=None,
                                                           op0=ALU.mult), reads=[bxx, self.bmixc], writes=[self.bjunk])
                    S.op("pool", lambda e: e.tensor_tensor(out=self.xm[i][:, c, :], in0=self.junk[:, 0:G], in1=hT[:, c, 1:G + 1], op=ALU.add),
                         reads=[self.bjunk, bhT], writes=[self.bxm[i]])

    def release(self):
        for b in self.dsems:
            self.S.release(b)


def col_load(S, dst, src_row, track):
    S.dma("sp", dst, src_row.rearrange("o (c p) -> p (o c)", p=128), track, writes=[track], slow=True)


def rwkv_fm_phase(nc, S, x_in, xin_b, gain_row, mix, w0, w1, w2, a0, a1, a2, kk_, ka_, w_r, w_k,
                  RtT, KtT, AtT, BtT, WC, bouts, ntok):
    G = 512
    ng = ntok // G
    with ExitStack() as es:
        sb = lambda name, shape, dt: es.enter_context(nc.sbuf_tensor(_uniq(name), shape, dt))
        ps = lambda name, shape, dt: es.enter_context(nc.psum_tensor(_uniq(name), shape, dt))
        P = RwPrep(nc, S, es, G, (0, 1, 2, 4), x_in, xin_b, gain_row, mix)
        Wr = sb("Wr", [128, 8, D], BF16)
        Wk = sb("Wk", [128, 8, D], BF16)
        W1 = sb("W1", [128, 8, 64], BF16)
        A1 = sb("A1", [128, 8, 64], BF16)
        W2 = sb("W2", [64, D], BF16)
        A2 = sb("A2", [64, D], BF16)
        cols = sb("cols", [128, 4, 8], F32)
        bones = sb("bones", [128, 128], BF16)
        rmask = sb("rmask", [128, G], F32)
        tiny = sb("tiny", [128, 1], F32)
        tw = sb("tw", [64, G], BF16)
        ta = sb("ta", [64, G], BF16)
        names = ["sgu", "av", "kk0", "lnk", "rk", "kkn", "t1", "kp", "csg", "cse", "eW", "eWi", "eWe", "t2"]
        T = {n: sb(n, [128, G], F32) for n in names}
        sqk = sb("sqk", [128, G], BF16)
        wc = [sb("wc%d" % i, [128, G // 64], F32) for i in range(2)]
        ob = [sb("ob%d" % i, [128, G], BF16) for i in range(8)]
        p_r = ps("p_r", [128, G], F32)
        p_k = ps("p_k", [128, G], F32)
        p_u = ps("p_u", [128, G], F32)
        p_a = ps("p_a", [128, G], F32)
        p_ss = ps("p_ss", [128, G], F32)
        p_t = ps("p_t", [128, G], F32)
        bW = S.dbuf("W")
        bcols = S.dbuf("cols")
        bwc = [S.dbuf("wc") for _ in range(2)]
        bob = [S.dbuf("ob") for _ in range(8)]
        B = {n: S.buf(n) for n in names + ["sqk", "tw", "ta", "bones", "rmask"]}
        B.update({n: S.pbuf(n) for n in ["p_r", "p_k", "p_u", "p_a", "p_ss", "p_t"]})
        for c in range(8):
            S.dma("pool", Wr[:, c, :], w_r.rearrange("(c p) f -> p c f", p=128)[:, c, :], bW, writes=[bW])
            S.dma("pool", Wk[:, c, :], w_k.rearrange("(c p) f -> p c f", p=128)[:, c, :], bW, writes=[bW])
        S.dma("pool", W1[:], w1.rearrange("(c p) f -> p c f", p=128), bW, writes=[bW])
        S.dma("pool", A1[:], a1.rearrange("(c p) f -> p c f", p=128), bW, writes=[bW])
        S.dma("pool", W2[:], w2, bW, writes=[bW])
        S.dma("pool", A2[:], a2, bW, writes=[bW])
        for n, src in enumerate((w0, a0, kk_, ka_)):
            col_load(S, cols[:, n, :], src, bcols)
        S.op("pool", lambda e: e.memset(bones[:], 0.0), writes=[B["bones"]])
        S.op("pool", lambda e: e.memset(bones[0:64, 0:64], 1.0), writes=[B["bones"]])
        S.op("pool", lambda e: e.memset(bones[64:128, 64:128], 1.0), writes=[B["bones"]])
        S.op("pool", lambda e: e.memset(rmask[:], 1.0), writes=[B["rmask"]])
        S.op("pool", lambda e: e.memset(rmask[:, 0:G:64], 0.0), writes=[B["rmask"]])
        S.op("pool", lambda e: e.memset(tiny[:], 1e-18), writes=[B["rmask"]])
        RB, KB, AB, BB, WCB = bouts
        no = 0
        for g in range(ng):
            P.group(g)
            t0 = g * G
            xr, xw, xk, xa = P.xm[0], P.xm[1], P.xm[2], P.xm[4]
            bxr, bxw, bxk, bxa = P.bxm[0], P.bxm[1], P.bxm[2], P.bxm[4]
            for c in range(8):
                S.op("pe", lambda e: e.matmul(p_t[0:64, :], lhsT=W1[:, c, :], rhs=xw[:, c, :], start=(c == 0), stop=(c == 7)),
                     reads=[bW, bxw], writes=[B["p_t"]], sig=(c == 7))
            S.op("act", lambda e: e.activation(out=tw[:], in_=p_t[0:64, :], func=AF.Tanh), reads=[B["p_t"]], writes=[B["tw"]])
            for c in range(8):
                S.op("pe", lambda e: e.matmul(p_t[0:64, :], lhsT=A1[:, c, :], rhs=xa[:, c, :], start=(c == 0), stop=(c == 7)),
                     reads=[bW, bxa], writes=[B["p_t"]], sig=(c == 7))
            S.op("act", lambda e: e.copy(out=ta[:], in_=p_t[0:64, :]), reads=[B["p_t"]], writes=[B["ta"]])
            for cc in range(8):
                fs = slice(cc * 128, (cc + 1) * 128)
                for c in range(8):
                    S.op("pe", lambda e: e.matmul(p_r[:], lhsT=Wr[:, c, fs], rhs=xr[:, c, :], start=(c == 0), stop=(c == 7)),
                         reads=[bW, bxr], writes=[B["p_r"]], sig=(c == 7))
                for c in range(8):
                    S.op("pe", lambda e: e.matmul(p_k[:], lhsT=Wk[:, c, fs], rhs=xk[:, c, :], start=(c == 0), stop=(c == 7)),
                         reads=[bW, bxk], writes=[B["p_k"]], sig=(c == 7))
                S.op("pe", lambda e: e.matmul(p_u[:], lhsT=W2[:, fs], rhs=tw[:], start=True, stop=True), reads=[bW, B["tw"]], writes=[B["p_u"]])
                S.op("pe", lambda e: e.matmul(p_a[:], lhsT=A2[:, fs], rhs=ta[:], start=True, stop=True), reads=[bW, B["ta"]], writes=[B["p_a"]])
                S.op("act", lambda e: e.activation(out=T["sgu"][:], in_=p_u[:], func=AF.Sigmoid, bias=cols[:, 0, cc:cc + 1]),
                     reads=[B["p_u"], bcols], writes=[B["sgu"]])
                S.op("act", lambda e: e.activation(out=T["av"][:], in_=p_a[:], func=AF.Sigmoid, bias=cols[:, 1, cc:cc + 1]),
                     reads=[B["p_a"], bcols], writes=[B["av"]])
                S.op("dve", lambda e: e.tensor_scalar(out=T["kk0"][:], in0=p_k[:], scalar1=cols[:, 2, cc:cc + 1], scalar2=None, op0=ALU.mult),
                     reads=[B["p_k"], bcols], writes=[B["kk0"]])
                S.op("act", lambda e: e.activation(out=sqk[:], in_=T["kk0"][:], func=AF.Square), reads=[B["kk0"]], writes=[B["sqk"]])
                S.op("pe", lambda e: e.matmul(p_ss[:], lhsT=bones[:], rhs=sqk[:], start=True, stop=True), reads=[B["bones"], B["sqk"]],
                     writes=[B["p_ss"]])
                S.op("act", lambda e: e.activation(out=T["lnk"][:], in_=p_ss[:], func=AF.Ln, bias=tiny[:]), reads=[B["p_ss"], B["rmask"]],
                     writes=[B["lnk"]])
                S.op("act", lambda e: e.activation(out=T["rk"][:], in_=T["lnk"][:], func=AF.Exp, scale=-0.5), reads=[B["lnk"]], writes=[B["rk"]])
                S.op("pool", lambda e: e.tensor_tensor(out=T["kkn"][:], in0=T["kk0"][:], in1=T["rk"][:], op=ALU.mult),
                     reads=[B["kk0"], B["rk"]], writes=[B["kkn"]])
                S.op("dve", lambda e: e.tensor_scalar(out=T["t1"][:], in0=T["av"][:], scalar1=-1.0, scalar2=cols[:, 3, cc:cc + 1],
                                                      op0=ALU.add, op1=ALU.mult), reads=[B["av"], bcols], writes=[B["t1"]])
                S.op("dve", lambda e: e.scalar_tensor_tensor(out=T["kp"][:], in0=T["t1"][:], scalar=1.0, in1=p_k[:], op0=ALU.add, op1=ALU.mult),
                     reads=[B["t1"], B["p_k"]], writes=[B["kp"]])
                S.op("dve", lambda e: e.tensor_tensor_scan(out=T["csg"][:], data0=rmask[:], data1=T["sgu"][:], initial=0.0,
                                                           op0=ALU.mult, op1=ALU.add), reads=[B["rmask"], B["sgu"]], writes=[B["csg"]])
                S.op("pool", lambda e: e.tensor_tensor(out=T["cse"][:], in0=T["csg"][:], in1=T["sgu"][:], op=ALU.subtract),
                     reads=[B["csg"], B["sgu"]], writes=[B["cse"]])
                S.op("act", lambda e: e.activation(out=T["eW"][:], in_=T["csg"][:], func=AF.Exp, scale=-C0), reads=[B["csg"]], writes=[B["eW"]])
                S.op("act", lambda e: e.activation(out=T["eWi"][:], in_=T["csg"][:], func=AF.Exp, scale=C0), reads=[B["csg"]], writes=[B["eWi"]])
                S.op("act", lambda e: e.activation(out=T["eWe"][:], in_=T["cse"][:], func=AF.Exp, scale=-C0), reads=[B["cse"]], writes=[B["eWe"]])
                S.op("pool", lambda e: e.tensor_tensor(out=T["t2"][:], in0=T["kkn"][:], in1=T["av"][:], op=ALU.mult),
                     reads=[B["kkn"], B["av"]], writes=[B["t2"]])
                outs = []
                o, bo = ob[no % 8], bob[no % 8]; no += 1
                S.op("dve", lambda e: e.tensor_tensor(out=o[:], in0=p_r[:], in1=T["eW"][:], op=ALU.mult), reads=[B["p_r"], B["eW"]], writes=[bo])
                outs.append((o, bo, RtT, RB))
                o, bo = ob[no % 8], bob[no % 8]; no += 1
                S.op("dve", lambda e: e.tensor_tensor(out=o[:], in0=T["kp"][:], in1=T["eWi"][:], op=ALU.mult), reads=[B["kp"], B["eWi"]], writes=[bo])
                outs.append((o, bo, KtT, KB))
                o, bo = ob[no % 8], bob[no % 8]; no += 1
                S.op("dve", lambda e: e.scalar_tensor_tensor(out=o[:], in0=T["kkn"][:], scalar=-1.0, in1=T["eWe"][:], op0=ALU.mult, op1=ALU.mult),
                     reads=[B["kkn"], B["eWe"]], writes=[bo])
                outs.append((o, bo, AtT, AB))
                o, bo = ob[no % 8], bob[no % 8]; no += 1
                S.op("pool", lambda e: e.tensor_tensor(out=o[:], in0=T["t2"][:], in1=T["eWi"][:], op=ALU.mult), reads=[B["t2"], B["eWi"]], writes=[bo])
                outs.append((o, bo, BtT, BB))
                for (o, bo, dst, bdst) in outs:
                    S.dma("sp", dst[fs, t0:t0 + G], o[:], bo, reads=[bo], writes=[bdst])
                w_, bw_ = wc[cc % 2], bwc[cc % 2]
                S.op("dve", lambda e: e.tensor_copy(out=w_[:], in_=T["eW"][:, 63:G:64]), reads=[B["eW"]], writes=[bw_])
                S.dma("sp", WC[fs, g * (G // 64):(g + 1) * (G // 64)], w_[:], bw_, reads=[bw_], writes=[WCB])
        S.barrier()
        P.release()
        for b in [bW, bcols] + bwc + bob:
            S.release(b)


def rwkv_tm_phase(nc, S, x_in, xin_b, gain_row, mix, a0, a1, a2, g1, g2, ka_, rk_, w_r, w_k, w_v,
                  Vtok, BV, Gt, bouts, ntok):
    G = 256
    ng = ntok // G
    with ExitStack() as es:
        sb = lambda name, shape, dt: es.enter_context(nc.sbuf_tensor(_uniq(name), shape, dt))
        ps = lambda name, shape, dt: es.enter_context(nc.psum_tensor(_uniq(name), shape, dt))
        P = RwPrep(nc, S, es, G, (0, 2, 3, 4, 5), x_in, xin_b, gain_row, mix)
        Wr = sb("Wr", [128, 8, D], BF16)
        Wk = sb("Wk", [128, 8, D], BF16)
        Wv = sb("Wv", [128, 8, D], BF16)
        A1 = sb("A1", [128, 8, 64], BF16)
        A2 = sb("A2", [64, D], BF16)
        G1 = sb("G1", [128, 8, 160], BF16)
        G2a = sb("G2a", [128, D], BF16)
        G2b = sb("G2b", [32, D], BF16)
        a0B = sb("a0B", [128, D], F32)
        kaB = sb("kaB", [128, D], F32)
        rkB = sb("rkB", [128, D], F32)
        ta = sb("ta", [64, G], BF16)
        sg1a = sb("sg1a", [128, G], BF16)
        sg1b = sb("sg1b", [32, G], BF16)
        tmp = sb("tmp", [128, D], F32)
        av = sb("av", [128, D], F32)
        t1 = sb("t1", [128, D], F32)
        kp = sb("kp", [128, D], F32)
        tmp2 = sb("tmp2", [128, D], F32)
        tmp3 = sb("tmp3", [128, 16, 64], F32)
        bsum = sb("bsum", [128, 16, 1], F32)
        bvo = [sb("bvo%d" % i, [128, 16, 64], F32) for i in range(2)]
        vto = [sb("vto%d" % i, [128, D], BF16) for i in range(2)]
        gto = [sb("gto%d" % i, [128, D], F32) for i in range(2)]
        p_t = ps("p_t", [128, G], F32)
        pp = [ps("pp%d" % i, [128, 2, 512], F32) for i in range(2)]
        bW, bB = S.dbuf("W"), S.dbuf("B")
        bbvo = [S.dbuf("bvo") for _ in range(2)]
        bvto = [S.dbuf("vto") for _ in range(2)]
        bgto = [S.dbuf("gto") for _ in range(2)]
        B = {n: S.buf(n) for n in ["ta", "sg1a", "sg1b", "tmp", "av", "t1", "kp", "tmp2", "tmp3", "bsum"]}
        B.update({n: S.pbuf(n) for n in ["p_t", "pp0", "pp1"]})
        bpp = [B["pp0"], B["pp1"]]
        for c in range(8):
            for (W_, w_) in ((Wr, w_r), (Wk, w_k), (Wv, w_v)):
                S.dma("pool", W_[:, c, :], w_.rearrange("(c p) f -> p c f", p=128)[:, c, :], bW, writes=[bW])
        S.dma("pool", A1[:], a1.rearrange("(c p) f -> p c f", p=128), bW, writes=[bW])
        S.dma("pool", G1[:], g1.rearrange("(c p) f -> p c f", p=128), bW, writes=[bW])
        S.dma("pool", A2[:], a2, bW, writes=[bW])
        S.dma("pool", G2a[:], g2[0:128, :], bW, writes=[bW])
        S.dma("pool", G2b[:], g2[128:160, :], bW, writes=[bW])
        for (t_, src) in ((a0B, a0), (kaB, ka_), (rkB, rk_)):
            S.dma("sp", t_[:], src.to_broadcast([128, D]), bB, writes=[bB])
        VB, BVB, GB = bouts
        npp = 0
        k = 0
        for g in range(ng):
            P.group(g)
            xr, xk, xv, xa, xg = P.xm[0], P.xm[2], P.xm[3], P.xm[4], P.xm[5]
            bxr, bxk, bxv, bxa, bxg = P.bxm[0], P.bxm[2], P.bxm[3], P.bxm[4], P.bxm[5]
            for c in range(8):
                S.op("pe", lambda e: e.matmul(p_t[0:64, :], lhsT=A1[:, c, :], rhs=xa[:, c, :], start=(c == 0), stop=(c == 7)),
                     reads=[bW, bxa], writes=[B["p_t"]], sig=(c == 7))
            S.op("act", lambda e: e.copy(out=ta[:], in_=p_t[0:64, :]), reads=[B["p_t"]], writes=[B["ta"]])
            for c in range(8):
                S.op("pe", lambda e: e.matmul(p_t[:, :], lhsT=G1[:, c, 0:128], rhs=xg[:, c, :], start=(c == 0), stop=(c == 7)),
                     reads=[bW, bxg], writes=[B["p_t"]], sig=(c == 7))
            S.op("act", lambda e: e.activation(out=sg1a[:], in_=p_t[:, :], func=AF.Sigmoid), reads=[B["p_t"]], writes=[B["sg1a"]])
            for c in range(8):
                S.op("pe", lambda e: e.matmul(p_t[0:32, :], lhsT=G1[:, c, 128:160], rhs=xg[:, c, :], start=(c == 0), stop=(c == 7)),
                     reads=[bW, bxg], writes=[B["p_t"]], sig=(c == 7))
            S.op("act", lambda e: e.activation(out=sg1b[:], in_=p_t[0:32, :], func=AF.Sigmoid), reads=[B["p_t"]], writes=[B["sg1b"]])
            for s in range(G // 128):
                ts = slice(s * 128, (s + 1) * 128)
                t0 = g * G + s * 128

                def big(xm_, bxm_, W_):
                    nonlocal npp
                    p, bp = pp[npp % 2], bpp[npp % 2]
                    npp += 1
                    for hh in range(2):
                        for c in range(8):
                            S.op("pe", lambda e: e.matmul(p[:, hh, :], lhsT=xm_[:, c, ts], rhs=W_[:, c, hh * 512:(hh + 1) * 512],
                                                          start=(c == 0), stop=(c == 7)), reads=[bW, bxm_], writes=[bp], sig=(c == 7))
                    return p, bp
                p, bp = pp[npp % 2], bpp[npp % 2]
                npp += 1
                for hh in range(2):
                    S.op("pe", lambda e: e.matmul(p[:, hh, :], lhsT=ta[:, ts], rhs=A2[:, hh * 512:(hh + 1) * 512], start=True, stop=True),
                         reads=[bW, B["ta"]], writes=[bp])
                S.op("dve", lambda e: e.tensor_tensor(out=tmp[:], in0=p[:].rearrange("p a b -> p (a b)"), in1=a0B[:], op=ALU.add),
                     reads=[bp, bB], writes=[B["tmp"]])
                S.op("act", lambda e: e.activation(out=av[:], in_=tmp[:], func=AF.Sigmoid), reads=[B["tmp"]], writes=[B["av"]])
                S.op("dve", lambda e: e.scalar_tensor_tensor(out=t1[:], in0=av[:], scalar=-1.0, in1=kaB[:], op0=ALU.add, op1=ALU.mult),
                     reads=[B["av"], bB], writes=[B["t1"]])
                p, bp = big(xk, bxk, Wk)
                S.op("dve", lambda e: e.scalar_tensor_tensor(out=kp[:], in0=t1[:], scalar=1.0, in1=p[:].rearrange("p a b -> p (a b)"),
                                                             op0=ALU.add, op1=ALU.mult), reads=[B["t1"], bp], writes=[B["kp"]])
                p, bp = big(xr, bxr, Wr)
                S.op("dve", lambda e: e.tensor_tensor(out=tmp2[:], in0=p[:].rearrange("p a b -> p (a b)"), in1=rkB[:], op=ALU.mult),
                     reads=[bp, bB], writes=[B["tmp2"]])
                S.op("pool", lambda e: e.tensor_tensor(out=tmp3[:].rearrange("p a b -> p (a b)"), in0=tmp2[:], in1=kp[:], op=ALU.mult),
                     reads=[B["tmp2"], B["kp"]], writes=[B["tmp3"]])
                S.op("dve", lambda e: e.tensor_reduce(out=bsum[:], in_=tmp3[:], axis=AX.X, op=ALU.add), reads=[B["tmp3"]], writes=[B["bsum"]])
                p, bp = big(xv, bxv, Wv)
                o, bo = bvo[k % 2], bbvo[k % 2]
                S.op("dve", lambda e: e.tensor_tensor(out=o[:], in0=p[:].rearrange("p a (h d) -> p (a h) d", d=64),
                                                      in1=bsum[:].to_broadcast([128, 16, 64]), op=ALU.mult), reads=[bp, B["bsum"]], writes=[bo])
                S.dma("sp", BV[t0:t0 + 128, :], o[:].rearrange("p a b -> p (a b)"), bo, reads=[bo], writes=[BVB])
                o, bo = vto[k % 2], bvto[k % 2]
                S.op("act", lambda e: e.copy(out=o[:], in_=p[:].rearrange("p a b -> p (a b)")), reads=[bp], writes=[bo])
                S.dma("sp", Vtok[t0:t0 + 128, :], o[:], bo, reads=[bo], writes=[VB])
                p, bp = pp[npp % 2], bpp[npp % 2]
                npp += 1
                for hh in range(2):
                    S.op("pe", lambda e: e.matmul(p[:, hh, :], lhsT=sg1a[:, ts], rhs=G2a[:, hh * 512:(hh + 1) * 512], start=True, stop=False),
                         reads=[bW, B["sg1a"]], writes=[bp], sig=False)
                    S.op("pe", lambda e: e.matmul(p[:, hh, :], lhsT=sg1b[:, ts], rhs=G2b[:, hh * 512:(hh + 1) * 512], start=False, stop=True),
                         reads=[bW, B["sg1b"]], writes=[bp])
                o, bo = gto[k % 2], bgto[k % 2]
                S.op("act", lambda e: e.copy(out=o[:], in_=p[:].rearrange("p a b -> p (a b)")), reads=[bp], writes=[bo])
                S.dma("sp", Gt[t0:t0 + 128, :], o[:], bo, reads=[bo], writes=[GB])
                k += 1
        S.barrier()
        P.release()
        for b in [bW, bB] + bbvo + bvto + bgto:
            S.release(b)


def rwkv_scan_phase(nc, S, RtT, KtT, AtT, BtT, WC, Vtok, Ysc, bins, bY, ntok, NI=4):
    nch = ntok // 64
    ngr = nch // 8
    with ExitStack() as es:
        sb = lambda name, shape, dt: es.enter_context(nc.sbuf_tensor(_uniq(name), shape, dt))
        ps = lambda name, shape, dt: es.enter_context(nc.psum_tensor(_uniq(name), shape, dt))
        MU = sb("MU", [128, 128], F32)
        MUI = sb("MUI", [128, 128], F32)
        ML = sb("ML", [128, 128], F32)
        I32 = sb("I32", [128, 128], F32)
        identb = sb("identb", [128, 128], BF16)
        bconst = S.buf("const")
        for (m, chm, pat, op) in ((MU, -1, 1, ALU.is_gt), (MUI, -1, 1, ALU.is_ge), (ML, 1, -1, ALU.is_gt)):
            S.op("pool", lambda e: e.memset(m[:], 1.0), writes=[bconst])
            S.op("pool", lambda e: e.affine_select(out=m[:], in_=m[:], pattern=[[pat, 128]], compare_op=op, fill=0.0, base=0,
                                                   channel_multiplier=chm), reads=[bconst], writes=[bconst])
        make_ident(nc, S, I32, bconst)
        make_ident(nc, S, identb, bconst)

        class Slot:
            pass
        slots = []
        for si in range(NI):
            s = Slot()
            s.i = si
            n_ = lambda x: "%s_%d" % (x, si)
            s.bd = {k: [sb(n_(k) + "_%d" % j, [128, 8, 128], BF16) for j in range(2)] for k in ("A", "B", "K", "R")}
            s.bbd = [S.dbuf(n_("bd0")), S.dbuf(n_("bd1"))]
            s.Vs = [sb(n_("Vs%d" % j), [128, 8, 64], BF16) for j in range(2)]
            s.Yo = [sb(n_("Yo%d" % j), [128, 8, 64], F32) for j in range(2)]
            s.bYo = [S.dbuf(n_("Yo0")), S.dbuf(n_("Yo1"))]
            s.wcs = sb(n_("wcs"), [128, nch], F32)
            s.bwcs = S.dbuf(n_("wcs"))
            s.N = [sb(n_("N%d" % j), [128, 128], F32) for j in range(2)]
            s.P = [sb(n_("P%d" % j), [128, 128], F32) for j in range(2)]
            s.X = [sb(n_("X%d" % j), [128, 128], F32) for j in range(2)]
            s.bN = [S.buf("N") for _ in range(2)]
            s.bP = [S.buf("P") for _ in range(2)]
            s.bX = [S.buf("X") for _ in range(2)]
            for k in ("Mak", "Mrb", "Mrk", "BtT", "KtT"):
                setattr(s, k, sb(n_(k), [128, 128], BF16))
                setattr(s, "b" + k, S.buf(k))
            s.Xs = sb(n_("Xs"), [128, 64], F32)
            s.Ub = sb(n_("Ub"), [128, 64], BF16)
            s.Sw = sb(n_("Sw"), [128, 64], F32)
            s.St = sb(n_("St"), [128, 64], F32)
            s.Sb = sb(n_("Sb"), [128, 64], BF16)
            s.bXs, s.bUb, s.bSw, s.bSt, s.bSb = [S.buf(k) for k in ("Xs", "Ub", "Sw", "St", "Sb")]
            s.psA = ps(n_("psA"), [128, 4, 128], F32)
            s.psB = ps(n_("psB"), [128, 4, 128], F32)
            s.bA, s.bB = S.pbuf("bankA"), S.pbuf("bankB")
            s.ptr = s.psB[:, 3, :].bitcast(BF16)
            s.bptr = s.bB
            for k in ("A", "B", "K", "R"):
                for j in range(2):
                    S.op("pool", lambda e: e.memset(s.bd[k][j][:], 0.0), writes=[s.bbd[j]])
            slots.append(s)

        def PA(s, k):
            return s.psA[:, k, :], s.bA

        def PB(s, k):
            return s.psB[:, k, :], s.bB

        srcs = {"A": AtT, "B": BtT, "K": KtT, "R": RtT}
        bsrc = {"A": bins[2], "B": bins[3], "K": bins[1], "R": bins[0]}
        bWC, bV = bins[4], bins[5]

        def load_group(s, hp, gg):
            j = gg % 2
            t0 = gg * 512
            for k in ("A", "B", "K", "R"):
                for h in range(2):
                    r0 = hp * 128 + h * 64
                    S.dma("sp", s.bd[k][j][h * 64:(h + 1) * 64, :, h * 64:(h + 1) * 64],
                          srcs[k][r0:r0 + 64, t0:t0 + 512].rearrange("p (c j) -> p c j", j=64), s.bbd[j],
                          reads=[bsrc[k]], writes=[s.bbd[j]])
            for h in range(2):
                c0 = hp * 128 + h * 64
                S.dma("sp", s.Vs[j][h * 64:(h + 1) * 64, :, :], Vtok[t0:t0 + 512, c0:c0 + 64].rearrange("(c j) v -> j c v", j=64),
                      s.bbd[j], reads=[bV], writes=[s.bbd[j]])

        for rnd in range(8 // NI):
            hps = [rnd * NI + i for i in range(NI)]
            for s, hp in zip(slots, hps):
                S.dma("sp", s.wcs[:], WC[hp * 128:(hp + 1) * 128, 0:nch], s.bwcs, reads=[bWC], writes=[s.bwcs])
                S.op("pool", lambda e: e.memset(s.St[:], 0.0), writes=[s.bSt])
                S.op("pool", lambda e: e.memset(s.Sb[:], 0.0), writes=[s.bSb])
                load_group(s, hp, 0)
            for gg in range(ngr):
                j = gg % 2
                if gg + 1 < ngr:
                    for s, hp in zip(slots, hps):
                        load_group(s, hp, gg + 1)
                for c in range(8):
                    ch = gg * 8 + c
                    for s in slots:
                        A, Bd, K, R = [s.bd[k][j][:, c, :] for k in ("A", "B", "K", "R")]
                        bb = s.bbd[j]
                        s.p1, s.bp1 = PA(s, 0)
                        S.op("pe", lambda e: e.matmul(s.p1, lhsT=Bd, rhs=A, start=True, stop=True), reads=[bb], writes=[s.bp1], sig=False)
                        s.p2, s.bp2 = PA(s, 1)
                        S.op("pe", lambda e: e.matmul(s.p2, lhsT=A, rhs=Bd, start=True, stop=True), reads=[bb], writes=[s.bp2])
                    for s in slots:
                        S.op("dve", lambda e: e.tensor_tensor(out=s.N[0][:], in0=s.p1, in1=MU[:], op=ALU.mult), reads=[s.bp1, bconst], writes=[s.bN[0]])
                        S.op("dve", lambda e: e.tensor_tensor(out=s.P[0][:], in0=s.p2, in1=ML[:], op=ALU.mult), reads=[s.bp2, bconst], writes=[s.bP[0]])
                        S.op("pool", lambda e: e.tensor_tensor(out=s.X[0][:], in0=s.N[0][:], in1=I32[:], op=ALU.add), reads=[s.bN[0], bconst], writes=[s.bX[0]])
                    for s in slots:
                        A, Bd, K, R = [s.bd[k][j][:, c, :] for k in ("A", "B", "K", "R")]
                        bb = s.bbd[j]
                        trip = (("Mak", K, A, MU), ("Mrb", Bd, R, MUI), ("Mrk", K, R, MUI))
                        for n_, (nm, l_, r_, msk) in enumerate(trip):
                            p, bp = PB(s, n_)
                            S.op("pe", lambda e: e.matmul(p, lhsT=l_, rhs=r_, start=True, stop=True), reads=[bb], writes=[bp], sig=False)
                        S.op("pe", lambda e: e.transpose(s.ptr[:, 0:128], Bd, identb[:]), reads=[bb, bconst], writes=[s.bptr], sig=False)
                        S.op("pe", lambda e: e.transpose(s.ptr[:, 128:256], K, identb[:]), reads=[bb, bconst], writes=[s.bptr])
                        for n_, (nm, l_, r_, msk) in enumerate(trip):
                            p, bp = PB(s, n_)
                            S.op("dve", lambda e: e.tensor_tensor(out=getattr(s, nm)[:], in0=p, in1=msk[:], op=ALU.mult), reads=[bp, bconst],
                                 writes=[getattr(s, "b" + nm)])
                        S.op("act", lambda e: e.copy(out=s.BtT[:], in_=s.ptr[:, 0:128]), reads=[s.bptr], writes=[s.bBtT])
                        S.op("act", lambda e: e.copy(out=s.KtT[:], in_=s.ptr[:, 128:256]), reads=[s.bptr], writes=[s.bKtT])
                    cur = 0
                    for lvl in range(5):
                        nxt = 1 - cur
                        last = (lvl == 4)
                        for s in slots:
                            if not last:
                                s.pq, s.bpq = PA(s, 0)
                                S.op("pe", lambda e: e.matmul(s.pq, lhsT=s.P[cur][:], rhs=s.N[cur][:], start=True, stop=True),
                                     reads=[s.bP[cur], s.bN[cur]], writes=[s.bpq])
                            s.pp_, s.bpp = PA(s, 1)
                            S.op("pe", lambda e: e.matmul(s.pp_, lhsT=s.N[cur][:], rhs=s.P[cur][:], start=True, stop=True),
                                 reads=[s.bP[cur], s.bN[cur]], writes=[s.bpp])
                        for s in slots:
                            if not last:
                                S.op("act", lambda e: e.copy(out=s.N[nxt][:], in_=s.pq), reads=[s.bpq], writes=[s.bN[nxt]])
                            S.op("act", lambda e: e.copy(out=s.P[nxt][:], in_=s.pp_), reads=[s.bpp], writes=[s.bP[nxt]])
                        for s in slots:
                            s.px, s.bpx = PB(s, 0)
                            S.op("pe", lambda e: e.matmul(s.px, lhsT=s.P[nxt][:], rhs=s.X[cur][:], start=True, stop=True),
                                 reads=[s.bP[nxt], s.bX[cur]], writes=[s.bpx])
                        for s in slots:
                            S.op("dve", lambda e: e.tensor_tensor(out=s.X[nxt][:], in0=s.px, in1=s.X[cur][:], op=ALU.add),
                                 reads=[s.bpx, s.bX[cur]], writes=[s.bX[nxt]])
                        cur = nxt
                    TT, bTT = cur, None
                    for s in slots:
                        A = s.bd["A"][j][:, c, :]
                        s.pX, s.bpX = PA(s, 2)
                        S.op("pe", lambda e: e.matmul(s.pX[:, 0:64], lhsT=A, rhs=s.Sb[:], start=True, stop=False),
                             reads=[s.bbd[j], s.bSb], writes=[s.bpX], sig=False)
                        S.op("pe", lambda e: e.matmul(s.pX[:, 0:64], lhsT=s.Mak[:], rhs=s.Vs[j][:, c, :], start=False, stop=True),
                             reads=[s.bMak, s.bbd[j]], writes=[s.bpX])
                        S.op("pool", lambda e: e.tensor_scalar(out=s.Sw[:], in0=s.St[:], scalar1=s.wcs[:, ch:ch + 1], scalar2=None, op0=ALU.mult),
                             reads=[s.bSt, s.bwcs], writes=[s.bSw])
                    for s in slots:
                        S.op("act", lambda e: e.copy(out=s.Xs[:], in_=s.pX[:, 0:64]), reads=[s.bpX], writes=[s.bXs])
                    for s in slots:
                        s.pU, s.bpU = PB(s, 1)
                        S.op("pe", lambda e: e.matmul(s.pU[:, 0:64], lhsT=s.X[TT][:], rhs=s.Xs[:], start=True, stop=True),
                             reads=[s.bX[TT], s.bXs], writes=[s.bpU])
                    for s in slots:
                        S.op("dve", lambda e: e.tensor_copy(out=s.Ub[:], in_=s.pU[:, 0:64]), reads=[s.bpU], writes=[s.bUb])
                    for s in slots:
                        R = s.bd["R"][j][:, c, :]
                        s.pY, s.bpY = PA(s, 0)
                        S.op("pe", lambda e: e.matmul(s.pY[:, 0:64], lhsT=R, rhs=s.Sb[:], start=True, stop=False),
                             reads=[s.bbd[j], s.bSb], writes=[s.bpY], sig=False)
                        S.op("pe", lambda e: e.matmul(s.pY[:, 0:64], lhsT=s.Mrb[:], rhs=s.Ub[:], start=False, stop=False),
                             reads=[s.bMrb, s.bUb], writes=[s.bpY], sig=False)
                        S.op("pe", lambda e: e.matmul(s.pY[:, 0:64], lhsT=s.Mrk[:], rhs=s.Vs[j][:, c, :], start=False, stop=True),
                             reads=[s.bMrk, s.bbd[j]], writes=[s.bpY])
                        s.pS, s.bpS = PA(s, 1)
                        S.op("pe", lambda e: e.matmul(s.pS[:, 0:64], lhsT=s.BtT[:], rhs=s.Ub[:], start=True, stop=False),
                             reads=[s.bBtT, s.bUb], writes=[s.bpS], sig=False)
                        S.op("pe", lambda e: e.matmul(s.pS[:, 0:64], lhsT=s.KtT[:], rhs=s.Vs[j][:, c, :], start=False, stop=True),
                             reads=[s.bKtT, s.bbd[j]], writes=[s.bpS])
                    for s in slots:
                        S.op("act", lambda e: e.copy(out=s.Yo[j][:, c, :], in_=s.pY[:, 0:64]), reads=[s.bpY], writes=[s.bYo[j]])
                        S.op("dve", lambda e: e.scalar_tensor_tensor(out=s.St[:], in0=s.pS[:, 0:64], scalar=s.wcs[:, ch:ch + 1], in1=s.Sw[:],
                                                                     op0=ALU.mult, op1=ALU.add), reads=[s.bpS, s.bwcs, s.bSw], writes=[s.bSt])
                        S.op("act", lambda e: e.copy(out=s.Sb[:], in_=s.St[:]), reads=[s.bSt], writes=[s.bSb])
                for s, hp in zip(slots, hps):
                    for h in range(2):
                        c0 = hp * 128 + h * 64
                        S.dma("sp", Ysc[gg * 512:(gg + 1) * 512, c0:c0 + 64].rearrange("(c j) v -> j c v", j=64),
                              s.Yo[j][h * 64:(h + 1) * 64, :, :], s.bYo[j], reads=[s.bYo[j]], writes=[bY])
        S.barrier()
        for s in slots:
            for b in s.bbd + s.bYo + [s.bwcs]:
                S.release(b)


def rwkv_post_phase(nc, S, Ysc, BV, Gt, lg_row, lb_row, ZT, bins, bZT, ntok, gn_eps=64e-5):
    with ExitStack() as es:
        sb = lambda name, shape, dt: es.enter_context(nc.sbuf_tensor(_uniq(name), shape, dt))
        ps = lambda name, shape, dt: es.enter_context(nc.psum_tensor(_uniq(name), shape, dt))
        lgB = sb("lgB", [128, D], F32)
        lbB = sb("lbB", [128, D], F32)
        ident = sb("ident", [128, 128], BF16)
        nh = sb("nh", [128, 16, 1], F32)
        yt = [sb("yt%d" % i, [128, 16, 64], F32) for i in range(2)]
        bvt = [sb("bvt%d" % i, [128, D], F32) for i in range(2)]
        gt = [sb("gt%d" % i, [128, D], F32) for i in range(2)]
        sm = sb("sm", [128, 16, 1], F32)
        vr = sb("vr", [128, 16, 1], F32)
        rstd = sb("rstd", [128, 16, 1], F32)
        yc = sb("yc", [128, 16, 64], F32)
        sq = sb("sq", [128, 16, 64], F32)
        yn = sb("yn", [128, 16, 64], F32)
        y2 = sb("y2", [128, D], F32)
        zb = [sb("zb%d" % i, [128, D], BF16) for i in range(2)]
        zT = [sb("zT%d" % i, [128, 8, 512], BF16) for i in range(2)]
        p_tr = [ps("p_tr%d" % i, [128, 8, 128], BF16) for i in range(2)]
        bC = S.dbuf("C")
        byt = [S.dbuf("yt") for _ in range(2)]
        bbvt = [S.dbuf("bvt") for _ in range(2)]
        bgt = [S.dbuf("gt") for _ in range(2)]
        bzT = [S.dbuf("zT") for _ in range(2)]
        B = {n: S.buf(n) for n in ["ident", "nh", "sm", "vr", "rstd", "yc", "sq", "yn", "y2", "zb0", "zb1"]}
        bp_tr = [S.pbuf("ptr") for _ in range(2)]
        bYs, bBV, bG = bins
        make_ident(nc, S, ident, B["ident"])
        S.op("pool", lambda e: e.memset(nh[:], -0.5), writes=[B["nh"]])
        S.dma("sp", lgB[:], lg_row.to_broadcast([128, D]), bC, writes=[bC])
        S.dma("sp", lbB[:], lb_row.to_broadcast([128, D]), bC, writes=[bC])
        nt = ntok // 128
        for t in range(nt):
            i = t % 2
            t0 = t * 128
            S.dma("sp", yt[i][:].rearrange("p a b -> p (a b)"), Ysc[t0:t0 + 128, :], byt[i], reads=[bYs], writes=[byt[i]])
            S.dma("sp", bvt[i][:], BV[t0:t0 + 128, :], bbvt[i], reads=[bBV], writes=[bbvt[i]])
            S.dma("sp", gt[i][:], Gt[t0:t0 + 128, :], bgt[i], reads=[bG], writes=[bgt[i]])
            y3 = yt[i]
            S.op("dve", lambda e: e.tensor_reduce(out=sm[:], in_=y3[:], axis=AX.X, op=ALU.add), reads=[byt[i]], writes=[B["sm"]])
            S.op("dve", lambda e: e.tensor_scalar(out=sm[:], in0=sm[:], scalar1=1.0 / 64, scalar2=None, op0=ALU.mult), reads=[B["sm"]], writes=[B["sm"]])
            S.op("dve", lambda e: e.tensor_tensor(out=yc[:], in0=y3[:], in1=sm[:].to_broadcast([128, 16, 64]), op=ALU.subtract),
                 reads=[byt[i], B["sm"]], writes=[B["yc"]])
            S.op("pool", lambda e: e.tensor_tensor(out=sq[:], in0=yc[:], in1=yc[:], op=ALU.mult), reads=[B["yc"]], writes=[B["sq"]])
            S.op("dve", lambda e: e.tensor_reduce(out=vr[:], in_=sq[:], axis=AX.X, op=ALU.add), reads=[B["sq"]], writes=[B["vr"]])
            S.op("dve", lambda e: e.tensor_scalar(out=vr[:], in0=vr[:], scalar1=1.0 / 64, scalar2=gn_eps, op0=ALU.mult, op1=ALU.add),
                 reads=[B["vr"]], writes=[B["vr"]])
            S.op("pool", lambda e: e.tensor_tensor(out=rstd[:], in0=vr[:], in1=nh[:], op=ALU.pow), reads=[B["vr"], B["nh"]], writes=[B["rstd"]])
            S.op("dve", lambda e: e.tensor_tensor(out=yn[:], in0=yc[:], in1=rstd[:].to_broadcast([128, 16, 64]), op=ALU.mult),
                 reads=[B["yc"], B["rstd"]], writes=[B["yn"]])
            ynf = yn[:].rearrange("p a b -> p (a b)")
            S.op("pool", lambda e: e.tensor_tensor(out=y2[:], in0=ynf, in1=lgB[:], op=ALU.mult), reads=[B["yn"], bC], writes=[B["y2"]])
            S.op("pool", lambda e: e.tensor_tensor(out=y2[:], in0=y2[:], in1=lbB[:], op=ALU.add), reads=[B["y2"], bC], writes=[B["y2"]])
            S.op("dve", lambda e: e.tensor_tensor(out=y2[:], in0=y2[:], in1=bvt[i][:], op=ALU.add), reads=[B["y2"], bbvt[i]], writes=[B["y2"]])
            z, bz = zb[i], B["zb%d" % i]
            S.op("dve", lambda e: e.tensor_tensor(out=z[:], in0=y2[:], in1=gt[i][:], op=ALU.mult), reads=[B["y2"], bgt[i]], writes=[bz])
            pt, bpt = p_tr[i], bp_tr[i]
            for c in range(8):
                S.op("pe", lambda e: e.transpose(pt[:, c, :], z[:, c * 128:(c + 1) * 128], ident[:]), reads=[bz, B["ident"]], writes=[bpt], sig=(c == 7))
            gi = (t // 4) % 2
            S.op("act", lambda e: e.copy(out=zT[gi][:, :, (t % 4) * 128:(t % 4 + 1) * 128], in_=pt[:]), reads=[bpt], writes=[bzT[gi]])
            if t % 4 == 3:
                g0 = (t // 4) * 512
                for c in range(8):
                    S.dma("sp", ZT[c * 128:(c + 1) * 128, g0:g0 + 512], zT[gi][:, c, :], bzT[gi], reads=[bzT[gi]], writes=[bZT])
        S.barrier()
        for b in [bC] + byt + bbvt + bgt + bzT:
            S.release(b)


def build_program(ntok=SEQ):
    nc = bass.Bass("TRN2", target_bir_lowering=False)
    di = lambda n, s: nc.dram_tensor(n, list(s), F32, kind="ExternalInput").ap()
    x = di("x", [ntok, D])
    ffn_norm = di("ffn_norm", [4, D])
    wg = di("ffn_w_gate", [2, 2, D, DFF])
    wu = di("ffn_w_up", [2, 2, D, DFF])
    wd = di("ffn_w_down", [2, 2, DFF, D])
    mix_norm = di("mix_norm", [2, D])
    dbias = di("dbias", [12, 128, 2, 128])
    w_in = di("attn_w_in", [D, 3072])
    qn = di("attn_q_norm", [64, 1])
    kn = di("attn_k_norm", [64, 1])
    w_out = di("attn_w_out", [512, D])
    rw_mix = di("rw_mix", [6, D])
    rows = {n: di(n, [1, D]) for n in ("rw_w0", "rw_a0", "rw_kk", "rw_ka", "rw_rk", "rw_lnx_g", "rw_lnx_b")}
    rw_w1 = di("rw_w1", [D, 64]); rw_w2 = di("rw_w2", [64, D]); rw_a1 = di("rw_a1", [D, 64]); rw_a2 = di("rw_a2", [64, D])
    rw_g1 = di("rw_g1", [D, 160]); rw_g2 = di("rw_g2", [160, D])
    rw_wr = di("rw_wr", [D, D]); rw_wk = di("rw_wk", [D, D]); rw_wv = di("rw_wv", [D, D]); rw_wo = di("rw_wo", [D, D])
    out = nc.dram_tensor("out", [ntok, D], F32, kind="ExternalOutput").ap()
    scr = lambda n, s, dt: nc.dram_tensor(n, list(s), dt, kind="Internal").ap()
    xa = scr("xa", [ntok, D], F32); xb = scr("xb", [ntok, D], F32)
    QT = scr("QT", [D, ntok], BF16); KT = scr("KT", [D, ntok], BF16); V = scr("V", [ntok, D], BF16); MT = scr("MT", [512, ntok], BF16)
    RtT, KtT, AtT, BtT = [scr(n, [D, ntok], BF16) for n in ("RtT", "KtT", "AtT", "BtT")]
    WC = scr("WC", [D, ntok // 64], F32)
    Vtok = scr("Vtok", [ntok, D], BF16); BV = scr("BV", [ntok, D], F32); Gt = scr("Gt", [ntok, D], F32); Ysc = scr("Ysc", [ntok, D], F32)
    ZT = scr("ZT", [D, ntok], BF16)

    S = Sched(nc, n_dma_sems=24)
    nb = lambda n: S.buf(n, acc=True)
    bx, bxa, bxb, bout = nb("x"), nb("xa"), nb("xb"), nb("out")
    bQT, bKT, bV, bMT = nb("QT"), nb("KT"), nb("V"), nb("MT")
    bR, bK, bA, bB, bWC, bVt, bBV, bG, bY, bZ = [nb(n) for n in ("R", "K", "A", "B", "WC", "Vt", "BV", "G", "Y", "Z")]

    ffn_phase(nc, S, x, xa, bx, bxa, wg[0, 0], wu[0, 0], wd[0, 0], ffn_norm[0:1, :], ntok)
    attn_in_phase(nc, S, xa, bxa, w_in, mix_norm[0:1, :], qn, kn, QT, KT, V, bQT, bKT, bV, ntok)
    sb_attn_phase(nc, S, QT, KT, V, MT, bQT, bKT, bV, bMT, ntok)
    dil_attn_phase(nc, S, QT, KT, V, MT, dbias, bQT, bKT, bV, bMT, ntok)
    out_proj_phase(nc, S, xa, xb, bxa, bxb, MT, bMT, w_out, 512, ntok)
    ffn_phase(nc, S, xb, xa, bxb, bxa, wg[0, 1], wu[0, 1], wd[0, 1], ffn_norm[1:2, :], ntok)
    ffn_phase(nc, S, xa, xb, bxa, bxb, wg[1, 0], wu[1, 0], wd[1, 0], ffn_norm[2:3, :], ntok)
    rwkv_fm_phase(nc, S, xb, bxb, mix_norm[1:2, :], rw_mix, rows["rw_w0"], rw_w1, rw_w2, rows["rw_a0"], rw_a1, rw_a2,
                  rows["rw_kk"], rows["rw_ka"], rw_wr, rw_wk, RtT, KtT, AtT, BtT, WC, [bR, bK, bA, bB, bWC], ntok)
    rwkv_tm_phase(nc, S, xb, bxb, mix_norm[1:2, :], rw_mix, rows["rw_a0"], rw_a1, rw_a2, rw_g1, rw_g2, rows["rw_ka"], rows["rw_rk"],
                  rw_wr, rw_wk, rw_wv, Vtok, BV, Gt, [bVt, bBV, bG], ntok)
    rwkv_scan_phase(nc, S, RtT, KtT, AtT, BtT, WC, Vtok, Ysc, [bR, bK, bA, bB, bWC, bVt], bY, ntok)
    rwkv_post_phase(nc, S, Ysc, BV, Gt, rows["rw_lnx_g"], rows["rw_lnx_b"], ZT, [bY, bBV, bG], bZ, ntok)
    out_proj_phase(nc, S, xb, xa, bxb, bxa, ZT, bZ, rw_wo, 1024, ntok)
    ffn_phase(nc, S, xa, out, bxa, bout, wg[1, 1], wu[1, 1], wd[1, 1], ffn_norm[3:4, :], ntok)
    S.wait_for("sp", [bout])
    return nc


def kernel(x, ffn_norm, ffn_w_gate, ffn_w_up, ffn_w_down, mix_norm, rel_bias,
           attn_w_in, attn_q_norm, attn_k_norm, attn_w_out,
           rw_mix, rw_w0, rw_w1, rw_w2, rw_a0, rw_a1, rw_a2, rw_g1, rw_g2,
           rw_kk, rw_ka, rw_rk, rw_wr, rw_wk, rw_wv, rw_wo, rw_lnx_g, rw_lnx_b):
    f = lambda a: np.ascontiguousarray(np.asarray(a, dtype=np.float32))
    x = f(x)
    n = x.shape[0]
    shared = {
        "ffn_norm": f(ffn_norm).reshape(4, D), "ffn_w_gate": f(ffn_w_gate), "ffn_w_up": f(ffn_w_up), "ffn_w_down": f(ffn_w_down),
        "mix_norm": f(mix_norm), "dbias": dil_bias_host(f(rel_bias)),
        "attn_w_in": f(attn_w_in)[0], "attn_q_norm": f(attn_q_norm).reshape(64, 1), "attn_k_norm": f(attn_k_norm).reshape(64, 1),
        "attn_w_out": f(attn_w_out)[0], "rw_mix": f(rw_mix)[0],
        "rw_w0": f(rw_w0).reshape(1, D), "rw_a0": f(rw_a0).reshape(1, D), "rw_kk": f(rw_kk).reshape(1, D), "rw_ka": f(rw_ka).reshape(1, D),
        "rw_rk": f(rw_rk).reshape(1, D), "rw_lnx_g": f(rw_lnx_g).reshape(1, D), "rw_lnx_b": f(rw_lnx_b).reshape(1, D),
        "rw_w1": f(rw_w1)[0], "rw_w2": f(rw_w2)[0], "rw_a1": f(rw_a1)[0], "rw_a2": f(rw_a2)[0], "rw_g1": f(rw_g1)[0], "rw_g2": f(rw_g2)[0],
        "rw_wr": f(rw_wr)[0], "rw_wk": f(rw_wk)[0], "rw_wv": f(rw_wv)[0], "rw_wo": f(rw_wo)[0],
    }
    nc = build_program(x.shape[1])
    in_maps = [dict(shared, x=x[i]) for i in range(n)]
    res = run_bass_kernel_spmd(nc, in_maps, core_ids=list(range(n)))
    return np.stack([np.asarray(r["out"]) for r in res.results], axis=0).astype(np.float32)
```

```python
import numpy as np
from contextlib import ExitStack
import concourse.bass as bass
import concourse.mybir as mybir
from concourse.bass_utils import run_bass_kernel_spmd

F32 = mybir.dt.float32
BF16 = mybir.dt.bfloat16
AF = mybir.ActivationFunctionType
ALU = mybir.AluOpType
AX = mybir.AxisListType

D = 1024
DFF = 2816
NF = DFF // 128
SEQ = 4096


_UID = [0]


def _uniq(name):
    _UID[0] += 1
    return "%s_u%d" % (name, _UID[0])


def _merge(d, s):
    for k, v in s.items():
        if d.get(k, 0) < v:
            d[k] = v


class Buf:
    __slots__ = ("name", "wr", "rd", "acc", "dkey", "excl")

    def __init__(self, name, acc=False, excl=False):
        self.name = name
        self.wr = {}
        self.rd = {}
        self.acc = acc
        self.dkey = None
        self.excl = excl


class Sched:
    ENG = ("pe", "act", "dve", "pool", "sp")

    def __init__(self, nc, n_dma_sems=40):
        self.nc = nc
        self.eng = {"pe": nc.tensor, "act": nc.scalar, "dve": nc.vector, "pool": nc.gpsimd, "sp": nc.sync}
        self.sems = {}
        self.val = {}
        self.seen = {e: {} for e in self.ENG}
        self.epoch = 0
        self.ekey = {}
        self._new_engine_sems()
        self.dma_pool = []
        for i in range(n_dma_sems):
            k = "dma%d" % i
            self.sems[k] = nc.semaphore(k).__enter__()
            self.val[k] = 0
            self.dma_pool.append(k)
        self.nwait = 0

    def _new_engine_sems(self):
        for e in self.ENG:
            k = "%s_e%d" % (e, self.epoch)
            self.sems[k] = self.nc.semaphore(k).__enter__()
            self.val[k] = 0
            self.ekey[e] = k

    def buf(self, name, acc=False):
        return Buf(name, acc)

    def pbuf(self, name):
        return Buf(name, False, True)

    def dbuf(self, name, acc=False):
        b = Buf(name, acc)
        b.dkey = self.dma_pool.pop()
        return b

    def release(self, b):
        self.dma_pool.append(b.dkey)
        b.dkey = None

    def _wait(self, e, deps):
        for k, v in deps.items():
            if v <= 0:
                continue
            if e == "pe" and k == self.ekey["pe"]:
                continue
            if self.seen[e].get(k, 0) < v:
                self.eng[e].wait_ge(self.sems[k], v)
                self.seen[e][k] = v
                self.nwait += 1

    def _deps(self, reads, writes, e=None):
        deps = {}
        for b in reads:
            _merge(deps, b.wr)
            if b.excl:
                own = self.ekey.get(e)
                _merge(deps, {k: v for k, v in b.rd.items() if k != own})
        for b in writes:
            _merge(deps, b.wr)
            _merge(deps, b.rd)
        return deps

    def _record(self, ev, reads, writes):
        for b in reads:
            _merge(b.rd, ev)
        for b in writes:
            if b.acc:
                _merge(b.wr, ev)
            else:
                b.wr = dict(ev)
                b.rd = {}

    def op(self, e, fn, reads=(), writes=(), sig=True):
        self._wait(e, self._deps(reads, writes, e))
        ins = fn(self.eng[e])
        k = self.ekey[e]
        if sig:
            ins.then_inc(self.sems[k], 1)
            self.val[k] += 1
            v = self.val[k]
        else:
            v = self.val[k] + 1
        self._record({k: v}, reads, writes)
        return ins

    def dma(self, q, out, in_, track, reads=(), writes=(), slow=False):
        self._wait(q, self._deps(reads, writes))
        if slow:
            ins = self.eng[q].dma_start(out=out, in_=in_, allow_slow_non_contiguous=True)
        else:
            ins = self.eng[q].dma_start(out=out, in_=in_)
        k = track.dkey
        ins.then_inc(self.sems[k], 16)
        self.val[k] += 16
        self._record({k: self.val[k]}, reads, writes)
        return ins

    def barrier(self, new_epoch=True):
        allv = {k: v for k, v in self.val.items() if v > 0}
        for e in self.ENG:
            self._wait(e, allv)
        if new_epoch:
            self.epoch += 1
            self._new_engine_sems()

    def wait_for(self, e, bufs):
        deps = {}
        for b in bufs:
            _merge(deps, b.wr)
            _merge(deps, b.rd)
        self._wait(e, deps)


def ffn_phase(nc, S, x_in, x_out, xin_b, xout_b, wg, wu, wd, gain_row, ntok, eps=1e-6):
    G = 256
    ng = ntok // G
    with ExitStack() as es:
        sb = lambda name, shape, dt: es.enter_context(nc.sbuf_tensor(_uniq(name), shape, dt))
        ps = lambda name, shape, dt: es.enter_context(nc.psum_tensor(_uniq(name), shape, dt))
        Wg = sb("Wg", [128, 8, DFF], BF16)
        Wu = sb("Wu", [128, 8, DFF], BF16)
        Wd = sb("Wd", [128, NF, D], BF16)
        gB = sb("gB", [128, D], F32)
        ident = sb("ident", [128, 128], BF16)
        xt = [sb("xt%d" % i, [128, D], F32) for i in range(4)]
        ot = [sb("ot%d" % i, [128, D], F32) for i in range(2)]
        hb = [sb("hb%d" % i, [128, D], BF16) for i in range(2)]
        hT = [sb("hT%d" % i, [128, 8, G], BF16) for i in range(2)]
        aT = [sb("aT%d" % i, [128, G], BF16) for i in range(3)]
        sg = [sb("sg%d" % i, [128, G], F32) for i in range(2)]
        junk = sb("junk", [128, D], BF16)
        ss = [sb("ss%d" % i, [128, 1], F32) for i in range(2)]
        rs = [sb("rs%d" % i, [128, 1], F32) for i in range(2)]
        nh = sb("nh", [128, 1], F32)
        p_gu = [ps("p_gu%d" % i, [128, 2, G], F32) for i in range(2)]
        p_dn = [ps("p_dn%d" % i, [128, 512], F32) for i in range(4)]
        p_tr = [ps("p_tr%d" % i, [128, 8, 128], BF16) for i in range(2)]

        bWg, bWu, bWd, bgB = S.dbuf("Wg"), S.dbuf("Wu"), S.dbuf("Wd"), S.dbuf("gB")
        bxt = [S.dbuf("xt%d" % i) for i in range(4)]
        bot = [S.dbuf("ot%d" % i) for i in range(2)]
        bhb = [S.buf("hb") for _ in range(2)]
        bhT = [S.buf("hT") for _ in range(2)]
        baT = [S.buf("aT") for _ in range(3)]
        bsg = [S.buf("sg") for _ in range(2)]
        bjunk = S.buf("junk")
        bss = [S.buf("ss") for _ in range(2)]
        brs = [S.buf("rs") for _ in range(2)]
        bnh, bident = S.buf("nh"), S.buf("ident")
        bp_gu = [S.pbuf("pgu") for _ in range(2)]
        bp_dn = [S.pbuf("pdn") for _ in range(4)]
        bp_tr = [S.pbuf("ptr") for _ in range(2)]

        S.op("pool", lambda e: e.memset(nh[:], -0.5), writes=[bnh])
        S.op("pool", lambda e: e.memset(ident[:], 0.0), writes=[bident])
        S.op("pool", lambda e: e.affine_select(out=ident[:], in_=ident[:], pattern=[[-1, 128]],
                                               compare_op=ALU.not_equal, fill=1.0, base=0,
                                               channel_multiplier=1), reads=[bident], writes=[bident])
        S.dma("sp", gB[:], gain_row.to_broadcast([128, D]), bgB, writes=[bgB])
        wg_v = wg.rearrange("(c p) f -> p c f", p=128)
        wu_v = wu.rearrange("(c p) f -> p c f", p=128)
        wd_v = wd.rearrange("(c p) f -> p c f", p=128)
        for c in range(8):
            S.dma("pool", Wg[:, c, :], wg_v[:, c, :], bWg, writes=[bWg])
            S.dma("pool", Wu[:, c, :], wu_v[:, c, :], bWu, writes=[bWu])
        for c in range(NF):
            S.dma("pool", Wd[:, c, :], wd_v[:, c, :], bWd, writes=[bWd])

        def load(g):
            for s in range(2):
                i = (g % 2) * 2 + s
                t0 = g * G + s * 128
                S.dma("sp", xt[i][:], x_in[t0:t0 + 128, :], bxt[i], reads=[xin_b], writes=[bxt[i]])

        def prep_dve(g):
            for s in range(2):
                i = (g % 2) * 2 + s
                S.op("dve", lambda e: e.scalar_tensor_tensor(out=junk[:], in0=xt[i][:], scalar=1.0, in1=xt[i][:],
                                                             op0=ALU.mult, op1=ALU.mult, accum_out=ss[s][:]),
                     reads=[bxt[i]], writes=[bjunk, bss[s]])
                S.op("dve", lambda e: e.tensor_scalar(out=ss[s][:], in0=ss[s][:], scalar1=1.0 / D, scalar2=eps,
                                                      op0=ALU.mult, op1=ALU.add), reads=[bss[s]], writes=[bss[s]])
                S.op("pool", lambda e: e.tensor_tensor(out=rs[s][:], in0=ss[s][:], in1=nh[:], op=ALU.pow),
                     reads=[bss[s], bnh], writes=[brs[s]])
                S.op("dve", lambda e: e.scalar_tensor_tensor(out=hb[s][:], in0=xt[i][:], scalar=rs[s][:], in1=gB[:],
                                                             op0=ALU.mult, op1=ALU.mult),
                     reads=[bxt[i], brs[s], bgB], writes=[bhb[s]])

        def prep_pe(g):
            for s in range(2):
                for c in range(8):
                    S.op("pe", lambda e: e.transpose(p_tr[s][:, c, :], hb[s][:, c * 128:(c + 1) * 128], ident[:]),
                         reads=[bhb[s], bident], writes=[bp_tr[s]], sig=(c == 7))
                S.op("act", lambda e: e.copy(out=hT[g % 2][:, :, s * 128:(s + 1) * 128], in_=p_tr[s][:]),
                     reads=[bp_tr[s]], writes=[bhT[g % 2]])

        def gate_up(g, j):
            h = hT[g % 2]
            pg = p_gu[j % 2]
            for c in range(8):
                S.op("pe", lambda e: e.matmul(pg[:, 0, :], lhsT=Wg[:, c, j * 128:(j + 1) * 128], rhs=h[:, c, :],
                                              start=(c == 0), stop=(c == 7)),
                     reads=[bWg, bhT[g % 2]], writes=[bp_gu[j % 2]], sig=False)
            for c in range(8):
                S.op("pe", lambda e: e.matmul(pg[:, 1, :], lhsT=Wu[:, c, j * 128:(j + 1) * 128], rhs=h[:, c, :],
                                              start=(c == 0), stop=(c == 7)),
                     reads=[bWu, bhT[g % 2]], writes=[bp_gu[j % 2]], sig=(c == 7))
            S.op("act", lambda e: e.activation(out=sg[j % 2][:], in_=pg[:, 0, :], func=AF.Silu),
                 reads=[bp_gu[j % 2]], writes=[bsg[j % 2]])
            S.op("dve", lambda e: e.tensor_tensor(out=aT[j % 3][:], in0=sg[j % 2][:], in1=pg[:, 1, :], op=ALU.mult),
                 reads=[bsg[j % 2], bp_gu[j % 2]], writes=[baT[j % 3]])

        def down(g, j):
            for s in range(2):
                for hh in range(2):
                    S.op("pe", lambda e: e.matmul(p_dn[s * 2 + hh][:], lhsT=aT[j % 3][:, s * 128:(s + 1) * 128],
                                                  rhs=Wd[:, j, hh * 512:(hh + 1) * 512],
                                                  start=(j == 0), stop=(j == NF - 1)),
                         reads=[baT[j % 3], bWd], writes=[bp_dn[s * 2 + hh]], sig=(j == NF - 1 or (s == 1 and hh == 1)))

        def epilogue(g):
            for s in range(2):
                i = (g % 2) * 2 + s
                for hh in range(2):
                    S.op("dve", lambda e: e.scalar_tensor_tensor(out=ot[s][:, hh * 512:(hh + 1) * 512], in0=p_dn[s * 2 + hh][:],
                                                                 scalar=0.5, in1=xt[i][:, hh * 512:(hh + 1) * 512],
                                                                 op0=ALU.mult, op1=ALU.add),
                         reads=[bp_dn[s * 2 + hh], bxt[i]], writes=[bot[s]])
                t0 = g * G + s * 128
                S.dma("sp", x_out[t0:t0 + 128, :], ot[s][:], bot[s], reads=[bot[s]], writes=[xout_b])

        load(0)
        if ng > 1:
            load(1)
        prep_dve(0)
        prep_pe(0)
        for g in range(ng):
            gate_up(g, 0)
            for j in range(NF):
                if j + 1 < NF:
                    gate_up(g, j + 1)
                elif g + 1 < ng:
                    pass
                down(g, j)
                if j == 3 and g + 1 < ng:
                    prep_dve(g + 1)
                if j == 14 and g + 1 < ng:
                    prep_pe(g + 1)
            epilogue(g)
            if g + 2 < ng:
                load(g + 2)
        S.barrier()
        for b in [bWg, bWu, bWd, bgB] + bxt + bot:
            S.release(b)


def make_ident(nc, S, ident, bident, dt_is_bf16=True):
    S.op("pool", lambda e: e.memset(ident[:], 0.0), writes=[bident])
    S.op("pool", lambda e: e.affine_select(out=ident[:], in_=ident[:], pattern=[[-1, 128]],
                                           compare_op=ALU.not_equal, fill=1.0, base=0,
                                           channel_multiplier=1), reads=[bident], writes=[bident])


def attn_in_phase(nc, S, x_in, xin_b, w_in, gain_row, qn, kn, QT, KT, V, bQT, bKT, bV, ntok, eps=1e-6):
    G = 512
    ng = ntok // G
    with ExitStack() as es:
        sb = lambda name, shape, dt: es.enter_context(nc.sbuf_tensor(_uniq(name), shape, dt))
        ps = lambda name, shape, dt: es.enter_context(nc.psum_tensor(_uniq(name), shape, dt))
        Win = sb("Win", [128, 8, 3072], BF16)
        gB = sb("gB", [128, D], F32)
        ident = sb("ident", [128, 128], BF16)
        bones = sb("bones", [128, 128], BF16)
        gq = sb("gq", [128, 1], F32)
        gk = sb("gk", [128, 1], F32)
        nh = sb("nh", [128, 1], F32)
        eps_ap = sb("eps_ap", [128, 1], F32)
        xt = [sb("xt%d" % i, [128, D], F32) for i in range(4)]
        hb = [sb("hb%d" % i, [128, D], BF16) for i in range(2)]
        hT = [sb("hT%d" % i, [128, 8, G], BF16) for i in range(2)]
        junk = sb("junk", [128, D], BF16)
        ss = [sb("ss%d" % i, [128, 1], F32) for i in range(2)]
        rs = [sb("rs%d" % i, [128, 1], F32) for i in range(2)]
        ob = [sb("ob%d" % i, [128, G], BF16) for i in range(3)]
        sq = [sb("sq%d" % i, [128, G], BF16) for i in range(2)]
        lt = [sb("lt%d" % i, [128, G], F32) for i in range(2)]
        rr = [sb("rr%d" % i, [128, G], F32) for i in range(2)]
        vb = [sb("vb%d" % i, [128, D], BF16) for i in range(2)]
        p_q = [ps("p_q%d" % i, [128, G], F32) for i in range(2)]
        p_s = [ps("p_s%d" % i, [128, G], F32) for i in range(2)]
        p_v = [ps("p_v%d" % i, [128, 512], F32) for i in range(2)]
        p_tr = [ps("p_tr%d" % i, [128, 8, 128], BF16) for i in range(2)]

        bWin, bgB, bgq, bgk = S.dbuf("Win"), S.dbuf("gB"), S.dbuf("gq"), S.dbuf("gk")
        bxt = [S.dbuf("xt") for _ in range(4)]
        bob = [S.dbuf("ob") for _ in range(3)]
        bvb = [S.dbuf("vb") for _ in range(2)]
        bhb = [S.buf("hb") for _ in range(2)]
        bhT = [S.buf("hT") for _ in range(2)]
        bjunk, bnh, bident, bbones = S.buf("junk"), S.buf("nh"), S.buf("ident"), S.buf("bones")
        bss = [S.buf("ss") for _ in range(2)]
        brs = [S.buf("rs") for _ in range(2)]
        bsq = [S.buf("sq") for _ in range(2)]
        blt = [S.buf("lt") for _ in range(2)]
        brr = [S.buf("rr") for _ in range(2)]
        bp_q = [S.pbuf("pq") for _ in range(2)]
        bp_s = [S.pbuf("ps") for _ in range(2)]
        bp_v = [S.pbuf("pv") for _ in range(2)]
        bp_tr = [S.pbuf("ptr") for _ in range(2)]

        S.op("pool", lambda e: e.memset(nh[:], -0.5), writes=[bnh])
        S.op("pool", lambda e: e.memset(eps_ap[:], eps), writes=[bnh])
        make_ident(nc, S, ident, bident)
        S.op("pool", lambda e: e.memset(bones[:], 0.0), writes=[bbones])
        S.op("pool", lambda e: e.memset(bones[0:64, 0:64], 1.0), writes=[bbones])
        S.op("pool", lambda e: e.memset(bones[64:128, 64:128], 1.0), writes=[bbones])
        S.dma("sp", gB[:], gain_row.to_broadcast([128, D]), bgB, writes=[bgB])
        for hh in range(2):
            S.dma("sp", gq[hh * 64:(hh + 1) * 64, :], qn, bgq, writes=[bgq])
            S.dma("sp", gk[hh * 64:(hh + 1) * 64, :], kn, bgk, writes=[bgk])
        S.op("dve", lambda e: e.tensor_scalar(out=gq[:], in0=gq[:], scalar1=0.125, scalar2=None, op0=ALU.mult),
             reads=[bgq], writes=[bgq])
        w_v = w_in.rearrange("(c p) f -> p c f", p=128)
        for c in range(8):
            S.dma("pool", Win[:, c, :], w_v[:, c, :], bWin, writes=[bWin])

        def load(g):
            for s in range(4):
                t0 = g * G + s * 128
                S.dma("sp", xt[s][:], x_in[t0:t0 + 128, :], bxt[s], reads=[xin_b], writes=[bxt[s]])

        def prep(g):
            for s in range(4):
                k = s % 2
                S.op("dve", lambda e: e.scalar_tensor_tensor(out=junk[:], in0=xt[s][:], scalar=1.0, in1=xt[s][:],
                                                             op0=ALU.mult, op1=ALU.mult, accum_out=ss[k][:]),
                     reads=[bxt[s]], writes=[bjunk, bss[k]])
                S.op("dve", lambda e: e.tensor_scalar(out=ss[k][:], in0=ss[k][:], scalar1=1.0 / D, scalar2=eps,
                                                      op0=ALU.mult, op1=ALU.add), reads=[bss[k]], writes=[bss[k]])
                S.op("pool", lambda e: e.tensor_tensor(out=rs[k][:], in0=ss[k][:], in1=nh[:], op=ALU.pow),
                     reads=[bss[k], bnh], writes=[brs[k]])
                S.op("dve", lambda e: e.scalar_tensor_tensor(out=hb[k][:], in0=xt[s][:], scalar=rs[k][:], in1=gB[:],
                                                             op0=ALU.mult, op1=ALU.mult),
                     reads=[bxt[s], brs[k], bgB], writes=[bhb[k]])
                for c in range(8):
                    S.op("pe", lambda e: e.transpose(p_tr[k][:, c, :], hb[k][:, c * 128:(c + 1) * 128], ident[:]),
                         reads=[bhb[k], bident], writes=[bp_tr[k]], sig=(c == 7))
                S.op("act", lambda e: e.copy(out=hT[g % 2][:, :, s * 128:(s + 1) * 128], in_=p_tr[k][:]),
                     reads=[bp_tr[k]], writes=[bhT[g % 2]])

        nob = 0
        for g in range(ng):
            load(g)
            prep(g)
            h = hT[g % 2]
            bh = bhT[g % 2]
            t0 = g * G
            for fc in range(16):
                isq = fc < 8
                ch = fc % 8
                if ch < 2:
                    col0 = (0 if isq else 256) + ch * 128
                else:
                    col0 = (768 if isq else 1536) + (ch - 2) * 128
                pq = p_q[fc % 2]
                for c in range(8):
                    S.op("pe", lambda e: e.matmul(pq[:], lhsT=Win[:, c, col0:col0 + 128], rhs=h[:, c, :],
                                                  start=(c == 0), stop=(c == 7)),
                         reads=[bWin, bh], writes=[bp_q[fc % 2]], sig=(c == 7))
                o = ob[nob % 3]
                bo = bob[nob % 3]
                nob += 1
                if ch < 2:
                    S.op("act", lambda e: e.activation(out=o[:], in_=pq[:], func=AF.Copy, scale=(0.125 if isq else 1.0)),
                         reads=[bp_q[fc % 2]], writes=[bo])
                else:
                    k = fc % 2
                    S.op("act", lambda e: e.activation(out=sq[k][:], in_=pq[:], func=AF.Square),
                         reads=[bp_q[k]], writes=[bsq[k]])
                    S.op("pe", lambda e: e.matmul(p_s[k][:], lhsT=bones[:], rhs=sq[k][:], start=True, stop=True),
                         reads=[bbones, bsq[k]], writes=[bp_s[k]])
                    S.op("act", lambda e: e.activation(out=lt[k][:], in_=p_s[k][:], func=AF.Ln, scale=1.0 / 64, bias=eps_ap[:]),
                         reads=[bp_s[k], bnh], writes=[blt[k]])
                    S.op("act", lambda e: e.activation(out=rr[k][:], in_=lt[k][:], func=AF.Exp, scale=-0.5),
                         reads=[blt[k]], writes=[brr[k]])
                    gcol = gq if isq else gk
                    S.op("dve", lambda e: e.scalar_tensor_tensor(out=o[:], in0=pq[:], scalar=gcol[:], in1=rr[k][:],
                                                                 op0=ALU.mult, op1=ALU.mult),
                         reads=[bp_q[k], brr[k], bgq, bgk], writes=[bo])
                dst, bdst = (QT, bQT) if isq else (KT, bKT)
                S.dma("sp", dst[ch * 128:(ch + 1) * 128, t0:t0 + G], o[:], bo, reads=[bo], writes=[bdst])
            for s in range(4):
                k = s % 2
                for (pv, cols, off) in ((p_v[0], (512, 768), 0), (p_v[0], (2304, 2560), 256), (p_v[1], (2560, 3072), 0)):
                    n = cols[1] - cols[0]
                    for c in range(8):
                        S.op("pe", lambda e: e.matmul(pv[:, off:off + n], lhsT=h[:, c, s * 128:(s + 1) * 128],
                                                      rhs=Win[:, c, cols[0]:cols[1]], start=(c == 0), stop=(c == 7)),
                             reads=[bWin, bh], writes=[bp_v[0], bp_v[1]], sig=(c == 7))
                S.op("act", lambda e: e.copy(out=vb[k][:, 0:512], in_=p_v[0][:]), reads=[bp_v[0]], writes=[bvb[k]])
                S.op("dve", lambda e: e.tensor_copy(out=vb[k][:, 512:1024], in_=p_v[1][:]), reads=[bp_v[1]], writes=[bvb[k]])
                S.dma("sp", V[t0 + s * 128:t0 + (s + 1) * 128, :], vb[k][:], bvb[k], reads=[bvb[k]], writes=[bV])
        S.barrier()
        for b in [bWin, bgB, bgq, bgk] + bxt + bob + bvb:
            S.release(b)


def sb_attn_phase(nc, S, QT, KT, V, MT, bQT, bKT, bV, bMT, ntok):
    nblk = ntok // 128
    NH = 4
    with ExitStack() as es:
        sb = lambda name, shape, dt: es.enter_context(nc.sbuf_tensor(_uniq(name), shape, dt))
        ps = lambda name, shape, dt: es.enter_context(nc.psum_tensor(_uniq(name), shape, dt))
        qT = sb("qT", [128, 2, ntok], BF16)
        kT = sb("kT", [128, 2, ntok], BF16)
        v = sb("v", [128, nblk, 256], BF16)
        ones = sb("ones", [128, 512], F32)
        onec = sb("onec", [128, 1], F32)
        mneg = sb("mneg", [128, 128], BF16)
        ident = sb("ident", [128, 128], BF16)
        mk2 = lambda nm, shape, dt: [[sb("%s%d_%d" % (nm, h, i), shape, dt) for i in range(2)] for h in range(NH)]
        e_ = mk2("e", [128, 512], F32)
        sp_ = mk2("sp", [128, 512], F32)
        cs_ = mk2("cs", [128, 512], F32)
        lw_ = mk2("lw", [128, 512], F32)
        w_ = mk2("w", [128, 512], BF16)
        wT_ = mk2("wT", [128, 4, 128], BF16)
        oT = [[sb("oT%d_%d" % (pr, i), [128, 512], BF16) for i in range(2)] for pr in range(2)]
        p_z = [ps("p_z%d" % h, [128, 512], F32) for h in range(NH)]
        p_w = [ps("p_w%d" % i, [128, 4, 128], BF16) for i in range(2)]
        p_o = [ps("p_o%d" % i, [128, 128], F32) for i in range(2)]

        bq, bk, bv = S.dbuf("qT"), S.dbuf("kT"), S.dbuf("v")
        boT = [[S.dbuf("oT") for _ in range(2)] for _ in range(2)]
        bones, bmneg, bident = S.buf("ones"), S.buf("mneg"), S.buf("ident")
        bb2 = lambda nm: [[S.buf(nm) for _ in range(2)] for _ in range(NH)]
        be, bsp, bcs, blw, bw, bwT = bb2("e"), bb2("sp"), bb2("cs"), bb2("lw"), bb2("w"), bb2("wT")
        bp_z = [S.pbuf("pz") for _ in range(NH)]
        bp_w = [S.pbuf("pw") for _ in range(2)]
        bp_o = [S.pbuf("po") for _ in range(2)]

        S.op("pool", lambda e: e.memset(ones[:], 1.0), writes=[bones])
        S.op("pool", lambda e: e.memset(onec[:], 1.0), writes=[bones])
        make_ident(nc, S, ident, bident)
        S.op("pool", lambda e: e.memset(mneg[:], 0.0), writes=[bmneg])
        S.op("pool", lambda e: e.affine_select(out=mneg[:], in_=mneg[:], pattern=[[-1, 128]], compare_op=ALU.is_gt,
                                               fill=-30000.0, base=0, channel_multiplier=1),
             reads=[bmneg], writes=[bmneg])
        for pr in range(2):
            S.dma("sp", qT[:, pr, :], QT[pr * 128:(pr + 1) * 128, 0:ntok], bq, reads=[bQT], writes=[bq])
            S.dma("sp", kT[:, pr, :], KT[pr * 128:(pr + 1) * 128, 0:ntok], bk, reads=[bKT], writes=[bk])
        S.dma("sp", v[:], V[0:ntok, 0:256].rearrange("(b p) c -> p b c", p=128), bv, reads=[bV], writes=[bv])

        heads = [(h, h // 2, h % 2, slice(64 * (h % 2), 64 * (h % 2) + 64)) for h in range(NH)]
        u = 0
        for qb in range(nblk):
            chunks = [(4 * (qb // 4), qb + 1, True)]
            for c in range(qb // 4 - 1, -1, -1):
                chunks.append((4 * c, 4 * c + 4, False))
            for ci, (b0, b1, diag) in enumerate(chunks):
                W = (b1 - b0) * 128
                nb = b1 - b0
                i = u % 2
                u += 1
                rev = (lambda t: t[:, W - 1::-1] if W < 512 else t[:, ::-1])
                for (h, pr, hh, P) in heads:
                    S.op("pe", lambda e: e.matmul(p_z[h][:, 0:W], lhsT=qT[P, pr, qb * 128:(qb + 1) * 128],
                                                  rhs=kT[P, pr, b0 * 128:b1 * 128], start=True, stop=(not diag)),
                         reads=[bq, bk], writes=[bp_z[h]], sig=(not diag))
                    if diag:
                        S.op("pe", lambda e: e.matmul(p_z[h][:, W - 128:W], lhsT=ident[:], rhs=mneg[:], start=False, stop=True),
                             reads=[bident, bmneg], writes=[bp_z[h]])
                for (h, pr, hh, P) in heads:
                    S.op("act", lambda e: e.activation(out=e_[h][i][:, 0:W], in_=p_z[h][:, 0:W], func=AF.Exp),
                         reads=[bp_z[h]], writes=[be[h][i]])
                for (h, pr, hh, P) in heads:
                    S.op("act", lambda e: e.activation(out=sp_[h][i][:, 0:W], in_=e_[h][i][:, 0:W], func=AF.Ln, bias=onec[:]),
                         reads=[be[h][i], bones], writes=[bsp[h][i]])
                for (h, pr, hh, P) in heads:
                    if ci == 0:
                        init, rd = 0.0, [bsp[h][i], bones]
                    else:
                        init, rd = cs_[h][1 - i][:, 0:1], [bsp[h][i], bones, bcs[h][1 - i]]
                    S.op("dve", lambda e: e.tensor_tensor_scan(out=rev(cs_[h][i]), data0=ones[:, 0:W], data1=rev(sp_[h][i]),
                                                               initial=init, op0=ALU.mult, op1=ALU.add),
                         reads=rd, writes=[bcs[h][i]])
                for (h, pr, hh, P) in heads:
                    S.op("dve", lambda e: e.tensor_tensor(out=lw_[h][i][:, 0:W], in0=p_z[h][:, 0:W], in1=cs_[h][i][:, 0:W], op=ALU.subtract),
                         reads=[bp_z[h], bcs[h][i]], writes=[blw[h][i]])
                for (h, pr, hh, P) in heads:
                    S.op("act", lambda e: e.activation(out=w_[h][i][:, 0:W], in_=lw_[h][i][:, 0:W], func=AF.Exp),
                         reads=[blw[h][i]], writes=[bw[h][i]])
                for (h, pr, hh, P) in heads:
                    pw, bpw = p_w[h % 2], bp_w[h % 2]
                    for b in range(nb):
                        S.op("pe", lambda e: e.transpose(pw[:, b, :], w_[h][i][:, b * 128:(b + 1) * 128], ident[:]),
                             reads=[bw[h][i], bident], writes=[bpw], sig=(b == nb - 1))
                    if h % 2 == 0:
                        S.op("act", lambda e: e.copy(out=wT_[h][i][:, 0:nb, :], in_=pw[:, 0:nb, :]), reads=[bpw], writes=[bwT[h][i]])
                    else:
                        S.op("dve", lambda e: e.tensor_copy(out=wT_[h][i][:, 0:nb, :], in_=pw[:, 0:nb, :]), reads=[bpw], writes=[bwT[h][i]])
                for (h, pr, hh, P) in heads:
                    for b in range(nb):
                        first = (ci == 0 and b == 0)
                        last = (ci == len(chunks) - 1 and b == nb - 1)
                        S.op("pe", lambda e: e.matmul(p_o[pr][P, :], lhsT=v[:, b0 + b, h * 64:(h + 1) * 64], rhs=wT_[h][i][:, b, :],
                                                      start=first, stop=last),
                             reads=[bv, bwT[h][i]], writes=[bp_o[pr]], sig=(b == nb - 1))
            k = (qb // 4) % 2
            for (h, pr, hh, P) in heads:
                S.op("dve", lambda e: e.tensor_copy(out=oT[pr][k][P, (qb % 4) * 128:(qb % 4 + 1) * 128], in_=p_o[pr][P, :]),
                     reads=[bp_o[pr]], writes=[boT[pr][k]])
            if qb % 4 == 3 or qb == nblk - 1:
                q0 = 4 * (qb // 4)
                n = (qb - q0 + 1) * 128
                for pr in range(2):
                    S.dma("sp", MT[pr * 128:(pr + 1) * 128, q0 * 128:q0 * 128 + n], oT[pr][k][:, 0:n], boT[pr][k],
                          reads=[boT[pr][k]], writes=[bMT])
        S.barrier()
        for b in [bq, bk, bv] + boT[0] + boT[1]:
            S.release(b)


def dil_bias_host(rel_bias):
    out = np.empty((12, 128, 2, 128), np.float32)
    kj = np.arange(128)[:, None]
    q = np.arange(128)[None, :]
    for g, r in enumerate((1, 4, 16)):
        for part, dist in ((1, q - kj), (0, q + 128 - kj)):
            valid = (dist >= 0) & (dist <= 128)
            dd = np.maximum(dist, 0) * r
            d = np.maximum(dd, 1).astype(np.float32)
            large = 16 + (np.log(d / np.float32(16)) / np.float32(np.log(2048 / 16)) * np.float32(16)).astype(np.int32)
            large = np.minimum(large, 31)
            bucket = np.where(dd < 16, dd, large)
            for j in range(4):
                hd = 4 * g + j
                out[hd, :, part, :] = np.where(valid, rel_bias[bucket, hd], np.float32(-30000.0))
    return out


def dil_attn_phase(nc, S, QT, KT, V, MT, dbias, bQT, bKT, bV, bMT, ntok):
    with ExitStack() as es:
        sb = lambda name, shape, dt: es.enter_context(nc.sbuf_tensor(_uniq(name), shape, dt))
        ps = lambda name, shape, dt: es.enter_context(nc.psum_tensor(_uniq(name), shape, dt))
        qT = sb("qT", [128, 2, ntok], BF16)
        kT = sb("kT", [128, 2, ntok], BF16)
        v = sb("v", [128, ntok // 128, 256], BF16)
        bias = sb("bias", [128, 4, 256], F32)
        onesb = sb("onesb", [128, 64], BF16)
        Nacc = sb("Nacc", [128, 2, ntok], F32)
        Dacc = sb("Dacc", [128, 2, ntok], F32)
        s_ = [sb("s%d" % i, [128, 256], F32) for i in range(3)]
        pT_ = [sb("pT%d" % i, [128, 2, 128], BF16) for i in range(3)]
        ob = [sb("ob%d" % i, [128, 1024], BF16) for i in range(2)]
        p_s = [ps("p_s%d" % i, [128, 2, 128], F32) for i in range(2)]
        p_n = [ps("p_n%d" % i, [128, 128], F32) for i in range(2)]
        p_d = [ps("p_d%d" % i, [128, 128], F32) for i in range(2)]
        p_pad = [ps("p_pad%d" % i, [128, 256], F32) for i in range(0)]

        bq, bk, bv, bbias = S.dbuf("qT"), S.dbuf("kT"), S.dbuf("v"), S.dbuf("bias")
        bob = [S.dbuf("ob") for _ in range(2)]
        bones, bN, bD = S.buf("ones"), S.buf("N"), S.buf("D")
        bs = [S.buf("s") for _ in range(3)]
        bpT = [S.buf("pT") for _ in range(3)]
        bp_s = [S.pbuf("ps") for _ in range(2)]
        bp_n = [S.pbuf("pn") for _ in range(2)]
        bp_d = [S.pbuf("pd") for _ in range(2)]

        S.op("pool", lambda e: e.memset(onesb[:], 1.0), writes=[bones])
        u = 0
        for g, r in enumerate((1, 4, 16)):
            L = ntok // r
            nb = L // 128
            for pr in range(2):
                r0 = 256 + (2 * g + pr) * 128
                S.dma("sp", qT[:, pr, :], QT[r0:r0 + 128, 0:ntok], bq, reads=[bQT], writes=[bq])
                S.dma("sp", kT[:, pr, :], KT[r0:r0 + 128, 0:ntok], bk, reads=[bKT], writes=[bk])
            vsrc = V[0:ntok, 256 + g * 256:256 + (g + 1) * 256].rearrange("(n i c) f -> c i n f", i=128, c=r)
            for c in range(r):
                S.dma("sp", v[:, c * nb:(c + 1) * nb, :], vsrc[c], bv, reads=[bV], writes=[bv])
            for j in range(4):
                S.dma("sp", bias[:, j, :], dbias[4 * g + j].rearrange("k a q -> k (a q)"), bbias, writes=[bbias])
            for j in range(4):
                pr, hh = j // 2, j % 2
                P = slice(64 * hh, 64 * hh + 64)
                for c in range(r):
                    for n in range(nb):
                        def tok(nn):
                            st = c + r * 128 * nn
                            return slice(st, st + r * 127 + 1, r)
                        i = u % 3
                        pss, bpss = p_s[u % 2], bp_s[u % 2]
                        pn, bpn = p_n[u % 2], bp_n[u % 2]
                        pd, bpd = p_d[u % 2], bp_d[u % 2]
                        u += 1
                        a0 = 0 if n > 0 else 1
                        if n > 0:
                            S.op("pe", lambda e: e.matmul(pss[:, 0, :], lhsT=kT[P, pr, tok(n - 1)], rhs=qT[P, pr, tok(n)],
                                                          start=True, stop=True), reads=[bq, bk], writes=[bpss], sig=False)
                        S.op("pe", lambda e: e.matmul(pss[:, 1, :], lhsT=kT[P, pr, tok(n)], rhs=qT[P, pr, tok(n)],
                                                      start=True, stop=True), reads=[bq, bk], writes=[bpss])
                        S.op("dve", lambda e: e.tensor_tensor(out=s_[i][:, a0 * 128:256], in0=pss[:, a0:2, :],
                                                              in1=bias[:, j, a0 * 128:256], op=ALU.add),
                             reads=[bpss, bbias], writes=[bs[i]])
                        S.op("act", lambda e: e.activation(out=pT_[i][:, a0:2, :], in_=s_[i][:, a0 * 128:256], func=AF.Exp),
                             reads=[bs[i]], writes=[bpT[i]])
                        for a in range(a0, 2):
                            S.op("pe", lambda e: e.matmul(pn[P, :], lhsT=v[:, c * nb + n - 1 + a, j * 64:(j + 1) * 64],
                                                          rhs=pT_[i][:, a, :], start=(a == a0), stop=(a == 1)),
                                 reads=[bv, bpT[i]], writes=[bpn], sig=(a == 1))
                        for a in range(a0, 2):
                            S.op("pe", lambda e: e.matmul(pd[P, :], lhsT=onesb[:, :], rhs=pT_[i][:, a, :],
                                                          start=(a == a0), stop=(a == 1)),
                                 reads=[bones, bpT[i]], writes=[bpd], sig=(a == 1))
                        if g == 0:
                            S.op("act", lambda e: e.copy(out=Nacc[P, pr, tok(n)], in_=pn[P, :]), reads=[bpn], writes=[bN])
                            S.op("dve", lambda e: e.tensor_copy(out=Dacc[P, pr, tok(n)], in_=pd[P, :]), reads=[bpd], writes=[bD])
                        else:
                            S.op("dve", lambda e: e.tensor_tensor(out=Nacc[P, pr, tok(n)], in0=pn[P, :], in1=Nacc[P, pr, tok(n)],
                                                                  op=ALU.add), reads=[bpn, bN], writes=[bN])
                            S.op("dve", lambda e: e.tensor_tensor(out=Dacc[P, pr, tok(n)], in0=pd[P, :], in1=Dacc[P, pr, tok(n)],
                                                                  op=ALU.add), reads=[bpd, bD], writes=[bD])
        k = 0
        for pr in range(2):
            for c0 in range(0, ntok, 1024):
                n = min(1024, ntok - c0)
                S.op("dve", lambda e: e.reciprocal(out=Dacc[:, pr, c0:c0 + n], in_=Dacc[:, pr, c0:c0 + n]), reads=[bD], writes=[bD])
                S.op("dve", lambda e: e.tensor_tensor(out=ob[k % 2][:, 0:n], in0=Nacc[:, pr, c0:c0 + n], in1=Dacc[:, pr, c0:c0 + n],
                                                      op=ALU.mult), reads=[bN, bD], writes=[bob[k % 2]])
                S.dma("sp", MT[256 + pr * 128:256 + (pr + 1) * 128, c0:c0 + n], ob[k % 2][:, 0:n], bob[k % 2],
                      reads=[bob[k % 2]], writes=[bMT])
                k += 1
        S.barrier()
        for b in [bq, bk, bv, bbias] + bob:
            S.release(b)


def out_proj_phase(nc, S, x_in, x_out, xin_b, xout_b, MT, bMT, w_out, kdim, ntok):
    G = 512
    ng = ntok // G
    kc = kdim // 128
    with ExitStack() as es:
        sb = lambda name, shape, dt: es.enter_context(nc.sbuf_tensor(_uniq(name), shape, dt))
        ps = lambda name, shape, dt: es.enter_context(nc.psum_tensor(_uniq(name), shape, dt))
        Wo = sb("Wo", [128, kc, D], BF16)
        mT = [sb("mT%d" % i, [128, kc, G], BF16) for i in range(2)]
        xt = [sb("xt%d" % i, [128, D], F32) for i in range(3)]
        ot = [sb("ot%d" % i, [128, D], F32) for i in range(2)]
        p_y = [ps("p_y%d" % i, [128, 512], F32) for i in range(4)]
        bWo = S.dbuf("Wo")
        bmT = [S.dbuf("mT") for _ in range(2)]
        bxt = [S.dbuf("xt") for _ in range(3)]
        bot = [S.dbuf("ot") for _ in range(2)]
        bp_y = [S.pbuf("py") for _ in range(4)]
        w_v = w_out.rearrange("(c p) f -> p c f", p=128)
        for c in range(kc):
            S.dma("pool", Wo[:, c, :], w_v[:, c, :], bWo, writes=[bWo])
        k = 0
        for g in range(ng):
            m, bm = mT[g % 2], bmT[g % 2]
            for c in range(kc):
                S.dma("sp", m[:, c, :], MT[c * 128:(c + 1) * 128, g * G:(g + 1) * G], bm, reads=[bMT], writes=[bm])
            for s in range(4):
                t0 = g * G + s * 128
                x_, bx = xt[k % 3], bxt[k % 3]
                o_, bo = ot[k % 2], bot[k % 2]
                S.dma("sp", x_[:], x_in[t0:t0 + 128, :], bx, reads=[xin_b], writes=[bx])
                for hh in range(2):
                    py, bpy = p_y[(2 * k + hh) % 4], bp_y[(2 * k + hh) % 4]
                    for c in range(kc):
                        S.op("pe", lambda e: e.matmul(py[:], lhsT=m[:, c, s * 128:(s + 1) * 128], rhs=Wo[:, c, hh * 512:(hh + 1) * 512],
                                                      start=(c == 0), stop=(c == kc - 1)),
                             reads=[bm, bWo], writes=[bpy], sig=(c == kc - 1))
                    S.op("dve", lambda e: e.tensor_tensor(out=o_[:, hh * 512:(hh + 1) * 512], in0=py[:], in1=x_[:, hh * 512:(hh + 1) * 512],
                                                          op=ALU.add), reads=[bpy, bx], writes=[bo])
                S.dma("sp", x_out[t0:t0 + 128, :], o_[:], bo, reads=[bo], writes=[xout_b])
                k += 1
        S.barrier()
        for b in [bWo] + bmT + bxt + bot:
            S.release(b)


C0 = float(np.exp(-0.5))


class RwPrep:
    def __init__(self, nc, S, es, G, mix_ids, x_in, xin_b, gain_row, mix, eps=1e-6):
        sb = lambda name, shape, dt: es.enter_context(nc.sbuf_tensor(_uniq(name), shape, dt))
        ps = lambda name, shape, dt: es.enter_context(nc.psum_tensor(_uniq(name), shape, dt))
        self.nc, self.S, self.G, self.mix_ids, self.x_in, self.xin_b, self.eps = nc, S, G, mix_ids, x_in, xin_b, eps
        self.gB = sb("gB", [128, D], F32)
        self.identf = sb("identf", [128, 128], F32)
        self.nh = sb("nh", [128, 1], F32)
        self.mixc = sb("mixc", [128, 6, 8], F32)
        self.xt = [sb("xt%d" % i, [128, D], F32) for i in range(2)]
        self.hn = [sb("hn%d" % i, [128, D], F32) for i in range(2)]
        self.junk = sb("junk", [128, D], BF16)
        self.ss = [sb("ss%d" % i, [128, 1], F32) for i in range(2)]
        self.rs = [sb("rs%d" % i, [128, 1], F32) for i in range(2)]
        self.hT = [sb("hT%d" % i, [128, 8, G + 1], F32) for i in range(2)]
        self.xx = [sb("xx%d" % i, [128, G], F32) for i in range(2)]
        self.xm = {i: sb("xm%d" % i, [128, 8, G], BF16) for i in mix_ids}
        self.p_tr = [ps("p_tr%d" % i, [128, 4, 128], F32) for i in range(2)]
        self.bgB, self.bmixc = S.dbuf("gB"), S.dbuf("mixc")
        self.bxt = [S.dbuf("xt") for _ in range(2)]
        self.bhn = [S.buf("hn") for _ in range(2)]
        self.bident, self.bnh, self.bjunk = S.buf("identf"), S.buf("nh"), S.buf("junk")
        self.bss = [S.buf("ss") for _ in range(2)]
        self.brs = [S.buf("rs") for _ in range(2)]
        self.bhT = [S.buf("hT") for _ in range(2)]
        self.bxx = [S.buf("xx") for _ in range(2)]
        self.bxm = {i: S.buf("xm") for i in mix_ids}
        self.bp_tr = [S.pbuf("ptr") for _ in range(2)]
        S.op("pool", lambda e: e.memset(self.nh[:], -0.5), writes=[self.bnh])
        make_ident(nc, S, self.identf, self.bident)
        S.dma("sp", self.gB[:], gain_row.to_broadcast([128, D]), self.bgB, writes=[self.bgB])
        for i in range(6):
            S.dma("sp", self.mixc[:, i, :], mix[i:i + 1, :].rearrange("o (c p) -> p (o c)", p=128), self.bmixc, writes=[self.bmixc], slow=True)
        S.op("dve", lambda e: e.memset(self.hT[1][:, :, G:G + 1], 0.0), writes=[self.bhT[1]])
        self.dsems = [self.bgB, self.bmixc] + self.bxt

    def group(self, g):
        nc, S, G = self.nc, self.S, self.G
        hT, bhT = self.hT[g % 2], self.bhT[g % 2]
        hTp, bhTp = self.hT[(g + 1) % 2], self.bhT[(g + 1) % 2]
        S.op("pool", lambda e: e.tensor_copy(out=hT[:, :, 0:1], in_=hTp[:, :, G:G + 1]), reads=[bhTp], writes=[bhT])
        k = 0
        for s in range(G // 128):
            t0 = g * G + s * 128
            x_, bx = self.xt[s % 2], self.bxt[s % 2]
            h_, bh = self.hn[s % 2], self.bhn[s % 2]
            ss, bss, rs, brs = self.ss[s % 2], self.bss[s % 2], self.rs[s % 2], self.brs[s % 2]
            S.dma("sp", x_[:], self.x_in[t0:t0 + 128, :], bx, reads=[self.xin_b], writes=[bx])
            S.op("dve", lambda e: e.scalar_tensor_tensor(out=self.junk[:], in0=x_[:], scalar=1.0, in1=x_[:], op0=ALU.mult, op1=ALU.mult,
                                                         accum_out=ss[:]), reads=[bx], writes=[self.bjunk, bss])
            S.op("dve", lambda e: e.tensor_scalar(out=ss[:], in0=ss[:], scalar1=1.0 / D, scalar2=self.eps, op0=ALU.mult, op1=ALU.add),
                 reads=[bss], writes=[bss])
            S.op("pool", lambda e: e.tensor_tensor(out=rs[:], in0=ss[:], in1=self.nh[:], op=ALU.pow), reads=[bss, self.bnh], writes=[brs])
            S.op("dve", lambda e: e.scalar_tensor_tensor(out=h_[:], in0=x_[:], scalar=rs[:], in1=self.gB[:], op0=ALU.mult, op1=ALU.mult),
                 reads=[bx, brs, self.bgB], writes=[bh])
            for half in range(2):
                pt, bpt = self.p_tr[k % 2], self.bp_tr[k % 2]
                k += 1
                for c4 in range(4):
                    c = half * 4 + c4
                    S.op("pe", lambda e: e.transpose(pt[:, c4, :], h_[:, c * 128:(c + 1) * 128], self.identf[:]),
                         reads=[bh, self.bident], writes=[bpt], sig=(c4 == 3))
                S.op("act", lambda e: e.copy(out=hT[:, half * 4:half * 4 + 4, 1 + s * 128:1 + (s + 1) * 128], in_=pt[:]),
                     reads=[bpt], writes=[bhT])
        for c in range(8):
            xx, bxx = self.xx[c % 2], self.bxx[c % 2]
            S.op("dve", lambda e: e.tensor_tensor(out=xx[:], in0=hT[:, c, 0:G], in1=hT[:, c, 1:G + 1], op=ALU.subtract),
                 reads=[bhT], writes=[bxx])
            for n, i in enumerate(self.mix_ids):
                eng = "dve" if n % 2 == 0 else "pool"
                if eng == "dve":
                    S.op("dve", lambda e: e.scalar_tensor_tensor(out=self.xm[i][:, c, :], in0=xx[:], scalar=self.mixc[:, i, c:c + 1],
                                                                 in1=hT[:, c, 1:G + 1], op0=ALU.mult, op1=ALU.add),
                         reads=[bxx, bhT, self.bmixc], writes=[self.bxm[i]])
                else:
                    S.op("pool", lambda e: e.tensor_scalar(out=self.junk[:, 0:G], in0=xx[:], scalar1=self.mixc[:, i, c:c + 1], scalar2=None,
                                                           op0=ALU.mult), reads=[bxx, self.bmixc], writes=[self.bjunk])
                    S.op("pool", lambda e: e.tensor_tensor(out=self.xm[i][:, c, :], in0=self.junk[:, 0:G], in1=hT[:, c, 1:G + 1], op=ALU.add),
                         reads=[self.bjunk, bhT], writes=[self.bxm[i]])

    def release(self):
        for b in self.dsems:
            self.S.release(b)


def col_load(S, dst, src_row, track):
    S.dma("sp", dst, src_row.rearrange("o (c p) -> p (o c)", p=128), track, writes=[track], slow=True)


def rwkv_fm_phase(nc, S, x_in, xin_b, gain_row, mix, w0, w1, w2, a0, a1, a2, kk_, ka_, w_r, w_k,
                  RtT, KtT, AtT, BtT, WC, bouts, ntok):
    G = 512
    ng = ntok // G
    with ExitStack() as es:
        sb = lambda name, shape, dt: es.enter_context(nc.sbuf_tensor(_uniq(name), shape, dt))
        ps = lambda name, shape, dt: es.enter_context(nc.psum_tensor(_uniq(name), shape, dt))
        P = RwPrep(nc, S, es, G, (0, 1, 2, 4), x_in, xin_b, gain_row, mix)
        Wr = sb("Wr", [128, 8, D], BF16)
        Wk = sb("Wk", [128, 8, D], BF16)
        W1 = sb("W1", [128, 8, 64], BF16)
        A1 = sb("A1", [128, 8, 64], BF16)
        W2 = sb("W2", [64, D], BF16)
        A2 = sb("A2", [64, D], BF16)
        cols = sb("cols", [128, 4, 8], F32)
        bones = sb("bones", [128, 128], BF16)
        rmask = sb("rmask", [128, G], F32)
        tiny = sb("tiny", [128, 1], F32)
        tw = sb("tw", [64, G], BF16)
        ta = sb("ta", [64, G], BF16)
        names = ["sgu", "av", "kk0", "lnk", "rk", "kkn", "t1", "kp", "csg", "cse", "eW", "eWi", "eWe", "t2"]
        T = {n: sb(n, [128, G], F32) for n in names}
        sqk = sb("sqk", [128, G], BF16)
        wc = [sb("wc%d" % i, [128, G // 64], F32) for i in range(2)]
        ob = [sb("ob%d" % i, [128, G], BF16) for i in range(8)]
        p_r = ps("p_r", [128, G], F32)
        p_k = ps("p_k", [128, G], F32)
        p_u = ps("p_u", [128, G], F32)
        p_a = ps("p_a", [128, G], F32)
        p_ss = ps("p_ss", [128, G], F32)
        p_t = ps("p_t", [128, G], F32)
        bW = S.dbuf("W")
        bcols = S.dbuf("cols")
        bwc = [S.dbuf("wc") for _ in range(2)]
        bob = [S.dbuf("ob") for _ in range(8)]
        B = {n: S.buf(n) for n in names + ["sqk", "tw", "ta", "bones", "rmask"]}
        B.update({n: S.pbuf(n) for n in ["p_r", "p_k", "p_u", "p_a", "p_ss", "p_t"]})
        for c in range(8):
            S.dma("pool", Wr[:, c, :], w_r.rearrange("(c p) f -> p c f", p=128)[:, c, :], bW, writes=[bW])
            S.dma("pool", Wk[:, c, :], w_k.rearrange("(c p) f -> p c f", p=128)[:, c, :], bW, writes=[bW])
        S.dma("pool", W1[:], w1.rearrange("(c p) f -> p c f", p=128), bW, writes=[bW])
        S.dma("pool", A1[:], a1.rearrange("(c p) f -> p c f", p=128), bW, writes=[bW])
        S.dma("pool", W2[:], w2, bW, writes=[bW])
        S.dma("pool", A2[:], a2, bW, writes=[bW])
        for n, src in enumerate((w0, a0, kk_, ka_)):
            col_load(S, cols[:, n, :], src, bcols)
        S.op("pool", lambda e: e.memset(bones[:], 0.0), writes=[B["bones"]])
        S.op("pool", lambda e: e.memset(bones[0:64, 0:64], 1.0), writes=[B["bones"]])
        S.op("pool", lambda e: e.memset(bones[64:128, 64:128], 1.0), writes=[B["bones"]])
        S.op("pool", lambda e: e.memset(rmask[:], 1.0), writes=[B["rmask"]])
        S.op("pool", lambda e: e.memset(rmask[:, 0:G:64], 0.0), writes=[B["rmask"]])
        S.op("pool", lambda e: e.memset(tiny[:], 1e-18), writes=[B["rmask"]])
        RB, KB, AB, BB, WCB = bouts
        no = 0
        for g in range(ng):
            P.group(g)
            t0 = g * G
            xr, xw, xk, xa = P.xm[0], P.xm[1], P.xm[2], P.xm[4]
            bxr, bxw, bxk, bxa = P.bxm[0], P.bxm[1], P.bxm[2], P.bxm[4]
            for c in range(8):
                S.op("pe", lambda e: e.matmul(p_t[0:64, :], lhsT=W1[:, c, :], rhs=xw[:, c, :], start=(c == 0), stop=(c == 7)),
                     reads=[bW, bxw], writes=[B["p_t"]], sig=(c == 7))
            S.op("act", lambda e: e.activation(out=tw[:], in_=p_t[0:64, :], func=AF.Tanh), reads=[B["p_t"]], writes=[B["tw"]])
            for c in range(8):
                S.op("pe", lambda e: e.matmul(p_t[0:64, :], lhsT=A1[:, c, :], rhs=xa[:, c, :], start=(c == 0), stop=(c == 7)),
                     reads=[bW, bxa], writes=[B["p_t"]], sig=(c == 7))
            S.op("act", lambda e: e.copy(out=ta[:], in_=p_t[0:64, :]), reads=[B["p_t"]], writes=[B["ta"]])
            for cc in range(8):
                fs = slice(cc * 128, (cc + 1) * 128)
                for c in range(8):
                    S.op("pe", lambda e: e.matmul(p_r[:], lhsT=Wr[:, c, fs], rhs=xr[:, c, :], start=(c == 0), stop=(c == 7)),
                         reads=[bW, bxr], writes=[B["p_r"]], sig=(c == 7))
                for c in range(8):
                    S.op("pe", lambda e: e.matmul(p_k[:], lhsT=Wk[:, c, fs], rhs=xk[:, c, :], start=(c == 0), stop=(c == 7)),
                         reads=[bW, bxk], writes=[B["p_k"]], sig=(c == 7))
                S.op("pe", lambda e: e.matmul(p_u[:], lhsT=W2[:, fs], rhs=tw[:], start=True, stop=True), reads=[bW, B["tw"]], writes=[B["p_u"]])
                S.op("pe", lambda e: e.matmul(p_a[:], lhsT=A2[:, fs], rhs=ta[:], start=True, stop=True), reads=[bW, B["ta"]], writes=[B["p_a"]])
                S.op("act", lambda e: e.activation(out=T["sgu"][:], in_=p_u[:], func=AF.Sigmoid, bias=cols[:, 0, cc:cc + 1]),
                     reads=[B["p_u"], bcols], writes=[B["sgu"]])
                S.op("act", lambda e: e.activation(out=T["av"][:], in_=p_a[:], func=AF.Sigmoid, bias=cols[:, 1, cc:cc + 1]),
                     reads=[B["p_a"], bcols], writes=[B["av"]])
                S.op("dve", lambda e: e.tensor_scalar(out=T["kk0"][:], in0=p_k[:], scalar1=cols[:, 2, cc:cc + 1], scalar2=None, op0=ALU.mult),
                     reads=[B["p_k"], bcols], writes=[B["kk0"]])
                S.op("act", lambda e: e.activation(out=sqk[:], in_=T["kk0"][:], func=AF.Square), reads=[B["kk0"]], writes=[B["sqk"]])
                S.op("pe", lambda e: e.matmul(p_ss[:], lhsT=bones[:], rhs=sqk[:], start=True, stop=True), reads=[B["bones"], B["sqk"]],
                     writes=[B["p_ss"]])
                S.op("act", lambda e: e.activation(out=T["lnk"][:], in_=p_ss[:], func=AF.Ln, bias=tiny[:]), reads=[B["p_ss"], B["rmask"]],
                     writes=[B["lnk"]])
                S.op("act", lambda e: e.activation(out=T["rk"][:], in_=T["lnk"][:], func=AF.Exp, scale=-0.5), reads=[B["lnk"]], writes=[B["rk"]])
                S.op("pool", lambda e: e.tensor_tensor(out=T["kkn"][:], in0=T["kk0"][:], in1=T["rk"][:], op=ALU.mult),
                     reads=[B["kk0"], B["rk"]], writes=[B["kkn"]])
                S.op("dve", lambda e: e.tensor_scalar(out=T["t1"][:], in0=T["av"][:], scalar1=-1.0, scalar2=cols[:, 3, cc:cc + 1],
                                                      op0=ALU.add, op1=ALU.mult), reads=[B["av"], bcols], writes=[B["t1"]])
                S.op("dve", lambda e: e.scalar_tensor_tensor(out=T["kp"][:], in0=T["t1"][:], scalar=1.0, in1=p_k[:], op0=ALU.add, op1=ALU.mult),
                     reads=[B["t1"], B["p_k"]], writes=[B["kp"]])
                S.op("dve", lambda e: e.tensor_tensor_scan(out=T["csg"][:], data0=rmask[:], data1=T["sgu"][:], initial=0.0,
                                                           op0=ALU.mult, op1=ALU.add), reads=[B["rmask"], B["sgu"]], writes=[B["csg"]])
                S.op("pool", lambda e: e.tensor_tensor(out=T["cse"][:], in0=T["csg"][:], in1=T["sgu"][:], op=ALU.subtract),
                     reads=[B["csg"], B["sgu"]], writes=[B["cse"]])
                S.op("act", lambda e: e.activation(out=T["eW"][:], in_=T["csg"][:], func=AF.Exp, scale=-C0), reads=[B["csg"]], writes=[B["eW"]])
                S.op("act", lambda e: e.activation(out=T["eWi"][:], in_=T["csg"][:], func=AF.Exp, scale=C0), reads=[B["csg"]], writes=[B["eWi"]])
                S.op("act", lambda e: e.activation(out=T["eWe"][:], in_=T["cse"][:], func=AF.Exp, scale=-C0), reads=[B["cse"]], writes=[B["eWe"]])
                S.op("pool", lambda e: e.tensor_tensor(out=T["t2"][:], in0=T["kkn"][:], in1=T["av"][:], op=ALU.mult),
                     reads=[B["kkn"], B["av"]], writes=[B["t2"]])
                outs = []
                o, bo = ob[no % 8], bob[no % 8]; no += 1
                S.op("dve", lambda e: e.tensor_tensor(out=o[:], in0=p_r[:], in1=T["eW"][:], op=ALU.mult), reads=[B["p_r"], B["eW"]], writes=[bo])
                outs.append((o, bo, RtT, RB))
                o, bo = ob[no % 8], bob[no % 8]; no += 1
                S.op("dve", lambda e: e.tensor_tensor(out=o[:], in0=T["kp"][:], in1=T["eWi"][:], op=ALU.mult), reads=[B["kp"], B["eWi"]], writes=[bo])
                outs.append((o, bo, KtT, KB))
                o, bo = ob[no % 8], bob[no % 8]; no += 1
                S.op("dve", lambda e: e.scalar_tensor_tensor(out=o[:], in0=T["kkn"][:], scalar=-1.0, in1=T["eWe"][:], op0=ALU.mult, op1=ALU.mult),
                     reads=[B["kkn"], B["eWe"]], writes=[bo])
                outs.append((o, bo, AtT, AB))
                o, bo = ob[no % 8], bob[no % 8]; no += 1
                S.op("pool", lambda e: e.tensor_tensor(out=o[:], in0=T["t2"][:], in1=T["eWi"][:], op=ALU.mult), reads=[B["t2"], B["eWi"]], writes=[bo])
                outs.append((o, bo, BtT, BB))
                for (o, bo, dst, bdst) in outs:
                    S.dma("sp", dst[fs, t0:t0 + G], o[:], bo, reads=[bo], writes=[bdst])
                w_, bw_ = wc[cc % 2], bwc[cc % 2]
                S.op("dve", lambda e: e.tensor_copy(out=w_[:], in_=T["eW"][:, 63:G:64]), reads=[B["eW"]], writes=[bw_])
                S.dma("sp", WC[fs, g * (G // 64):(g + 1) * (G // 64)], w_[:], bw_, reads=[bw_], writes=[WCB])
        S.barrier()
        P.release()
        for b in [bW, bcols] + bwc + bob:
            S.release(b)


def rwkv_tm_phase(nc, S, x_in, xin_b, gain_row, mix, a0, a1, a2, g1, g2, ka_, rk_, w_r, w_k, w_v,
                  Vtok, BV, Gt, bouts, ntok):
    G = 256
    ng = ntok // G
    with ExitStack() as es:
        sb = lambda name, shape, dt: es.enter_context(nc.sbuf_tensor(_uniq(name), shape, dt))
        ps = lambda name, shape, dt: es.enter_context(nc.psum_tensor(_uniq(name), shape, dt))
        P = RwPrep(nc, S, es, G, (0, 2, 3, 4, 5), x_in, xin_b, gain_row, mix)
        Wr = sb("Wr", [128, 8, D], BF16)
        Wk = sb("Wk", [128, 8, D], BF16)
        Wv = sb("Wv", [128, 8, D], BF16)
        A1 = sb("A1", [128, 8, 64], BF16)
        A2 = sb("A2", [64, D], BF16)
        G1 = sb("G1", [128, 8, 160], BF16)
        G2a = sb("G2a", [128, D], BF16)
        G2b = sb("G2b", [32, D], BF16)
        a0B = sb("a0B", [128, D], F32)
        kaB = sb("kaB", [128, D], F32)
        rkB = sb("rkB", [128, D], F32)
        ta = sb("ta", [64, G], BF16)
        sg1a = sb("sg1a", [128, G], BF16)
        sg1b = sb("sg1b", [32, G], BF16)
        tmp = sb("tmp", [128, D], F32)
        av = sb("av", [128, D], F32)
        t1 = sb("t1", [128, D], F32)
        kp = sb("kp", [128, D], F32)
        tmp2 = sb("tmp2", [128, D], F32)
        tmp3 = sb("tmp3", [128, 16, 64], F32)
        bsum = sb("bsum", [128, 16, 1], F32)
        bvo = [sb("bvo%d" % i, [128, 16, 64], F32) for i in range(2)]
        vto = [sb("vto%d" % i, [128, D], BF16) for i in range(2)]
        gto = [sb("gto%d" % i, [128, D], F32) for i in range(2)]
        p_t = ps("p_t", [128, G], F32)
        pp = [ps("pp%d" % i, [128, 2, 512], F32) for i in range(2)]
        bW, bB = S.dbuf("W"), S.dbuf("B")
        bbvo = [S.dbuf("bvo") for _ in range(2)]
        bvto = [S.dbuf("vto") for _ in range(2)]
        bgto = [S.dbuf("gto") for _ in range(2)]
        B = {n: S.buf(n) for n in ["ta", "sg1a", "sg1b", "tmp", "av", "t1", "kp", "tmp2", "tmp3", "bsum"]}
        B.update({n: S.pbuf(n) for n in ["p_t", "pp0", "pp1"]})
        bpp = [B["pp0"], B["pp1"]]
        for c in range(8):
            for (W_, w_) in ((Wr, w_r), (Wk, w_k), (Wv, w_v)):
                S.dma("pool", W_[:, c, :], w_.rearrange("(c p) f -> p c f", p=128)[:, c, :], bW, writes=[bW])
        S.dma("pool", A1[:], a1.rearrange("(c p) f -> p c f", p=128), bW, writes=[bW])
        S.dma("pool", G1[:], g1.rearrange("(c p) f -> p c f", p=128), bW, writes=[bW])
        S.dma("pool", A2[:], a2, bW, writes=[bW])
        S.dma("pool", G2a[:], g2[0:128, :], bW, writes=[bW])
        S.dma("pool", G2b[:], g2[128:160, :], bW, writes=[bW])
        for (t_, src) in ((a0B, a0), (kaB, ka_), (rkB, rk_)):
            S.dma("sp", t_[:], src.to_broadcast([128, D]), bB, writes=[bB])
        VB, BVB, GB = bouts
        npp = 0
        k = 0
        for g in range(ng):
            P.group(g)
            xr, xk, xv, xa, xg = P.xm[0], P.xm[2], P.xm[3], P.xm[4], P.xm[5]
            bxr, bxk, bxv, bxa, bxg = P.bxm[0], P.bxm[2], P.bxm[3], P.bxm[4], P.bxm[5]
            for c in range(8):
                S.op("pe", lambda e: e.matmul(p_t[0:64, :], lhsT=A1[:, c, :], rhs=xa[:, c, :], start=(c == 0), stop=(c == 7)),
                     reads=[bW, bxa], writes=[B["p_t"]], sig=(c == 7))
            S.op("act", lambda e: e.copy(out=ta[:], in_=p_t[0:64, :]), reads=[B["p_t"]], writes=[B["ta"]])
            for c in range(8):
                S.op("pe", lambda e: e.matmul(p_t[:, :], lhsT=G1[:, c, 0:128], rhs=xg[:, c, :], start=(c == 0), stop=(c == 7)),
                     reads=[bW, bxg], writes=[B["p_t"]], sig=(c == 7))
            S.op("act", lambda e: e.activation(out=sg1a[:], in_=p_t[:, :], func=AF.Sigmoid), reads=[B["p_t"]], writes=[B["sg1a"]])
            for c in range(8):
                S.op("pe", lambda e: e.matmul(p_t[0:32, :], lhsT=G1[:, c, 128:160], rhs=xg[:, c, :], start=(c == 0), stop=(c == 7)),
                     reads=[bW, bxg], writes=[B["p_t"]], sig=(c == 7))
            S.op("act", lambda e: e.activation(out=sg1b[:], in_=p_t[0:32, :], func=AF.Sigmoid), reads=[B["p_t"]], writes=[B["sg1b"]])
            for s in range(G // 128):
                ts = slice(s * 128, (s + 1) * 128)
                t0 = g * G + s * 128

                def big(xm_, bxm_, W_):
                    nonlocal npp
                    p, bp = pp[npp % 2], bpp[npp % 2]
                    npp += 1
                    for hh in range(2):
                        for c in range(8):
                            S.op("pe", lambda e: e.matmul(p[:, hh, :], lhsT=xm_[:, c, ts], rhs=W_[:, c, hh * 512:(hh + 1) * 512],
                                                          start=(c == 0), stop=(c == 7)), reads=[bW, bxm_], writes=[bp], sig=(c == 7))
                    return p, bp
                p, bp = pp[npp % 2], bpp[npp % 2]
                npp += 1
                for hh in range(2):
                    S.op("pe", lambda e: e.matmul(p[:, hh, :], lhsT=ta[:, ts], rhs=A2[:, hh * 512:(hh + 1) * 512], start=True, stop=True),
                         reads=[bW, B["ta"]], writes=[bp])
                S.op("dve", lambda e: e.tensor_tensor(out=tmp[:], in0=p[:].rearrange("p a b -> p (a b)"), in1=a0B[:], op=ALU.add),
                     reads=[bp, bB], writes=[B["tmp"]])
                S.op("act", lambda e: e.activation(out=av[:], in_=tmp[:], func=AF.Sigmoid), reads=[B["tmp"]], writes=[B["av"]])
                S.op("dve", lambda e: e.scalar_tensor_tensor(out=t1[:], in0=av[:], scalar=-1.0, in1=kaB[:], op0=ALU.add, op1=ALU.mult),
                     reads=[B["av"], bB], writes=[B["t1"]])
                p, bp = big(xk, bxk, Wk)
                S.op("dve", lambda e: e.scalar_tensor_tensor(out=kp[:], in0=t1[:], scalar=1.0, in1=p[:].rearrange("p a b -> p (a b)"),
                                                             op0=ALU.add, op1=ALU.mult), reads=[B["t1"], bp], writes=[B["kp"]])
                p, bp = big(xr, bxr, Wr)
                S.op("dve", lambda e: e.tensor_tensor(out=tmp2[:], in0=p[:].rearrange("p a b -> p (a b)"), in1=rkB[:], op=ALU.mult),
                     reads=[bp, bB], writes=[B["tmp2"]])
                S.op("pool", lambda e: e.tensor_tensor(out=tmp3[:].rearrange("p a b -> p (a b)"), in0=tmp2[:], in1=kp[:], op=ALU.mult),
                     reads=[B["tmp2"], B["kp"]], writes=[B["tmp3"]])
                S.op("dve", lambda e: e.tensor_reduce(out=bsum[:], in_=tmp3[:], axis=AX.X, op=ALU.add), reads=[B["tmp3"]], writes=[B["bsum"]])
                p, bp = big(xv, bxv, Wv)
                o, bo = bvo[k % 2], bbvo[k % 2]
                S.op("dve", lambda e: e.tensor_tensor(out=o[:], in0=p[:].rearrange("p a (h d) -> p (a h) d", d=64),
                                                      in1=bsum[:].to_broadcast([128, 16, 64]), op=ALU.mult), reads=[bp, B["bsum"]], writes=[bo])
                S.dma("sp", BV[t0:t0 + 128, :], o[:].rearrange("p a b -> p (a b)"), bo, reads=[bo], writes=[BVB])
                o, bo = vto[k % 2], bvto[k % 2]
                S.op("act", lambda e: e.copy(out=o[:], in_=p[:].rearrange("p a b -> p (a b)")), reads=[bp], writes=[bo])
                S.dma("sp", Vtok[t0:t0 + 128, :], o[:], bo, reads=[bo], writes=[VB])
                p, bp = pp[npp % 2], bpp[npp % 2]
                npp += 1
                for hh in range(2):
                    S.op("pe", lambda e: e.matmul(p[:, hh, :], lhsT=sg1a[:, ts], rhs=G2a[:, hh * 512:(hh + 1) * 512], start=True, stop=False),
                         reads=[bW, B["sg1a"]], writes=[bp], sig=False)
                    S.op("pe", lambda e: e.matmul(p[:, hh, :], lhsT=sg1b[:, ts], rhs=G2b[:, hh * 512:(hh + 1) * 512], start=False, stop=True),
                         reads=[bW, B["sg1b"]], writes=[bp])
                o, bo = gto[k % 2], bgto[k % 2]
                S.op("act", lambda e: e.copy(out=o[:], in_=p[:].rearrange("p a b -> p (a b)")), reads=[bp], writes=[bo])
                S.dma("sp", Gt[t0:t0 + 128, :], o[:], bo, reads=[bo], writes=[GB])
                k += 1
        S.barrier()
        P.release()
        for b in [bW, bB] + bbvo + bvto + bgto:
            S.release(b)


def rwkv_scan_phase(nc, S, RtT, KtT, AtT, BtT, WC, Vtok, Ysc, bins, bY, ntok, NI=4):
    nch = ntok // 64
    ngr = nch // 8
    with ExitStack() as es:
        sb = lambda name, shape, dt: es.enter_context(nc.sbuf_tensor(_uniq(name), shape, dt))
        ps = lambda name, shape, dt: es.enter_context(nc.psum_tensor(_uniq(name), shape, dt))
        MU = sb("MU", [128, 128], F32)
        MUI = sb("MUI", [128, 128], F32)
        ML = sb("ML", [128, 128], F32)
        I32 = sb("I32", [128, 128], F32)
        identb = sb("identb", [128, 128], BF16)
        bconst = S.buf("const")
        for (m, chm, pat, op) in ((MU, -1, 1, ALU.is_gt), (MUI, -1, 1, ALU.is_ge), (ML, 1, -1, ALU.is_gt)):
            S.op("pool", lambda e: e.memset(m[:], 1.0), writes=[bconst])
            S.op("pool", lambda e: e.affine_select(out=m[:], in_=m[:], pattern=[[pat, 128]], compare_op=op, fill=0.0, base=0,
                                                   channel_multiplier=chm), reads=[bconst], writes=[bconst])
        make_ident(nc, S, I32, bconst)
        make_ident(nc, S, identb, bconst)

        class Slot:
            pass
        slots = []
        for si in range(NI):
            s = Slot()
            s.i = si
            n_ = lambda x: "%s_%d" % (x, si)
            s.bd = {k: [sb(n_(k) + "_%d" % j, [128, 8, 128], BF16) for j in range(2)] for k in ("A", "B", "K", "R")}
            s.bbd = [S.dbuf(n_("bd0")), S.dbuf(n_("bd1"))]
            s.Vs = [sb(n_("Vs%d" % j), [128, 8, 64], BF16) for j in range(2)]
            s.Yo = [sb(n_("Yo%d" % j), [128, 8, 64], F32) for j in range(2)]
            s.bYo = [S.dbuf(n_("Yo0")), S.dbuf(n_("Yo1"))]
            s.wcs = sb(n_("wcs"), [128, nch], F32)
            s.bwcs = S.dbuf(n_("wcs"))
            s.N = [sb(n_("N%d" % j), [128, 128], F32) for j in range(2)]
            s.P = [sb(n_("P%d" % j), [128, 128], F32) for j in range(2)]
            s.X = [sb(n_("X%d" % j), [128, 128], F32) for j in range(2)]
            s.bN = [S.buf("N") for _ in range(2)]
            s.bP = [S.buf("P") for _ in range(2)]
            s.bX = [S.buf("X") for _ in range(2)]
            for k in ("Mak", "Mrb", "Mrk", "BtT", "KtT"):
                setattr(s, k, sb(n_(k), [128, 128], BF16))
                setattr(s, "b" + k, S.buf(k))
            s.Xs = sb(n_("Xs"), [128, 64], F32)
            s.Ub = sb(n_("Ub"), [128, 64], BF16)
            s.Sw = sb(n_("Sw"), [128, 64], F32)
            s.St = sb(n_("St"), [128, 64], F32)
            s.Sb = sb(n_("Sb"), [128, 64], BF16)
            s.bXs, s.bUb, s.bSw, s.bSt, s.bSb = [S.buf(k) for k in ("Xs", "Ub", "Sw", "St", "Sb")]
            s.psA = ps(n_("psA"), [128, 4, 128], F32)
            s.psB = ps(n_("psB"), [128, 4, 128], F32)
            s.bA, s.bB = S.pbuf("bankA"), S.pbuf("bankB")
            s.ptr = s.psB[:, 3, :].bitcast(BF16)
            s.bptr = s.bB
            for k in ("A", "B", "K", "R"):
                for j in range(2):
                    S.op("pool", lambda e: e.memset(s.bd[k][j][:], 0.0), writes=[s.bbd[j]])
            slots.append(s)

        def PA(s, k):
            return s.psA[:, k, :], s.bA

        def PB(s, k):
            return s.psB[:, k, :], s.bB

        srcs = {"A": AtT, "B": BtT, "K": KtT, "R": RtT}
        bsrc = {"A": bins[2], "B": bins[3], "K": bins[1], "R": bins[0]}
        bWC, bV = bins[4], bins[5]

        def load_group(s, hp, gg):
            j = gg % 2
            t0 = gg * 512
            for k in ("A", "B", "K", "R"):
                for h in range(2):
                    r0 = hp * 128 + h * 64
                    S.dma("sp", s.bd[k][j][h * 64:(h + 1) * 64, :, h * 64:(h + 1) * 64],
                          srcs[k][r0:r0 + 64, t0:t0 + 512].rearrange("p (c j) -> p c j", j=64), s.bbd[j],
                          reads=[bsrc[k]], writes=[s.bbd[j]])
            for h in range(2):
                c0 = hp * 128 + h * 64
                S.dma("sp", s.Vs[j][h * 64:(h + 1) * 64, :, :], Vtok[t0:t0 + 512, c0:c0 + 64].rearrange("(c j) v -> j c v", j=64),
                      s.bbd[j], reads=[bV], writes=[s.bbd[j]])

        for rnd in range(8 // NI):
            hps = [rnd * NI + i for i in range(NI)]
            for s, hp in zip(slots, hps):
                S.dma("sp", s.wcs[:], WC[hp * 128:(hp + 1) * 128, 0:nch], s.bwcs, reads=[bWC], writes=[s.bwcs])
                S.op("pool", lambda e: e.memset(s.St[:], 0.0), writes=[s.bSt])
                S.op("pool", lambda e: e.memset(s.Sb[:], 0.0), writes=[s.bSb])
                load_group(s, hp, 0)
            for gg in range(ngr):
                j = gg % 2
                if gg + 1 < ngr:
                    for s, hp in zip(slots, hps):
                        load_group(s, hp, gg + 1)
                for c in range(8):
                    ch = gg * 8 + c
                    for s in slots:
                        A, Bd, K, R = [s.bd[k][j][:, c, :] for k in ("A", "B", "K", "R")]
                        bb = s.bbd[j]
                        s.p1, s.bp1 = PA(s, 0)
                        S.op("pe", lambda e: e.matmul(s.p1, lhsT=Bd, rhs=A, start=True, stop=True), reads=[bb], writes=[s.bp1], sig=False)
                        s.p2, s.bp2 = PA(s, 1)
                        S.op("pe", lambda e: e.matmul(s.p2, lhsT=A, rhs=Bd, start=True, stop=True), reads=[bb], writes=[s.bp2])
                    for s in slots:
                        S.op("dve", lambda e: e.tensor_tensor(out=s.N[0][:], in0=s.p1, in1=MU[:], op=ALU.mult), reads=[s.bp1, bconst], writes=[s.bN[0]])
                        S.op("dve", lambda e: e.tensor_tensor(out=s.P[0][:], in0=s.p2, in1=ML[:], op=ALU.mult), reads=[s.bp2, bconst], writes=[s.bP[0]])
                        S.op("pool", lambda e: e.tensor_tensor(out=s.X[0][:], in0=s.N[0][:], in1=I32[:], op=ALU.add), reads=[s.bN[0], bconst], writes=[s.bX[0]])
                    for s in slots:
                        A, Bd, K, R = [s.bd[k][j][:, c, :] for k in ("A", "B", "K", "R")]
                        bb = s.bbd[j]
                        trip = (("Mak", K, A, MU), ("Mrb", Bd, R, MUI), ("Mrk", K, R, MUI))
                        for n_, (nm, l_, r_, msk) in enumerate(trip):
                            p, bp = PB(s, n_)
                            S.op("pe", lambda e: e.matmul(p, lhsT=l_, rhs=r_, start=True, stop=True), reads=[bb], writes=[bp], sig=False)
                        S.op("pe", lambda e: e.transpose(s.ptr[:, 0:128], Bd, identb[:]), reads=[bb, bconst], writes=[s.bptr], sig=False)
                        S.op("pe", lambda e: e.transpose(s.ptr[:, 128:256], K, identb[:]), reads=[bb, bconst], writes=[s.bptr])
                        for n_, (nm, l_, r_, msk) in enumerate(trip):
                            p, bp = PB(s, n_)
                            S.op("dve", lambda e: e.tensor_tensor(out=getattr(s, nm)[:], in0=p, in1=msk[:], op=ALU.mult), reads=[bp, bconst],
                                 writes=[getattr(s, "b" + nm)])
                        S.op("act", lambda e: e.copy(out=s.BtT[:], in_=s.ptr[:, 0:128]), reads=[s.bptr], writes=[s.bBtT])
                        S.op("act", lambda e: e.copy(out=s.KtT[:], in_=s.ptr[:, 128:256]), reads=[s.bptr], writes=[s.bKtT])
                    cur = 0
                    for lvl in range(5):
                        nxt = 1 - cur
                        last = (lvl == 4)
                        for s in slots:
                            if not last:
                                s.pq, s.bpq = PA(s, 0)
                                S.op("pe", lambda e: e.matmul(s.pq, lhsT=s.P[cur][:], rhs=s.N[cur][:], start=True, stop=True),
                                     reads=[s.bP[cur], s.bN[cur]], writes=[s.bpq])
                            s.pp_, s.bpp = PA(s, 1)
                            S.op("pe", lambda e: e.matmul(s.pp_, lhsT=s.N[cur][:], rhs=s.P[cur][:], start=True, stop=True),
                                 reads=[s.bP[cur], s.bN[cur]], writes=[s.bpp])
                        for s in slots:
                            if not last:
                                S.op("act", lambda e: e.copy(out=s.N[nxt][:], in_=s.pq), reads=[s.bpq], writes=[s.bN[nxt]])
                            S.op("act", lambda e: e.copy(out=s.P[nxt][:], in_=s.pp_), reads=[s.bpp], writes=[s.bP[nxt]])
                        for s in slots:
                            s.px, s.bpx = PB(s, 0)
                            S.op("pe", lambda e: e.matmul(s.px, lhsT=s.P[nxt][:], rhs=s.X[cur][:], start=True, stop=True),
                                 reads=[s.bP[nxt], s.bX[cur]], writes=[s.bpx])
                        for s in slots:
                            S.op("dve", lambda e: e.tensor_tensor(out=s.X[nxt][:], in0=s.px, in1=s.X[cur][:], op=ALU.add),
                                 reads=[s.bpx, s.bX[cur]], writes=[s.bX[nxt]])
                        cur = nxt
                    TT, bTT = cur, None
                    for s in slots:
                        A = s.bd["A"][j][:, c, :]
                        s.pX, s.bpX = PA(s, 2)
                        S.op("pe", lambda e: e.matmul(s.pX[:, 0:64], lhsT=A, rhs=s.Sb[:], start=True, stop=False),
                             reads=[s.bbd[j], s.bSb], writes=[s.bpX], sig=False)
                        S.op("pe", lambda e: e.matmul(s.pX[:, 0:64], lhsT=s.Mak[:], rhs=s.Vs[j][:, c, :], start=False, stop=True),
                             reads=[s.bMak, s.bbd[j]], writes=[s.bpX])
                        S.op("pool", lambda e: e.tensor_scalar(out=s.Sw[:], in0=s.St[:], scalar1=s.wcs[:, ch:ch + 1], scalar2=None, op0=ALU.mult),
                             reads=[s.bSt, s.bwcs], writes=[s.bSw])
                    for s in slots:
                        S.op("act", lambda e: e.copy(out=s.Xs[:], in_=s.pX[:, 0:64]), reads=[s.bpX], writes=[s.bXs])
                    for s in slots:
                        s.pU, s.bpU = PB(s, 1)
                        S.op("pe", lambda e: e.matmul(s.pU[:, 0:64], lhsT=s.X[TT][:], rhs=s.Xs[:], start=True, stop=True),
                             reads=[s.bX[TT], s.bXs], writes=[s.bpU])
                    for s in slots:
                        S.op("dve", lambda e: e.tensor_copy(out=s.Ub[:], in_=s.pU[:, 0:64]), reads=[s.bpU], writes=[s.bUb])
                    for s in slots:
                        R = s.bd["R"][j][:, c, :]
                        s.pY, s.bpY = PA(s, 0)
                        S.op("pe", lambda e: e.matmul(s.pY[:, 0:64], lhsT=R, rhs=s.Sb[:], start=True, stop=False),
                             reads=[s.bbd[j], s.bSb], writes=[s.bpY], sig=False)
                        S.op("pe", lambda e: e.matmul(s.pY[:, 0:64], lhsT=s.Mrb[:], rhs=s.Ub[:], start=False, stop=False),
                             reads=[s.bMrb, s.bUb], writes=[s.bpY], sig=False)
                        S.op("pe", lambda e: e.matmul(s.pY[:, 0:64], lhsT=s.Mrk[:], rhs=s.Vs[j][:, c, :], start=False, stop=True),
                             reads=[s.bMrk, s.bbd[j]], writes=[s.bpY])
                        s.pS, s.bpS = PA(s, 1)
                        S.op("pe", lambda e: e.matmul(s.pS[:, 0:64], lhsT=s.BtT[:], rhs=s.Ub[:], start=True, stop=False),
                             reads=[s.bBtT, s.bUb], writes=[s.bpS], sig=False)
                        S.op("pe", lambda e: e.matmul(s.pS[:, 0:64], lhsT=s.KtT[:], rhs=s.Vs[j][:, c, :], start=False, stop=True),
                             reads=[s.bKtT, s.bbd[j]], writes=[s.bpS])
                    for s in slots:
                        S.op("act", lambda e: e.copy(out=s.Yo[j][:, c, :], in_=s.pY[:, 0:64]), reads=[s.bpY], writes=[s.bYo[j]])
                        S.op("dve", lambda e: e.scalar_tensor_tensor(out=s.St[:], in0=s.pS[:, 0:64], scalar=s.wcs[:, ch:ch + 1], in1=s.Sw[:],
                                                                     op0=ALU.mult, op1=ALU.add), reads=[s.bpS, s.bwcs, s.bSw], writes=[s.bSt])
                        S.op("act", lambda e: e.copy(out=s.Sb[:], in_=s.St[:]), reads=[s.bSt], writes=[s.bSb])
                for s, hp in zip(slots, hps):
                    for h in range(2):
                        c0 = hp * 128 + h * 64
                        S.dma("sp", Ysc[gg * 512:(gg + 1) * 512, c0:c0 + 64].rearrange("(c j) v -> j c v", j=64),
                              s.Yo[j][h * 64:(h + 1) * 64, :, :], s.bYo[j], reads=[s.bYo[j]], writes=[bY])
        S.barrier()
        for s in slots:
            for b in s.bbd + s.bYo + [s.bwcs]:
                S.release(b)


def rwkv_post_phase(nc, S, Ysc, BV, Gt, lg_row, lb_row, ZT, bins, bZT, ntok, gn_eps=64e-5):
    with ExitStack() as es:
        sb = lambda name, shape, dt: es.enter_context(nc.sbuf_tensor(_uniq(name), shape, dt))
        ps = lambda name, shape, dt: es.enter_context(nc.psum_tensor(_uniq(name), shape, dt))
        lgB = sb("lgB", [128, D], F32)
        lbB = sb("lbB", [128, D], F32)
        ident = sb("ident", [128, 128], BF16)
        nh = sb("nh", [128, 16, 1], F32)
        yt = [sb("yt%d" % i, [128, 16, 64], F32) for i in range(2)]
        bvt = [sb("bvt%d" % i, [128, D], F32) for i in range(2)]
        gt = [sb("gt%d" % i, [128, D], F32) for i in range(2)]
        sm = sb("sm", [128, 16, 1], F32)
        vr = sb("vr", [128, 16, 1], F32)
        rstd = sb("rstd", [128, 16, 1], F32)
        yc = sb("yc", [128, 16, 64], F32)
        sq = sb("sq", [128, 16, 64], F32)
        yn = sb("yn", [128, 16, 64], F32)
        y2 = sb("y2", [128, D], F32)
        zb = [sb("zb%d" % i, [128, D], BF16) for i in range(2)]
        zT = [sb("zT%d" % i, [128, 8, 512], BF16) for i in range(2)]
        p_tr = [ps("p_tr%d" % i, [128, 8, 128], BF16) for i in range(2)]
        bC = S.dbuf("C")
        byt = [S.dbuf("yt") for _ in range(2)]
        bbvt = [S.dbuf("bvt") for _ in range(2)]
        bgt = [S.dbuf("gt") for _ in range(2)]
        bzT = [S.dbuf("zT") for _ in range(2)]
        B = {n: S.buf(n) for n in ["ident", "nh", "sm", "vr", "rstd", "yc", "sq", "yn", "y2", "zb0", "zb1"]}
        bp_tr = [S.pbuf("ptr") for _ in range(2)]
        bYs, bBV, bG = bins
        make_ident(nc, S, ident, B["ident"])
        S.op("pool", lambda e: e.memset(nh[:], -0.5), writes=[B["nh"]])
        S.dma("sp", lgB[:], lg_row.to_broadcast([128, D]), bC, writes=[bC])
        S.dma("sp", lbB[:], lb_row.to_broadcast([128, D]), bC, writes=[bC])
        nt = ntok // 128
        for t in range(nt):
            i = t % 2
            t0 = t * 128
            S.dma("sp", yt[i][:].rearrange("p a b -> p (a b)"), Ysc[t0:t0 + 128, :], byt[i], reads=[bYs], writes=[byt[i]])
            S.dma("sp", bvt[i][:], BV[t0:t0 + 128, :], bbvt[i], reads=[bBV], writes=[bbvt[i]])
            S.dma("sp", gt[i][:], Gt[t0:t0 + 128, :], bgt[i], reads=[bG], writes=[bgt[i]])
            y3 = yt[i]
            S.op("dve", lambda e: e.tensor_reduce(out=sm[:], in_=y3[:], axis=AX.X, op=ALU.add), reads=[byt[i]], writes=[B["sm"]])
            S.op("dve", lambda e: e.tensor_scalar(out=sm[:], in0=sm[:], scalar1=1.0 / 64, scalar2=None, op0=ALU.mult), reads=[B["sm"]], writes=[B["sm"]])
            S.op("dve", lambda e: e.tensor_tensor(out=yc[:], in0=y3[:], in1=sm[:].to_broadcast([128, 16, 64]), op=ALU.subtract),
                 reads=[byt[i], B["sm"]], writes=[B["yc"]])
            S.op("pool", lambda e: e.tensor_tensor(out=sq[:], in0=yc[:], in1=yc[:], op=ALU.mult), reads=[B["yc"]], writes=[B["sq"]])
            S.op("dve", lambda e: e.tensor_reduce(out=vr[:], in_=sq[:], axis=AX.X, op=ALU.add), reads=[B["sq"]], writes=[B["vr"]])
            S.op("dve", lambda e: e.tensor_scalar(out=vr[:], in0=vr[:], scalar1=1.0 / 64, scalar2=gn_eps, op0=ALU.mult, op1=ALU.add),
                 reads=[B["vr"]], writes=[B["vr"]])
            S.op("pool", lambda e: e.tensor_tensor(out=rstd[:], in0=vr[:], in1=nh[:], op=ALU.pow), reads=[B["vr"], B["nh"]], writes=[B["rstd"]])
            S.op("dve", lambda e: e.tensor_tensor(out=yn[:], in0=yc[:], in1=rstd[:].to_broadcast([128, 16, 64]), op=ALU.mult),
                 reads=[B["yc"], B["rstd"]], writes=[B["yn"]])
            ynf = yn[:].rearrange("p a b -> p (a b)")
            S.op("pool", lambda e: e.tensor_tensor(out=y2[:], in0=ynf, in1=lgB[:], op=ALU.mult), reads=[B["yn"], bC], writes=[B["y2"]])
            S.op("pool", lambda e: e.tensor_tensor(out=y2[:], in0=y2[:], in1=lbB[:], op=ALU.add), reads=[B["y2"], bC], writes=[B["y2"]])
            S.op("dve", lambda e: e.tensor_tensor(out=y2[:], in0=y2[:], in1=bvt[i][:], op=ALU.add), reads=[B["y2"], bbvt[i]], writes=[B["y2"]])
            z, bz = zb[i], B["zb%d" % i]
            S.op("dve", lambda e: e.tensor_tensor(out=z[:], in0=y2[:], in1=gt[i][:], op=ALU.mult), reads=[B["y2"], bgt[i]], writes=[bz])
            pt, bpt = p_tr[i], bp_tr[i]
            for c in range(8):
                S.op("pe", lambda e: e.transpose(pt[:, c, :], z[:, c * 128:(c + 1) * 128], ident[:]), reads=[bz, B["ident"]], writes=[bpt], sig=(c == 7))
            gi = (t // 4) % 2
            S.op("act", lambda e: e.copy(out=zT[gi][:, :, (t % 4) * 128:(t % 4 + 1) * 128], in_=pt[:]), reads=[bpt], writes=[bzT[gi]])
            if t % 4 == 3:
                g0 = (t // 4) * 512
                for c in range(8):
                    S.dma("sp", ZT[c * 128:(c + 1) * 128, g0:g0 + 512], zT[gi][:, c, :], bzT[gi], reads=[bzT[gi]], writes=[bZT])
        S.barrier()
        for b in [bC] + byt + bbvt + bgt + bzT:
            S.release(b)


def build_program(ntok=SEQ):
    nc = bass.Bass("TRN2", target_bir_lowering=False)
    di = lambda n, s: nc.dram_tensor(n, list(s), F32, kind="ExternalInput").ap()
    x = di("x", [ntok, D])
    ffn_norm = di("ffn_norm", [4, D])
    wg = di("ffn_w_gate", [2, 2, D, DFF])
    wu = di("ffn_w_up", [2, 2, D, DFF])
    wd = di("ffn_w_down", [2, 2, DFF, D])
    mix_norm = di("mix_norm", [2, D])
    dbias = di("dbias", [12, 128, 2, 128])
    w_in = di("attn_w_in", [D, 3072])
    qn = di("attn_q_norm", [64, 1])
    kn = di("attn_k_norm", [64, 1])
    w_out = di("attn_w_out", [512, D])
    rw_mix = di("rw_mix", [6, D])
    rows = {n: di(n, [1, D]) for n in ("rw_w0", "rw_a0", "rw_kk", "rw_ka", "rw_rk", "rw_lnx_g", "rw_lnx_b")}
    rw_w1 = di("rw_w1", [D, 64]); rw_w2 = di("rw_w2", [64, D]); rw_a1 = di("rw_a1", [D, 64]); rw_a2 = di("rw_a2", [64, D])
    rw_g1 = di("rw_g1", [D, 160]); rw_g2 = di("rw_g2", [160, D])
    rw_wr = di("rw_wr", [D, D]); rw_wk = di("rw_wk", [D, D]); rw_wv = di("rw_wv", [D, D]); rw_wo = di("rw_wo", [D, D])
    out = nc.dram_tensor("out", [ntok, D], F32, kind="ExternalOutput").ap()
    scr = lambda n, s, dt: nc.dram_tensor(n, list(s), dt, kind="Internal").ap()
    xa = scr("xa", [ntok, D], F32); xb = scr("xb", [ntok, D], F32)
    QT = scr("QT", [D, ntok], BF16); KT = scr("KT", [D, ntok], BF16); V = scr("V", [ntok, D], BF16); MT = scr("MT", [512, ntok], BF16)
    RtT, KtT, AtT, BtT = [scr(n, [D, ntok], BF16) for n in ("RtT", "KtT", "AtT", "BtT")]
    WC = scr("WC", [D, ntok // 64], F32)
    Vtok = scr("Vtok", [ntok, D], BF16); BV = scr("BV", [ntok, D], F32); Gt = scr("Gt", [ntok, D], F32); Ysc = scr("Ysc", [ntok, D], F32)
    ZT = scr("ZT", [D, ntok], BF16)

    S = Sched(nc, n_dma_sems=24)
    nb = lambda n: S.buf(n, acc=True)
    bx, bxa, bxb, bout = nb("x"), nb("xa"), nb("xb"), nb("out")
    bQT, bKT, bV, bMT = nb("QT"), nb("KT"), nb("V"), nb("MT")
    bR, bK, bA, bB, bWC, bVt, bBV, bG, bY, bZ = [nb(n) for n in ("R", "K", "A", "B", "WC", "Vt", "BV", "G", "Y", "Z")]

    ffn_phase(nc, S, x, xa, bx, bxa, wg[0, 0], wu[0, 0], wd[0, 0], ffn_norm[0:1, :], ntok)
    attn_in_phase(nc, S, xa, bxa, w_in, mix_norm[0:1, :], qn, kn, QT, KT, V, bQT, bKT, bV, ntok)
    sb_attn_phase(nc, S, QT, KT, V, MT, bQT, bKT, bV, bMT, ntok)
    dil_attn_phase(nc, S, QT, KT, V, MT, dbias, bQT, bKT, bV, bMT, ntok)
    out_proj_phase(nc, S, xa, xb, bxa, bxb, MT, bMT, w_out, 512, ntok)
    ffn_phase(nc, S, xb, xa, bxb, bxa, wg[0, 1], wu[0, 1], wd[0, 1], ffn_norm[1:2, :], ntok)
    ffn_phase(nc, S, xa, xb, bxa, bxb, wg[1, 0], wu[1, 0], wd[1, 0], ffn_norm[2:3, :], ntok)
    rwkv_fm_phase(nc, S, xb, bxb, mix_norm[1:2, :], rw_mix, rows["rw_w0"], rw_w1, rw_w2, rows["rw_a0"], rw_a1, rw_a2,
                  rows["rw_kk"], rows["rw_ka"], rw_wr, rw_wk, RtT, KtT, AtT, BtT, WC, [bR, bK, bA, bB, bWC], ntok)
    rwkv_tm_phase(nc, S, xb, bxb, mix_norm[1:2, :], rw_mix, rows["rw_a0"], rw_a1, rw_a2, rw_g1, rw_g2, rows["rw_ka"], rows["rw_rk"],
                  rw_wr, rw_wk, rw_wv, Vtok, BV, Gt, [bVt, bBV, bG], ntok)
    rwkv_scan_phase(nc, S, RtT, KtT, AtT, BtT, WC, Vtok, Ysc, [bR, bK, bA, bB, bWC, bVt], bY, ntok)
    rwkv_post_phase(nc, S, Ysc, BV, Gt, rows["rw_lnx_g"], rows["rw_lnx_b"], ZT, [bY, bBV, bG], bZ, ntok)
    out_proj_phase(nc, S, xb, xa, bxb, bxa, ZT, bZ, rw_wo, 1024, ntok)
    ffn_phase(nc, S, xa, out, bxa, bout, wg[1, 1], wu[1, 1], wd[1, 1], ffn_norm[3:4, :], ntok)
    S.wait_for("sp", [bout])
    return nc


def kernel(x, ffn_norm, ffn_w_gate, ffn_w_up, ffn_w_down, mix_norm, rel_bias,
           attn_w_in, attn_q_norm, attn_k_norm, attn_w_out,
           rw_mix, rw_w0, rw_w1, rw_w2, rw_a0, rw_a1, rw_a2, rw_g1, rw_g2,
           rw_kk, rw_ka, rw_rk, rw_wr, rw_wk, rw_wv, rw_wo, rw_lnx_g, rw_lnx_b):
    f = lambda a: np.ascontiguousarray(np.asarray(a, dtype=np.float32))
    x = f(x)
    n = x.shape[0]
    shared = {
        "ffn_norm": f(ffn_norm).reshape(4, D), "ffn_w_gate": f(ffn_w_gate), "ffn_w_up": f(ffn_w_up), "ffn_w_down": f(ffn_w_down),
        "mix_norm": f(mix_norm), "dbias": dil_bias_host(f(rel_bias)),
        "attn_w_in": f(attn_w_in)[0], "attn_q_norm": f(attn_q_norm).reshape(64, 1), "attn_k_norm": f(attn_k_norm).reshape(64, 1),
        "attn_w_out": f(attn_w_out)[0], "rw_mix": f(rw_mix)[0],
        "rw_w0": f(rw_w0).reshape(1, D), "rw_a0": f(rw_a0).reshape(1, D), "rw_kk": f(rw_kk).reshape(1, D), "rw_ka": f(rw_ka).reshape(1, D),
        "rw_rk": f(rw_rk).reshape(1, D), "rw_lnx_g": f(rw_lnx_g).reshape(1, D), "rw_lnx_b": f(rw_lnx_b).reshape(1, D),
        "rw_w1": f(rw_w1)[0], "rw_w2": f(rw_w2)[0], "rw_a1": f(rw_a1)[0], "rw_a2": f(rw_a2)[0], "rw_g1": f(rw_g1)[0], "rw_g2": f(rw_g2)[0],
        "rw_wr": f(rw_wr)[0], "rw_wk": f(rw_wk)[0], "rw_wv": f(rw_wv)[0], "rw_wo": f(rw_wo)[0],
    }
    nc = build_program(x.shape[1])
    in_maps = [dict(shared, x=x[i]) for i in range(n)]
    res = run_bass_kernel_spmd(nc, in_maps, core_ids=list(range(n)))
    return np.stack([np.asarray(r["out"]) for r in res.results], axis=0).astype(np.float32)
```

```python
import numpy as np
from contextlib import ExitStack
import concourse.bass as bass
import concourse.mybir as mybir
from concourse.bass_utils import run_bass_kernel_spmd

F32 = mybir.dt.float32
BF16 = mybir.dt.bfloat16
AF = mybir.ActivationFunctionType
ALU = mybir.AluOpType
AX = mybir.AxisListType

D = 1024
DFF = 2816
NF = DFF // 128
SEQ = 4096


_UID = [0]


def _uniq(name):
    _UID[0] += 1
    return "%s_u%d" % (name, _UID[0])


def _merge(d, s):
    for k, v in s.items():
        if d.get(k, 0) < v:
            d[k] = v


class Buf:
    __slots__ = ("name", "wr", "rd", "acc", "dkey", "excl")

    def __init__(self, name, acc=False, excl=False):
        self.name = name
        self.wr = {}
        self.rd = {}
        self.acc = acc
        self.dkey = None
        self.excl = excl


class Sched:
    ENG = ("pe", "act", "dve", "pool", "sp")

    def __init__(self, nc, n_dma_sems=40):
        self.nc = nc
        self.eng = {"pe": nc.tensor, "act": nc.scalar, "dve": nc.vector, "pool": nc.gpsimd, "sp": nc.sync}
        self.sems = {}
        self.val = {}
        self.seen = {e: {} for e in self.ENG}
        self.epoch = 0
        self.ekey = {}
        self._new_engine_sems()
        self.dma_pool = []
        for i in range(n_dma_sems):
            k = "dma%d" % i
            self.sems[k] = nc.semaphore(k).__enter__()
            self.val[k] = 0
            self.dma_pool.append(k)
        self.nwait = 0

    def _new_engine_sems(self):
        for e in self.ENG:
            k = "%s_e%d" % (e, self.epoch)
            self.sems[k] = self.nc.semaphore(k).__enter__()
            self.val[k] = 0
            self.ekey[e] = k

    def buf(self, name, acc=False):
        return Buf(name, acc)

    def pbuf(self, name):
        return Buf(name, False, True)

    def dbuf(self, name, acc=False):
        b = Buf(name, acc)
        b.dkey = self.dma_pool.pop()
        return b

    def release(self, b):
        self.dma_pool.append(b.dkey)
        b.dkey = None

    def _wait(self, e, deps):
        for k, v in deps.items():
            if v <= 0:
                continue
            if e == "pe" and k == self.ekey["pe"]:
                continue
            if self.seen[e].get(k, 0) < v:
                self.eng[e].wait_ge(self.sems[k], v)
                self.seen[e][k] = v
                self.nwait += 1

    def _deps(self, reads, writes, e=None):
        deps = {}
        for b in reads:
            _merge(deps, b.wr)
            if b.excl:
                own = self.ekey.get(e)
                _merge(deps, {k: v for k, v in b.rd.items() if k != own})
        for b in writes:
            _merge(deps, b.wr)
            _merge(deps, b.rd)
        return deps

    def _record(self, ev, reads, writes):
        for b in reads:
            _merge(b.rd, ev)
        for b in writes:
            if b.acc:
                _merge(b.wr, ev)
            else:
                b.wr = dict(ev)
                b.rd = {}

    def op(self, e, fn, reads=(), writes=(), sig=True):
        self._wait(e, self._deps(reads, writes, e))
        ins = fn(self.eng[e])
        k = self.ekey[e]
        if sig:
            ins.then_inc(self.sems[k], 1)
            self.val[k] += 1
            v = self.val[k]
        else:
            v = self.val[k] + 1
        self._record({k: v}, reads, writes)
        return ins

    def dma(self, q, out, in_, track, reads=(), writes=(), slow=False):
        self._wait(q, self._deps(reads, writes))
        if slow:
            ins = self.eng[q].dma_start(out=out, in_=in_, allow_slow_non_contiguous=True)
        else:
            ins = self.eng[q].dma_start(out=out, in_=in_)
        k = track.dkey
        ins.then_inc(self.sems[k], 16)
        self.val[k] += 16
        self._record({k: self.val[k]}, reads, writes)
        return ins

    def barrier(self, new_epoch=True):
        allv = {k: v for k, v in self.val.items() if v > 0}
        for e in self.ENG:
            self._wait(e, allv)
        if new_epoch:
            self.epoch += 1
            self._new_engine_sems()

    def wait_for(self, e, bufs):
        deps = {}
        for b in bufs:
            _merge(deps, b.wr)
            _merge(deps, b.rd)
        self._wait(e, deps)


def ffn_phase(nc, S, x_in, x_out, xin_b, xout_b, wg, wu, wd, gain_row, ntok, eps=1e-6):
    G = 256
    ng = ntok // G
    with ExitStack() as es:
        sb = lambda name, shape, dt: es.enter_context(nc.sbuf_tensor(_uniq(name), shape, dt))
        ps = lambda name, shape, dt: es.enter_context(nc.psum_tensor(_uniq(name), shape, dt))
        Wg = sb("Wg", [128, 8, DFF], BF16)
        Wu = sb("Wu", [128, 8, DFF], BF16)
        Wd = sb("Wd", [128, NF, D], BF16)
        gB = sb("gB", [128, D], F32)
        ident = sb("ident", [128, 128], BF16)
        xt = [sb("xt%d" % i, [128, D], F32) for i in range(4)]
        ot = [sb("ot%d" % i, [128, D], F32) for i in range(2)]
        hb = [sb("hb%d" % i, [128, D], BF16) for i in range(2)]
        hT = [sb("hT%d" % i, [128, 8, G], BF16) for i in range(2)]
        aT = [sb("aT%d" % i, [128, G], BF16) for i in range(3)]
        sg = [sb("sg%d" % i, [128, G], F32) for i in range(2)]
        junk = sb("junk", [128, D], BF16)
        ss = [sb("ss%d" % i, [128, 1], F32) for i in range(2)]
        rs = [sb("rs%d" % i, [128, 1], F32) for i in range(2)]
        nh = sb("nh", [128, 1], F32)
        p_gu = [ps("p_gu%d" % i, [128, 2, G], F32) for i in range(2)]
        p_dn = [ps("p_dn%d" % i, [128, 512], F32) for i in range(4)]
        p_tr = [ps("p_tr%d" % i, [128, 8, 128], BF16) for i in range(2)]

        bWg, bWu, bWd, bgB = S.dbuf("Wg"), S.dbuf("Wu"), S.dbuf("Wd"), S.dbuf("gB")
        bxt = [S.dbuf("xt%d" % i) for i in range(4)]
        bot = [S.dbuf("ot%d" % i) for i in range(2)]
        bhb = [S.buf("hb") for _ in range(2)]
        bhT = [S.buf("hT") for _ in range(2)]
        baT = [S.buf("aT") for _ in range(3)]
        bsg = [S.buf("sg") for _ in range(2)]
        bjunk = S.buf("junk")
        bss = [S.buf("ss") for _ in range(2)]
        brs = [S.buf("rs") for _ in range(2)]
        bnh, bident = S.buf("nh"), S.buf("ident")
        bp_gu = [S.pbuf("pgu") for _ in range(2)]
        bp_dn = [S.pbuf("pdn") for _ in range(4)]
        bp_tr = [S.pbuf("ptr") for _ in range(2)]

        S.op("pool", lambda e: e.memset(nh[:], -0.5), writes=[bnh])
        S.op("pool", lambda e: e.memset(ident[:], 0.0), writes=[bident])
        S.op("pool", lambda e: e.affine_select(out=ident[:], in_=ident[:], pattern=[[-1, 128]],
                                               compare_op=ALU.not_equal, fill=1.0, base=0,
                                               channel_multiplier=1), reads=[bident], writes=[bident])
        S.dma("sp", gB[:], gain_row.to_broadcast([128, D]), bgB, writes=[bgB])
        wg_v = wg.rearrange("(c p) f -> p c f", p=128)
        wu_v = wu.rearrange("(c p) f -> p c f", p=128)
        wd_v = wd.rearrange("(c p) f -> p c f", p=128)
        for c in range(8):
            S.dma("pool", Wg[:, c, :], wg_v[:, c, :], bWg, writes=[bWg])
            S.dma("pool", Wu[:, c, :], wu_v[:, c, :], bWu, writes=[bWu])
        for c in range(NF):
            S.dma("pool", Wd[:, c, :], wd_v[:, c, :], bWd, writes=[bWd])

        def load(g):
            for s in range(2):
                i = (g % 2) * 2 + s
                t0 = g * G + s * 128
                S.dma("sp", xt[i][:], x_in[t0:t0 + 128, :], bxt[i], reads=[xin_b], writes=[bxt[i]])

        def prep_dve(g):
            for s in range(2):
                i = (g % 2) * 2 + s
                S.op("dve", lambda e: e.scalar_tensor_tensor(out=junk[:], in0=xt[i][:], scalar=1.0, in1=xt[i][:],
                                                             op0=ALU.mult, op1=ALU.mult, accum_out=ss[s][:]),
                     reads=[bxt[i]], writes=[bjunk, bss[s]])
                S.op("dve", lambda e: e.tensor_scalar(out=ss[s][:], in0=ss[s][:], scalar1=1.0 / D, scalar2=eps,
                                                      op0=ALU.mult, op1=ALU.add), reads=[bss[s]], writes=[bss[s]])
                S.op("pool", lambda e: e.tensor_tensor(out=rs[s][:], in0=ss[s][:], in1=nh[:], op=ALU.pow),
                     reads=[bss[s], bnh], writes=[brs[s]])
                S.op("dve", lambda e: e.scalar_tensor_tensor(out=hb[s][:], in0=xt[i][:], scalar=rs[s][:], in1=gB[:],
                                                             op0=ALU.mult, op1=ALU.mult),
                     reads=[bxt[i], brs[s], bgB], writes=[bhb[s]])

        def prep_pe(g):
            for s in range(2):
                for c in range(8):
                    S.op("pe", lambda e: e.transpose(p_tr[s][:, c, :], hb[s][:, c * 128:(c + 1) * 128], ident[:]),
                         reads=[bhb[s], bident], writes=[bp_tr[s]], sig=(c == 7))
                S.op("act", lambda e: e.copy(out=hT[g % 2][:, :, s * 128:(s + 1) * 128], in_=p_tr[s][:]),
                     reads=[bp_tr[s]], writes=[bhT[g % 2]])

        def gate_up(g, j):
            h = hT[g % 2]
            pg = p_gu[j % 2]
            for c in range(8):
                S.op("pe", lambda e: e.matmul(pg[:, 0, :], lhsT=Wg[:, c, j * 128:(j + 1) * 128], rhs=h[:, c, :],
                                              start=(c == 0), stop=(c == 7)),
                     reads=[bWg, bhT[g % 2]], writes=[bp_gu[j % 2]], sig=False)
            for c in range(8):
                S.op("pe", lambda e: e.matmul(pg[:, 1, :], lhsT=Wu[:, c, j * 128:(j + 1) * 128], rhs=h[:, c, :],
                                              start=(c == 0), stop=(c == 7)),
                     reads=[bWu, bhT[g % 2]], writes=[bp_gu[j % 2]], sig=(c == 7))
            S.op("act", lambda e: e.activation(out=sg[j % 2][:], in_=pg[:, 0, :], func=AF.Silu),
                 reads=[bp_gu[j % 2]], writes=[bsg[j % 2]])
            S.op("dve", lambda e: e.tensor_tensor(out=aT[j % 3][:], in0=sg[j % 2][:], in1=pg[:, 1, :], op=ALU.mult),
                 reads=[bsg[j % 2], bp_gu[j % 2]], writes=[baT[j % 3]])

        def down(g, j):
            for s in range(2):
                for hh in range(2):
                    S.op("pe", lambda e: e.matmul(p_dn[s * 2 + hh][:], lhsT=aT[j % 3][:, s * 128:(s + 1) * 128],
                                                  rhs=Wd[:, j, hh * 512:(hh + 1) * 512],
                                                  start=(j == 0), stop=(j == NF - 1)),
                         reads=[baT[j % 3], bWd], writes=[bp_dn[s * 2 + hh]], sig=(j == NF - 1 or (s == 1 and hh == 1)))

        def epilogue(g):
            for s in range(2):
                i = (g % 2) * 2 + s
                for hh in range(2):
                    S.op("dve", lambda e: e.scalar_tensor_tensor(out=ot[s][:, hh * 512:(hh + 1) * 512], in0=p_dn[s * 2 + hh][:],
                                                                 scalar=0.5, in1=xt[i][:, hh * 512:(hh + 1) * 512],
                                                                 op0=ALU.mult, op1=ALU.add),
                         reads=[bp_dn[s * 2 + hh], bxt[i]], writes=[bot[s]])
                t0 = g * G + s * 128
                S.dma("sp", x_out[t0:t0 + 128, :], ot[s][:], bot[s], reads=[bot[s]], writes=[xout_b])

        load(0)
        if ng > 1:
            load(1)
        prep_dve(0)
        prep_pe(0)
        for g in range(ng):
            gate_up(g, 0)
            for j in range(NF):
                if j + 1 < NF:
                    gate_up(g, j + 1)
                elif g + 1 < ng:
                    pass
                down(g, j)
                if j == 3 and g + 1 < ng:
                    prep_dve(g + 1)
                if j == 14 and g + 1 < ng:
                    prep_pe(g + 1)
            epilogue(g)
            if g + 2 < ng:
                load(g + 2)
        S.barrier()
        for b in [bWg, bWu, bWd, bgB] + bxt + bot:
            S.release(b)


def make_ident(nc, S, ident, bident, dt_is_bf16=True):
    S.op("pool", lambda e: e.memset(ident[:], 0.0), writes=[bident])
    S.op("pool", lambda e: e.affine_select(out=ident[:], in_=ident[:], pattern=[[-1, 128]],
                                           compare_op=ALU.not_equal, fill=1.0, base=0,
                                           channel_multiplier=1), reads=[bident], writes=[bident])


def attn_in_phase(nc, S, x_in, xin_b, w_in, gain_row, qn, kn, QT, KT, V, bQT, bKT, bV, ntok, eps=1e-6):
    G = 512
    ng = ntok // G
    with ExitStack() as es:
        sb = lambda name, shape, dt: es.enter_context(nc.sbuf_tensor(_uniq(name), shape, dt))
        ps = lambda name, shape, dt: es.enter_context(nc.psum_tensor(_uniq(name), shape, dt))
        Win = sb("Win", [128, 8, 3072], BF16)
        gB = sb("gB", [128, D], F32)
        ident = sb("ident", [128, 128], BF16)
        bones = sb("bones", [128, 128], BF16)
        gq = sb("gq", [128, 1], F32)
        gk = sb("gk", [128, 1], F32)
        nh = sb("nh", [128, 1], F32)
        eps_ap = sb("eps_ap", [128, 1], F32)
        xt = [sb("xt%d" % i, [128, D], F32) for i in range(4)]
        hb = [sb("hb%d" % i, [128, D], BF16) for i in range(2)]
        hT = [sb("hT%d" % i, [128, 8, G], BF16) for i in range(2)]
        junk = sb("junk", [128, D], BF16)
        ss = [sb("ss%d" % i, [128, 1], F32) for i in range(2)]
        rs = [sb("rs%d" % i, [128, 1], F32) for i in range(2)]
        ob = [sb("ob%d" % i, [128, G], BF16) for i in range(3)]
        sq = [sb("sq%d" % i, [128, G], BF16) for i in range(2)]
        lt = [sb("lt%d" % i, [128, G], F32) for i in range(2)]
        rr = [sb("rr%d" % i, [128, G], F32) for i in range(2)]
        vb = [sb("vb%d" % i, [128, D], BF16) for i in range(2)]
        p_q = [ps("p_q%d" % i, [128, G], F32) for i in range(2)]
        p_s = [ps("p_s%d" % i, [128, G], F32) for i in range(2)]
        p_v = [ps("p_v%d" % i, [128, 512], F32) for i in range(2)]
        p_tr = [ps("p_tr%d" % i, [128, 8, 128], BF16) for i in range(2)]

        bWin, bgB, bgq, bgk = S.dbuf("Win"), S.dbuf("gB"), S.dbuf("gq"), S.dbuf("gk")
        bxt = [S.dbuf("xt") for _ in range(4)]
        bob = [S.dbuf("ob") for _ in range(3)]
        bvb = [S.dbuf("vb") for _ in range(2)]
        bhb = [S.buf("hb") for _ in range(2)]
        bhT = [S.buf("hT") for _ in range(2)]
        bjunk, bnh, bident, bbones = S.buf("junk"), S.buf("nh"), S.buf("ident"), S.buf("bones")
        bss = [S.buf("ss") for _ in range(2)]
        brs = [S.buf("rs") for _ in range(2)]
        bsq = [S.buf("sq") for _ in range(2)]
        blt = [S.buf("lt") for _ in range(2)]
        brr = [S.buf("rr") for _ in range(2)]
        bp_q = [S.pbuf("pq") for _ in range(2)]
        bp_s = [S.pbuf("ps") for _ in range(2)]
        bp_v = [S.pbuf("pv") for _ in range(2)]
        bp_tr = [S.pbuf("ptr") for _ in range(2)]

        S.op("pool", lambda e: e.memset(nh[:], -0.5), writes=[bnh])
        S.op("pool", lambda e: e.memset(eps_ap[:], eps), writes=[bnh])
        make_ident(nc, S, ident, bident)
        S.op("pool", lambda e: e.memset(bones[:], 0.0), writes=[bbones])
        S.op("pool", lambda e: e.memset(bones[0:64, 0:64], 1.0), writes=[bbones])
        S.op("pool", lambda e: e.memset(bones[64:128, 64:128], 1.0), writes=[bbones])
        S.dma("sp", gB[:], gain_row.to_broadcast([128, D]), bgB, writes=[bgB])
        for hh in range(2):
            S.dma("sp", gq[hh * 64:(hh + 1) * 64, :], qn, bgq, writes=[bgq])
            S.dma("sp", gk[hh * 64:(hh + 1) * 64, :], kn, bgk, writes=[bgk])
        S.op("dve", lambda e: e.tensor_scalar(out=gq[:], in0=gq[:], scalar1=0.125, scalar2=None, op0=ALU.mult),
             reads=[bgq], writes=[bgq])
        w_v = w_in.rearrange("(c p) f -> p c f", p=128)
        for c in range(8):
            S.dma("pool", Win[:, c, :], w_v[:, c, :], bWin, writes=[bWin])

        def load(g):
            for s in range(4):
                t0 = g * G + s * 128
                S.dma("sp", xt[s][:], x_in[t0:t0 + 128, :], bxt[s], reads=[xin_b], writes=[bxt[s]])

        def prep(g):
            for s in range(4):
                k = s % 2
                S.op("dve", lambda e: e.scalar_tensor_tensor(out=junk[:], in0=xt[s][:], scalar=1.0, in1=xt[s][:],
                                                             op0=ALU.mult, op1=ALU.mult, accum_out=ss[k][:]),
                     reads=[bxt[s]], writes=[bjunk, bss[k]])
                S.op("dve", lambda e: e.tensor_scalar(out=ss[k][:], in0=ss[k][:], scalar1=1.0 / D, scalar2=eps,
                                                      op0=ALU.mult, op1=ALU.add), reads=[bss[k]], writes=[bss[k]])
                S.op("pool", lambda e: e.tensor_tensor(out=rs[k][:], in0=ss[k][:], in1=nh[:], op=ALU.pow),
                     reads=[bss[k], bnh], writes=[brs[k]])
                S.op("dve", lambda e: e.scalar_tensor_tensor(out=hb[k][:], in0=xt[s][:], scalar=rs[k][:], in1=gB[:],
                                                             op0=ALU.mult, op1=ALU.mult),
                     reads=[bxt[s], brs[k], bgB], writes=[bhb[k]])
                for c in range(8):
                    S.op("pe", lambda e: e.transpose(p_tr[k][:, c, :], hb[k][:, c * 128:(c + 1) * 128], ident[:]),
                         reads=[bhb[k], bident], writes=[bp_tr[k]], sig=(c == 7))
                S.op("act", lambda e: e.copy(out=hT[g % 2][:, :, s * 128:(s + 1) * 128], in_=p_tr[k][:]),
                     reads=[bp_tr[k]], writes=[bhT[g % 2]])

        nob = 0
        for g in range(ng):
            load(g)
            prep(g)
            h = hT[g % 2]
            bh = bhT[g % 2]
            t0 = g * G
            for fc in range(16):
                isq = fc < 8
                ch = fc % 8
                if ch < 2:
                    col0 = (0 if isq else 256) + ch * 128
                else:
                    col0 = (768 if isq else 1536) + (ch - 2) * 128
                pq = p_q[fc % 2]
                for c in range(8):
                    S.op("pe", lambda e: e.matmul(pq[:], lhsT=Win[:, c, col0:col0 + 128], rhs=h[:, c, :],
                                                  start=(c == 0), stop=(c == 7)),
                         reads=[bWin, bh], writes=[bp_q[fc % 2]], sig=(c == 7))
                o = ob[nob % 3]
                bo = bob[nob % 3]
                nob += 1
                if ch < 2:
                    S.op("act", lambda e: e.activation(out=o[:], in_=pq[:], func=AF.Copy, scale=(0.125 if isq else 1.0)),
                         reads=[bp_q[fc % 2]], writes=[bo])
                else:
                    k = fc % 2
                    S.op("act", lambda e: e.activation(out=sq[k][:], in_=pq[:], func=AF.Square),
                         reads=[bp_q[k]], writes=[bsq[k]])
                    S.op("pe", lambda e: e.matmul(p_s[k][:], lhsT=bones[:], rhs=sq[k][:], start=True, stop=True),
                         reads=[bbones, bsq[k]], writes=[bp_s[k]])
                    S.op("act", lambda e: e.activation(out=lt[k][:], in_=p_s[k][:], func=AF.Ln, scale=1.0 / 64, bias=eps_ap[:]),
                         reads=[bp_s[k], bnh], writes=[blt[k]])
                    S.op("act", lambda e: e.activation(out=rr[k][:], in_=lt[k][:], func=AF.Exp, scale=-0.5),
                         reads=[blt[k]], writes=[brr[k]])
                    gcol = gq if isq else gk
                    S.op("dve", lambda e: e.scalar_tensor_tensor(out=o[:], in0=pq[:], scalar=gcol[:], in1=rr[k][:],
                                                                 op0=ALU.mult, op1=ALU.mult),
                         reads=[bp_q[k], brr[k], bgq, bgk], writes=[bo])
                dst, bdst = (QT, bQT) if isq else (KT, bKT)
                S.dma("sp", dst[ch * 128:(ch + 1) * 128, t0:t0 + G], o[:], bo, reads=[bo], writes=[bdst])
            for s in range(4):
                k = s % 2
                for (pv, cols, off) in ((p_v[0], (512, 768), 0), (p_v[0], (2304, 2560), 256), (p_v[1], (2560, 3072), 0)):
                    n = cols[1] - cols[0]
                    for c in range(8):
                        S.op("pe", lambda e: e.matmul(pv[:, off:off + n], lhsT=h[:, c, s * 128:(s + 1) * 128],
                                                      rhs=Win[:, c, cols[0]:cols[1]], start=(c == 0), stop=(c == 7)),
                             reads=[bWin, bh], writes=[bp_v[0], bp_v[1]], sig=(c == 7))
                S.op("act", lambda e: e.copy(out=vb[k][:, 0:512], in_=p_v[0][:]), reads=[bp_v[0]], writes=[bvb[k]])
                S.op("dve", lambda e: e.tensor_copy(out=vb[k][:, 512:1024], in_=p_v[1][:]), reads=[bp_v[1]], writes=[bvb[k]])
                S.dma("sp", V[t0 + s * 128:t0 + (s + 1) * 128, :], vb[k][:], bvb[k], reads=[bvb[k]], writes=[bV])
        S.barrier()
        for b in [bWin, bgB, bgq, bgk] + bxt + bob + bvb:
            S.release(b)


def sb_attn_phase(nc, S, QT, KT, V, MT, bQT, bKT, bV, bMT, ntok):
    nblk = ntok // 128
    NH = 4
    with ExitStack() as es:
        sb = lambda name, shape, dt: es.enter_context(nc.sbuf_tensor(_uniq(name), shape, dt))
        ps = lambda name, shape, dt: es.enter_context(nc.psum_tensor(_uniq(name), shape, dt))
        qT = sb("qT", [128, 2, ntok], BF16)
        kT = sb("kT", [128, 2, ntok], BF16)
        v = sb("v", [128, nblk, 256], BF16)
        ones = sb("ones", [128, 512], F32)
        onec = sb("onec", [128, 1], F32)
        mneg = sb("mneg", [128, 128], BF16)
        ident = sb("ident", [128, 128], BF16)
        mk2 = lambda nm, shape, dt: [[sb("%s%d_%d" % (nm, h, i), shape, dt) for i in range(2)] for h in range(NH)]
        e_ = mk2("e", [128, 512], F32)
        sp_ = mk2("sp", [128, 512], F32)
        cs_ = mk2("cs", [128, 512], F32)
        lw_ = mk2("lw", [128, 512], F32)
        w_ = mk2("w", [128, 512], BF16)
        wT_ = mk2("wT", [128, 4, 128], BF16)
        oT = [[sb("oT%d_%d" % (pr, i), [128, 512], BF16) for i in range(2)] for pr in range(2)]
        p_z = [ps("p_z%d" % h, [128, 512], F32) for h in range(NH)]
        p_w = [ps("p_w%d" % i, [128, 4, 128], BF16) for i in range(2)]
        p_o = [ps("p_o%d" % i, [128, 128], F32) for i in range(2)]

        bq, bk, bv = S.dbuf("qT"), S.dbuf("kT"), S.dbuf("v")
        boT = [[S.dbuf("oT") for _ in range(2)] for _ in range(2)]
        bones, bmneg, bident = S.buf("ones"), S.buf("mneg"), S.buf("ident")
        bb2 = lambda nm: [[S.buf(nm) for _ in range(2)] for _ in range(NH)]
        be, bsp, bcs, blw, bw, bwT = bb2("e"), bb2("sp"), bb2("cs"), bb2("lw"), bb2("w"), bb2("wT")
        bp_z = [S.pbuf("pz") for _ in range(NH)]
        bp_w = [S.pbuf("pw") for _ in range(2)]
        bp_o = [S.pbuf("po") for _ in range(2)]

        S.op("pool", lambda e: e.memset(ones[:], 1.0), writes=[bones])
        S.op("pool", lambda e: e.memset(onec[:], 1.0), writes=[bones])
        make_ident(nc, S, ident, bident)
        S.op("pool", lambda e: e.memset(mneg[:], 0.0), writes=[bmneg])
        S.op("pool", lambda e: e.affine_select(out=mneg[:], in_=mneg[:], pattern=[[-1, 128]], compare_op=ALU.is_gt,
                                               fill=-30000.0, base=0, channel_multiplier=1),
             reads=[bmneg], writes=[bmneg])
        for pr in range(2):
            S.dma("sp", qT[:, pr, :], QT[pr * 128:(pr + 1) * 128, 0:ntok], bq, reads=[bQT], writes=[bq])
            S.dma("sp", kT[:, pr, :], KT[pr * 128:(pr + 1) * 128, 0:ntok], bk, reads=[bKT], writes=[bk])
        S.dma("sp", v[:], V[0:ntok, 0:256].rearrange("(b p) c -> p b c", p=128), bv, reads=[bV], writes=[bv])

        heads = [(h, h // 2, h % 2, slice(64 * (h % 2), 64 * (h % 2) + 64)) for h in range(NH)]
        u = 0
        for qb in range(nblk):
            chunks = [(4 * (qb // 4), qb + 1, True)]
            for c in range(qb // 4 - 1, -1, -1):
                chunks.append((4 * c, 4 * c + 4, False))
            for ci, (b0, b1, diag) in enumerate(chunks):
                W = (b1 - b0) * 128
                nb = b1 - b0
                i = u % 2
                u += 1
                rev = (lambda t: t[:, W - 1::-1] if W < 512 else t[:, ::-1])
                for (h, pr, hh, P) in heads:
                    S.op("pe", lambda e: e.matmul(p_z[h][:, 0:W], lhsT=qT[P, pr, qb * 128:(qb + 1) * 128],
                                                  rhs=kT[P, pr, b0 * 128:b1 * 128], start=True, stop=(not diag)),
                         reads=[bq, bk], writes=[bp_z[h]], sig=(not diag))
                    if diag:
                        S.op("pe", lambda e: e.matmul(p_z[h][:, W - 128:W], lhsT=ident[:], rhs=mneg[:], start=False, stop=True),
                             reads=[bident, bmneg], writes=[bp_z[h]])
                for (h, pr, hh, P) in heads:
                    S.op("act", lambda e: e.activation(out=e_[h][i][:, 0:W], in_=p_z[h][:, 0:W], func=AF.Exp),
                         reads=[bp_z[h]], writes=[be[h][i]])
                for (h, pr, hh, P) in heads:
                    S.op("act", lambda e: e.activation(out=sp_[h][i][:, 0:W], in_=e_[h][i][:, 0:W], func=AF.Ln, bias=onec[:]),
                         reads=[be[h][i], bones], writes=[bsp[h][i]])
                for (h, pr, hh, P) in heads:
                    if ci == 0:
                        init, rd = 0.0, [bsp[h][i], bones]
                    else:
                        init, rd = cs_[h][1 - i][:, 0:1], [bsp[h][i], bones, bcs[h][1 - i]]
                    S.op("dve", lambda e: e.tensor_tensor_scan(out=rev(cs_[h][i]), data0=ones[:, 0:W], data1=rev(sp_[h][i]),
                                                               initial=init, op0=ALU.mult, op1=ALU.add),
                         reads=rd, writes=[bcs[h][i]])
                for (h, pr, hh, P) in heads:
                    S.op("dve", lambda e: e.tensor_tensor(out=lw_[h][i][:, 0:W], in0=p_z[h][:, 0:W], in1=cs_[h][i][:, 0:W], op=ALU.subtract),
                         reads=[bp_z[h], bcs[h][i]], writes=[blw[h][i]])
                for (h, pr, hh, P) in heads:
                    S.op("act", lambda e: e.activation(out=w_[h][i][:, 0:W], in_=lw_[h][i][:, 0:W], func=AF.Exp),
                         reads=[blw[h][i]], writes=[bw[h][i]])
                for (h, pr, hh, P) in heads:
                    pw, bpw = p_w[h % 2], bp_w[h % 2]
                    for b in range(nb):
                        S.op("pe", lambda e: e.transpose(pw[:, b, :], w_[h][i][:, b * 128:(b + 1) * 128], ident[:]),
                             reads=[bw[h][i], bident], writes=[bpw], sig=(b == nb - 1))
                    if h % 2 == 0:
                        S.op("act", lambda e: e.copy(out=wT_[h][i][:, 0:nb, :], in_=pw[:, 0:nb, :]), reads=[bpw], writes=[bwT[h][i]])
                    else:
                        S.op("dve", lambda e: e.tensor_copy(out=wT_[h][i][:, 0:nb, :], in_=pw[:, 0:nb, :]), reads=[bpw], writes=[bwT[h][i]])
                for (h, pr, hh, P) in heads:
                    for b in range(nb):
                        first = (ci == 0 and b == 0)
                        last = (ci == len(chunks) - 1 and b == nb - 1)
                        S.op("pe", lambda e: e.matmul(p_o[pr][P, :], lhsT=v[:, b0 + b, h * 64:(h + 1) * 64], rhs=wT_[h][i][:, b, :],
                                                      start=first, stop=last),
                             reads=[bv, bwT[h][i]], writes=[bp_o[pr]], sig=(b == nb - 1))
            k = (qb // 4) % 2
            for (h, pr, hh, P) in heads:
                S.op("dve", lambda e: e.tensor_copy(out=oT[pr][k][P, (qb % 4) * 128:(qb % 4 + 1) * 128], in_=p_o[pr][P, :]),
                     reads=[bp_o[pr]], writes=[boT[pr][k]])
            if qb % 4 == 3 or qb == nblk - 1:
                q0 = 4 * (qb // 4)
                n = (qb - q0 + 1) * 128
                for pr in range(2):
                    S.dma("sp", MT[pr * 128:(pr + 1) * 128, q0 * 128:q0 * 128 + n], oT[pr][k][:, 0:n], boT[pr][k],
                          reads=[boT[pr][k]], writes=[bMT])
        S.barrier()
        for b in [bq, bk, bv] + boT[0] + boT[1]:
            S.release(b)


def dil_bias_host(rel_bias):
    out = np.empty((12, 128, 2, 128), np.float32)
    kj = np.arange(128)[:, None]
    q = np.arange(128)[None, :]
    for g, r in enumerate((1, 4, 16)):
        for part, dist in ((1, q - kj), (0, q + 128 - kj)):
            valid = (dist >= 0) & (dist <= 128)
            dd = np.maximum(dist, 0) * r
            d = np.maximum(dd, 1).astype(np.float32)
            large = 16 + (np.log(d / np.float32(16)) / np.float32(np.log(2048 / 16)) * np.float32(16)).astype(np.int32)
            large = np.minimum(large, 31)
            bucket = np.where(dd < 16, dd, large)
            for j in range(4):
                hd = 4 * g + j
                out[hd, :, part, :] = np.where(valid, rel_bias[bucket, hd], np.float32(-30000.0))
    return out


def dil_attn_phase(nc, S, QT, KT, V, MT, dbias, bQT, bKT, bV, bMT, ntok):
    with ExitStack() as es:
        sb = lambda name, shape, dt: es.enter_context(nc.sbuf_tensor(_uniq(name), shape, dt))
        ps = lambda name, shape, dt: es.enter_context(nc.psum_tensor(_uniq(name), shape, dt))
        qT = sb("qT", [128, 2, ntok], BF16)
        kT = sb("kT", [128, 2, ntok], BF16)
        v = sb("v", [128, ntok // 128, 256], BF16)
        bias = sb("bias", [128, 4, 256], F32)
        onesb = sb("onesb", [128, 64], BF16)
        Nacc = sb("Nacc", [128, 2, ntok], F32)
        Dacc = sb("Dacc", [128, 2, ntok], F32)
        s_ = [sb("s%d" % i, [128, 256], F32) for i in range(3)]
        pT_ = [sb("pT%d" % i, [128, 2, 128], BF16) for i in range(3)]
        ob = [sb("ob%d" % i, [128, 1024], BF16) for i in range(2)]
        p_s = [ps("p_s%d" % i, [128, 2, 128], F32) for i in range(2)]
        p_n = [ps("p_n%d" % i, [128, 128], F32) for i in range(2)]
        p_d = [ps("p_d%d" % i, [128, 128], F32) for i in range(2)]
        p_pad = [ps("p_pad%d" % i, [128, 256], F32) for i in range(0)]

        bq, bk, bv, bbias = S.dbuf("qT"), S.dbuf("kT"), S.dbuf("v"), S.dbuf("bias")
        bob = [S.dbuf("ob") for _ in range(2)]
        bones, bN, bD = S.buf("ones"), S.buf("N"), S.buf("D")
        bs = [S.buf("s") for _ in range(3)]
        bpT = [S.buf("pT") for _ in range(3)]
        bp_s = [S.pbuf("ps") for _ in range(2)]
        bp_n = [S.pbuf("pn") for _ in range(2)]
        bp_d = [S.pbuf("pd") for _ in range(2)]

        S.op("pool", lambda e: e.memset(onesb[:], 1.0), writes=[bones])
        u = 0
        for g, r in enumerate((1, 4, 16)):
            L = ntok // r
            nb = L // 128
            for pr in range(2):
                r0 = 256 + (2 * g + pr) * 128
                S.dma("sp", qT[:, pr, :], QT[r0:r0 + 128, 0:ntok], bq, reads=[bQT], writes=[bq])
                S.dma("sp", kT[:, pr, :], KT[r0:r0 + 128, 0:ntok], bk, reads=[bKT], writes=[bk])
            vsrc = V[0:ntok, 256 + g * 256:256 + (g + 1) * 256].rearrange("(n i c) f -> c i n f", i=128, c=r)
            for c in range(r):
                S.dma("sp", v[:, c * nb:(c + 1) * nb, :], vsrc[c], bv, reads=[bV], writes=[bv])
            for j in range(4):
                S.dma("sp", bias[:, j, :], dbias[4 * g + j].rearrange("k a q -> k (a q)"), bbias, writes=[bbias])
            for j in range(4):
                pr, hh = j // 2, j % 2
                P = slice(64 * hh, 64 * hh + 64)
                for c in range(r):
                    for n in range(nb):
                        def tok(nn):
                            st = c + r * 128 * nn
                            return slice(st, st + r * 127 + 1, r)
                        i = u % 3
                        pss, bpss = p_s[u % 2], bp_s[u % 2]
                        pn, bpn = p_n[u % 2], bp_n[u % 2]
                        pd, bpd = p_d[u % 2], bp_d[u % 2]
                        u += 1
                        a0 = 0 if n > 0 else 1
                        if n > 0:
                            S.op("pe", lambda e: e.matmul(pss[:, 0, :], lhsT=kT[P, pr, tok(n - 1)], rhs=qT[P, pr, tok(n)],
                                                          start=True, stop=True), reads=[bq, bk], writes=[bpss], sig=False)
                        S.op("pe", lambda e: e.matmul(pss[:, 1, :], lhsT=kT[P, pr, tok(n)], rhs=qT[P, pr, tok(n)],
                                                      start=True, stop=True), reads=[bq, bk], writes=[bpss])
                        S.op("dve", lambda e: e.tensor_tensor(out=s_[i][:, a0 * 128:256], in0=pss[:, a0:2, :],
                                                              in1=bias[:, j, a0 * 128:256], op=ALU.add),
                             reads=[bpss, bbias], writes=[bs[i]])
                        S.op("act", lambda e: e.activation(out=pT_[i][:, a0:2, :], in_=s_[i][:, a0 * 128:256], func=AF.Exp),
                             reads=[bs[i]], writes=[bpT[i]])
                        for a in range(a0, 2):
                            S.op("pe", lambda e: e.matmul(pn[P, :], lhsT=v[:, c * nb + n - 1 + a, j * 64:(j + 1) * 64],
                                                          rhs=pT_[i][:, a, :], start=(a == a0), stop=(a == 1)),
                                 reads=[bv, bpT[i]], writes=[bpn], sig=(a == 1))
                        for a in range(a0, 2):
                            S.op("pe", lambda e: e.matmul(pd[P, :], lhsT=onesb[:, :], rhs=pT_[i][:, a, :],
                                                          start=(a == a0), stop=(a == 1)),
                                 reads=[bones, bpT[i]], writes=[bpd], sig=(a == 1))
                        if g == 0:
                            S.op("act", lambda e: e.copy(out=Nacc[P, pr, tok(n)], in_=pn[P, :]), reads=[bpn], writes=[bN])
                            S.op("dve", lambda e: e.tensor_copy(out=Dacc[P, pr, tok(n)], in_=pd[P, :]), reads=[bpd], writes=[bD])
                        else:
                            S.op("dve", lambda e: e.tensor_tensor(out=Nacc[P, pr, tok(n)], in0=pn[P, :], in1=Nacc[P, pr, tok(n)],
                                                                  op=ALU.add), reads=[bpn, bN], writes=[bN])
                            S.op("dve", lambda e: e.tensor_tensor(out=Dacc[P, pr, tok(n)], in0=pd[P, :], in1=Dacc[P, pr, tok(n)],
                                                                  op=ALU.add), reads=[bpd, bD], writes=[bD])
        k = 0
        for pr in range(2):
            for c0 in range(0, ntok, 1024):
                n = min(1024, ntok - c0)
                S.op("dve", lambda e: e.reciprocal(out=Dacc[:, pr, c0:c0 + n], in_=Dacc[:, pr, c0:c0 + n]), reads=[bD], writes=[bD])
                S.op("dve", lambda e: e.tensor_tensor(out=ob[k % 2][:, 0:n], in0=Nacc[:, pr, c0:c0 + n], in1=Dacc[:, pr, c0:c0 + n],
                                                      op=ALU.mult), reads=[bN, bD], writes=[bob[k % 2]])
                S.dma("sp", MT[256 + pr * 128:256 + (pr + 1) * 128, c0:c0 + n], ob[k % 2][:, 0:n], bob[k % 2],
                      reads=[bob[k % 2]], writes=[bMT])
                k += 1
        S.barrier()
        for b in [bq, bk, bv, bbias] + bob:
            S.release(b)


def out_proj_phase(nc, S, x_in, x_out, xin_b, xout_b, MT, bMT, w_out, kdim, ntok):
    G = 512
    ng = ntok // G
    kc = kdim // 128
    with ExitStack() as es:
        sb = lambda name, shape, dt: es.enter_context(nc.sbuf_tensor(_uniq(name), shape, dt))
        ps = lambda name, shape, dt: es.enter_context(nc.psum_tensor(_uniq(name), shape, dt))
        Wo = sb("Wo", [128, kc, D], BF16)
        mT = [sb("mT%d" % i, [128, kc, G], BF16) for i in range(2)]
        xt = [sb("xt%d" % i, [128, D], F32) for i in range(3)]
        ot = [sb("ot%d" % i, [128, D], F32) for i in range(2)]
        p_y = [ps("p_y%d" % i, [128, 512], F32) for i in range(4)]
        bWo = S.dbuf("Wo")
        bmT = [S.dbuf("mT") for _ in range(2)]
        bxt = [S.dbuf("xt") for _ in range(3)]
        bot = [S.dbuf("ot") for _ in range(2)]
        bp_y = [S.pbuf("py") for _ in range(4)]
        w_v = w_out.rearrange("(c p) f -> p c f", p=128)
        for c in range(kc):
            S.dma("pool", Wo[:, c, :], w_v[:, c, :], bWo, writes=[bWo])
        k = 0
        for g in range(ng):
            m, bm = mT[g % 2], bmT[g % 2]
            for c in range(kc):
                S.dma("sp", m[:, c, :], MT[c * 128:(c + 1) * 128, g * G:(g + 1) * G], bm, reads=[bMT], writes=[bm])
            for s in range(4):
                t0 = g * G + s * 128
                x_, bx = xt[k % 3], bxt[k % 3]
                o_, bo = ot[k % 2], bot[k % 2]
                S.dma("sp", x_[:], x_in[t0:t0 + 128, :], bx, reads=[xin_b], writes=[bx])
                for hh in range(2):
                    py, bpy = p_y[(2 * k + hh) % 4], bp_y[(2 * k + hh) % 4]
                    for c in range(kc):
                        S.op("pe", lambda e: e.matmul(py[:], lhsT=m[:, c, s * 128:(s + 1) * 128], rhs=Wo[:, c, hh * 512:(hh + 1) * 512],
                                                      start=(c == 0), stop=(c == kc - 1)),
                             reads=[bm, bWo], writes=[bpy], sig=(c == kc - 1))
                    S.op("dve", lambda e: e.tensor_tensor(out=o_[:, hh * 512:(hh + 1) * 512], in0=py[:], in1=x_[:, hh * 512:(hh + 1) * 512],
                                                          op=ALU.add), reads=[bpy, bx], writes=[bo])
                S.dma("sp", x_out[t0:t0 + 128, :], o_[:], bo, reads=[bo], writes=[xout_b])
                k += 1
        S.barrier()
        for b in [bWo] + bmT + bxt + bot:
            S.release(b)


C0 = float(np.exp(-0.5))


class RwPrep:
    def __init__(self, nc, S, es, G, mix_ids, x_in, xin_b, gain_row, mix, eps=1e-6):
        sb = lambda name, shape, dt: es.enter_context(nc.sbuf_tensor(_uniq(name), shape, dt))
        ps = lambda name, shape, dt: es.enter_context(nc.psum_tensor(_uniq(name), shape, dt))
        self.nc, self.S, self.G, self.mix_ids, self.x_in, self.xin_b, self.eps = nc, S, G, mix_ids, x_in, xin_b, eps
        self.gB = sb("gB", [128, D], F32)
        self.identf = sb("identf", [128, 128], F32)
        self.nh = sb("nh", [128, 1], F32)
        self.mixc = sb("mixc", [128, 6, 8], F32)
        self.xt = [sb("xt%d" % i, [128, D], F32) for i in range(2)]
        self.hn = [sb("hn%d" % i, [128, D], F32) for i in range(2)]
        self.junk = sb("junk", [128, D], BF16)
        self.ss = [sb("ss%d" % i, [128, 1], F32) for i in range(2)]
        self.rs = [sb("rs%d" % i, [128, 1], F32) for i in range(2)]
        self.hT = [sb("hT%d" % i, [128, 8, G + 1], F32) for i in range(2)]
        self.xx = [sb("xx%d" % i, [128, G], F32) for i in range(2)]
        self.xm = {i: sb("xm%d" % i, [128, 8, G], BF16) for i in mix_ids}
        self.p_tr = [ps("p_tr%d" % i, [128, 4, 128], F32) for i in range(2)]
        self.bgB, self.bmixc = S.dbuf("gB"), S.dbuf("mixc")
        self.bxt = [S.dbuf("xt") for _ in range(2)]
        self.bhn = [S.buf("hn") for _ in range(2)]
        self.bident, self.bnh, self.bjunk = S.buf("identf"), S.buf("nh"), S.buf("junk")
        self.bss = [S.buf("ss") for _ in range(2)]
        self.brs = [S.buf("rs") for _ in range(2)]
        self.bhT = [S.buf("hT") for _ in range(2)]
        self.bxx = [S.buf("xx") for _ in range(2)]
        self.bxm = {i: S.buf("xm") for i in mix_ids}
        self.bp_tr = [S.pbuf("ptr") for _ in range(2)]
        S.op("pool", lambda e: e.memset(self.nh[:], -0.5), writes=[self.bnh])
        make_ident(nc, S, self.identf, self.bident)
        S.dma("sp", self.gB[:], gain_row.to_broadcast([128, D]), self.bgB, writes=[self.bgB])
        for i in range(6):
            S.dma("sp", self.mixc[:, i, :], mix[i:i + 1, :].rearrange("o (c p) -> p (o c)", p=128), self.bmixc, writes=[self.bmixc], slow=True)
        S.op("dve", lambda e: e.memset(self.hT[1][:, :, G:G + 1], 0.0), writes=[self.bhT[1]])
        self.dsems = [self.bgB, self.bmixc] + self.bxt

    def group(self, g):
        nc, S, G = self.nc, self.S, self.G
        hT, bhT = self.hT[g % 2], self.bhT[g % 2]
        hTp, bhTp = self.hT[(g + 1) % 2], self.bhT[(g + 1) % 2]
        S.op("pool", lambda e: e.tensor_copy(out=hT[:, :, 0:1], in_=hTp[:, :, G:G + 1]), reads=[bhTp], writes=[bhT])
        k = 0
        for s in range(G // 128):
            t0 = g * G + s * 128
            x_, bx = self.xt[s % 2], self.bxt[s % 2]
            h_, bh = self.hn[s % 2], self.bhn[s % 2]
            ss, bss, rs, brs = self.ss[s % 2], self.bss[s % 2], self.rs[s % 2], self.brs[s % 2]
            S.dma("sp", x_[:], self.x_in[t0:t0 + 128, :], bx, reads=[self.xin_b], writes=[bx])
            S.op("dve", lambda e: e.scalar_tensor_tensor(out=self.junk[:], in0=x_[:], scalar=1.0, in1=x_[:], op0=ALU.mult, op1=ALU.mult,
                                                         accum_out=ss[:]), reads=[bx], writes=[self.bjunk, bss])
            S.op("dve", lambda e: e.tensor_scalar(out=ss[:], in0=ss[:], scalar1=1.0 / D, scalar2=self.eps, op0=ALU.mult, op1=ALU.add),
                 reads=[bss], writes=[bss])
            S.op("pool", lambda e: e.tensor_tensor(out=rs[:], in0=ss[:], in1=self.nh[:], op=ALU.pow), reads=[bss, self.bnh], writes=[brs])
            S.op("dve", lambda e: e.scalar_tensor_tensor(out=h_[:], in0=x_[:], scalar=rs[:], in1=self.gB[:], op0=ALU.mult, op1=ALU.mult),
                 reads=[bx, brs, self.bgB], writes=[bh])
            for half in range(2):
                pt, bpt = self.p_tr[k % 2], self.bp_tr[k % 2]
                k += 1
                for c4 in range(4):
                    c = half * 4 + c4
                    S.op("pe", lambda e: e.transpose(pt[:, c4, :], h_[:, c * 128:(c + 1) * 128], self.identf[:]),
                         reads=[bh, self.bident], writes=[bpt], sig=(c4 == 3))
                S.op("act", lambda e: e.copy(out=hT[:, half * 4:half * 4 + 4, 1 + s * 128:1 + (s + 1) * 128], in_=pt[:]),
                     reads=[bpt], writes=[bhT])
        for c in range(8):
            xx, bxx = self.xx[c % 2], self.bxx[c % 2]
            S.op("dve", lambda e: e.tensor_tensor(out=xx[:], in0=hT[:, c, 0:G], in1=hT[:, c, 1:G + 1], op=ALU.subtract),
                 reads=[bhT], writes=[bxx])
            for n, i in enumerate(self.mix_ids):
                S.op("dve", lambda e: e.scalar_tensor_tensor(out=self.xm[i][:, c, :], in0=xx[:], scalar=self.mixc[:, i, c:c + 1],
                                                             in1=hT[:, c, 1:G + 1], op0=ALU.mult, op1=ALU.add),
                     reads=[bxx, bhT, self.bmixc], writes=[self.bxm[i]])

    def release(self):
        for b in self.dsems:
            self.S.release(b)


def col_load(S, dst, src_row, track):
    S.dma("sp", dst, src_row.rearrange("o (c p) -> p (o c)", p=128), track, writes=[track], slow=True)


def rwkv_fm_phase(nc, S, x_in, xin_b, gain_row, mix, w0, w1, w2, a0, a1, a2, kk_, ka_, w_r, w_k,
                  RtT, KtT, AtT, BtT, WC, bouts, ntok):
    G = 512
    ng = ntok // G
    with ExitStack() as es:
        sb = lambda name, shape, dt: es.enter_context(nc.sbuf_tensor(_uniq(name), shape, dt))
        ps = lambda name, shape, dt: es.enter_context(nc.psum_tensor(_uniq(name), shape, dt))
        P = RwPrep(nc, S, es, G, (0, 1, 2, 4), x_in, xin_b, gain_row, mix)
        Wr = sb("Wr", [128, 8, D], BF16)
        Wk = sb("Wk", [128, 8, D], BF16)
        W1 = sb("W1", [128, 8, 64], BF16)
        A1 = sb("A1", [128, 8, 64], BF16)
        W2 = sb("W2", [64, D], BF16)
        A2 = sb("A2", [64, D], BF16)
        cols = sb("cols", [128, 4, 8], F32)
        bones = sb("bones", [128, 128], BF16)
        rmask = sb("rmask", [128, G], F32)
        tiny = sb("tiny", [128, 1], F32)
        tw = sb("tw", [64, G], BF16)
        ta = sb("ta", [64, G], BF16)
        names = ["sgu", "av", "kk0", "lnk", "rk", "kkn", "t1", "kp", "csg", "cse", "eW", "eWi", "eWe", "t2"]
        T = {n: sb(n, [128, G], F32) for n in names}
        sqk = sb("sqk", [128, G], BF16)
        wc = [sb("wc%d" % i, [128, G // 64], F32) for i in range(2)]
        ob = [sb("ob%d" % i, [128, G], BF16) for i in range(8)]
        p_r = ps("p_r", [128, G], F32)
        p_k = ps("p_k", [128, G], F32)
        p_u = ps("p_u", [128, G], F32)
        p_a = ps("p_a", [128, G], F32)
        p_ss = ps("p_ss", [128, G], F32)
        p_t = ps("p_t", [128, G], F32)
        bW = S.dbuf("W")
        bcols = S.dbuf("cols")
        bwc = [S.dbuf("wc") for _ in range(2)]
        bob = [S.dbuf("ob") for _ in range(8)]
        B = {n: S.buf(n) for n in names + ["sqk", "tw", "ta", "bones", "rmask"]}
        B.update({n: S.pbuf(n) for n in ["p_r", "p_k", "p_u", "p_a", "p_ss", "p_t"]})
        for c in range(8):
            S.dma("pool", Wr[:, c, :], w_r.rearrange("(c p) f -> p c f", p=128)[:, c, :], bW, writes=[bW])
            S.dma("pool", Wk[:, c, :], w_k.rearrange("(c p) f -> p c f", p=128)[:, c, :], bW, writes=[bW])
        S.dma("pool", W1[:], w1.rearrange("(c p) f -> p c f", p=128), bW, writes=[bW])
        S.dma("pool", A1[:], a1.rearrange("(c p) f -> p c f", p=128), bW, writes=[bW])
        S.dma("pool", W2[:], w2, bW, writes=[bW])
        S.dma("pool", A2[:], a2, bW, writes=[bW])
        for n, src in enumerate((w0, a0, kk_, ka_)):
            col_load(S, cols[:, n, :], src, bcols)
        S.op("pool", lambda e: e.memset(bones[:], 0.0), writes=[B["bones"]])
        S.op("pool", lambda e: e.memset(bones[0:64, 0:64], 1.0), writes=[B["bones"]])
        S.op("pool", lambda e: e.memset(bones[64:128, 64:128], 1.0), writes=[B["bones"]])
        S.op("pool", lambda e: e.memset(rmask[:], 1.0), writes=[B["rmask"]])
        S.op("pool", lambda e: e.memset(rmask[:, 0:G:64], 0.0), writes=[B["rmask"]])
        S.op("pool", lambda e: e.memset(tiny[:], 1e-18), writes=[B["rmask"]])
        RB, KB, AB, BB, WCB = bouts
        no = 0
        for g in range(ng):
            P.group(g)
            t0 = g * G
            xr, xw, xk, xa = P.xm[0], P.xm[1], P.xm[2], P.xm[4]
            bxr, bxw, bxk, bxa = P.bxm[0], P.bxm[1], P.bxm[2], P.bxm[4]
            for c in range(8):
                S.op("pe", lambda e: e.matmul(p_t[0:64, :], lhsT=W1[:, c, :], rhs=xw[:, c, :], start=(c == 0), stop=(c == 7)),
                     reads=[bW, bxw], writes=[B["p_t"]], sig=(c == 7))
            S.op("act", lambda e: e.activation(out=tw[:], in_=p_t[0:64, :], func=AF.Tanh), reads=[B["p_t"]], writes=[B["tw"]])
            for c in range(8):
                S.op("pe", lambda e: e.matmul(p_t[0:64, :], lhsT=A1[:, c, :], rhs=xa[:, c, :], start=(c == 0), stop=(c == 7)),
                     reads=[bW, bxa], writes=[B["p_t"]], sig=(c == 7))
            S.op("act", lambda e: e.copy(out=ta[:], in_=p_t[0:64, :]), reads=[B["p_t"]], writes=[B["ta"]])
            for cc in range(8):
                fs = slice(cc * 128, (cc + 1) * 128)
                for c in range(8):
                    S.op("pe", lambda e: e.matmul(p_r[:], lhsT=Wr[:, c, fs], rhs=xr[:, c, :], start=(c == 0), stop=(c == 7)),
                         reads=[bW, bxr], writes=[B["p_r"]], sig=(c == 7))
                for c in range(8):
                    S.op("pe", lambda e: e.matmul(p_k[:], lhsT=Wk[:, c, fs], rhs=xk[:, c, :], start=(c == 0), stop=(c == 7)),
                         reads=[bW, bxk], writes=[B["p_k"]], sig=(c == 7))
                S.op("pe", lambda e: e.matmul(p_u[:], lhsT=W2[:, fs], rhs=tw[:], start=True, stop=True), reads=[bW, B["tw"]], writes=[B["p_u"]])
                S.op("pe", lambda e: e.matmul(p_a[:], lhsT=A2[:, fs], rhs=ta[:], start=True, stop=True), reads=[bW, B["ta"]], writes=[B["p_a"]])
                S.op("act", lambda e: e.activation(out=T["sgu"][:], in_=p_u[:], func=AF.Sigmoid, bias=cols[:, 0, cc:cc + 1]),
                     reads=[B["p_u"], bcols], writes=[B["sgu"]])
                S.op("act", lambda e: e.activation(out=T["av"][:], in_=p_a[:], func=AF.Sigmoid, bias=cols[:, 1, cc:cc + 1]),
                     reads=[B["p_a"], bcols], writes=[B["av"]])
                S.op("dve", lambda e: e.tensor_scalar(out=T["kk0"][:], in0=p_k[:], scalar1=cols[:, 2, cc:cc + 1], scalar2=None, op0=ALU.mult),
                     reads=[B["p_k"], bcols], writes=[B["kk0"]])
                S.op("act", lambda e: e.activation(out=sqk[:], in_=T["kk0"][:], func=AF.Square), reads=[B["kk0"]], writes=[B["sqk"]])
                S.op("pe", lambda e: e.matmul(p_ss[:], lhsT=bones[:], rhs=sqk[:], start=True, stop=True), reads=[B["bones"], B["sqk"]],
                     writes=[B["p_ss"]])
                S.op("act", lambda e: e.activation(out=T["lnk"][:], in_=p_ss[:], func=AF.Ln, bias=tiny[:]), reads=[B["p_ss"], B["rmask"]],
                     writes=[B["lnk"]])
                S.op("act", lambda e: e.activation(out=T["rk"][:], in_=T["lnk"][:], func=AF.Exp, scale=-0.5), reads=[B["lnk"]], writes=[B["rk"]])
                S.op("dve", lambda e: e.tensor_tensor(out=T["kkn"][:], in0=T["kk0"][:], in1=T["rk"][:], op=ALU.mult),
                     reads=[B["kk0"], B["rk"]], writes=[B["kkn"]])
                S.op("dve", lambda e: e.tensor_scalar(out=T["t1"][:], in0=T["av"][:], scalar1=-1.0, scalar2=cols[:, 3, cc:cc + 1],
                                                      op0=ALU.add, op1=ALU.mult), reads=[B["av"], bcols], writes=[B["t1"]])
                S.op("dve", lambda e: e.scalar_tensor_tensor(out=T["kp"][:], in0=T["t1"][:], scalar=1.0, in1=p_k[:], op0=ALU.add, op1=ALU.mult),
                     reads=[B["t1"], B["p_k"]], writes=[B["kp"]])
                S.op("dve", lambda e: e.tensor_tensor_scan(out=T["csg"][:], data0=rmask[:], data1=T["sgu"][:], initial=0.0,
                                                           op0=ALU.mult, op1=ALU.add), reads=[B["rmask"], B["sgu"]], writes=[B["csg"]])
                S.op("dve", lambda e: e.tensor_tensor(out=T["cse"][:], in0=T["csg"][:], in1=T["sgu"][:], op=ALU.subtract),
                     reads=[B["csg"], B["sgu"]], writes=[B["cse"]])
                S.op("act", lambda e: e.activation(out=T["eW"][:], in_=T["csg"][:], func=AF.Exp, scale=-C0), reads=[B["csg"]], writes=[B["eW"]])
                S.op("act", lambda e: e.activation(out=T["eWi"][:], in_=T["csg"][:], func=AF.Exp, scale=C0), reads=[B["csg"]], writes=[B["eWi"]])
                S.op("act", lambda e: e.activation(out=T["eWe"][:], in_=T["cse"][:], func=AF.Exp, scale=-C0), reads=[B["cse"]], writes=[B["eWe"]])
                S.op("dve", lambda e: e.tensor_tensor(out=T["t2"][:], in0=T["kkn"][:], in1=T["av"][:], op=ALU.mult),
                     reads=[B["kkn"], B["av"]], writes=[B["t2"]])
                outs = []
                o, bo = ob[no % 8], bob[no % 8]; no += 1
                S.op("dve", lambda e: e.tensor_tensor(out=o[:], in0=p_r[:], in1=T["eW"][:], op=ALU.mult), reads=[B["p_r"], B["eW"]], writes=[bo])
                outs.append((o, bo, RtT, RB))
                o, bo = ob[no % 8], bob[no % 8]; no += 1
                S.op("dve", lambda e: e.tensor_tensor(out=o[:], in0=T["kp"][:], in1=T["eWi"][:], op=ALU.mult), reads=[B["kp"], B["eWi"]], writes=[bo])
                outs.append((o, bo, KtT, KB))
                o, bo = ob[no % 8], bob[no % 8]; no += 1
                S.op("dve", lambda e: e.scalar_tensor_tensor(out=o[:], in0=T["kkn"][:], scalar=-1.0, in1=T["eWe"][:], op0=ALU.mult, op1=ALU.mult),
                     reads=[B["kkn"], B["eWe"]], writes=[bo])
                outs.append((o, bo, AtT, AB))
                o, bo = ob[no % 8], bob[no % 8]; no += 1
                S.op("dve", lambda e: e.tensor_tensor(out=o[:], in0=T["t2"][:], in1=T["eWi"][:], op=ALU.mult), reads=[B["t2"], B["eWi"]], writes=[bo])
                outs.append((o, bo, BtT, BB))
                for (o, bo, dst, bdst) in outs:
                    S.dma("sp", dst[fs, t0:t0 + G], o[:], bo, reads=[bo], writes=[bdst])
                w_, bw_ = wc[cc % 2], bwc[cc % 2]
                S.op("dve", lambda e: e.tensor_copy(out=w_[:], in_=T["eW"][:, 63:G:64]), reads=[B["eW"]], writes=[bw_])
                S.dma("sp", WC[fs, g * (G // 64):(g + 1) * (G // 64)], w_[:], bw_, reads=[bw_], writes=[WCB])
        S.barrier()
        P.release()
        for b in [bW, bcols] + bwc + bob:
            S.release(b)


def rwkv_tm_phase(nc, S, x_in, xin_b, gain_row, mix, a0, a1, a2, g1, g2, ka_, rk_, w_r, w_k, w_v,
                  Vtok, BV, Gt, bouts, ntok):
    G = 256
    ng = ntok // G
    with ExitStack() as es:
        sb = lambda name, shape, dt: es.enter_context(nc.sbuf_tensor(_uniq(name), shape, dt))
        ps = lambda name, shape, dt: es.enter_context(nc.psum_tensor(_uniq(name), shape, dt))
        P = RwPrep(nc, S, es, G, (0, 2, 3, 4, 5), x_in, xin_b, gain_row, mix)
        Wr = sb("Wr", [128, 8, D], BF16)
        Wk = sb("Wk", [128, 8, D], BF16)
        Wv = sb("Wv", [128, 8, D], BF16)
        A1 = sb("A1", [128, 8, 64], BF16)
        A2 = sb("A2", [64, D], BF16)
        G1 = sb("G1", [128, 8, 160], BF16)
        G2a = sb("G2a", [128, D], BF16)
        G2b = sb("G2b", [32, D], BF16)
        a0B = sb("a0B", [128, D], F32)
        kaB = sb("kaB", [128, D], F32)
        rkB = sb("rkB", [128, D], F32)
        ta = sb("ta", [64, G], BF16)
        sg1a = sb("sg1a", [128, G], BF16)
        sg1b = sb("sg1b", [32, G], BF16)
        tmp = sb("tmp", [128, D], F32)
        av = sb("av", [128, D], F32)
        t1 = sb("t1", [128, D], F32)
        kp = sb("kp", [128, D], F32)
        tmp2 = sb("tmp2", [128, D], F32)
        tmp3 = sb("tmp3", [128, 16, 64], F32)
        bsum = sb("bsum", [128, 16, 1], F32)
        bvo = [sb("bvo%d" % i, [128, 16, 64], F32) for i in range(2)]
        vto = [sb("vto%d" % i, [128, D], BF16) for i in range(2)]
        gto = [sb("gto%d" % i, [128, D], F32) for i in range(2)]
        p_t = ps("p_t", [128, G], F32)
        pp = [ps("pp%d" % i, [128, 2, 512], F32) for i in range(2)]
        bW, bB = S.dbuf("W"), S.dbuf("B")
        bbvo = [S.dbuf("bvo") for _ in range(2)]
        bvto = [S.dbuf("vto") for _ in range(2)]
        bgto = [S.dbuf("gto") for _ in range(2)]
        B = {n: S.buf(n) for n in ["ta", "sg1a", "sg1b", "tmp", "av", "t1", "kp", "tmp2", "tmp3", "bsum"]}
        B.update({n: S.pbuf(n) for n in ["p_t", "pp0", "pp1"]})
        bpp = [B["pp0"], B["pp1"]]
        for c in range(8):
            for (W_, w_) in ((Wr, w_r), (Wk, w_k), (Wv, w_v)):
                S.dma("pool", W_[:, c, :], w_.rearrange("(c p) f -> p c f", p=128)[:, c, :], bW, writes=[bW])
        S.dma("pool", A1[:], a1.rearrange("(c p) f -> p c f", p=128), bW, writes=[bW])
        S.dma("pool", G1[:], g1.rearrange("(c p) f -> p c f", p=128), bW, writes=[bW])
        S.dma("pool", A2[:], a2, bW, writes=[bW])
        S.dma("pool", G2a[:], g2[0:128, :], bW, writes=[bW])
        S.dma("pool", G2b[:], g2[128:160, :], bW, writes=[bW])
        for (t_, src) in ((a0B, a0), (kaB, ka_), (rkB, rk_)):
            S.dma("sp", t_[:], src.to_broadcast([128, D]), bB, writes=[bB])
        VB, BVB, GB = bouts
        npp = 0
        k = 0
        for g in range(ng):
            P.group(g)
            xr, xk, xv, xa, xg = P.xm[0], P.xm[2], P.xm[3], P.xm[4], P.xm[5]
            bxr, bxk, bxv, bxa, bxg = P.bxm[0], P.bxm[2], P.bxm[3], P.bxm[4], P.bxm[5]
            for c in range(8):
                S.op("pe", lambda e: e.matmul(p_t[0:64, :], lhsT=A1[:, c, :], rhs=xa[:, c, :], start=(c == 0), stop=(c == 7)),
                     reads=[bW, bxa], writes=[B["p_t"]], sig=(c == 7))
            S.op("act", lambda e: e.copy(out=ta[:], in_=p_t[0:64, :]), reads=[B["p_t"]], writes=[B["ta"]])
            for c in range(8):
                S.op("pe", lambda e: e.matmul(p_t[:, :], lhsT=G1[:, c, 0:128], rhs=xg[:, c, :], start=(c == 0), stop=(c == 7)),
                     reads=[bW, bxg], writes=[B["p_t"]], sig=(c == 7))
            S.op("act", lambda e: e.activation(out=sg1a[:], in_=p_t[:, :], func=AF.Sigmoid), reads=[B["p_t"]], writes=[B["sg1a"]])
            for c in range(8):
                S.op("pe", lambda e: e.matmul(p_t[0:32, :], lhsT=G1[:, c, 128:160], rhs=xg[:, c, :], start=(c == 0), stop=(c == 7)),
                     reads=[bW, bxg], writes=[B["p_t"]], sig=(c == 7))
            S.op("act", lambda e: e.activation(out=sg1b[:], in_=p_t[0:32, :], func=AF.Sigmoid), reads=[B["p_t"]], writes=[B["sg1b"]])
            for s in range(G // 128):
                ts = slice(s * 128, (s + 1) * 128)
                t0 = g * G + s * 128

                def big(xm_, bxm_, W_):
                    nonlocal npp
                    p, bp = pp[npp % 2], bpp[npp % 2]
                    npp += 1
                    for hh in range(2):
                        for c in range(8):
                            S.op("pe", lambda e: e.matmul(p[:, hh, :], lhsT=xm_[:, c, ts], rhs=W_[:, c, hh * 512:(hh + 1) * 512],
                                                          start=(c == 0), stop=(c == 7)), reads=[bW, bxm_], writes=[bp], sig=(c == 7))
                    return p, bp
                p, bp = pp[npp % 2], bpp[npp % 2]
                npp += 1
                for hh in range(2):
                    S.op("pe", lambda e: e.matmul(p[:, hh, :], lhsT=ta[:, ts], rhs=A2[:, hh * 512:(hh + 1) * 512], start=True, stop=True),
                         reads=[bW, B["ta"]], writes=[bp])
                S.op("dve", lambda e: e.tensor_tensor(out=tmp[:], in0=p[:].rearrange("p a b -> p (a b)"), in1=a0B[:], op=ALU.add),
                     reads=[bp, bB], writes=[B["tmp"]])
                S.op("act", lambda e: e.activation(out=av[:], in_=tmp[:], func=AF.Sigmoid), reads=[B["tmp"]], writes=[B["av"]])
                S.op("dve", lambda e: e.scalar_tensor_tensor(out=t1[:], in0=av[:], scalar=-1.0, in1=kaB[:], op0=ALU.add, op1=ALU.mult),
                     reads=[B["av"], bB], writes=[B["t1"]])
                p, bp = big(xk, bxk, Wk)
                S.op("dve", lambda e: e.scalar_tensor_tensor(out=kp[:], in0=t1[:], scalar=1.0, in1=p[:].rearrange("p a b -> p (a b)"),
                                                             op0=ALU.add, op1=ALU.mult), reads=[B["t1"], bp], writes=[B["kp"]])
                p, bp = big(xr, bxr, Wr)
                S.op("dve", lambda e: e.tensor_tensor(out=tmp2[:], in0=p[:].rearrange("p a b -> p (a b)"), in1=rkB[:], op=ALU.mult),
                     reads=[bp, bB], writes=[B["tmp2"]])
                S.op("dve", lambda e: e.tensor_tensor(out=tmp3[:].rearrange("p a b -> p (a b)"), in0=tmp2[:], in1=kp[:], op=ALU.mult),
                     reads=[B["tmp2"], B["kp"]], writes=[B["tmp3"]])
                S.op("dve", lambda e: e.tensor_reduce(out=bsum[:], in_=tmp3[:], axis=AX.X, op=ALU.add), reads=[B["tmp3"]], writes=[B["bsum"]])
                p, bp = big(xv, bxv, Wv)
                o, bo = bvo[k % 2], bbvo[k % 2]
                S.op("dve", lambda e: e.tensor_tensor(out=o[:], in0=p[:].rearrange("p a (h d) -> p (a h) d", d=64),
                                                      in1=bsum[:].to_broadcast([128, 16, 64]), op=ALU.mult), reads=[bp, B["bsum"]], writes=[bo])
                S.dma("sp", BV[t0:t0 + 128, :], o[:].rearrange("p a b -> p (a b)"), bo, reads=[bo], writes=[BVB])
                o, bo = vto[k % 2], bvto[k % 2]
                S.op("act", lambda e: e.copy(out=o[:], in_=p[:].rearrange("p a b -> p (a b)")), reads=[bp], writes=[bo])
                S.dma("sp", Vtok[t0:t0 + 128, :], o[:], bo, reads=[bo], writes=[VB])
                p, bp = pp[npp % 2], bpp[npp % 2]
                npp += 1
                for hh in range(2):
                    S.op("pe", lambda e: e.matmul(p[:, hh, :], lhsT=sg1a[:, ts], rhs=G2a[:, hh * 512:(hh + 1) * 512], start=True, stop=False),
                         reads=[bW, B["sg1a"]], writes=[bp], sig=False)
                    S.op("pe", lambda e: e.matmul(p[:, hh, :], lhsT=sg1b[:, ts], rhs=G2b[:, hh * 512:(hh + 1) * 512], start=False, stop=True),
                         reads=[bW, B["sg1b"]], writes=[bp])
                o, bo = gto[k % 2], bgto[k % 2]
                S.op("act", lambda e: e.copy(out=o[:], in_=p[:].rearrange("p a b -> p (a b)")), reads=[bp], writes=[bo])
                S.dma("sp", Gt[t0:t0 + 128, :], o[:], bo, reads=[bo], writes=[GB])
                k += 1
        S.barrier()
        P.release()
        for b in [bW, bB] + bbvo + bvto + bgto:
            S.release(b)


def rwkv_scan_phase(nc, S, RtT, KtT, AtT, BtT, WC, Vtok, Ysc, bins, bY, ntok, NI=4):
    nch = ntok // 64
    ngr = nch // 8
    with ExitStack() as es:
        sb = lambda name, shape, dt: es.enter_context(nc.sbuf_tensor(_uniq(name), shape, dt))
        ps = lambda name, shape, dt: es.enter_context(nc.psum_tensor(_uniq(name), shape, dt))
        MU = sb("MU", [128, 128], F32)
        MUI = sb("MUI", [128, 128], F32)
        ML = sb("ML", [128, 128], F32)
        I32 = sb("I32", [128, 128], F32)
        identb = sb("identb", [128, 128], BF16)
        bconst = S.buf("const")
        for (m, chm, pat, op) in ((MU, -1, 1, ALU.is_gt), (MUI, -1, 1, ALU.is_ge), (ML, 1, -1, ALU.is_gt)):
            S.op("pool", lambda e: e.memset(m[:], 1.0), writes=[bconst])
            S.op("pool", lambda e: e.affine_select(out=m[:], in_=m[:], pattern=[[pat, 128]], compare_op=op, fill=0.0, base=0,
                                                   channel_multiplier=chm), reads=[bconst], writes=[bconst])
        make_ident(nc, S, I32, bconst)
        make_ident(nc, S, identb, bconst)

        class Slot:
            pass
        slots = []
        for si in range(NI):
            s = Slot()
            s.i = si
            n_ = lambda x: "%s_%d" % (x, si)
            s.bd = {k: [sb(n_(k) + "_%d" % j, [128, 8, 128], BF16) for j in range(2)] for k in ("A", "B", "K", "R")}
            s.bbd = [S.dbuf(n_("bd0")), S.dbuf(n_("bd1"))]
            s.Vs = [sb(n_("Vs%d" % j), [128, 8, 64], BF16) for j in range(2)]
            s.Yo = [sb(n_("Yo%d" % j), [128, 8, 64], F32) for j in range(2)]
            s.bYo = [S.dbuf(n_("Yo0")), S.dbuf(n_("Yo1"))]
            s.wcs = sb(n_("wcs"), [128, nch], F32)
            s.bwcs = S.dbuf(n_("wcs"))
            s.N = [sb(n_("N%d" % j), [128, 128], F32) for j in range(2)]
            s.P = [sb(n_("P%d" % j), [128, 128], F32) for j in range(2)]
            s.X = [sb(n_("X%d" % j), [128, 128], F32) for j in range(2)]
            s.bN = [S.buf("N") for _ in range(2)]
            s.bP = [S.buf("P") for _ in range(2)]
            s.bX = [S.buf("X") for _ in range(2)]
            for k in ("Mak", "Mrb", "Mrk", "BtT", "KtT"):
                setattr(s, k, sb(n_(k), [128, 128], BF16))
                setattr(s, "b" + k, S.buf(k))
            s.Xs = sb(n_("Xs"), [128, 64], F32)
            s.Ub = sb(n_("Ub"), [128, 64], BF16)
            s.Sw = sb(n_("Sw"), [128, 64], F32)
            s.St = sb(n_("St"), [128, 64], F32)
            s.Sb = sb(n_("Sb"), [128, 64], BF16)
            s.bXs, s.bUb, s.bSw, s.bSt, s.bSb = [S.buf(k) for k in ("Xs", "Ub", "Sw", "St", "Sb")]
            s.psA = ps(n_("psA"), [128, 4, 128], F32)
            s.psB = ps(n_("psB"), [128, 4, 128], F32)
            s.bA, s.bB = S.pbuf("bankA"), S.pbuf("bankB")
            s.ptr = s.psB[:, 3, :].bitcast(BF16)
            s.bptr = s.bB
            for k in ("A", "B", "K", "R"):
                for j in range(2):
                    S.op("pool", lambda e: e.memset(s.bd[k][j][:], 0.0), writes=[s.bbd[j]])
            slots.append(s)

        def PA(s, k):
            return s.psA[:, k, :], s.bA

        def PB(s, k):
            return s.psB[:, k, :], s.bB

        srcs = {"A": AtT, "B": BtT, "K": KtT, "R": RtT}
        bsrc = {"A": bins[2], "B": bins[3], "K": bins[1], "R": bins[0]}
        bWC, bV = bins[4], bins[5]

        def load_group(s, hp, gg):
            j = gg % 2
            t0 = gg * 512
            for k in ("A", "B", "K", "R"):
                for h in range(2):
                    r0 = hp * 128 + h * 64
                    S.dma("sp", s.bd[k][j][h * 64:(h + 1) * 64, :, h * 64:(h + 1) * 64],
                          srcs[k][r0:r0 + 64, t0:t0 + 512].rearrange("p (c j) -> p c j", j=64), s.bbd[j],
                          reads=[bsrc[k]], writes=[s.bbd[j]])
            for h in range(2):
                c0 = hp * 128 + h * 64
                S.dma("sp", s.Vs[j][h * 64:(h + 1) * 64, :, :], Vtok[t0:t0 + 512, c0:c0 + 64].rearrange("(c j) v -> j c v", j=64),
                      s.bbd[j], reads=[bV], writes=[s.bbd[j]])

        for rnd in range(8 // NI):
            hps = [rnd * NI + i for i in range(NI)]
            for s, hp in zip(slots, hps):
                S.dma("sp", s.wcs[:], WC[hp * 128:(hp + 1) * 128, 0:nch], s.bwcs, reads=[bWC], writes=[s.bwcs])
                S.op("pool", lambda e: e.memset(s.St[:], 0.0), writes=[s.bSt])
                S.op("pool", lambda e: e.memset(s.Sb[:], 0.0), writes=[s.bSb])
                load_group(s, hp, 0)
            for gg in range(ngr):
                j = gg % 2
                if gg + 1 < ngr:
                    for s, hp in zip(slots, hps):
                        load_group(s, hp, gg + 1)
                for c in range(8):
                    ch = gg * 8 + c
                    for s in slots:
                        A, Bd, K, R = [s.bd[k][j][:, c, :] for k in ("A", "B", "K", "R")]
                        bb = s.bbd[j]
                        s.p1, s.bp1 = PA(s, 0)
                        S.op("pe", lambda e: e.matmul(s.p1, lhsT=Bd, rhs=A, start=True, stop=True), reads=[bb], writes=[s.bp1], sig=False)
                        s.p2, s.bp2 = PA(s, 1)
                        S.op("pe", lambda e: e.matmul(s.p2, lhsT=A, rhs=Bd, start=True, stop=True), reads=[bb], writes=[s.bp2])
                    for s in slots:
                        S.op("dve", lambda e: e.tensor_tensor(out=s.N[0][:], in0=s.p1, in1=MU[:], op=ALU.mult), reads=[s.bp1, bconst], writes=[s.bN[0]])
                        S.op("dve", lambda e: e.tensor_tensor(out=s.P[0][:], in0=s.p2, in1=ML[:], op=ALU.mult), reads=[s.bp2, bconst], writes=[s.bP[0]])
                        S.op("pool", lambda e: e.tensor_tensor(out=s.X[0][:], in0=s.N[0][:], in1=I32[:], op=ALU.add), reads=[s.bN[0], bconst], writes=[s.bX[0]])
                    for s in slots:
                        A, Bd, K, R = [s.bd[k][j][:, c, :] for k in ("A", "B", "K", "R")]
                        bb = s.bbd[j]
                        trip = (("Mak", K, A, MU), ("Mrb", Bd, R, MUI), ("Mrk", K, R, MUI))
                        for n_, (nm, l_, r_, msk) in enumerate(trip):
                            p, bp = PB(s, n_)
                            S.op("pe", lambda e: e.matmul(p, lhsT=l_, rhs=r_, start=True, stop=True), reads=[bb], writes=[bp], sig=False)
                        S.op("pe", lambda e: e.transpose(s.ptr[:, 0:128], Bd, identb[:]), reads=[bb, bconst], writes=[s.bptr], sig=False)
                        S.op("pe", lambda e: e.transpose(s.ptr[:, 128:256], K, identb[:]), reads=[bb, bconst], writes=[s.bptr])
                        for n_, (nm, l_, r_, msk) in enumerate(trip):
                            p, bp = PB(s, n_)
                            S.op("dve", lambda e: e.tensor_tensor(out=getattr(s, nm)[:], in0=p, in1=msk[:], op=ALU.mult), reads=[bp, bconst],
                                 writes=[getattr(s, "b" + nm)])
                        S.op("act", lambda e: e.copy(out=s.BtT[:], in_=s.ptr[:, 0:128]), reads=[s.bptr], writes=[s.bBtT])
                        S.op("act", lambda e: e.copy(out=s.KtT[:], in_=s.ptr[:, 128:256]), reads=[s.bptr], writes=[s.bKtT])
                    cur = 0
                    for lvl in range(5):
                        nxt = 1 - cur
                        last = (lvl == 4)
                        for s in slots:
                            if not last:
                                s.pq, s.bpq = PA(s, 0)
                                S.op("pe", lambda e: e.matmul(s.pq, lhsT=s.P[cur][:], rhs=s.N[cur][:], start=True, stop=True),
                                     reads=[s.bP[cur], s.bN[cur]], writes=[s.bpq])
                            s.pp_, s.bpp = PA(s, 1)
                            S.op("pe", lambda e: e.matmul(s.pp_, lhsT=s.N[cur][:], rhs=s.P[cur][:], start=True, stop=True),
                                 reads=[s.bP[cur], s.bN[cur]], writes=[s.bpp])
                        for s in slots:
                            if not last:
                                S.op("act", lambda e: e.copy(out=s.N[nxt][:], in_=s.pq), reads=[s.bpq], writes=[s.bN[nxt]])
                            S.op("act", lambda e: e.copy(out=s.P[nxt][:], in_=s.pp_), reads=[s.bpp], writes=[s.bP[nxt]])
                        for s in slots:
                            s.px, s.bpx = PB(s, 0)
                            S.op("pe", lambda e: e.matmul(s.px, lhsT=s.P[nxt][:], rhs=s.X[cur][:], start=True, stop=True),
                                 reads=[s.bP[nxt], s.bX[cur]], writes=[s.bpx])
                        for s in slots:
                            S.op("dve", lambda e: e.tensor_tensor(out=s.X[nxt][:], in0=s.px, in1=s.X[cur][:], op=ALU.add),
                                 reads=[s.bpx, s.bX[cur]], writes=[s.bX[nxt]])
                        cur = nxt
                    TT, bTT = cur, None
                    for s in slots:
                        A = s.bd["A"][j][:, c, :]
                        s.pX, s.bpX = PA(s, 2)
                        S.op("pe", lambda e: e.matmul(s.pX[:, 0:64], lhsT=A, rhs=s.Sb[:], start=True, stop=False),
                             reads=[s.bbd[j], s.bSb], writes=[s.bpX], sig=False)
                        S.op("pe", lambda e: e.matmul(s.pX[:, 0:64], lhsT=s.Mak[:], rhs=s.Vs[j][:, c, :], start=False, stop=True),
                             reads=[s.bMak, s.bbd[j]], writes=[s.bpX])
                        S.op("pool", lambda e: e.tensor_scalar(out=s.Sw[:], in0=s.St[:], scalar1=s.wcs[:, ch:ch + 1], scalar2=None, op0=ALU.mult),
                             reads=[s.bSt, s.bwcs], writes=[s.bSw])
                    for s in slots:
                        S.op("act", lambda e: e.copy(out=s.Xs[:], in_=s.pX[:, 0:64]), reads=[s.bpX], writes=[s.bXs])
                    for s in slots:
                        s.pU, s.bpU = PB(s, 1)
                        S.op("pe", lambda e: e.matmul(s.pU[:, 0:64], lhsT=s.X[TT][:], rhs=s.Xs[:], start=True, stop=True),
                             reads=[s.bX[TT], s.bXs], writes=[s.bpU])
                    for s in slots:
                        S.op("dve", lambda e: e.tensor_copy(out=s.Ub[:], in_=s.pU[:, 0:64]), reads=[s.bpU], writes=[s.bUb])
                    for s in slots:
                        R = s.bd["R"][j][:, c, :]
                        s.pY, s.bpY = PA(s, 0)
                        S.op("pe", lambda e: e.matmul(s.pY[:, 0:64], lhsT=R, rhs=s.Sb[:], start=True, stop=False),
                             reads=[s.bbd[j], s.bSb], writes=[s.bpY], sig=False)
                        S.op("pe", lambda e: e.matmul(s.pY[:, 0:64], lhsT=s.Mrb[:], rhs=s.Ub[:], start=False, stop=False),
                             reads=[s.bMrb, s.bUb], writes=[s.bpY], sig=False)
                        S.op("pe", lambda e: e.matmul(s.pY[:, 0:64], lhsT=s.Mrk[:], rhs=s.Vs[j][:, c, :], start=False, stop=True),
                             reads=[s.bMrk, s.bbd[j]], writes=[s.bpY])
                        s.pS, s.bpS = PA(s, 1)
                        S.op("pe", lambda e: e.matmul(s.pS[:, 0:64], lhsT=s.BtT[:], rhs=s.Ub[:], start=True, stop=False),
                             reads=[s.bBtT, s.bUb], writes=[s.bpS], sig=False)
                        S.op("pe", lambda e: e.matmul(s.pS[:, 0:64], lhsT=s.KtT[:], rhs=s.Vs[j][:, c, :], start=False, stop=True),
                             reads=[s.bKtT, s.bbd[j]], writes=[s.bpS])
                    for s in slots:
                        S.op("act", lambda e: e.copy(out=s.Yo[j][:, c, :], in_=s.pY[:, 0:64]), reads=[s.bpY], writes=[s.bYo[j]])
                        S.op("dve", lambda e: e.scalar_tensor_tensor(out=s.St[:], in0=s.pS[:, 0:64], scalar=s.wcs[:, ch:ch + 1], in1=s.Sw[:],
                                                                     op0=ALU.mult, op1=ALU.add), reads=[s.bpS, s.bwcs, s.bSw], writes=[s.bSt])
                        S.op("act", lambda e: e.copy(out=s.Sb[:], in_=s.St[:]), reads=[s.bSt], writes=[s.bSb])
                for s, hp in zip(slots, hps):
                    for h in range(2):
                        c0 = hp * 128 + h * 64
                        S.dma("sp", Ysc[gg * 512:(gg + 1) * 512, c0:c0 + 64].rearrange("(c j) v -> j c v", j=64),
                              s.Yo[j][h * 64:(h + 1) * 64, :, :], s.bYo[j], reads=[s.bYo[j]], writes=[bY])
        S.barrier()
        for s in slots:
            for b in s.bbd + s.bYo + [s.bwcs]:
                S.release(b)


def rwkv_post_phase(nc, S, Ysc, BV, Gt, lg_row, lb_row, ZT, bins, bZT, ntok, gn_eps=64e-5):
    with ExitStack() as es:
        sb = lambda name, shape, dt: es.enter_context(nc.sbuf_tensor(_uniq(name), shape, dt))
        ps = lambda name, shape, dt: es.enter_context(nc.psum_tensor(_uniq(name), shape, dt))
        lgB = sb("lgB", [128, D], F32)
        lbB = sb("lbB", [128, D], F32)
        ident = sb("ident", [128, 128], BF16)
        nh = sb("nh", [128, 16, 1], F32)
        yt = [sb("yt%d" % i, [128, 16, 64], F32) for i in range(2)]
        bvt = [sb("bvt%d" % i, [128, D], F32) for i in range(2)]
        gt = [sb("gt%d" % i, [128, D], F32) for i in range(2)]
        sm = sb("sm", [128, 16, 1], F32)
        vr = sb("vr", [128, 16, 1], F32)
        rstd = sb("rstd", [128, 16, 1], F32)
        yc = sb("yc", [128, 16, 64], F32)
        sq = sb("sq", [128, 16, 64], F32)
        yn = sb("yn", [128, 16, 64], F32)
        y2 = sb("y2", [128, D], F32)
        zb = [sb("zb%d" % i, [128, D], BF16) for i in range(2)]
        zT = [sb("zT%d" % i, [128, 8, 512], BF16) for i in range(2)]
        p_tr = [ps("p_tr%d" % i, [128, 8, 128], BF16) for i in range(2)]
        bC = S.dbuf("C")
        byt = [S.dbuf("yt") for _ in range(2)]
        bbvt = [S.dbuf("bvt") for _ in range(2)]
        bgt = [S.dbuf("gt") for _ in range(2)]
        bzT = [S.dbuf("zT") for _ in range(2)]
        B = {n: S.buf(n) for n in ["ident", "nh", "sm", "vr", "rstd", "yc", "sq", "yn", "y2", "zb0", "zb1"]}
        bp_tr = [S.pbuf("ptr") for _ in range(2)]
        bYs, bBV, bG = bins
        make_ident(nc, S, ident, B["ident"])
        S.op("pool", lambda e: e.memset(nh[:], -0.5), writes=[B["nh"]])
        S.dma("sp", lgB[:], lg_row.to_broadcast([128, D]), bC, writes=[bC])
        S.dma("sp", lbB[:], lb_row.to_broadcast([128, D]), bC, writes=[bC])
        nt = ntok // 128
        for t in range(nt):
            i = t % 2
            t0 = t * 128
            S.dma("sp", yt[i][:].rearrange("p a b -> p (a b)"), Ysc[t0:t0 + 128, :], byt[i], reads=[bYs], writes=[byt[i]])
            S.dma("sp", bvt[i][:], BV[t0:t0 + 128, :], bbvt[i], reads=[bBV], writes=[bbvt[i]])
            S.dma("sp", gt[i][:], Gt[t0:t0 + 128, :], bgt[i], reads=[bG], writes=[bgt[i]])
            y3 = yt[i]
            S.op("dve", lambda e: e.tensor_reduce(out=sm[:], in_=y3[:], axis=AX.X, op=ALU.add), reads=[byt[i]], writes=[B["sm"]])
            S.op("dve", lambda e: e.tensor_scalar(out=sm[:], in0=sm[:], scalar1=1.0 / 64, scalar2=None, op0=ALU.mult), reads=[B["sm"]], writes=[B["sm"]])
            S.op("dve", lambda e: e.tensor_tensor(out=yc[:], in0=y3[:], in1=sm[:].to_broadcast([128, 16, 64]), op=ALU.subtract),
                 reads=[byt[i], B["sm"]], writes=[B["yc"]])
            S.op("dve", lambda e: e.tensor_tensor(out=sq[:], in0=yc[:], in1=yc[:], op=ALU.mult), reads=[B["yc"]], writes=[B["sq"]])
            S.op("dve", lambda e: e.tensor_reduce(out=vr[:], in_=sq[:], axis=AX.X, op=ALU.add), reads=[B["sq"]], writes=[B["vr"]])
            S.op("dve", lambda e: e.tensor_scalar(out=vr[:], in0=vr[:], scalar1=1.0 / 64, scalar2=gn_eps, op0=ALU.mult, op1=ALU.add),
                 reads=[B["vr"]], writes=[B["vr"]])
            S.op("pool", lambda e: e.tensor_tensor(out=rstd[:], in0=vr[:], in1=nh[:], op=ALU.pow), reads=[B["vr"], B["nh"]], writes=[B["rstd"]])
            S.op("dve", lambda e: e.tensor_tensor(out=yn[:], in0=yc[:], in1=rstd[:].to_broadcast([128, 16, 64]), op=ALU.mult),
                 reads=[B["yc"], B["rstd"]], writes=[B["yn"]])
            ynf = yn[:].rearrange("p a b -> p (a b)")
            S.op("dve", lambda e: e.tensor_tensor(out=y2[:], in0=ynf, in1=lgB[:], op=ALU.mult), reads=[B["yn"], bC], writes=[B["y2"]])
            S.op("dve", lambda e: e.tensor_tensor(out=y2[:], in0=y2[:], in1=lbB[:], op=ALU.add), reads=[B["y2"], bC], writes=[B["y2"]])
            S.op("dve", lambda e: e.tensor_tensor(out=y2[:], in0=y2[:], in1=bvt[i][:], op=ALU.add), reads=[B["y2"], bbvt[i]], writes=[B["y2"]])
            z, bz = zb[i], B["zb%d" % i]
            S.op("dve", lambda e: e.tensor_tensor(out=z[:], in0=y2[:], in1=gt[i][:], op=ALU.mult), reads=[B["y2"], bgt[i]], writes=[bz])
            pt, bpt = p_tr[i], bp_tr[i]
            for c in range(8):
                S.op("pe", lambda e: e.transpose(pt[:, c, :], z[:, c * 128:(c + 1) * 128], ident[:]), reads=[bz, B["ident"]], writes=[bpt], sig=(c == 7))
            gi = (t // 4) % 2
            S.op("act", lambda e: e.copy(out=zT[gi][:, :, (t % 4) * 128:(t % 4 + 1) * 128], in_=pt[:]), reads=[bpt], writes=[bzT[gi]])
            if t % 4 == 3:
                g0 = (t // 4) * 512
                for c in range(8):
                    S.dma("sp", ZT[c * 128:(c + 1) * 128, g0:g0 + 512], zT[gi][:, c, :], bzT[gi], reads=[bzT[gi]], writes=[bZT])
        S.barrier()
        for b in [bC] + byt + bbvt + bgt + bzT:
            S.release(b)


def build_program(ntok=SEQ):
    nc = bass.Bass("TRN2", target_bir_lowering=False)
    di = lambda n, s: nc.dram_tensor(n, list(s), F32, kind="ExternalInput").ap()
    x = di("x", [ntok, D])
    ffn_norm = di("ffn_norm", [4, D])
    wg = di("ffn_w_gate", [2, 2, D, DFF])
    wu = di("ffn_w_up", [2, 2, D, DFF])
    wd = di("ffn_w_down", [2, 2, DFF, D])
    mix_norm = di("mix_norm", [2, D])
    dbias = di("dbias", [12, 128, 2, 128])
    w_in = di("attn_w_in", [D, 3072])
    qn = di("attn_q_norm", [64, 1])
    kn = di("attn_k_norm", [64, 1])
    w_out = di("attn_w_out", [512, D])
    rw_mix = di("rw_mix", [6, D])
    rows = {n: di(n, [1, D]) for n in ("rw_w0", "rw_a0", "rw_kk", "rw_ka", "rw_rk", "rw_lnx_g", "rw_lnx_b")}
    rw_w1 = di("rw_w1", [D, 64]); rw_w2 = di("rw_w2", [64, D]); rw_a1 = di("rw_a1", [D, 64]); rw_a2 = di("rw_a2", [64, D])
    rw_g1 = di("rw_g1", [D, 160]); rw_g2 = di("rw_g2", [160, D])
    rw_wr = di("rw_wr", [D, D]); rw_wk = di("rw_wk", [D, D]); rw_wv = di("rw_wv", [D, D]); rw_wo = di("rw_wo", [D, D])
    out = nc.dram_tensor("out", [ntok, D], F32, kind="ExternalOutput").ap()
    scr = lambda n, s, dt: nc.dram_tensor(n, list(s), dt, kind="Internal").ap()
    xa = scr("xa", [ntok, D], F32); xb = scr("xb", [ntok, D], F32)
    QT = scr("QT", [D, ntok], BF16); KT = scr("KT", [D, ntok], BF16); V = scr("V", [ntok, D], BF16); MT = scr("MT", [512, ntok], BF16)
    RtT, KtT, AtT, BtT = [scr(n, [D, ntok], BF16) for n in ("RtT", "KtT", "AtT", "BtT")]
    WC = scr("WC", [D, ntok // 64], F32)
    Vtok = scr("Vtok", [ntok, D], BF16); BV = scr("BV", [ntok, D], F32); Gt = scr("Gt", [ntok, D], F32); Ysc = scr("Ysc", [ntok, D], F32)
    ZT = scr("ZT", [D, ntok], BF16)

    S = Sched(nc, n_dma_sems=24)
    nb = lambda n: S.buf(n, acc=True)
    bx, bxa, bxb, bout = nb("x"), nb("xa"), nb("xb"), nb("out")
    bQT, bKT, bV, bMT = nb("QT"), nb("KT"), nb("V"), nb("MT")
    bR, bK, bA, bB, bWC, bVt, bBV, bG, bY, bZ = [nb(n) for n in ("R", "K", "A", "B", "WC", "Vt", "BV", "G", "Y", "Z")]

    ffn_phase(nc, S, x, xa, bx, bxa, wg[0, 0], wu[0, 0], wd[0, 0], ffn_norm[0:1, :], ntok)
    attn_in_phase(nc, S, xa, bxa, w_in, mix_norm[0:1, :], qn, kn, QT, KT, V, bQT, bKT, bV, ntok)
    sb_attn_phase(nc, S, QT, KT, V, MT, bQT, bKT, bV, bMT, ntok)
    dil_attn_phase(nc, S, QT, KT, V, MT, dbias, bQT, bKT, bV, bMT, ntok)
    out_proj_phase(nc, S, xa, xb, bxa, bxb, MT, bMT, w_out, 512, ntok)
    ffn_phase(nc, S, xb, xa, bxb, bxa, wg[0, 1], wu[0, 1], wd[0, 1], ffn_norm[1:2, :], ntok)
    ffn_phase(nc, S, xa, xb, bxa, bxb, wg[1, 0], wu[1, 0], wd[1, 0], ffn_norm[2:3, :], ntok)
    rwkv_fm_phase(nc, S, xb, bxb, mix_norm[1:2, :], rw_mix, rows["rw_w0"], rw_w1, rw_w2, rows["rw_a0"], rw_a1, rw_a2,
                  rows["rw_kk"], rows["rw_ka"], rw_wr, rw_wk, RtT, KtT, AtT, BtT, WC, [bR, bK, bA, bB, bWC], ntok)
    rwkv_tm_phase(nc, S, xb, bxb, mix_norm[1:2, :], rw_mix, rows["rw_a0"], rw_a1, rw_a2, rw_g1, rw_g2, rows["rw_ka"], rows["rw_rk"],
                  rw_wr, rw_wk, rw_wv, Vtok, BV, Gt, [bVt, bBV, bG], ntok)
    rwkv_scan_phase(nc, S, RtT, KtT, AtT, BtT, WC, Vtok, Ysc, [bR, bK, bA, bB, bWC, bVt], bY, ntok)
    rwkv_post_phase(nc, S, Ysc, BV, Gt, rows["rw_lnx_g"], rows["rw_lnx_b"], ZT, [bY, bBV, bG], bZ, ntok)
    out_proj_phase(nc, S, xb, xa, bxb, bxa, ZT, bZ, rw_wo, 1024, ntok)
    ffn_phase(nc, S, xa, out, bxa, bout, wg[1, 1], wu[1, 1], wd[1, 1], ffn_norm[3:4, :], ntok)
    S.wait_for("sp", [bout])
    return nc


def kernel(x, ffn_norm, ffn_w_gate, ffn_w_up, ffn_w_down, mix_norm, rel_bias,
           attn_w_in, attn_q_norm, attn_k_norm, attn_w_out,
           rw_mix, rw_w0, rw_w1, rw_w2, rw_a0, rw_a1, rw_a2, rw_g1, rw_g2,
           rw_kk, rw_ka, rw_rk, rw_wr, rw_wk, rw_wv, rw_wo, rw_lnx_g, rw_lnx_b):
    f = lambda a: np.ascontiguousarray(np.asarray(a, dtype=np.float32))
    x = f(x)
    n = x.shape[0]
    shared = {
        "ffn_norm": f(ffn_norm).reshape(4, D), "ffn_w_gate": f(ffn_w_gate), "ffn_w_up": f(ffn_w_up), "ffn_w_down": f(ffn_w_down),
        "mix_norm": f(mix_norm), "dbias": dil_bias_host(f(rel_bias)),
        "attn_w_in": f(attn_w_in)[0], "attn_q_norm": f(attn_q_norm).reshape(64, 1), "attn_k_norm": f(attn_k_norm).reshape(64, 1),
        "attn_w_out": f(attn_w_out)[0], "rw_mix": f(rw_mix)[0],
        "rw_w0": f(rw_w0).reshape(1, D), "rw_a0": f(rw_a0).reshape(1, D), "rw_kk": f(rw_kk).reshape(1, D), "rw_ka": f(rw_ka).reshape(1, D),
        "rw_rk": f(rw_rk).reshape(1, D), "rw_lnx_g": f(rw_lnx_g).reshape(1, D), "rw_lnx_b": f(rw_lnx_b).reshape(1, D),
        "rw_w1": f(rw_w1)[0], "rw_w2": f(rw_w2)[0], "rw_a1": f(rw_a1)[0], "rw_a2": f(rw_a2)[0], "rw_g1": f(rw_g1)[0], "rw_g2": f(rw_g2)[0],
        "rw_wr": f(rw_wr)[0], "rw_wk": f(rw_wk)[0], "rw_wv": f(rw_wv)[0], "rw_wo": f(rw_wo)[0],
    }
    nc = build_program(x.shape[1])
    in_maps = [dict(shared, x=x[i]) for i in range(n)]
    res = run_bass_kernel_spmd(nc, in_maps, core_ids=list(range(n)))
    return np.stack([np.asarray(r["out"]) for r in res.results], axis=0).astype(np.float32)
```

```python
import numpy as np
from contextlib import ExitStack
import concourse.bass as bass
import concourse.mybir as mybir
from concourse.bass_utils import run_bass_kernel_spmd

F32 = mybir.dt.float32
BF16 = mybir.dt.bfloat16
AF = mybir.ActivationFunctionType
ALU = mybir.AluOpType
AX = mybir.AxisListType

D = 1024
DFF = 2816
NF = DFF // 128
SEQ = 4096


_UID = [0]


def _uniq(name):
    _UID[0] += 1
    return "%s_u%d" % (name, _UID[0])


def _merge(d, s):
    for k, v in s.items():
        if d.get(k, 0) < v:
            d[k] = v


class Buf:
    __slots__ = ("name", "wr", "rd", "acc", "dkey", "excl")

    def __init__(self, name, acc=False, excl=False):
        self.name = name
        self.wr = {}
        self.rd = {}
        self.acc = acc
        self.dkey = None
        self.excl = excl


class Sched:
    ENG = ("pe", "act", "dve", "pool", "sp")

    def __init__(self, nc, n_dma_sems=40):
        self.nc = nc
        self.eng = {"pe": nc.tensor, "act": nc.scalar, "dve": nc.vector, "pool": nc.gpsimd, "sp": nc.sync}
        self.sems = {}
        self.val = {}
        self.seen = {e: {} for e in self.ENG}
        self.epoch = 0
        self.ekey = {}
        self._new_engine_sems()
        self.dma_pool = []
        for i in range(n_dma_sems):
            k = "dma%d" % i
            self.sems[k] = nc.semaphore(k).__enter__()
            self.val[k] = 0
            self.dma_pool.append(k)
        self.nwait = 0

    def _new_engine_sems(self):
        for e in self.ENG:
            k = "%s_e%d" % (e, self.epoch)
            self.sems[k] = self.nc.semaphore(k).__enter__()
            self.val[k] = 0
            self.ekey[e] = k

    def buf(self, name, acc=False):
        return Buf(name, acc)

    def pbuf(self, name):
        return Buf(name, False, True)

    def dbuf(self, name, acc=False):
        b = Buf(name, acc)
        b.dkey = self.dma_pool.pop()
        return b

    def release(self, b):
        self.dma_pool.append(b.dkey)
        b.dkey = None

    def _wait(self, e, deps):
        for k, v in deps.items():
            if v <= 0:
                continue
            if e == "pe" and k == self.ekey["pe"]:
                continue
            if self.seen[e].get(k, 0) < v:
                self.eng[e].wait_ge(self.sems[k], v)
                self.seen[e][k] = v
                self.nwait += 1

    def _deps(self, reads, writes, e=None):
        deps = {}
        for b in reads:
            _merge(deps, b.wr)
            if b.excl:
                own = self.ekey.get(e)
                _merge(deps, {k: v for k, v in b.rd.items() if k != own})
        for b in writes:
            _merge(deps, b.wr)
            _merge(deps, b.rd)
        return deps

    def _record(self, ev, reads, writes):
        for b in reads:
            _merge(b.rd, ev)
        for b in writes:
            if b.acc:
                _merge(b.wr, ev)
            else:
                b.wr = dict(ev)
                b.rd = {}

    def op(self, e, fn, reads=(), writes=(), sig=True):
        self._wait(e, self._deps(reads, writes, e))
        ins = fn(self.eng[e])
        k = self.ekey[e]
        if sig:
            ins.then_inc(self.sems[k], 1)
            self.val[k] += 1
            v = self.val[k]
        else:
            v = self.val[k] + 1
        self._record({k: v}, reads, writes)
        return ins

    def dma(self, q, out, in_, track, reads=(), writes=(), slow=False):
        self._wait(q, self._deps(reads, writes))
        if slow:
            ins = self.eng[q].dma_start(out=out, in_=in_, allow_slow_non_contiguous=True)
        else:
            ins = self.eng[q].dma_start(out=out, in_=in_)
        k = track.dkey
        ins.then_inc(self.sems[k], 16)
        self.val[k] += 16
        self._record({k: self.val[k]}, reads, writes)
        return ins

    def barrier(self, new_epoch=True):
        allv = {k: v for k, v in self.val.items() if v > 0}
        for e in self.ENG:
            self._wait(e, allv)
        if new_epoch:
            self.epoch += 1
            self._new_engine_sems()

    def wait_for(self, e, bufs):
        deps = {}
        for b in bufs:
            _merge(deps, b.wr)
            _merge(deps, b.rd)
        self._wait(e, deps)


def ffn_phase(nc, S, x_in, x_out, xin_b, xout_b, wg, wu, wd, gain_row, ntok, eps=1e-6):
    G = 256
    ng = ntok // G
    with ExitStack() as es:
        sb = lambda name, shape, dt: es.enter_context(nc.sbuf_tensor(_uniq(name), shape, dt))
        ps = lambda name, shape, dt: es.enter_context(nc.psum_tensor(_uniq(name), shape, dt))
        Wg = sb("Wg", [128, 8, DFF], BF16)
        Wu = sb("Wu", [128, 8, DFF], BF16)
        Wd = sb("Wd", [128, NF, D], BF16)
        gB = sb("gB", [128, D], F32)
        ident = sb("ident", [128, 128], BF16)
        xt = [sb("xt%d" % i, [128, D], F32) for i in range(4)]
        ot = [sb("ot%d" % i, [128, D], F32) for i in range(2)]
        hb = [sb("hb%d" % i, [128, D], BF16) for i in range(2)]
        hT = [sb("hT%d" % i, [128, 8, G], BF16) for i in range(2)]
        aT = [sb("aT%d" % i, [128, G], BF16) for i in range(3)]
        sg = [sb("sg%d" % i, [128, G], F32) for i in range(2)]
        junk = sb("junk", [128, D], BF16)
        ss = [sb("ss%d" % i, [128, 1], F32) for i in range(2)]
        rs = [sb("rs%d" % i, [128, 1], F32) for i in range(2)]
        nh = sb("nh", [128, 1], F32)
        p_gu = [ps("p_gu%d" % i, [128, 2, G], F32) for i in range(2)]
        p_dn = [ps("p_dn%d" % i, [128, 512], F32) for i in range(4)]
        p_tr = [ps("p_tr%d" % i, [128, 8, 128], BF16) for i in range(2)]

        bWg, bWu, bWd, bgB = S.dbuf("Wg"), S.dbuf("Wu"), S.dbuf("Wd"), S.dbuf("gB")
        bxt = [S.dbuf("xt%d" % i) for i in range(4)]
        bot = [S.dbuf("ot%d" % i) for i in range(2)]
        bhb = [S.buf("hb") for _ in range(2)]
        bhT = [S.buf("hT") for _ in range(2)]
        baT = [S.buf("aT") for _ in range(3)]
        bsg = [S.buf("sg") for _ in range(2)]
        bjunk = S.buf("junk")
        bss = [S.buf("ss") for _ in range(2)]
        brs = [S.buf("rs") for _ in range(2)]
        bnh, bident = S.buf("nh"), S.buf("ident")
        bp_gu = [S.pbuf("pgu") for _ in range(2)]
        bp_dn = [S.pbuf("pdn") for _ in range(4)]
        bp_tr = [S.pbuf("ptr") for _ in range(2)]

        S.op("pool", lambda e: e.memset(nh[:], -0.5), writes=[bnh])
        S.op("pool", lambda e: e.memset(ident[:], 0.0), writes=[bident])
        S.op("pool", lambda e: e.affine_select(out=ident[:], in_=ident[:], pattern=[[-1, 128]],
                                               compare_op=ALU.not_equal, fill=1.0, base=0,
                                               channel_multiplier=1), reads=[bident], writes=[bident])
        S.dma("sp", gB[:], gain_row.to_broadcast([128, D]), bgB, writes=[bgB])
        wg_v = wg.rearrange("(c p) f -> p c f", p=128)
        wu_v = wu.rearrange("(c p) f -> p c f", p=128)
        wd_v = wd.rearrange("(c p) f -> p c f", p=128)
        for c in range(8):
            S.dma("pool", Wg[:, c, :], wg_v[:, c, :], bWg, writes=[bWg])
            S.dma("pool", Wu[:, c, :], wu_v[:, c, :], bWu, writes=[bWu])
        for c in range(NF):
            S.dma("pool", Wd[:, c, :], wd_v[:, c, :], bWd, writes=[bWd])

        def load(g):
            for s in range(2):
                i = (g % 2) * 2 + s
                t0 = g * G + s * 128
                S.dma("sp", xt[i][:], x_in[t0:t0 + 128, :], bxt[i], reads=[xin_b], writes=[bxt[i]])

        def prep_dve(g):
            for s in range(2):
                i = (g % 2) * 2 + s
                S.op("dve", lambda e: e.scalar_tensor_tensor(out=junk[:], in0=xt[i][:], scalar=1.0, in1=xt[i][:],
                                                             op0=ALU.mult, op1=ALU.mult, accum_out=ss[s][:]),
                     reads=[bxt[i]], writes=[bjunk, bss[s]])
                S.op("dve", lambda e: e.tensor_scalar(out=ss[s][:], in0=ss[s][:], scalar1=1.0 / D, scalar2=eps,
                                                      op0=ALU.mult, op1=ALU.add), reads=[bss[s]], writes=[bss[s]])
                S.op("pool", lambda e: e.tensor_tensor(out=rs[s][:], in0=ss[s][:], in1=nh[:], op=ALU.pow),
                     reads=[bss[s], bnh], writes=[brs[s]])
                S.op("dve", lambda e: e.scalar_tensor_tensor(out=hb[s][:], in0=xt[i][:], scalar=rs[s][:], in1=gB[:],
                                                             op0=ALU.mult, op1=ALU.mult),
                     reads=[bxt[i], brs[s], bgB], writes=[bhb[s]])

        def prep_pe(g):
            for s in range(2):
                for c in range(8):
                    S.op("pe", lambda e: e.transpose(p_tr[s][:, c, :], hb[s][:, c * 128:(c + 1) * 128], ident[:]),
                         reads=[bhb[s], bident], writes=[bp_tr[s]], sig=(c == 7))
                S.op("act", lambda e: e.copy(out=hT[g % 2][:, :, s * 128:(s + 1) * 128], in_=p_tr[s][:]),
                     reads=[bp_tr[s]], writes=[bhT[g % 2]])

        def gate_up(g, j):
            h = hT[g % 2]
            pg = p_gu[j % 2]
            for c in range(8):
                S.op("pe", lambda e: e.matmul(pg[:, 0, :], lhsT=Wg[:, c, j * 128:(j + 1) * 128], rhs=h[:, c, :],
                                              start=(c == 0), stop=(c == 7)),
                     reads=[bWg, bhT[g % 2]], writes=[bp_gu[j % 2]], sig=False)
            for c in range(8):
                S.op("pe", lambda e: e.matmul(pg[:, 1, :], lhsT=Wu[:, c, j * 128:(j + 1) * 128], rhs=h[:, c, :],
                                              start=(c == 0), stop=(c == 7)),
                     reads=[bWu, bhT[g % 2]], writes=[bp_gu[j % 2]], sig=(c == 7))
            S.op("act", lambda e: e.activation(out=sg[j % 2][:], in_=pg[:, 0, :], func=AF.Silu),
                 reads=[bp_gu[j % 2]], writes=[bsg[j % 2]])
            S.op("dve", lambda e: e.tensor_tensor(out=aT[j % 3][:], in0=sg[j % 2][:], in1=pg[:, 1, :], op=ALU.mult),
                 reads=[bsg[j % 2], bp_gu[j % 2]], writes=[baT[j % 3]])

        def down(g, j):
            for s in range(2):
                for hh in range(2):
                    S.op("pe", lambda e: e.matmul(p_dn[s * 2 + hh][:], lhsT=aT[j % 3][:, s * 128:(s + 1) * 128],
                                                  rhs=Wd[:, j, hh * 512:(hh + 1) * 512],
                                                  start=(j == 0), stop=(j == NF - 1)),
                         reads=[baT[j % 3], bWd], writes=[bp_dn[s * 2 + hh]], sig=(j == NF - 1 or (s == 1 and hh == 1)))

        def epilogue(g):
            for s in range(2):
                i = (g % 2) * 2 + s
                for hh in range(2):
                    S.op("dve", lambda e: e.scalar_tensor_tensor(out=ot[s][:, hh * 512:(hh + 1) * 512], in0=p_dn[s * 2 + hh][:],
                                                                 scalar=0.5, in1=xt[i][:, hh * 512:(hh + 1) * 512],
                                                                 op0=ALU.mult, op1=ALU.add),
                         reads=[bp_dn[s * 2 + hh], bxt[i]], writes=[bot[s]])
                t0 = g * G + s * 128
                S.dma("sp", x_out[t0:t0 + 128, :], ot[s][:], bot[s], reads=[bot[s]], writes=[xout_b])

        load(0)
        if ng > 1:
            load(1)
        prep_dve(0)
        prep_pe(0)
        for g in range(ng):
            gate_up(g, 0)
            for j in range(NF):
                if j + 1 < NF:
                    gate_up(g, j + 1)
                elif g + 1 < ng:
                    pass
                down(g, j)
                if j == 3 and g + 1 < ng:
                    prep_dve(g + 1)
                if j == 14 and g + 1 < ng:
                    prep_pe(g + 1)
            epilogue(g)
            if g + 2 < ng:
                load(g + 2)
        S.barrier()
        for b in [bWg, bWu, bWd, bgB] + bxt + bot:
            S.release(b)


def make_ident(nc, S, ident, bident, dt_is_bf16=True):
    S.op("pool", lambda e: e.memset(ident[:], 0.0), writes=[bident])
    S.op("pool", lambda e: e.affine_select(out=ident[:], in_=ident[:], pattern=[[-1, 128]],
                                           compare_op=ALU.not_equal, fill=1.0, base=0,
                                           channel_multiplier=1), reads=[bident], writes=[bident])


def attn_in_phase(nc, S, x_in, xin_b, w_in, gain_row, qn, kn, QT, KT, V, bQT, bKT, bV, ntok, eps=1e-6):
    G = 512
    ng = ntok // G
    with ExitStack() as es:
        sb = lambda name, shape, dt: es.enter_context(nc.sbuf_tensor(_uniq(name), shape, dt))
        ps = lambda name, shape, dt: es.enter_context(nc.psum_tensor(_uniq(name), shape, dt))
        Win = sb("Win", [128, 8, 3072], BF16)
        gB = sb("gB", [128, D], F32)
        ident = sb("ident", [128, 128], BF16)
        bones = sb("bones", [128, 128], BF16)
        gq = sb("gq", [128, 1], F32)
        gk = sb("gk", [128, 1], F32)
        nh = sb("nh", [128, 1], F32)
        eps_ap = sb("eps_ap", [128, 1], F32)
        xt = [sb("xt%d" % i, [128, D], F32) for i in range(4)]
        hb = [sb("hb%d" % i, [128, D], BF16) for i in range(2)]
        hT = [sb("hT%d" % i, [128, 8, G], BF16) for i in range(2)]
        junk = sb("junk", [128, D], BF16)
        ss = [sb("ss%d" % i, [128, 1], F32) for i in range(2)]
        rs = [sb("rs%d" % i, [128, 1], F32) for i in range(2)]
        ob = [sb("ob%d" % i, [128, G], BF16) for i in range(3)]
        sq = [sb("sq%d" % i, [128, G], BF16) for i in range(2)]
        lt = [sb("lt%d" % i, [128, G], F32) for i in range(2)]
        rr = [sb("rr%d" % i, [128, G], F32) for i in range(2)]
        vb = [sb("vb%d" % i, [128, D], BF16) for i in range(2)]
        p_q = [ps("p_q%d" % i, [128, G], F32) for i in range(2)]
        p_s = [ps("p_s%d" % i, [128, G], F32) for i in range(2)]
        p_v = [ps("p_v%d" % i, [128, 512], F32) for i in range(2)]
        p_tr = [ps("p_tr%d" % i, [128, 8, 128], BF16) for i in range(2)]

        bWin, bgB, bgq, bgk = S.dbuf("Win"), S.dbuf("gB"), S.dbuf("gq"), S.dbuf("gk")
        bxt = [S.dbuf("xt") for _ in range(4)]
        bob = [S.dbuf("ob") for _ in range(3)]
        bvb = [S.dbuf("vb") for _ in range(2)]
        bhb = [S.buf("hb") for _ in range(2)]
        bhT = [S.buf("hT") for _ in range(2)]
        bjunk, bnh, bident, bbones = S.buf("junk"), S.buf("nh"), S.buf("ident"), S.buf("bones")
        bss = [S.buf("ss") for _ in range(2)]
        brs = [S.buf("rs") for _ in range(2)]
        bsq = [S.buf("sq") for _ in range(2)]
        blt = [S.buf("lt") for _ in range(2)]
        brr = [S.buf("rr") for _ in range(2)]
        bp_q = [S.pbuf("pq") for _ in range(2)]
        bp_s = [S.pbuf("ps") for _ in range(2)]
        bp_v = [S.pbuf("pv") for _ in range(2)]
        bp_tr = [S.pbuf("ptr") for _ in range(2)]

        S.op("pool", lambda e: e.memset(nh[:], -0.5), writes=[bnh])
        S.op("pool", lambda e: e.memset(eps_ap[:], eps), writes=[bnh])
        make_ident(nc, S, ident, bident)
        S.op("pool", lambda e: e.memset(bones[:], 0.0), writes=[bbones])
        S.op("pool", lambda e: e.memset(bones[0:64, 0:64], 1.0), writes=[bbones])
        S.op("pool", lambda e: e.memset(bones[64:128, 64:128], 1.0), writes=[bbones])
        S.dma("sp", gB[:], gain_row.to_broadcast([128, D]), bgB, writes=[bgB])
        for hh in range(2):
            S.dma("sp", gq[hh * 64:(hh + 1) * 64, :], qn, bgq, writes=[bgq])
            S.dma("sp", gk[hh * 64:(hh + 1) * 64, :], kn, bgk, writes=[bgk])
        S.op("dve", lambda e: e.tensor_scalar(out=gq[:], in0=gq[:], scalar1=0.125, scalar2=None, op0=ALU.mult),
             reads=[bgq], writes=[bgq])
        w_v = w_in.rearrange("(c p) f -> p c f", p=128)
        for c in range(8):
            S.dma("pool", Win[:, c, :], w_v[:, c, :], bWin, writes=[bWin])

        def load(g):
            for s in range(4):
                t0 = g * G + s * 128
                S.dma("sp", xt[s][:], x_in[t0:t0 + 128, :], bxt[s], reads=[xin_b], writes=[bxt[s]])

        def prep(g):
            for s in range(4):
                k = s % 2
                S.op("dve", lambda e: e.scalar_tensor_tensor(out=junk[:], in0=xt[s][:], scalar=1.0, in1=xt[s][:],
                                                             op0=ALU.mult, op1=ALU.mult, accum_out=ss[k][:]),
                     reads=[bxt[s]], writes=[bjunk, bss[k]])
                S.op("dve", lambda e: e.tensor_scalar(out=ss[k][:], in0=ss[k][:], scalar1=1.0 / D, scalar2=eps,
                                                      op0=ALU.mult, op1=ALU.add), reads=[bss[k]], writes=[bss[k]])
                S.op("pool", lambda e: e.tensor_tensor(out=rs[k][:], in0=ss[k][:], in1=nh[:], op=ALU.pow),
                     reads=[bss[k], bnh], writes=[brs[k]])
                S.op("dve", lambda e: e.scalar_tensor_tensor(out=hb[k][:], in0=xt[s][:], scalar=rs[k][:], in1=gB[:],
                                                             op0=ALU.mult, op1=ALU.mult),
                     reads=[bxt[s], brs[k], bgB], writes=[bhb[k]])
                for c in range(8):
                    S.op("pe", lambda e: e.transpose(p_tr[k][:, c, :], hb[k][:, c * 128:(c + 1) * 128], ident[:]),
                         reads=[bhb[k], bident], writes=[bp_tr[k]], sig=(c == 7))
                S.op("act", lambda e: e.copy(out=hT[g % 2][:, :, s * 128:(s + 1) * 128], in_=p_tr[k][:]),
                     reads=[bp_tr[k]], writes=[bhT[g % 2]])

        nob = 0
        for g in range(ng):
            load(g)
            prep(g)
            h = hT[g % 2]
            bh = bhT[g % 2]
            t0 = g * G
            for fc in range(16):
                isq = fc < 8
                ch = fc % 8
                if ch < 2:
                    col0 = (0 if isq else 256) + ch * 128
                else:
                    col0 = (768 if isq else 1536) + (ch - 2) * 128
                pq = p_q[fc % 2]
                for c in range(8):
                    S.op("pe", lambda e: e.matmul(pq[:], lhsT=Win[:, c, col0:col0 + 128], rhs=h[:, c, :],
                                                  start=(c == 0), stop=(c == 7)),
                         reads=[bWin, bh], writes=[bp_q[fc % 2]], sig=(c == 7))
                o = ob[nob % 3]
                bo = bob[nob % 3]
                nob += 1
                if ch < 2:
                    S.op("act", lambda e: e.activation(out=o[:], in_=pq[:], func=AF.Copy, scale=(0.125 if isq else 1.0)),
                         reads=[bp_q[fc % 2]], writes=[bo])
                else:
                    k = fc % 2
                    S.op("act", lambda e: e.activation(out=sq[k][:], in_=pq[:], func=AF.Square),
                         reads=[bp_q[k]], writes=[bsq[k]])
                    S.op("pe", lambda e: e.matmul(p_s[k][:], lhsT=bones[:], rhs=sq[k][:], start=True, stop=True),
                         reads=[bbones, bsq[k]], writes=[bp_s[k]])
                    S.op("act", lambda e: e.activation(out=lt[k][:], in_=p_s[k][:], func=AF.Ln, scale=1.0 / 64, bias=eps_ap[:]),
                         reads=[bp_s[k], bnh], writes=[blt[k]])
                    S.op("act", lambda e: e.activation(out=rr[k][:], in_=lt[k][:], func=AF.Exp, scale=-0.5),
                         reads=[blt[k]], writes=[brr[k]])
                    gcol = gq if isq else gk
                    S.op("dve", lambda e: e.scalar_tensor_tensor(out=o[:], in0=pq[:], scalar=gcol[:], in1=rr[k][:],
                                                                 op0=ALU.mult, op1=ALU.mult),
                         reads=[bp_q[k], brr[k], bgq, bgk], writes=[bo])
                dst, bdst = (QT, bQT) if isq else (KT, bKT)
                S.dma("sp", dst[ch * 128:(ch + 1) * 128, t0:t0 + G], o[:], bo, reads=[bo], writes=[bdst])
            for s in range(4):
                k = s % 2
                for (pv, cols, off) in ((p_v[0], (512, 768), 0), (p_v[0], (2304, 2560), 256), (p_v[1], (2560, 3072), 0)):
                    n = cols[1] - cols[0]
                    for c in range(8):
                        S.op("pe", lambda e: e.matmul(pv[:, off:off + n], lhsT=h[:, c, s * 128:(s + 1) * 128],
                                                      rhs=Win[:, c, cols[0]:cols[1]], start=(c == 0), stop=(c == 7)),
                             reads=[bWin, bh], writes=[bp_v[0], bp_v[1]], sig=(c == 7))
                S.op("act", lambda e: e.copy(out=vb[k][:, 0:512], in_=p_v[0][:]), reads=[bp_v[0]], writes=[bvb[k]])
                S.op("dve", lambda e: e.tensor_copy(out=vb[k][:, 512:1024], in_=p_v[1][:]), reads=[bp_v[1]], writes=[bvb[k]])
                S.dma("sp", V[t0 + s * 128:t0 + (s + 1) * 128, :], vb[k][:], bvb[k], reads=[bvb[k]], writes=[bV])
        S.barrier()
        for b in [bWin, bgB, bgq, bgk] + bxt + bob + bvb:
            S.release(b)


def sb_attn_phase(nc, S, QT, KT, V, MT, bQT, bKT, bV, bMT, ntok):
    nblk = ntok // 128
    NH = 4
    with ExitStack() as es:
        sb = lambda name, shape, dt: es.enter_context(nc.sbuf_tensor(_uniq(name), shape, dt))
        ps = lambda name, shape, dt: es.enter_context(nc.psum_tensor(_uniq(name), shape, dt))
        qT = sb("qT", [128, 2, ntok], BF16)
        kT = sb("kT", [128, 2, ntok], BF16)
        v = sb("v", [128, nblk, 256], BF16)
        ones = sb("ones", [128, 512], F32)
        onec = sb("onec", [128, 1], F32)
        mneg = sb("mneg", [128, 128], BF16)
        ident = sb("ident", [128, 128], BF16)
        mk2 = lambda nm, shape, dt: [[sb("%s%d_%d" % (nm, h, i), shape, dt) for i in range(2)] for h in range(NH)]
        e_ = mk2("e", [128, 512], F32)
        sp_ = mk2("sp", [128, 512], F32)
        cs_ = mk2("cs", [128, 512], F32)
        lw_ = mk2("lw", [128, 512], F32)
        w_ = mk2("w", [128, 512], BF16)
        wT_ = mk2("wT", [128, 4, 128], BF16)
        oT = [[sb("oT%d_%d" % (pr, i), [128, 512], BF16) for i in range(2)] for pr in range(2)]
        p_z = [ps("p_z%d" % h, [128, 512], F32) for h in range(NH)]
        p_w = [ps("p_w%d" % i, [128, 4, 128], BF16) for i in range(2)]
        p_o = [ps("p_o%d" % i, [128, 128], F32) for i in range(2)]

        bq, bk, bv = S.dbuf("qT"), S.dbuf("kT"), S.dbuf("v")
        boT = [[S.dbuf("oT") for _ in range(2)] for _ in range(2)]
        bones, bmneg, bident = S.buf("ones"), S.buf("mneg"), S.buf("ident")
        bb2 = lambda nm: [[S.buf(nm) for _ in range(2)] for _ in range(NH)]
        be, bsp, bcs, blw, bw, bwT = bb2("e"), bb2("sp"), bb2("cs"), bb2("lw"), bb2("w"), bb2("wT")
        bp_z = [S.pbuf("pz") for _ in range(NH)]
        bp_w = [S.pbuf("pw") for _ in range(2)]
        bp_o = [S.pbuf("po") for _ in range(2)]

        S.op("pool", lambda e: e.memset(ones[:], 1.0), writes=[bones])
        S.op("pool", lambda e: e.memset(onec[:], 1.0), writes=[bones])
        make_ident(nc, S, ident, bident)
        S.op("pool", lambda e: e.memset(mneg[:], 0.0), writes=[bmneg])
        S.op("pool", lambda e: e.affine_select(out=mneg[:], in_=mneg[:], pattern=[[-1, 128]], compare_op=ALU.is_gt,
                                               fill=-30000.0, base=0, channel_multiplier=1),
             reads=[bmneg], writes=[bmneg])
        for pr in range(2):
            S.dma("sp", qT[:, pr, :], QT[pr * 128:(pr + 1) * 128, 0:ntok], bq, reads=[bQT], writes=[bq])
            S.dma("sp", kT[:, pr, :], KT[pr * 128:(pr + 1) * 128, 0:ntok], bk, reads=[bKT], writes=[bk])
        S.dma("sp", v[:], V[0:ntok, 0:256].rearrange("(b p) c -> p b c", p=128), bv, reads=[bV], writes=[bv])

        heads = [(h, h // 2, h % 2, slice(64 * (h % 2), 64 * (h % 2) + 64)) for h in range(NH)]
        u = 0
        for qb in range(nblk):
            chunks = [(4 * (qb // 4), qb + 1, True)]
            for c in range(qb // 4 - 1, -1, -1):
                chunks.append((4 * c, 4 * c + 4, False))
            for ci, (b0, b1, diag) in enumerate(chunks):
                W = (b1 - b0) * 128
                nb = b1 - b0
                i = u % 2
                u += 1
                rev = (lambda t: t[:, W - 1::-1] if W < 512 else t[:, ::-1])
                for (h, pr, hh, P) in heads:
                    S.op("pe", lambda e: e.matmul(p_z[h][:, 0:W], lhsT=qT[P, pr, qb * 128:(qb + 1) * 128],
                                                  rhs=kT[P, pr, b0 * 128:b1 * 128], start=True, stop=(not diag)),
                         reads=[bq, bk], writes=[bp_z[h]], sig=(not diag))
                    if diag:
                        S.op("pe", lambda e: e.matmul(p_z[h][:, W - 128:W], lhsT=ident[:], rhs=mneg[:], start=False, stop=True),
                             reads=[bident, bmneg], writes=[bp_z[h]])
                for (h, pr, hh, P) in heads:
                    S.op("act", lambda e: e.activation(out=e_[h][i][:, 0:W], in_=p_z[h][:, 0:W], func=AF.Exp),
                         reads=[bp_z[h]], writes=[be[h][i]])
                for (h, pr, hh, P) in heads:
                    S.op("act", lambda e: e.activation(out=sp_[h][i][:, 0:W], in_=e_[h][i][:, 0:W], func=AF.Ln, bias=onec[:]),
                         reads=[be[h][i], bones], writes=[bsp[h][i]])
                for (h, pr, hh, P) in heads:
                    if ci == 0:
                        init, rd = 0.0, [bsp[h][i], bones]
                    else:
                        init, rd = cs_[h][1 - i][:, 0:1], [bsp[h][i], bones, bcs[h][1 - i]]
                    S.op("dve", lambda e: e.tensor_tensor_scan(out=rev(cs_[h][i]), data0=ones[:, 0:W], data1=rev(sp_[h][i]),
                                                               initial=init, op0=ALU.mult, op1=ALU.add),
                         reads=rd, writes=[bcs[h][i]])
                for (h, pr, hh, P) in heads:
                    S.op("dve", lambda e: e.tensor_tensor(out=lw_[h][i][:, 0:W], in0=p_z[h][:, 0:W], in1=cs_[h][i][:, 0:W], op=ALU.subtract),
                         reads=[bp_z[h], bcs[h][i]], writes=[blw[h][i]])
                for (h, pr, hh, P) in heads:
                    S.op("act", lambda e: e.activation(out=w_[h][i][:, 0:W], in_=lw_[h][i][:, 0:W], func=AF.Exp),
                         reads=[blw[h][i]], writes=[bw[h][i]])
                for (h, pr, hh, P) in heads:
                    pw, bpw = p_w[h % 2], bp_w[h % 2]
                    for b in range(nb):
                        S.op("pe", lambda e: e.transpose(pw[:, b, :], w_[h][i][:, b * 128:(b + 1) * 128], ident[:]),
                             reads=[bw[h][i], bident], writes=[bpw], sig=(b == nb - 1))
                    if h % 2 == 0:
                        S.op("act", lambda e: e.copy(out=wT_[h][i][:, 0:nb, :], in_=pw[:, 0:nb, :]), reads=[bpw], writes=[bwT[h][i]])
                    else:
                        S.op("dve", lambda e: e.tensor_copy(out=wT_[h][i][:, 0:nb, :], in_=pw[:, 0:nb, :]), reads=[bpw], writes=[bwT[h][i]])
                for (h, pr, hh, P) in heads:
                    for b in range(nb):
                        first = (ci == 0 and b == 0)
                        last = (ci == len(chunks) - 1 and b == nb - 1)
                        S.op("pe", lambda e: e.matmul(p_o[pr][P, :], lhsT=v[:, b0 + b, h * 64:(h + 1) * 64], rhs=wT_[h][i][:, b, :],
                                                      start=first, stop=last),
                             reads=[bv, bwT[h][i]], writes=[bp_o[pr]], sig=(b == nb - 1))
            k = (qb // 4) % 2
            for (h, pr, hh, P) in heads:
                S.op("dve", lambda e: e.tensor_copy(out=oT[pr][k][P, (qb % 4) * 128:(qb % 4 + 1) * 128], in_=p_o[pr][P, :]),
                     reads=[bp_o[pr]], writes=[boT[pr][k]])
            if qb % 4 == 3 or qb == nblk - 1:
                q0 = 4 * (qb // 4)
                n = (qb - q0 + 1) * 128
                for pr in range(2):
                    S.dma("sp", MT[pr * 128:(pr + 1) * 128, q0 * 128:q0 * 128 + n], oT[pr][k][:, 0:n], boT[pr][k],
                          reads=[boT[pr][k]], writes=[bMT])
        S.barrier()
        for b in [bq, bk, bv] + boT[0] + boT[1]:
            S.release(b)


def dil_bias_host(rel_bias):
    out = np.empty((12, 128, 2, 128), np.float32)
    kj = np.arange(128)[:, None]
    q = np.arange(128)[None, :]
    for g, r in enumerate((1, 4, 16)):
        for part, dist in ((1, q - kj), (0, q + 128 - kj)):
            valid = (dist >= 0) & (dist <= 128)
            dd = np.maximum(dist, 0) * r
            d = np.maximum(dd, 1).astype(np.float32)
            large = 16 + (np.log(d / np.float32(16)) / np.float32(np.log(2048 / 16)) * np.float32(16)).astype(np.int32)
            large = np.minimum(large, 31)
            bucket = np.where(dd < 16, dd, large)
            for j in range(4):
                hd = 4 * g + j
                out[hd, :, part, :] = np.where(valid, rel_bias[bucket, hd], np.float32(-30000.0))
    return out


def dil_attn_phase(nc, S, QT, KT, V, MT, dbias, bQT, bKT, bV, bMT, ntok):
    with ExitStack() as es:
        sb = lambda name, shape, dt: es.enter_context(nc.sbuf_tensor(_uniq(name), shape, dt))
        ps = lambda name, shape, dt: es.enter_context(nc.psum_tensor(_uniq(name), shape, dt))
        qT = sb("qT", [128, 2, ntok], BF16)
        kT = sb("kT", [128, 2, ntok], BF16)
        v = sb("v", [128, ntok // 128, 256], BF16)
        bias = sb("bias", [128, 4, 256], F32)
        onesb = sb("onesb", [128, 64], BF16)
        Nacc = sb("Nacc", [128, 2, ntok], F32)
        Dacc = sb("Dacc", [128, 2, ntok], F32)
        s_ = [sb("s%d" % i, [128, 256], F32) for i in range(3)]
        pT_ = [sb("pT%d" % i, [128, 2, 128], BF16) for i in range(3)]
        ob = [sb("ob%d" % i, [128, 1024], BF16) for i in range(2)]
        p_s = [ps("p_s%d" % i, [128, 2, 128], F32) for i in range(2)]
        p_n = [ps("p_n%d" % i, [128, 128], F32) for i in range(2)]
        p_d = [ps("p_d%d" % i, [128, 128], F32) for i in range(2)]
        p_pad = [ps("p_pad%d" % i, [128, 256], F32) for i in range(0)]

        bq, bk, bv, bbias = S.dbuf("qT"), S.dbuf("kT"), S.dbuf("v"), S.dbuf("bias")
        bob = [S.dbuf("ob") for _ in range(2)]
        bones, bN, bD = S.buf("ones"), S.buf("N"), S.buf("D")
        bs = [S.buf("s") for _ in range(3)]
        bpT = [S.buf("pT") for _ in range(3)]
        bp_s = [S.pbuf("ps") for _ in range(2)]
        bp_n = [S.pbuf("pn") for _ in range(2)]
        bp_d = [S.pbuf("pd") for _ in range(2)]

        S.op("pool", lambda e: e.memset(onesb[:], 1.0), writes=[bones])
        u = 0
        for g, r in enumerate((1, 4, 16)):
            L = ntok // r
            nb = L // 128
            for pr in range(2):
                r0 = 256 + (2 * g + pr) * 128
                S.dma("sp", qT[:, pr, :], QT[r0:r0 + 128, 0:ntok], bq, reads=[bQT], writes=[bq])
                S.dma("sp", kT[:, pr, :], KT[r0:r0 + 128, 0:ntok], bk, reads=[bKT], writes=[bk])
            vsrc = V[0:ntok, 256 + g * 256:256 + (g + 1) * 256].rearrange("(n i c) f -> c i n f", i=128, c=r)
            for c in range(r):
                S.dma("sp", v[:, c * nb:(c + 1) * nb, :], vsrc[c], bv, reads=[bV], writes=[bv])
            for j in range(4):
                S.dma("sp", bias[:, j, :], dbias[4 * g + j].rearrange("k a q -> k (a q)"), bbias, writes=[bbias])
            for j in range(4):
                pr, hh = j // 2, j % 2
                P = slice(64 * hh, 64 * hh + 64)
                for c in range(r):
                    for n in range(nb):
                        def tok(nn):
                            st = c + r * 128 * nn
                            return slice(st, st + r * 127 + 1, r)
                        i = u % 3
                        pss, bpss = p_s[u % 2], bp_s[u % 2]
                        pn, bpn = p_n[u % 2], bp_n[u % 2]
                        pd, bpd = p_d[u % 2], bp_d[u % 2]
                        u += 1
                        a0 = 0 if n > 0 else 1
                        if n > 0:
                            S.op("pe", lambda e: e.matmul(pss[:, 0, :], lhsT=kT[P, pr, tok(n - 1)], rhs=qT[P, pr, tok(n)],
                                                          start=True, stop=True), reads=[bq, bk], writes=[bpss], sig=False)
                        S.op("pe", lambda e: e.matmul(pss[:, 1, :], lhsT=kT[P, pr, tok(n)], rhs=qT[P, pr, tok(n)],
                                                      start=True, stop=True), reads=[bq, bk], writes=[bpss])
                        S.op("dve", lambda e: e.tensor_tensor(out=s_[i][:, a0 * 128:256], in0=pss[:, a0:2, :],
                                                              in1=bias[:, j, a0 * 128:256], op=ALU.add),
                             reads=[bpss, bbias], writes=[bs[i]])
                        S.op("act", lambda e: e.activation(out=pT_[i][:, a0:2, :], in_=s_[i][:, a0 * 128:256], func=AF.Exp),
                             reads=[bs[i]], writes=[bpT[i]])
                        for a in range(a0, 2):
                            S.op("pe", lambda e: e.matmul(pn[P, :], lhsT=v[:, c * nb + n - 1 + a, j * 64:(j + 1) * 64],
                                                          rhs=pT_[i][:, a, :], start=(a == a0), stop=(a == 1)),
                                 reads=[bv, bpT[i]], writes=[bpn], sig=(a == 1))
                        for a in range(a0, 2):
                            S.op("pe", lambda e: e.matmul(pd[P, :], lhsT=onesb[:, :], rhs=pT_[i][:, a, :],
                                                          start=(a == a0), stop=(a == 1)),
                                 reads=[bones, bpT[i]], writes=[bpd], sig=(a == 1))
                        if g == 0:
                            S.op("act", lambda e: e.copy(out=Nacc[P, pr, tok(n)], in_=pn[P, :]), reads=[bpn], writes=[bN])
                            S.op("dve", lambda e: e.tensor_copy(out=Dacc[P, pr, tok(n)], in_=pd[P, :]), reads=[bpd], writes=[bD])
                        else:
                            S.op("dve", lambda e: e.tensor_tensor(out=Nacc[P, pr, tok(n)], in0=pn[P, :], in1=Nacc[P, pr, tok(n)],
                                                                  op=ALU.add), reads=[bpn, bN], writes=[bN])
                            S.op("dve", lambda e: e.tensor_tensor(out=Dacc[P, pr, tok(n)], in0=pd[P, :], in1=Dacc[P, pr, tok(n)],
                                                                  op=ALU.add), reads=[bpd, bD], writes=[bD])
        k = 0
        for pr in range(2):
            for c0 in range(0, ntok, 1024):
                n = min(1024, ntok - c0)
                S.op("dve", lambda e: e.reciprocal(out=Dacc[:, pr, c0:c0 + n], in_=Dacc[:, pr, c0:c0 + n]), reads=[bD], writes=[bD])
                S.op("dve", lambda e: e.tensor_tensor(out=ob[k % 2][:, 0:n], in0=Nacc[:, pr, c0:c0 + n], in1=Dacc[:, pr, c0:c0 + n],
                                                      op=ALU.mult), reads=[bN, bD], writes=[bob[k % 2]])
                S.dma("sp", MT[256 + pr * 128:256 + (pr + 1) * 128, c0:c0 + n], ob[k % 2][:, 0:n], bob[k % 2],
                      reads=[bob[k % 2]], writes=[bMT])
                k += 1
        S.barrier()
        for b in [bq, bk, bv, bbias] + bob:
            S.release(b)


def out_proj_phase(nc, S, x_in, x_out, xin_b, xout_b, MT, bMT, w_out, kdim, ntok):
    G = 512
    ng = ntok // G
    kc = kdim // 128
    with ExitStack() as es:
        sb = lambda name, shape, dt: es.enter_context(nc.sbuf_tensor(_uniq(name), shape, dt))
        ps = lambda name, shape, dt: es.enter_context(nc.psum_tensor(_uniq(name), shape, dt))
        Wo = sb("Wo", [128, kc, D], BF16)
        mT = [sb("mT%d" % i, [128, kc, G], BF16) for i in range(2)]
        xt = [sb("xt%d" % i, [128, D], F32) for i in range(3)]
        ot = [sb("ot%d" % i, [128, D], F32) for i in range(2)]
        p_y = [ps("p_y%d" % i, [128, 512], F32) for i in range(4)]
        bWo = S.dbuf("Wo")
        bmT = [S.dbuf("mT") for _ in range(2)]
        bxt = [S.dbuf("xt") for _ in range(3)]
        bot = [S.dbuf("ot") for _ in range(2)]
        bp_y = [S.pbuf("py") for _ in range(4)]
        w_v = w_out.rearrange("(c p) f -> p c f", p=128)
        for c in range(kc):
            S.dma("pool", Wo[:, c, :], w_v[:, c, :], bWo, writes=[bWo])
        k = 0
        for g in range(ng):
            m, bm = mT[g % 2], bmT[g % 2]
            for c in range(kc):
                S.dma("sp", m[:, c, :], MT[c * 128:(c + 1) * 128, g * G:(g + 1) * G], bm, reads=[bMT], writes=[bm])
            for s in range(4):
                t0 = g * G + s * 128
                x_, bx = xt[k % 3], bxt[k % 3]
                o_, bo = ot[k % 2], bot[k % 2]
                S.dma("sp", x_[:], x_in[t0:t0 + 128, :], bx, reads=[xin_b], writes=[bx])
                for hh in range(2):
                    py, bpy = p_y[(2 * k + hh) % 4], bp_y[(2 * k + hh) % 4]
                    for c in range(kc):
                        S.op("pe", lambda e: e.matmul(py[:], lhsT=m[:, c, s * 128:(s + 1) * 128], rhs=Wo[:, c, hh * 512:(hh + 1) * 512],
                                                      start=(c == 0), stop=(c == kc - 1)),
                             reads=[bm, bWo], writes=[bpy], sig=(c == kc - 1))
                    S.op("dve", lambda e: e.tensor_tensor(out=o_[:, hh * 512:(hh + 1) * 512], in0=py[:], in1=x_[:, hh * 512:(hh + 1) * 512],
                                                          op=ALU.add), reads=[bpy, bx], writes=[bo])
                S.dma("sp", x_out[t0:t0 + 128, :], o_[:], bo, reads=[bo], writes=[xout_b])
                k += 1
        S.barrier()
        for b in [bWo] + bmT + bxt + bot:
            S.release(b)


C0 = float(np.exp(-0.5))


class RwPrep:
    def __init__(self, nc, S, es, G, mix_ids, x_in, xin_b, gain_row, mix, eps=1e-6):
        sb = lambda name, shape, dt: es.enter_context(nc.sbuf_tensor(_uniq(name), shape, dt))
        ps = lambda name, shape, dt: es.enter_context(nc.psum_tensor(_uniq(name), shape, dt))
        self.nc, self.S, self.G, self.mix_ids, self.x_in, self.xin_b, self.eps = nc, S, G, mix_ids, x_in, xin_b, eps
        self.gB = sb("gB", [128, D], F32)
        self.identf = sb("identf", [128, 128], F32)
        self.nh = sb("nh", [128, 1], F32)
        self.mixc = sb("mixc", [128, 6, 8], F32)
        self.xt = [sb("xt%d" % i, [128, D], F32) for i in range(2)]
        self.hn = [sb("hn%d" % i, [128, D], F32) for i in range(2)]
        self.junk = sb("junk", [128, D], BF16)
        self.ss = [sb("ss%d" % i, [128, 1], F32) for i in range(2)]
        self.rs = [sb("rs%d" % i, [128, 1], F32) for i in range(2)]
        self.hT = [sb("hT%d" % i, [128, 8, G + 1], F32) for i in range(2)]
        self.xx = [sb("xx%d" % i, [128, G], F32) for i in range(2)]
        self.xm = {i: sb("xm%d" % i, [128, 8, G], BF16) for i in mix_ids}
        self.p_tr = [ps("p_tr%d" % i, [128, 4, 128], F32) for i in range(2)]
        self.bgB, self.bmixc = S.dbuf("gB"), S.dbuf("mixc")
        self.bxt = [S.dbuf("xt") for _ in range(2)]
        self.bhn = [S.buf("hn") for _ in range(2)]
        self.bident, self.bnh, self.bjunk = S.buf("identf"), S.buf("nh"), S.buf("junk")
        self.bss = [S.buf("ss") for _ in range(2)]
        self.brs = [S.buf("rs") for _ in range(2)]
        self.bhT = [S.buf("hT") for _ in range(2)]
        self.bxx = [S.buf("xx") for _ in range(2)]
        self.bxm = {i: S.buf("xm") for i in mix_ids}
        self.bp_tr = [S.pbuf("ptr") for _ in range(2)]
        S.op("pool", lambda e: e.memset(self.nh[:], -0.5), writes=[self.bnh])
        make_ident(nc, S, self.identf, self.bident)
        S.dma("sp", self.gB[:], gain_row.to_broadcast([128, D]), self.bgB, writes=[self.bgB])
        for i in range(6):
            S.dma("sp", self.mixc[:, i, :], mix[i:i + 1, :].rearrange("o (c p) -> p (o c)", p=128), self.bmixc, writes=[self.bmixc], slow=True)
        S.op("dve", lambda e: e.memset(self.hT[1][:, :, G:G + 1], 0.0), writes=[self.bhT[1]])
        self.dsems = [self.bgB, self.bmixc] + self.bxt

    def group(self, g):
        nc, S, G = self.nc, self.S, self.G
        hT, bhT = self.hT[g % 2], self.bhT[g % 2]
        hTp, bhTp = self.hT[(g + 1) % 2], self.bhT[(g + 1) % 2]
        S.op("pool", lambda e: e.tensor_copy(out=hT[:, :, 0:1], in_=hTp[:, :, G:G + 1]), reads=[bhTp], writes=[bhT])
        k = 0
        for s in range(G // 128):
            t0 = g * G + s * 128
            x_, bx = self.xt[s % 2], self.bxt[s % 2]
            h_, bh = self.hn[s % 2], self.bhn[s % 2]
            ss, bss, rs, brs = self.ss[s % 2], self.bss[s % 2], self.rs[s % 2], self.brs[s % 2]
            S.dma("sp", x_[:], self.x_in[t0:t0 + 128, :], bx, reads=[self.xin_b], writes=[bx])
            S.op("dve", lambda e: e.scalar_tensor_tensor(out=self.junk[:], in0=x_[:], scalar=1.0, in1=x_[:], op0=ALU.mult, op1=ALU.mult,
                                                         accum_out=ss[:]), reads=[bx], writes=[self.bjunk, bss])
            S.op("dve", lambda e: e.tensor_scalar(out=ss[:], in0=ss[:], scalar1=1.0 / D, scalar2=self.eps, op0=ALU.mult, op1=ALU.add),
                 reads=[bss], writes=[bss])
            S.op("pool", lambda e: e.tensor_tensor(out=rs[:], in0=ss[:], in1=self.nh[:], op=ALU.pow), reads=[bss, self.bnh], writes=[brs])
            S.op("dve", lambda e: e.scalar_tensor_tensor(out=h_[:], in0=x_[:], scalar=rs[:], in1=self.gB[:], op0=ALU.mult, op1=ALU.mult),
                 reads=[bx, brs, self.bgB], writes=[bh])
            for half in range(2):
                pt, bpt = self.p_tr[k % 2], self.bp_tr[k % 2]
                k += 1
                for c4 in range(4):
                    c = half * 4 + c4
                    S.op("pe", lambda e: e.transpose(pt[:, c4, :], h_[:, c * 128:(c + 1) * 128], self.identf[:]),
                         reads=[bh, self.bident], writes=[bpt], sig=(c4 == 3))
                S.op("act", lambda e: e.copy(out=hT[:, half * 4:half * 4 + 4, 1 + s * 128:1 + (s + 1) * 128], in_=pt[:]),
                     reads=[bpt], writes=[bhT])
        for c in range(8):
            xx, bxx = self.xx[c % 2], self.bxx[c % 2]
            S.op("dve", lambda e: e.tensor_tensor(out=xx[:], in0=hT[:, c, 0:G], in1=hT[:, c, 1:G + 1], op=ALU.subtract),
                 reads=[bhT], writes=[bxx])
            for n, i in enumerate(self.mix_ids):
                S.op("dve", lambda e: e.scalar_tensor_tensor(out=self.xm[i][:, c, :], in0=xx[:], scalar=self.mixc[:, i, c:c + 1],
                                                             in1=hT[:, c, 1:G + 1], op0=ALU.mult, op1=ALU.add),
                     reads=[bxx, bhT, self.bmixc], writes=[self.bxm[i]])

    def release(self):
        for b in self.dsems:
            self.S.release(b)


def col_load(S, dst, src_row, track):
    S.dma("sp", dst, src_row.rearrange("o (c p) -> p (o c)", p=128), track, writes=[track], slow=True)


def rwkv_fm_phase(nc, S, x_in, xin_b, gain_row, mix, w0, w1, w2, a0, a1, a2, kk_, ka_, w_r, w_k,
                  RtT, KtT, AtT, BtT, WC, bouts, ntok):
    G = 512
    ng = ntok // G
    with ExitStack() as es:
        sb = lambda name, shape, dt: es.enter_context(nc.sbuf_tensor(_uniq(name), shape, dt))
        ps = lambda name, shape, dt: es.enter_context(nc.psum_tensor(_uniq(name), shape, dt))
        P = RwPrep(nc, S, es, G, (0, 1, 2, 4), x_in, xin_b, gain_row, mix)
        Wr = sb("Wr", [128, 8, D], BF16)
        Wk = sb("Wk", [128, 8, D], BF16)
        W1 = sb("W1", [128, 8, 64], BF16)
        A1 = sb("A1", [128, 8, 64], BF16)
        W2 = sb("W2", [64, D], BF16)
        A2 = sb("A2", [64, D], BF16)
        cols = sb("cols", [128, 4, 8], F32)
        bones = sb("bones", [128, 128], BF16)
        rmask = sb("rmask", [128, G], F32)
        tiny = sb("tiny", [128, 1], F32)
        tw = sb("tw", [64, G], BF16)
        ta = sb("ta", [64, G], BF16)
        names = ["sgu", "av", "kk0", "lnk", "rk", "kkn", "t1", "kp", "csg", "cse", "eW", "eWi", "eWe", "t2"]
        T = {n: sb(n, [128, G], F32) for n in names}
        sqk = sb("sqk", [128, G], BF16)
        wc = [sb("wc%d" % i, [128, G // 64], F32) for i in range(2)]
        ob = [sb("ob%d" % i, [128, G], BF16) for i in range(8)]
        p_r = ps("p_r", [128, G], F32)
        p_k = ps("p_k", [128, G], F32)
        p_u = ps("p_u", [128, G], F32)
        p_a = ps("p_a", [128, G], F32)
        p_ss = ps("p_ss", [128, G], F32)
        p_t = ps("p_t", [128, G], F32)
        bW = S.dbuf("W")
        bcols = S.dbuf("cols")
        bwc = [S.dbuf("wc") for _ in range(2)]
        bob = [S.dbuf("ob") for _ in range(8)]
        B = {n: S.buf(n) for n in names + ["sqk", "tw", "ta", "bones", "rmask"]}
        B.update({n: S.pbuf(n) for n in ["p_r", "p_k", "p_u", "p_a", "p_ss", "p_t"]})
        for c in range(8):
            S.dma("pool", Wr[:, c, :], w_r.rearrange("(c p) f -> p c f", p=128)[:, c, :], bW, writes=[bW])
            S.dma("pool", Wk[:, c, :], w_k.rearrange("(c p) f -> p c f", p=128)[:, c, :], bW, writes=[bW])
        S.dma("pool", W1[:], w1.rearrange("(c p) f -> p c f", p=128), bW, writes=[bW])
        S.dma("pool", A1[:], a1.rearrange("(c p) f -> p c f", p=128), bW, writes=[bW])
        S.dma("pool", W2[:], w2, bW, writes=[bW])
        S.dma("pool", A2[:], a2, bW, writes=[bW])
        for n, src in enumerate((w0, a0, kk_, ka_)):
            col_load(S, cols[:, n, :], src, bcols)
        S.op("pool", lambda e: e.memset(bones[:], 0.0), writes=[B["bones"]])
        S.op("pool", lambda e: e.memset(bones[0:64, 0:64], 1.0), writes=[B["bones"]])
        S.op("pool", lambda e: e.memset(bones[64:128, 64:128], 1.0), writes=[B["bones"]])
        S.op("pool", lambda e: e.memset(rmask[:], 1.0), writes=[B["rmask"]])
        S.op("pool", lambda e: e.memset(rmask[:, 0:G:64], 0.0), writes=[B["rmask"]])
        S.op("pool", lambda e: e.memset(tiny[:], 1e-18), writes=[B["rmask"]])
        RB, KB, AB, BB, WCB = bouts
        no = 0
        for g in range(ng):
            P.group(g)
            t0 = g * G
            xr, xw, xk, xa = P.xm[0], P.xm[1], P.xm[2], P.xm[4]
            bxr, bxw, bxk, bxa = P.bxm[0], P.bxm[1], P.bxm[2], P.bxm[4]
            for c in range(8):
                S.op("pe", lambda e: e.matmul(p_t[0:64, :], lhsT=W1[:, c, :], rhs=xw[:, c, :], start=(c == 0), stop=(c == 7)),
                     reads=[bW, bxw], writes=[B["p_t"]], sig=(c == 7))
            S.op("act", lambda e: e.activation(out=tw[:], in_=p_t[0:64, :], func=AF.Tanh), reads=[B["p_t"]], writes=[B["tw"]])
            for c in range(8):
                S.op("pe", lambda e: e.matmul(p_t[0:64, :], lhsT=A1[:, c, :], rhs=xa[:, c, :], start=(c == 0), stop=(c == 7)),
                     reads=[bW, bxa], writes=[B["p_t"]], sig=(c == 7))
            S.op("act", lambda e: e.copy(out=ta[:], in_=p_t[0:64, :]), reads=[B["p_t"]], writes=[B["ta"]])
            for cc in range(8):
                fs = slice(cc * 128, (cc + 1) * 128)
                for c in range(8):
                    S.op("pe", lambda e: e.matmul(p_r[:], lhsT=Wr[:, c, fs], rhs=xr[:, c, :], start=(c == 0), stop=(c == 7)),
                         reads=[bW, bxr], writes=[B["p_r"]], sig=(c == 7))
                for c in range(8):
                    S.op("pe", lambda e: e.matmul(p_k[:], lhsT=Wk[:, c, fs], rhs=xk[:, c, :], start=(c == 0), stop=(c == 7)),
                         reads=[bW, bxk], writes=[B["p_k"]], sig=(c == 7))
                S.op("pe", lambda e: e.matmul(p_u[:], lhsT=W2[:, fs], rhs=tw[:], start=True, stop=True), reads=[bW, B["tw"]], writes=[B["p_u"]])
                S.op("pe", lambda e: e.matmul(p_a[:], lhsT=A2[:, fs], rhs=ta[:], start=True, stop=True), reads=[bW, B["ta"]], writes=[B["p_a"]])
                S.op("act", lambda e: e.activation(out=T["sgu"][:], in_=p_u[:], func=AF.Sigmoid, bias=cols[:, 0, cc:cc + 1]),
                     reads=[B["p_u"], bcols], writes=[B["sgu"]])
                S.op("act", lambda e: e.activation(out=T["av"][:], in_=p_a[:], func=AF.Sigmoid, bias=cols[:, 1, cc:cc + 1]),
                     reads=[B["p_a"], bcols], writes=[B["av"]])
                S.op("dve", lambda e: e.tensor_scalar(out=T["kk0"][:], in0=p_k[:], scalar1=cols[:, 2, cc:cc + 1], scalar2=None, op0=ALU.mult),
                     reads=[B["p_k"], bcols], writes=[B["kk0"]])
                S.op("act", lambda e: e.activation(out=sqk[:], in_=T["kk0"][:], func=AF.Square), reads=[B["kk0"]], writes=[B["sqk"]])
                S.op("pe", lambda e: e.matmul(p_ss[:], lhsT=bones[:], rhs=sqk[:], start=True, stop=True), reads=[B["bones"], B["sqk"]],
                     writes=[B["p_ss"]])
                S.op("act", lambda e: e.activation(out=T["lnk"][:], in_=p_ss[:], func=AF.Ln, bias=tiny[:]), reads=[B["p_ss"], B["rmask"]],
                     writes=[B["lnk"]])
                S.op("act", lambda e: e.activation(out=T["rk"][:], in_=T["lnk"][:], func=AF.Exp, scale=-0.5), reads=[B["lnk"]], writes=[B["rk"]])
                S.op("dve", lambda e: e.tensor_tensor(out=T["kkn"][:], in0=T["kk0"][:], in1=T["rk"][:], op=ALU.mult),
                     reads=[B["kk0"], B["rk"]], writes=[B["kkn"]])
                S.op("dve", lambda e: e.tensor_scalar(out=T["t1"][:], in0=T["av"][:], scalar1=-1.0, scalar2=cols[:, 3, cc:cc + 1],
                                                      op0=ALU.add, op1=ALU.mult), reads=[B["av"], bcols], writes=[B["t1"]])
                S.op("dve", lambda e: e.scalar_tensor_tensor(out=T["kp"][:], in0=T["t1"][:], scalar=1.0, in1=p_k[:], op0=ALU.add, op1=ALU.mult),
                     reads=[B["t1"], B["p_k"]], writes=[B["kp"]])
                S.op("dve", lambda e: e.tensor_tensor_scan(out=T["csg"][:], data0=rmask[:], data1=T["sgu"][:], initial=0.0,
                                                           op0=ALU.mult, op1=ALU.add), reads=[B["rmask"], B["sgu"]], writes=[B["csg"]])
                S.op("dve", lambda e: e.tensor_tensor(out=T["cse"][:], in0=T["csg"][:], in1=T["sgu"][:], op=ALU.subtract),
                     reads=[B["csg"], B["sgu"]], writes=[B["cse"]])
                S.op("act", lambda e: e.activation(out=T["eW"][:], in_=T["csg"][:], func=AF.Exp, scale=-C0), reads=[B["csg"]], writes=[B["eW"]])
                S.op("act", lambda e: e.activation(out=T["eWi"][:], in_=T["csg"][:], func=AF.Exp, scale=C0), reads=[B["csg"]], writes=[B["eWi"]])
                S.op("act", lambda e: e.activation(out=T["eWe"][:], in_=T["cse"][:], func=AF.Exp, scale=-C0), reads=[B["cse"]], writes=[B["eWe"]])
                S.op("dve", lambda e: e.tensor_tensor(out=T["t2"][:], in0=T["kkn"][:], in1=T["av"][:], op=ALU.mult),
                     reads=[B["kkn"], B["av"]], writes=[B["t2"]])
                outs = []
                o, bo = ob[no % 8], bob[no % 8]; no += 1
                S.op("dve", lambda e: e.tensor_tensor(out=o[:], in0=p_r[:], in1=T["eW"][:], op=ALU.mult), reads=[B["p_r"], B["eW"]], writes=[bo])
                outs.append((o, bo, RtT, RB))
                o, bo = ob[no % 8], bob[no % 8]; no += 1
                S.op("dve", lambda e: e.tensor_tensor(out=o[:], in0=T["kp"][:], in1=T["eWi"][:], op=ALU.mult), reads=[B["kp"], B["eWi"]], writes=[bo])
                outs.append((o, bo, KtT, KB))
                o, bo = ob[no % 8], bob[no % 8]; no += 1
                S.op("dve", lambda e: e.scalar_tensor_tensor(out=o[:], in0=T["kkn"][:], scalar=-1.0, in1=T["eWe"][:], op0=ALU.mult, op1=ALU.mult),
                     reads=[B["kkn"], B["eWe"]], writes=[bo])
                outs.append((o, bo, AtT, AB))
                o, bo = ob[no % 8], bob[no % 8]; no += 1
                S.op("dve", lambda e: e.tensor_tensor(out=o[:], in0=T["t2"][:], in1=T["eWi"][:], op=ALU.mult), reads=[B["t2"], B["eWi"]], writes=[bo])
                outs.append((o, bo, BtT, BB))
                for (o, bo, dst, bdst) in outs:
                    S.dma("sp", dst[fs, t0:t0 + G], o[:], bo, reads=[bo], writes=[bdst])
                w_, bw_ = wc[cc % 2], bwc[cc % 2]
                S.op("dve", lambda e: e.tensor_copy(out=w_[:], in_=T["eW"][:, 63:G:64]), reads=[B["eW"]], writes=[bw_])
                S.dma("sp", WC[fs, g * (G // 64):(g + 1) * (G // 64)], w_[:], bw_, reads=[bw_], writes=[WCB])
        S.barrier()
        P.release()
        for b in [bW, bcols] + bwc + bob:
            S.release(b)


def rwkv_tm_phase(nc, S, x_in, xin_b, gain_row, mix, a0, a1, a2, g1, g2, ka_, rk_, w_r, w_k, w_v,
                  Vtok, BV, Gt, bouts, ntok):
    G = 256
    ng = ntok // G
    with ExitStack() as es:
        sb = lambda name, shape, dt: es.enter_context(nc.sbuf_tensor(_uniq(name), shape, dt))
        ps = lambda name, shape, dt: es.enter_context(nc.psum_tensor(_uniq(name), shape, dt))
        P = RwPrep(nc, S, es, G, (0, 2, 3, 4, 5), x_in, xin_b, gain_row, mix)
        Wr = sb("Wr", [128, 8, D], BF16)
        Wk = sb("Wk", [128, 8, D], BF16)
        Wv = sb("Wv", [128, 8, D], BF16)
        A1 = sb("A1", [128, 8, 64], BF16)
        A2 = sb("A2", [64, D], BF16)
        G1 = sb("G1", [128, 8, 160], BF16)
        G2a = sb("G2a", [128, D], BF16)
        G2b = sb("G2b", [32, D], BF16)
        a0B = sb("a0B", [128, D], F32)
        kaB = sb("kaB", [128, D], F32)
        rkB = sb("rkB", [128, D], F32)
        ta = sb("ta", [64, G], BF16)
        sg1a = sb("sg1a", [128, G], BF16)
        sg1b = sb("sg1b", [32, G], BF16)
        tmp = sb("tmp", [128, D], F32)
        av = sb("av", [128, D], F32)
        t1 = sb("t1", [128, D], F32)
        kp = sb("kp", [128, D], F32)
        tmp2 = sb("tmp2", [128, D], F32)
        tmp3 = sb("tmp3", [128, 16, 64], F32)
        bsum = sb("bsum", [128, 16, 1], F32)
        bvo = [sb("bvo%d" % i, [128, 16, 64], F32) for i in range(2)]
        vto = [sb("vto%d" % i, [128, D], BF16) for i in range(2)]
        gto = [sb("gto%d" % i, [128, D], F32) for i in range(2)]
        p_t = ps("p_t", [128, G], F32)
        pp = [ps("pp%d" % i, [128, 2, 512], F32) for i in range(2)]
        bW, bB = S.dbuf("W"), S.dbuf("B")
        bbvo = [S.dbuf("bvo") for _ in range(2)]
        bvto = [S.dbuf("vto") for _ in range(2)]
        bgto = [S.dbuf("gto") for _ in range(2)]
        B = {n: S.buf(n) for n in ["ta", "sg1a", "sg1b", "tmp", "av", "t1", "kp", "tmp2", "tmp3", "bsum"]}
        B.update({n: S.pbuf(n) for n in ["p_t", "pp0", "pp1"]})
        bpp = [B["pp0"], B["pp1"]]
        for c in range(8):
            for (W_, w_) in ((Wr, w_r), (Wk, w_k), (Wv, w_v)):
                S.dma("pool", W_[:, c, :], w_.rearrange("(c p) f -> p c f", p=128)[:, c, :], bW, writes=[bW])
        S.dma("pool", A1[:], a1.rearrange("(c p) f -> p c f", p=128), bW, writes=[bW])
        S.dma("pool", G1[:], g1.rearrange("(c p) f -> p c f", p=128), bW, writes=[bW])
        S.dma("pool", A2[:], a2, bW, writes=[bW])
        S.dma("pool", G2a[:], g2[0:128, :], bW, writes=[bW])
        S.dma("pool", G2b[:], g2[128:160, :], bW, writes=[bW])
        for (t_, src) in ((a0B, a0), (kaB, ka_), (rkB, rk_)):
            S.dma("sp", t_[:], src.to_broadcast([128, D]), bB, writes=[bB])
        VB, BVB, GB = bouts
        npp = 0
        k = 0
        for g in range(ng):
            P.group(g)
            xr, xk, xv, xa, xg = P.xm[0], P.xm[2], P.xm[3], P.xm[4], P.xm[5]
            bxr, bxk, bxv, bxa, bxg = P.bxm[0], P.bxm[2], P.bxm[3], P.bxm[4], P.bxm[5]
            for c in range(8):
                S.op("pe", lambda e: e.matmul(p_t[0:64, :], lhsT=A1[:, c, :], rhs=xa[:, c, :], start=(c == 0), stop=(c == 7)),
                     reads=[bW, bxa], writes=[B["p_t"]], sig=(c == 7))
            S.op("act", lambda e: e.copy(out=ta[:], in_=p_t[0:64, :]), reads=[B["p_t"]], writes=[B["ta"]])
            for c in range(8):
                S.op("pe", lambda e: e.matmul(p_t[:, :], lhsT=G1[:, c, 0:128], rhs=xg[:, c, :], start=(c == 0), stop=(c == 7)),
                     reads=[bW, bxg], writes=[B["p_t"]], sig=(c == 7))
            S.op("act", lambda e: e.activation(out=sg1a[:], in_=p_t[:, :], func=AF.Sigmoid), reads=[B["p_t"]], writes=[B["sg1a"]])
            for c in range(8):
                S.op("pe", lambda e: e.matmul(p_t[0:32, :], lhsT=G1[:, c, 128:160], rhs=xg[:, c, :], start=(c == 0), stop=(c == 7)),
                     reads=[bW, bxg], writes=[B["p_t"]], sig=(c == 7))
            S.op("act", lambda e: e.activation(out=sg1b[:], in_=p_t[0:32, :], func=AF.Sigmoid), reads=[B["p_t"]], writes=[B["sg1b"]])
            for s in range(G // 128):
                ts = slice(s * 128, (s + 1) * 128)
                t0 = g * G + s * 128

                def big(xm_, bxm_, W_):
                    nonlocal npp
                    p, bp = pp[npp % 2], bpp[npp % 2]
                    npp += 1
                    for hh in range(2):
                        for c in range(8):
                            S.op("pe", lambda e: e.matmul(p[:, hh, :], lhsT=xm_[:, c, ts], rhs=W_[:, c, hh * 512:(hh + 1) * 512],
                                                          start=(c == 0), stop=(c == 7)), reads=[bW, bxm_], writes=[bp], sig=(c == 7))
                    return p, bp
                p, bp = pp[npp % 2], bpp[npp % 2]
                npp += 1
                for hh in range(2):
                    S.op("pe", lambda e: e.matmul(p[:, hh, :], lhsT=ta[:, ts], rhs=A2[:, hh * 512:(hh + 1) * 512], start=True, stop=True),
                         reads=[bW, B["ta"]], writes=[bp])
                S.op("dve", lambda e: e.tensor_tensor(out=tmp[:], in0=p[:].rearrange("p a b -> p (a b)"), in1=a0B[:], op=ALU.add),
                     reads=[bp, bB], writes=[B["tmp"]])
                S.op("act", lambda e: e.activation(out=av[:], in_=tmp[:], func=AF.Sigmoid), reads=[B["tmp"]], writes=[B["av"]])
                S.op("dve", lambda e: e.scalar_tensor_tensor(out=t1[:], in0=av[:], scalar=-1.0, in1=kaB[:], op0=ALU.add, op1=ALU.mult),
                     reads=[B["av"], bB], writes=[B["t1"]])
                p, bp = big(xk, bxk, Wk)
                S.op("dve", lambda e: e.scalar_tensor_tensor(out=kp[:], in0=t1[:], scalar=1.0, in1=p[:].rearrange("p a b -> p (a b)"),
                                                             op0=ALU.add, op1=ALU.mult), reads=[B["t1"], bp], writes=[B["kp"]])
                p, bp = big(xr, bxr, Wr)
                S.op("dve", lambda e: e.tensor_tensor(out=tmp2[:], in0=p[:].rearrange("p a b -> p (a b)"), in1=rkB[:], op=ALU.mult),
                     reads=[bp, bB], writes=[B["tmp2"]])
                S.op("dve", lambda e: e.tensor_tensor(out=tmp3[:].rearrange("p a b -> p (a b)"), in0=tmp2[:], in1=kp[:], op=ALU.mult),
                     reads=[B["tmp2"], B["kp"]], writes=[B["tmp3"]])
                S.op("dve", lambda e: e.tensor_reduce(out=bsum[:], in_=tmp3[:], axis=AX.X, op=ALU.add), reads=[B["tmp3"]], writes=[B["bsum"]])
                p, bp = big(xv, bxv, Wv)
                o, bo = bvo[k % 2], bbvo[k % 2]
                S.op("dve", lambda e: e.tensor_tensor(out=o[:], in0=p[:].rearrange("p a (h d) -> p (a h) d", d=64),
                                                      in1=bsum[:].to_broadcast([128, 16, 64]), op=ALU.mult), reads=[bp, B["bsum"]], writes=[bo])
                S.dma("sp", BV[t0:t0 + 128, :], o[:].rearrange("p a b -> p (a b)"), bo, reads=[bo], writes=[BVB])
                o, bo = vto[k % 2], bvto[k % 2]
                S.op("act", lambda e: e.copy(out=o[:], in_=p[:].rearrange("p a b -> p (a b)")), reads=[bp], writes=[bo])
                S.dma("sp", Vtok[t0:t0 + 128, :], o[:], bo, reads=[bo], writes=[VB])
                p, bp = pp[npp % 2], bpp[npp % 2]
                npp += 1
                for hh in range(2):
                    S.op("pe", lambda e: e.matmul(p[:, hh, :], lhsT=sg1a[:, ts], rhs=G2a[:, hh * 512:(hh + 1) * 512], start=True, stop=False),
                         reads=[bW, B["sg1a"]], writes=[bp], sig=False)
                    S.op("pe", lambda e: e.matmul(p[:, hh, :], lhsT=sg1b[:, ts], rhs=G2b[:, hh * 512:(hh + 1) * 512], start=False, stop=True),
                         reads=[bW, B["sg1b"]], writes=[bp])
                o, bo = gto[k % 2], bgto[k % 2]
                S.op("act", lambda e: e.copy(out=o[:], in_=p[:].rearrange("p a b -> p (a b)")), reads=[bp], writes=[bo])
                S.dma("sp", Gt[t0:t0 + 128, :], o[:], bo, reads=[bo], writes=[GB])
                k += 1
        S.barrier()
        P.release()
        for b in [bW, bB] + bbvo + bvto + bgto:
            S.release(b)


def rwkv_scan_phase(nc, S, RtT, KtT, AtT, BtT, WC, Vtok, Ysc, bins, bY, ntok, NI=4):
    nch = ntok // 64
    ngr = nch // 8
    with ExitStack() as es:
        sb = lambda name, shape, dt: es.enter_context(nc.sbuf_tensor(_uniq(name), shape, dt))
        ps = lambda name, shape, dt: es.enter_context(nc.psum_tensor(_uniq(name), shape, dt))
        MU = sb("MU", [128, 128], F32)
        MUI = sb("MUI", [128, 128], F32)
        ML = sb("ML", [128, 128], F32)
        I32 = sb("I32", [128, 128], F32)
        identb = sb("identb", [128, 128], BF16)
        bconst = S.buf("const")
        for (m, chm, pat, op) in ((MU, -1, 1, ALU.is_gt), (MUI, -1, 1, ALU.is_ge), (ML, 1, -1, ALU.is_gt)):
            S.op("pool", lambda e: e.memset(m[:], 1.0), writes=[bconst])
            S.op("pool", lambda e: e.affine_select(out=m[:], in_=m[:], pattern=[[pat, 128]], compare_op=op, fill=0.0, base=0,
                                                   channel_multiplier=chm), reads=[bconst], writes=[bconst])
        make_ident(nc, S, I32, bconst)
        make_ident(nc, S, identb, bconst)

        class Slot:
            pass
        slots = []
        for si in range(NI):
            s = Slot()
            n_ = lambda x: "%s_%d" % (x, si)
            s.AR = [sb(n_("AR%d" % j), [128, 8, 2, 128], BF16) for j in range(2)]
            s.Bd = [sb(n_("Bd%d" % j), [128, 8, 128], BF16) for j in range(2)]
            s.Kd = [sb(n_("Kd%d" % j), [128, 8, 128], BF16) for j in range(2)]
            s.bbd = [S.dbuf(n_("bd0")), S.dbuf(n_("bd1"))]
            s.Vs = [sb(n_("Vs%d" % j), [128, 8, 64], BF16) for j in range(2)]
            s.Yo = [sb(n_("Yo%d" % j), [128, 8, 64], F32) for j in range(2)]
            s.bYo = [S.dbuf(n_("Yo0")), S.dbuf(n_("Yo1"))]
            s.wcs = sb(n_("wcs"), [128, nch], F32)
            s.bwcs = S.dbuf(n_("wcs"))
            s.QX = [sb(n_("QX%d" % j), [128, 2, 128], BF16) for j in range(2)]
            s.P = [sb(n_("P%d" % j), [128, 128], BF16) for j in range(2)]
            s.bQX = [S.buf("QX") for _ in range(2)]
            s.bP = [S.buf("P") for _ in range(2)]
            for k in ("Mak", "Mrb", "Mrk", "BtT", "KtT"):
                setattr(s, k, [sb(n_(k) + "_%d" % q_, [128, 128], BF16) for q_ in range(2)])
                setattr(s, "b" + k, [S.buf(k) for _ in range(2)])
            s.TT = [sb(n_("TT%d" % q_), [128, 128], BF16) for q_ in range(2)]
            s.bTT = [S.buf("TT") for _ in range(2)]
            s.Xs = sb(n_("Xs"), [128, 64], BF16)
            s.Ub = sb(n_("Ub"), [128, 64], BF16)
            s.Sw = sb(n_("Sw"), [128, 64], F32)
            s.St = sb(n_("St"), [128, 64], F32)
            s.Sb = sb(n_("Sb"), [128, 64], BF16)
            s.bXs, s.bUb, s.bSw, s.bSt, s.bSb = [S.buf(k) for k in ("Xs", "Ub", "Sw", "St", "Sb")]
            s.psA = ps(n_("psA"), [128, 512], F32)
            s.psB = ps(n_("psB"), [128, 4, 128], F32)
            s.bA, s.bB = S.pbuf("bankA"), S.pbuf("bankB")
            s.ptr = s.psB[:, 3, :].bitcast(BF16)
            for j in range(2):
                S.op("pool", lambda e: e.memset(s.AR[j][:], 0.0), writes=[s.bbd[j]])
                S.op("pool", lambda e: e.memset(s.Bd[j][:], 0.0), writes=[s.bbd[j]])
                S.op("pool", lambda e: e.memset(s.Kd[j][:], 0.0), writes=[s.bbd[j]])
            slots.append(s)

        bRs, bKs, bAs, bBs, bWC, bV = bins

        def load_group(s, hp, gg):
            j = gg % 2
            t0 = gg * 512
            for h in range(2):
                r0 = hp * 128 + h * 64
                hs, cs = slice(h * 64, (h + 1) * 64), slice(h * 64, (h + 1) * 64)
                for (dst, src, bsrc) in ((s.AR[j][hs, :, 0, cs], AtT, bAs), (s.AR[j][hs, :, 1, cs], RtT, bRs),
                                         (s.Bd[j][hs, :, cs], BtT, bBs), (s.Kd[j][hs, :, cs], KtT, bKs)):
                    S.dma("sp", dst, src[r0:r0 + 64, t0:t0 + 512].rearrange("p (c j) -> p c j", j=64), s.bbd[j],
                          reads=[bsrc], writes=[s.bbd[j]])
                S.dma("sp", s.Vs[j][hs, :, :], Vtok[t0:t0 + 512, r0:r0 + 64].rearrange("(c j) v -> j c v", j=64),
                      s.bbd[j], reads=[bV], writes=[s.bbd[j]])

        def T_steps(gg, c):
            j = gg % 2
            q = (gg * 8 + c) % 2
            steps = []

            def st1():
                for s in slots:
                    AR, A, Bd, K = s.AR[j][:, c, :, :], s.AR[j][:, c, 0, :], s.Bd[j][:, c, :], s.Kd[j][:, c, :]
                    S.op("pe", lambda e: e.matmul(s.psA[:, 0:256], lhsT=Bd, rhs=AR, start=True, stop=True), reads=[s.bbd[j]], writes=[s.bA], sig=False)
                    S.op("pe", lambda e: e.matmul(s.psA[:, 256:512], lhsT=K, rhs=AR, start=True, stop=True), reads=[s.bbd[j]], writes=[s.bA])
                    S.op("pe", lambda e: e.matmul(s.psB[:, 1, :], lhsT=A, rhs=Bd, start=True, stop=True), reads=[s.bbd[j]], writes=[s.bB], sig=False)
                    S.op("pe", lambda e: e.transpose(s.ptr[:, 0:128], Bd, identb[:]), reads=[s.bbd[j], bconst], writes=[s.bB], sig=False)
                    S.op("pe", lambda e: e.transpose(s.ptr[:, 128:256], K, identb[:]), reads=[s.bbd[j], bconst], writes=[s.bB])
            steps.append(st1)

            def st2():
                for s in slots:
                    S.op("dve", lambda e: e.tensor_tensor(out=s.QX[0][:, 0, :], in0=s.psA[:, 0:128], in1=MU[:], op=ALU.mult),
                         reads=[s.bA, bconst], writes=[s.bQX[0]])
                    S.op("dve", lambda e: e.tensor_tensor(out=s.QX[1][:, 1, :], in0=s.QX[0][:, 0, :], in1=I32[:], op=ALU.add),
                         reads=[s.bQX[0], bconst], writes=[s.bQX[1]])
                    S.op("dve", lambda e: e.tensor_tensor(out=s.P[0][:], in0=s.psB[:, 1, :], in1=ML[:], op=ALU.mult),
                         reads=[s.bB, bconst], writes=[s.bP[0]])
            steps.append(st2)

            def st3():
                for s in slots:
                    for (nm, lo, msk) in (("Mrb", 128, MUI), ("Mak", 256, MU), ("Mrk", 384, MUI)):
                        S.op("dve", lambda e: e.tensor_tensor(out=getattr(s, nm)[q][:], in0=s.psA[:, lo:lo + 128], in1=msk[:], op=ALU.mult),
                             reads=[s.bA, bconst], writes=[getattr(s, "b" + nm)[q]])
                    S.op("act", lambda e: e.copy(out=s.BtT[q][:], in_=s.ptr[:, 0:128]), reads=[s.bB], writes=[s.bBtT[q]])
                    S.op("act", lambda e: e.copy(out=s.KtT[q][:], in_=s.ptr[:, 128:256]), reads=[s.bB], writes=[s.bKtT[q]])
            steps.append(st3)

            for lvl in range(6):
                cur = 0 if lvl == 0 else lvl % 2
                nxt = 1 - cur

                def sa(lvl=lvl, cur=cur):
                    for s in slots:
                        Q, X, Pm = s.QX[cur][:, 0, :], s.QX[cur][:, 1, :], s.P[cur][:]
                        rd = [s.bP[cur], s.bQX[cur]]
                        if lvl == 0:
                            S.op("pe", lambda e: e.matmul(s.psA[:, 0:128], lhsT=Pm, rhs=Q, start=True, stop=True), reads=rd, writes=[s.bA], sig=False)
                        elif lvl <= 3:
                            S.op("pe", lambda e: e.matmul(s.psA[:, 0:256], lhsT=Pm, rhs=s.QX[cur][:, :, :], start=True, stop=True),
                                 reads=rd, writes=[s.bA], sig=False)
                        else:
                            S.op("pe", lambda e: e.matmul(s.psA[:, 128:256], lhsT=Pm, rhs=X, start=True, stop=True), reads=rd, writes=[s.bA],
                                 sig=(lvl == 5))
                        if lvl <= 4:
                            S.op("pe", lambda e: e.matmul(s.psA[:, 256:384], lhsT=Q, rhs=Pm, start=True, stop=True), reads=rd, writes=[s.bA])

                def sb_(lvl=lvl, cur=cur, nxt=nxt):
                    for s in slots:
                        if lvl <= 3:
                            S.op("act", lambda e: e.copy(out=s.QX[nxt][:, 0, :], in_=s.psA[:, 0:128]), reads=[s.bA], writes=[s.bQX[nxt]])
                        if lvl <= 4:
                            S.op("act", lambda e: e.copy(out=s.P[nxt][:], in_=s.psA[:, 256:384]), reads=[s.bA], writes=[s.bP[nxt]])
                        if 1 <= lvl <= 4:
                            S.op("dve", lambda e: e.tensor_tensor(out=s.QX[nxt][:, 1, :], in0=s.psA[:, 128:256], in1=s.QX[cur][:, 1, :], op=ALU.add),
                                 reads=[s.bA, s.bQX[cur]], writes=[s.bQX[nxt]])
                        if lvl == 5:
                            S.op("dve", lambda e: e.tensor_tensor(out=s.TT[q][:], in0=s.psA[:, 128:256], in1=s.QX[cur][:, 1, :], op=ALU.add),
                                 reads=[s.bA, s.bQX[cur]], writes=[s.bTT[q]])
                steps += [sa, sb_]
            return steps

        def S_steps(gg, c):
            j = gg % 2
            ch = gg * 8 + c
            q = ch % 2

            def s1():
                for s in slots:
                    A = s.AR[j][:, c, 0, :]
                    S.op("pe", lambda e: e.matmul(s.psB[:, 0, 0:64], lhsT=A, rhs=s.Sb[:], start=True, stop=False),
                         reads=[s.bbd[j], s.bSb], writes=[s.bB], sig=False)
                    S.op("pe", lambda e: e.matmul(s.psB[:, 0, 0:64], lhsT=s.Mak[q][:], rhs=s.Vs[j][:, c, :], start=False, stop=True),
                         reads=[s.bMak[q], s.bbd[j]], writes=[s.bB])
                    S.op("pool", lambda e: e.tensor_scalar(out=s.Sw[:], in0=s.St[:], scalar1=s.wcs[:, ch:ch + 1], scalar2=None, op0=ALU.mult),
                         reads=[s.bSt, s.bwcs], writes=[s.bSw])

            def s2():
                for s in slots:
                    S.op("act", lambda e: e.copy(out=s.Xs[:], in_=s.psB[:, 0, 0:64]), reads=[s.bB], writes=[s.bXs])

            def s3():
                for s in slots:
                    S.op("pe", lambda e: e.matmul(s.psB[:, 1, 0:64], lhsT=s.TT[q][:], rhs=s.Xs[:], start=True, stop=True),
                         reads=[s.bTT[q], s.bXs], writes=[s.bB])

            def s4():
                for s in slots:
                    S.op("act", lambda e: e.copy(out=s.Ub[:], in_=s.psB[:, 1, 0:64]), reads=[s.bB], writes=[s.bUb])

            def s5():
                for s in slots:
                    R = s.AR[j][:, c, 1, :]
                    pY, pS = s.psB[:, 2, 0:64], s.psB[:, 0, 0:64]
                    S.op("pe", lambda e: e.matmul(pY, lhsT=R, rhs=s.Sb[:], start=True, stop=False),
                         reads=[s.bbd[j], s.bSb], writes=[s.bB], sig=False)
                    S.op("pe", lambda e: e.matmul(pY, lhsT=s.Mrb[q][:], rhs=s.Ub[:], start=False, stop=False),
                         reads=[s.bMrb[q], s.bUb], writes=[s.bB], sig=False)
                    S.op("pe", lambda e: e.matmul(pY, lhsT=s.Mrk[q][:], rhs=s.Vs[j][:, c, :], start=False, stop=True),
                         reads=[s.bMrk[q], s.bbd[j]], writes=[s.bB], sig=False)
                    S.op("pe", lambda e: e.matmul(pS, lhsT=s.BtT[q][:], rhs=s.Ub[:], start=True, stop=False),
                         reads=[s.bBtT[q], s.bUb], writes=[s.bB], sig=False)
                    S.op("pe", lambda e: e.matmul(pS, lhsT=s.KtT[q][:], rhs=s.Vs[j][:, c, :], start=False, stop=True),
                         reads=[s.bKtT[q], s.bbd[j]], writes=[s.bB])

            def s6():
                for s in slots:
                    S.op("dve", lambda e: e.scalar_tensor_tensor(out=s.St[:], in0=s.psB[:, 0, 0:64], scalar=s.wcs[:, ch:ch + 1], in1=s.Sw[:],
                                                                 op0=ALU.mult, op1=ALU.add), reads=[s.bB, s.bwcs, s.bSw], writes=[s.bSt])
                    S.op("dve", lambda e: e.tensor_copy(out=s.Yo[j][:, c, :], in_=s.psB[:, 2, 0:64]), reads=[s.bB], writes=[s.bYo[j]])
                    S.op("act", lambda e: e.copy(out=s.Sb[:], in_=s.St[:]), reads=[s.bSt], writes=[s.bSb])
            return [s1, s2, s3, s4, s5, s6]

        for rnd in range(8 // NI):
            hps = [rnd * NI + i for i in range(NI)]
            for s, hp in zip(slots, hps):
                S.dma("sp", s.wcs[:], WC[hp * 128:(hp + 1) * 128, 0:nch], s.bwcs, reads=[bWC], writes=[s.bwcs])
                S.op("pool", lambda e: e.memset(s.St[:], 0.0), writes=[s.bSt])
                S.op("pool", lambda e: e.memset(s.Sb[:], 0.0), writes=[s.bSb])
                load_group(s, hp, 0)
            if ngr > 1:
                for s, hp in zip(slots, hps):
                    load_group(s, hp, 1)
            for st in T_steps(0, 0):
                st()
            for gg in range(ngr):
                j = gg % 2
                for c in range(8):
                    ch = gg * 8 + c
                    if ch + 1 < nch:
                        ng_, nc_ = (gg, c + 1) if c < 7 else (gg + 1, 0)
                        tsteps = T_steps(ng_, nc_)
                    else:
                        tsteps = []
                    ssteps = S_steps(gg, c)
                    ti = 0
                    for ss_ in ssteps:
                        for _ in range(3):
                            if ti < len(tsteps):
                                tsteps[ti]()
                                ti += 1
                        ss_()
                    while ti < len(tsteps):
                        tsteps[ti]()
                        ti += 1
                for s, hp in zip(slots, hps):
                    for h in range(2):
                        c0 = hp * 128 + h * 64
                        S.dma("sp", Ysc[gg * 512:(gg + 1) * 512, c0:c0 + 64].rearrange("(c j) v -> j c v", j=64),
                              s.Yo[j][h * 64:(h + 1) * 64, :, :], s.bYo[j], reads=[s.bYo[j]], writes=[bY])
                if gg + 2 < ngr:
                    for s, hp in zip(slots, hps):
                        load_group(s, hp, gg + 2)
        S.barrier()
        for s in slots:
            for b_ in s.bbd + s.bYo + [s.bwcs]:
                S.release(b_)


def rwkv_post_phase(nc, S, Ysc, BV, Gt, lg_row, lb_row, ZT, bins, bZT, ntok, gn_eps=64e-5):
    with ExitStack() as es:
        sb = lambda name, shape, dt: es.enter_context(nc.sbuf_tensor(_uniq(name), shape, dt))
        ps = lambda name, shape, dt: es.enter_context(nc.psum_tensor(_uniq(name), shape, dt))
        lgB = sb("lgB", [128, D], F32)
        lbB = sb("lbB", [128, D], F32)
        ident = sb("ident", [128, 128], BF16)
        nh = sb("nh", [128, 16, 1], F32)
        yt = [sb("yt%d" % i, [128, 16, 64], F32) for i in range(2)]
        bvt = [sb("bvt%d" % i, [128, D], F32) for i in range(2)]
        gt = [sb("gt%d" % i, [128, D], F32) for i in range(2)]
        sm = sb("sm", [128, 16, 1], F32)
        vr = sb("vr", [128, 16, 1], F32)
        rstd = sb("rstd", [128, 16, 1], F32)
        yc = sb("yc", [128, 16, 64], F32)
        sq = sb("sq", [128, 16, 64], F32)
        yn = sb("yn", [128, 16, 64], F32)
        y2 = sb("y2", [128, D], F32)
        zb = [sb("zb%d" % i, [128, D], BF16) for i in range(2)]
        zT = [sb("zT%d" % i, [128, 8, 512], BF16) for i in range(2)]
        p_tr = [ps("p_tr%d" % i, [128, 8, 128], BF16) for i in range(2)]
        bC = S.dbuf("C")
        byt = [S.dbuf("yt") for _ in range(2)]
        bbvt = [S.dbuf("bvt") for _ in range(2)]
        bgt = [S.dbuf("gt") for _ in range(2)]
        bzT = [S.dbuf("zT") for _ in range(2)]
        B = {n: S.buf(n) for n in ["ident", "nh", "sm", "vr", "rstd", "yc", "sq", "yn", "y2", "zb0", "zb1"]}
        bp_tr = [S.pbuf("ptr") for _ in range(2)]
        bYs, bBV, bG = bins
        make_ident(nc, S, ident, B["ident"])
        S.op("pool", lambda e: e.memset(nh[:], -0.5), writes=[B["nh"]])
        S.dma("sp", lgB[:], lg_row.to_broadcast([128, D]), bC, writes=[bC])
        S.dma("sp", lbB[:], lb_row.to_broadcast([128, D]), bC, writes=[bC])
        nt = ntok // 128
        for t in range(nt):
            i = t % 2
            t0 = t * 128
            S.dma("sp", yt[i][:].rearrange("p a b -> p (a b)"), Ysc[t0:t0 + 128, :], byt[i], reads=[bYs], writes=[byt[i]])
            S.dma("sp", bvt[i][:], BV[t0:t0 + 128, :], bbvt[i], reads=[bBV], writes=[bbvt[i]])
            S.dma("sp", gt[i][:], Gt[t0:t0 + 128, :], bgt[i], reads=[bG], writes=[bgt[i]])
            y3 = yt[i]
            S.op("dve", lambda e: e.tensor_reduce(out=sm[:], in_=y3[:], axis=AX.X, op=ALU.add), reads=[byt[i]], writes=[B["sm"]])
            S.op("dve", lambda e: e.tensor_scalar(out=sm[:], in0=sm[:], scalar1=1.0 / 64, scalar2=None, op0=ALU.mult), reads=[B["sm"]], writes=[B["sm"]])
            S.op("dve", lambda e: e.tensor_tensor(out=yc[:], in0=y3[:], in1=sm[:].to_broadcast([128, 16, 64]), op=ALU.subtract),
                 reads=[byt[i], B["sm"]], writes=[B["yc"]])
            S.op("dve", lambda e: e.tensor_tensor(out=sq[:], in0=yc[:], in1=yc[:], op=ALU.mult), reads=[B["yc"]], writes=[B["sq"]])
            S.op("dve", lambda e: e.tensor_reduce(out=vr[:], in_=sq[:], axis=AX.X, op=ALU.add), reads=[B["sq"]], writes=[B["vr"]])
            S.op("dve", lambda e: e.tensor_scalar(out=vr[:], in0=vr[:], scalar1=1.0 / 64, scalar2=gn_eps, op0=ALU.mult, op1=ALU.add),
                 reads=[B["vr"]], writes=[B["vr"]])
            S.op("pool", lambda e: e.tensor_tensor(out=rstd[:], in0=vr[:], in1=nh[:], op=ALU.pow), reads=[B["vr"], B["nh"]], writes=[B["rstd"]])
            S.op("dve", lambda e: e.tensor_tensor(out=yn[:], in0=yc[:], in1=rstd[:].to_broadcast([128, 16, 64]), op=ALU.mult),
                 reads=[B["yc"], B["rstd"]], writes=[B["yn"]])
            ynf = yn[:].rearrange("p a b -> p (a b)")
            S.op("dve", lambda e: e.tensor_tensor(out=y2[:], in0=ynf, in1=lgB[:], op=ALU.mult), reads=[B["yn"], bC], writes=[B["y2"]])
            S.op("dve", lambda e: e.tensor_tensor(out=y2[:], in0=y2[:], in1=lbB[:], op=ALU.add), reads=[B["y2"], bC], writes=[B["y2"]])
            S.op("dve", lambda e: e.tensor_tensor(out=y2[:], in0=y2[:], in1=bvt[i][:], op=ALU.add), reads=[B["y2"], bbvt[i]], writes=[B["y2"]])
            z, bz = zb[i], B["zb%d" % i]
            S.op("dve", lambda e: e.tensor_tensor(out=z[:], in0=y2[:], in1=gt[i][:], op=ALU.mult), reads=[B["y2"], bgt[i]], writes=[bz])
            pt, bpt = p_tr[i], bp_tr[i]
            for c in range(8):
                S.op("pe", lambda e: e.transpose(pt[:, c, :], z[:, c * 128:(c + 1) * 128], ident[:]), reads=[bz, B["ident"]], writes=[bpt], sig=(c == 7))
            gi = (t // 4) % 2
            S.op("act", lambda e: e.copy(out=zT[gi][:, :, (t % 4) * 128:(t % 4 + 1) * 128], in_=pt[:]), reads=[bpt], writes=[bzT[gi]])
            if t % 4 == 3:
                g0 = (t // 4) * 512
                for c in range(8):
                    S.dma("sp", ZT[c * 128:(c + 1) * 128, g0:g0 + 512], zT[gi][:, c, :], bzT[gi], reads=[bzT[gi]], writes=[bZT])
        S.barrier()
        for b in [bC] + byt + bbvt + bgt + bzT:
            S.release(b)


def build_program(ntok=SEQ):
    nc = bass.Bass("TRN2", target_bir_lowering=False)
    di = lambda n, s: nc.dram_tensor(n, list(s), F32, kind="ExternalInput").ap()
    x = di("x", [ntok, D])
    ffn_norm = di("ffn_norm", [4, D])
    wg = di("ffn_w_gate", [2, 2, D, DFF])
    wu = di("ffn_w_up", [2, 2, D, DFF])
    wd = di("ffn_w_down", [2, 2, DFF, D])
    mix_norm = di("mix_norm", [2, D])
    dbias = di("dbias", [12, 128, 2, 128])
    w_in = di("attn_w_in", [D, 3072])
    qn = di("attn_q_norm", [64, 1])
    kn = di("attn_k_norm", [64, 1])
    w_out = di("attn_w_out", [512, D])
    rw_mix = di("rw_mix", [6, D])
    rows = {n: di(n, [1, D]) for n in ("rw_w0", "rw_a0", "rw_kk", "rw_ka", "rw_rk", "rw_lnx_g", "rw_lnx_b")}
    rw_w1 = di("rw_w1", [D, 64]); rw_w2 = di("rw_w2", [64, D]); rw_a1 = di("rw_a1", [D, 64]); rw_a2 = di("rw_a2", [64, D])
    rw_g1 = di("rw_g1", [D, 160]); rw_g2 = di("rw_g2", [160, D])
    rw_wr = di("rw_wr", [D, D]); rw_wk = di("rw_wk", [D, D]); rw_wv = di("rw_wv", [D, D]); rw_wo = di("rw_wo", [D, D])
    out = nc.dram_tensor("out", [ntok, D], F32, kind="ExternalOutput").ap()
    scr = lambda n, s, dt: nc.dram_tensor(n, list(s), dt, kind="Internal").ap()
    xa = scr("xa", [ntok, D], F32); xb = scr("xb", [ntok, D], F32)
    QT = scr("QT", [D, ntok], BF16); KT = scr("KT", [D, ntok], BF16); V = scr("V", [ntok, D], BF16); MT = scr("MT", [512, ntok], BF16)
    RtT, KtT, AtT, BtT = [scr(n, [D, ntok], BF16) for n in ("RtT", "KtT", "AtT", "BtT")]
    WC = scr("WC", [D, ntok // 64], F32)
    Vtok = scr("Vtok", [ntok, D], BF16); BV = scr("BV", [ntok, D], F32); Gt = scr("Gt", [ntok, D], F32); Ysc = scr("Ysc", [ntok, D], F32)
    ZT = scr("ZT", [D, ntok], BF16)

    S = Sched(nc, n_dma_sems=24)
    nb = lambda n: S.buf(n, acc=True)
    bx, bxa, bxb, bout = nb("x"), nb("xa"), nb("xb"), nb("out")
    bQT, bKT, bV, bMT = nb("QT"), nb("KT"), nb("V"), nb("MT")
    bR, bK, bA, bB, bWC, bVt, bBV, bG, bY, bZ = [nb(n) for n in ("R", "K", "A", "B", "WC", "Vt", "BV", "G", "Y", "Z")]

    ffn_phase(nc, S, x, xa, bx, bxa, wg[0, 0], wu[0, 0], wd[0, 0], ffn_norm[0:1, :], ntok)
    attn_in_phase(nc, S, xa, bxa, w_in, mix_norm[0:1, :], qn, kn, QT, KT, V, bQT, bKT, bV, ntok)
    sb_attn_phase(nc, S, QT, KT, V, MT, bQT, bKT, bV, bMT, ntok)
    dil_attn_phase(nc, S, QT, KT, V, MT, dbias, bQT, bKT, bV, bMT, ntok)
    out_proj_phase(nc, S, xa, xb, bxa, bxb, MT, bMT, w_out, 512, ntok)
    ffn_phase(nc, S, xb, xa, bxb, bxa, wg[0, 1], wu[0, 1], wd[0, 1], ffn_norm[1:2, :], ntok)
    ffn_phase(nc, S, xa, xb, bxa, bxb, wg[1, 0], wu[1, 0], wd[1, 0], ffn_norm[2:3, :], ntok)
    rwkv_fm_phase(nc, S, xb, bxb, mix_norm[1:2, :], rw_mix, rows["rw_w0"], rw_w1, rw_w2, rows["rw_a0"], rw_a1, rw_a2,
                  rows["rw_kk"], rows["rw_ka"], rw_wr, rw_wk, RtT, KtT, AtT, BtT, WC, [bR, bK, bA, bB, bWC], ntok)
    rwkv_tm_phase(nc, S, xb, bxb, mix_norm[1:2, :], rw_mix, rows["rw_a0"], rw_a1, rw_a2, rw_g1, rw_g2, rows["rw_ka"], rows["rw_rk"],
                  rw_wr, rw_wk, rw_wv, Vtok, BV, Gt, [bVt, bBV, bG], ntok)
    rwkv_scan_phase(nc, S, RtT, KtT, AtT, BtT, WC, Vtok, Ysc, [bR, bK, bA, bB, bWC, bVt], bY, ntok)
    rwkv_post_phase(nc, S, Ysc, BV, Gt, rows["rw_lnx_g"], rows["rw_lnx_b"], ZT, [bY, bBV, bG], bZ, ntok)
    out_proj_phase(nc, S, xb, xa, bxb, bxa, ZT, bZ, rw_wo, 1024, ntok)
    ffn_phase(nc, S, xa, out, bxa, bout, wg[1, 1], wu[1, 1], wd[1, 1], ffn_norm[3:4, :], ntok)
    S.wait_for("sp", [bout])
    return nc


def kernel(x, ffn_norm, ffn_w_gate, ffn_w_up, ffn_w_down, mix_norm, rel_bias,
           attn_w_in, attn_q_norm, attn_k_norm, attn_w_out,
           rw_mix, rw_w0, rw_w1, rw_w2, rw_a0, rw_a1, rw_a2, rw_g1, rw_g2,
           rw_kk, rw_ka, rw_rk, rw_wr, rw_wk, rw_wv, rw_wo, rw_lnx_g, rw_lnx_b):
    f = lambda a: np.ascontiguousarray(np.asarray(a, dtype=np.float32))
    x = f(x)
    n = x.shape[0]
    shared = {
        "ffn_norm": f(ffn_norm).reshape(4, D), "ffn_w_gate": f(ffn_w_gate), "ffn_w_up": f(ffn_w_up), "ffn_w_down": f(ffn_w_down),
        "mix_norm": f(mix_norm), "dbias": dil_bias_host(f(rel_bias)),
        "attn_w_in": f(attn_w_in)[0], "attn_q_norm": f(attn_q_norm).reshape(64, 1), "attn_k_norm": f(attn_k_norm).reshape(64, 1),
        "attn_w_out": f(attn_w_out)[0], "rw_mix": f(rw_mix)[0],
        "rw_w0": f(rw_w0).reshape(1, D), "rw_a0": f(rw_a0).reshape(1, D), "rw_kk": f(rw_kk).reshape(1, D), "rw_ka": f(rw_ka).reshape(1, D),
        "rw_rk": f(rw_rk).reshape(1, D), "rw_lnx_g": f(rw_lnx_g).reshape(1, D), "rw_lnx_b": f(rw_lnx_b).reshape(1, D),
        "rw_w1": f(rw_w1)[0], "rw_w2": f(rw_w2)[0], "rw_a1": f(rw_a1)[0], "rw_a2": f(rw_a2)[0], "rw_g1": f(rw_g1)[0], "rw_g2": f(rw_g2)[0],
        "rw_wr": f(rw_wr)[0], "rw_wk": f(rw_wk)[0], "rw_wv": f(rw_wv)[0], "rw_wo": f(rw_wo)[0],
    }
    nc = build_program(x.shape[1])
    in_maps = [dict(shared, x=x[i]) for i in range(n)]
    res = run_bass_kernel_spmd(nc, in_maps, core_ids=list(range(n)))
    return np.stack([np.asarray(r["out"]) for r in res.results], axis=0).astype(np.float32)
```

```python
import numpy as np
from contextlib import ExitStack
import concourse.bass as bass
import concourse.mybir as mybir
from concourse.bass_utils import run_bass_kernel_spmd

F32 = mybir.dt.float32
BF16 = mybir.dt.bfloat16
AF = mybir.ActivationFunctionType
ALU = mybir.AluOpType
AX = mybir.AxisListType

D = 1024
DFF = 2816
NF = DFF // 128
SEQ = 4096


_UID = [0]


def _uniq(name):
    _UID[0] += 1
    return "%s_u%d" % (name, _UID[0])


def _merge(d, s):
    for k, v in s.items():
        if d.get(k, 0) < v:
            d[k] = v


class Buf:
    __slots__ = ("name", "wr", "rd", "acc", "dkey", "excl")

    def __init__(self, name, acc=False, excl=False):
        self.name = name
        self.wr = {}
        self.rd = {}
        self.acc = acc
        self.dkey = None
        self.excl = excl


class Sched:
    ENG = ("pe", "act", "dve", "pool", "sp")

    def __init__(self, nc, n_dma_sems=40):
        self.nc = nc
        self.eng = {"pe": nc.tensor, "act": nc.scalar, "dve": nc.vector, "pool": nc.gpsimd, "sp": nc.sync}
        self.sems = {}
        self.val = {}
        self.seen = {e: {} for e in self.ENG}
        self.epoch = 0
        self.ekey = {}
        self._new_engine_sems()
        self.dma_pool = []
        for i in range(n_dma_sems):
            k = "dma%d" % i
            self.sems[k] = nc.semaphore(k).__enter__()
            self.val[k] = 0
            self.dma_pool.append(k)
        self.nwait = 0

    def _new_engine_sems(self):
        for e in self.ENG:
            k = "%s_e%d" % (e, self.epoch)
            self.sems[k] = self.nc.semaphore(k).__enter__()
            self.val[k] = 0
            self.ekey[e] = k

    def buf(self, name, acc=False):
        return Buf(name, acc)

    def pbuf(self, name):
        return Buf(name, False, True)

    def dbuf(self, name, acc=False):
        b = Buf(name, acc)
        b.dkey = self.dma_pool.pop()
        return b

    def release(self, b):
        self.dma_pool.append(b.dkey)
        b.dkey = None

    def _wait(self, e, deps):
        for k, v in deps.items():
            if v <= 0:
                continue
            if e == "pe" and k == self.ekey["pe"]:
                continue
            if self.seen[e].get(k, 0) < v:
                self.eng[e].wait_ge(self.sems[k], v)
                self.seen[e][k] = v
                self.nwait += 1

    def _deps(self, reads, writes, e=None):
        deps = {}
        for b in reads:
            _merge(deps, b.wr)
            if b.excl:
                own = self.ekey.get(e)
                _merge(deps, {k: v for k, v in b.rd.items() if k != own})
        for b in writes:
            _merge(deps, b.wr)
            _merge(deps, b.rd)
        return deps

    def _record(self, ev, reads, writes):
        for b in reads:
            _merge(b.rd, ev)
        for b in writes:
            if b.acc:
                _merge(b.wr, ev)
            else:
                b.wr = dict(ev)
                b.rd = {}

    def op(self, e, fn, reads=(), writes=(), sig=True):
        self._wait(e, self._deps(reads, writes, e))
        ins = fn(self.eng[e])
        k = self.ekey[e]
        if sig:
            ins.then_inc(self.sems[k], 1)
            self.val[k] += 1
            v = self.val[k]
        else:
            v = self.val[k] + 1
        self._record({k: v}, reads, writes)
        return ins

    def dma(self, q, out, in_, track, reads=(), writes=(), slow=False):
        self._wait(q, self._deps(reads, writes))
        if slow:
            ins = self.eng[q].dma_start(out=out, in_=in_, allow_slow_non_contiguous=True)
        else:
            ins = self.eng[q].dma_start(out=out, in_=in_)
        k = track.dkey
        ins.then_inc(self.sems[k], 16)
        self.val[k] += 16
        self._record({k: self.val[k]}, reads, writes)
        return ins

    def barrier(self, new_epoch=True):
        allv = {k: v for k, v in self.val.items() if v > 0}
        for e in self.ENG:
            self._wait(e, allv)
        if new_epoch:
            self.epoch += 1
            self._new_engine_sems()

    def wait_for(self, e, bufs):
        deps = {}
        for b in bufs:
            _merge(deps, b.wr)
            _merge(deps, b.rd)
        self._wait(e, deps)


def ffn_phase(nc, S, x_in, x_out, xin_b, xout_b, wg, wu, wd, gain_row, ntok, eps=1e-6):
    G = 256
    ng = ntok // G
    with ExitStack() as es:
        sb = lambda name, shape, dt: es.enter_context(nc.sbuf_tensor(_uniq(name), shape, dt))
        ps = lambda name, shape, dt: es.enter_context(nc.psum_tensor(_uniq(name), shape, dt))
        Wg = sb("Wg", [128, 8, DFF], BF16)
        Wu = sb("Wu", [128, 8, DFF], BF16)
        Wd = sb("Wd", [128, NF, D], BF16)
        gB = sb("gB", [128, D], F32)
        ident = sb("ident", [128, 128], BF16)
        xt = [sb("xt%d" % i, [128, D], F32) for i in range(4)]
        ot = [sb("ot%d" % i, [128, D], F32) for i in range(2)]
        hb = [sb("hb%d" % i, [128, D], BF16) for i in range(2)]
        hT = [sb("hT%d" % i, [128, 8, G], BF16) for i in range(2)]
        aT = [sb("aT%d" % i, [128, G], BF16) for i in range(3)]
        sg = [sb("sg%d" % i, [128, G], F32) for i in range(2)]
        junk = sb("junk", [128, D], BF16)
        ss = [sb("ss%d" % i, [128, 1], F32) for i in range(2)]
        rs = [sb("rs%d" % i, [128, 1], F32) for i in range(2)]
        nh = sb("nh", [128, 1], F32)
        p_gu = [ps("p_gu%d" % i, [128, 2, G], F32) for i in range(2)]
        p_dn = [ps("p_dn%d" % i, [128, 512], F32) for i in range(4)]
        p_tr = [ps("p_tr%d" % i, [128, 8, 128], BF16) for i in range(2)]

        bWg, bWu, bWd, bgB = S.dbuf("Wg"), S.dbuf("Wu"), S.dbuf("Wd"), S.dbuf("gB")
        bxt = [S.dbuf("xt%d" % i) for i in range(4)]
        bot = [S.dbuf("ot%d" % i) for i in range(2)]
        bhb = [S.buf("hb") for _ in range(2)]
        bhT = [S.buf("hT") for _ in range(2)]
        baT = [S.buf("aT") for _ in range(3)]
        bsg = [S.buf("sg") for _ in range(2)]
        bjunk = S.buf("junk")
        bss = [S.buf("ss") for _ in range(2)]
        brs = [S.buf("rs") for _ in range(2)]
        bnh, bident = S.buf("nh"), S.buf("ident")
        bp_gu = [S.pbuf("pgu") for _ in range(2)]
        bp_dn = [S.pbuf("pdn") for _ in range(4)]
        bp_tr = [S.pbuf("ptr") for _ in range(2)]

        S.op("pool", lambda e: e.memset(nh[:], -0.5), writes=[bnh])
        S.op("pool", lambda e: e.memset(ident[:], 0.0), writes=[bident])
        S.op("pool", lambda e: e.affine_select(out=ident[:], in_=ident[:], pattern=[[-1, 128]],
                                               compare_op=ALU.not_equal, fill=1.0, base=0,
                                               channel_multiplier=1), reads=[bident], writes=[bident])
        S.dma("sp", gB[:], gain_row.to_broadcast([128, D]), bgB, writes=[bgB])
        wg_v = wg.rearrange("(c p) f -> p c f", p=128)
        wu_v = wu.rearrange("(c p) f -> p c f", p=128)
        wd_v = wd.rearrange("(c p) f -> p c f", p=128)
        for c in range(8):
            S.dma("pool", Wg[:, c, :], wg_v[:, c, :], bWg, writes=[bWg])
            S.dma("pool", Wu[:, c, :], wu_v[:, c, :], bWu, writes=[bWu])
        for c in range(NF):
            S.dma("pool", Wd[:, c, :], wd_v[:, c, :], bWd, writes=[bWd])

        def load(g):
            for s in range(2):
                i = (g % 2) * 2 + s
                t0 = g * G + s * 128
                S.dma("sp", xt[i][:], x_in[t0:t0 + 128, :], bxt[i], reads=[xin_b], writes=[bxt[i]])

        def prep_dve(g):
            for s in range(2):
                i = (g % 2) * 2 + s
                S.op("dve", lambda e: e.scalar_tensor_tensor(out=junk[:], in0=xt[i][:], scalar=1.0, in1=xt[i][:],
                                                             op0=ALU.mult, op1=ALU.mult, accum_out=ss[s][:]),
                     reads=[bxt[i]], writes=[bjunk, bss[s]])
                S.op("dve", lambda e: e.tensor_scalar(out=ss[s][:], in0=ss[s][:], scalar1=1.0 / D, scalar2=eps,
                                                      op0=ALU.mult, op1=ALU.add), reads=[bss[s]], writes=[bss[s]])
                S.op("pool", lambda e: e.tensor_tensor(out=rs[s][:], in0=ss[s][:], in1=nh[:], op=ALU.pow),
                     reads=[bss[s], bnh], writes=[brs[s]])
                S.op("dve", lambda e: e.scalar_tensor_tensor(out=hb[s][:], in0=xt[i][:], scalar=rs[s][:], in1=gB[:],
                                                             op0=ALU.mult, op1=ALU.mult),
                     reads=[bxt[i], brs[s], bgB], writes=[bhb[s]])

        def prep_pe(g):
            for s in range(2):
                for c in range(8):
                    S.op("pe", lambda e: e.transpose(p_tr[s][:, c, :], hb[s][:, c * 128:(c + 1) * 128], ident[:]),
                         reads=[bhb[s], bident], writes=[bp_tr[s]], sig=(c == 7))
                S.op("act", lambda e: e.copy(out=hT[g % 2][:, :, s * 128:(s + 1) * 128], in_=p_tr[s][:]),
                     reads=[bp_tr[s]], writes=[bhT[g % 2]])

        def gate_up(g, j):
            h = hT[g % 2]
            pg = p_gu[j % 2]
            for c in range(8):
                S.op("pe", lambda e: e.matmul(pg[:, 0, :], lhsT=Wg[:, c, j * 128:(j + 1) * 128], rhs=h[:, c, :],
                                              start=(c == 0), stop=(c == 7)),
                     reads=[bWg, bhT[g % 2]], writes=[bp_gu[j % 2]], sig=False)
            for c in range(8):
                S.op("pe", lambda e: e.matmul(pg[:, 1, :], lhsT=Wu[:, c, j * 128:(j + 1) * 128], rhs=h[:, c, :],
                                              start=(c == 0), stop=(c == 7)),
                     reads=[bWu, bhT[g % 2]], writes=[bp_gu[j % 2]], sig=(c == 7))
            S.op("act", lambda e: e.activation(out=sg[j % 2][:], in_=pg[:, 0, :], func=AF.Silu),
                 reads=[bp_gu[j % 2]], writes=[bsg[j % 2]])
            S.op("dve", lambda e: e.tensor_tensor(out=aT[j % 3][:], in0=sg[j % 2][:], in1=pg[:, 1, :], op=ALU.mult),
                 reads=[bsg[j % 2], bp_gu[j % 2]], writes=[baT[j % 3]])

        def down(g, j):
            for s in range(2):
                for hh in range(2):
                    S.op("pe", lambda e: e.matmul(p_dn[s * 2 + hh][:], lhsT=aT[j % 3][:, s * 128:(s + 1) * 128],
                                                  rhs=Wd[:, j, hh * 512:(hh + 1) * 512],
                                                  start=(j == 0), stop=(j == NF - 1)),
                         reads=[baT[j % 3], bWd], writes=[bp_dn[s * 2 + hh]], sig=(j == NF - 1 or (s == 1 and hh == 1)))

        def epilogue(g):
            for s in range(2):
                i = (g % 2) * 2 + s
                for hh in range(2):
                    S.op("dve", lambda e: e.scalar_tensor_tensor(out=ot[s][:, hh * 512:(hh + 1) * 512], in0=p_dn[s * 2 + hh][:],
                                                                 scalar=0.5, in1=xt[i][:, hh * 512:(hh + 1) * 512],
                                                                 op0=ALU.mult, op1=ALU.add),
                         reads=[bp_dn[s * 2 + hh], bxt[i]], writes=[bot[s]])
                t0 = g * G + s * 128
                S.dma("sp", x_out[t0:t0 + 128, :], ot[s][:], bot[s], reads=[bot[s]], writes=[xout_b])

        load(0)
        if ng > 1:
            load(1)
        prep_dve(0)
        prep_pe(0)
        for g in range(ng):
            gate_up(g, 0)
            for j in range(NF):
                if j + 1 < NF:
                    gate_up(g, j + 1)
                elif g + 1 < ng:
                    pass
                down(g, j)
                if j == 3 and g + 1 < ng:
                    prep_dve(g + 1)
                if j == 14 and g + 1 < ng:
                    prep_pe(g + 1)
            epilogue(g)
            if g + 2 < ng:
                load(g + 2)
        S.barrier()
        for b in [bWg, bWu, bWd, bgB] + bxt + bot:
            S.release(b)


def make_ident(nc, S, ident, bident, dt_is_bf16=True):
    S.op("pool", lambda e: e.memset(ident[:], 0.0), writes=[bident])
    S.op("pool", lambda e: e.affine_select(out=ident[:], in_=ident[:], pattern=[[-1, 128]],
                                           compare_op=ALU.not_equal, fill=1.0, base=0,
                                           channel_multiplier=1), reads=[bident], writes=[bident])


def attn_in_phase(nc, S, x_in, xin_b, w_in, gain_row, qn, kn, QT, KT, V, bQT, bKT, bV, ntok, eps=1e-6):
    G = 512
    ng = ntok // G
    with ExitStack() as es:
        sb = lambda name, shape, dt: es.enter_context(nc.sbuf_tensor(_uniq(name), shape, dt))
        ps = lambda name, shape, dt: es.enter_context(nc.psum_tensor(_uniq(name), shape, dt))
        Win = sb("Win", [128, 8, 3072], BF16)
        gB = sb("gB", [128, D], F32)
        ident = sb("ident", [128, 128], BF16)
        bones = sb("bones", [128, 128], BF16)
        gq = sb("gq", [128, 1], F32)
        gk = sb("gk", [128, 1], F32)
        nh = sb("nh", [128, 1], F32)
        eps_ap = sb("eps_ap", [128, 1], F32)
        xt = [sb("xt%d" % i, [128, D], F32) for i in range(4)]
        hb = [sb("hb%d" % i, [128, D], BF16) for i in range(2)]
        hT = [sb("hT%d" % i, [128, 8, G], BF16) for i in range(2)]
        junk = sb("junk", [128, D], BF16)
        ss = [sb("ss%d" % i, [128, 1], F32) for i in range(2)]
        rs = [sb("rs%d" % i, [128, 1], F32) for i in range(2)]
        ob = [sb("ob%d" % i, [128, G], BF16) for i in range(3)]
        sq = [sb("sq%d" % i, [128, G], BF16) for i in range(2)]
        lt = [sb("lt%d" % i, [128, G], F32) for i in range(2)]
        rr = [sb("rr%d" % i, [128, G], F32) for i in range(2)]
        vb = [sb("vb%d" % i, [128, D], BF16) for i in range(2)]
        p_q = [ps("p_q%d" % i, [128, G], F32) for i in range(2)]
        p_s = [ps("p_s%d" % i, [128, G], F32) for i in range(2)]
        p_v = [ps("p_v%d" % i, [128, 512], F32) for i in range(2)]
        p_tr = [ps("p_tr%d" % i, [128, 8, 128], BF16) for i in range(2)]

        bWin, bgB, bgq, bgk = S.dbuf("Win"), S.dbuf("gB"), S.dbuf("gq"), S.dbuf("gk")
        bxt = [S.dbuf("xt") for _ in range(4)]
        bob = [S.dbuf("ob") for _ in range(3)]
        bvb = [S.dbuf("vb") for _ in range(2)]
        bhb = [S.buf("hb") for _ in range(2)]
        bhT = [S.buf("hT") for _ in range(2)]
        bjunk, bnh, bident, bbones = S.buf("junk"), S.buf("nh"), S.buf("ident"), S.buf("bones")
        bss = [S.buf("ss") for _ in range(2)]
        brs = [S.buf("rs") for _ in range(2)]
        bsq = [S.buf("sq") for _ in range(2)]
        blt = [S.buf("lt") for _ in range(2)]
        brr = [S.buf("rr") for _ in range(2)]
        bp_q = [S.pbuf("pq") for _ in range(2)]
        bp_s = [S.pbuf("ps") for _ in range(2)]
        bp_v = [S.pbuf("pv") for _ in range(2)]
        bp_tr = [S.pbuf("ptr") for _ in range(2)]

        S.op("pool", lambda e: e.memset(nh[:], -0.5), writes=[bnh])
        S.op("pool", lambda e: e.memset(eps_ap[:], eps), writes=[bnh])
        make_ident(nc, S, ident, bident)
        S.op("pool", lambda e: e.memset(bones[:], 0.0), writes=[bbones])
        S.op("pool", lambda e: e.memset(bones[0:64, 0:64], 1.0), writes=[bbones])
        S.op("pool", lambda e: e.memset(bones[64:128, 64:128], 1.0), writes=[bbones])
        S.dma("sp", gB[:], gain_row.to_broadcast([128, D]), bgB, writes=[bgB])
        for hh in range(2):
            S.dma("sp", gq[hh * 64:(hh + 1) * 64, :], qn, bgq, writes=[bgq])
            S.dma("sp", gk[hh * 64:(hh + 1) * 64, :], kn, bgk, writes=[bgk])
        S.op("dve", lambda e: e.tensor_scalar(out=gq[:], in0=gq[:], scalar1=0.125, scalar2=None, op0=ALU.mult),
             reads=[bgq], writes=[bgq])
        w_v = w_in.rearrange("(c p) f -> p c f", p=128)
        for c in range(8):
            S.dma("pool", Win[:, c, :], w_v[:, c, :], bWin, writes=[bWin])

        def load(g):
            for s in range(4):
                t0 = g * G + s * 128
                S.dma("sp", xt[s][:], x_in[t0:t0 + 128, :], bxt[s], reads=[xin_b], writes=[bxt[s]])

        def prep(g):
            for s in range(4):
                k = s % 2
                S.op("dve", lambda e: e.scalar_tensor_tensor(out=junk[:], in0=xt[s][:], scalar=1.0, in1=xt[s][:],
                                                             op0=ALU.mult, op1=ALU.mult, accum_out=ss[k][:]),
                     reads=[bxt[s]], writes=[bjunk, bss[k]])
                S.op("dve", lambda e: e.tensor_scalar(out=ss[k][:], in0=ss[k][:], scalar1=1.0 / D, scalar2=eps,
                                                      op0=ALU.mult, op1=ALU.add), reads=[bss[k]], writes=[bss[k]])
                S.op("pool", lambda e: e.tensor_tensor(out=rs[k][:], in0=ss[k][:], in1=nh[:], op=ALU.pow),
                     reads=[bss[k], bnh], writes=[brs[k]])
                S.op("dve", lambda e: e.scalar_tensor_tensor(out=hb[k][:], in0=xt[s][:], scalar=rs[k][:], in1=gB[:],
                                                             op0=ALU.mult, op1=ALU.mult),
                     reads=[bxt[s], brs[k], bgB], writes=[bhb[k]])
                for c in range(8):
                    S.op("pe", lambda e: e.transpose(p_tr[k][:, c, :], hb[k][:, c * 128:(c + 1) * 128], ident[:]),
                         reads=[bhb[k], bident], writes=[bp_tr[k]], sig=(c == 7))
                S.op("act", lambda e: e.copy(out=hT[g % 2][:, :, s * 128:(s + 1) * 128], in_=p_tr[k][:]),
                     reads=[bp_tr[k]], writes=[bhT[g % 2]])

        nob = 0
        for g in range(ng):
            load(g)
            prep(g)
            h = hT[g % 2]
            bh = bhT[g % 2]
            t0 = g * G
            for fc in range(16):
                isq = fc < 8
                ch = fc % 8
                if ch < 2:
                    col0 = (0 if isq else 256) + ch * 128
                else:
                    col0 = (768 if isq else 1536) + (ch - 2) * 128
                pq = p_q[fc % 2]
                for c in range(8):
                    S.op("pe", lambda e: e.matmul(pq[:], lhsT=Win[:, c, col0:col0 + 128], rhs=h[:, c, :],
                                                  start=(c == 0), stop=(c == 7)),
                         reads=[bWin, bh], writes=[bp_q[fc % 2]], sig=(c == 7))
                o = ob[nob % 3]
                bo = bob[nob % 3]
                nob += 1
                if ch < 2:
                    S.op("act", lambda e: e.activation(out=o[:], in_=pq[:], func=AF.Copy, scale=(0.125 if isq else 1.0)),
                         reads=[bp_q[fc % 2]], writes=[bo])
                else:
                    k = fc % 2
                    S.op("act", lambda e: e.activation(out=sq[k][:], in_=pq[:], func=AF.Square),
                         reads=[bp_q[k]], writes=[bsq[k]])
                    S.op("pe", lambda e: e.matmul(p_s[k][:], lhsT=bones[:], rhs=sq[k][:], start=True, stop=True),
                         reads=[bbones, bsq[k]], writes=[bp_s[k]])
                    S.op("act", lambda e: e.activation(out=lt[k][:], in_=p_s[k][:], func=AF.Ln, scale=1.0 / 64, bias=eps_ap[:]),
                         reads=[bp_s[k], bnh], writes=[blt[k]])
                    S.op("act", lambda e: e.activation(out=rr[k][:], in_=lt[k][:], func=AF.Exp, scale=-0.5),
                         reads=[blt[k]], writes=[brr[k]])
                    gcol = gq if isq else gk
                    S.op("dve", lambda e: e.scalar_tensor_tensor(out=o[:], in0=pq[:], scalar=gcol[:], in1=rr[k][:],
                                                                 op0=ALU.mult, op1=ALU.mult),
                         reads=[bp_q[k], brr[k], bgq, bgk], writes=[bo])
                dst, bdst = (QT, bQT) if isq else (KT, bKT)
                S.dma("sp", dst[ch * 128:(ch + 1) * 128, t0:t0 + G], o[:], bo, reads=[bo], writes=[bdst])
            for s in range(4):
                k = s % 2
                for (pv, cols, off) in ((p_v[0], (512, 768), 0), (p_v[0], (2304, 2560), 256), (p_v[1], (2560, 3072), 0)):
                    n = cols[1] - cols[0]
                    for c in range(8):
                        S.op("pe", lambda e: e.matmul(pv[:, off:off + n], lhsT=h[:, c, s * 128:(s + 1) * 128],
                                                      rhs=Win[:, c, cols[0]:cols[1]], start=(c == 0), stop=(c == 7)),
                             reads=[bWin, bh], writes=[bp_v[0], bp_v[1]], sig=(c == 7))
                S.op("act", lambda e: e.copy(out=vb[k][:, 0:512], in_=p_v[0][:]), reads=[bp_v[0]], writes=[bvb[k]])
                S.op("dve", lambda e: e.tensor_copy(out=vb[k][:, 512:1024], in_=p_v[1][:]), reads=[bp_v[1]], writes=[bvb[k]])
                S.dma("sp", V[t0 + s * 128:t0 + (s + 1) * 128, :], vb[k][:], bvb[k], reads=[bvb[k]], writes=[bV])
        S.barrier()
        for b in [bWin, bgB, bgq, bgk] + bxt + bob + bvb:
            S.release(b)


def sb_attn_phase(nc, S, QT, KT, V, MT, bQT, bKT, bV, bMT, ntok):
    nblk = ntok // 128
    with ExitStack() as es:
        sb = lambda name, shape, dt: es.enter_context(nc.sbuf_tensor(_uniq(name), shape, dt))
        ps = lambda name, shape, dt: es.enter_context(nc.psum_tensor(_uniq(name), shape, dt))
        qT = sb("qT", [128, 2, ntok], BF16)
        kT = sb("kT", [128, 2, ntok], BF16)
        v = sb("v", [128, nblk, 256], BF16)
        ones = sb("ones", [128, 512], F32)
        onec = sb("onec", [128, 1], F32)
        mneg = sb("mneg", [128, 128], BF16)
        ident = sb("ident", [128, 128], BF16)
        mk2 = lambda nm, shape, dt: [[sb("%s%d_%d" % (nm, h, i), shape, dt) for i in range(2)] for h in range(2)]
        e_ = mk2("e", [128, 512], F32)
        sp_ = mk2("sp", [128, 512], F32)
        cs_ = mk2("cs", [128, 512], F32)
        lw_ = mk2("lw", [128, 512], F32)
        w_ = mk2("w", [128, 512], BF16)
        wT_ = mk2("wT", [128, 4, 128], BF16)
        oT = [sb("oT%d" % i, [128, 512], BF16) for i in range(2)]
        p_z = [[ps("p_z%d_%d" % (h, i), [128, 512], F32) for i in range(2)] for h in range(2)]
        p_w = [ps("p_w%d" % i, [128, 4, 128], BF16) for i in range(2)]
        p_o = [ps("p_o%d" % i, [128, 128], F32) for i in range(2)]

        bq, bk, bv = S.dbuf("qT"), S.dbuf("kT"), S.dbuf("v")
        boT = [S.dbuf("oT") for _ in range(2)]
        bones, bmneg, bident = S.buf("ones"), S.buf("mneg"), S.buf("ident")
        bb2 = lambda nm: [[S.buf(nm) for _ in range(2)] for _ in range(2)]
        be, bsp, bcs, blw, bw, bwT = bb2("e"), bb2("sp"), bb2("cs"), bb2("lw"), bb2("w"), bb2("wT")
        bp_z = [[S.pbuf("pz") for _ in range(2)] for _ in range(2)]
        bp_w = [S.pbuf("pw") for _ in range(2)]
        bp_o = [S.pbuf("po") for _ in range(2)]

        S.op("pool", lambda e: e.memset(ones[:], 1.0), writes=[bones])
        S.op("pool", lambda e: e.memset(onec[:], 1.0), writes=[bones])
        make_ident(nc, S, ident, bident)
        S.op("pool", lambda e: e.memset(mneg[:], 0.0), writes=[bmneg])
        S.op("pool", lambda e: e.affine_select(out=mneg[:], in_=mneg[:], pattern=[[-1, 128]], compare_op=ALU.is_gt,
                                               fill=-30000.0, base=0, channel_multiplier=1),
             reads=[bmneg], writes=[bmneg])
        for pr in range(2):
            S.dma("sp", qT[:, pr, :], QT[pr * 128:(pr + 1) * 128, 0:ntok], bq, reads=[bQT], writes=[bq])
            S.dma("sp", kT[:, pr, :], KT[pr * 128:(pr + 1) * 128, 0:ntok], bk, reads=[bKT], writes=[bk])
        S.dma("sp", v[:], V[0:ntok, 0:256].rearrange("(b p) c -> p b c", p=128), bv, reads=[bV], writes=[bv])

        steps = []
        for pr in range(2):
            for qb in range(nblk):
                chunks = [(4 * (qb // 4), qb + 1, True)]
                for c in range(qb // 4 - 1, -1, -1):
                    chunks.append((4 * c, 4 * c + 4, False))
                for ci, (b0, b1, diag) in enumerate(chunks):
                    steps.append((pr, qb, ci, b0, b1, diag, len(chunks)))
        HH = [(0, slice(0, 64)), (1, slice(64, 128))]

        def front(t):
            pr, qb, ci, b0, b1, diag, nchk = steps[t]
            W = (b1 - b0) * 128
            i = t % 2
            for (hh, P) in HH:
                pz, bpz = p_z[hh][i], bp_z[hh][i]
                S.op("pe", lambda e: e.matmul(pz[:, 0:W], lhsT=qT[P, pr, qb * 128:(qb + 1) * 128],
                                              rhs=kT[P, pr, b0 * 128:b1 * 128], start=True, stop=(not diag)),
                     reads=[bq, bk], writes=[bpz], sig=(not diag))
                if diag:
                    S.op("pe", lambda e: e.matmul(pz[:, W - 128:W], lhsT=ident[:], rhs=mneg[:], start=False, stop=True),
                         reads=[bident, bmneg], writes=[bpz])
            for (hh, P) in HH:
                S.op("act", lambda e: e.activation(out=e_[hh][i][:, 0:W], in_=p_z[hh][i][:, 0:W], func=AF.Exp),
                     reads=[bp_z[hh][i]], writes=[be[hh][i]])
            for (hh, P) in HH:
                S.op("act", lambda e: e.activation(out=sp_[hh][i][:, 0:W], in_=e_[hh][i][:, 0:W], func=AF.Ln, bias=onec[:]),
                     reads=[be[hh][i], bones], writes=[bsp[hh][i]])

        def back(t):
            pr, qb, ci, b0, b1, diag, nchk = steps[t]
            W = (b1 - b0) * 128
            nb = b1 - b0
            i = t % 2
            rev = (lambda tt: tt[:, W - 1::-1] if W < 512 else tt[:, ::-1])
            for (hh, P) in HH:
                if ci == 0:
                    init, rd = 0.0, [bsp[hh][i], bones]
                else:
                    init, rd = cs_[hh][1 - i][:, 0:1], [bsp[hh][i], bones, bcs[hh][1 - i]]
                S.op("dve", lambda e: e.tensor_tensor_scan(out=rev(cs_[hh][i]), data0=ones[:, 0:W], data1=rev(sp_[hh][i]),
                                                           initial=init, op0=ALU.mult, op1=ALU.add),
                     reads=rd, writes=[bcs[hh][i]])
            for (hh, P) in HH:
                S.op("dve", lambda e: e.tensor_tensor(out=lw_[hh][i][:, 0:W], in0=p_z[hh][i][:, 0:W], in1=cs_[hh][i][:, 0:W], op=ALU.subtract),
                     reads=[bp_z[hh][i], bcs[hh][i]], writes=[blw[hh][i]])
            for (hh, P) in HH:
                S.op("act", lambda e: e.activation(out=w_[hh][i][:, 0:W], in_=lw_[hh][i][:, 0:W], func=AF.Exp),
                     reads=[blw[hh][i]], writes=[bw[hh][i]])
            for (hh, P) in HH:
                pw, bpw = p_w[hh], bp_w[hh]
                for b in range(nb):
                    S.op("pe", lambda e: e.transpose(pw[:, b, :], w_[hh][i][:, b * 128:(b + 1) * 128], ident[:]),
                         reads=[bw[hh][i], bident], writes=[bpw], sig=(b == nb - 1))
                if hh == 0:
                    S.op("act", lambda e: e.copy(out=wT_[hh][i][:, 0:nb, :], in_=pw[:, 0:nb, :]), reads=[bpw], writes=[bwT[hh][i]])
                else:
                    S.op("dve", lambda e: e.tensor_copy(out=wT_[hh][i][:, 0:nb, :], in_=pw[:, 0:nb, :]), reads=[bpw], writes=[bwT[hh][i]])
            po, bpo = p_o[qb % 2], bp_o[qb % 2]
            for (hh, P) in HH:
                h = 2 * pr + hh
                for b in range(nb):
                    first = (ci == 0 and b == 0)
                    last = (ci == nchk - 1 and b == nb - 1)
                    S.op("pe", lambda e: e.matmul(po[P, :], lhsT=v[:, b0 + b, h * 64:(h + 1) * 64], rhs=wT_[hh][i][:, b, :],
                                                  start=first, stop=last),
                         reads=[bv, bwT[hh][i]], writes=[bpo], sig=(b == nb - 1))
            if ci == nchk - 1:
                k = (qb // 4) % 2
                S.op("dve", lambda e: e.tensor_copy(out=oT[k][:, (qb % 4) * 128:(qb % 4 + 1) * 128], in_=po[:, :]),
                     reads=[bpo], writes=[boT[k]])
                if qb % 4 == 3 or qb == nblk - 1:
                    q0 = 4 * (qb // 4)
                    n = (qb - q0 + 1) * 128
                    S.dma("sp", MT[pr * 128:(pr + 1) * 128, q0 * 128:q0 * 128 + n], oT[k][:, 0:n], boT[k],
                          reads=[boT[k]], writes=[bMT])

        front(0)
        for t in range(len(steps)):
            if t + 1 < len(steps):
                front(t + 1)
            back(t)
        S.barrier()
        for b_ in [bq, bk, bv] + boT:
            S.release(b_)


def dil_bias_host(rel_bias):
    out = np.empty((12, 128, 2, 128), np.float32)
    kj = np.arange(128)[:, None]
    q = np.arange(128)[None, :]
    for g, r in enumerate((1, 4, 16)):
        for part, dist in ((1, q - kj), (0, q + 128 - kj)):
            valid = (dist >= 0) & (dist <= 128)
            dd = np.maximum(dist, 0) * r
            d = np.maximum(dd, 1).astype(np.float32)
            large = 16 + (np.log(d / np.float32(16)) / np.float32(np.log(2048 / 16)) * np.float32(16)).astype(np.int32)
            large = np.minimum(large, 31)
            bucket = np.where(dd < 16, dd, large)
            for j in range(4):
                hd = 4 * g + j
                out[hd, :, part, :] = np.where(valid, rel_bias[bucket, hd], np.float32(-30000.0))
    return out


def dil_attn_phase(nc, S, QT, KT, V, MT, dbias, bQT, bKT, bV, bMT, ntok):
    with ExitStack() as es:
        sb = lambda name, shape, dt: es.enter_context(nc.sbuf_tensor(_uniq(name), shape, dt))
        ps = lambda name, shape, dt: es.enter_context(nc.psum_tensor(_uniq(name), shape, dt))
        qT = sb("qT", [128, 2, ntok], BF16)
        kT = sb("kT", [128, 2, ntok], BF16)
        v = sb("v", [128, ntok // 128, 256], BF16)
        bias = sb("bias", [128, 4, 256], F32)
        onesb = sb("onesb", [128, 64], BF16)
        Nacc = sb("Nacc", [128, 2, ntok], F32)
        Dacc = sb("Dacc", [128, 2, ntok], F32)
        s_ = [sb("s%d" % i, [128, 256], F32) for i in range(3)]
        pT_ = [sb("pT%d" % i, [128, 2, 128], BF16) for i in range(3)]
        ob = [sb("ob%d" % i, [128, 1024], BF16) for i in range(2)]
        p_s = [ps("p_s%d" % i, [128, 2, 128], F32) for i in range(2)]
        p_n = [ps("p_n%d" % i, [128, 128], F32) for i in range(2)]
        p_d = [ps("p_d%d" % i, [128, 128], F32) for i in range(2)]
        p_pad = [ps("p_pad%d" % i, [128, 256], F32) for i in range(0)]

        bq, bk, bv, bbias = S.dbuf("qT"), S.dbuf("kT"), S.dbuf("v"), S.dbuf("bias")
        bob = [S.dbuf("ob") for _ in range(2)]
        bones, bN, bD = S.buf("ones"), S.buf("N"), S.buf("D")
        bs = [S.buf("s") for _ in range(3)]
        bpT = [S.buf("pT") for _ in range(3)]
        bp_s = [S.pbuf("ps") for _ in range(2)]
        bp_n = [S.pbuf("pn") for _ in range(2)]
        bp_d = [S.pbuf("pd") for _ in range(2)]

        S.op("pool", lambda e: e.memset(onesb[:], 1.0), writes=[bones])
        u = 0
        for g, r in enumerate((1, 4, 16)):
            L = ntok // r
            nb = L // 128
            for pr in range(2):
                r0 = 256 + (2 * g + pr) * 128
                S.dma("sp", qT[:, pr, :], QT[r0:r0 + 128, 0:ntok], bq, reads=[bQT], writes=[bq])
                S.dma("sp", kT[:, pr, :], KT[r0:r0 + 128, 0:ntok], bk, reads=[bKT], writes=[bk])
            vsrc = V[0:ntok, 256 + g * 256:256 + (g + 1) * 256].rearrange("(n i c) f -> c i n f", i=128, c=r)
            for c in range(r):
                S.dma("sp", v[:, c * nb:(c + 1) * nb, :], vsrc[c], bv, reads=[bV], writes=[bv])
            for j in range(4):
                S.dma("sp", bias[:, j, :], dbias[4 * g + j].rearrange("k a q -> k (a q)"), bbias, writes=[bbias])
            units = []
            for j in range(4):
                for c in range(r):
                    for n in range(nb):
                        units.append((j, c, n))

            def tokf(c, nn):
                st = c + r * 128 * nn
                return slice(st, st + r * 127 + 1, r)

            def front(uu, u):
                j, c, n = units[uu]
                pr, hh = j // 2, j % 2
                P = slice(64 * hh, 64 * hh + 64)
                i = u % 3
                pss, bpss = p_s[u % 2], bp_s[u % 2]
                a0 = 0 if n > 0 else 1
                if n > 0:
                    S.op("pe", lambda e: e.matmul(pss[:, 0, :], lhsT=kT[P, pr, tokf(c, n - 1)], rhs=qT[P, pr, tokf(c, n)],
                                                  start=True, stop=True), reads=[bq, bk], writes=[bpss], sig=False)
                S.op("pe", lambda e: e.matmul(pss[:, 1, :], lhsT=kT[P, pr, tokf(c, n)], rhs=qT[P, pr, tokf(c, n)],
                                              start=True, stop=True), reads=[bq, bk], writes=[bpss])
                S.op("dve", lambda e: e.tensor_tensor(out=s_[i][:, a0 * 128:256], in0=pss[:, a0:2, :],
                                                      in1=bias[:, j, a0 * 128:256], op=ALU.add),
                     reads=[bpss, bbias], writes=[bs[i]])
                S.op("act", lambda e: e.activation(out=pT_[i][:, a0:2, :], in_=s_[i][:, a0 * 128:256], func=AF.Exp),
                     reads=[bs[i]], writes=[bpT[i]])

            def back(uu, u):
                j, c, n = units[uu]
                pr, hh = j // 2, j % 2
                P = slice(64 * hh, 64 * hh + 64)
                i = u % 3
                pn, bpn = p_n[u % 2], bp_n[u % 2]
                pd, bpd = p_d[u % 2], bp_d[u % 2]
                a0 = 0 if n > 0 else 1
                for a_ in range(a0, 2):
                    S.op("pe", lambda e: e.matmul(pn[P, :], lhsT=v[:, c * nb + n - 1 + a_, j * 64:(j + 1) * 64],
                                                  rhs=pT_[i][:, a_, :], start=(a_ == a0), stop=(a_ == 1)),
                         reads=[bv, bpT[i]], writes=[bpn], sig=(a_ == 1))
                for a_ in range(a0, 2):
                    S.op("pe", lambda e: e.matmul(pd[P, :], lhsT=onesb[:, :], rhs=pT_[i][:, a_, :],
                                                  start=(a_ == a0), stop=(a_ == 1)),
                         reads=[bones, bpT[i]], writes=[bpd], sig=(a_ == 1))
                tk = tokf(c, n)
                if g == 0:
                    S.op("act", lambda e: e.copy(out=Nacc[P, pr, tk], in_=pn[P, :]), reads=[bpn], writes=[bN])
                    S.op("dve", lambda e: e.tensor_copy(out=Dacc[P, pr, tk], in_=pd[P, :]), reads=[bpd], writes=[bD])
                else:
                    S.op("dve", lambda e: e.tensor_tensor(out=Nacc[P, pr, tk], in0=pn[P, :], in1=Nacc[P, pr, tk],
                                                          op=ALU.add), reads=[bpn, bN], writes=[bN])
                    S.op("dve", lambda e: e.tensor_tensor(out=Dacc[P, pr, tk], in0=pd[P, :], in1=Dacc[P, pr, tk],
                                                          op=ALU.add), reads=[bpd, bD], writes=[bD])

            front(0, u)
            for uu in range(len(units)):
                if uu + 1 < len(units):
                    front(uu + 1, u + 1)
                back(uu, u)
                u += 1
        k = 0
        for pr in range(2):
            for c0 in range(0, ntok, 1024):
                n = min(1024, ntok - c0)
                S.op("dve", lambda e: e.reciprocal(out=Dacc[:, pr, c0:c0 + n], in_=Dacc[:, pr, c0:c0 + n]), reads=[bD], writes=[bD])
                S.op("dve", lambda e: e.tensor_tensor(out=ob[k % 2][:, 0:n], in0=Nacc[:, pr, c0:c0 + n], in1=Dacc[:, pr, c0:c0 + n],
                                                      op=ALU.mult), reads=[bN, bD], writes=[bob[k % 2]])
                S.dma("sp", MT[256 + pr * 128:256 + (pr + 1) * 128, c0:c0 + n], ob[k % 2][:, 0:n], bob[k % 2],
                      reads=[bob[k % 2]], writes=[bMT])
                k += 1
        S.barrier()
        for b in [bq, bk, bv, bbias] + bob:
            S.release(b)


def out_proj_phase(nc, S, x_in, x_out, xin_b, xout_b, MT, bMT, w_out, kdim, ntok):
    G = 512
    ng = ntok // G
    kc = kdim // 128
    with ExitStack() as es:
        sb = lambda name, shape, dt: es.enter_context(nc.sbuf_tensor(_uniq(name), shape, dt))
        ps = lambda name, shape, dt: es.enter_context(nc.psum_tensor(_uniq(name), shape, dt))
        Wo = sb("Wo", [128, kc, D], BF16)
        mT = [sb("mT%d" % i, [128, kc, G], BF16) for i in range(2)]
        xt = [sb("xt%d" % i, [128, D], F32) for i in range(3)]
        ot = [sb("ot%d" % i, [128, D], F32) for i in range(2)]
        p_y = [ps("p_y%d" % i, [128, 512], F32) for i in range(4)]
        bWo = S.dbuf("Wo")
        bmT = [S.dbuf("mT") for _ in range(2)]
        bxt = [S.dbuf("xt") for _ in range(3)]
        bot = [S.dbuf("ot") for _ in range(2)]
        bp_y = [S.pbuf("py") for _ in range(4)]
        w_v = w_out.rearrange("(c p) f -> p c f", p=128)
        for c in range(kc):
            S.dma("pool", Wo[:, c, :], w_v[:, c, :], bWo, writes=[bWo])
        k = 0
        for g in range(ng):
            m, bm = mT[g % 2], bmT[g % 2]
            for c in range(kc):
                S.dma("sp", m[:, c, :], MT[c * 128:(c + 1) * 128, g * G:(g + 1) * G], bm, reads=[bMT], writes=[bm])
            for s in range(4):
                t0 = g * G + s * 128
                x_, bx = xt[k % 3], bxt[k % 3]
                o_, bo = ot[k % 2], bot[k % 2]
                S.dma("sp", x_[:], x_in[t0:t0 + 128, :], bx, reads=[xin_b], writes=[bx])
                for hh in range(2):
                    py, bpy = p_y[(2 * k + hh) % 4], bp_y[(2 * k + hh) % 4]
                    for c in range(kc):
                        S.op("pe", lambda e: e.matmul(py[:], lhsT=m[:, c, s * 128:(s + 1) * 128], rhs=Wo[:, c, hh * 512:(hh + 1) * 512],
                                                      start=(c == 0), stop=(c == kc - 1)),
                             reads=[bm, bWo], writes=[bpy], sig=(c == kc - 1))
                    S.op("dve", lambda e: e.tensor_tensor(out=o_[:, hh * 512:(hh + 1) * 512], in0=py[:], in1=x_[:, hh * 512:(hh + 1) * 512],
                                                          op=ALU.add), reads=[bpy, bx], writes=[bo])
                S.dma("sp", x_out[t0:t0 + 128, :], o_[:], bo, reads=[bo], writes=[xout_b])
                k += 1
        S.barrier()
        for b in [bWo] + bmT + bxt + bot:
            S.release(b)


C0 = float(np.exp(-0.5))


class RwPrep:
    def __init__(self, nc, S, es, G, mix_ids, x_in, xin_b, gain_row, mix, eps=1e-6):
        sb = lambda name, shape, dt: es.enter_context(nc.sbuf_tensor(_uniq(name), shape, dt))
        ps = lambda name, shape, dt: es.enter_context(nc.psum_tensor(_uniq(name), shape, dt))
        self.nc, self.S, self.G, self.mix_ids, self.x_in, self.xin_b, self.eps = nc, S, G, mix_ids, x_in, xin_b, eps
        self.gB = sb("gB", [128, D], F32)
        self.identf = sb("identf", [128, 128], F32)
        self.nh = sb("nh", [128, 1], F32)
        self.mixc = sb("mixc", [128, 6, 8], F32)
        self.xt = [sb("xt%d" % i, [128, D], F32) for i in range(2)]
        self.hn = [sb("hn%d" % i, [128, D], F32) for i in range(2)]
        self.junk = sb("junk", [128, D], BF16)
        self.ss = [sb("ss%d" % i, [128, 1], F32) for i in range(2)]
        self.rs = [sb("rs%d" % i, [128, 1], F32) for i in range(2)]
        self.hT = [sb("hT%d" % i, [128, 8, G + 1], F32) for i in range(2)]
        self.xx = [sb("xx%d" % i, [128, G], F32) for i in range(2)]
        self.xm = {i: sb("xm%d" % i, [128, 8, G], BF16) for i in mix_ids}
        self.p_tr = [ps("p_tr%d" % i, [128, 4, 128], F32) for i in range(2)]
        self.bgB, self.bmixc = S.dbuf("gB"), S.dbuf("mixc")
        self.bxt = [S.dbuf("xt") for _ in range(2)]
        self.bhn = [S.buf("hn") for _ in range(2)]
        self.bident, self.bnh, self.bjunk = S.buf("identf"), S.buf("nh"), S.buf("junk")
        self.bss = [S.buf("ss") for _ in range(2)]
        self.brs = [S.buf("rs") for _ in range(2)]
        self.bhT = [S.buf("hT") for _ in range(2)]
        self.bxx = [S.buf("xx") for _ in range(2)]
        self.bxm = {i: S.buf("xm") for i in mix_ids}
        self.bp_tr = [S.pbuf("ptr") for _ in range(2)]
        S.op("pool", lambda e: e.memset(self.nh[:], -0.5), writes=[self.bnh])
        make_ident(nc, S, self.identf, self.bident)
        S.dma("sp", self.gB[:], gain_row.to_broadcast([128, D]), self.bgB, writes=[self.bgB])
        for i in range(6):
            S.dma("sp", self.mixc[:, i, :], mix[i:i + 1, :].rearrange("o (c p) -> p (o c)", p=128), self.bmixc, writes=[self.bmixc], slow=True)
        S.op("dve", lambda e: e.memset(self.hT[1][:, :, G:G + 1], 0.0), writes=[self.bhT[1]])
        self.dsems = [self.bgB, self.bmixc] + self.bxt

    def group(self, g):
        nc, S, G = self.nc, self.S, self.G
        hT, bhT = self.hT[g % 2], self.bhT[g % 2]
        hTp, bhTp = self.hT[(g + 1) % 2], self.bhT[(g + 1) % 2]
        S.op("pool", lambda e: e.tensor_copy(out=hT[:, :, 0:1], in_=hTp[:, :, G:G + 1]), reads=[bhTp], writes=[bhT])
        k = 0
        for s in range(G // 128):
            t0 = g * G + s * 128
            x_, bx = self.xt[s % 2], self.bxt[s % 2]
            h_, bh = self.hn[s % 2], self.bhn[s % 2]
            ss, bss, rs, brs = self.ss[s % 2], self.bss[s % 2], self.rs[s % 2], self.brs[s % 2]
            S.dma("sp", x_[:], self.x_in[t0:t0 + 128, :], bx, reads=[self.xin_b], writes=[bx])
            S.op("dve", lambda e: e.scalar_tensor_tensor(out=self.junk[:], in0=x_[:], scalar=1.0, in1=x_[:], op0=ALU.mult, op1=ALU.mult,
                                                         accum_out=ss[:]), reads=[bx], writes=[self.bjunk, bss])
            S.op("dve", lambda e: e.tensor_scalar(out=ss[:], in0=ss[:], scalar1=1.0 / D, scalar2=self.eps, op0=ALU.mult, op1=ALU.add),
                 reads=[bss], writes=[bss])
            S.op("pool", lambda e: e.tensor_tensor(out=rs[:], in0=ss[:], in1=self.nh[:], op=ALU.pow), reads=[bss, self.bnh], writes=[brs])
            S.op("dve", lambda e: e.scalar_tensor_tensor(out=h_[:], in0=x_[:], scalar=rs[:], in1=self.gB[:], op0=ALU.mult, op1=ALU.mult),
                 reads=[bx, brs, self.bgB], writes=[bh])
            for half in range(2):
                pt, bpt = self.p_tr[k % 2], self.bp_tr[k % 2]
                k += 1
                for c4 in range(4):
                    c = half * 4 + c4
                    S.op("pe", lambda e: e.transpose(pt[:, c4, :], h_[:, c * 128:(c + 1) * 128], self.identf[:]),
                         reads=[bh, self.bident], writes=[bpt], sig=(c4 == 3))
                S.op("act", lambda e: e.copy(out=hT[:, half * 4:half * 4 + 4, 1 + s * 128:1 + (s + 1) * 128], in_=pt[:]),
                     reads=[bpt], writes=[bhT])
        for c in range(8):
            xx, bxx = self.xx[c % 2], self.bxx[c % 2]
            S.op("dve", lambda e: e.tensor_tensor(out=xx[:], in0=hT[:, c, 0:G], in1=hT[:, c, 1:G + 1], op=ALU.subtract),
                 reads=[bhT], writes=[bxx])
            for n, i in enumerate(self.mix_ids):
                S.op("dve", lambda e: e.scalar_tensor_tensor(out=self.xm[i][:, c, :], in0=xx[:], scalar=self.mixc[:, i, c:c + 1],
                                                             in1=hT[:, c, 1:G + 1], op0=ALU.mult, op1=ALU.add),
                     reads=[bxx, bhT, self.bmixc], writes=[self.bxm[i]])

    def release(self):
        for b in self.dsems:
            self.S.release(b)


def col_load(S, dst, src_row, track):
    S.dma("sp", dst, src_row.rearrange("o (c p) -> p (o c)", p=128), track, writes=[track], slow=True)


def rwkv_fm_phase(nc, S, x_in, xin_b, gain_row, mix, w0, w1, w2, a0, a1, a2, kk_, ka_, w_r, w_k,
                  RtT, KtT, AtT, BtT, WC, bouts, ntok):
    G = 512
    ng = ntok // G
    with ExitStack() as es:
        sb = lambda name, shape, dt: es.enter_context(nc.sbuf_tensor(_uniq(name), shape, dt))
        ps = lambda name, shape, dt: es.enter_context(nc.psum_tensor(_uniq(name), shape, dt))
        P = RwPrep(nc, S, es, G, (0, 1, 2, 4), x_in, xin_b, gain_row, mix)
        Wr = sb("Wr", [128, 8, D], BF16)
        Wk = sb("Wk", [128, 8, D], BF16)
        W1 = sb("W1", [128, 8, 64], BF16)
        A1 = sb("A1", [128, 8, 64], BF16)
        W2 = sb("W2", [64, D], BF16)
        A2 = sb("A2", [64, D], BF16)
        cols = sb("cols", [128, 4, 8], F32)
        bones = sb("bones", [128, 128], BF16)
        rmask = sb("rmask", [128, G], F32)
        tiny = sb("tiny", [128, 1], F32)
        tw = sb("tw", [64, G], BF16)
        ta = sb("ta", [64, G], BF16)
        names = ["sgu", "av", "kk0", "lnk", "rk", "kkn", "t1", "kp", "csg", "cse", "eW", "eWi", "eWe", "t2"]
        T = {n: sb(n, [128, G], F32) for n in names}
        sqk = sb("sqk", [128, G], BF16)
        wc = [sb("wc%d" % i, [128, G // 64], F32) for i in range(2)]
        ob = [sb("ob%d" % i, [128, G], BF16) for i in range(8)]
        p_r = ps("p_r", [128, G], F32)
        p_k = ps("p_k", [128, G], F32)
        p_u = ps("p_u", [128, G], F32)
        p_a = ps("p_a", [128, G], F32)
        p_ss = ps("p_ss", [128, G], F32)
        p_t = ps("p_t", [128, G], F32)
        bW = S.dbuf("W")
        bcols = S.dbuf("cols")
        bwc = [S.dbuf("wc") for _ in range(2)]
        bob = [S.dbuf("ob") for _ in range(8)]
        B = {n: S.buf(n) for n in names + ["sqk", "tw", "ta", "bones", "rmask"]}
        B.update({n: S.pbuf(n) for n in ["p_r", "p_k", "p_u", "p_a", "p_ss", "p_t"]})
        for c in range(8):
            S.dma("pool", Wr[:, c, :], w_r.rearrange("(c p) f -> p c f", p=128)[:, c, :], bW, writes=[bW])
            S.dma("pool", Wk[:, c, :], w_k.rearrange("(c p) f -> p c f", p=128)[:, c, :], bW, writes=[bW])
        S.dma("pool", W1[:], w1.rearrange("(c p) f -> p c f", p=128), bW, writes=[bW])
        S.dma("pool", A1[:], a1.rearrange("(c p) f -> p c f", p=128), bW, writes=[bW])
        S.dma("pool", W2[:], w2, bW, writes=[bW])
        S.dma("pool", A2[:], a2, bW, writes=[bW])
        for n, src in enumerate((w0, a0, kk_, ka_)):
            col_load(S, cols[:, n, :], src, bcols)
        S.op("pool", lambda e: e.memset(bones[:], 0.0), writes=[B["bones"]])
        S.op("pool", lambda e: e.memset(bones[0:64, 0:64], 1.0), writes=[B["bones"]])
        S.op("pool", lambda e: e.memset(bones[64:128, 64:128], 1.0), writes=[B["bones"]])
        S.op("pool", lambda e: e.memset(rmask[:], 1.0), writes=[B["rmask"]])
        S.op("pool", lambda e: e.memset(rmask[:, 0:G:64], 0.0), writes=[B["rmask"]])
        S.op("pool", lambda e: e.memset(tiny[:], 1e-18), writes=[B["rmask"]])
        RB, KB, AB, BB, WCB = bouts
        no = 0
        for g in range(ng):
            P.group(g)
            t0 = g * G
            xr, xw, xk, xa = P.xm[0], P.xm[1], P.xm[2], P.xm[4]
            bxr, bxw, bxk, bxa = P.bxm[0], P.bxm[1], P.bxm[2], P.bxm[4]
            for c in range(8):
                S.op("pe", lambda e: e.matmul(p_t[0:64, :], lhsT=W1[:, c, :], rhs=xw[:, c, :], start=(c == 0), stop=(c == 7)),
                     reads=[bW, bxw], writes=[B["p_t"]], sig=(c == 7))
            S.op("act", lambda e: e.activation(out=tw[:], in_=p_t[0:64, :], func=AF.Tanh), reads=[B["p_t"]], writes=[B["tw"]])
            for c in range(8):
                S.op("pe", lambda e: e.matmul(p_t[0:64, :], lhsT=A1[:, c, :], rhs=xa[:, c, :], start=(c == 0), stop=(c == 7)),
                     reads=[bW, bxa], writes=[B["p_t"]], sig=(c == 7))
            S.op("act", lambda e: e.copy(out=ta[:], in_=p_t[0:64, :]), reads=[B["p_t"]], writes=[B["ta"]])
            for cc in range(8):
                fs = slice(cc * 128, (cc + 1) * 128)
                for c in range(8):
                    S.op("pe", lambda e: e.matmul(p_r[:], lhsT=Wr[:, c, fs], rhs=xr[:, c, :], start=(c == 0), stop=(c == 7)),
                         reads=[bW, bxr], writes=[B["p_r"]], sig=(c == 7))
                for c in range(8):
                    S.op("pe", lambda e: e.matmul(p_k[:], lhsT=Wk[:, c, fs], rhs=xk[:, c, :], start=(c == 0), stop=(c == 7)),
                         reads=[bW, bxk], writes=[B["p_k"]], sig=(c == 7))
                S.op("pe", lambda e: e.matmul(p_u[:], lhsT=W2[:, fs], rhs=tw[:], start=True, stop=True), reads=[bW, B["tw"]], writes=[B["p_u"]])
                S.op("pe", lambda e: e.matmul(p_a[:], lhsT=A2[:, fs], rhs=ta[:], start=True, stop=True), reads=[bW, B["ta"]], writes=[B["p_a"]])
                S.op("act", lambda e: e.activation(out=T["sgu"][:], in_=p_u[:], func=AF.Sigmoid, bias=cols[:, 0, cc:cc + 1]),
                     reads=[B["p_u"], bcols], writes=[B["sgu"]])
                S.op("act", lambda e: e.activation(out=T["av"][:], in_=p_a[:], func=AF.Sigmoid, bias=cols[:, 1, cc:cc + 1]),
                     reads=[B["p_a"], bcols], writes=[B["av"]])
                S.op("dve", lambda e: e.tensor_scalar(out=T["kk0"][:], in0=p_k[:], scalar1=cols[:, 2, cc:cc + 1], scalar2=None, op0=ALU.mult),
                     reads=[B["p_k"], bcols], writes=[B["kk0"]])
                S.op("act", lambda e: e.activation(out=sqk[:], in_=T["kk0"][:], func=AF.Square), reads=[B["kk0"]], writes=[B["sqk"]])
                S.op("pe", lambda e: e.matmul(p_ss[:], lhsT=bones[:], rhs=sqk[:], start=True, stop=True), reads=[B["bones"], B["sqk"]],
                     writes=[B["p_ss"]])
                S.op("act", lambda e: e.activation(out=T["lnk"][:], in_=p_ss[:], func=AF.Ln, bias=tiny[:]), reads=[B["p_ss"], B["rmask"]],
                     writes=[B["lnk"]])
                S.op("act", lambda e: e.activation(out=T["rk"][:], in_=T["lnk"][:], func=AF.Exp, scale=-0.5), reads=[B["lnk"]], writes=[B["rk"]])
                S.op("dve", lambda e: e.tensor_tensor(out=T["kkn"][:], in0=T["kk0"][:], in1=T["rk"][:], op=ALU.mult),
                     reads=[B["kk0"], B["rk"]], writes=[B["kkn"]])
                S.op("dve", lambda e: e.tensor_scalar(out=T["t1"][:], in0=T["av"][:], scalar1=-1.0, scalar2=cols[:, 3, cc:cc + 1],
                                                      op0=ALU.add, op1=ALU.mult), reads=[B["av"], bcols], writes=[B["t1"]])
                S.op("dve", lambda e: e.scalar_tensor_tensor(out=T["kp"][:], in0=T["t1"][:], scalar=1.0, in1=p_k[:], op0=ALU.add, op1=ALU.mult),
                     reads=[B["t1"], B["p_k"]], writes=[B["kp"]])
                S.op("dve", lambda e: e.tensor_tensor_scan(out=T["csg"][:], data0=rmask[:], data1=T["sgu"][:], initial=0.0,
                                                           op0=ALU.mult, op1=ALU.add), reads=[B["rmask"], B["sgu"]], writes=[B["csg"]])
                S.op("dve", lambda e: e.tensor_tensor(out=T["cse"][:], in0=T["csg"][:], in1=T["sgu"][:], op=ALU.subtract),
                     reads=[B["csg"], B["sgu"]], writes=[B["cse"]])
                S.op("act", lambda e: e.activation(out=T["eW"][:], in_=T["csg"][:], func=AF.Exp, scale=-C0), reads=[B["csg"]], writes=[B["eW"]])
                S.op("act", lambda e: e.activation(out=T["eWi"][:], in_=T["csg"][:], func=AF.Exp, scale=C0), reads=[B["csg"]], writes=[B["eWi"]])
                S.op("act", lambda e: e.activation(out=T["eWe"][:], in_=T["cse"][:], func=AF.Exp, scale=-C0), reads=[B["cse"]], writes=[B["eWe"]])
                S.op("dve", lambda e: e.tensor_tensor(out=T["t2"][:], in0=T["kkn"][:], in1=T["av"][:], op=ALU.mult),
                     reads=[B["kkn"], B["av"]], writes=[B["t2"]])
                outs = []
                o, bo = ob[no % 8], bob[no % 8]; no += 1
                S.op("dve", lambda e: e.tensor_tensor(out=o[:], in0=p_r[:], in1=T["eW"][:], op=ALU.mult), reads=[B["p_r"], B["eW"]], writes=[bo])
                outs.append((o, bo, RtT, RB))
                o, bo = ob[no % 8], bob[no % 8]; no += 1
                S.op("dve", lambda e: e.tensor_tensor(out=o[:], in0=T["kp"][:], in1=T["eWi"][:], op=ALU.mult), reads=[B["kp"], B["eWi"]], writes=[bo])
                outs.append((o, bo, KtT, KB))
                o, bo = ob[no % 8], bob[no % 8]; no += 1
                S.op("dve", lambda e: e.scalar_tensor_tensor(out=o[:], in0=T["kkn"][:], scalar=-1.0, in1=T["eWe"][:], op0=ALU.mult, op1=ALU.mult),
                     reads=[B["kkn"], B["eWe"]], writes=[bo])
                outs.append((o, bo, AtT, AB))
                o, bo = ob[no % 8], bob[no % 8]; no += 1
                S.op("dve", lambda e: e.tensor_tensor(out=o[:], in0=T["t2"][:], in1=T["eWi"][:], op=ALU.mult), reads=[B["t2"], B["eWi"]], writes=[bo])
                outs.append((o, bo, BtT, BB))
                for (o, bo, dst, bdst) in outs:
                    S.dma("sp", dst[fs, t0:t0 + G], o[:], bo, reads=[bo], writes=[bdst])
                w_, bw_ = wc[cc % 2], bwc[cc % 2]
                S.op("dve", lambda e: e.tensor_copy(out=w_[:], in_=T["eW"][:, 63:G:64]), reads=[B["eW"]], writes=[bw_])
                S.dma("sp", WC[fs, g * (G // 64):(g + 1) * (G // 64)], w_[:], bw_, reads=[bw_], writes=[WCB])
        S.barrier()
        P.release()
        for b in [bW, bcols] + bwc + bob:
            S.release(b)


def rwkv_tm_phase(nc, S, x_in, xin_b, gain_row, mix, a0, a1, a2, g1, g2, ka_, rk_, w_r, w_k, w_v,
                  Vtok, BV, Gt, bouts, ntok):
    G = 256
    ng = ntok // G
    with ExitStack() as es:
        sb = lambda name, shape, dt: es.enter_context(nc.sbuf_tensor(_uniq(name), shape, dt))
        ps = lambda name, shape, dt: es.enter_context(nc.psum_tensor(_uniq(name), shape, dt))
        P = RwPrep(nc, S, es, G, (0, 2, 3, 4, 5), x_in, xin_b, gain_row, mix)
        Wr = sb("Wr", [128, 8, D], BF16)
        Wk = sb("Wk", [128, 8, D], BF16)
        Wv = sb("Wv", [128, 8, D], BF16)
        A1 = sb("A1", [128, 8, 64], BF16)
        A2 = sb("A2", [64, D], BF16)
        G1 = sb("G1", [128, 8, 160], BF16)
        G2a = sb("G2a", [128, D], BF16)
        G2b = sb("G2b", [32, D], BF16)
        a0B = sb("a0B", [128, D], F32)
        kaB = sb("kaB", [128, D], F32)
        rkB = sb("rkB", [128, D], F32)
        ta = sb("ta", [64, G], BF16)
        sg1a = sb("sg1a", [128, G], BF16)
        sg1b = sb("sg1b", [32, G], BF16)
        tmp = sb("tmp", [128, D], F32)
        av = sb("av", [128, D], F32)
        t1 = sb("t1", [128, D], F32)
        kp = sb("kp", [128, D], F32)
        tmp2 = sb("tmp2", [128, D], F32)
        tmp3 = sb("tmp3", [128, 16, 64], F32)
        bsum = sb("bsum", [128, 16, 1], F32)
        bvo = [sb("bvo%d" % i, [128, 16, 64], F32) for i in range(2)]
        vto = [sb("vto%d" % i, [128, D], BF16) for i in range(2)]
        gto = [sb("gto%d" % i, [128, D], F32) for i in range(2)]
        p_t = ps("p_t", [128, G], F32)
        pp = [ps("pp%d" % i, [128, 2, 512], F32) for i in range(2)]
        bW, bB = S.dbuf("W"), S.dbuf("B")
        bbvo = [S.dbuf("bvo") for _ in range(2)]
        bvto = [S.dbuf("vto") for _ in range(2)]
        bgto = [S.dbuf("gto") for _ in range(2)]
        B = {n: S.buf(n) for n in ["ta", "sg1a", "sg1b", "tmp", "av", "t1", "kp", "tmp2", "tmp3", "bsum"]}
        B.update({n: S.pbuf(n) for n in ["p_t", "pp0", "pp1"]})
        bpp = [B["pp0"], B["pp1"]]
        for c in range(8):
            for (W_, w_) in ((Wr, w_r), (Wk, w_k), (Wv, w_v)):
                S.dma("pool", W_[:, c, :], w_.rearrange("(c p) f -> p c f", p=128)[:, c, :], bW, writes=[bW])
        S.dma("pool", A1[:], a1.rearrange("(c p) f -> p c f", p=128), bW, writes=[bW])
        S.dma("pool", G1[:], g1.rearrange("(c p) f -> p c f", p=128), bW, writes=[bW])
        S.dma("pool", A2[:], a2, bW, writes=[bW])
        S.dma("pool", G2a[:], g2[0:128, :], bW, writes=[bW])
        S.dma("pool", G2b[:], g2[128:160, :], bW, writes=[bW])
        for (t_, src) in ((a0B, a0), (kaB, ka_), (rkB, rk_)):
            S.dma("sp", t_[:], src.to_broadcast([128, D]), bB, writes=[bB])
        VB, BVB, GB = bouts
        npp = 0
        k = 0
        for g in range(ng):
            P.group(g)
            xr, xk, xv, xa, xg = P.xm[0], P.xm[2], P.xm[3], P.xm[4], P.xm[5]
            bxr, bxk, bxv, bxa, bxg = P.bxm[0], P.bxm[2], P.bxm[3], P.bxm[4], P.bxm[5]
            for c in range(8):
                S.op("pe", lambda e: e.matmul(p_t[0:64, :], lhsT=A1[:, c, :], rhs=xa[:, c, :], start=(c == 0), stop=(c == 7)),
                     reads=[bW, bxa], writes=[B["p_t"]], sig=(c == 7))
            S.op("act", lambda e: e.copy(out=ta[:], in_=p_t[0:64, :]), reads=[B["p_t"]], writes=[B["ta"]])
            for c in range(8):
                S.op("pe", lambda e: e.matmul(p_t[:, :], lhsT=G1[:, c, 0:128], rhs=xg[:, c, :], start=(c == 0), stop=(c == 7)),
                     reads=[bW, bxg], writes=[B["p_t"]], sig=(c == 7))
            S.op("act", lambda e: e.activation(out=sg1a[:], in_=p_t[:, :], func=AF.Sigmoid), reads=[B["p_t"]], writes=[B["sg1a"]])
            for c in range(8):
                S.op("pe", lambda e: e.matmul(p_t[0:32, :], lhsT=G1[:, c, 128:160], rhs=xg[:, c, :], start=(c == 0), stop=(c == 7)),
                     reads=[bW, bxg], writes=[B["p_t"]], sig=(c == 7))
            S.op("act", lambda e: e.activation(out=sg1b[:], in_=p_t[0:32, :], func=AF.Sigmoid), reads=[B["p_t"]], writes=[B["sg1b"]])
            for s in range(G // 128):
                ts = slice(s * 128, (s + 1) * 128)
                t0 = g * G + s * 128

                def big(xm_, bxm_, W_):
                    nonlocal npp
                    p, bp = pp[npp % 2], bpp[npp % 2]
                    npp += 1
                    for hh in range(2):
                        for c in range(8):
                            S.op("pe", lambda e: e.matmul(p[:, hh, :], lhsT=xm_[:, c, ts], rhs=W_[:, c, hh * 512:(hh + 1) * 512],
                                                          start=(c == 0), stop=(c == 7)), reads=[bW, bxm_], writes=[bp], sig=(c == 7))
                    return p, bp
                p, bp = pp[npp % 2], bpp[npp % 2]
                npp += 1
                for hh in range(2):
                    S.op("pe", lambda e: e.matmul(p[:, hh, :], lhsT=ta[:, ts], rhs=A2[:, hh * 512:(hh + 1) * 512], start=True, stop=True),
                         reads=[bW, B["ta"]], writes=[bp])
                S.op("dve", lambda e: e.tensor_tensor(out=tmp[:], in0=p[:].rearrange("p a b -> p (a b)"), in1=a0B[:], op=ALU.add),
                     reads=[bp, bB], writes=[B["tmp"]])
                S.op("act", lambda e: e.activation(out=av[:], in_=tmp[:], func=AF.Sigmoid), reads=[B["tmp"]], writes=[B["av"]])
                S.op("dve", lambda e: e.scalar_tensor_tensor(out=t1[:], in0=av[:], scalar=-1.0, in1=kaB[:], op0=ALU.add, op1=ALU.mult),
                     reads=[B["av"], bB], writes=[B["t1"]])
                p, bp = big(xk, bxk, Wk)
                S.op("dve", lambda e: e.scalar_tensor_tensor(out=kp[:], in0=t1[:], scalar=1.0, in1=p[:].rearrange("p a b -> p (a b)"),
                                                             op0=ALU.add, op1=ALU.mult), reads=[B["t1"], bp], writes=[B["kp"]])
                p, bp = big(xr, bxr, Wr)
                S.op("dve", lambda e: e.tensor_tensor(out=tmp2[:], in0=p[:].rearrange("p a b -> p (a b)"), in1=rkB[:], op=ALU.mult),
                     reads=[bp, bB], writes=[B["tmp2"]])
                S.op("dve", lambda e: e.tensor_tensor(out=tmp3[:].rearrange("p a b -> p (a b)"), in0=tmp2[:], in1=kp[:], op=ALU.mult),
                     reads=[B["tmp2"], B["kp"]], writes=[B["tmp3"]])
                S.op("dve", lambda e: e.tensor_reduce(out=bsum[:], in_=tmp3[:], axis=AX.X, op=ALU.add), reads=[B["tmp3"]], writes=[B["bsum"]])
                p, bp = big(xv, bxv, Wv)
                o, bo = bvo[k % 2], bbvo[k % 2]
                S.op("dve", lambda e: e.tensor_tensor(out=o[:], in0=p[:].rearrange("p a (h d) -> p (a h) d", d=64),
                                                      in1=bsum[:].to_broadcast([128, 16, 64]), op=ALU.mult), reads=[bp, B["bsum"]], writes=[bo])
                S.dma("sp", BV[t0:t0 + 128, :], o[:].rearrange("p a b -> p (a b)"), bo, reads=[bo], writes=[BVB])
                o, bo = vto[k % 2], bvto[k % 2]
                S.op("act", lambda e: e.copy(out=o[:], in_=p[:].rearrange("p a b -> p (a b)")), reads=[bp], writes=[bo])
                S.dma("sp", Vtok[t0:t0 + 128, :], o[:], bo, reads=[bo], writes=[VB])
                p, bp = pp[npp % 2], bpp[npp % 2]
                npp += 1
                for hh in range(2):
                    S.op("pe", lambda e: e.matmul(p[:, hh, :], lhsT=sg1a[:, ts], rhs=G2a[:, hh * 512:(hh + 1) * 512], start=True, stop=False),
                         reads=[bW, B["sg1a"]], writes=[bp], sig=False)
                    S.op("pe", lambda e: e.matmul(p[:, hh, :], lhsT=sg1b[:, ts], rhs=G2b[:, hh * 512:(hh + 1) * 512], start=False, stop=True),
                         reads=[bW, B["sg1b"]], writes=[bp])
                o, bo = gto[k % 2], bgto[k % 2]
                S.op("act", lambda e: e.copy(out=o[:], in_=p[:].rearrange("p a b -> p (a b)")), reads=[bp], writes=[bo])
                S.dma("sp", Gt[t0:t0 + 128, :], o[:], bo, reads=[bo], writes=[GB])
                k += 1
        S.barrier()
        P.release()
        for b in [bW, bB] + bbvo + bvto + bgto:
            S.release(b)


def rwkv_scan_phase(nc, S, RtT, KtT, AtT, BtT, WC, Vtok, Ysc, bins, bY, ntok, NI=4):
    nch = ntok // 64
    ngr = nch // 8
    with ExitStack() as es:
        sb = lambda name, shape, dt: es.enter_context(nc.sbuf_tensor(_uniq(name), shape, dt))
        ps = lambda name, shape, dt: es.enter_context(nc.psum_tensor(_uniq(name), shape, dt))
        MU = sb("MU", [128, 128], F32)
        MUI = sb("MUI", [128, 128], F32)
        ML = sb("ML", [128, 128], F32)
        I32 = sb("I32", [128, 128], F32)
        identb = sb("identb", [128, 128], BF16)
        bconst = S.buf("const")
        for (m, chm, pat, op) in ((MU, -1, 1, ALU.is_gt), (MUI, -1, 1, ALU.is_ge), (ML, 1, -1, ALU.is_gt)):
            S.op("pool", lambda e: e.memset(m[:], 1.0), writes=[bconst])
            S.op("pool", lambda e: e.affine_select(out=m[:], in_=m[:], pattern=[[pat, 128]], compare_op=op, fill=0.0, base=0,
                                                   channel_multiplier=chm), reads=[bconst], writes=[bconst])
        make_ident(nc, S, I32, bconst)
        make_ident(nc, S, identb, bconst)

        class Slot:
            pass
        slots = []
        for si in range(NI):
            s = Slot()
            n_ = lambda x: "%s_%d" % (x, si)
            s.AR = [sb(n_("AR%d" % j), [128, 8, 2, 128], BF16) for j in range(2)]
            s.Bd = [sb(n_("Bd%d" % j), [128, 8, 128], BF16) for j in range(2)]
            s.Kd = [sb(n_("Kd%d" % j), [128, 8, 128], BF16) for j in range(2)]
            s.bbd = [S.dbuf(n_("bd0")), S.dbuf(n_("bd1"))]
            s.Vs = [sb(n_("Vs%d" % j), [128, 8, 64], BF16) for j in range(2)]
            s.Yo = [sb(n_("Yo%d" % j), [128, 8, 64], F32) for j in range(2)]
            s.bYo = [S.dbuf(n_("Yo0")), S.dbuf(n_("Yo1"))]
            s.wcs = sb(n_("wcs"), [128, nch], F32)
            s.bwcs = S.dbuf(n_("wcs"))
            s.QX = [sb(n_("QX%d" % j), [128, 2, 128], BF16) for j in range(2)]
            s.P = [sb(n_("P%d" % j), [128, 128], BF16) for j in range(2)]
            s.bQX = [S.buf("QX") for _ in range(2)]
            s.bP = [S.buf("P") for _ in range(2)]
            for k in ("Mak", "Mrb", "Mrk", "BtT", "KtT"):
                setattr(s, k, [sb(n_(k) + "_%d" % q_, [128, 128], BF16) for q_ in range(2)])
                setattr(s, "b" + k, [S.buf(k) for _ in range(2)])
            s.TT = [sb(n_("TT%d" % q_), [128, 128], BF16) for q_ in range(2)]
            s.bTT = [S.buf("TT") for _ in range(2)]
            s.Xs = sb(n_("Xs"), [128, 64], BF16)
            s.Ub = sb(n_("Ub"), [128, 64], BF16)
            s.Sw = sb(n_("Sw"), [128, 64], F32)
            s.St = sb(n_("St"), [128, 64], F32)
            s.Sb = sb(n_("Sb"), [128, 64], BF16)
            s.bXs, s.bUb, s.bSw, s.bSt, s.bSb = [S.buf(k) for k in ("Xs", "Ub", "Sw", "St", "Sb")]
            s.psA = ps(n_("psA"), [128, 512], F32)
            s.psB = ps(n_("psB"), [128, 4, 128], F32)
            s.bA, s.bB = S.pbuf("bankA"), S.pbuf("bankB")
            s.ptr = s.psB[:, 3, :].bitcast(BF16)
            for j in range(2):
                S.op("pool", lambda e: e.memset(s.AR[j][:], 0.0), writes=[s.bbd[j]])
                S.op("pool", lambda e: e.memset(s.Bd[j][:], 0.0), writes=[s.bbd[j]])
                S.op("pool", lambda e: e.memset(s.Kd[j][:], 0.0), writes=[s.bbd[j]])
            slots.append(s)

        bRs, bKs, bAs, bBs, bWC, bV = bins

        def load_group(s, hp, gg):
            j = gg % 2
            t0 = gg * 512
            for h in range(2):
                r0 = hp * 128 + h * 64
                hs, cs = slice(h * 64, (h + 1) * 64), slice(h * 64, (h + 1) * 64)
                for (dst, src, bsrc) in ((s.AR[j][hs, :, 0, cs], AtT, bAs), (s.AR[j][hs, :, 1, cs], RtT, bRs),
                                         (s.Bd[j][hs, :, cs], BtT, bBs), (s.Kd[j][hs, :, cs], KtT, bKs)):
                    S.dma("sp", dst, src[r0:r0 + 64, t0:t0 + 512].rearrange("p (c j) -> p c j", j=64), s.bbd[j],
                          reads=[bsrc], writes=[s.bbd[j]])
                S.dma("sp", s.Vs[j][hs, :, :], Vtok[t0:t0 + 512, r0:r0 + 64].rearrange("(c j) v -> j c v", j=64),
                      s.bbd[j], reads=[bV], writes=[s.bbd[j]])

        def T_steps(gg, c):
            j = gg % 2
            q = (gg * 8 + c) % 2
            steps = []

            def st1():
                for s in slots:
                    AR, A, Bd, K = s.AR[j][:, c, :, :], s.AR[j][:, c, 0, :], s.Bd[j][:, c, :], s.Kd[j][:, c, :]
                    S.op("pe", lambda e: e.matmul(s.psA[:, 0:256], lhsT=Bd, rhs=AR, start=True, stop=True), reads=[s.bbd[j]], writes=[s.bA], sig=False)
                    S.op("pe", lambda e: e.matmul(s.psA[:, 256:512], lhsT=K, rhs=AR, start=True, stop=True), reads=[s.bbd[j]], writes=[s.bA])
                    S.op("pe", lambda e: e.matmul(s.psB[:, 1, :], lhsT=A, rhs=Bd, start=True, stop=True), reads=[s.bbd[j]], writes=[s.bB], sig=False)
                    S.op("pe", lambda e: e.transpose(s.ptr[:, 0:128], Bd, identb[:]), reads=[s.bbd[j], bconst], writes=[s.bB], sig=False)
                    S.op("pe", lambda e: e.transpose(s.ptr[:, 128:256], K, identb[:]), reads=[s.bbd[j], bconst], writes=[s.bB])
            steps.append(st1)

            def st2():
                for s in slots:
                    S.op("dve", lambda e: e.tensor_tensor(out=s.QX[0][:, 0, :], in0=s.psA[:, 0:128], in1=MU[:], op=ALU.mult),
                         reads=[s.bA, bconst], writes=[s.bQX[0]])
                    S.op("dve", lambda e: e.tensor_tensor(out=s.QX[1][:, 1, :], in0=s.QX[0][:, 0, :], in1=I32[:], op=ALU.add),
                         reads=[s.bQX[0], bconst], writes=[s.bQX[1]])
                    S.op("dve", lambda e: e.tensor_tensor(out=s.P[0][:], in0=s.psB[:, 1, :], in1=ML[:], op=ALU.mult),
                         reads=[s.bB, bconst], writes=[s.bP[0]])
            steps.append(st2)

            def st3():
                for s in slots:
                    for (nm, lo, msk) in (("Mrb", 128, MUI), ("Mak", 256, MU), ("Mrk", 384, MUI)):
                        S.op("dve", lambda e: e.tensor_tensor(out=getattr(s, nm)[q][:], in0=s.psA[:, lo:lo + 128], in1=msk[:], op=ALU.mult),
                             reads=[s.bA, bconst], writes=[getattr(s, "b" + nm)[q]])
                    S.op("act", lambda e: e.copy(out=s.BtT[q][:], in_=s.ptr[:, 0:128]), reads=[s.bB], writes=[s.bBtT[q]])
                    S.op("act", lambda e: e.copy(out=s.KtT[q][:], in_=s.ptr[:, 128:256]), reads=[s.bB], writes=[s.bKtT[q]])
            steps.append(st3)

            for lvl in range(6):
                cur = 0 if lvl == 0 else lvl % 2
                nxt = 1 - cur

                def sa(lvl=lvl, cur=cur):
                    for s in slots:
                        Q, X, Pm = s.QX[cur][:, 0, :], s.QX[cur][:, 1, :], s.P[cur][:]
                        rd = [s.bP[cur], s.bQX[cur]]
                        if lvl == 0:
                            S.op("pe", lambda e: e.matmul(s.psA[:, 0:128], lhsT=Pm, rhs=Q, start=True, stop=True), reads=rd, writes=[s.bA], sig=False)
                        elif lvl <= 3:
                            S.op("pe", lambda e: e.matmul(s.psA[:, 0:256], lhsT=Pm, rhs=s.QX[cur][:, :, :], start=True, stop=True),
                                 reads=rd, writes=[s.bA], sig=False)
                        else:
                            S.op("pe", lambda e: e.matmul(s.psA[:, 128:256], lhsT=Pm, rhs=X, start=True, stop=True), reads=rd, writes=[s.bA],
                                 sig=(lvl == 5))
                        if lvl <= 4:
                            S.op("pe", lambda e: e.matmul(s.psA[:, 256:384], lhsT=Q, rhs=Pm, start=True, stop=True), reads=rd, writes=[s.bA])

                def sb_(lvl=lvl, cur=cur, nxt=nxt):
                    for s in slots:
                        if lvl <= 3:
                            S.op("act", lambda e: e.copy(out=s.QX[nxt][:, 0, :], in_=s.psA[:, 0:128]), reads=[s.bA], writes=[s.bQX[nxt]])
                        if lvl <= 4:
                            S.op("act", lambda e: e.copy(out=s.P[nxt][:], in_=s.psA[:, 256:384]), reads=[s.bA], writes=[s.bP[nxt]])
                        if 1 <= lvl <= 4:
                            S.op("dve", lambda e: e.tensor_tensor(out=s.QX[nxt][:, 1, :], in0=s.psA[:, 128:256], in1=s.QX[cur][:, 1, :], op=ALU.add),
                                 reads=[s.bA, s.bQX[cur]], writes=[s.bQX[nxt]])
                        if lvl == 5:
                            S.op("dve", lambda e: e.tensor_tensor(out=s.TT[q][:], in0=s.psA[:, 128:256], in1=s.QX[cur][:, 1, :], op=ALU.add),
                                 reads=[s.bA, s.bQX[cur]], writes=[s.bTT[q]])
                steps += [sa, sb_]
            return steps

        def S_steps(gg, c):
            j = gg % 2
            ch = gg * 8 + c
            q = ch % 2

            def s1():
                for s in slots:
                    A = s.AR[j][:, c, 0, :]
                    S.op("pe", lambda e: e.matmul(s.psB[:, 0, 0:64], lhsT=A, rhs=s.Sb[:], start=True, stop=False),
                         reads=[s.bbd[j], s.bSb], writes=[s.bB], sig=False)
                    S.op("pe", lambda e: e.matmul(s.psB[:, 0, 0:64], lhsT=s.Mak[q][:], rhs=s.Vs[j][:, c, :], start=False, stop=True),
                         reads=[s.bMak[q], s.bbd[j]], writes=[s.bB])
                    S.op("pool", lambda e: e.tensor_scalar(out=s.Sw[:], in0=s.St[:], scalar1=s.wcs[:, ch:ch + 1], scalar2=None, op0=ALU.mult),
                         reads=[s.bSt, s.bwcs], writes=[s.bSw])

            def s2():
                for s in slots:
                    S.op("act", lambda e: e.copy(out=s.Xs[:], in_=s.psB[:, 0, 0:64]), reads=[s.bB], writes=[s.bXs])

            def s3():
                for s in slots:
                    S.op("pe", lambda e: e.matmul(s.psB[:, 1, 0:64], lhsT=s.TT[q][:], rhs=s.Xs[:], start=True, stop=True),
                         reads=[s.bTT[q], s.bXs], writes=[s.bB])

            def s4():
                for s in slots:
                    S.op("act", lambda e: e.copy(out=s.Ub[:], in_=s.psB[:, 1, 0:64]), reads=[s.bB], writes=[s.bUb])

            def s5():
                for s in slots:
                    R = s.AR[j][:, c, 1, :]
                    pY, pS = s.psB[:, 2, 0:64], s.psB[:, 0, 0:64]
                    S.op("pe", lambda e: e.matmul(pY, lhsT=R, rhs=s.Sb[:], start=True, stop=False),
                         reads=[s.bbd[j], s.bSb], writes=[s.bB], sig=False)
                    S.op("pe", lambda e: e.matmul(pY, lhsT=s.Mrb[q][:], rhs=s.Ub[:], start=False, stop=False),
                         reads=[s.bMrb[q], s.bUb], writes=[s.bB], sig=False)
                    S.op("pe", lambda e: e.matmul(pY, lhsT=s.Mrk[q][:], rhs=s.Vs[j][:, c, :], start=False, stop=True),
                         reads=[s.bMrk[q], s.bbd[j]], writes=[s.bB], sig=False)
                    S.op("pe", lambda e: e.matmul(pS, lhsT=s.BtT[q][:], rhs=s.Ub[:], start=True, stop=False),
                         reads=[s.bBtT[q], s.bUb], writes=[s.bB], sig=False)
                    S.op("pe", lambda e: e.matmul(pS, lhsT=s.KtT[q][:], rhs=s.Vs[j][:, c, :], start=False, stop=True),
                         reads=[s.bKtT[q], s.bbd[j]], writes=[s.bB])

            def s6():
                for s in slots:
                    S.op("dve", lambda e: e.scalar_tensor_tensor(out=s.St[:], in0=s.psB[:, 0, 0:64], scalar=s.wcs[:, ch:ch + 1], in1=s.Sw[:],
                                                                 op0=ALU.mult, op1=ALU.add), reads=[s.bB, s.bwcs, s.bSw], writes=[s.bSt])
                    S.op("dve", lambda e: e.tensor_copy(out=s.Yo[j][:, c, :], in_=s.psB[:, 2, 0:64]), reads=[s.bB], writes=[s.bYo[j]])
                    S.op("act", lambda e: e.copy(out=s.Sb[:], in_=s.St[:]), reads=[s.bSt], writes=[s.bSb])
            return [s1, s2, s3, s4, s5, s6]

        for rnd in range(8 // NI):
            hps = [rnd * NI + i for i in range(NI)]
            for s, hp in zip(slots, hps):
                S.dma("sp", s.wcs[:], WC[hp * 128:(hp + 1) * 128, 0:nch], s.bwcs, reads=[bWC], writes=[s.bwcs])
                S.op("pool", lambda e: e.memset(s.St[:], 0.0), writes=[s.bSt])
                S.op("pool", lambda e: e.memset(s.Sb[:], 0.0), writes=[s.bSb])
                load_group(s, hp, 0)
            if ngr > 1:
                for s, hp in zip(slots, hps):
                    load_group(s, hp, 1)
            for st in T_steps(0, 0):
                st()
            for gg in range(ngr):
                j = gg % 2
                for c in range(8):
                    ch = gg * 8 + c
                    if ch + 1 < nch:
                        ng_, nc_ = (gg, c + 1) if c < 7 else (gg + 1, 0)
                        tsteps = T_steps(ng_, nc_)
                    else:
                        tsteps = []
                    ssteps = S_steps(gg, c)
                    ti = 0
                    for ss_ in ssteps:
                        for _ in range(3):
                            if ti < len(tsteps):
                                tsteps[ti]()
                                ti += 1
                        ss_()
                    while ti < len(tsteps):
                        tsteps[ti]()
                        ti += 1
                for s, hp in zip(slots, hps):
                    for h in range(2):
                        c0 = hp * 128 + h * 64
                        S.dma("sp", Ysc[gg * 512:(gg + 1) * 512, c0:c0 + 64].rearrange("(c j) v -> j c v", j=64),
                              s.Yo[j][h * 64:(h + 1) * 64, :, :], s.bYo[j], reads=[s.bYo[j]], writes=[bY])
                if gg + 2 < ngr:
                    for s, hp in zip(slots, hps):
                        load_group(s, hp, gg + 2)
        S.barrier()
        for s in slots:
            for b_ in s.bbd + s.bYo + [s.bwcs]:
                S.release(b_)


def rwkv_post_phase(nc, S, Ysc, BV, Gt, lg_row, lb_row, ZT, bins, bZT, ntok, gn_eps=64e-5):
    with ExitStack() as es:
        sb = lambda name, shape, dt: es.enter_context(nc.sbuf_tensor(_uniq(name), shape, dt))
        ps = lambda name, shape, dt: es.enter_context(nc.psum_tensor(_uniq(name), shape, dt))
        lgB = sb("lgB", [128, D], F32)
        lbB = sb("lbB", [128, D], F32)
        ident = sb("ident", [128, 128], BF16)
        nh = sb("nh", [128, 16, 1], F32)
        yt = [sb("yt%d" % i, [128, 16, 64], F32) for i in range(2)]
        bvt = [sb("bvt%d" % i, [128, D], F32) for i in range(2)]
        gt = [sb("gt%d" % i, [128, D], F32) for i in range(2)]
        sm = sb("sm", [128, 16, 1], F32)
        vr = sb("vr", [128, 16, 1], F32)
        rstd = sb("rstd", [128, 16, 1], F32)
        yc = sb("yc", [128, 16, 64], F32)
        sq = sb("sq", [128, 16, 64], F32)
        yn = sb("yn", [128, 16, 64], F32)
        y2 = sb("y2", [128, D], F32)
        zb = [sb("zb%d" % i, [128, D], BF16) for i in range(2)]
        zT = [sb("zT%d" % i, [128, 8, 512], BF16) for i in range(2)]
        p_tr = [ps("p_tr%d" % i, [128, 8, 128], BF16) for i in range(2)]
        bC = S.dbuf("C")
        byt = [S.dbuf("yt") for _ in range(2)]
        bbvt = [S.dbuf("bvt") for _ in range(2)]
        bgt = [S.dbuf("gt") for _ in range(2)]
        bzT = [S.dbuf("zT") for _ in range(2)]
        B = {n: S.buf(n) for n in ["ident", "nh", "sm", "vr", "rstd", "yc", "sq", "yn", "y2", "zb0", "zb1"]}
        bp_tr = [S.pbuf("ptr") for _ in range(2)]
        bYs, bBV, bG = bins
        make_ident(nc, S, ident, B["ident"])
        S.op("pool", lambda e: e.memset(nh[:], -0.5), writes=[B["nh"]])
        S.dma("sp", lgB[:], lg_row.to_broadcast([128, D]), bC, writes=[bC])
        S.dma("sp", lbB[:], lb_row.to_broadcast([128, D]), bC, writes=[bC])
        nt = ntok // 128
        for t in range(nt):
            i = t % 2
            t0 = t * 128
            S.dma("sp", yt[i][:].rearrange("p a b -> p (a b)"), Ysc[t0:t0 + 128, :], byt[i], reads=[bYs], writes=[byt[i]])
            S.dma("sp", bvt[i][:], BV[t0:t0 + 128, :], bbvt[i], reads=[bBV], writes=[bbvt[i]])
            S.dma("sp", gt[i][:], Gt[t0:t0 + 128, :], bgt[i], reads=[bG], writes=[bgt[i]])
            y3 = yt[i]
            S.op("dve", lambda e: e.tensor_reduce(out=sm[:], in_=y3[:], axis=AX.X, op=ALU.add), reads=[byt[i]], writes=[B["sm"]])
            S.op("dve", lambda e: e.tensor_scalar(out=sm[:], in0=sm[:], scalar1=1.0 / 64, scalar2=None, op0=ALU.mult), reads=[B["sm"]], writes=[B["sm"]])
            S.op("dve", lambda e: e.tensor_tensor(out=yc[:], in0=y3[:], in1=sm[:].to_broadcast([128, 16, 64]), op=ALU.subtract),
                 reads=[byt[i], B["sm"]], writes=[B["yc"]])
            S.op("dve", lambda e: e.tensor_tensor(out=sq[:], in0=yc[:], in1=yc[:], op=ALU.mult), reads=[B["yc"]], writes=[B["sq"]])
            S.op("dve", lambda e: e.tensor_reduce(out=vr[:], in_=sq[:], axis=AX.X, op=ALU.add), reads=[B["sq"]], writes=[B["vr"]])
            S.op("dve", lambda e: e.tensor_scalar(out=vr[:], in0=vr[:], scalar1=1.0 / 64, scalar2=gn_eps, op0=ALU.mult, op1=ALU.add),
                 reads=[B["vr"]], writes=[B["vr"]])
            S.op("pool", lambda e: e.tensor_tensor(out=rstd[:], in0=vr[:], in1=nh[:], op=ALU.pow), reads=[B["vr"], B["nh"]], writes=[B["rstd"]])
            S.op("dve", lambda e: e.tensor_tensor(out=yn[:], in0=yc[:], in1=rstd[:].to_broadcast([128, 16, 64]), op=ALU.mult),
                 reads=[B["yc"], B["rstd"]], writes=[B["yn"]])
            ynf = yn[:].rearrange("p a b -> p (a b)")
            S.op("dve", lambda e: e.tensor_tensor(out=y2[:], in0=ynf, in1=lgB[:], op=ALU.mult), reads=[B["yn"], bC], writes=[B["y2"]])
            S.op("dve", lambda e: e.tensor_tensor(out=y2[:], in0=y2[:], in1=lbB[:], op=ALU.add), reads=[B["y2"], bC], writes=[B["y2"]])
            S.op("dve", lambda e: e.tensor_tensor(out=y2[:], in0=y2[:], in1=bvt[i][:], op=ALU.add), reads=[B["y2"], bbvt[i]], writes=[B["y2"]])
            z, bz = zb[i], B["zb%d" % i]
            S.op("dve", lambda e: e.tensor_tensor(out=z[:], in0=y2[:], in1=gt[i][:], op=ALU.mult), reads=[B["y2"], bgt[i]], writes=[bz])
            pt, bpt = p_tr[i], bp_tr[i]
            for c in range(8):
                S.op("pe", lambda e: e.transpose(pt[:, c, :], z[:, c * 128:(c + 1) * 128], ident[:]), reads=[bz, B["ident"]], writes=[bpt], sig=(c == 7))
            gi = (t // 4) % 2
            S.op("act", lambda e: e.copy(out=zT[gi][:, :, (t % 4) * 128:(t % 4 + 1) * 128], in_=pt[:]), reads=[bpt], writes=[bzT[gi]])
            if t % 4 == 3:
                g0 = (t // 4) * 512
                for c in range(8):
                    S.dma("sp", ZT[c * 128:(c + 1) * 128, g0:g0 + 512], zT[gi][:, c, :], bzT[gi], reads=[bzT[gi]], writes=[bZT])
        S.barrier()
        for b in [bC] + byt + bbvt + bgt + bzT:
            S.release(b)


def build_program(ntok=SEQ):
    nc = bass.Bass("TRN2", target_bir_lowering=False)
    di = lambda n, s: nc.dram_tensor(n, list(s), F32, kind="ExternalInput").ap()
    x = di("x", [ntok, D])
    ffn_norm = di("ffn_norm", [4, D])
    wg = di("ffn_w_gate", [2, 2, D, DFF])
    wu = di("ffn_w_up", [2, 2, D, DFF])
    wd = di("ffn_w_down", [2, 2, DFF, D])
    mix_norm = di("mix_norm", [2, D])
    dbias = di("dbias", [12, 128, 2, 128])
    w_in = di("attn_w_in", [D, 3072])
    qn = di("attn_q_norm", [64, 1])
    kn = di("attn_k_norm", [64, 1])
    w_out = di("attn_w_out", [512, D])
    rw_mix = di("rw_mix", [6, D])
    rows = {n: di(n, [1, D]) for n in ("rw_w0", "rw_a0", "rw_kk", "rw_ka", "rw_rk", "rw_lnx_g", "rw_lnx_b")}
    rw_w1 = di("rw_w1", [D, 64]); rw_w2 = di("rw_w2", [64, D]); rw_a1 = di("rw_a1", [D, 64]); rw_a2 = di("rw_a2", [64, D])
    rw_g1 = di("rw_g1", [D, 160]); rw_g2 = di("rw_g2", [160, D])
    rw_wr = di("rw_wr", [D, D]); rw_wk = di("rw_wk", [D, D]); rw_wv = di("rw_wv", [D, D]); rw_wo = di("rw_wo", [D, D])
    out = nc.dram_tensor("out", [ntok, D], F32, kind="ExternalOutput").ap()
    scr = lambda n, s, dt: nc.dram_tensor(n, list(s), dt, kind="Internal").ap()
    xa = scr("xa", [ntok, D], F32); xb = scr("xb", [ntok, D], F32)
    QT = scr("QT", [D, ntok], BF16); KT = scr("KT", [D, ntok], BF16); V = scr("V", [ntok, D], BF16); MT = scr("MT", [512, ntok], BF16)
    RtT, KtT, AtT, BtT = [scr(n, [D, ntok], BF16) for n in ("RtT", "KtT", "AtT", "BtT")]
    WC = scr("WC", [D, ntok // 64], F32)
    Vtok = scr("Vtok", [ntok, D], BF16); BV = scr("BV", [ntok, D], F32); Gt = scr("Gt", [ntok, D], F32); Ysc = scr("Ysc", [ntok, D], F32)
    ZT = scr("ZT", [D, ntok], BF16)

    S = Sched(nc, n_dma_sems=24)
    nb = lambda n: S.buf(n, acc=True)
    bx, bxa, bxb, bout = nb("x"), nb("xa"), nb("xb"), nb("out")
    bQT, bKT, bV, bMT = nb("QT"), nb("KT"), nb("V"), nb("MT")
    bR, bK, bA, bB, bWC, bVt, bBV, bG, bY, bZ = [nb(n) for n in ("R", "K", "A", "B", "WC", "Vt", "BV", "G", "Y", "Z")]

    ffn_phase(nc, S, x, xa, bx, bxa, wg[0, 0], wu[0, 0], wd[0, 0], ffn_norm[0:1, :], ntok)
    attn_in_phase(nc, S, xa, bxa, w_in, mix_norm[0:1, :], qn, kn, QT, KT, V, bQT, bKT, bV, ntok)
    sb_attn_phase(nc, S, QT, KT, V, MT, bQT, bKT, bV, bMT, ntok)
    dil_attn_phase(nc, S, QT, KT, V, MT, dbias, bQT, bKT, bV, bMT, ntok)
    out_proj_phase(nc, S, xa, xb, bxa, bxb, MT, bMT, w_out, 512, ntok)
    ffn_phase(nc, S, xb, xa, bxb, bxa, wg[0, 1], wu[0, 1], wd[0, 1], ffn_norm[1:2, :], ntok)
    ffn_phase(nc, S, xa, xb, bxa, bxb, wg[1, 0], wu[1, 0], wd[1, 0], ffn_norm[2:3, :], ntok)
    rwkv_fm_phase(nc, S, xb, bxb, mix_norm[1:2, :], rw_mix, rows["rw_w0"], rw_w1, rw_w2, rows["rw_a0"], rw_a1, rw_a2,
                  rows["rw_kk"], rows["rw_ka"], rw_wr, rw_wk, RtT, KtT, AtT, BtT, WC, [bR, bK, bA, bB, bWC], ntok)
    rwkv_tm_phase(nc, S, xb, bxb, mix_norm[1:2, :], rw_mix, rows["rw_a0"], rw_a1, rw_a2, rw_g1, rw_g2, rows["rw_ka"], rows["rw_rk"],
                  rw_wr, rw_wk, rw_wv, Vtok, BV, Gt, [bVt, bBV, bG], ntok)
    rwkv_scan_phase(nc, S, RtT, KtT, AtT, BtT, WC, Vtok, Ysc, [bR, bK, bA, bB, bWC, bVt], bY, ntok)
    rwkv_post_phase(nc, S, Ysc, BV, Gt, rows["rw_lnx_g"], rows["rw_lnx_b"], ZT, [bY, bBV, bG], bZ, ntok)
    out_proj_phase(nc, S, xb, xa, bxb, bxa, ZT, bZ, rw_wo, 1024, ntok)
    ffn_phase(nc, S, xa, out, bxa, bout, wg[1, 1], wu[1, 1], wd[1, 1], ffn_norm[3:4, :], ntok)
    S.wait_for("sp", [bout])
    return nc


def kernel(x, ffn_norm, ffn_w_gate, ffn_w_up, ffn_w_down, mix_norm, rel_bias,
           attn_w_in, attn_q_norm, attn_k_norm, attn_w_out,
           rw_mix, rw_w0, rw_w1, rw_w2, rw_a0, rw_a1, rw_a2, rw_g1, rw_g2,
           rw_kk, rw_ka, rw_rk, rw_wr, rw_wk, rw_wv, rw_wo, rw_lnx_g, rw_lnx_b):
    f = lambda a: np.ascontiguousarray(np.asarray(a, dtype=np.float32))
    x = f(x)
    n = x.shape[0]
    shared = {
        "ffn_norm": f(ffn_norm).reshape(4, D), "ffn_w_gate": f(ffn_w_gate), "ffn_w_up": f(ffn_w_up), "ffn_w_down": f(ffn_w_down),
        "mix_norm": f(mix_norm), "dbias": dil_bias_host(f(rel_bias)),
        "attn_w_in": f(attn_w_in)[0], "attn_q_norm": f(attn_q_norm).reshape(64, 1), "attn_k_norm": f(attn_k_norm).reshape(64, 1),
        "attn_w_out": f(attn_w_out)[0], "rw_mix": f(rw_mix)[0],
        "rw_w0": f(rw_w0).reshape(1, D), "rw_a0": f(rw_a0).reshape(1, D), "rw_kk": f(rw_kk).reshape(1, D), "rw_ka": f(rw_ka).reshape(1, D),
        "rw_rk": f(rw_rk).reshape(1, D), "rw_lnx_g": f(rw_lnx_g).reshape(1, D), "rw_lnx_b": f(rw_lnx_b).reshape(1, D),
        "rw_w1": f(rw_w1)[0], "rw_w2": f(rw_w2)[0], "rw_a1": f(rw_a1)[0], "rw_a2": f(rw_a2)[0], "rw_g1": f(rw_g1)[0], "rw_g2": f(rw_g2)[0],
        "rw_wr": f(rw_wr)[0], "rw_wk": f(rw_wk)[0], "rw_wv": f(rw_wv)[0], "rw_wo": f(rw_wo)[0],
    }
    nc = build_program(x.shape[1])
    in_maps = [dict(shared, x=x[i]) for i in range(n)]
    res = run_bass_kernel_spmd(nc, in_maps, core_ids=list(range(n)))
    return np.stack([np.asarray(r["out"]) for r in res.results], axis=0).astype(np.float32)
```

```python
import numpy as np
from contextlib import ExitStack
import concourse.bass as bass
import concourse.mybir as mybir
from concourse.bass_utils import run_bass_kernel_spmd

F32 = mybir.dt.float32
BF16 = mybir.dt.bfloat16
AF = mybir.ActivationFunctionType
ALU = mybir.AluOpType
AX = mybir.AxisListType

D = 1024
DFF = 2816
NF = DFF // 128
SEQ = 4096


_UID = [0]


def _uniq(name):
    _UID[0] += 1
    return "%s_u%d" % (name, _UID[0])


def _merge(d, s):
    for k, v in s.items():
        if d.get(k, 0) < v:
            d[k] = v


class Buf:
    __slots__ = ("name", "wr", "rd", "acc", "dkey", "excl")

    def __init__(self, name, acc=False, excl=False):
        self.name = name
        self.wr = {}
        self.rd = {}
        self.acc = acc
        self.dkey = None
        self.excl = excl


class Sched:
    ENG = ("pe", "act", "dve", "pool", "sp")

    def __init__(self, nc, n_dma_sems=40):
        self.nc = nc
        self.eng = {"pe": nc.tensor, "act": nc.scalar, "dve": nc.vector, "pool": nc.gpsimd, "sp": nc.sync}
        self.sems = {}
        self.val = {}
        self.seen = {e: {} for e in self.ENG}
        self.epoch = 0
        self.ekey = {}
        self._new_engine_sems()
        self.dma_pool = []
        for i in range(n_dma_sems):
            k = "dma%d" % i
            self.sems[k] = nc.semaphore(k).__enter__()
            self.val[k] = 0
            self.dma_pool.append(k)
        self.nwait = 0

    def _new_engine_sems(self):
        for e in self.ENG:
            k = "%s_e%d" % (e, self.epoch)
            self.sems[k] = self.nc.semaphore(k).__enter__()
            self.val[k] = 0
            self.ekey[e] = k

    def buf(self, name, acc=False):
        return Buf(name, acc)

    def pbuf(self, name):
        return Buf(name, False, True)

    def dbuf(self, name, acc=False):
        b = Buf(name, acc)
        b.dkey = self.dma_pool.pop()
        return b

    def release(self, b):
        self.dma_pool.append(b.dkey)
        b.dkey = None

    def _wait(self, e, deps):
        for k, v in deps.items():
            if v <= 0:
                continue
            if e == "pe" and k == self.ekey["pe"]:
                continue
            if self.seen[e].get(k, 0) < v:
                self.eng[e].wait_ge(self.sems[k], v)
                self.seen[e][k] = v
                self.nwait += 1

    def _deps(self, reads, writes, e=None):
        deps = {}
        for b in reads:
            _merge(deps, b.wr)
            if b.excl:
                own = self.ekey.get(e)
                _merge(deps, {k: v for k, v in b.rd.items() if k != own})
        for b in writes:
            _merge(deps, b.wr)
            _merge(deps, b.rd)
        return deps

    def _record(self, ev, reads, writes):
        for b in reads:
            _merge(b.rd, ev)
        for b in writes:
            if b.acc:
                _merge(b.wr, ev)
            else:
                b.wr = dict(ev)
                b.rd = {}

    def op(self, e, fn, reads=(), writes=(), sig=True):
        self._wait(e, self._deps(reads, writes, e))
        ins = fn(self.eng[e])
        k = self.ekey[e]
        if sig:
            ins.then_inc(self.sems[k], 1)
            self.val[k] += 1
            v = self.val[k]
        else:
            v = self.val[k] + 1
        self._record({k: v}, reads, writes)
        return ins

    def dma(self, q, out, in_, track, reads=(), writes=(), slow=False):
        self._wait(q, self._deps(reads, writes))
        if slow:
            ins = self.eng[q].dma_start(out=out, in_=in_, allow_slow_non_contiguous=True)
        else:
            ins = self.eng[q].dma_start(out=out, in_=in_)
        k = track.dkey
        ins.then_inc(self.sems[k], 16)
        self.val[k] += 16
        self._record({k: self.val[k]}, reads, writes)
        return ins

    def barrier(self, new_epoch=True):
        allv = {k: v for k, v in self.val.items() if v > 0}
        for e in self.ENG:
            self._wait(e, allv)
        if new_epoch:
            self.epoch += 1
            self._new_engine_sems()

    def wait_for(self, e, bufs):
        deps = {}
        for b in bufs:
            _merge(deps, b.wr)
            _merge(deps, b.rd)
        self._wait(e, deps)


def ffn_phase(nc, S, x_in, x_out, xin_b, xout_b, wg, wu, wd, gain_row, ntok, eps=1e-6):
    G = 256
    ng = ntok // G
    with ExitStack() as es:
        sb = lambda name, shape, dt: es.enter_context(nc.sbuf_tensor(_uniq(name), shape, dt))
        ps = lambda name, shape, dt: es.enter_context(nc.psum_tensor(_uniq(name), shape, dt))
        Wg = sb("Wg", [128, 8, DFF], BF16)
        Wu = sb("Wu", [128, 8, DFF], BF16)
        Wd = sb("Wd", [128, NF, D], BF16)
        gB = sb("gB", [128, D], F32)
        ident = sb("ident", [128, 128], BF16)
        xt = [sb("xt%d" % i, [128, D], F32) for i in range(4)]
        ot = [sb("ot%d" % i, [128, D], F32) for i in range(2)]
        hb = [sb("hb%d" % i, [128, D], BF16) for i in range(2)]
        hT = [sb("hT%d" % i, [128, 8, G], BF16) for i in range(2)]
        aT = [sb("aT%d" % i, [128, G], BF16) for i in range(3)]
        sg = [sb("sg%d" % i, [128, G], F32) for i in range(2)]
        junk = sb("junk", [128, D], BF16)
        ss = [sb("ss%d" % i, [128, 1], F32) for i in range(2)]
        rs = [sb("rs%d" % i, [128, 1], F32) for i in range(2)]
        nh = sb("nh", [128, 1], F32)
        p_gu = [ps("p_gu%d" % i, [128, 2, G], F32) for i in range(2)]
        p_dn = [ps("p_dn%d" % i, [128, 512], F32) for i in range(4)]
        p_tr = [ps("p_tr%d" % i, [128, 8, 128], BF16) for i in range(2)]

        FB = [(0, 6), (6, 12), (12, 17), (17, 22)]
        blk_of = {}
        for bi, (j0, j1) in enumerate(FB):
            for j in range(j0, j1):
                blk_of[j] = bi
        bWgL = [S.dbuf("Wg") for _ in FB]
        bWuL = [S.dbuf("Wu") for _ in FB]
        bWdL = [S.dbuf("Wd") for _ in FB]
        bgB = S.dbuf("gB")
        bxt = [S.dbuf("xt%d" % i) for i in range(4)]
        bot = [S.dbuf("ot%d" % i) for i in range(2)]
        bhb = [S.buf("hb") for _ in range(2)]
        bhT = [S.buf("hT") for _ in range(2)]
        baT = [S.buf("aT") for _ in range(3)]
        bsg = [S.buf("sg") for _ in range(2)]
        bjunk = S.buf("junk")
        bss = [S.buf("ss") for _ in range(2)]
        brs = [S.buf("rs") for _ in range(2)]
        bnh, bident = S.buf("nh"), S.buf("ident")
        bp_gu = [S.pbuf("pgu") for _ in range(2)]
        bp_dn = [S.pbuf("pdn") for _ in range(4)]
        bp_tr = [S.pbuf("ptr") for _ in range(2)]

        S.op("pool", lambda e: e.memset(nh[:], -0.5), writes=[bnh])
        S.op("pool", lambda e: e.memset(ident[:], 0.0), writes=[bident])
        S.op("pool", lambda e: e.affine_select(out=ident[:], in_=ident[:], pattern=[[-1, 128]],
                                               compare_op=ALU.not_equal, fill=1.0, base=0,
                                               channel_multiplier=1), reads=[bident], writes=[bident])
        S.dma("sp", gB[:], gain_row.to_broadcast([128, D]), bgB, writes=[bgB])
        wg_v = wg.rearrange("(c p) f -> p c f", p=128)
        wu_v = wu.rearrange("(c p) f -> p c f", p=128)
        wd_v = wd.rearrange("(c p) f -> p c f", p=128)
        for bi, (j0, j1) in enumerate(FB):
            f0, f1 = j0 * 128, j1 * 128
            S.dma("pool", Wg[:, :, f0:f1], wg_v[:, :, f0:f1], bWgL[bi], writes=[bWgL[bi]])
            S.dma("pool", Wu[:, :, f0:f1], wu_v[:, :, f0:f1], bWuL[bi], writes=[bWuL[bi]])
            S.dma("pool", Wd[:, j0:j1, :], wd_v[:, j0:j1, :], bWdL[bi], writes=[bWdL[bi]])

        def load(g):
            for s in range(2):
                i = (g % 2) * 2 + s
                t0 = g * G + s * 128
                S.dma("sp", xt[i][:], x_in[t0:t0 + 128, :], bxt[i], reads=[xin_b], writes=[bxt[i]])

        def prep_dve(g):
            for s in range(2):
                i = (g % 2) * 2 + s
                S.op("dve", lambda e: e.scalar_tensor_tensor(out=junk[:], in0=xt[i][:], scalar=1.0, in1=xt[i][:],
                                                             op0=ALU.mult, op1=ALU.mult, accum_out=ss[s][:]),
                     reads=[bxt[i]], writes=[bjunk, bss[s]])
                S.op("dve", lambda e: e.tensor_scalar(out=ss[s][:], in0=ss[s][:], scalar1=1.0 / D, scalar2=eps,
                                                      op0=ALU.mult, op1=ALU.add), reads=[bss[s]], writes=[bss[s]])
                S.op("pool", lambda e: e.tensor_tensor(out=rs[s][:], in0=ss[s][:], in1=nh[:], op=ALU.pow),
                     reads=[bss[s], bnh], writes=[brs[s]])
                S.op("dve", lambda e: e.scalar_tensor_tensor(out=hb[s][:], in0=xt[i][:], scalar=rs[s][:], in1=gB[:],
                                                             op0=ALU.mult, op1=ALU.mult),
                     reads=[bxt[i], brs[s], bgB], writes=[bhb[s]])

        def prep_pe(g):
            for s in range(2):
                for c in range(8):
                    S.op("pe", lambda e: e.transpose(p_tr[s][:, c, :], hb[s][:, c * 128:(c + 1) * 128], ident[:]),
                         reads=[bhb[s], bident], writes=[bp_tr[s]], sig=(c == 7))
                S.op("act", lambda e: e.copy(out=hT[g % 2][:, :, s * 128:(s + 1) * 128], in_=p_tr[s][:]),
                     reads=[bp_tr[s]], writes=[bhT[g % 2]])

        def gate_up(g, j):
            h = hT[g % 2]
            pg = p_gu[j % 2]
            for c in range(8):
                S.op("pe", lambda e: e.matmul(pg[:, 0, :], lhsT=Wg[:, c, j * 128:(j + 1) * 128], rhs=h[:, c, :],
                                              start=(c == 0), stop=(c == 7)),
                     reads=[bWgL[blk_of[j]], bhT[g % 2]], writes=[bp_gu[j % 2]], sig=False)
            for c in range(8):
                S.op("pe", lambda e: e.matmul(pg[:, 1, :], lhsT=Wu[:, c, j * 128:(j + 1) * 128], rhs=h[:, c, :],
                                              start=(c == 0), stop=(c == 7)),
                     reads=[bWuL[blk_of[j]], bhT[g % 2]], writes=[bp_gu[j % 2]], sig=(c == 7))
            S.op("act", lambda e: e.activation(out=sg[j % 2][:], in_=pg[:, 0, :], func=AF.Silu),
                 reads=[bp_gu[j % 2]], writes=[bsg[j % 2]])
            S.op("dve", lambda e: e.tensor_tensor(out=aT[j % 3][:], in0=sg[j % 2][:], in1=pg[:, 1, :], op=ALU.mult),
                 reads=[bsg[j % 2], bp_gu[j % 2]], writes=[baT[j % 3]])

        def down(g, j):
            for s in range(2):
                for hh in range(2):
                    S.op("pe", lambda e: e.matmul(p_dn[s * 2 + hh][:], lhsT=aT[j % 3][:, s * 128:(s + 1) * 128],
                                                  rhs=Wd[:, j, hh * 512:(hh + 1) * 512],
                                                  start=(j == 0), stop=(j == NF - 1)),
                         reads=[baT[j % 3], bWdL[blk_of[j]]], writes=[bp_dn[s * 2 + hh]], sig=(j == NF - 1 or (s == 1 and hh == 1)))

        def epilogue(g):
            for s in range(2):
                i = (g % 2) * 2 + s
                for hh in range(2):
                    S.op("dve", lambda e: e.scalar_tensor_tensor(out=ot[s][:, hh * 512:(hh + 1) * 512], in0=p_dn[s * 2 + hh][:],
                                                                 scalar=0.5, in1=xt[i][:, hh * 512:(hh + 1) * 512],
                                                                 op0=ALU.mult, op1=ALU.add),
                         reads=[bp_dn[s * 2 + hh], bxt[i]], writes=[bot[s]])
                t0 = g * G + s * 128
                S.dma("sp", x_out[t0:t0 + 128, :], ot[s][:], bot[s], reads=[bot[s]], writes=[xout_b])

        load(0)
        if ng > 1:
            load(1)
        prep_dve(0)
        prep_pe(0)
        for g in range(ng):
            gate_up(g, 0)
            for j in range(NF):
                if j + 1 < NF:
                    gate_up(g, j + 1)
                elif g + 1 < ng:
                    pass
                down(g, j)
                if j == 3 and g + 1 < ng:
                    prep_dve(g + 1)
                if j == 14 and g + 1 < ng:
                    prep_pe(g + 1)
            epilogue(g)
            if g + 2 < ng:
                load(g + 2)
        S.barrier()
        for b in bWgL + bWuL + bWdL + [bgB] + bxt + bot:
            S.release(b)


def make_ident(nc, S, ident, bident, dt_is_bf16=True):
    S.op("pool", lambda e: e.memset(ident[:], 0.0), writes=[bident])
    S.op("pool", lambda e: e.affine_select(out=ident[:], in_=ident[:], pattern=[[-1, 128]],
                                           compare_op=ALU.not_equal, fill=1.0, base=0,
                                           channel_multiplier=1), reads=[bident], writes=[bident])


def attn_in_phase(nc, S, x_in, xin_b, w_in, gain_row, qn, kn, QT, KT, V, bQT, bKT, bV, ntok, eps=1e-6):
    G = 512
    ng = ntok // G
    with ExitStack() as es:
        sb = lambda name, shape, dt: es.enter_context(nc.sbuf_tensor(_uniq(name), shape, dt))
        ps = lambda name, shape, dt: es.enter_context(nc.psum_tensor(_uniq(name), shape, dt))
        Win = sb("Win", [128, 8, 3072], BF16)
        gB = sb("gB", [128, D], F32)
        ident = sb("ident", [128, 128], BF16)
        bones = sb("bones", [128, 128], BF16)
        gq = sb("gq", [128, 1], F32)
        gk = sb("gk", [128, 1], F32)
        nh = sb("nh", [128, 1], F32)
        eps_ap = sb("eps_ap", [128, 1], F32)
        xt = [sb("xt%d" % i, [128, D], F32) for i in range(4)]
        hb = [sb("hb%d" % i, [128, D], BF16) for i in range(2)]
        hT = [sb("hT%d" % i, [128, 8, G], BF16) for i in range(2)]
        junk = sb("junk", [128, D], BF16)
        ss = [sb("ss%d" % i, [128, 1], F32) for i in range(2)]
        rs = [sb("rs%d" % i, [128, 1], F32) for i in range(2)]
        ob = [sb("ob%d" % i, [128, G], BF16) for i in range(3)]
        sq = [sb("sq%d" % i, [128, G], BF16) for i in range(2)]
        lt = [sb("lt%d" % i, [128, G], F32) for i in range(2)]
        rr = [sb("rr%d" % i, [128, G], F32) for i in range(2)]
        vb = [sb("vb%d" % i, [128, D], BF16) for i in range(2)]
        p_q = [ps("p_q%d" % i, [128, G], F32) for i in range(3)]
        p_s = [ps("p_s%d" % i, [128, G], F32) for i in range(1)]
        p_v = [ps("p_v%d" % i, [128, 512], F32) for i in range(2)]
        p_tr = [ps("p_tr%d" % i, [128, 8, 128], BF16) for i in range(2)]

        bWin, bgB, bgq, bgk = S.dbuf("Win"), S.dbuf("gB"), S.dbuf("gq"), S.dbuf("gk")
        bxt = [S.dbuf("xt") for _ in range(4)]
        bob = [S.dbuf("ob") for _ in range(3)]
        bvb = [S.dbuf("vb") for _ in range(2)]
        bhb = [S.buf("hb") for _ in range(2)]
        bhT = [S.buf("hT") for _ in range(2)]
        bjunk, bnh, bident, bbones = S.buf("junk"), S.buf("nh"), S.buf("ident"), S.buf("bones")
        bss = [S.buf("ss") for _ in range(2)]
        brs = [S.buf("rs") for _ in range(2)]
        bsq = [S.buf("sq") for _ in range(2)]
        blt = [S.buf("lt") for _ in range(2)]
        brr = [S.buf("rr") for _ in range(2)]
        bp_q = [S.pbuf("pq") for _ in range(3)]
        bp_s = [S.pbuf("ps") for _ in range(1)]
        bp_v = [S.pbuf("pv") for _ in range(2)]
        bp_tr = [S.pbuf("ptr") for _ in range(2)]

        S.op("pool", lambda e: e.memset(nh[:], -0.5), writes=[bnh])
        S.op("pool", lambda e: e.memset(eps_ap[:], eps), writes=[bnh])
        make_ident(nc, S, ident, bident)
        S.op("pool", lambda e: e.memset(bones[:], 0.0), writes=[bbones])
        S.op("pool", lambda e: e.memset(bones[0:64, 0:64], 1.0), writes=[bbones])
        S.op("pool", lambda e: e.memset(bones[64:128, 64:128], 1.0), writes=[bbones])
        S.dma("sp", gB[:], gain_row.to_broadcast([128, D]), bgB, writes=[bgB])
        for hh in range(2):
            S.dma("sp", gq[hh * 64:(hh + 1) * 64, :], qn, bgq, writes=[bgq])
            S.dma("sp", gk[hh * 64:(hh + 1) * 64, :], kn, bgk, writes=[bgk])
        S.op("dve", lambda e: e.tensor_scalar(out=gq[:], in0=gq[:], scalar1=0.125, scalar2=None, op0=ALU.mult),
             reads=[bgq], writes=[bgq])
        w_v = w_in.rearrange("(c p) f -> p c f", p=128)
        for c in range(8):
            S.dma("pool", Win[:, c, :], w_v[:, c, :], bWin, writes=[bWin])

        def load(g):
            for s in range(4):
                t0 = g * G + s * 128
                S.dma("sp", xt[s][:], x_in[t0:t0 + 128, :], bxt[s], reads=[xin_b], writes=[bxt[s]])

        def prep(g):
            for s in range(4):
                k = s % 2
                S.op("dve", lambda e: e.scalar_tensor_tensor(out=junk[:], in0=xt[s][:], scalar=1.0, in1=xt[s][:],
                                                             op0=ALU.mult, op1=ALU.mult, accum_out=ss[k][:]),
                     reads=[bxt[s]], writes=[bjunk, bss[k]])
                S.op("dve", lambda e: e.tensor_scalar(out=ss[k][:], in0=ss[k][:], scalar1=1.0 / D, scalar2=eps,
                                                      op0=ALU.mult, op1=ALU.add), reads=[bss[k]], writes=[bss[k]])
                S.op("pool", lambda e: e.tensor_tensor(out=rs[k][:], in0=ss[k][:], in1=nh[:], op=ALU.pow),
                     reads=[bss[k], bnh], writes=[brs[k]])
                S.op("dve", lambda e: e.scalar_tensor_tensor(out=hb[k][:], in0=xt[s][:], scalar=rs[k][:], in1=gB[:],
                                                             op0=ALU.mult, op1=ALU.mult),
                     reads=[bxt[s], brs[k], bgB], writes=[bhb[k]])
                for c in range(8):
                    S.op("pe", lambda e: e.transpose(p_tr[k][:, c, :], hb[k][:, c * 128:(c + 1) * 128], ident[:]),
                         reads=[bhb[k], bident], writes=[bp_tr[k]], sig=(c == 7))
                S.op("act", lambda e: e.copy(out=hT[g % 2][:, :, s * 128:(s + 1) * 128], in_=p_tr[k][:]),
                     reads=[bp_tr[k]], writes=[bhT[g % 2]])

        nob = 0

        def _dummy():
            pass
        load(0)
        prep(0)
        for g in range(ng):
            h = hT[g % 2]
            bh = bhT[g % 2]
            t0 = g * G
            def fcinfo(fc):
                isq = fc < 8
                ch = fc % 8
                if ch < 2:
                    col0 = (0 if isq else 256) + ch * 128
                else:
                    col0 = (768 if isq else 1536) + (ch - 2) * 128
                return isq, ch, col0

            def main(fc):
                isq, ch, col0 = fcinfo(fc)
                pq = p_q[fc % 3]
                for c in range(8):
                    S.op("pe", lambda e: e.matmul(pq[:], lhsT=Win[:, c, col0:col0 + 128], rhs=h[:, c, :],
                                                  start=(c == 0), stop=(c == 7)),
                         reads=[bWin, bh], writes=[bp_q[fc % 3]], sig=(c == 7))

            def tail(fc):
                nonlocal nob
                isq, ch, col0 = fcinfo(fc)
                pq = p_q[fc % 3]
                bpq = bp_q[fc % 3]
                o = ob[nob % 3]
                bo = bob[nob % 3]
                nob += 1
                if ch < 2:
                    S.op("act", lambda e: e.activation(out=o[:], in_=pq[:], func=AF.Copy, scale=(0.125 if isq else 1.0)),
                         reads=[bpq], writes=[bo])
                else:
                    k = fc % 2
                    S.op("act", lambda e: e.activation(out=sq[k][:], in_=pq[:], func=AF.Square),
                         reads=[bpq], writes=[bsq[k]])
                    S.op("pe", lambda e: e.matmul(p_s[0][:], lhsT=bones[:], rhs=sq[k][:], start=True, stop=True),
                         reads=[bbones, bsq[k]], writes=[bp_s[0]])
                    S.op("act", lambda e: e.activation(out=lt[k][:], in_=p_s[0][:], func=AF.Ln, scale=1.0 / 64, bias=eps_ap[:]),
                         reads=[bp_s[0], bnh], writes=[blt[k]])
                    S.op("act", lambda e: e.activation(out=rr[k][:], in_=lt[k][:], func=AF.Exp, scale=-0.5),
                         reads=[blt[k]], writes=[brr[k]])
                    gcol = gq if isq else gk
                    S.op("dve", lambda e: e.scalar_tensor_tensor(out=o[:], in0=pq[:], scalar=gcol[:], in1=rr[k][:],
                                                                 op0=ALU.mult, op1=ALU.mult),
                         reads=[bpq, brr[k], bgq, bgk], writes=[bo])
                dst, bdst = (QT, bQT) if isq else (KT, bKT)
                S.dma("sp", dst[ch * 128:(ch + 1) * 128, t0:t0 + G], o[:], bo, reads=[bo], writes=[bdst])

            main(0)
            for fc in range(16):
                if fc + 1 < 16:
                    main(fc + 1)
                tail(fc)
                if fc == 1 and g + 1 < ng:
                    load(g + 1)
                if fc == 6 and g + 1 < ng:
                    prep(g + 1)
            for s in range(4):
                k = s % 2
                for (pv, cols, off) in ((p_v[0], (512, 768), 0), (p_v[0], (2304, 2560), 256), (p_v[1], (2560, 3072), 0)):
                    n = cols[1] - cols[0]
                    for c in range(8):
                        S.op("pe", lambda e: e.matmul(pv[:, off:off + n], lhsT=h[:, c, s * 128:(s + 1) * 128],
                                                      rhs=Win[:, c, cols[0]:cols[1]], start=(c == 0), stop=(c == 7)),
                             reads=[bWin, bh], writes=[bp_v[0], bp_v[1]], sig=(c == 7))
                S.op("act", lambda e: e.copy(out=vb[k][:, 0:512], in_=p_v[0][:]), reads=[bp_v[0]], writes=[bvb[k]])
                S.op("dve", lambda e: e.tensor_copy(out=vb[k][:, 512:1024], in_=p_v[1][:]), reads=[bp_v[1]], writes=[bvb[k]])
                S.dma("sp", V[t0 + s * 128:t0 + (s + 1) * 128, :], vb[k][:], bvb[k], reads=[bvb[k]], writes=[bV])
        S.barrier()
        for b in [bWin, bgB, bgq, bgk] + bxt + bob + bvb:
            S.release(b)


def sb_attn_phase(nc, S, QT, KT, V, MT, bQT, bKT, bV, bMT, ntok):
    nblk = ntok // 128
    with ExitStack() as es:
        sb = lambda name, shape, dt: es.enter_context(nc.sbuf_tensor(_uniq(name), shape, dt))
        ps = lambda name, shape, dt: es.enter_context(nc.psum_tensor(_uniq(name), shape, dt))
        qT = sb("qT", [128, 2, ntok], BF16)
        kT = sb("kT", [128, 2, ntok], BF16)
        v = sb("v", [128, nblk, 256], BF16)
        ones = sb("ones", [128, 512], F32)
        onec = sb("onec", [128, 1], F32)
        mneg = sb("mneg", [128, 128], BF16)
        ident = sb("ident", [128, 128], BF16)
        mk2 = lambda nm, shape, dt: [[sb("%s%d_%d" % (nm, h, i), shape, dt) for i in range(2)] for h in range(2)]
        e_ = mk2("e", [128, 512], F32)
        sp_ = mk2("sp", [128, 512], F32)
        cs_ = mk2("cs", [128, 512], F32)
        lw_ = mk2("lw", [128, 512], F32)
        w_ = mk2("w", [128, 512], BF16)
        wT_ = mk2("wT", [128, 4, 128], BF16)
        oT = [sb("oT%d" % i, [128, 512], BF16) for i in range(2)]
        p_z = [[ps("p_z%d_%d" % (h, i), [128, 512], F32) for i in range(2)] for h in range(2)]
        p_w = [ps("p_w%d" % i, [128, 4, 128], BF16) for i in range(2)]
        p_o = [ps("p_o%d" % i, [128, 128], F32) for i in range(2)]

        bq, bk, bv = S.dbuf("qT"), S.dbuf("kT"), S.dbuf("v")
        boT = [S.dbuf("oT") for _ in range(2)]
        bones, bmneg, bident = S.buf("ones"), S.buf("mneg"), S.buf("ident")
        bb2 = lambda nm: [[S.buf(nm) for _ in range(2)] for _ in range(2)]
        be, bsp, bcs, blw, bw, bwT = bb2("e"), bb2("sp"), bb2("cs"), bb2("lw"), bb2("w"), bb2("wT")
        bp_z = [[S.pbuf("pz") for _ in range(2)] for _ in range(2)]
        bp_w = [S.pbuf("pw") for _ in range(2)]
        bp_o = [S.pbuf("po") for _ in range(2)]

        S.op("pool", lambda e: e.memset(ones[:], 1.0), writes=[bones])
        S.op("pool", lambda e: e.memset(onec[:], 1.0), writes=[bones])
        make_ident(nc, S, ident, bident)
        S.op("pool", lambda e: e.memset(mneg[:], 0.0), writes=[bmneg])
        S.op("pool", lambda e: e.affine_select(out=mneg[:], in_=mneg[:], pattern=[[-1, 128]], compare_op=ALU.is_gt,
                                               fill=-30000.0, base=0, channel_multiplier=1),
             reads=[bmneg], writes=[bmneg])
        for pr in range(2):
            S.dma("sp", qT[:, pr, :], QT[pr * 128:(pr + 1) * 128, 0:ntok], bq, reads=[bQT], writes=[bq])
            S.dma("sp", kT[:, pr, :], KT[pr * 128:(pr + 1) * 128, 0:ntok], bk, reads=[bKT], writes=[bk])
        S.dma("sp", v[:], V[0:ntok, 0:256].rearrange("(b p) c -> p b c", p=128), bv, reads=[bV], writes=[bv])

        steps = []
        for pr in range(2):
            for qb in range(nblk):
                chunks = [(4 * (qb // 4), qb + 1, True)]
                for c in range(qb // 4 - 1, -1, -1):
                    chunks.append((4 * c, 4 * c + 4, False))
                for ci, (b0, b1, diag) in enumerate(chunks):
                    steps.append((pr, qb, ci, b0, b1, diag, len(chunks)))
        HH = [(0, slice(0, 64)), (1, slice(64, 128))]

        def front(t):
            pr, qb, ci, b0, b1, diag, nchk = steps[t]
            W = (b1 - b0) * 128
            i = t % 2
            for (hh, P) in HH:
                pz, bpz = p_z[hh][i], bp_z[hh][i]
                S.op("pe", lambda e: e.matmul(pz[:, 0:W], lhsT=qT[P, pr, qb * 128:(qb + 1) * 128],
                                              rhs=kT[P, pr, b0 * 128:b1 * 128], start=True, stop=(not diag)),
                     reads=[bq, bk], writes=[bpz], sig=(not diag))
                if diag:
                    S.op("pe", lambda e: e.matmul(pz[:, W - 128:W], lhsT=ident[:], rhs=mneg[:], start=False, stop=True),
                         reads=[bident, bmneg], writes=[bpz])
            for (hh, P) in HH:
                S.op("act", lambda e: e.activation(out=e_[hh][i][:, 0:W], in_=p_z[hh][i][:, 0:W], func=AF.Exp),
                     reads=[bp_z[hh][i]], writes=[be[hh][i]])
            for (hh, P) in HH:
                S.op("act", lambda e: e.activation(out=sp_[hh][i][:, 0:W], in_=e_[hh][i][:, 0:W], func=AF.Ln, bias=onec[:]),
                     reads=[be[hh][i], bones], writes=[bsp[hh][i]])

        def back(t):
            pr, qb, ci, b0, b1, diag, nchk = steps[t]
            W = (b1 - b0) * 128
            nb = b1 - b0
            i = t % 2
            rev = (lambda tt: tt[:, W - 1::-1] if W < 512 else tt[:, ::-1])
            for (hh, P) in HH:
                if ci == 0:
                    init, rd = 0.0, [bsp[hh][i], bones]
                else:
                    init, rd = cs_[hh][1 - i][:, 0:1], [bsp[hh][i], bones, bcs[hh][1 - i]]
                S.op("dve", lambda e: e.tensor_tensor_scan(out=rev(cs_[hh][i]), data0=ones[:, 0:W], data1=rev(sp_[hh][i]),
                                                           initial=init, op0=ALU.mult, op1=ALU.add),
                     reads=rd, writes=[bcs[hh][i]])
            for (hh, P) in HH:
                S.op("dve", lambda e: e.tensor_tensor(out=lw_[hh][i][:, 0:W], in0=p_z[hh][i][:, 0:W], in1=cs_[hh][i][:, 0:W], op=ALU.subtract),
                     reads=[bp_z[hh][i], bcs[hh][i]], writes=[blw[hh][i]])
            for (hh, P) in HH:
                S.op("act", lambda e: e.activation(out=w_[hh][i][:, 0:W], in_=lw_[hh][i][:, 0:W], func=AF.Exp),
                     reads=[blw[hh][i]], writes=[bw[hh][i]])
            for (hh, P) in HH:
                pw, bpw = p_w[hh], bp_w[hh]
                for b in range(nb):
                    S.op("pe", lambda e: e.transpose(pw[:, b, :], w_[hh][i][:, b * 128:(b + 1) * 128], ident[:]),
                         reads=[bw[hh][i], bident], writes=[bpw], sig=(b == nb - 1))
                if hh == 0:
                    S.op("act", lambda e: e.copy(out=wT_[hh][i][:, 0:nb, :], in_=pw[:, 0:nb, :]), reads=[bpw], writes=[bwT[hh][i]])
                else:
                    S.op("dve", lambda e: e.tensor_copy(out=wT_[hh][i][:, 0:nb, :], in_=pw[:, 0:nb, :]), reads=[bpw], writes=[bwT[hh][i]])
            po, bpo = p_o[qb % 2], bp_o[qb % 2]
            for (hh, P) in HH:
                h = 2 * pr + hh
                for b in range(nb):
                    first = (ci == 0 and b == 0)
                    last = (ci == nchk - 1 and b == nb - 1)
                    S.op("pe", lambda e: e.matmul(po[P, :], lhsT=v[:, b0 + b, h * 64:(h + 1) * 64], rhs=wT_[hh][i][:, b, :],
                                                  start=first, stop=last),
                         reads=[bv, bwT[hh][i]], writes=[bpo], sig=(b == nb - 1))
            if ci == nchk - 1:
                k = (qb // 4) % 2
                S.op("dve", lambda e: e.tensor_copy(out=oT[k][:, (qb % 4) * 128:(qb % 4 + 1) * 128], in_=po[:, :]),
                     reads=[bpo], writes=[boT[k]])
                if qb % 4 == 3 or qb == nblk - 1:
                    q0 = 4 * (qb // 4)
                    n = (qb - q0 + 1) * 128
                    S.dma("sp", MT[pr * 128:(pr + 1) * 128, q0 * 128:q0 * 128 + n], oT[k][:, 0:n], boT[k],
                          reads=[boT[k]], writes=[bMT])

        front(0)
        for t in range(len(steps)):
            if t + 1 < len(steps):
                front(t + 1)
            back(t)
        S.barrier()
        for b_ in [bq, bk, bv] + boT:
            S.release(b_)


def dil_bias_host(rel_bias):
    out = np.empty((12, 128, 2, 128), np.float32)
    kj = np.arange(128)[:, None]
    q = np.arange(128)[None, :]
    for g, r in enumerate((1, 4, 16)):
        for part, dist in ((1, q - kj), (0, q + 128 - kj)):
            valid = (dist >= 0) & (dist <= 128)
            dd = np.maximum(dist, 0) * r
            d = np.maximum(dd, 1).astype(np.float32)
            large = 16 + (np.log(d / np.float32(16)) / np.float32(np.log(2048 / 16)) * np.float32(16)).astype(np.int32)
            large = np.minimum(large, 31)
            bucket = np.where(dd < 16, dd, large)
            for j in range(4):
                hd = 4 * g + j
                out[hd, :, part, :] = np.where(valid, rel_bias[bucket, hd], np.float32(-30000.0))
    return out


def dil_attn_phase(nc, S, QT, KT, V, MT, dbias, bQT, bKT, bV, bMT, ntok):
    with ExitStack() as es:
        sb = lambda name, shape, dt: es.enter_context(nc.sbuf_tensor(_uniq(name), shape, dt))
        ps = lambda name, shape, dt: es.enter_context(nc.psum_tensor(_uniq(name), shape, dt))
        qT = sb("qT", [128, 2, ntok], BF16)
        kT = sb("kT", [128, 2, ntok], BF16)
        v = sb("v", [128, ntok // 128, 256], BF16)
        bias = sb("bias", [128, 4, 256], F32)
        onesb = sb("onesb", [128, 64], BF16)
        Nacc = sb("Nacc", [128, 2, ntok], F32)
        Dacc = sb("Dacc", [128, 2, ntok], F32)
        s_ = [sb("s%d" % i, [128, 256], F32) for i in range(3)]
        pT_ = [sb("pT%d" % i, [128, 2, 128], BF16) for i in range(3)]
        ob = [sb("ob%d" % i, [128, 1024], BF16) for i in range(2)]
        p_s = [ps("p_s%d" % i, [128, 2, 128], F32) for i in range(2)]
        p_n = [ps("p_n%d" % i, [128, 128], F32) for i in range(2)]
        p_d = [ps("p_d%d" % i, [128, 128], F32) for i in range(2)]
        p_pad = [ps("p_pad%d" % i, [128, 256], F32) for i in range(0)]

        bq, bk, bv, bbias = S.dbuf("qT"), S.dbuf("kT"), S.dbuf("v"), S.dbuf("bias")
        bob = [S.dbuf("ob") for _ in range(2)]
        bones, bN, bD = S.buf("ones"), S.buf("N"), S.buf("D")
        bs = [S.buf("s") for _ in range(3)]
        bpT = [S.buf("pT") for _ in range(3)]
        bp_s = [S.pbuf("ps") for _ in range(2)]
        bp_n = [S.pbuf("pn") for _ in range(2)]
        bp_d = [S.pbuf("pd") for _ in range(2)]

        S.op("pool", lambda e: e.memset(onesb[:], 1.0), writes=[bones])
        u = 0
        for g, r in enumerate((1, 4, 16)):
            L = ntok // r
            nb = L // 128
            for pr in range(2):
                r0 = 256 + (2 * g + pr) * 128
                S.dma("sp", qT[:, pr, :], QT[r0:r0 + 128, 0:ntok], bq, reads=[bQT], writes=[bq])
                S.dma("sp", kT[:, pr, :], KT[r0:r0 + 128, 0:ntok], bk, reads=[bKT], writes=[bk])
            vsrc = V[0:ntok, 256 + g * 256:256 + (g + 1) * 256].rearrange("(n i c) f -> c i n f", i=128, c=r)
            for c in range(r):
                S.dma("sp", v[:, c * nb:(c + 1) * nb, :], vsrc[c], bv, reads=[bV], writes=[bv])
            for j in range(4):
                S.dma("sp", bias[:, j, :], dbias[4 * g + j].rearrange("k a q -> k (a q)"), bbias, writes=[bbias])
            units = []
            for j in range(4):
                for c in range(r):
                    for n in range(nb):
                        units.append((j, c, n))

            def tokf(c, nn):
                st = c + r * 128 * nn
                return slice(st, st + r * 127 + 1, r)

            def front(uu, u):
                j, c, n = units[uu]
                pr, hh = j // 2, j % 2
                P = slice(64 * hh, 64 * hh + 64)
                i = u % 3
                pss, bpss = p_s[u % 2], bp_s[u % 2]
                a0 = 0 if n > 0 else 1
                if n > 0:
                    S.op("pe", lambda e: e.matmul(pss[:, 0, :], lhsT=kT[P, pr, tokf(c, n - 1)], rhs=qT[P, pr, tokf(c, n)],
                                                  start=True, stop=True), reads=[bq, bk], writes=[bpss], sig=False)
                S.op("pe", lambda e: e.matmul(pss[:, 1, :], lhsT=kT[P, pr, tokf(c, n)], rhs=qT[P, pr, tokf(c, n)],
                                              start=True, stop=True), reads=[bq, bk], writes=[bpss])
                S.op("dve", lambda e: e.tensor_tensor(out=s_[i][:, a0 * 128:256], in0=pss[:, a0:2, :],
                                                      in1=bias[:, j, a0 * 128:256], op=ALU.add),
                     reads=[bpss, bbias], writes=[bs[i]])
                S.op("act", lambda e: e.activation(out=pT_[i][:, a0:2, :], in_=s_[i][:, a0 * 128:256], func=AF.Exp),
                     reads=[bs[i]], writes=[bpT[i]])

            def back(uu, u):
                j, c, n = units[uu]
                pr, hh = j // 2, j % 2
                P = slice(64 * hh, 64 * hh + 64)
                i = u % 3
                pn, bpn = p_n[u % 2], bp_n[u % 2]
                pd, bpd = p_d[u % 2], bp_d[u % 2]
                a0 = 0 if n > 0 else 1
                for a_ in range(a0, 2):
                    S.op("pe", lambda e: e.matmul(pn[P, :], lhsT=v[:, c * nb + n - 1 + a_, j * 64:(j + 1) * 64],
                                                  rhs=pT_[i][:, a_, :], start=(a_ == a0), stop=(a_ == 1)),
                         reads=[bv, bpT[i]], writes=[bpn], sig=(a_ == 1))
                for a_ in range(a0, 2):
                    S.op("pe", lambda e: e.matmul(pd[P, :], lhsT=onesb[:, :], rhs=pT_[i][:, a_, :],
                                                  start=(a_ == a0), stop=(a_ == 1)),
                         reads=[bones, bpT[i]], writes=[bpd], sig=(a_ == 1))
                tk = tokf(c, n)
                if g == 0:
                    S.op("act", lambda e: e.copy(out=Nacc[P, pr, tk], in_=pn[P, :]), reads=[bpn], writes=[bN])
                    S.op("dve", lambda e: e.tensor_copy(out=Dacc[P, pr, tk], in_=pd[P, :]), reads=[bpd], writes=[bD])
                else:
                    S.op("dve", lambda e: e.tensor_tensor(out=Nacc[P, pr, tk], in0=pn[P, :], in1=Nacc[P, pr, tk],
                                                          op=ALU.add), reads=[bpn, bN], writes=[bN])
                    S.op("dve", lambda e: e.tensor_tensor(out=Dacc[P, pr, tk], in0=pd[P, :], in1=Dacc[P, pr, tk],
                                                          op=ALU.add), reads=[bpd, bD], writes=[bD])

            front(0, u)
            for uu in range(len(units)):
                if uu + 1 < len(units):
                    front(uu + 1, u + 1)
                back(uu, u)
                u += 1
        k = 0
        for pr in range(2):
            for c0 in range(0, ntok, 1024):
                n = min(1024, ntok - c0)
                S.op("dve", lambda e: e.reciprocal(out=Dacc[:, pr, c0:c0 + n], in_=Dacc[:, pr, c0:c0 + n]), reads=[bD], writes=[bD])
                S.op("dve", lambda e: e.tensor_tensor(out=ob[k % 2][:, 0:n], in0=Nacc[:, pr, c0:c0 + n], in1=Dacc[:, pr, c0:c0 + n],
                                                      op=ALU.mult), reads=[bN, bD], writes=[bob[k % 2]])
                S.dma("sp", MT[256 + pr * 128:256 + (pr + 1) * 128, c0:c0 + n], ob[k % 2][:, 0:n], bob[k % 2],
                      reads=[bob[k % 2]], writes=[bMT])
                k += 1
        S.barrier()
        for b in [bq, bk, bv, bbias] + bob:
            S.release(b)


def out_proj_phase(nc, S, x_in, x_out, xin_b, xout_b, MT, bMT, w_out, kdim, ntok):
    G = 512
    ng = ntok // G
    kc = kdim // 128
    with ExitStack() as es:
        sb = lambda name, shape, dt: es.enter_context(nc.sbuf_tensor(_uniq(name), shape, dt))
        ps = lambda name, shape, dt: es.enter_context(nc.psum_tensor(_uniq(name), shape, dt))
        Wo = sb("Wo", [128, kc, D], BF16)
        mT = [sb("mT%d" % i, [128, kc, G], BF16) for i in range(2)]
        xt = [sb("xt%d" % i, [128, D], F32) for i in range(3)]
        ot = [sb("ot%d" % i, [128, D], F32) for i in range(2)]
        p_y = [ps("p_y%d" % i, [128, 512], F32) for i in range(4)]
        bWo = S.dbuf("Wo")
        bmT = [S.dbuf("mT") for _ in range(2)]
        bxt = [S.dbuf("xt") for _ in range(3)]
        bot = [S.dbuf("ot") for _ in range(2)]
        bp_y = [S.pbuf("py") for _ in range(4)]
        w_v = w_out.rearrange("(c p) f -> p c f", p=128)
        for c in range(kc):
            S.dma("pool", Wo[:, c, :], w_v[:, c, :], bWo, writes=[bWo])
        k = 0
        for g in range(ng):
            m, bm = mT[g % 2], bmT[g % 2]
            for c in range(kc):
                S.dma("sp", m[:, c, :], MT[c * 128:(c + 1) * 128, g * G:(g + 1) * G], bm, reads=[bMT], writes=[bm])
            for s in range(4):
                t0 = g * G + s * 128
                x_, bx = xt[k % 3], bxt[k % 3]
                o_, bo = ot[k % 2], bot[k % 2]
                S.dma("sp", x_[:], x_in[t0:t0 + 128, :], bx, reads=[xin_b], writes=[bx])
                for hh in range(2):
                    py, bpy = p_y[(2 * k + hh) % 4], bp_y[(2 * k + hh) % 4]
                    for c in range(kc):
                        S.op("pe", lambda e: e.matmul(py[:], lhsT=m[:, c, s * 128:(s + 1) * 128], rhs=Wo[:, c, hh * 512:(hh + 1) * 512],
                                                      start=(c == 0), stop=(c == kc - 1)),
                             reads=[bm, bWo], writes=[bpy], sig=(c == kc - 1))
                    S.op("dve", lambda e: e.tensor_tensor(out=o_[:, hh * 512:(hh + 1) * 512], in0=py[:], in1=x_[:, hh * 512:(hh + 1) * 512],
                                                          op=ALU.add), reads=[bpy, bx], writes=[bo])
                S.dma("sp", x_out[t0:t0 + 128, :], o_[:], bo, reads=[bo], writes=[xout_b])
                k += 1
        S.barrier()
        for b in [bWo] + bmT + bxt + bot:
            S.release(b)


C0 = float(np.exp(-0.5))


class RwPrep:
    def __init__(self, nc, S, es, G, mix_ids, x_in, xin_b, gain_row, mix, eps=1e-6):
        sb = lambda name, shape, dt: es.enter_context(nc.sbuf_tensor(_uniq(name), shape, dt))
        ps = lambda name, shape, dt: es.enter_context(nc.psum_tensor(_uniq(name), shape, dt))
        self.nc, self.S, self.G, self.mix_ids, self.x_in, self.xin_b, self.eps = nc, S, G, mix_ids, x_in, xin_b, eps
        self.gB = sb("gB", [128, D], F32)
        self.identf = sb("identf", [128, 128], F32)
        self.nh = sb("nh", [128, 1], F32)
        self.mixc = sb("mixc", [128, 6, 8], F32)
        self.xt = [sb("xt%d" % i, [128, D], F32) for i in range(2)]
        self.hn = [sb("hn%d" % i, [128, D], F32) for i in range(2)]
        self.junk = sb("junk", [128, D], BF16)
        self.ss = [sb("ss%d" % i, [128, 1], F32) for i in range(2)]
        self.rs = [sb("rs%d" % i, [128, 1], F32) for i in range(2)]
        self.hT = [sb("hT%d" % i, [128, 8, G + 1], F32) for i in range(2)]
        self.xx = [sb("xx%d" % i, [128, G], F32) for i in range(2)]
        self.xm = {i: sb("xm%d" % i, [128, 8, G], BF16) for i in mix_ids}
        self.p_tr = [ps("p_tr%d" % i, [128, 4, 128], F32) for i in range(2)]
        self.bgB, self.bmixc = S.dbuf("gB"), S.dbuf("mixc")
        self.bxt = [S.dbuf("xt") for _ in range(2)]
        self.bhn = [S.buf("hn") for _ in range(2)]
        self.bident, self.bnh, self.bjunk = S.buf("identf"), S.buf("nh"), S.buf("junk")
        self.bss = [S.buf("ss") for _ in range(2)]
        self.brs = [S.buf("rs") for _ in range(2)]
        self.bhT = [S.buf("hT") for _ in range(2)]
        self.bxx = [S.buf("xx") for _ in range(2)]
        self.bxm = {i: S.buf("xm") for i in mix_ids}
        self.bp_tr = [S.pbuf("ptr") for _ in range(2)]
        S.op("pool", lambda e: e.memset(self.nh[:], -0.5), writes=[self.bnh])
        make_ident(nc, S, self.identf, self.bident)
        S.dma("sp", self.gB[:], gain_row.to_broadcast([128, D]), self.bgB, writes=[self.bgB])
        for i in range(6):
            S.dma("sp", self.mixc[:, i, :], mix[i:i + 1, :].rearrange("o (c p) -> p (o c)", p=128), self.bmixc, writes=[self.bmixc], slow=True)
        S.op("dve", lambda e: e.memset(self.hT[1][:, :, G:G + 1], 0.0), writes=[self.bhT[1]])
        self.dsems = [self.bgB, self.bmixc] + self.bxt

    def group(self, g):
        nc, S, G = self.nc, self.S, self.G
        hT, bhT = self.hT[g % 2], self.bhT[g % 2]
        hTp, bhTp = self.hT[(g + 1) % 2], self.bhT[(g + 1) % 2]
        S.op("pool", lambda e: e.tensor_copy(out=hT[:, :, 0:1], in_=hTp[:, :, G:G + 1]), reads=[bhTp], writes=[bhT])
        k = 0
        for s in range(G // 128):
            t0 = g * G + s * 128
            x_, bx = self.xt[s % 2], self.bxt[s % 2]
            h_, bh = self.hn[s % 2], self.bhn[s % 2]
            ss, bss, rs, brs = self.ss[s % 2], self.bss[s % 2], self.rs[s % 2], self.brs[s % 2]
            S.dma("sp", x_[:], self.x_in[t0:t0 + 128, :], bx, reads=[self.xin_b], writes=[bx])
            S.op("dve", lambda e: e.scalar_tensor_tensor(out=self.junk[:], in0=x_[:], scalar=1.0, in1=x_[:], op0=ALU.mult, op1=ALU.mult,
                                                         accum_out=ss[:]), reads=[bx], writes=[self.bjunk, bss])
            S.op("dve", lambda e: e.tensor_scalar(out=ss[:], in0=ss[:], scalar1=1.0 / D, scalar2=self.eps, op0=ALU.mult, op1=ALU.add),
                 reads=[bss], writes=[bss])
            S.op("pool", lambda e: e.tensor_tensor(out=rs[:], in0=ss[:], in1=self.nh[:], op=ALU.pow), reads=[bss, self.bnh], writes=[brs])
            S.op("dve", lambda e: e.scalar_tensor_tensor(out=h_[:], in0=x_[:], scalar=rs[:], in1=self.gB[:], op0=ALU.mult, op1=ALU.mult),
                 reads=[bx, brs, self.bgB], writes=[bh])
            for half in range(2):
                pt, bpt = self.p_tr[k % 2], self.bp_tr[k % 2]
                k += 1
                for c4 in range(4):
                    c = half * 4 + c4
                    S.op("pe", lambda e: e.transpose(pt[:, c4, :], h_[:, c * 128:(c + 1) * 128], self.identf[:]),
                         reads=[bh, self.bident], writes=[bpt], sig=(c4 == 3))
                S.op("act", lambda e: e.copy(out=hT[:, half * 4:half * 4 + 4, 1 + s * 128:1 + (s + 1) * 128], in_=pt[:]),
                     reads=[bpt], writes=[bhT])
        for c in range(8):
            xx, bxx = self.xx[c % 2], self.bxx[c % 2]
            S.op("dve", lambda e: e.tensor_tensor(out=xx[:], in0=hT[:, c, 0:G], in1=hT[:, c, 1:G + 1], op=ALU.subtract),
                 reads=[bhT], writes=[bxx])
            for n, i in enumerate(self.mix_ids):
                S.op("dve", lambda e: e.scalar_tensor_tensor(out=self.xm[i][:, c, :], in0=xx[:], scalar=self.mixc[:, i, c:c + 1],
                                                             in1=hT[:, c, 1:G + 1], op0=ALU.mult, op1=ALU.add),
                     reads=[bxx, bhT, self.bmixc], writes=[self.bxm[i]])

    def release(self):
        for b in self.dsems:
            self.S.release(b)


def col_load(S, dst, src_row, track):
    S.dma("sp", dst, src_row.rearrange("o (c p) -> p (o c)", p=128), track, writes=[track], slow=True)


def rwkv_fm_phase(nc, S, x_in, xin_b, gain_row, mix, w0, w1, w2, a0, a1, a2, kk_, ka_, w_r, w_k,
                  RtT, KtT, AtT, BtT, WC, bouts, ntok):
    G = 512
    ng = ntok // G
    with ExitStack() as es:
        sb = lambda name, shape, dt: es.enter_context(nc.sbuf_tensor(_uniq(name), shape, dt))
        ps = lambda name, shape, dt: es.enter_context(nc.psum_tensor(_uniq(name), shape, dt))
        P = RwPrep(nc, S, es, G, (0, 1, 2, 4), x_in, xin_b, gain_row, mix)
        Wr = sb("Wr", [128, 8, D], BF16)
        Wk = sb("Wk", [128, 8, D], BF16)
        W1 = sb("W1", [128, 8, 64], BF16)
        A1 = sb("A1", [128, 8, 64], BF16)
        W2 = sb("W2", [64, D], BF16)
        A2 = sb("A2", [64, D], BF16)
        cols = sb("cols", [128, 4, 8], F32)
        bones = sb("bones", [128, 128], BF16)
        rmask = sb("rmask", [128, G], F32)
        tiny = sb("tiny", [128, 1], F32)
        tw = sb("tw", [64, G], BF16)
        ta = sb("ta", [64, G], BF16)
        names = ["sgu", "av", "kk0", "lnk", "rk", "kkn", "t1", "kp", "csg", "cse", "eW", "eWi", "eWe", "t2"]
        T = {n: sb(n, [128, G], F32) for n in names}
        sqk = sb("sqk", [128, G], BF16)
        wc = [sb("wc%d" % i, [128, G // 64], F32) for i in range(2)]
        ob = [sb("ob%d" % i, [128, G], BF16) for i in range(8)]
        p_r = ps("p_r", [128, G], F32)
        p_k = ps("p_k", [128, G], F32)
        p_u = ps("p_u", [128, G], F32)
        p_a = ps("p_a", [128, G], F32)
        p_ss = ps("p_ss", [128, G], F32)
        p_t = ps("p_t", [128, G], F32)
        bW = S.dbuf("W")
        bcols = S.dbuf("cols")
        bwc = [S.dbuf("wc") for _ in range(2)]
        bob = [S.dbuf("ob") for _ in range(8)]
        B = {n: S.buf(n) for n in names + ["sqk", "tw", "ta", "bones", "rmask"]}
        B.update({n: S.pbuf(n) for n in ["p_r", "p_k", "p_u", "p_a", "p_ss", "p_t"]})
        for c in range(8):
            S.dma("pool", Wr[:, c, :], w_r.rearrange("(c p) f -> p c f", p=128)[:, c, :], bW, writes=[bW])
            S.dma("pool", Wk[:, c, :], w_k.rearrange("(c p) f -> p c f", p=128)[:, c, :], bW, writes=[bW])
        S.dma("pool", W1[:], w1.rearrange("(c p) f -> p c f", p=128), bW, writes=[bW])
        S.dma("pool", A1[:], a1.rearrange("(c p) f -> p c f", p=128), bW, writes=[bW])
        S.dma("pool", W2[:], w2, bW, writes=[bW])
        S.dma("pool", A2[:], a2, bW, writes=[bW])
        for n, src in enumerate((w0, a0, kk_, ka_)):
            col_load(S, cols[:, n, :], src, bcols)
        S.op("pool", lambda e: e.memset(bones[:], 0.0), writes=[B["bones"]])
        S.op("pool", lambda e: e.memset(bones[0:64, 0:64], 1.0), writes=[B["bones"]])
        S.op("pool", lambda e: e.memset(bones[64:128, 64:128], 1.0), writes=[B["bones"]])
        S.op("pool", lambda e: e.memset(rmask[:], 1.0), writes=[B["rmask"]])
        S.op("pool", lambda e: e.memset(rmask[:, 0:G:64], 0.0), writes=[B["rmask"]])
        S.op("pool", lambda e: e.memset(tiny[:], 1e-18), writes=[B["rmask"]])
        RB, KB, AB, BB, WCB = bouts
        no = 0
        for g in range(ng):
            P.group(g)
            t0 = g * G
            xr, xw, xk, xa = P.xm[0], P.xm[1], P.xm[2], P.xm[4]
            bxr, bxw, bxk, bxa = P.bxm[0], P.bxm[1], P.bxm[2], P.bxm[4]
            for c in range(8):
                S.op("pe", lambda e: e.matmul(p_t[0:64, :], lhsT=W1[:, c, :], rhs=xw[:, c, :], start=(c == 0), stop=(c == 7)),
                     reads=[bW, bxw], writes=[B["p_t"]], sig=(c == 7))
            S.op("act", lambda e: e.activation(out=tw[:], in_=p_t[0:64, :], func=AF.Tanh), reads=[B["p_t"]], writes=[B["tw"]])
            for c in range(8):
                S.op("pe", lambda e: e.matmul(p_t[0:64, :], lhsT=A1[:, c, :], rhs=xa[:, c, :], start=(c == 0), stop=(c == 7)),
                     reads=[bW, bxa], writes=[B["p_t"]], sig=(c == 7))
            S.op("act", lambda e: e.copy(out=ta[:], in_=p_t[0:64, :]), reads=[B["p_t"]], writes=[B["ta"]])
            for cc in range(8):
                fs = slice(cc * 128, (cc + 1) * 128)
                for c in range(8):
                    S.op("pe", lambda e: e.matmul(p_r[:], lhsT=Wr[:, c, fs], rhs=xr[:, c, :], start=(c == 0), stop=(c == 7)),
                         reads=[bW, bxr], writes=[B["p_r"]], sig=(c == 7))
                for c in range(8):
                    S.op("pe", lambda e: e.matmul(p_k[:], lhsT=Wk[:, c, fs], rhs=xk[:, c, :], start=(c == 0), stop=(c == 7)),
                         reads=[bW, bxk], writes=[B["p_k"]], sig=(c == 7))
                S.op("pe", lambda e: e.matmul(p_u[:], lhsT=W2[:, fs], rhs=tw[:], start=True, stop=True), reads=[bW, B["tw"]], writes=[B["p_u"]])
                S.op("pe", lambda e: e.matmul(p_a[:], lhsT=A2[:, fs], rhs=ta[:], start=True, stop=True), reads=[bW, B["ta"]], writes=[B["p_a"]])
                S.op("act", lambda e: e.activation(out=T["sgu"][:], in_=p_u[:], func=AF.Sigmoid, bias=cols[:, 0, cc:cc + 1]),
                     reads=[B["p_u"], bcols], writes=[B["sgu"]])
                S.op("act", lambda e: e.activation(out=T["av"][:], in_=p_a[:], func=AF.Sigmoid, bias=cols[:, 1, cc:cc + 1]),
                     reads=[B["p_a"], bcols], writes=[B["av"]])
                S.op("dve", lambda e: e.tensor_scalar(out=T["kk0"][:], in0=p_k[:], scalar1=cols[:, 2, cc:cc + 1], scalar2=None, op0=ALU.mult),
                     reads=[B["p_k"], bcols], writes=[B["kk0"]])
                S.op("act", lambda e: e.activation(out=sqk[:], in_=T["kk0"][:], func=AF.Square), reads=[B["kk0"]], writes=[B["sqk"]])
                S.op("pe", lambda e: e.matmul(p_ss[:], lhsT=bones[:], rhs=sqk[:], start=True, stop=True), reads=[B["bones"], B["sqk"]],
                     writes=[B["p_ss"]])
                S.op("act", lambda e: e.activation(out=T["lnk"][:], in_=p_ss[:], func=AF.Ln, bias=tiny[:]), reads=[B["p_ss"], B["rmask"]],
                     writes=[B["lnk"]])
                S.op("act", lambda e: e.activation(out=T["rk"][:], in_=T["lnk"][:], func=AF.Exp, scale=-0.5), reads=[B["lnk"]], writes=[B["rk"]])
                S.op("dve", lambda e: e.tensor_tensor(out=T["kkn"][:], in0=T["kk0"][:], in1=T["rk"][:], op=ALU.mult),
                     reads=[B["kk0"], B["rk"]], writes=[B["kkn"]])
                S.op("dve", lambda e: e.tensor_scalar(out=T["t1"][:], in0=T["av"][:], scalar1=-1.0, scalar2=cols[:, 3, cc:cc + 1],
                                                      op0=ALU.add, op1=ALU.mult), reads=[B["av"], bcols], writes=[B["t1"]])
                S.op("dve", lambda e: e.scalar_tensor_tensor(out=T["kp"][:], in0=T["t1"][:], scalar=1.0, in1=p_k[:], op0=ALU.add, op1=ALU.mult),
                     reads=[B["t1"], B["p_k"]], writes=[B["kp"]])
                S.op("dve", lambda e: e.tensor_tensor_scan(out=T["csg"][:], data0=rmask[:], data1=T["sgu"][:], initial=0.0,
                                                           op0=ALU.mult, op1=ALU.add), reads=[B["rmask"], B["sgu"]], writes=[B["csg"]])
                S.op("dve", lambda e: e.tensor_tensor(out=T["cse"][:], in0=T["csg"][:], in1=T["sgu"][:], op=ALU.subtract),
                     reads=[B["csg"], B["sgu"]], writes=[B["cse"]])
                S.op("act", lambda e: e.activation(out=T["eW"][:], in_=T["csg"][:], func=AF.Exp, scale=-C0), reads=[B["csg"]], writes=[B["eW"]])
                S.op("act", lambda e: e.activation(out=T["eWi"][:], in_=T["csg"][:], func=AF.Exp, scale=C0), reads=[B["csg"]], writes=[B["eWi"]])
                S.op("act", lambda e: e.activation(out=T["eWe"][:], in_=T["cse"][:], func=AF.Exp, scale=-C0), reads=[B["cse"]], writes=[B["eWe"]])
                S.op("dve", lambda e: e.tensor_tensor(out=T["t2"][:], in0=T["kkn"][:], in1=T["av"][:], op=ALU.mult),
                     reads=[B["kkn"], B["av"]], writes=[B["t2"]])
                outs = []
                o, bo = ob[no % 8], bob[no % 8]; no += 1
                S.op("dve", lambda e: e.tensor_tensor(out=o[:], in0=p_r[:], in1=T["eW"][:], op=ALU.mult), reads=[B["p_r"], B["eW"]], writes=[bo])
                outs.append((o, bo, RtT, RB))
                o, bo = ob[no % 8], bob[no % 8]; no += 1
                S.op("dve", lambda e: e.tensor_tensor(out=o[:], in0=T["kp"][:], in1=T["eWi"][:], op=ALU.mult), reads=[B["kp"], B["eWi"]], writes=[bo])
                outs.append((o, bo, KtT, KB))
                o, bo = ob[no % 8], bob[no % 8]; no += 1
                S.op("dve", lambda e: e.scalar_tensor_tensor(out=o[:], in0=T["kkn"][:], scalar=-1.0, in1=T["eWe"][:], op0=ALU.mult, op1=ALU.mult),
                     reads=[B["kkn"], B["eWe"]], writes=[bo])
                outs.append((o, bo, AtT, AB))
                o, bo = ob[no % 8], bob[no % 8]; no += 1
                S.op("dve", lambda e: e.tensor_tensor(out=o[:], in0=T["t2"][:], in1=T["eWi"][:], op=ALU.mult), reads=[B["t2"], B["eWi"]], writes=[bo])
                outs.append((o, bo, BtT, BB))
                for (o, bo, dst, bdst) in outs:
                    S.dma("sp", dst[fs, t0:t0 + G], o[:], bo, reads=[bo], writes=[bdst])
                w_, bw_ = wc[cc % 2], bwc[cc % 2]
                S.op("dve", lambda e: e.tensor_copy(out=w_[:], in_=T["eW"][:, 63:G:64]), reads=[B["eW"]], writes=[bw_])
                S.dma("sp", WC[fs, g * (G // 64):(g + 1) * (G // 64)], w_[:], bw_, reads=[bw_], writes=[WCB])
        S.barrier()
        P.release()
        for b in [bW, bcols] + bwc + bob:
            S.release(b)


def rwkv_tm_phase(nc, S, x_in, xin_b, gain_row, mix, a0, a1, a2, g1, g2, ka_, rk_, w_r, w_k, w_v,
                  Vtok, BV, Gt, bouts, ntok):
    G = 256
    ng = ntok // G
    with ExitStack() as es:
        sb = lambda name, shape, dt: es.enter_context(nc.sbuf_tensor(_uniq(name), shape, dt))
        ps = lambda name, shape, dt: es.enter_context(nc.psum_tensor(_uniq(name), shape, dt))
        P = RwPrep(nc, S, es, G, (0, 2, 3, 4, 5), x_in, xin_b, gain_row, mix)
        Wr = sb("Wr", [128, 8, D], BF16)
        Wk = sb("Wk", [128, 8, D], BF16)
        Wv = sb("Wv", [128, 8, D], BF16)
        A1 = sb("A1", [128, 8, 64], BF16)
        A2 = sb("A2", [64, D], BF16)
        G1 = sb("G1", [128, 8, 160], BF16)
        G2a = sb("G2a", [128, D], BF16)
        G2b = sb("G2b", [32, D], BF16)
        a0B = sb("a0B", [128, D], F32)
        kaB = sb("kaB", [128, D], F32)
        rkB = sb("rkB", [128, D], F32)
        ta = sb("ta", [64, G], BF16)
        sg1a = sb("sg1a", [128, G], BF16)
        sg1b = sb("sg1b", [32, G], BF16)
        tmp = sb("tmp", [128, D], F32)
        av = sb("av", [128, D], F32)
        t1 = sb("t1", [128, D], F32)
        kp = sb("kp", [128, D], F32)
        tmp2 = sb("tmp2", [128, D], F32)
        tmp3 = sb("tmp3", [128, 16, 64], F32)
        bsum = sb("bsum", [128, 16, 1], F32)
        bvo = [sb("bvo%d" % i, [128, 16, 64], F32) for i in range(2)]
        vto = [sb("vto%d" % i, [128, D], BF16) for i in range(2)]
        gto = [sb("gto%d" % i, [128, D], F32) for i in range(2)]
        p_t = ps("p_t", [128, G], F32)
        pp = [ps("pp%d" % i, [128, 2, 512], F32) for i in range(2)]
        bW, bB = S.dbuf("W"), S.dbuf("B")
        bbvo = [S.dbuf("bvo") for _ in range(2)]
        bvto = [S.dbuf("vto") for _ in range(2)]
        bgto = [S.dbuf("gto") for _ in range(2)]
        B = {n: S.buf(n) for n in ["ta", "sg1a", "sg1b", "tmp", "av", "t1", "kp", "tmp2", "tmp3", "bsum"]}
        B.update({n: S.pbuf(n) for n in ["p_t", "pp0", "pp1"]})
        bpp = [B["pp0"], B["pp1"]]
        for c in range(8):
            for (W_, w_) in ((Wr, w_r), (Wk, w_k), (Wv, w_v)):
                S.dma("pool", W_[:, c, :], w_.rearrange("(c p) f -> p c f", p=128)[:, c, :], bW, writes=[bW])
        S.dma("pool", A1[:], a1.rearrange("(c p) f -> p c f", p=128), bW, writes=[bW])
        S.dma("pool", G1[:], g1.rearrange("(c p) f -> p c f", p=128), bW, writes=[bW])
        S.dma("pool", A2[:], a2, bW, writes=[bW])
        S.dma("pool", G2a[:], g2[0:128, :], bW, writes=[bW])
        S.dma("pool", G2b[:], g2[128:160, :], bW, writes=[bW])
        for (t_, src) in ((a0B, a0), (kaB, ka_), (rkB, rk_)):
            S.dma("sp", t_[:], src.to_broadcast([128, D]), bB, writes=[bB])
        VB, BVB, GB = bouts
        npp = 0
        k = 0
        for g in range(ng):
            P.group(g)
            xr, xk, xv, xa, xg = P.xm[0], P.xm[2], P.xm[3], P.xm[4], P.xm[5]
            bxr, bxk, bxv, bxa, bxg = P.bxm[0], P.bxm[2], P.bxm[3], P.bxm[4], P.bxm[5]
            for c in range(8):
                S.op("pe", lambda e: e.matmul(p_t[0:64, :], lhsT=A1[:, c, :], rhs=xa[:, c, :], start=(c == 0), stop=(c == 7)),
                     reads=[bW, bxa], writes=[B["p_t"]], sig=(c == 7))
            S.op("act", lambda e: e.copy(out=ta[:], in_=p_t[0:64, :]), reads=[B["p_t"]], writes=[B["ta"]])
            for c in range(8):
                S.op("pe", lambda e: e.matmul(p_t[:, :], lhsT=G1[:, c, 0:128], rhs=xg[:, c, :], start=(c == 0), stop=(c == 7)),
                     reads=[bW, bxg], writes=[B["p_t"]], sig=(c == 7))
            S.op("act", lambda e: e.activation(out=sg1a[:], in_=p_t[:, :], func=AF.Sigmoid), reads=[B["p_t"]], writes=[B["sg1a"]])
            for c in range(8):
                S.op("pe", lambda e: e.matmul(p_t[0:32, :], lhsT=G1[:, c, 128:160], rhs=xg[:, c, :], start=(c == 0), stop=(c == 7)),
                     reads=[bW, bxg], writes=[B["p_t"]], sig=(c == 7))
            S.op("act", lambda e: e.activation(out=sg1b[:], in_=p_t[0:32, :], func=AF.Sigmoid), reads=[B["p_t"]], writes=[B["sg1b"]])
            for s in range(G // 128):
                ts = slice(s * 128, (s + 1) * 128)
                t0 = g * G + s * 128

                def big(xm_, bxm_, W_):
                    nonlocal npp
                    p, bp = pp[npp % 2], bpp[npp % 2]
                    npp += 1
                    for hh in range(2):
                        for c in range(8):
                            S.op("pe", lambda e: e.matmul(p[:, hh, :], lhsT=xm_[:, c, ts], rhs=W_[:, c, hh * 512:(hh + 1) * 512],
                                                          start=(c == 0), stop=(c == 7)), reads=[bW, bxm_], writes=[bp], sig=(c == 7))
                    return p, bp
                p, bp = pp[npp % 2], bpp[npp % 2]
                npp += 1
                for hh in range(2):
                    S.op("pe", lambda e: e.matmul(p[:, hh, :], lhsT=ta[:, ts], rhs=A2[:, hh * 512:(hh + 1) * 512], start=True, stop=True),
                         reads=[bW, B["ta"]], writes=[bp])
                S.op("dve", lambda e: e.tensor_tensor(out=tmp[:], in0=p[:].rearrange("p a b -> p (a b)"), in1=a0B[:], op=ALU.add),
                     reads=[bp, bB], writes=[B["tmp"]])
                S.op("act", lambda e: e.activation(out=av[:], in_=tmp[:], func=AF.Sigmoid), reads=[B["tmp"]], writes=[B["av"]])
                S.op("dve", lambda e: e.scalar_tensor_tensor(out=t1[:], in0=av[:], scalar=-1.0, in1=kaB[:], op0=ALU.add, op1=ALU.mult),
                     reads=[B["av"], bB], writes=[B["t1"]])
                p, bp = big(xk, bxk, Wk)
                S.op("dve", lambda e: e.scalar_tensor_tensor(out=kp[:], in0=t1[:], scalar=1.0, in1=p[:].rearrange("p a b -> p (a b)"),
                                                             op0=ALU.add, op1=ALU.mult), reads=[B["t1"], bp], writes=[B["kp"]])
                p, bp = big(xr, bxr, Wr)
                S.op("dve", lambda e: e.tensor_tensor(out=tmp2[:], in0=p[:].rearrange("p a b -> p (a b)"), in1=rkB[:], op=ALU.mult),
                     reads=[bp, bB], writes=[B["tmp2"]])
                S.op("dve", lambda e: e.tensor_tensor(out=tmp3[:].rearrange("p a b -> p (a b)"), in0=tmp2[:], in1=kp[:], op=ALU.mult),
                     reads=[B["tmp2"], B["kp"]], writes=[B["tmp3"]])
                S.op("dve", lambda e: e.tensor_reduce(out=bsum[:], in_=tmp3[:], axis=AX.X, op=ALU.add), reads=[B["tmp3"]], writes=[B["bsum"]])
                p, bp = big(xv, bxv, Wv)
                o, bo = bvo[k % 2], bbvo[k % 2]
                S.op("dve", lambda e: e.tensor_tensor(out=o[:], in0=p[:].rearrange("p a (h d) -> p (a h) d", d=64),
                                                      in1=bsum[:].to_broadcast([128, 16, 64]), op=ALU.mult), reads=[bp, B["bsum"]], writes=[bo])
                S.dma("sp", BV[t0:t0 + 128, :], o[:].rearrange("p a b -> p (a b)"), bo, reads=[bo], writes=[BVB])
                o, bo = vto[k % 2], bvto[k % 2]
                S.op("act", lambda e: e.copy(out=o[:], in_=p[:].rearrange("p a b -> p (a b)")), reads=[bp], writes=[bo])
                S.dma("sp", Vtok[t0:t0 + 128, :], o[:], bo, reads=[bo], writes=[VB])
                p, bp = pp[npp % 2], bpp[npp % 2]
                npp += 1
                for hh in range(2):
                    S.op("pe", lambda e: e.matmul(p[:, hh, :], lhsT=sg1a[:, ts], rhs=G2a[:, hh * 512:(hh + 1) * 512], start=True, stop=False),
                         reads=[bW, B["sg1a"]], writes=[bp], sig=False)
                    S.op("pe", lambda e: e.matmul(p[:, hh, :], lhsT=sg1b[:, ts], rhs=G2b[:, hh * 512:(hh + 1) * 512], start=False, stop=True),
                         reads=[bW, B["sg1b"]], writes=[bp])
                o, bo = gto[k % 2], bgto[k % 2]
                S.op("act", lambda e: e.copy(out=o[:], in_=p[:].rearrange("p a b -> p (a b)")), reads=[bp], writes=[bo])
                S.dma("sp", Gt[t0:t0 + 128, :], o[:], bo, reads=[bo], writes=[GB])
                k += 1
        S.barrier()
        P.release()
        for b in [bW, bB] + bbvo + bvto + bgto:
            S.release(b)


def rwkv_scan_phase(nc, S, RtT, KtT, AtT, BtT, WC, Vtok, Ysc, bins, bY, ntok, NI=4):
    nch = ntok // 64
    ngr = nch // 8
    with ExitStack() as es:
        sb = lambda name, shape, dt: es.enter_context(nc.sbuf_tensor(_uniq(name), shape, dt))
        ps = lambda name, shape, dt: es.enter_context(nc.psum_tensor(_uniq(name), shape, dt))
        MU = sb("MU", [128, 128], F32)
        MUI = sb("MUI", [128, 128], F32)
        ML = sb("ML", [128, 128], F32)
        I32 = sb("I32", [128, 128], F32)
        identb = sb("identb", [128, 128], BF16)
        bconst = S.buf("const")
        for (m, chm, pat, op) in ((MU, -1, 1, ALU.is_gt), (MUI, -1, 1, ALU.is_ge), (ML, 1, -1, ALU.is_gt)):
            S.op("pool", lambda e: e.memset(m[:], 1.0), writes=[bconst])
            S.op("pool", lambda e: e.affine_select(out=m[:], in_=m[:], pattern=[[pat, 128]], compare_op=op, fill=0.0, base=0,
                                                   channel_multiplier=chm), reads=[bconst], writes=[bconst])
        make_ident(nc, S, I32, bconst)
        make_ident(nc, S, identb, bconst)

        class Slot:
            pass
        slots = []
        for si in range(NI):
            s = Slot()
            n_ = lambda x: "%s_%d" % (x, si)
            s.AR = [sb(n_("AR%d" % j), [128, 8, 2, 128], BF16) for j in range(2)]
            s.Bd = [sb(n_("Bd%d" % j), [128, 8, 128], BF16) for j in range(2)]
            s.Kd = [sb(n_("Kd%d" % j), [128, 8, 128], BF16) for j in range(2)]
            s.bbd = [S.dbuf(n_("bd0")), S.dbuf(n_("bd1"))]
            s.Vs = [sb(n_("Vs%d" % j), [128, 8, 64], BF16) for j in range(2)]
            s.Yo = [sb(n_("Yo%d" % j), [128, 8, 64], F32) for j in range(2)]
            s.bYo = [S.dbuf(n_("Yo0")), S.dbuf(n_("Yo1"))]
            s.wcs = sb(n_("wcs"), [128, nch], F32)
            s.bwcs = S.dbuf(n_("wcs"))
            s.QX = [sb(n_("QX%d" % j), [128, 2, 128], BF16) for j in range(2)]
            s.P = [sb(n_("P%d" % j), [128, 128], BF16) for j in range(2)]
            s.bQX = [S.buf("QX") for _ in range(2)]
            s.bP = [S.buf("P") for _ in range(2)]
            for k in ("Mak", "Mrb", "Mrk", "BtT", "KtT"):
                setattr(s, k, [sb(n_(k) + "_%d" % q_, [128, 128], BF16) for q_ in range(2)])
                setattr(s, "b" + k, [S.buf(k) for _ in range(2)])
            s.TT = [sb(n_("TT%d" % q_), [128, 128], BF16) for q_ in range(2)]
            s.bTT = [S.buf("TT") for _ in range(2)]
            s.Xs = sb(n_("Xs"), [128, 64], BF16)
            s.Ub = sb(n_("Ub"), [128, 64], BF16)
            s.Sw = sb(n_("Sw"), [128, 64], F32)
            s.St = sb(n_("St"), [128, 64], F32)
            s.Sb = sb(n_("Sb"), [128, 64], BF16)
            s.bXs, s.bUb, s.bSw, s.bSt, s.bSb = [S.buf(k) for k in ("Xs", "Ub", "Sw", "St", "Sb")]
            s.psA = ps(n_("psA"), [128, 512], F32)
            s.psB = ps(n_("psB"), [128, 4, 128], F32)
            s.bA, s.bB = S.pbuf("bankA"), S.pbuf("bankB")
            s.ptr = s.psB[:, 3, :].bitcast(BF16)
            for j in range(2):
                S.op("pool", lambda e: e.memset(s.AR[j][:], 0.0), writes=[s.bbd[j]])
                S.op("pool", lambda e: e.memset(s.Bd[j][:], 0.0), writes=[s.bbd[j]])
                S.op("pool", lambda e: e.memset(s.Kd[j][:], 0.0), writes=[s.bbd[j]])
            slots.append(s)

        bRs, bKs, bAs, bBs, bWC, bV = bins

        def load_group(s, hp, gg):
            j = gg % 2
            t0 = gg * 512
            for h in range(2):
                r0 = hp * 128 + h * 64
                hs, cs = slice(h * 64, (h + 1) * 64), slice(h * 64, (h + 1) * 64)
                for (dst, src, bsrc) in ((s.AR[j][hs, :, 0, cs], AtT, bAs), (s.AR[j][hs, :, 1, cs], RtT, bRs),
                                         (s.Bd[j][hs, :, cs], BtT, bBs), (s.Kd[j][hs, :, cs], KtT, bKs)):
                    S.dma("sp", dst, src[r0:r0 + 64, t0:t0 + 512].rearrange("p (c j) -> p c j", j=64), s.bbd[j],
                          reads=[bsrc], writes=[s.bbd[j]])
                S.dma("sp", s.Vs[j][hs, :, :], Vtok[t0:t0 + 512, r0:r0 + 64].rearrange("(c j) v -> j c v", j=64),
                      s.bbd[j], reads=[bV], writes=[s.bbd[j]])

        def T_steps(gg, c):
            j = gg % 2
            q = (gg * 8 + c) % 2
            steps = []

            def st1():
                for s in slots:
                    AR, A, Bd, K = s.AR[j][:, c, :, :], s.AR[j][:, c, 0, :], s.Bd[j][:, c, :], s.Kd[j][:, c, :]
                    S.op("pe", lambda e: e.matmul(s.psA[:, 0:256], lhsT=Bd, rhs=AR, start=True, stop=True), reads=[s.bbd[j]], writes=[s.bA], sig=False)
                    S.op("pe", lambda e: e.matmul(s.psA[:, 256:512], lhsT=K, rhs=AR, start=True, stop=True), reads=[s.bbd[j]], writes=[s.bA])
                    S.op("pe", lambda e: e.matmul(s.psB[:, 1, :], lhsT=A, rhs=Bd, start=True, stop=True), reads=[s.bbd[j]], writes=[s.bB], sig=False)
                    S.op("pe", lambda e: e.transpose(s.ptr[:, 0:128], Bd, identb[:]), reads=[s.bbd[j], bconst], writes=[s.bB], sig=False)
                    S.op("pe", lambda e: e.transpose(s.ptr[:, 128:256], K, identb[:]), reads=[s.bbd[j], bconst], writes=[s.bB])
            steps.append(st1)

            def st2():
                for s in slots:
                    S.op("dve", lambda e: e.tensor_tensor(out=s.QX[0][:, 0, :], in0=s.psA[:, 0:128], in1=MU[:], op=ALU.mult),
                         reads=[s.bA, bconst], writes=[s.bQX[0]])
                    S.op("dve", lambda e: e.tensor_tensor(out=s.QX[1][:, 1, :], in0=s.QX[0][:, 0, :], in1=I32[:], op=ALU.add),
                         reads=[s.bQX[0], bconst], writes=[s.bQX[1]])
                    S.op("dve", lambda e: e.tensor_tensor(out=s.P[0][:], in0=s.psB[:, 1, :], in1=ML[:], op=ALU.mult),
                         reads=[s.bB, bconst], writes=[s.bP[0]])
            steps.append(st2)

            def st3():
                for s in slots:
                    for (nm, lo, msk) in (("Mrb", 128, MUI), ("Mak", 256, MU), ("Mrk", 384, MUI)):
                        S.op("dve", lambda e: e.tensor_tensor(out=getattr(s, nm)[q][:], in0=s.psA[:, lo:lo + 128], in1=msk[:], op=ALU.mult),
                             reads=[s.bA, bconst], writes=[getattr(s, "b" + nm)[q]])
                    S.op("act", lambda e: e.copy(out=s.BtT[q][:], in_=s.ptr[:, 0:128]), reads=[s.bB], writes=[s.bBtT[q]])
                    S.op("act", lambda e: e.copy(out=s.KtT[q][:], in_=s.ptr[:, 128:256]), reads=[s.bB], writes=[s.bKtT[q]])
            steps.append(st3)

            for lvl in range(6):
                cur = 0 if lvl == 0 else lvl % 2
                nxt = 1 - cur

                def sa(lvl=lvl, cur=cur):
                    for s in slots:
                        Q, X, Pm = s.QX[cur][:, 0, :], s.QX[cur][:, 1, :], s.P[cur][:]
                        rd = [s.bP[cur], s.bQX[cur]]
                        if lvl == 0:
                            S.op("pe", lambda e: e.matmul(s.psA[:, 0:128], lhsT=Pm, rhs=Q, start=True, stop=True), reads=rd, writes=[s.bA], sig=False)
                        elif lvl <= 3:
                            S.op("pe", lambda e: e.matmul(s.psA[:, 0:256], lhsT=Pm, rhs=s.QX[cur][:, :, :], start=True, stop=True),
                                 reads=rd, writes=[s.bA], sig=False)
                        else:
                            S.op("pe", lambda e: e.matmul(s.psA[:, 128:256], lhsT=Pm, rhs=X, start=True, stop=True), reads=rd, writes=[s.bA],
                                 sig=(lvl == 5))
                        if lvl <= 4:
                            S.op("pe", lambda e: e.matmul(s.psA[:, 256:384], lhsT=Q, rhs=Pm, start=True, stop=True), reads=rd, writes=[s.bA])

                def sb_(lvl=lvl, cur=cur, nxt=nxt):
                    for s in slots:
                        if lvl <= 3:
                            S.op("act", lambda e: e.copy(out=s.QX[nxt][:, 0, :], in_=s.psA[:, 0:128]), reads=[s.bA], writes=[s.bQX[nxt]])
                        if lvl <= 4:
                            S.op("act", lambda e: e.copy(out=s.P[nxt][:], in_=s.psA[:, 256:384]), reads=[s.bA], writes=[s.bP[nxt]])
                        if 1 <= lvl <= 4:
                            S.op("dve", lambda e: e.tensor_tensor(out=s.QX[nxt][:, 1, :], in0=s.psA[:, 128:256], in1=s.QX[cur][:, 1, :], op=ALU.add),
                                 reads=[s.bA, s.bQX[cur]], writes=[s.bQX[nxt]])
                        if lvl == 5:
                            S.op("dve", lambda e: e.tensor_tensor(out=s.TT[q][:], in0=s.psA[:, 128:256], in1=s.QX[cur][:, 1, :], op=ALU.add),
                                 reads=[s.bA, s.bQX[cur]], writes=[s.bTT[q]])
                steps += [sa, sb_]
            return steps

        def S_steps(gg, c):
            j = gg % 2
            ch = gg * 8 + c
            q = ch % 2

            def s1():
                for s in slots:
                    A = s.AR[j][:, c, 0, :]
                    S.op("pe", lambda e: e.matmul(s.psB[:, 0, 0:64], lhsT=A, rhs=s.Sb[:], start=True, stop=False),
                         reads=[s.bbd[j], s.bSb], writes=[s.bB], sig=False)
                    S.op("pe", lambda e: e.matmul(s.psB[:, 0, 0:64], lhsT=s.Mak[q][:], rhs=s.Vs[j][:, c, :], start=False, stop=True),
                         reads=[s.bMak[q], s.bbd[j]], writes=[s.bB])
                    S.op("pool", lambda e: e.tensor_scalar(out=s.Sw[:], in0=s.St[:], scalar1=s.wcs[:, ch:ch + 1], scalar2=None, op0=ALU.mult),
                         reads=[s.bSt, s.bwcs], writes=[s.bSw])

            def s2():
                for s in slots:
                    S.op("act", lambda e: e.copy(out=s.Xs[:], in_=s.psB[:, 0, 0:64]), reads=[s.bB], writes=[s.bXs])

            def s3():
                for s in slots:
                    S.op("pe", lambda e: e.matmul(s.psB[:, 1, 0:64], lhsT=s.TT[q][:], rhs=s.Xs[:], start=True, stop=True),
                         reads=[s.bTT[q], s.bXs], writes=[s.bB])

            def s4():
                for s in slots:
                    S.op("act", lambda e: e.copy(out=s.Ub[:], in_=s.psB[:, 1, 0:64]), reads=[s.bB], writes=[s.bUb])

            def s5():
                for s in slots:
                    R = s.AR[j][:, c, 1, :]
                    pY, pS = s.psB[:, 2, 0:64], s.psB[:, 0, 0:64]
                    S.op("pe", lambda e: e.matmul(pY, lhsT=R, rhs=s.Sb[:], start=True, stop=False),
                         reads=[s.bbd[j], s.bSb], writes=[s.bB], sig=False)
                    S.op("pe", lambda e: e.matmul(pY, lhsT=s.Mrb[q][:], rhs=s.Ub[:], start=False, stop=False),
                         reads=[s.bMrb[q], s.bUb], writes=[s.bB], sig=False)
                    S.op("pe", lambda e: e.matmul(pY, lhsT=s.Mrk[q][:], rhs=s.Vs[j][:, c, :], start=False, stop=True),
                         reads=[s.bMrk[q], s.bbd[j]], writes=[s.bB], sig=False)
                    S.op("pe", lambda e: e.matmul(pS, lhsT=s.BtT[q][:], rhs=s.Ub[:], start=True, stop=False),
                         reads=[s.bBtT[q], s.bUb], writes=[s.bB], sig=False)
                    S.op("pe", lambda e: e.matmul(pS, lhsT=s.KtT[q][:], rhs=s.Vs[j][:, c, :], start=False, stop=True),
                         reads=[s.bKtT[q], s.bbd[j]], writes=[s.bB])

            def s6():
                for s in slots:
                    S.op("dve", lambda e: e.scalar_tensor_tensor(out=s.St[:], in0=s.psB[:, 0, 0:64], scalar=s.wcs[:, ch:ch + 1], in1=s.Sw[:],
                                                                 op0=ALU.mult, op1=ALU.add), reads=[s.bB, s.bwcs, s.bSw], writes=[s.bSt])
                    S.op("dve", lambda e: e.tensor_copy(out=s.Yo[j][:, c, :], in_=s.psB[:, 2, 0:64]), reads=[s.bB], writes=[s.bYo[j]])
                    S.op("act", lambda e: e.copy(out=s.Sb[:], in_=s.St[:]), reads=[s.bSt], writes=[s.bSb])
            return [s1, s2, s3, s4, s5, s6]

        for rnd in range(8 // NI):
            hps = [rnd * NI + i for i in range(NI)]
            for s, hp in zip(slots, hps):
                S.dma("sp", s.wcs[:], WC[hp * 128:(hp + 1) * 128, 0:nch], s.bwcs, reads=[bWC], writes=[s.bwcs])
                S.op("pool", lambda e: e.memset(s.St[:], 0.0), writes=[s.bSt])
                S.op("pool", lambda e: e.memset(s.Sb[:], 0.0), writes=[s.bSb])
                load_group(s, hp, 0)
            if ngr > 1:
                for s, hp in zip(slots, hps):
                    load_group(s, hp, 1)
            for st in T_steps(0, 0):
                st()
            for gg in range(ngr):
                j = gg % 2
                for c in range(8):
                    ch = gg * 8 + c
                    if ch + 1 < nch:
                        ng_, nc_ = (gg, c + 1) if c < 7 else (gg + 1, 0)
                        tsteps = T_steps(ng_, nc_)
                    else:
                        tsteps = []
                    ssteps = S_steps(gg, c)
                    ti = 0
                    for ss_ in ssteps:
                        for _ in range(3):
                            if ti < len(tsteps):
                                tsteps[ti]()
                                ti += 1
                        ss_()
                    while ti < len(tsteps):
                        tsteps[ti]()
                        ti += 1
                for s, hp in zip(slots, hps):
                    for h in range(2):
                        c0 = hp * 128 + h * 64
                        S.dma("sp", Ysc[gg * 512:(gg + 1) * 512, c0:c0 + 64].rearrange("(c j) v -> j c v", j=64),
                              s.Yo[j][h * 64:(h + 1) * 64, :, :], s.bYo[j], reads=[s.bYo[j]], writes=[bY])
                if gg + 2 < ngr:
                    for s, hp in zip(slots, hps):
                        load_group(s, hp, gg + 2)
        S.barrier()
        for s in slots:
            for b_ in s.bbd + s.bYo + [s.bwcs]:
                S.release(b_)


def rwkv_post_phase(nc, S, Ysc, BV, Gt, lg_row, lb_row, ZT, bins, bZT, ntok, gn_eps=64e-5):
    with ExitStack() as es:
        sb = lambda name, shape, dt: es.enter_context(nc.sbuf_tensor(_uniq(name), shape, dt))
        ps = lambda name, shape, dt: es.enter_context(nc.psum_tensor(_uniq(name), shape, dt))
        lgB = sb("lgB", [128, D], F32)
        lbB = sb("lbB", [128, D], F32)
        ident = sb("ident", [128, 128], BF16)
        nh = sb("nh", [128, 16, 1], F32)
        yt = [sb("yt%d" % i, [128, 16, 64], F32) for i in range(2)]
        bvt = [sb("bvt%d" % i, [128, D], F32) for i in range(2)]
        gt = [sb("gt%d" % i, [128, D], F32) for i in range(2)]
        sm = sb("sm", [128, 16, 1], F32)
        vr = sb("vr", [128, 16, 1], F32)
        rstd = sb("rstd", [128, 16, 1], F32)
        yc = sb("yc", [128, 16, 64], F32)
        sq = sb("sq", [128, 16, 64], F32)
        yn = sb("yn", [128, 16, 64], F32)
        y2 = sb("y2", [128, D], F32)
        zb = [sb("zb%d" % i, [128, D], BF16) for i in range(2)]
        zT = [sb("zT%d" % i, [128, 8, 512], BF16) for i in range(2)]
        p_tr = [ps("p_tr%d" % i, [128, 8, 128], BF16) for i in range(2)]
        bC = S.dbuf("C")
        byt = [S.dbuf("yt") for _ in range(2)]
        bbvt = [S.dbuf("bvt") for _ in range(2)]
        bgt = [S.dbuf("gt") for _ in range(2)]
        bzT = [S.dbuf("zT") for _ in range(2)]
        B = {n: S.buf(n) for n in ["ident", "nh", "sm", "vr", "rstd", "yc", "sq", "yn", "y2", "zb0", "zb1"]}
        bp_tr = [S.pbuf("ptr") for _ in range(2)]
        bYs, bBV, bG = bins
        make_ident(nc, S, ident, B["ident"])
        S.op("pool", lambda e: e.memset(nh[:], -0.5), writes=[B["nh"]])
        S.dma("sp", lgB[:], lg_row.to_broadcast([128, D]), bC, writes=[bC])
        S.dma("sp", lbB[:], lb_row.to_broadcast([128, D]), bC, writes=[bC])
        nt = ntok // 128
        for t in range(nt):
            i = t % 2
            t0 = t * 128
            S.dma("sp", yt[i][:].rearrange("p a b -> p (a b)"), Ysc[t0:t0 + 128, :], byt[i], reads=[bYs], writes=[byt[i]])
            S.dma("sp", bvt[i][:], BV[t0:t0 + 128, :], bbvt[i], reads=[bBV], writes=[bbvt[i]])
            S.dma("sp", gt[i][:], Gt[t0:t0 + 128, :], bgt[i], reads=[bG], writes=[bgt[i]])
            y3 = yt[i]
            S.op("dve", lambda e: e.tensor_reduce(out=sm[:], in_=y3[:], axis=AX.X, op=ALU.add), reads=[byt[i]], writes=[B["sm"]])
            S.op("dve", lambda e: e.tensor_scalar(out=sm[:], in0=sm[:], scalar1=1.0 / 64, scalar2=None, op0=ALU.mult), reads=[B["sm"]], writes=[B["sm"]])
            S.op("dve", lambda e: e.tensor_tensor(out=yc[:], in0=y3[:], in1=sm[:].to_broadcast([128, 16, 64]), op=ALU.subtract),
                 reads=[byt[i], B["sm"]], writes=[B["yc"]])
            S.op("dve", lambda e: e.tensor_tensor(out=sq[:], in0=yc[:], in1=yc[:], op=ALU.mult), reads=[B["yc"]], writes=[B["sq"]])
            S.op("dve", lambda e: e.tensor_reduce(out=vr[:], in_=sq[:], axis=AX.X, op=ALU.add), reads=[B["sq"]], writes=[B["vr"]])
            S.op("dve", lambda e: e.tensor_scalar(out=vr[:], in0=vr[:], scalar1=1.0 / 64, scalar2=gn_eps, op0=ALU.mult, op1=ALU.add),
                 reads=[B["vr"]], writes=[B["vr"]])
            S.op("pool", lambda e: e.tensor_tensor(out=rstd[:], in0=vr[:], in1=nh[:], op=ALU.pow), reads=[B["vr"], B["nh"]], writes=[B["rstd"]])
            S.op("dve", lambda e: e.tensor_tensor(out=yn[:], in0=yc[:], in1=rstd[:].to_broadcast([128, 16, 64]), op=ALU.mult),
                 reads=[B["yc"], B["rstd"]], writes=[B["yn"]])
            ynf = yn[:].rearrange("p a b -> p (a b)")
            S.op("dve", lambda e: e.tensor_tensor(out=y2[:], in0=ynf, in1=lgB[:], op=ALU.mult), reads=[B["yn"], bC], writes=[B["y2"]])
            S.op("dve", lambda e: e.tensor_tensor(out=y2[:], in0=y2[:], in1=lbB[:], op=ALU.add), reads=[B["y2"], bC], writes=[B["y2"]])
            S.op("dve", lambda e: e.tensor_tensor(out=y2[:], in0=y2[:], in1=bvt[i][:], op=ALU.add), reads=[B["y2"], bbvt[i]], writes=[B["y2"]])
            z, bz = zb[i], B["zb%d" % i]
            S.op("dve", lambda e: e.tensor_tensor(out=z[:], in0=y2[:], in1=gt[i][:], op=ALU.mult), reads=[B["y2"], bgt[i]], writes=[bz])
            pt, bpt = p_tr[i], bp_tr[i]
            for c in range(8):
                S.op("pe", lambda e: e.transpose(pt[:, c, :], z[:, c * 128:(c + 1) * 128], ident[:]), reads=[bz, B["ident"]], writes=[bpt], sig=(c == 7))
            gi = (t // 4) % 2
            S.op("act", lambda e: e.copy(out=zT[gi][:, :, (t % 4) * 128:(t % 4 + 1) * 128], in_=pt[:]), reads=[bpt], writes=[bzT[gi]])
            if t % 4 == 3:
                g0 = (t // 4) * 512
                for c in range(8):
                    S.dma("sp", ZT[c * 128:(c + 1) * 128, g0:g0 + 512], zT[gi][:, c, :], bzT[gi], reads=[bzT[gi]], writes=[bZT])
        S.barrier()
        for b in [bC] + byt + bbvt + bgt + bzT:
            S.release(b)


def build_program(ntok=SEQ):
    nc = bass.Bass("TRN2", target_bir_lowering=False)
    di = lambda n, s: nc.dram_tensor(n, list(s), F32, kind="ExternalInput").ap()
    x = di("x", [ntok, D])
    ffn_norm = di("ffn_norm", [4, D])
    wg = di("ffn_w_gate", [2, 2, D, DFF])
    wu = di("ffn_w_up", [2, 2, D, DFF])
    wd = di("ffn_w_down", [2, 2, DFF, D])
    mix_norm = di("mix_norm", [2, D])
    dbias = di("dbias", [12, 128, 2, 128])
    w_in = di("attn_w_in", [D, 3072])
    qn = di("attn_q_norm", [64, 1])
    kn = di("attn_k_norm", [64, 1])
    w_out = di("attn_w_out", [512, D])
    rw_mix = di("rw_mix", [6, D])
    rows = {n: di(n, [1, D]) for n in ("rw_w0", "rw_a0", "rw_kk", "rw_ka", "rw_rk", "rw_lnx_g", "rw_lnx_b")}
    rw_w1 = di("rw_w1", [D, 64]); rw_w2 = di("rw_w2", [64, D]); rw_a1 = di("rw_a1", [D, 64]); rw_a2 = di("rw_a2", [64, D])
    rw_g1 = di("rw_g1", [D, 160]); rw_g2 = di("rw_g2", [160, D])
    rw_wr = di("rw_wr", [D, D]); rw_wk = di("rw_wk", [D, D]); rw_wv = di("rw_wv", [D, D]); rw_wo = di("rw_wo", [D, D])
    out = nc.dram_tensor("out", [ntok, D], F32, kind="ExternalOutput").ap()
    scr = lambda n, s, dt: nc.dram_tensor(n, list(s), dt, kind="Internal").ap()
    xa = scr("xa", [ntok, D], F32); xb = scr("xb", [ntok, D], F32)
    QT = scr("QT", [D, ntok], BF16); KT = scr("KT", [D, ntok], BF16); V = scr("V", [ntok, D], BF16); MT = scr("MT", [512, ntok], BF16)
    RtT, KtT, AtT, BtT = [scr(n, [D, ntok], BF16) for n in ("RtT", "KtT", "AtT", "BtT")]
    WC = scr("WC", [D, ntok // 64], F32)
    Vtok = scr("Vtok", [ntok, D], BF16); BV = scr("BV", [ntok, D], F32); Gt = scr("Gt", [ntok, D], F32); Ysc = scr("Ysc", [ntok, D], F32)
    ZT = scr("ZT", [D, ntok], BF16)

    S = Sched(nc, n_dma_sems=24)
    nb = lambda n: S.buf(n, acc=True)
    bx, bxa, bxb, bout = nb("x"), nb("xa"), nb("xb"), nb("out")
    bQT, bKT, bV, bMT = nb("QT"), nb("KT"), nb("V"), nb("MT")
    bR, bK, bA, bB, bWC, bVt, bBV, bG, bY, bZ = [nb(n) for n in ("R", "K", "A", "B", "WC", "Vt", "BV", "G", "Y", "Z")]

    ffn_phase(nc, S, x, xa, bx, bxa, wg[0, 0], wu[0, 0], wd[0, 0], ffn_norm[0:1, :], ntok)
    attn_in_phase(nc, S, xa, bxa, w_in, mix_norm[0:1, :], qn, kn, QT, KT, V, bQT, bKT, bV, ntok)
    sb_attn_phase(nc, S, QT, KT, V, MT, bQT, bKT, bV, bMT, ntok)
    dil_attn_phase(nc, S, QT, KT, V, MT, dbias, bQT, bKT, bV, bMT, ntok)
    out_proj_phase(nc, S, xa, xb, bxa, bxb, MT, bMT, w_out, 512, ntok)
    ffn_phase(nc, S, xb, xa, bxb, bxa, wg[0, 1], wu[0, 1], wd[0, 1], ffn_norm[1:2, :], ntok)
    ffn_phase(nc, S, xa, xb, bxa, bxb, wg[1, 0], wu[1, 0], wd[1, 0], ffn_norm[2:3, :], ntok)
    rwkv_fm_phase(nc, S, xb, bxb, mix_norm[1:2, :], rw_mix, rows["rw_w0"], rw_w1, rw_w2, rows["rw_a0"], rw_a1, rw_a2,
                  rows["rw_kk"], rows["rw_ka"], rw_wr, rw_wk, RtT, KtT, AtT, BtT, WC, [bR, bK, bA, bB, bWC], ntok)
    rwkv_tm_phase(nc, S, xb, bxb, mix_norm[1:2, :], rw_mix, rows["rw_a0"], rw_a1, rw_a2, rw_g1, rw_g2, rows["rw_ka"], rows["rw_rk"],
                  rw_wr, rw_wk, rw_wv, Vtok, BV, Gt, [bVt, bBV, bG], ntok)
    rwkv_scan_phase(nc, S, RtT, KtT, AtT, BtT, WC, Vtok, Ysc, [bR, bK, bA, bB, bWC, bVt], bY, ntok)
    rwkv_post_phase(nc, S, Ysc, BV, Gt, rows["rw_lnx_g"], rows["rw_lnx_b"], ZT, [bY, bBV, bG], bZ, ntok)
    out_proj_phase(nc, S, xb, xa, bxb, bxa, ZT, bZ, rw_wo, 1024, ntok)
    ffn_phase(nc, S, xa, out, bxa, bout, wg[1, 1], wu[1, 1], wd[1, 1], ffn_norm[3:4, :], ntok)
    S.wait_for("sp", [bout])
    return nc


def kernel(x, ffn_norm, ffn_w_gate, ffn_w_up, ffn_w_down, mix_norm, rel_bias,
           attn_w_in, attn_q_norm, attn_k_norm, attn_w_out,
           rw_mix, rw_w0, rw_w1, rw_w2, rw_a0, rw_a1, rw_a2, rw_g1, rw_g2,
           rw_kk, rw_ka, rw_rk, rw_wr, rw_wk, rw_wv, rw_wo, rw_lnx_g, rw_lnx_b):
    f = lambda a: np.ascontiguousarray(np.asarray(a, dtype=np.float32))
    x = f(x)
    n = x.shape[0]
    shared = {
        "ffn_norm": f(ffn_norm).reshape(4, D), "ffn_w_gate": f(ffn_w_gate), "ffn_w_up": f(ffn_w_up), "ffn_w_down": f(ffn_w_down),
        "mix_norm": f(mix_norm), "dbias": dil_bias_host(f(rel_bias)),
        "attn_w_in": f(attn_w_in)[0], "attn_q_norm": f(attn_q_norm).reshape(64, 1), "attn_k_norm": f(attn_k_norm).reshape(64, 1),
        "attn_w_out": f(attn_w_out)[0], "rw_mix": f(rw_mix)[0],
        "rw_w0": f(rw_w0).reshape(1, D), "rw_a0": f(rw_a0).reshape(1, D), "rw_kk": f(rw_kk).reshape(1, D), "rw_ka": f(rw_ka).reshape(1, D),
        "rw_rk": f(rw_rk).reshape(1, D), "rw_lnx_g": f(rw_lnx_g).reshape(1, D), "rw_lnx_b": f(rw_lnx_b).reshape(1, D),
        "rw_w1": f(rw_w1)[0], "rw_w2": f(rw_w2)[0], "rw_a1": f(rw_a1)[0], "rw_a2": f(rw_a2)[0], "rw_g1": f(rw_g1)[0], "rw_g2": f(rw_g2)[0],
        "rw_wr": f(rw_wr)[0], "rw_wk": f(rw_wk)[0], "rw_wv": f(rw_wv)[0], "rw_wo": f(rw_wo)[0],
    }
    nc = build_program(x.shape[1])
    in_maps = [dict(shared, x=x[i]) for i in range(n)]
    res = run_bass_kernel_spmd(nc, in_maps, core_ids=list(range(n)))
    return np.stack([np.asarray(r["out"]) for r in res.results], axis=0).astype(np.float32)
```

```python
import numpy as np
from contextlib import ExitStack
import concourse.bass as bass
import concourse.mybir as mybir
from concourse.bass_utils import run_bass_kernel_spmd

F32 = mybir.dt.float32
BF16 = mybir.dt.bfloat16
AF = mybir.ActivationFunctionType
ALU = mybir.AluOpType
AX = mybir.AxisListType

D = 1024
DFF = 2816
NF = DFF // 128
SEQ = 4096


_UID = [0]


def _uniq(name):
    _UID[0] += 1
    return "%s_u%d" % (name, _UID[0])


def _merge(d, s):
    for k, v in s.items():
        if d.get(k, 0) < v:
            d[k] = v


class Buf:
    __slots__ = ("name", "wr", "rd", "acc", "dkey", "excl")

    def __init__(self, name, acc=False, excl=False):
        self.name = name
        self.wr = {}
        self.rd = {}
        self.acc = acc
        self.dkey = None
        self.excl = excl


class Sched:
    ENG = ("pe", "act", "dve", "pool", "sp")

    def __init__(self, nc, n_dma_sems=40):
        self.nc = nc
        self.eng = {"pe": nc.tensor, "act": nc.scalar, "dve": nc.vector, "pool": nc.gpsimd, "sp": nc.sync}
        self.sems = {}
        self.val = {}
        self.seen = {e: {} for e in self.ENG}
        self.epoch = 0
        self.ekey = {}
        self._new_engine_sems()
        self.dma_pool = []
        for i in range(n_dma_sems):
            k = "dma%d" % i
            self.sems[k] = nc.semaphore(k).__enter__()
            self.val[k] = 0
            self.dma_pool.append(k)
        self.nwait = 0

    def _new_engine_sems(self):
        for e in self.ENG:
            k = "%s_e%d" % (e, self.epoch)
            self.sems[k] = self.nc.semaphore(k).__enter__()
            self.val[k] = 0
            self.ekey[e] = k

    def buf(self, name, acc=False):
        return Buf(name, acc)

    def pbuf(self, name):
        return Buf(name, False, True)

    def dbuf(self, name, acc=False):
        b = Buf(name, acc)
        b.dkey = self.dma_pool.pop()
        return b

    def release(self, b):
        self.dma_pool.append(b.dkey)
        b.dkey = None

    def _wait(self, e, deps):
        for k, v in deps.items():
            if v <= 0:
                continue
            if e == "pe" and k == self.ekey["pe"]:
                continue
            if self.seen[e].get(k, 0) < v:
                self.eng[e].wait_ge(self.sems[k], v)
                self.seen[e][k] = v
                self.nwait += 1

    def _deps(self, reads, writes, e=None):
        deps = {}
        for b in reads:
            _merge(deps, b.wr)
            if b.excl:
                own = self.ekey.get(e)
                _merge(deps, {k: v for k, v in b.rd.items() if k != own})
        for b in writes:
            _merge(deps, b.wr)
            _merge(deps, b.rd)
        return deps

    def _record(self, ev, reads, writes):
        for b in reads:
            _merge(b.rd, ev)
        for b in writes:
            if b.acc:
                _merge(b.wr, ev)
            else:
                b.wr = dict(ev)
                b.rd = {}

    def op(self, e, fn, reads=(), writes=(), sig=True):
        self._wait(e, self._deps(reads, writes, e))
        ins = fn(self.eng[e])
        k = self.ekey[e]
        if sig:
            ins.then_inc(self.sems[k], 1)
            self.val[k] += 1
            v = self.val[k]
        else:
            v = self.val[k] + 1
        self._record({k: v}, reads, writes)
        return ins

    def dma(self, q, out, in_, track, reads=(), writes=(), slow=False):
        self._wait(q, self._deps(reads, writes))
        if slow:
            ins = self.eng[q].dma_start(out=out, in_=in_, allow_slow_non_contiguous=True)
        else:
            ins = self.eng[q].dma_start(out=out, in_=in_)
        k = track.dkey
        ins.then_inc(self.sems[k], 16)
        self.val[k] += 16
        self._record({k: self.val[k]}, reads, writes)
        return ins

    def barrier(self, new_epoch=True):
        allv = {k: v for k, v in self.val.items() if v > 0}
        for e in self.ENG:
            self._wait(e, allv)
        if new_epoch:
            self.epoch += 1
            self._new_engine_sems()

    def wait_for(self, e, bufs):
        deps = {}
        for b in bufs:
            _merge(deps, b.wr)
            _merge(deps, b.rd)
        self._wait(e, deps)


def ffn_phase(nc, S, x_in, x_out, xin_b, xout_b, wg, wu, wd, gain_row, ntok, eps=1e-6):
    G = 256
    ng = ntok // G
    with ExitStack() as es:
        sb = lambda name, shape, dt: es.enter_context(nc.sbuf_tensor(_uniq(name), shape, dt))
        ps = lambda name, shape, dt: es.enter_context(nc.psum_tensor(_uniq(name), shape, dt))
        Wg = sb("Wg", [128, 8, DFF], BF16)
        Wu = sb("Wu", [128, 8, DFF], BF16)
        Wd = sb("Wd", [128, NF, D], BF16)
        gB = sb("gB", [128, D], F32)
        ident = sb("ident", [128, 128], BF16)
        xt = [sb("xt%d" % i, [128, D], F32) for i in range(4)]
        ot = [sb("ot%d" % i, [128, D], F32) for i in range(2)]
        hb = [sb("hb%d" % i, [128, D], BF16) for i in range(2)]
        hT = [sb("hT%d" % i, [128, 8, G], BF16) for i in range(2)]
        aT = [sb("aT%d" % i, [128, G], BF16) for i in range(3)]
        sg = [sb("sg%d" % i, [128, G], F32) for i in range(2)]
        junk = sb("junk", [128, D], BF16)
        ss = [sb("ss%d" % i, [128, 1], F32) for i in range(2)]
        rs = [sb("rs%d" % i, [128, 1], F32) for i in range(2)]
        nh = sb("nh", [128, 1], F32)
        p_gu = [ps("p_gu%d" % i, [128, 2, G], F32) for i in range(2)]
        p_dn = [ps("p_dn%d" % i, [128, 512], F32) for i in range(4)]
        p_tr = [ps("p_tr%d" % i, [128, 8, 128], BF16) for i in range(2)]

        FB = [(0, 6), (6, 12), (12, 17), (17, 22)]
        blk_of = {}
        for bi, (j0, j1) in enumerate(FB):
            for j in range(j0, j1):
                blk_of[j] = bi
        bWgL = [S.dbuf("Wg") for _ in FB]
        bWuL = [S.dbuf("Wu") for _ in FB]
        bWdL = [S.dbuf("Wd") for _ in FB]
        bgB = S.dbuf("gB")
        bxt = [S.dbuf("xt%d" % i) for i in range(4)]
        bot = [S.dbuf("ot%d" % i) for i in range(2)]
        bhb = [S.buf("hb") for _ in range(2)]
        bhT = [S.buf("hT") for _ in range(2)]
        baT = [S.buf("aT") for _ in range(3)]
        bsg = [S.buf("sg") for _ in range(2)]
        bjunk = S.buf("junk")
        bss = [S.buf("ss") for _ in range(2)]
        brs = [S.buf("rs") for _ in range(2)]
        bnh, bident = S.buf("nh"), S.buf("ident")
        bp_gu = [S.pbuf("pgu") for _ in range(2)]
        bp_dn = [S.pbuf("pdn") for _ in range(4)]
        bp_tr = [S.pbuf("ptr") for _ in range(2)]

        S.op("pool", lambda e: e.memset(nh[:], -0.5), writes=[bnh])
        S.op("pool", lambda e: e.memset(ident[:], 0.0), writes=[bident])
        S.op("pool", lambda e: e.affine_select(out=ident[:], in_=ident[:], pattern=[[-1, 128]],
                                               compare_op=ALU.not_equal, fill=1.0, base=0,
                                               channel_multiplier=1), reads=[bident], writes=[bident])
        S.dma("sp", gB[:], gain_row.to_broadcast([128, D]), bgB, writes=[bgB])
        wg_v = wg.rearrange("(c p) f -> p c f", p=128)
        wu_v = wu.rearrange("(c p) f -> p c f", p=128)
        wd_v = wd.rearrange("(c p) f -> p c f", p=128)
        for bi, (j0, j1) in enumerate(FB):
            f0, f1 = j0 * 128, j1 * 128
            S.dma("pool", Wg[:, :, f0:f1], wg_v[:, :, f0:f1], bWgL[bi], writes=[bWgL[bi]])
            S.dma("pool", Wu[:, :, f0:f1], wu_v[:, :, f0:f1], bWuL[bi], writes=[bWuL[bi]])
            S.dma("pool", Wd[:, j0:j1, :], wd_v[:, j0:j1, :], bWdL[bi], writes=[bWdL[bi]])

        def load(g):
            for s in range(2):
                i = (g % 2) * 2 + s
                t0 = g * G + s * 128
                S.dma("sp", xt[i][:], x_in[t0:t0 + 128, :], bxt[i], reads=[xin_b], writes=[bxt[i]])

        def prep_dve(g):
            for s in range(2):
                i = (g % 2) * 2 + s
                S.op("dve", lambda e: e.scalar_tensor_tensor(out=junk[:], in0=xt[i][:], scalar=1.0, in1=xt[i][:],
                                                             op0=ALU.mult, op1=ALU.mult, accum_out=ss[s][:]),
                     reads=[bxt[i]], writes=[bjunk, bss[s]])
                S.op("dve", lambda e: e.tensor_scalar(out=ss[s][:], in0=ss[s][:], scalar1=1.0 / D, scalar2=eps,
                                                      op0=ALU.mult, op1=ALU.add), reads=[bss[s]], writes=[bss[s]])
                S.op("pool", lambda e: e.tensor_tensor(out=rs[s][:], in0=ss[s][:], in1=nh[:], op=ALU.pow),
                     reads=[bss[s], bnh], writes=[brs[s]])
                S.op("dve", lambda e: e.scalar_tensor_tensor(out=hb[s][:], in0=xt[i][:], scalar=rs[s][:], in1=gB[:],
                                                             op0=ALU.mult, op1=ALU.mult),
                     reads=[bxt[i], brs[s], bgB], writes=[bhb[s]])

        def prep_pe(g):
            for s in range(2):
                for c in range(8):
                    S.op("pe", lambda e: e.transpose(p_tr[s][:, c, :], hb[s][:, c * 128:(c + 1) * 128], ident[:]),
                         reads=[bhb[s], bident], writes=[bp_tr[s]], sig=(c == 7))
                S.op("act", lambda e: e.copy(out=hT[g % 2][:, :, s * 128:(s + 1) * 128], in_=p_tr[s][:]),
                     reads=[bp_tr[s]], writes=[bhT[g % 2]])

        def gate_up(g, j):
            h = hT[g % 2]
            pg = p_gu[j % 2]
            for c in range(8):
                S.op("pe", lambda e: e.matmul(pg[:, 0, :], lhsT=Wg[:, c, j * 128:(j + 1) * 128], rhs=h[:, c, :],
                                              start=(c == 0), stop=(c == 7)),
                     reads=[bWgL[blk_of[j]], bhT[g % 2]], writes=[bp_gu[j % 2]], sig=False)
            for c in range(8):
                S.op("pe", lambda e: e.matmul(pg[:, 1, :], lhsT=Wu[:, c, j * 128:(j + 1) * 128], rhs=h[:, c, :],
                                              start=(c == 0), stop=(c == 7)),
                     reads=[bWuL[blk_of[j]], bhT[g % 2]], writes=[bp_gu[j % 2]], sig=(c == 7))
            S.op("act", lambda e: e.activation(out=sg[j % 2][:], in_=pg[:, 0, :], func=AF.Silu),
                 reads=[bp_gu[j % 2]], writes=[bsg[j % 2]])
            S.op("dve", lambda e: e.tensor_tensor(out=aT[j % 3][:], in0=sg[j % 2][:], in1=pg[:, 1, :], op=ALU.mult),
                 reads=[bsg[j % 2], bp_gu[j % 2]], writes=[baT[j % 3]])

        def down(g, j):
            for s in range(2):
                for hh in range(2):
                    S.op("pe", lambda e: e.matmul(p_dn[s * 2 + hh][:], lhsT=aT[j % 3][:, s * 128:(s + 1) * 128],
                                                  rhs=Wd[:, j, hh * 512:(hh + 1) * 512],
                                                  start=(j == 0), stop=(j == NF - 1)),
                         reads=[baT[j % 3], bWdL[blk_of[j]]], writes=[bp_dn[s * 2 + hh]], sig=(j == NF - 1 or (s == 1 and hh == 1)))

        def epilogue(g):
            for s in range(2):
                i = (g % 2) * 2 + s
                for hh in range(2):
                    S.op("dve", lambda e: e.scalar_tensor_tensor(out=ot[s][:, hh * 512:(hh + 1) * 512], in0=p_dn[s * 2 + hh][:],
                                                                 scalar=0.5, in1=xt[i][:, hh * 512:(hh + 1) * 512],
                                                                 op0=ALU.mult, op1=ALU.add),
                         reads=[bp_dn[s * 2 + hh], bxt[i]], writes=[bot[s]])
                t0 = g * G + s * 128
                S.dma("sp", x_out[t0:t0 + 128, :], ot[s][:], bot[s], reads=[bot[s]], writes=[xout_b])

        load(0)
        if ng > 1:
            load(1)
        prep_dve(0)
        prep_pe(0)
        for g in range(ng):
            gate_up(g, 0)
            for j in range(NF):
                if j + 1 < NF:
                    gate_up(g, j + 1)
                elif g + 1 < ng:
                    pass
                down(g, j)
                if j == 3 and g + 1 < ng:
                    prep_dve(g + 1)
                if j == 14 and g + 1 < ng:
                    prep_pe(g + 1)
            epilogue(g)
            if g + 2 < ng:
                load(g + 2)
        S.barrier()
        for b in bWgL + bWuL + bWdL + [bgB] + bxt + bot:
            S.release(b)


def make_ident(nc, S, ident, bident, dt_is_bf16=True):
    S.op("pool", lambda e: e.memset(ident[:], 0.0), writes=[bident])
    S.op("pool", lambda e: e.affine_select(out=ident[:], in_=ident[:], pattern=[[-1, 128]],
                                           compare_op=ALU.not_equal, fill=1.0, base=0,
                                           channel_multiplier=1), reads=[bident], writes=[bident])


def attn_in_phase(nc, S, x_in, xin_b, w_in, gain_row, qn, kn, QT, KT, V, bQT, bKT, bV, ntok, eps=1e-6):
    G = 512
    ng = ntok // G
    with ExitStack() as es:
        sb = lambda name, shape, dt: es.enter_context(nc.sbuf_tensor(_uniq(name), shape, dt))
        ps = lambda name, shape, dt: es.enter_context(nc.psum_tensor(_uniq(name), shape, dt))
        Win = sb("Win", [128, 8, 3072], BF16)
        gB = sb("gB", [128, D], F32)
        ident = sb("ident", [128, 128], BF16)
        bones = sb("bones", [128, 128], BF16)
        gq = sb("gq", [128, 1], F32)
        gk = sb("gk", [128, 1], F32)
        nh = sb("nh", [128, 1], F32)
        eps_ap = sb("eps_ap", [128, 1], F32)
        xt = [sb("xt%d" % i, [128, D], F32) for i in range(4)]
        hb = [sb("hb%d" % i, [128, D], BF16) for i in range(2)]
        hT = [sb("hT%d" % i, [128, 8, G], BF16) for i in range(2)]
        junk = sb("junk", [128, D], BF16)
        ss = [sb("ss%d" % i, [128, 1], F32) for i in range(2)]
        rs = [sb("rs%d" % i, [128, 1], F32) for i in range(2)]
        ob = [sb("ob%d" % i, [128, G], BF16) for i in range(3)]
        sq = [sb("sq%d" % i, [128, G], BF16) for i in range(2)]
        lt = [sb("lt%d" % i, [128, G], F32) for i in range(2)]
        rr = [sb("rr%d" % i, [128, G], F32) for i in range(2)]
        vb = [sb("vb%d" % i, [128, D], BF16) for i in range(2)]
        p_q = [ps("p_q%d" % i, [128, G], F32) for i in range(3)]
        p_s = [ps("p_s%d" % i, [128, G], F32) for i in range(1)]
        p_v = [ps("p_v%d" % i, [128, 512], F32) for i in range(2)]
        p_tr = [ps("p_tr%d" % i, [128, 8, 128], BF16) for i in range(2)]

        bWin, bgB, bgq, bgk = S.dbuf("Win"), S.dbuf("gB"), S.dbuf("gq"), S.dbuf("gk")
        bxt = [S.dbuf("xt") for _ in range(4)]
        bob = [S.dbuf("ob") for _ in range(3)]
        bvb = [S.dbuf("vb") for _ in range(2)]
        bhb = [S.buf("hb") for _ in range(2)]
        bhT = [S.buf("hT") for _ in range(2)]
        bjunk, bnh, bident, bbones = S.buf("junk"), S.buf("nh"), S.buf("ident"), S.buf("bones")
        bss = [S.buf("ss") for _ in range(2)]
        brs = [S.buf("rs") for _ in range(2)]
        bsq = [S.buf("sq") for _ in range(2)]
        blt = [S.buf("lt") for _ in range(2)]
        brr = [S.buf("rr") for _ in range(2)]
        bp_q = [S.pbuf("pq") for _ in range(3)]
        bp_s = [S.pbuf("ps") for _ in range(1)]
        bp_v = [S.pbuf("pv") for _ in range(2)]
        bp_tr = [S.pbuf("ptr") for _ in range(2)]

        S.op("pool", lambda e: e.memset(nh[:], -0.5), writes=[bnh])
        S.op("pool", lambda e: e.memset(eps_ap[:], eps), writes=[bnh])
        make_ident(nc, S, ident, bident)
        S.op("pool", lambda e: e.memset(bones[:], 0.0), writes=[bbones])
        S.op("pool", lambda e: e.memset(bones[0:64, 0:64], 1.0), writes=[bbones])
        S.op("pool", lambda e: e.memset(bones[64:128, 64:128], 1.0), writes=[bbones])
        S.dma("sp", gB[:], gain_row.to_broadcast([128, D]), bgB, writes=[bgB])
        for hh in range(2):
            S.dma("sp", gq[hh * 64:(hh + 1) * 64, :], qn, bgq, writes=[bgq])
            S.dma("sp", gk[hh * 64:(hh + 1) * 64, :], kn, bgk, writes=[bgk])
        S.op("dve", lambda e: e.tensor_scalar(out=gq[:], in0=gq[:], scalar1=0.125, scalar2=None, op0=ALU.mult),
             reads=[bgq], writes=[bgq])
        w_v = w_in.rearrange("(c p) f -> p c f", p=128)
        for c in range(8):
            S.dma("pool", Win[:, c, :], w_v[:, c, :], bWin, writes=[bWin])

        def load(g):
            for s in range(4):
                t0 = g * G + s * 128
                S.dma("sp", xt[s][:], x_in[t0:t0 + 128, :], bxt[s], reads=[xin_b], writes=[bxt[s]])

        def prep(g):
            for s in range(4):
                k = s % 2
                S.op("dve", lambda e: e.scalar_tensor_tensor(out=junk[:], in0=xt[s][:], scalar=1.0, in1=xt[s][:],
                                                             op0=ALU.mult, op1=ALU.mult, accum_out=ss[k][:]),
                     reads=[bxt[s]], writes=[bjunk, bss[k]])
                S.op("dve", lambda e: e.tensor_scalar(out=ss[k][:], in0=ss[k][:], scalar1=1.0 / D, scalar2=eps,
                                                      op0=ALU.mult, op1=ALU.add), reads=[bss[k]], writes=[bss[k]])
                S.op("pool", lambda e: e.tensor_tensor(out=rs[k][:], in0=ss[k][:], in1=nh[:], op=ALU.pow),
                     reads=[bss[k], bnh], writes=[brs[k]])
                S.op("dve", lambda e: e.scalar_tensor_tensor(out=hb[k][:], in0=xt[s][:], scalar=rs[k][:], in1=gB[:],
                                                             op0=ALU.mult, op1=ALU.mult),
                     reads=[bxt[s], brs[k], bgB], writes=[bhb[k]])
                for c in range(8):
                    S.op("pe", lambda e: e.transpose(p_tr[k][:, c, :], hb[k][:, c * 128:(c + 1) * 128], ident[:]),
                         reads=[bhb[k], bident], writes=[bp_tr[k]], sig=(c == 7))
                S.op("act", lambda e: e.copy(out=hT[g % 2][:, :, s * 128:(s + 1) * 128], in_=p_tr[k][:]),
                     reads=[bp_tr[k]], writes=[bhT[g % 2]])

        nob = 0

        def _dummy():
            pass
        load(0)
        prep(0)
        for g in range(ng):
            h = hT[g % 2]
            bh = bhT[g % 2]
            t0 = g * G
            def fcinfo(fc):
                isq = fc < 8
                ch = fc % 8
                if ch < 2:
                    col0 = (0 if isq else 256) + ch * 128
                else:
                    col0 = (768 if isq else 1536) + (ch - 2) * 128
                return isq, ch, col0

            def main(fc):
                isq, ch, col0 = fcinfo(fc)
                pq = p_q[fc % 3]
                for c in range(8):
                    S.op("pe", lambda e: e.matmul(pq[:], lhsT=Win[:, c, col0:col0 + 128], rhs=h[:, c, :],
                                                  start=(c == 0), stop=(c == 7)),
                         reads=[bWin, bh], writes=[bp_q[fc % 3]], sig=(c == 7))

            def tail(fc):
                nonlocal nob
                isq, ch, col0 = fcinfo(fc)
                pq = p_q[fc % 3]
                bpq = bp_q[fc % 3]
                o = ob[nob % 3]
                bo = bob[nob % 3]
                nob += 1
                if ch < 2:
                    S.op("act", lambda e: e.activation(out=o[:], in_=pq[:], func=AF.Copy, scale=(0.125 if isq else 1.0)),
                         reads=[bpq], writes=[bo])
                else:
                    k = fc % 2
                    S.op("act", lambda e: e.activation(out=sq[k][:], in_=pq[:], func=AF.Square),
                         reads=[bpq], writes=[bsq[k]])
                    S.op("pe", lambda e: e.matmul(p_s[0][:], lhsT=bones[:], rhs=sq[k][:], start=True, stop=True),
                         reads=[bbones, bsq[k]], writes=[bp_s[0]])
                    S.op("act", lambda e: e.activation(out=lt[k][:], in_=p_s[0][:], func=AF.Ln, scale=1.0 / 64, bias=eps_ap[:]),
                         reads=[bp_s[0], bnh], writes=[blt[k]])
                    S.op("act", lambda e: e.activation(out=rr[k][:], in_=lt[k][:], func=AF.Exp, scale=-0.5),
                         reads=[blt[k]], writes=[brr[k]])
                    gcol = gq if isq else gk
                    S.op("dve", lambda e: e.scalar_tensor_tensor(out=o[:], in0=pq[:], scalar=gcol[:], in1=rr[k][:],
                                                                 op0=ALU.mult, op1=ALU.mult),
                         reads=[bpq, brr[k], bgq, bgk], writes=[bo])
                dst, bdst = (QT, bQT) if isq else (KT, bKT)
                S.dma("sp", dst[ch * 128:(ch + 1) * 128, t0:t0 + G], o[:], bo, reads=[bo], writes=[bdst])

            main(0)
            for fc in range(16):
                if fc + 1 < 16:
                    main(fc + 1)
                tail(fc)
                if fc == 1 and g + 1 < ng:
                    load(g + 1)
                if fc == 6 and g + 1 < ng:
                    prep(g + 1)
            for s in range(4):
                k = s % 2
                for (pv, cols, off) in ((p_v[0], (512, 768), 0), (p_v[0], (2304, 2560), 256), (p_v[1], (2560, 3072), 0)):
                    n = cols[1] - cols[0]
                    for c in range(8):
                        S.op("pe", lambda e: e.matmul(pv[:, off:off + n], lhsT=h[:, c, s * 128:(s + 1) * 128],
                                                      rhs=Win[:, c, cols[0]:cols[1]], start=(c == 0), stop=(c == 7)),
                             reads=[bWin, bh], writes=[bp_v[0], bp_v[1]], sig=(c == 7))
                S.op("act", lambda e: e.copy(out=vb[k][:, 0:512], in_=p_v[0][:]), reads=[bp_v[0]], writes=[bvb[k]])
                S.op("dve", lambda e: e.tensor_copy(out=vb[k][:, 512:1024], in_=p_v[1][:]), reads=[bp_v[1]], writes=[bvb[k]])
                S.dma("sp", V[t0 + s * 128:t0 + (s + 1) * 128, :], vb[k][:], bvb[k], reads=[bvb[k]], writes=[bV])
        S.barrier()
        for b in [bWin, bgB, bgq, bgk] + bxt + bob + bvb:
            S.release(b)


def sb_attn_phase(nc, S, QT, KT, V, MT, bQT, bKT, bV, bMT, ntok):
    nblk = ntok // 128
    with ExitStack() as es:
        sb = lambda name, shape, dt: es.enter_context(nc.sbuf_tensor(_uniq(name), shape, dt))
        ps = lambda name, shape, dt: es.enter_context(nc.psum_tensor(_uniq(name), shape, dt))
        qT = sb("qT", [128, 2, ntok], BF16)
        kT = sb("kT", [128, 2, ntok], BF16)
        v = sb("v", [128, nblk, 256], BF16)
        ones = sb("ones", [128, 512], F32)
        onec = sb("onec", [128, 1], F32)
        mneg = sb("mneg", [128, 128], BF16)
        ident = sb("ident", [128, 128], BF16)
        mk2 = lambda nm, shape, dt: [[sb("%s%d_%d" % (nm, h, i), shape, dt) for i in range(2)] for h in range(2)]
        e_ = mk2("e", [128, 512], F32)
        sp_ = mk2("sp", [128, 512], F32)
        cs_ = mk2("cs", [128, 512], F32)
        lw_ = mk2("lw", [128, 512], F32)
        w_ = mk2("w", [128, 512], BF16)
        wT_ = mk2("wT", [128, 4, 128], BF16)
        oT = [sb("oT%d" % i, [128, 512], BF16) for i in range(2)]
        p_z = [[ps("p_z%d_%d" % (h, i), [128, 512], F32) for i in range(2)] for h in range(2)]
        p_w = [ps("p_w%d" % i, [128, 4, 128], BF16) for i in range(2)]
        p_o = [ps("p_o%d" % i, [128, 128], F32) for i in range(2)]

        bq, bk, bv = S.dbuf("qT"), S.dbuf("kT"), S.dbuf("v")
        boT = [S.dbuf("oT") for _ in range(2)]
        bones, bmneg, bident = S.buf("ones"), S.buf("mneg"), S.buf("ident")
        bb2 = lambda nm: [[S.buf(nm) for _ in range(2)] for _ in range(2)]
        be, bsp, bcs, blw, bw, bwT = bb2("e"), bb2("sp"), bb2("cs"), bb2("lw"), bb2("w"), bb2("wT")
        bp_z = [[S.pbuf("pz") for _ in range(2)] for _ in range(2)]
        bp_w = [S.pbuf("pw") for _ in range(2)]
        bp_o = [S.pbuf("po") for _ in range(2)]

        S.op("pool", lambda e: e.memset(ones[:], 1.0), writes=[bones])
        S.op("pool", lambda e: e.memset(onec[:], 1.0), writes=[bones])
        make_ident(nc, S, ident, bident)
        S.op("pool", lambda e: e.memset(mneg[:], 0.0), writes=[bmneg])
        S.op("pool", lambda e: e.affine_select(out=mneg[:], in_=mneg[:], pattern=[[-1, 128]], compare_op=ALU.is_gt,
                                               fill=-30000.0, base=0, channel_multiplier=1),
             reads=[bmneg], writes=[bmneg])
        for pr in range(2):
            S.dma("sp", qT[:, pr, :], QT[pr * 128:(pr + 1) * 128, 0:ntok], bq, reads=[bQT], writes=[bq])
            S.dma("sp", kT[:, pr, :], KT[pr * 128:(pr + 1) * 128, 0:ntok], bk, reads=[bKT], writes=[bk])
        S.dma("sp", v[:], V[0:ntok, 0:256].rearrange("(b p) c -> p b c", p=128), bv, reads=[bV], writes=[bv])

        steps = []
        for pr in range(2):
            for qb in range(nblk):
                chunks = [(4 * (qb // 4), qb + 1, True)]
                for c in range(qb // 4 - 1, -1, -1):
                    chunks.append((4 * c, 4 * c + 4, False))
                for ci, (b0, b1, diag) in enumerate(chunks):
                    steps.append((pr, qb, ci, b0, b1, diag, len(chunks)))
        HH = [(0, slice(0, 64)), (1, slice(64, 128))]

        def front(t):
            pr, qb, ci, b0, b1, diag, nchk = steps[t]
            W = (b1 - b0) * 128
            i = t % 2
            for (hh, P) in HH:
                pz, bpz = p_z[hh][i], bp_z[hh][i]
                S.op("pe", lambda e: e.matmul(pz[:, 0:W], lhsT=qT[P, pr, qb * 128:(qb + 1) * 128],
                                              rhs=kT[P, pr, b0 * 128:b1 * 128], start=True, stop=(not diag)),
                     reads=[bq, bk], writes=[bpz], sig=(not diag))
                if diag:
                    S.op("pe", lambda e: e.matmul(pz[:, W - 128:W], lhsT=ident[:], rhs=mneg[:], start=False, stop=True),
                         reads=[bident, bmneg], writes=[bpz])
            for (hh, P) in HH:
                S.op("act", lambda e: e.activation(out=e_[hh][i][:, 0:W], in_=p_z[hh][i][:, 0:W], func=AF.Exp),
                     reads=[bp_z[hh][i]], writes=[be[hh][i]])
            for (hh, P) in HH:
                S.op("act", lambda e: e.activation(out=sp_[hh][i][:, 0:W], in_=e_[hh][i][:, 0:W], func=AF.Ln, bias=onec[:]),
                     reads=[be[hh][i], bones], writes=[bsp[hh][i]])

        def back(t):
            pr, qb, ci, b0, b1, diag, nchk = steps[t]
            W = (b1 - b0) * 128
            nb = b1 - b0
            i = t % 2
            rev = (lambda tt: tt[:, W - 1::-1] if W < 512 else tt[:, ::-1])
            for (hh, P) in HH:
                if ci == 0:
                    init, rd = 0.0, [bsp[hh][i], bones]
                else:
                    init, rd = cs_[hh][1 - i][:, 0:1], [bsp[hh][i], bones, bcs[hh][1 - i]]
                S.op("dve", lambda e: e.tensor_tensor_scan(out=rev(cs_[hh][i]), data0=ones[:, 0:W], data1=rev(sp_[hh][i]),
                                                           initial=init, op0=ALU.mult, op1=ALU.add),
                     reads=rd, writes=[bcs[hh][i]])
            for (hh, P) in HH:
                S.op("dve", lambda e: e.tensor_tensor(out=lw_[hh][i][:, 0:W], in0=p_z[hh][i][:, 0:W], in1=cs_[hh][i][:, 0:W], op=ALU.subtract),
                     reads=[bp_z[hh][i], bcs[hh][i]], writes=[blw[hh][i]])
            for (hh, P) in HH:
                S.op("act", lambda e: e.activation(out=w_[hh][i][:, 0:W], in_=lw_[hh][i][:, 0:W], func=AF.Exp),
                     reads=[blw[hh][i]], writes=[bw[hh][i]])
            for (hh, P) in HH:
                pw, bpw = p_w[hh], bp_w[hh]
                for b in range(nb):
                    S.op("pe", lambda e: e.transpose(pw[:, b, :], w_[hh][i][:, b * 128:(b + 1) * 128], ident[:]),
                         reads=[bw[hh][i], bident], writes=[bpw], sig=(b == nb - 1))
                if hh == 0:
                    S.op("act", lambda e: e.copy(out=wT_[hh][i][:, 0:nb, :], in_=pw[:, 0:nb, :]), reads=[bpw], writes=[bwT[hh][i]])
                else:
                    S.op("dve", lambda e: e.tensor_copy(out=wT_[hh][i][:, 0:nb, :], in_=pw[:, 0:nb, :]), reads=[bpw], writes=[bwT[hh][i]])
            po, bpo = p_o[qb % 2], bp_o[qb % 2]
            for (hh, P) in HH:
                h = 2 * pr + hh
                for b in range(nb):
                    first = (ci == 0 and b == 0)
                    last = (ci == nchk - 1 and b == nb - 1)
                    S.op("pe", lambda e: e.matmul(po[P, :], lhsT=v[:, b0 + b, h * 64:(h + 1) * 64], rhs=wT_[hh][i][:, b, :],
                                                  start=first, stop=last),
                         reads=[bv, bwT[hh][i]], writes=[bpo], sig=(b == nb - 1))
            if ci == nchk - 1:
                k = (qb // 4) % 2
                S.op("dve", lambda e: e.tensor_copy(out=oT[k][:, (qb % 4) * 128:(qb % 4 + 1) * 128], in_=po[:, :]),
                     reads=[bpo], writes=[boT[k]])
                if qb % 4 == 3 or qb == nblk - 1:
                    q0 = 4 * (qb // 4)
                    n = (qb - q0 + 1) * 128
                    S.dma("sp", MT[pr * 128:(pr + 1) * 128, q0 * 128:q0 * 128 + n], oT[k][:, 0:n], boT[k],
                          reads=[boT[k]], writes=[bMT])

        front(0)
        for t in range(len(steps)):
            if t + 1 < len(steps):
                front(t + 1)
            back(t)
        S.barrier()
        for b_ in [bq, bk, bv] + boT:
            S.release(b_)


def dil_bias_host(rel_bias):
    out = np.empty((12, 128, 2, 128), np.float32)
    kj = np.arange(128)[:, None]
    q = np.arange(128)[None, :]
    for g, r in enumerate((1, 4, 16)):
        for part, dist in ((1, q - kj), (0, q + 128 - kj)):
            valid = (dist >= 0) & (dist <= 128)
            dd = np.maximum(dist, 0) * r
            d = np.maximum(dd, 1).astype(np.float32)
            large = 16 + (np.log(d / np.float32(16)) / np.float32(np.log(2048 / 16)) * np.float32(16)).astype(np.int32)
            large = np.minimum(large, 31)
            bucket = np.where(dd < 16, dd, large)
            for j in range(4):
                hd = 4 * g + j
                out[hd, :, part, :] = np.where(valid, rel_bias[bucket, hd], np.float32(-30000.0))
    return out


def dil_attn_phase(nc, S, QT, KT, V, MT, dbias, bQT, bKT, bV, bMT, ntok):
    with ExitStack() as es:
        sb = lambda name, shape, dt: es.enter_context(nc.sbuf_tensor(_uniq(name), shape, dt))
        ps = lambda name, shape, dt: es.enter_context(nc.psum_tensor(_uniq(name), shape, dt))
        qT = sb("qT", [128, 2, ntok], BF16)
        kT = sb("kT", [128, 2, ntok], BF16)
        v = sb("v", [128, ntok // 128, 256], BF16)
        bias = sb("bias", [128, 4, 256], F32)
        onesb = sb("onesb", [128, 64], BF16)
        Nacc = sb("Nacc", [128, 2, ntok], F32)
        Dacc = sb("Dacc", [128, 2, ntok], F32)
        s_ = [sb("s%d" % i, [128, 256], F32) for i in range(3)]
        pT_ = [sb("pT%d" % i, [128, 2, 128], BF16) for i in range(3)]
        ob = [sb("ob%d" % i, [128, 1024], BF16) for i in range(2)]
        p_s = [ps("p_s%d" % i, [128, 2, 128], F32) for i in range(2)]
        p_n = [ps("p_n%d" % i, [128, 128], F32) for i in range(2)]
        p_d = [ps("p_d%d" % i, [128, 128], F32) for i in range(2)]
        p_pad = [ps("p_pad%d" % i, [128, 256], F32) for i in range(0)]

        bq, bk, bv, bbias = S.dbuf("qT"), S.dbuf("kT"), S.dbuf("v"), S.dbuf("bias")
        bob = [S.dbuf("ob") for _ in range(2)]
        bones, bN, bD = S.buf("ones"), S.buf("N"), S.buf("D")
        bs = [S.buf("s") for _ in range(3)]
        bpT = [S.buf("pT") for _ in range(3)]
        bp_s = [S.pbuf("ps") for _ in range(2)]
        bp_n = [S.pbuf("pn") for _ in range(2)]
        bp_d = [S.pbuf("pd") for _ in range(2)]

        S.op("pool", lambda e: e.memset(onesb[:], 1.0), writes=[bones])
        u = 0
        for g, r in enumerate((1, 4, 16)):
            L = ntok // r
            nb = L // 128
            for pr in range(2):
                r0 = 256 + (2 * g + pr) * 128
                S.dma("sp", qT[:, pr, :], QT[r0:r0 + 128, 0:ntok], bq, reads=[bQT], writes=[bq])
                S.dma("sp", kT[:, pr, :], KT[r0:r0 + 128, 0:ntok], bk, reads=[bKT], writes=[bk])
            vsrc = V[0:ntok, 256 + g * 256:256 + (g + 1) * 256].rearrange("(n i c) f -> c i n f", i=128, c=r)
            for c in range(r):
                S.dma("sp", v[:, c * nb:(c + 1) * nb, :], vsrc[c], bv, reads=[bV], writes=[bv])
            for j in range(4):
                S.dma("sp", bias[:, j, :], dbias[4 * g + j].rearrange("k a q -> k (a q)"), bbias, writes=[bbias])
            units = []
            for j in range(4):
                for c in range(r):
                    for n in range(nb):
                        units.append((j, c, n))

            def tokf(c, nn):
                st = c + r * 128 * nn
                return slice(st, st + r * 127 + 1, r)

            def front(uu, u):
                j, c, n = units[uu]
                pr, hh = j // 2, j % 2
                P = slice(64 * hh, 64 * hh + 64)
                i = u % 3
                pss, bpss = p_s[u % 2], bp_s[u % 2]
                a0 = 0 if n > 0 else 1
                if n > 0:
                    S.op("pe", lambda e: e.matmul(pss[:, 0, :], lhsT=kT[P, pr, tokf(c, n - 1)], rhs=qT[P, pr, tokf(c, n)],
                                                  start=True, stop=True), reads=[bq, bk], writes=[bpss], sig=False)
                S.op("pe", lambda e: e.matmul(pss[:, 1, :], lhsT=kT[P, pr, tokf(c, n)], rhs=qT[P, pr, tokf(c, n)],
                                              start=True, stop=True), reads=[bq, bk], writes=[bpss])
                S.op("dve", lambda e: e.tensor_tensor(out=s_[i][:, a0 * 128:256], in0=pss[:, a0:2, :],
                                                      in1=bias[:, j, a0 * 128:256], op=ALU.add),
                     reads=[bpss, bbias], writes=[bs[i]])
                S.op("act", lambda e: e.activation(out=pT_[i][:, a0:2, :], in_=s_[i][:, a0 * 128:256], func=AF.Exp),
                     reads=[bs[i]], writes=[bpT[i]])

            def back(uu, u):
                j, c, n = units[uu]
                pr, hh = j // 2, j % 2
                P = slice(64 * hh, 64 * hh + 64)
                i = u % 3
                pn, bpn = p_n[u % 2], bp_n[u % 2]
                pd, bpd = p_d[u % 2], bp_d[u % 2]
                a0 = 0 if n > 0 else 1
                for a_ in range(a0, 2):
                    S.op("pe", lambda e: e.matmul(pn[P, :], lhsT=v[:, c * nb + n - 1 + a_, j * 64:(j + 1) * 64],
                                                  rhs=pT_[i][:, a_, :], start=(a_ == a0), stop=(a_ == 1)),
                         reads=[bv, bpT[i]], writes=[bpn], sig=(a_ == 1))
                for a_ in range(a0, 2):
                    S.op("pe", lambda e: e.matmul(pd[P, :], lhsT=onesb[:, :], rhs=pT_[i][:, a_, :],
                                                  start=(a_ == a0), stop=(a_ == 1)),
                         reads=[bones, bpT[i]], writes=[bpd], sig=(a_ == 1))
                tk = tokf(c, n)
                if g == 0:
                    S.op("act", lambda e: e.copy(out=Nacc[P, pr, tk], in_=pn[P, :]), reads=[bpn], writes=[bN])
                    S.op("dve", lambda e: e.tensor_copy(out=Dacc[P, pr, tk], in_=pd[P, :]), reads=[bpd], writes=[bD])
                else:
                    S.op("dve", lambda e: e.tensor_tensor(out=Nacc[P, pr, tk], in0=pn[P, :], in1=Nacc[P, pr, tk],
                                                          op=ALU.add), reads=[bpn, bN], writes=[bN])
                    S.op("dve", lambda e: e.tensor_tensor(out=Dacc[P, pr, tk], in0=pd[P, :], in1=Dacc[P, pr, tk],
                                                          op=ALU.add), reads=[bpd, bD], writes=[bD])

            front(0, u)
            for uu in range(len(units)):
                if uu + 1 < len(units):
                    front(uu + 1, u + 1)
                back(uu, u)
                u += 1
        k = 0
        for pr in range(2):
            for c0 in range(0, ntok, 1024):
                n = min(1024, ntok - c0)
                S.op("dve", lambda e: e.reciprocal(out=Dacc[:, pr, c0:c0 + n], in_=Dacc[:, pr, c0:c0 + n]), reads=[bD], writes=[bD])
                S.op("dve", lambda e: e.tensor_tensor(out=ob[k % 2][:, 0:n], in0=Nacc[:, pr, c0:c0 + n], in1=Dacc[:, pr, c0:c0 + n],
                                                      op=ALU.mult), reads=[bN, bD], writes=[bob[k % 2]])
                S.dma("sp", MT[256 + pr * 128:256 + (pr + 1) * 128, c0:c0 + n], ob[k % 2][:, 0:n], bob[k % 2],
                      reads=[bob[k % 2]], writes=[bMT])
                k += 1
        S.barrier()
        for b in [bq, bk, bv, bbias] + bob:
            S.release(b)


def out_proj_phase(nc, S, x_in, x_out, xin_b, xout_b, MT, bMT, w_out, kdim, ntok):
    G = 512
    ng = ntok // G
    kc = kdim // 128
    with ExitStack() as es:
        sb = lambda name, shape, dt: es.enter_context(nc.sbuf_tensor(_uniq(name), shape, dt))
        ps = lambda name, shape, dt: es.enter_context(nc.psum_tensor(_uniq(name), shape, dt))
        Wo = sb("Wo", [128, kc, D], BF16)
        mT = [sb("mT%d" % i, [128, kc, G], BF16) for i in range(2)]
        xt = [sb("xt%d" % i, [128, D], F32) for i in range(3)]
        ot = [sb("ot%d" % i, [128, D], F32) for i in range(2)]
        p_y = [ps("p_y%d" % i, [128, 512], F32) for i in range(4)]
        bWo = S.dbuf("Wo")
        bmT = [S.dbuf("mT") for _ in range(2)]
        bxt = [S.dbuf("xt") for _ in range(3)]
        bot = [S.dbuf("ot") for _ in range(2)]
        bp_y = [S.pbuf("py") for _ in range(4)]
        w_v = w_out.rearrange("(c p) f -> p c f", p=128)
        for c in range(kc):
            S.dma("pool", Wo[:, c, :], w_v[:, c, :], bWo, writes=[bWo])
        k = 0
        for g in range(ng):
            m, bm = mT[g % 2], bmT[g % 2]
            for c in range(kc):
                S.dma("sp", m[:, c, :], MT[c * 128:(c + 1) * 128, g * G:(g + 1) * G], bm, reads=[bMT], writes=[bm])
            for s in range(4):
                t0 = g * G + s * 128
                x_, bx = xt[k % 3], bxt[k % 3]
                o_, bo = ot[k % 2], bot[k % 2]
                S.dma("sp", x_[:], x_in[t0:t0 + 128, :], bx, reads=[xin_b], writes=[bx])
                for hh in range(2):
                    py, bpy = p_y[(2 * k + hh) % 4], bp_y[(2 * k + hh) % 4]
                    for c in range(kc):
                        S.op("pe", lambda e: e.matmul(py[:], lhsT=m[:, c, s * 128:(s + 1) * 128], rhs=Wo[:, c, hh * 512:(hh + 1) * 512],
                                                      start=(c == 0), stop=(c == kc - 1)),
                             reads=[bm, bWo], writes=[bpy], sig=(c == kc - 1))
                    S.op("dve", lambda e: e.tensor_tensor(out=o_[:, hh * 512:(hh + 1) * 512], in0=py[:], in1=x_[:, hh * 512:(hh + 1) * 512],
                                                          op=ALU.add), reads=[bpy, bx], writes=[bo])
                S.dma("sp", x_out[t0:t0 + 128, :], o_[:], bo, reads=[bo], writes=[xout_b])
                k += 1
        S.barrier()
        for b in [bWo] + bmT + bxt + bot:
            S.release(b)


C0 = float(np.exp(-0.5))


class RwPrep:
    def __init__(self, nc, S, es, G, mix_ids, x_in, xin_b, gain_row, mix, eps=1e-6):
        sb = lambda name, shape, dt: es.enter_context(nc.sbuf_tensor(_uniq(name), shape, dt))
        ps = lambda name, shape, dt: es.enter_context(nc.psum_tensor(_uniq(name), shape, dt))
        self.nc, self.S, self.G, self.mix_ids, self.x_in, self.xin_b, self.eps = nc, S, G, mix_ids, x_in, xin_b, eps
        self.gB = sb("gB", [128, D], F32)
        self.identf = sb("identf", [128, 128], F32)
        self.nh = sb("nh", [128, 1], F32)
        self.mixc = sb("mixc", [128, 6, 8], F32)
        self.xt = [sb("xt%d" % i, [128, D], F32) for i in range(2)]
        self.hn = [sb("hn%d" % i, [128, D], F32) for i in range(2)]
        self.junk = sb("junk", [128, D], BF16)
        self.ss = [sb("ss%d" % i, [128, 1], F32) for i in range(2)]
        self.rs = [sb("rs%d" % i, [128, 1], F32) for i in range(2)]
        self.hT = [sb("hT%d" % i, [128, 8, G + 1], F32) for i in range(2)]
        self.xx = [sb("xx%d" % i, [128, G], F32) for i in range(2)]
        self.xm = {i: sb("xm%d" % i, [128, 8, G], BF16) for i in mix_ids}
        self.p_tr = [ps("p_tr%d" % i, [128, 4, 128], F32) for i in range(2)]
        self.bgB, self.bmixc = S.dbuf("gB"), S.dbuf("mixc")
        self.bxt = [S.dbuf("xt") for _ in range(2)]
        self.bhn = [S.buf("hn") for _ in range(2)]
        self.bident, self.bnh, self.bjunk = S.buf("identf"), S.buf("nh"), S.buf("junk")
        self.bss = [S.buf("ss") for _ in range(2)]
        self.brs = [S.buf("rs") for _ in range(2)]
        self.bhT = [S.buf("hT") for _ in range(2)]
        self.bxx = [S.buf("xx") for _ in range(2)]
        self.bxm = {i: S.buf("xm") for i in mix_ids}
        self.bp_tr = [S.pbuf("ptr") for _ in range(2)]
        S.op("pool", lambda e: e.memset(self.nh[:], -0.5), writes=[self.bnh])
        make_ident(nc, S, self.identf, self.bident)
        S.dma("sp", self.gB[:], gain_row.to_broadcast([128, D]), self.bgB, writes=[self.bgB])
        for i in range(6):
            S.dma("sp", self.mixc[:, i, :], mix[i:i + 1, :].rearrange("o (c p) -> p (o c)", p=128), self.bmixc, writes=[self.bmixc], slow=True)
        S.op("dve", lambda e: e.memset(self.hT[1][:, :, G:G + 1], 0.0), writes=[self.bhT[1]])
        self.dsems = [self.bgB, self.bmixc] + self.bxt

    def group(self, g):
        nc, S, G = self.nc, self.S, self.G
        hT, bhT = self.hT[g % 2], self.bhT[g % 2]
        hTp, bhTp = self.hT[(g + 1) % 2], self.bhT[(g + 1) % 2]
        S.op("pool", lambda e: e.tensor_copy(out=hT[:, :, 0:1], in_=hTp[:, :, G:G + 1]), reads=[bhTp], writes=[bhT])
        k = 0
        for s in range(G // 128):
            t0 = g * G + s * 128
            x_, bx = self.xt[s % 2], self.bxt[s % 2]
            h_, bh = self.hn[s % 2], self.bhn[s % 2]
            ss, bss, rs, brs = self.ss[s % 2], self.bss[s % 2], self.rs[s % 2], self.brs[s % 2]
            S.dma("sp", x_[:], self.x_in[t0:t0 + 128, :], bx, reads=[self.xin_b], writes=[bx])
            S.op("dve", lambda e: e.scalar_tensor_tensor(out=self.junk[:], in0=x_[:], scalar=1.0, in1=x_[:], op0=ALU.mult, op1=ALU.mult,
                                                         accum_out=ss[:]), reads=[bx], writes=[self.bjunk, bss])
            S.op("dve", lambda e: e.tensor_scalar(out=ss[:], in0=ss[:], scalar1=1.0 / D, scalar2=self.eps, op0=ALU.mult, op1=ALU.add),
                 reads=[bss], writes=[bss])
            S.op("pool", lambda e: e.tensor_tensor(out=rs[:], in0=ss[:], in1=self.nh[:], op=ALU.pow), reads=[bss, self.bnh], writes=[brs])
            S.op("dve", lambda e: e.scalar_tensor_tensor(out=h_[:], in0=x_[:], scalar=rs[:], in1=self.gB[:], op0=ALU.mult, op1=ALU.mult),
                 reads=[bx, brs, self.bgB], writes=[bh])
            for half in range(2):
                pt, bpt = self.p_tr[k % 2], self.bp_tr[k % 2]
                k += 1
                for c4 in range(4):
                    c = half * 4 + c4
                    S.op("pe", lambda e: e.transpose(pt[:, c4, :], h_[:, c * 128:(c + 1) * 128], self.identf[:]),
                         reads=[bh, self.bident], writes=[bpt], sig=(c4 == 3))
                S.op("act", lambda e: e.copy(out=hT[:, half * 4:half * 4 + 4, 1 + s * 128:1 + (s + 1) * 128], in_=pt[:]),
                     reads=[bpt], writes=[bhT])
        for c in range(8):
            xx, bxx = self.xx[c % 2], self.bxx[c % 2]
            S.op("dve", lambda e: e.tensor_tensor(out=xx[:], in0=hT[:, c, 0:G], in1=hT[:, c, 1:G + 1], op=ALU.subtract),
                 reads=[bhT], writes=[bxx])
            for n, i in enumerate(self.mix_ids):
                S.op("dve", lambda e: e.scalar_tensor_tensor(out=self.xm[i][:, c, :], in0=xx[:], scalar=self.mixc[:, i, c:c + 1],
                                                             in1=hT[:, c, 1:G + 1], op0=ALU.mult, op1=ALU.add),
                     reads=[bxx, bhT, self.bmixc], writes=[self.bxm[i]])

    def release(self):
        for b in self.dsems:
            self.S.release(b)


def col_load(S, dst, src_row, track):
    S.dma("sp", dst, src_row.rearrange("o (c p) -> p (o c)", p=128), track, writes=[track], slow=True)


def rwkv_fm_phase(nc, S, x_in, xin_b, gain_row, mix, w0, w1, w2, a0, a1, a2, kk_, ka_, w_r, w_k,
                  RtT, KtT, AtT, BtT, WC, bouts, ntok):
    G = 512
    ng = ntok // G
    with ExitStack() as es:
        sb = lambda name, shape, dt: es.enter_context(nc.sbuf_tensor(_uniq(name), shape, dt))
        ps = lambda name, shape, dt: es.enter_context(nc.psum_tensor(_uniq(name), shape, dt))
        P = RwPrep(nc, S, es, G, (0, 1, 2, 4), x_in, xin_b, gain_row, mix)
        Wr = sb("Wr", [128, 8, D], BF16)
        Wk = sb("Wk", [128, 8, D], BF16)
        W1 = sb("W1", [128, 8, 64], BF16)
        A1 = sb("A1", [128, 8, 64], BF16)
        W2 = sb("W2", [64, D], BF16)
        A2 = sb("A2", [64, D], BF16)
        cols = sb("cols", [128, 5, 8], F32)
        bones = sb("bones", [128, 128], BF16)
        rmask = sb("rmask", [128, G], F32)
        tiny = sb("tiny", [128, 1], F32)
        tw = sb("tw", [64, G], BF16)
        ta = sb("ta", [64, G], BF16)
        names = ["sgu", "av", "kk0", "lnk", "rk", "kkn", "t1", "kp", "csg", "eW", "eWi", "t2"]
        T2 = {n: [sb(n + "_%d" % i, [128, G], F32) for i in range(2)] for n in names}
        sqk2 = [sb("sqk%d" % i, [128, G], BF16) for i in range(2)]
        wc = [sb("wc%d" % i, [128, G // 64], F32) for i in range(2)]
        ob = [sb("ob%d" % i, [128, G], BF16) for i in range(8)]
        p_r = ps("p_r", [128, G], F32)
        p_k = ps("p_k", [128, G], F32)
        p_u = ps("p_u", [128, G], F32)
        p_a = ps("p_a", [128, G], F32)
        p_ss = ps("p_ss", [128, G], F32)
        p_t = ps("p_t", [128, G], F32)
        bW = S.dbuf("W")
        bcols = S.dbuf("cols")
        bwc = [S.dbuf("wc") for _ in range(2)]
        bob = [S.dbuf("ob") for _ in range(8)]
        B2 = {n: [S.buf(n) for _ in range(2)] for n in names + ["sqk"]}
        B = {n: S.buf(n) for n in ["tw", "ta", "bones", "rmask"]}
        B.update({n: S.pbuf(n) for n in ["p_r", "p_k", "p_u", "p_a", "p_ss", "p_t"]})
        for c in range(8):
            S.dma("pool", Wr[:, c, :], w_r.rearrange("(c p) f -> p c f", p=128)[:, c, :], bW, writes=[bW])
            S.dma("pool", Wk[:, c, :], w_k.rearrange("(c p) f -> p c f", p=128)[:, c, :], bW, writes=[bW])
        S.dma("pool", W1[:], w1.rearrange("(c p) f -> p c f", p=128), bW, writes=[bW])
        S.dma("pool", A1[:], a1.rearrange("(c p) f -> p c f", p=128), bW, writes=[bW])
        S.dma("pool", W2[:], w2, bW, writes=[bW])
        S.dma("pool", A2[:], a2, bW, writes=[bW])
        for n, src in enumerate((w0, a0, kk_, ka_)):
            col_load(S, cols[:, n, :], src, bcols)
        S.op("dve", lambda e: e.tensor_scalar(out=cols[:, 4, :], in0=cols[:, 3, :], scalar1=-1.0, scalar2=None, op0=ALU.mult),
             reads=[bcols], writes=[bcols])
        S.op("pool", lambda e: e.memset(bones[:], 0.0), writes=[B["bones"]])
        S.op("pool", lambda e: e.memset(bones[0:64, 0:64], 1.0), writes=[B["bones"]])
        S.op("pool", lambda e: e.memset(bones[64:128, 64:128], 1.0), writes=[B["bones"]])
        S.op("pool", lambda e: e.memset(rmask[:], 1.0), writes=[B["rmask"]])
        S.op("pool", lambda e: e.memset(rmask[:, 0:G:64], 0.0), writes=[B["rmask"]])
        S.op("pool", lambda e: e.memset(tiny[:], 1e-18), writes=[B["rmask"]])
        RB, KB, AB, BB, WCB = bouts
        no = 0
        for g in range(ng):
            P.group(g)
            t0 = g * G
            xr, xw, xk, xa = P.xm[0], P.xm[1], P.xm[2], P.xm[4]
            bxr, bxw, bxk, bxa = P.bxm[0], P.bxm[1], P.bxm[2], P.bxm[4]
            for c in range(8):
                S.op("pe", lambda e: e.matmul(p_t[0:64, :], lhsT=W1[:, c, :], rhs=xw[:, c, :], start=(c == 0), stop=(c == 7)),
                     reads=[bW, bxw], writes=[B["p_t"]], sig=(c == 7))
            S.op("act", lambda e: e.activation(out=tw[:], in_=p_t[0:64, :], func=AF.Tanh), reads=[B["p_t"]], writes=[B["tw"]])
            for c in range(8):
                S.op("pe", lambda e: e.matmul(p_t[0:64, :], lhsT=A1[:, c, :], rhs=xa[:, c, :], start=(c == 0), stop=(c == 7)),
                     reads=[bW, bxa], writes=[B["p_t"]], sig=(c == 7))
            S.op("act", lambda e: e.copy(out=ta[:], in_=p_t[0:64, :]), reads=[B["p_t"]], writes=[B["ta"]])
            for cc in range(8):
                fs = slice(cc * 128, (cc + 1) * 128)
                par = cc % 2
                T = {n: T2[n][par] for n in names}
                sqk = sqk2[par]
                for n in names + ["sqk"]:
                    B[n] = B2[n][par]
                for c in range(8):
                    S.op("pe", lambda e: e.matmul(p_k[:], lhsT=Wk[:, c, fs], rhs=xk[:, c, :], start=(c == 0), stop=(c == 7)),
                         reads=[bW, bxk], writes=[B["p_k"]], sig=(c == 7))
                S.op("pe", lambda e: e.matmul(p_u[:], lhsT=W2[:, fs], rhs=tw[:], start=True, stop=True), reads=[bW, B["tw"]], writes=[B["p_u"]])
                S.op("pe", lambda e: e.matmul(p_a[:], lhsT=A2[:, fs], rhs=ta[:], start=True, stop=True), reads=[bW, B["ta"]], writes=[B["p_a"]])
                for c in range(8):
                    S.op("pe", lambda e: e.matmul(p_r[:], lhsT=Wr[:, c, fs], rhs=xr[:, c, :], start=(c == 0), stop=(c == 7)),
                         reads=[bW, bxr], writes=[B["p_r"]], sig=(c == 7))
                S.op("act", lambda e: e.activation(out=T["kk0"][:], in_=p_k[:], func=AF.Copy, scale=cols[:, 2, cc:cc + 1]),
                     reads=[B["p_k"], bcols], writes=[B["kk0"]])
                S.op("act", lambda e: e.activation(out=sqk[:], in_=p_k[:], func=AF.Square, scale=cols[:, 2, cc:cc + 1]),
                     reads=[B["p_k"], bcols], writes=[B["sqk"]])
                S.op("pe", lambda e: e.matmul(p_ss[:], lhsT=bones[:], rhs=sqk[:], start=True, stop=True), reads=[B["bones"], B["sqk"]],
                     writes=[B["p_ss"]])
                S.op("act", lambda e: e.activation(out=T["sgu"][:], in_=p_u[:], func=AF.Sigmoid, bias=cols[:, 0, cc:cc + 1]),
                     reads=[B["p_u"], bcols], writes=[B["sgu"]])
                S.op("act", lambda e: e.activation(out=T["av"][:], in_=p_a[:], func=AF.Sigmoid, bias=cols[:, 1, cc:cc + 1]),
                     reads=[B["p_a"], bcols], writes=[B["av"]])
                S.op("act", lambda e: e.activation(out=T["lnk"][:], in_=p_ss[:], func=AF.Ln, bias=tiny[:]), reads=[B["p_ss"], B["rmask"]],
                     writes=[B["lnk"]])
                S.op("act", lambda e: e.activation(out=T["rk"][:], in_=T["lnk"][:], func=AF.Exp, scale=-0.5), reads=[B["lnk"]], writes=[B["rk"]])
                S.op("dve", lambda e: e.tensor_tensor_scan(out=T["csg"][:], data0=rmask[:], data1=T["sgu"][:], initial=0.0,
                                                           op0=ALU.mult, op1=ALU.add), reads=[B["rmask"], B["sgu"]], writes=[B["csg"]])
                S.op("act", lambda e: e.activation(out=T["eW"][:], in_=T["csg"][:], func=AF.Exp, scale=-C0), reads=[B["csg"]], writes=[B["eW"]])
                S.op("act", lambda e: e.activation(out=T["eWi"][:], in_=T["csg"][:], func=AF.Exp, scale=C0), reads=[B["csg"]], writes=[B["eWi"]])
                S.op("act", lambda e: e.activation(out=T["t1"][:], in_=T["av"][:], func=AF.Identity, scale=cols[:, 3, cc:cc + 1],
                                                   bias=cols[:, 4, cc:cc + 1]), reads=[B["av"], bcols], writes=[B["t1"]])
                S.op("dve", lambda e: e.tensor_tensor(out=T["kkn"][:], in0=T["kk0"][:], in1=T["rk"][:], op=ALU.mult),
                     reads=[B["kk0"], B["rk"]], writes=[B["kkn"]])
                S.op("dve", lambda e: e.scalar_tensor_tensor(out=T["kp"][:], in0=T["t1"][:], scalar=1.0, in1=p_k[:], op0=ALU.add, op1=ALU.mult),
                     reads=[B["t1"], B["p_k"]], writes=[B["kp"]])
                S.op("dve", lambda e: e.tensor_tensor(out=T["t2"][:], in0=T["kkn"][:], in1=T["av"][:], op=ALU.mult),
                     reads=[B["kkn"], B["av"]], writes=[B["t2"]])
                outs = []
                o, bo = ob[no % 8], bob[no % 8]; no += 1
                S.op("dve", lambda e: e.tensor_tensor(out=o[:], in0=p_r[:], in1=T["eW"][:], op=ALU.mult), reads=[B["p_r"], B["eW"]], writes=[bo])
                outs.append((o, bo, RtT, RB))
                o, bo = ob[no % 8], bob[no % 8]; no += 1
                S.op("dve", lambda e: e.tensor_tensor(out=o[:], in0=T["kp"][:], in1=T["eWi"][:], op=ALU.mult), reads=[B["kp"], B["eWi"]], writes=[bo])
                outs.append((o, bo, KtT, KB))
                o, bo = ob[no % 8], bob[no % 8]; no += 1
                v3 = lambda t_: t_[:].rearrange("p (c j) -> p c j", j=64)
                S.op("dve", lambda e: e.scalar_tensor_tensor(out=v3(o)[:, :, 1:64], in0=v3(T["kkn"])[:, :, 1:64], scalar=-1.0,
                                                             in1=v3(T["eW"])[:, :, 0:63], op0=ALU.mult, op1=ALU.mult),
                     reads=[B["kkn"], B["eW"]], writes=[bo])
                S.op("dve", lambda e: e.tensor_scalar(out=v3(o)[:, :, 0:1], in0=v3(T["kkn"])[:, :, 0:1], scalar1=-1.0, scalar2=None, op0=ALU.mult),
                     reads=[B["kkn"]], writes=[bo])
                outs.append((o, bo, AtT, AB))
                o, bo = ob[no % 8], bob[no % 8]; no += 1
                S.op("dve", lambda e: e.tensor_tensor(out=o[:], in0=T["t2"][:], in1=T["eWi"][:], op=ALU.mult), reads=[B["t2"], B["eWi"]], writes=[bo])
                outs.append((o, bo, BtT, BB))
                for (o, bo, dst, bdst) in outs:
                    S.dma("sp", dst[fs, t0:t0 + G], o[:], bo, reads=[bo], writes=[bdst])
                w_, bw_ = wc[cc % 2], bwc[cc % 2]
                S.op("dve", lambda e: e.tensor_copy(out=w_[:], in_=T["eW"][:, 63:G:64]), reads=[B["eW"]], writes=[bw_])
                S.dma("sp", WC[fs, g * (G // 64):(g + 1) * (G // 64)], w_[:], bw_, reads=[bw_], writes=[WCB])
        S.barrier()
        P.release()
        for b in [bW, bcols] + bwc + bob:
            S.release(b)


def rwkv_tm_phase(nc, S, x_in, xin_b, gain_row, mix, a0, a1, a2, g1, g2, ka_, rk_, w_r, w_k, w_v,
                  Vtok, BV, Gt, bouts, ntok):
    G = 256
    ng = ntok // G
    with ExitStack() as es:
        sb = lambda name, shape, dt: es.enter_context(nc.sbuf_tensor(_uniq(name), shape, dt))
        ps = lambda name, shape, dt: es.enter_context(nc.psum_tensor(_uniq(name), shape, dt))
        P = RwPrep(nc, S, es, G, (0, 2, 3, 4, 5), x_in, xin_b, gain_row, mix)
        Wr = sb("Wr", [128, 8, D], BF16)
        Wk = sb("Wk", [128, 8, D], BF16)
        Wv = sb("Wv", [128, 8, D], BF16)
        A1 = sb("A1", [128, 8, 64], BF16)
        A2 = sb("A2", [64, D], BF16)
        G1 = sb("G1", [128, 8, 160], BF16)
        G2a = sb("G2a", [128, D], BF16)
        G2b = sb("G2b", [32, D], BF16)
        a0B = sb("a0B", [128, D], F32)
        kaB = sb("kaB", [128, D], F32)
        rkB = sb("rkB", [128, D], F32)
        ta = sb("ta", [64, G], BF16)
        sg1a = sb("sg1a", [128, G], BF16)
        sg1b = sb("sg1b", [32, G], BF16)
        tmp = sb("tmp", [128, D], F32)
        av = sb("av", [128, D], F32)
        t1 = sb("t1", [128, D], F32)
        kp = sb("kp", [128, D], F32)
        tmp2 = sb("tmp2", [128, D], F32)
        tmp3 = sb("tmp3", [128, 16, 64], F32)
        bsum = sb("bsum", [128, 16, 1], F32)
        bvo = [sb("bvo%d" % i, [128, 16, 64], F32) for i in range(2)]
        vto = [sb("vto%d" % i, [128, D], BF16) for i in range(2)]
        gto = [sb("gto%d" % i, [128, D], F32) for i in range(2)]
        p_t = ps("p_t", [128, G], F32)
        pp = [ps("pp%d" % i, [128, 2, 512], F32) for i in range(2)]
        bW, bB = S.dbuf("W"), S.dbuf("B")
        bbvo = [S.dbuf("bvo") for _ in range(2)]
        bvto = [S.dbuf("vto") for _ in range(2)]
        bgto = [S.dbuf("gto") for _ in range(2)]
        B = {n: S.buf(n) for n in ["ta", "sg1a", "sg1b", "tmp", "av", "t1", "kp", "tmp2", "tmp3", "bsum"]}
        B.update({n: S.pbuf(n) for n in ["p_t", "pp0", "pp1"]})
        bpp = [B["pp0"], B["pp1"]]
        for c in range(8):
            for (W_, w_) in ((Wr, w_r), (Wk, w_k), (Wv, w_v)):
                S.dma("pool", W_[:, c, :], w_.rearrange("(c p) f -> p c f", p=128)[:, c, :], bW, writes=[bW])
        S.dma("pool", A1[:], a1.rearrange("(c p) f -> p c f", p=128), bW, writes=[bW])
        S.dma("pool", G1[:], g1.rearrange("(c p) f -> p c f", p=128), bW, writes=[bW])
        S.dma("pool", A2[:], a2, bW, writes=[bW])
        S.dma("pool", G2a[:], g2[0:128, :], bW, writes=[bW])
        S.dma("pool", G2b[:], g2[128:160, :], bW, writes=[bW])
        for (t_, src) in ((a0B, a0), (kaB, ka_), (rkB, rk_)):
            S.dma("sp", t_[:], src.to_broadcast([128, D]), bB, writes=[bB])
        VB, BVB, GB = bouts
        npp = 0
        k = 0
        for g in range(ng):
            P.group(g)
            xr, xk, xv, xa, xg = P.xm[0], P.xm[2], P.xm[3], P.xm[4], P.xm[5]
            bxr, bxk, bxv, bxa, bxg = P.bxm[0], P.bxm[2], P.bxm[3], P.bxm[4], P.bxm[5]
            for c in range(8):
                S.op("pe", lambda e: e.matmul(p_t[0:64, :], lhsT=A1[:, c, :], rhs=xa[:, c, :], start=(c == 0), stop=(c == 7)),
                     reads=[bW, bxa], writes=[B["p_t"]], sig=(c == 7))
            S.op("act", lambda e: e.copy(out=ta[:], in_=p_t[0:64, :]), reads=[B["p_t"]], writes=[B["ta"]])
            for c in range(8):
                S.op("pe", lambda e: e.matmul(p_t[:, :], lhsT=G1[:, c, 0:128], rhs=xg[:, c, :], start=(c == 0), stop=(c == 7)),
                     reads=[bW, bxg], writes=[B["p_t"]], sig=(c == 7))
            S.op("act", lambda e: e.activation(out=sg1a[:], in_=p_t[:, :], func=AF.Sigmoid), reads=[B["p_t"]], writes=[B["sg1a"]])
            for c in range(8):
                S.op("pe", lambda e: e.matmul(p_t[0:32, :], lhsT=G1[:, c, 128:160], rhs=xg[:, c, :], start=(c == 0), stop=(c == 7)),
                     reads=[bW, bxg], writes=[B["p_t"]], sig=(c == 7))
            S.op("act", lambda e: e.activation(out=sg1b[:], in_=p_t[0:32, :], func=AF.Sigmoid), reads=[B["p_t"]], writes=[B["sg1b"]])
            for s in range(G // 128):
                ts = slice(s * 128, (s + 1) * 128)
                t0 = g * G + s * 128

                def big(xm_, bxm_, W_):
                    nonlocal npp
                    p, bp = pp[npp % 2], bpp[npp % 2]
                    npp += 1
                    for hh in range(2):
                        for c in range(8):
                            S.op("pe", lambda e: e.matmul(p[:, hh, :], lhsT=xm_[:, c, ts], rhs=W_[:, c, hh * 512:(hh + 1) * 512],
                                                          start=(c == 0), stop=(c == 7)), reads=[bW, bxm_], writes=[bp], sig=(c == 7))
                    return p, bp
                p, bp = pp[npp % 2], bpp[npp % 2]
                npp += 1
                for hh in range(2):
                    S.op("pe", lambda e: e.matmul(p[:, hh, :], lhsT=ta[:, ts], rhs=A2[:, hh * 512:(hh + 1) * 512], start=True, stop=True),
                         reads=[bW, B["ta"]], writes=[bp])
                S.op("dve", lambda e: e.tensor_tensor(out=tmp[:], in0=p[:].rearrange("p a b -> p (a b)"), in1=a0B[:], op=ALU.add),
                     reads=[bp, bB], writes=[B["tmp"]])
                S.op("act", lambda e: e.activation(out=av[:], in_=tmp[:], func=AF.Sigmoid), reads=[B["tmp"]], writes=[B["av"]])
                S.op("dve", lambda e: e.scalar_tensor_tensor(out=t1[:], in0=av[:], scalar=-1.0, in1=kaB[:], op0=ALU.add, op1=ALU.mult),
                     reads=[B["av"], bB], writes=[B["t1"]])
                p, bp = big(xk, bxk, Wk)
                S.op("dve", lambda e: e.scalar_tensor_tensor(out=kp[:], in0=t1[:], scalar=1.0, in1=p[:].rearrange("p a b -> p (a b)"),
                                                             op0=ALU.add, op1=ALU.mult), reads=[B["t1"], bp], writes=[B["kp"]])
                p, bp = big(xr, bxr, Wr)
                S.op("dve", lambda e: e.tensor_tensor(out=tmp2[:], in0=p[:].rearrange("p a b -> p (a b)"), in1=rkB[:], op=ALU.mult),
                     reads=[bp, bB], writes=[B["tmp2"]])
                S.op("dve", lambda e: e.tensor_tensor(out=tmp3[:].rearrange("p a b -> p (a b)"), in0=tmp2[:], in1=kp[:], op=ALU.mult),
                     reads=[B["tmp2"], B["kp"]], writes=[B["tmp3"]])
                S.op("dve", lambda e: e.tensor_reduce(out=bsum[:], in_=tmp3[:], axis=AX.X, op=ALU.add), reads=[B["tmp3"]], writes=[B["bsum"]])
                p, bp = big(xv, bxv, Wv)
                o, bo = bvo[k % 2], bbvo[k % 2]
                S.op("dve", lambda e: e.tensor_tensor(out=o[:], in0=p[:].rearrange("p a (h d) -> p (a h) d", d=64),
                                                      in1=bsum[:].to_broadcast([128, 16, 64]), op=ALU.mult), reads=[bp, B["bsum"]], writes=[bo])
                S.dma("sp", BV[t0:t0 + 128, :], o[:].rearrange("p a b -> p (a b)"), bo, reads=[bo], writes=[BVB])
                o, bo = vto[k % 2], bvto[k % 2]
                S.op("act", lambda e: e.copy(out=o[:], in_=p[:].rearrange("p a b -> p (a b)")), reads=[bp], writes=[bo])
                S.dma("sp", Vtok[t0:t0 + 128, :], o[:], bo, reads=[bo], writes=[VB])
                p, bp = pp[npp % 2], bpp[npp % 2]
                npp += 1
                for hh in range(2):
                    S.op("pe", lambda e: e.matmul(p[:, hh, :], lhsT=sg1a[:, ts], rhs=G2a[:, hh * 512:(hh + 1) * 512], start=True, stop=False),
                         reads=[bW, B["sg1a"]], writes=[bp], sig=False)
                    S.op("pe", lambda e: e.matmul(p[:, hh, :], lhsT=sg1b[:, ts], rhs=G2b[:, hh * 512:(hh + 1) * 512], start=False, stop=True),
                         reads=[bW, B["sg1b"]], writes=[bp])
                o, bo = gto[k % 2], bgto[k % 2]
                S.op("act", lambda e: e.copy(out=o[:], in_=p[:].rearrange("p a b -> p (a b)")), reads=[bp], writes=[bo])
                S.dma("sp", Gt[t0:t0 + 128, :], o[:], bo, reads=[bo], writes=[GB])
                k += 1
        S.barrier()
        P.release()
        for b in [bW, bB] + bbvo + bvto + bgto:
            S.release(b)


def rwkv_scan_phase(nc, S, RtT, KtT, AtT, BtT, WC, Vtok, Ysc, bins, bY, ntok, NI=4):
    nch = ntok // 64
    ngr = nch // 8
    with ExitStack() as es:
        sb = lambda name, shape, dt: es.enter_context(nc.sbuf_tensor(_uniq(name), shape, dt))
        ps = lambda name, shape, dt: es.enter_context(nc.psum_tensor(_uniq(name), shape, dt))
        MU = sb("MU", [128, 128], F32)
        MUI = sb("MUI", [128, 128], F32)
        ML = sb("ML", [128, 128], F32)
        I32 = sb("I32", [128, 128], F32)
        identb = sb("identb", [128, 128], BF16)
        bconst = S.buf("const")
        for (m, chm, pat, op) in ((MU, -1, 1, ALU.is_gt), (MUI, -1, 1, ALU.is_ge), (ML, 1, -1, ALU.is_gt)):
            S.op("pool", lambda e: e.memset(m[:], 1.0), writes=[bconst])
            S.op("pool", lambda e: e.affine_select(out=m[:], in_=m[:], pattern=[[pat, 128]], compare_op=op, fill=0.0, base=0,
                                                   channel_multiplier=chm), reads=[bconst], writes=[bconst])
        make_ident(nc, S, I32, bconst)
        make_ident(nc, S, identb, bconst)

        class Slot:
            pass
        slots = []
        for si in range(NI):
            s = Slot()
            n_ = lambda x: "%s_%d" % (x, si)
            s.AR = [sb(n_("AR%d" % j), [128, 8, 2, 128], BF16) for j in range(2)]
            s.Bd = [sb(n_("Bd%d" % j), [128, 8, 128], BF16) for j in range(2)]
            s.Kd = [sb(n_("Kd%d" % j), [128, 8, 128], BF16) for j in range(2)]
            s.bbd = [S.dbuf(n_("bd0")), S.dbuf(n_("bd1"))]
            s.Vs = [sb(n_("Vs%d" % j), [128, 8, 64], BF16) for j in range(2)]
            s.Yo = [sb(n_("Yo%d" % j), [128, 8, 64], F32) for j in range(2)]
            s.bYo = [S.dbuf(n_("Yo0")), S.dbuf(n_("Yo1"))]
            s.wcs = sb(n_("wcs"), [128, nch], F32)
            s.bwcs = S.dbuf(n_("wcs"))
            s.QX = [sb(n_("QX%d" % j), [128, 2, 128], BF16) for j in range(2)]
            s.P = [sb(n_("P%d" % j), [128, 128], BF16) for j in range(2)]
            s.bQX = [S.buf("QX") for _ in range(2)]
            s.bP = [S.buf("P") for _ in range(2)]
            for k in ("Mak", "Mrb", "Mrk", "BtT", "KtT"):
                setattr(s, k, [sb(n_(k) + "_%d" % q_, [128, 128], BF16) for q_ in range(2)])
                setattr(s, "b" + k, [S.buf(k) for _ in range(2)])
            s.TT = [sb(n_("TT%d" % q_), [128, 128], BF16) for q_ in range(2)]
            s.bTT = [S.buf("TT") for _ in range(2)]
            s.Xs = sb(n_("Xs"), [128, 64], BF16)
            s.Ub = sb(n_("Ub"), [128, 64], BF16)
            s.Sw = sb(n_("Sw"), [128, 64], F32)
            s.St = sb(n_("St"), [128, 64], F32)
            s.Sb = sb(n_("Sb"), [128, 64], BF16)
            s.bXs, s.bUb, s.bSw, s.bSt, s.bSb = [S.buf(k) for k in ("Xs", "Ub", "Sw", "St", "Sb")]
            s.psA = ps(n_("psA"), [128, 512], F32)
            s.psB = ps(n_("psB"), [128, 4, 128], F32)
            s.bA, s.bB = S.pbuf("bankA"), S.pbuf("bankB")
            s.ptr = s.psB[:, 3, :].bitcast(BF16)
            for j in range(2):
                S.op("pool", lambda e: e.memset(s.AR[j][:], 0.0), writes=[s.bbd[j]])
                S.op("pool", lambda e: e.memset(s.Bd[j][:], 0.0), writes=[s.bbd[j]])
                S.op("pool", lambda e: e.memset(s.Kd[j][:], 0.0), writes=[s.bbd[j]])
            slots.append(s)

        bRs, bKs, bAs, bBs, bWC, bV = bins

        def load_group(s, hp, gg):
            j = gg % 2
            t0 = gg * 512
            for h in range(2):
                r0 = hp * 128 + h * 64
                hs, cs = slice(h * 64, (h + 1) * 64), slice(h * 64, (h + 1) * 64)
                for (dst, src, bsrc) in ((s.AR[j][hs, :, 0, cs], AtT, bAs), (s.AR[j][hs, :, 1, cs], RtT, bRs),
                                         (s.Bd[j][hs, :, cs], BtT, bBs), (s.Kd[j][hs, :, cs], KtT, bKs)):
                    S.dma("sp", dst, src[r0:r0 + 64, t0:t0 + 512].rearrange("p (c j) -> p c j", j=64), s.bbd[j],
                          reads=[bsrc], writes=[s.bbd[j]])
                S.dma("sp", s.Vs[j][hs, :, :], Vtok[t0:t0 + 512, r0:r0 + 64].rearrange("(c j) v -> j c v", j=64),
                      s.bbd[j], reads=[bV], writes=[s.bbd[j]])

        def T_steps(gg, c):
            j = gg % 2
            q = (gg * 8 + c) % 2
            steps = []

            def st1():
                for s in slots:
                    AR, A, Bd, K = s.AR[j][:, c, :, :], s.AR[j][:, c, 0, :], s.Bd[j][:, c, :], s.Kd[j][:, c, :]
                    S.op("pe", lambda e: e.matmul(s.psA[:, 0:256], lhsT=Bd, rhs=AR, start=True, stop=True), reads=[s.bbd[j]], writes=[s.bA], sig=False)
                    S.op("pe", lambda e: e.matmul(s.psA[:, 256:512], lhsT=K, rhs=AR, start=True, stop=True), reads=[s.bbd[j]], writes=[s.bA])
                    S.op("pe", lambda e: e.matmul(s.psB[:, 1, :], lhsT=A, rhs=Bd, start=True, stop=True), reads=[s.bbd[j]], writes=[s.bB], sig=False)
                    S.op("pe", lambda e: e.transpose(s.ptr[:, 0:128], Bd, identb[:]), reads=[s.bbd[j], bconst], writes=[s.bB], sig=False)
                    S.op("pe", lambda e: e.transpose(s.ptr[:, 128:256], K, identb[:]), reads=[s.bbd[j], bconst], writes=[s.bB])
            steps.append(st1)

            def st2():
                for s in slots:
                    S.op("dve", lambda e: e.tensor_tensor(out=s.QX[0][:, 0, :], in0=s.psA[:, 0:128], in1=MU[:], op=ALU.mult),
                         reads=[s.bA, bconst], writes=[s.bQX[0]])
                    S.op("dve", lambda e: e.tensor_tensor(out=s.QX[1][:, 1, :], in0=s.QX[0][:, 0, :], in1=I32[:], op=ALU.add),
                         reads=[s.bQX[0], bconst], writes=[s.bQX[1]])
                    S.op("dve", lambda e: e.tensor_tensor(out=s.P[0][:], in0=s.psB[:, 1, :], in1=ML[:], op=ALU.mult),
                         reads=[s.bB, bconst], writes=[s.bP[0]])
            steps.append(st2)

            def st3():
                for s in slots:
                    for (nm, lo, msk) in (("Mrb", 128, MUI), ("Mak", 256, MU), ("Mrk", 384, MUI)):
                        S.op("dve", lambda e: e.tensor_tensor(out=getattr(s, nm)[q][:], in0=s.psA[:, lo:lo + 128], in1=msk[:], op=ALU.mult),
                             reads=[s.bA, bconst], writes=[getattr(s, "b" + nm)[q]])
                    S.op("act", lambda e: e.copy(out=s.BtT[q][:], in_=s.ptr[:, 0:128]), reads=[s.bB], writes=[s.bBtT[q]])
                    S.op("act", lambda e: e.copy(out=s.KtT[q][:], in_=s.ptr[:, 128:256]), reads=[s.bB], writes=[s.bKtT[q]])
            steps.append(st3)

            for lvl in range(6):
                cur = 0 if lvl == 0 else lvl % 2
                nxt = 1 - cur

                def sa(lvl=lvl, cur=cur):
                    for s in slots:
                        Q, X, Pm = s.QX[cur][:, 0, :], s.QX[cur][:, 1, :], s.P[cur][:]
                        rd = [s.bP[cur], s.bQX[cur]]
                        if lvl == 0:
                            S.op("pe", lambda e: e.matmul(s.psA[:, 0:128], lhsT=Pm, rhs=Q, start=True, stop=True), reads=rd, writes=[s.bA], sig=False)
                        elif lvl <= 3:
                            S.op("pe", lambda e: e.matmul(s.psA[:, 0:256], lhsT=Pm, rhs=s.QX[cur][:, :, :], start=True, stop=True),
                                 reads=rd, writes=[s.bA], sig=False)
                        else:
                            S.op("pe", lambda e: e.matmul(s.psA[:, 128:256], lhsT=Pm, rhs=X, start=True, stop=True), reads=rd, writes=[s.bA],
                                 sig=(lvl == 5))
                        if lvl <= 4:
                            S.op("pe", lambda e: e.matmul(s.psA[:, 256:384], lhsT=Q, rhs=Pm, start=True, stop=True), reads=rd, writes=[s.bA])

                def sb_(lvl=lvl, cur=cur, nxt=nxt):
                    for s in slots:
                        if lvl <= 3:
                            S.op("act", lambda e: e.copy(out=s.QX[nxt][:, 0, :], in_=s.psA[:, 0:128]), reads=[s.bA], writes=[s.bQX[nxt]])
                        if lvl <= 4:
                            S.op("act", lambda e: e.copy(out=s.P[nxt][:], in_=s.psA[:, 256:384]), reads=[s.bA], writes=[s.bP[nxt]])
                        if 1 <= lvl <= 4:
                            S.op("dve", lambda e: e.tensor_tensor(out=s.QX[nxt][:, 1, :], in0=s.psA[:, 128:256], in1=s.QX[cur][:, 1, :], op=ALU.add),
                                 reads=[s.bA, s.bQX[cur]], writes=[s.bQX[nxt]])
                        if lvl == 5:
                            S.op("dve", lambda e: e.tensor_tensor(out=s.TT[q][:], in0=s.psA[:, 128:256], in1=s.QX[cur][:, 1, :], op=ALU.add),
                                 reads=[s.bA, s.bQX[cur]], writes=[s.bTT[q]])
                steps += [sa, sb_]
            return steps

        def S_steps(gg, c):
            j = gg % 2
            ch = gg * 8 + c
            q = ch % 2

            def s1():
                for s in slots:
                    A = s.AR[j][:, c, 0, :]
                    S.op("pe", lambda e: e.matmul(s.psB[:, 0, 0:64], lhsT=A, rhs=s.Sb[:], start=True, stop=False),
                         reads=[s.bbd[j], s.bSb], writes=[s.bB], sig=False)
                    S.op("pe", lambda e: e.matmul(s.psB[:, 0, 0:64], lhsT=s.Mak[q][:], rhs=s.Vs[j][:, c, :], start=False, stop=True),
                         reads=[s.bMak[q], s.bbd[j]], writes=[s.bB])
                    S.op("pool", lambda e: e.tensor_scalar(out=s.Sw[:], in0=s.St[:], scalar1=s.wcs[:, ch:ch + 1], scalar2=None, op0=ALU.mult),
                         reads=[s.bSt, s.bwcs], writes=[s.bSw])

            def s2():
                for s in slots:
                    S.op("act", lambda e: e.copy(out=s.Xs[:], in_=s.psB[:, 0, 0:64]), reads=[s.bB], writes=[s.bXs])

            def s3():
                for s in slots:
                    S.op("pe", lambda e: e.matmul(s.psB[:, 1, 0:64], lhsT=s.TT[q][:], rhs=s.Xs[:], start=True, stop=True),
                         reads=[s.bTT[q], s.bXs], writes=[s.bB])

            def s4():
                for s in slots:
                    S.op("act", lambda e: e.copy(out=s.Ub[:], in_=s.psB[:, 1, 0:64]), reads=[s.bB], writes=[s.bUb])

            def s5():
                for s in slots:
                    R = s.AR[j][:, c, 1, :]
                    pY, pS = s.psB[:, 2, 0:64], s.psB[:, 0, 0:64]
                    S.op("pe", lambda e: e.matmul(pY, lhsT=R, rhs=s.Sb[:], start=True, stop=False),
                         reads=[s.bbd[j], s.bSb], writes=[s.bB], sig=False)
                    S.op("pe", lambda e: e.matmul(pY, lhsT=s.Mrb[q][:], rhs=s.Ub[:], start=False, stop=False),
                         reads=[s.bMrb[q], s.bUb], writes=[s.bB], sig=False)
                    S.op("pe", lambda e: e.matmul(pY, lhsT=s.Mrk[q][:], rhs=s.Vs[j][:, c, :], start=False, stop=True),
                         reads=[s.bMrk[q], s.bbd[j]], writes=[s.bB], sig=False)
                    S.op("pe", lambda e: e.matmul(pS, lhsT=s.BtT[q][:], rhs=s.Ub[:], start=True, stop=False),
                         reads=[s.bBtT[q], s.bUb], writes=[s.bB], sig=False)
                    S.op("pe", lambda e: e.matmul(pS, lhsT=s.KtT[q][:], rhs=s.Vs[j][:, c, :], start=False, stop=True),
                         reads=[s.bKtT[q], s.bbd[j]], writes=[s.bB])

            def s6():
                for s in slots:
                    S.op("dve", lambda e: e.scalar_tensor_tensor(out=s.St[:], in0=s.psB[:, 0, 0:64], scalar=s.wcs[:, ch:ch + 1], in1=s.Sw[:],
                                                                 op0=ALU.mult, op1=ALU.add), reads=[s.bB, s.bwcs, s.bSw], writes=[s.bSt])
                    S.op("dve", lambda e: e.tensor_copy(out=s.Yo[j][:, c, :], in_=s.psB[:, 2, 0:64]), reads=[s.bB], writes=[s.bYo[j]])
                    S.op("act", lambda e: e.copy(out=s.Sb[:], in_=s.St[:]), reads=[s.bSt], writes=[s.bSb])
            return [s1, s2, s3, s4, s5, s6]

        for rnd in range(8 // NI):
            hps = [rnd * NI + i for i in range(NI)]
            for s, hp in zip(slots, hps):
                S.dma("sp", s.wcs[:], WC[hp * 128:(hp + 1) * 128, 0:nch], s.bwcs, reads=[bWC], writes=[s.bwcs])
                S.op("pool", lambda e: e.memset(s.St[:], 0.0), writes=[s.bSt])
                S.op("pool", lambda e: e.memset(s.Sb[:], 0.0), writes=[s.bSb])
                load_group(s, hp, 0)
            if ngr > 1:
                for s, hp in zip(slots, hps):
                    load_group(s, hp, 1)
            for st in T_steps(0, 0):
                st()
            for gg in range(ngr):
                j = gg % 2
                for c in range(8):
                    ch = gg * 8 + c
                    if ch + 1 < nch:
                        ng_, nc_ = (gg, c + 1) if c < 7 else (gg + 1, 0)
                        tsteps = T_steps(ng_, nc_)
                    else:
                        tsteps = []
                    ssteps = S_steps(gg, c)
                    ti = 0
                    for ss_ in ssteps:
                        for _ in range(3):
                            if ti < len(tsteps):
                                tsteps[ti]()
                                ti += 1
                        ss_()
                    while ti < len(tsteps):
                        tsteps[ti]()
                        ti += 1
                for s, hp in zip(slots, hps):
                    for h in range(2):
                        c0 = hp * 128 + h * 64
                        S.dma("sp", Ysc[gg * 512:(gg + 1) * 512, c0:c0 + 64].rearrange("(c j) v -> j c v", j=64),
                              s.Yo[j][h * 64:(h + 1) * 64, :, :], s.bYo[j], reads=[s.bYo[j]], writes=[bY])
                if gg + 2 < ngr:
                    for s, hp in zip(slots, hps):
                        load_group(s, hp, gg + 2)
        S.barrier()
        for s in slots:
            for b_ in s.bbd + s.bYo + [s.bwcs]:
                S.release(b_)


def rwkv_post_phase(nc, S, Ysc, BV, Gt, lg_row, lb_row, ZT, bins, bZT, ntok, gn_eps=64e-5):
    with ExitStack() as es:
        sb = lambda name, shape, dt: es.enter_context(nc.sbuf_tensor(_uniq(name), shape, dt))
        ps = lambda name, shape, dt: es.enter_context(nc.psum_tensor(_uniq(name), shape, dt))
        lgB = sb("lgB", [128, D], F32)
        lbB = sb("lbB", [128, D], F32)
        ident = sb("ident", [128, 128], BF16)
        nh = sb("nh", [128, 16, 1], F32)
        yt = [sb("yt%d" % i, [128, 16, 64], F32) for i in range(2)]
        bvt = [sb("bvt%d" % i, [128, D], F32) for i in range(2)]
        gt = [sb("gt%d" % i, [128, D], F32) for i in range(2)]
        sm = sb("sm", [128, 16, 1], F32)
        vr = sb("vr", [128, 16, 1], F32)
        rstd = sb("rstd", [128, 16, 1], F32)
        yc = sb("yc", [128, 16, 64], F32)
        sq = sb("sq", [128, 16, 64], F32)
        yn = sb("yn", [128, 16, 64], F32)
        y2 = sb("y2", [128, D], F32)
        zb = [sb("zb%d" % i, [128, D], BF16) for i in range(2)]
        zT = [sb("zT%d" % i, [128, 8, 512], BF16) for i in range(2)]
        p_tr = [ps("p_tr%d" % i, [128, 8, 128], BF16) for i in range(2)]
        bC = S.dbuf("C")
        byt = [S.dbuf("yt") for _ in range(2)]
        bbvt = [S.dbuf("bvt") for _ in range(2)]
        bgt = [S.dbuf("gt") for _ in range(2)]
        bzT = [S.dbuf("zT") for _ in range(2)]
        B = {n: S.buf(n) for n in ["ident", "nh", "sm", "vr", "rstd", "yc", "sq", "yn", "y2", "zb0", "zb1"]}
        bp_tr = [S.pbuf("ptr") for _ in range(2)]
        bYs, bBV, bG = bins
        make_ident(nc, S, ident, B["ident"])
        S.op("pool", lambda e: e.memset(nh[:], -0.5), writes=[B["nh"]])
        S.dma("sp", lgB[:], lg_row.to_broadcast([128, D]), bC, writes=[bC])
        S.dma("sp", lbB[:], lb_row.to_broadcast([128, D]), bC, writes=[bC])
        nt = ntok // 128
        for t in range(nt):
            i = t % 2
            t0 = t * 128
            S.dma("sp", yt[i][:].rearrange("p a b -> p (a b)"), Ysc[t0:t0 + 128, :], byt[i], reads=[bYs], writes=[byt[i]])
            S.dma("sp", bvt[i][:], BV[t0:t0 + 128, :], bbvt[i], reads=[bBV], writes=[bbvt[i]])
            S.dma("sp", gt[i][:], Gt[t0:t0 + 128, :], bgt[i], reads=[bG], writes=[bgt[i]])
            y3 = yt[i]
            S.op("dve", lambda e: e.tensor_reduce(out=sm[:], in_=y3[:], axis=AX.X, op=ALU.add), reads=[byt[i]], writes=[B["sm"]])
            S.op("dve", lambda e: e.tensor_scalar(out=sm[:], in0=sm[:], scalar1=1.0 / 64, scalar2=None, op0=ALU.mult), reads=[B["sm"]], writes=[B["sm"]])
            S.op("dve", lambda e: e.tensor_tensor(out=yc[:], in0=y3[:], in1=sm[:].to_broadcast([128, 16, 64]), op=ALU.subtract),
                 reads=[byt[i], B["sm"]], writes=[B["yc"]])
            S.op("dve", lambda e: e.tensor_tensor(out=sq[:], in0=yc[:], in1=yc[:], op=ALU.mult), reads=[B["yc"]], writes=[B["sq"]])
            S.op("dve", lambda e: e.tensor_reduce(out=vr[:], in_=sq[:], axis=AX.X, op=ALU.add), reads=[B["sq"]], writes=[B["vr"]])
            S.op("dve", lambda e: e.tensor_scalar(out=vr[:], in0=vr[:], scalar1=1.0 / 64, scalar2=gn_eps, op0=ALU.mult, op1=ALU.add),
                 reads=[B["vr"]], writes=[B["vr"]])
            S.op("pool", lambda e: e.tensor_tensor(out=rstd[:], in0=vr[:], in1=nh[:], op=ALU.pow), reads=[B["vr"], B["nh"]], writes=[B["rstd"]])
            S.op("dve", lambda e: e.tensor_tensor(out=yn[:], in0=yc[:], in1=rstd[:].to_broadcast([128, 16, 64]), op=ALU.mult),
                 reads=[B["yc"], B["rstd"]], writes=[B["yn"]])
            ynf = yn[:].rearrange("p a b -> p (a b)")
            S.op("dve", lambda e: e.tensor_tensor(out=y2[:], in0=ynf, in1=lgB[:], op=ALU.mult), reads=[B["yn"], bC], writes=[B["y2"]])
            S.op("dve", lambda e: e.tensor_tensor(out=y2[:], in0=y2[:], in1=lbB[:], op=ALU.add), reads=[B["y2"], bC], writes=[B["y2"]])
            S.op("dve", lambda e: e.tensor_tensor(out=y2[:], in0=y2[:], in1=bvt[i][:], op=ALU.add), reads=[B["y2"], bbvt[i]], writes=[B["y2"]])
            z, bz = zb[i], B["zb%d" % i]
            S.op("dve", lambda e: e.tensor_tensor(out=z[:], in0=y2[:], in1=gt[i][:], op=ALU.mult), reads=[B["y2"], bgt[i]], writes=[bz])
            pt, bpt = p_tr[i], bp_tr[i]
            for c in range(8):
                S.op("pe", lambda e: e.transpose(pt[:, c, :], z[:, c * 128:(c + 1) * 128], ident[:]), reads=[bz, B["ident"]], writes=[bpt], sig=(c == 7))
            gi = (t // 4) % 2
            S.op("act", lambda e: e.copy(out=zT[gi][:, :, (t % 4) * 128:(t % 4 + 1) * 128], in_=pt[:]), reads=[bpt], writes=[bzT[gi]])
            if t % 4 == 3:
                g0 = (t // 4) * 512
                for c in range(8):
                    S.dma("sp", ZT[c * 128:(c + 1) * 128, g0:g0 + 512], zT[gi][:, c, :], bzT[gi], reads=[bzT[gi]], writes=[bZT])
        S.barrier()
        for b in [bC] + byt + bbvt + bgt + bzT:
            S.release(b)


def build_program(ntok=SEQ):
    nc = bass.Bass("TRN2", target_bir_lowering=False)
    di = lambda n, s: nc.dram_tensor(n, list(s), F32, kind="ExternalInput").ap()
    x = di("x", [ntok, D])
    ffn_norm = di("ffn_norm", [4, D])
    wg = di("ffn_w_gate", [2, 2, D, DFF])
    wu = di("ffn_w_up", [2, 2, D, DFF])
    wd = di("ffn_w_down", [2, 2, DFF, D])
    mix_norm = di("mix_norm", [2, D])
    dbias = di("dbias", [12, 128, 2, 128])
    w_in = di("attn_w_in", [D, 3072])
    qn = di("attn_q_norm", [64, 1])
    kn = di("attn_k_norm", [64, 1])
    w_out = di("attn_w_out", [512, D])
    rw_mix = di("rw_mix", [6, D])
    rows = {n: di(n, [1, D]) for n in ("rw_w0", "rw_a0", "rw_kk", "rw_ka", "rw_rk", "rw_lnx_g", "rw_lnx_b")}
    rw_w1 = di("rw_w1", [D, 64]); rw_w2 = di("rw_w2", [64, D]); rw_a1 = di("rw_a1", [D, 64]); rw_a2 = di("rw_a2", [64, D])
    rw_g1 = di("rw_g1", [D, 160]); rw_g2 = di("rw_g2", [160, D])
    rw_wr = di("rw_wr", [D, D]); rw_wk = di("rw_wk", [D, D]); rw_wv = di("rw_wv", [D, D]); rw_wo = di("rw_wo", [D, D])
    out = nc.dram_tensor("out", [ntok, D], F32, kind="ExternalOutput").ap()
    scr = lambda n, s, dt: nc.dram_tensor(n, list(s), dt, kind="Internal").ap()
    xa = scr("xa", [ntok, D], F32); xb = scr("xb", [ntok, D], F32)
    QT = scr("QT", [D, ntok], BF16); KT = scr("KT", [D, ntok], BF16); V = scr("V", [ntok, D], BF16); MT = scr("MT", [512, ntok], BF16)
    RtT, KtT, AtT, BtT = [scr(n, [D, ntok], BF16) for n in ("RtT", "KtT", "AtT", "BtT")]
    WC = scr("WC", [D, ntok // 64], F32)
    Vtok = scr("Vtok", [ntok, D], BF16); BV = scr("BV", [ntok, D], F32); Gt = scr("Gt", [ntok, D], F32); Ysc = scr("Ysc", [ntok, D], F32)
    ZT = scr("ZT", [D, ntok], BF16)

    S = Sched(nc, n_dma_sems=24)
    nb = lambda n: S.buf(n, acc=True)
    bx, bxa, bxb, bout = nb("x"), nb("xa"), nb("xb"), nb("out")
    bQT, bKT, bV, bMT = nb("QT"), nb("KT"), nb("V"), nb("MT")
    bR, bK, bA, bB, bWC, bVt, bBV, bG, bY, bZ = [nb(n) for n in ("R", "K", "A", "B", "WC", "Vt", "BV", "G", "Y", "Z")]

    ffn_phase(nc, S, x, xa, bx, bxa, wg[0, 0], wu[0, 0], wd[0, 0], ffn_norm[0:1, :], ntok)
    attn_in_phase(nc, S, xa, bxa, w_in, mix_norm[0:1, :], qn, kn, QT, KT, V, bQT, bKT, bV, ntok)
    sb_attn_phase(nc, S, QT, KT, V, MT, bQT, bKT, bV, bMT, ntok)
    dil_attn_phase(nc, S, QT, KT, V, MT, dbias, bQT, bKT, bV, bMT, ntok)
    out_proj_phase(nc, S, xa, xb, bxa, bxb, MT, bMT, w_out, 512, ntok)
    ffn_phase(nc, S, xb, xa, bxb, bxa, wg[0, 1], wu[0, 1], wd[0, 1], ffn_norm[1:2, :], ntok)
    ffn_phase(nc, S, xa, xb, bxa, bxb, wg[1, 0], wu[1, 0], wd[1, 0], ffn_norm[2:3, :], ntok)
    rwkv_fm_phase(nc, S, xb, bxb, mix_norm[1:2, :], rw_mix, rows["rw_w0"], rw_w1, rw_w2, rows["rw_a0"], rw_a1, rw_a2,
                  rows["rw_kk"], rows["rw_ka"], rw_wr, rw_wk, RtT, KtT, AtT, BtT, WC, [bR, bK, bA, bB, bWC], ntok)
    rwkv_tm_phase(nc, S, xb, bxb, mix_norm[1:2, :], rw_mix, rows["rw_a0"], rw_a1, rw_a2, rw_g1, rw_g2, rows["rw_ka"], rows["rw_rk"],
                  rw_wr, rw_wk, rw_wv, Vtok, BV, Gt, [bVt, bBV, bG], ntok)
    rwkv_scan_phase(nc, S, RtT, KtT, AtT, BtT, WC, Vtok, Ysc, [bR, bK, bA, bB, bWC, bVt], bY, ntok)
    rwkv_post_phase(nc, S, Ysc, BV, Gt, rows["rw_lnx_g"], rows["rw_lnx_b"], ZT, [bY, bBV, bG], bZ, ntok)
    out_proj_phase(nc, S, xb, xa, bxb, bxa, ZT, bZ, rw_wo, 1024, ntok)
    ffn_phase(nc, S, xa, out, bxa, bout, wg[1, 1], wu[1, 1], wd[1, 1], ffn_norm[3:4, :], ntok)
    S.wait_for("sp", [bout])
    return nc


def kernel(x, ffn_norm, ffn_w_gate, ffn_w_up, ffn_w_down, mix_norm, rel_bias,
           attn_w_in, attn_q_norm, attn_k_norm, attn_w_out,
           rw_mix, rw_w0, rw_w1, rw_w2, rw_a0, rw_a1, rw_a2, rw_g1, rw_g2,
           rw_kk, rw_ka, rw_rk, rw_wr, rw_wk, rw_wv, rw_wo, rw_lnx_g, rw_lnx_b):
    f = lambda a: np.ascontiguousarray(np.asarray(a, dtype=np.float32))
    x = f(x)
    n = x.shape[0]
    shared = {
        "ffn_norm": f(ffn_norm).reshape(4, D), "ffn_w_gate": f(ffn_w_gate), "ffn_w_up": f(ffn_w_up), "ffn_w_down": f(ffn_w_down),
        "mix_norm": f(mix_norm), "dbias": dil_bias_host(f(rel_bias)),
        "attn_w_in": f(attn_w_in)[0], "attn_q_norm": f(attn_q_norm).reshape(64, 1), "attn_k_norm": f(attn_k_norm).reshape(64, 1),
        "attn_w_out": f(attn_w_out)[0], "rw_mix": f(rw_mix)[0],
        "rw_w0": f(rw_w0).reshape(1, D), "rw_a0": f(rw_a0).reshape(1, D), "rw_kk": f(rw_kk).reshape(1, D), "rw_ka": f(rw_ka).reshape(1, D),
        "rw_rk": f(rw_rk).reshape(1, D), "rw_lnx_g": f(rw_lnx_g).reshape(1, D), "rw_lnx_b": f(rw_lnx_b).reshape(1, D),
        "rw_w1": f(rw_w1)[0], "rw_w2": f(rw_w2)[0], "rw_a1": f(rw_a1)[0], "rw_a2": f(rw_a2)[0], "rw_g1": f(rw_g1)[0], "rw_g2": f(rw_g2)[0],
        "rw_wr": f(rw_wr)[0], "rw_wk": f(rw_wk)[0], "rw_wv": f(rw_wv)[0], "rw_wo": f(rw_wo)[0],
    }
    nc = build_program(x.shape[1])
    in_maps = [dict(shared, x=x[i]) for i in range(n)]
    res = run_bass_kernel_spmd(nc, in_maps, core_ids=list(range(n)))
    return np.stack([np.asarray(r["out"]) for r in res.results], axis=0).astype(np.float32)
```

```python
import numpy as np
from contextlib import ExitStack
import concourse.bass as bass
import concourse.mybir as mybir
from concourse.bass_utils import run_bass_kernel_spmd

F32 = mybir.dt.float32
BF16 = mybir.dt.bfloat16
AF = mybir.ActivationFunctionType
ALU = mybir.AluOpType
AX = mybir.AxisListType

D = 1024
DFF = 2816
NF = DFF // 128
SEQ = 4096


_UID = [0]


def _uniq(name):
    _UID[0] += 1
    return "%s_u%d" % (name, _UID[0])


def _merge(d, s):
    for k, v in s.items():
        if d.get(k, 0) < v:
            d[k] = v


class Buf:
    __slots__ = ("name", "wr", "rd", "acc", "dkey", "excl")

    def __init__(self, name, acc=False, excl=False):
        self.name = name
        self.wr = {}
        self.rd = {}
        self.acc = acc
        self.dkey = None
        self.excl = excl


class Sched:
    ENG = ("pe", "act", "dve", "pool", "sp")

    def __init__(self, nc, n_dma_sems=40):
        self.nc = nc
        self.eng = {"pe": nc.tensor, "act": nc.scalar, "dve": nc.vector, "pool": nc.gpsimd, "sp": nc.sync}
        self.sems = {}
        self.val = {}
        self.seen = {e: {} for e in self.ENG}
        self.epoch = 0
        self.ekey = {}
        self._new_engine_sems()
        self.dma_pool = []
        for i in range(n_dma_sems):
            k = "dma%d" % i
            self.sems[k] = nc.semaphore(k).__enter__()
            self.val[k] = 0
            self.dma_pool.append(k)
        self.nwait = 0

    def _new_engine_sems(self):
        for e in self.ENG:
            k = "%s_e%d" % (e, self.epoch)
            self.sems[k] = self.nc.semaphore(k).__enter__()
            self.val[k] = 0
            self.ekey[e] = k

    def buf(self, name, acc=False):
        return Buf(name, acc)

    def pbuf(self, name):
        return Buf(name, False, True)

    def dbuf(self, name, acc=False):
        b = Buf(name, acc)
        b.dkey = self.dma_pool.pop()
        return b

    def release(self, b):
        self.dma_pool.append(b.dkey)
        b.dkey = None

    def _wait(self, e, deps):
        for k, v in deps.items():
            if v <= 0:
                continue
            if e == "pe" and k == self.ekey["pe"]:
                continue
            if self.seen[e].get(k, 0) < v:
                self.eng[e].wait_ge(self.sems[k], v)
                self.seen[e][k] = v
                self.nwait += 1

    def _deps(self, reads, writes, e=None):
        deps = {}
        for b in reads:
            _merge(deps, b.wr)
            if b.excl:
                own = self.ekey.get(e)
                _merge(deps, {k: v for k, v in b.rd.items() if k != own})
        for b in writes:
            _merge(deps, b.wr)
            _merge(deps, b.rd)
        return deps

    def _record(self, ev, reads, writes):
        for b in reads:
            _merge(b.rd, ev)
        for b in writes:
            if b.acc:
                _merge(b.wr, ev)
            else:
                b.wr = dict(ev)
                b.rd = {}

    def op(self, e, fn, reads=(), writes=(), sig=True):
        self._wait(e, self._deps(reads, writes, e))
        ins = fn(self.eng[e])
        k = self.ekey[e]
        if sig:
            ins.then_inc(self.sems[k], 1)
            self.val[k] += 1
            v = self.val[k]
        else:
            v = self.val[k] + 1
        self._record({k: v}, reads, writes)
        return ins

    def dma(self, q, out, in_, track, reads=(), writes=(), slow=False):
        self._wait(q, self._deps(reads, writes))
        if slow:
            ins = self.eng[q].dma_start(out=out, in_=in_, allow_slow_non_contiguous=True)
        else:
            ins = self.eng[q].dma_start(out=out, in_=in_)
        k = track.dkey
        ins.then_inc(self.sems[k], 16)
        self.val[k] += 16
        self._record({k: self.val[k]}, reads, writes)
        return ins

    def barrier(self, new_epoch=True):
        allv = {k: v for k, v in self.val.items() if v > 0}
        for e in self.ENG:
            self._wait(e, allv)
        if new_epoch:
            self.epoch += 1
            self._new_engine_sems()

    def wait_for(self, e, bufs):
        deps = {}
        for b in bufs:
            _merge(deps, b.wr)
            _merge(deps, b.rd)
        self._wait(e, deps)


def ffn_phase(nc, S, x_in, x_out, xin_b, xout_b, wg, wu, wd, gain_row, ntok, eps=1e-6):
    G = 256
    ng = ntok // G
    with ExitStack() as es:
        sb = lambda name, shape, dt: es.enter_context(nc.sbuf_tensor(_uniq(name), shape, dt))
        ps = lambda name, shape, dt: es.enter_context(nc.psum_tensor(_uniq(name), shape, dt))
        Wg = sb("Wg", [128, 8, DFF], BF16)
        Wu = sb("Wu", [128, 8, DFF], BF16)
        Wd = sb("Wd", [128, NF, D], BF16)
        gB = sb("gB", [128, D], F32)
        ident = sb("ident", [128, 128], BF16)
        xt = [sb("xt%d" % i, [128, D], F32) for i in range(4)]
        ot = [sb("ot%d" % i, [128, D], F32) for i in range(2)]
        hb = [sb("hb%d" % i, [128, D], BF16) for i in range(2)]
        hT = [sb("hT%d" % i, [128, 8, G], BF16) for i in range(2)]
        aT = [sb("aT%d" % i, [128, G], BF16) for i in range(3)]
        sg = [sb("sg%d" % i, [128, G], F32) for i in range(2)]
        junk = sb("junk", [128, D], BF16)
        ss = [sb("ss%d" % i, [128, 1], F32) for i in range(2)]
        rs = [sb("rs%d" % i, [128, 1], F32) for i in range(2)]
        nh = sb("nh", [128, 1], F32)
        p_gu = [ps("p_gu%d" % i, [128, 2, G], F32) for i in range(2)]
        p_dn = [ps("p_dn%d" % i, [128, 512], F32) for i in range(4)]
        p_tr = [ps("p_tr%d" % i, [128, 8, 128], BF16) for i in range(2)]

        FB = [(0, 6), (6, 12), (12, 17), (17, 22)]
        blk_of = {}
        for bi, (j0, j1) in enumerate(FB):
            for j in range(j0, j1):
                blk_of[j] = bi
        bWgL = [S.dbuf("Wg") for _ in FB]
        bWuL = [S.dbuf("Wu") for _ in FB]
        bWdL = [S.dbuf("Wd") for _ in FB]
        bgB = S.dbuf("gB")
        bxt = [S.dbuf("xt%d" % i) for i in range(4)]
        bot = [S.dbuf("ot%d" % i) for i in range(2)]
        bhb = [S.buf("hb") for _ in range(2)]
        bhT = [S.buf("hT") for _ in range(2)]
        baT = [S.buf("aT") for _ in range(3)]
        bsg = [S.buf("sg") for _ in range(2)]
        bjunk = S.buf("junk")
        bss = [S.buf("ss") for _ in range(2)]
        brs = [S.buf("rs") for _ in range(2)]
        bnh, bident = S.buf("nh"), S.buf("ident")
        bp_gu = [S.pbuf("pgu") for _ in range(2)]
        bp_dn = [S.pbuf("pdn") for _ in range(4)]
        bp_tr = [S.pbuf("ptr") for _ in range(2)]

        S.op("pool", lambda e: e.memset(nh[:], -0.5), writes=[bnh])
        S.op("pool", lambda e: e.memset(ident[:], 0.0), writes=[bident])
        S.op("pool", lambda e: e.affine_select(out=ident[:], in_=ident[:], pattern=[[-1, 128]],
                                               compare_op=ALU.not_equal, fill=1.0, base=0,
                                               channel_multiplier=1), reads=[bident], writes=[bident])
        S.dma("sp", gB[:], gain_row.to_broadcast([128, D]), bgB, writes=[bgB])
        wg_v = wg.rearrange("(c p) f -> p c f", p=128)
        wu_v = wu.rearrange("(c p) f -> p c f", p=128)
        wd_v = wd.rearrange("(c p) f -> p c f", p=128)
        for bi, (j0, j1) in enumerate(FB):
            f0, f1 = j0 * 128, j1 * 128
            S.dma("pool", Wg[:, :, f0:f1], wg_v[:, :, f0:f1], bWgL[bi], writes=[bWgL[bi]])
            S.dma("pool", Wu[:, :, f0:f1], wu_v[:, :, f0:f1], bWuL[bi], writes=[bWuL[bi]])
            S.dma("pool", Wd[:, j0:j1, :], wd_v[:, j0:j1, :], bWdL[bi], writes=[bWdL[bi]])

        def load(g):
            for s in range(2):
                i = (g % 2) * 2 + s
                t0 = g * G + s * 128
                S.dma("sp", xt[i][:], x_in[t0:t0 + 128, :], bxt[i], reads=[xin_b], writes=[bxt[i]])

        def prep_dve(g):
            for s in range(2):
                i = (g % 2) * 2 + s
                S.op("dve", lambda e: e.scalar_tensor_tensor(out=junk[:], in0=xt[i][:], scalar=1.0, in1=xt[i][:],
                                                             op0=ALU.mult, op1=ALU.mult, accum_out=ss[s][:]),
                     reads=[bxt[i]], writes=[bjunk, bss[s]])
                S.op("dve", lambda e: e.tensor_scalar(out=ss[s][:], in0=ss[s][:], scalar1=1.0 / D, scalar2=eps,
                                                      op0=ALU.mult, op1=ALU.add), reads=[bss[s]], writes=[bss[s]])
                S.op("pool", lambda e: e.tensor_tensor(out=rs[s][:], in0=ss[s][:], in1=nh[:], op=ALU.pow),
                     reads=[bss[s], bnh], writes=[brs[s]])
                S.op("dve", lambda e: e.scalar_tensor_tensor(out=hb[s][:], in0=xt[i][:], scalar=rs[s][:], in1=gB[:],
                                                             op0=ALU.mult, op1=ALU.mult),
                     reads=[bxt[i], brs[s], bgB], writes=[bhb[s]])

        def prep_pe(g):
            for s in range(2):
                for c in range(8):
                    S.op("pe", lambda e: e.transpose(p_tr[s][:, c, :], hb[s][:, c * 128:(c + 1) * 128], ident[:]),
                         reads=[bhb[s], bident], writes=[bp_tr[s]], sig=(c == 7))
                S.op("act", lambda e: e.copy(out=hT[g % 2][:, :, s * 128:(s + 1) * 128], in_=p_tr[s][:]),
                     reads=[bp_tr[s]], writes=[bhT[g % 2]])

        def gate_up(g, j):
            h = hT[g % 2]
            pg = p_gu[j % 2]
            for c in range(8):
                S.op("pe", lambda e: e.matmul(pg[:, 0, :], lhsT=Wg[:, c, j * 128:(j + 1) * 128], rhs=h[:, c, :],
                                              start=(c == 0), stop=(c == 7)),
                     reads=[bWgL[blk_of[j]], bhT[g % 2]], writes=[bp_gu[j % 2]], sig=False)
            for c in range(8):
                S.op("pe", lambda e: e.matmul(pg[:, 1, :], lhsT=Wu[:, c, j * 128:(j + 1) * 128], rhs=h[:, c, :],
                                              start=(c == 0), stop=(c == 7)),
                     reads=[bWuL[blk_of[j]], bhT[g % 2]], writes=[bp_gu[j % 2]], sig=(c == 7))
            S.op("act", lambda e: e.activation(out=sg[j % 2][:], in_=pg[:, 0, :], func=AF.Silu),
                 reads=[bp_gu[j % 2]], writes=[bsg[j % 2]])
            S.op("dve", lambda e: e.tensor_tensor(out=aT[j % 3][:], in0=sg[j % 2][:], in1=pg[:, 1, :], op=ALU.mult),
                 reads=[bsg[j % 2], bp_gu[j % 2]], writes=[baT[j % 3]])

        def down(g, j):
            for s in range(2):
                for hh in range(2):
                    S.op("pe", lambda e: e.matmul(p_dn[s * 2 + hh][:], lhsT=aT[j % 3][:, s * 128:(s + 1) * 128],
                                                  rhs=Wd[:, j, hh * 512:(hh + 1) * 512],
                                                  start=(j == 0), stop=(j == NF - 1)),
                         reads=[baT[j % 3], bWdL[blk_of[j]]], writes=[bp_dn[s * 2 + hh]], sig=(j == NF - 1 or (s == 1 and hh == 1)))

        def epilogue(g):
            for s in range(2):
                i = (g % 2) * 2 + s
                for hh in range(2):
                    S.op("dve", lambda e: e.scalar_tensor_tensor(out=ot[s][:, hh * 512:(hh + 1) * 512], in0=p_dn[s * 2 + hh][:],
                                                                 scalar=0.5, in1=xt[i][:, hh * 512:(hh + 1) * 512],
                                                                 op0=ALU.mult, op1=ALU.add),
                         reads=[bp_dn[s * 2 + hh], bxt[i]], writes=[bot[s]])
                t0 = g * G + s * 128
                S.dma("sp", x_out[t0:t0 + 128, :], ot[s][:], bot[s], reads=[bot[s]], writes=[xout_b])

        load(0)
        if ng > 1:
            load(1)
        prep_dve(0)
        prep_pe(0)
        for g in range(ng):
            gate_up(g, 0)
            for j in range(NF):
                if j + 1 < NF:
                    gate_up(g, j + 1)
                elif g + 1 < ng:
                    pass
                down(g, j)
                if j == 3 and g + 1 < ng:
                    prep_dve(g + 1)
                if j == 14 and g + 1 < ng:
                    prep_pe(g + 1)
            epilogue(g)
            if g + 2 < ng:
                load(g + 2)
        S.barrier()
        for b in bWgL + bWuL + bWdL + [bgB] + bxt + bot:
            S.release(b)


def make_ident(nc, S, ident, bident, dt_is_bf16=True):
    S.op("pool", lambda e: e.memset(ident[:], 0.0), writes=[bident])
    S.op("pool", lambda e: e.affine_select(out=ident[:], in_=ident[:], pattern=[[-1, 128]],
                                           compare_op=ALU.not_equal, fill=1.0, base=0,
                                           channel_multiplier=1), reads=[bident], writes=[bident])


def attn_in_phase(nc, S, x_in, xin_b, w_in, gain_row, qn, kn, QT, KT, V, bQT, bKT, bV, ntok, eps=1e-6):
    G = 512
    ng = ntok // G
    with ExitStack() as es:
        sb = lambda name, shape, dt: es.enter_context(nc.sbuf_tensor(_uniq(name), shape, dt))
        ps = lambda name, shape, dt: es.enter_context(nc.psum_tensor(_uniq(name), shape, dt))
        Win = sb("Win", [128, 8, 3072], BF16)
        gB = sb("gB", [128, D], F32)
        ident = sb("ident", [128, 128], BF16)
        bones = sb("bones", [128, 128], BF16)
        gq = sb("gq", [128, 1], F32)
        gk = sb("gk", [128, 1], F32)
        nh = sb("nh", [128, 1], F32)
        eps_ap = sb("eps_ap", [128, 1], F32)
        xt = [sb("xt%d" % i, [128, D], F32) for i in range(4)]
        hb = [sb("hb%d" % i, [128, D], BF16) for i in range(2)]
        hT = [sb("hT%d" % i, [128, 8, G], BF16) for i in range(2)]
        junk = sb("junk", [128, D], BF16)
        ss = [sb("ss%d" % i, [128, 1], F32) for i in range(2)]
        rs = [sb("rs%d" % i, [128, 1], F32) for i in range(2)]
        ob = [sb("ob%d" % i, [128, G], BF16) for i in range(3)]
        sq = [sb("sq%d" % i, [128, G], BF16) for i in range(2)]
        lt = [sb("lt%d" % i, [128, G], F32) for i in range(2)]
        rr = [sb("rr%d" % i, [128, G], F32) for i in range(2)]
        vb = [sb("vb%d" % i, [128, D], BF16) for i in range(2)]
        p_q = [ps("p_q%d" % i, [128, G], F32) for i in range(3)]
        p_s = [ps("p_s%d" % i, [128, G], F32) for i in range(1)]
        p_v = [ps("p_v%d" % i, [128, 512], F32) for i in range(2)]
        p_tr = [ps("p_tr%d" % i, [128, 8, 128], BF16) for i in range(2)]

        bWin, bgB, bgq, bgk = S.dbuf("Win"), S.dbuf("gB"), S.dbuf("gq"), S.dbuf("gk")
        bxt = [S.dbuf("xt") for _ in range(4)]
        bob = [S.dbuf("ob") for _ in range(3)]
        bvb = [S.dbuf("vb") for _ in range(2)]
        bhb = [S.buf("hb") for _ in range(2)]
        bhT = [S.buf("hT") for _ in range(2)]
        bjunk, bnh, bident, bbones = S.buf("junk"), S.buf("nh"), S.buf("ident"), S.buf("bones")
        bss = [S.buf("ss") for _ in range(2)]
        brs = [S.buf("rs") for _ in range(2)]
        bsq = [S.buf("sq") for _ in range(2)]
        blt = [S.buf("lt") for _ in range(2)]
        brr = [S.buf("rr") for _ in range(2)]
        bp_q = [S.pbuf("pq") for _ in range(3)]
        bp_s = [S.pbuf("ps") for _ in range(1)]
        bp_v = [S.pbuf("pv") for _ in range(2)]
        bp_tr = [S.pbuf("ptr") for _ in range(2)]

        S.op("pool", lambda e: e.memset(nh[:], -0.5), writes=[bnh])
        S.op("pool", lambda e: e.memset(eps_ap[:], eps), writes=[bnh])
        make_ident(nc, S, ident, bident)
        S.op("pool", lambda e: e.memset(bones[:], 0.0), writes=[bbones])
        S.op("pool", lambda e: e.memset(bones[0:64, 0:64], 1.0), writes=[bbones])
        S.op("pool", lambda e: e.memset(bones[64:128, 64:128], 1.0), writes=[bbones])
        S.dma("sp", gB[:], gain_row.to_broadcast([128, D]), bgB, writes=[bgB])
        for hh in range(2):
            S.dma("sp", gq[hh * 64:(hh + 1) * 64, :], qn, bgq, writes=[bgq])
            S.dma("sp", gk[hh * 64:(hh + 1) * 64, :], kn, bgk, writes=[bgk])
        S.op("dve", lambda e: e.tensor_scalar(out=gq[:], in0=gq[:], scalar1=0.125, scalar2=None, op0=ALU.mult),
             reads=[bgq], writes=[bgq])
        w_v = w_in.rearrange("(c p) f -> p c f", p=128)
        for c in range(8):
            S.dma("pool", Win[:, c, :], w_v[:, c, :], bWin, writes=[bWin])

        def load(g):
            for s in range(4):
                t0 = g * G + s * 128
                S.dma("sp", xt[s][:], x_in[t0:t0 + 128, :], bxt[s], reads=[xin_b], writes=[bxt[s]])

        def prep(g):
            for s in range(4):
                k = s % 2
                S.op("dve", lambda e: e.scalar_tensor_tensor(out=junk[:], in0=xt[s][:], scalar=1.0, in1=xt[s][:],
                                                             op0=ALU.mult, op1=ALU.mult, accum_out=ss[k][:]),
                     reads=[bxt[s]], writes=[bjunk, bss[k]])
                S.op("dve", lambda e: e.tensor_scalar(out=ss[k][:], in0=ss[k][:], scalar1=1.0 / D, scalar2=eps,
                                                      op0=ALU.mult, op1=ALU.add), reads=[bss[k]], writes=[bss[k]])
                S.op("pool", lambda e: e.tensor_tensor(out=rs[k][:], in0=ss[k][:], in1=nh[:], op=ALU.pow),
                     reads=[bss[k], bnh], writes=[brs[k]])
                S.op("dve", lambda e: e.scalar_tensor_tensor(out=hb[k][:], in0=xt[s][:], scalar=rs[k][:], in1=gB[:],
                                                             op0=ALU.mult, op1=ALU.mult),
                     reads=[bxt[s], brs[k], bgB], writes=[bhb[k]])
                for c in range(8):
                    S.op("pe", lambda e: e.transpose(p_tr[k][:, c, :], hb[k][:, c * 128:(c + 1) * 128], ident[:]),
                         reads=[bhb[k], bident], writes=[bp_tr[k]], sig=(c == 7))
                S.op("act", lambda e: e.copy(out=hT[g % 2][:, :, s * 128:(s + 1) * 128], in_=p_tr[k][:]),
                     reads=[bp_tr[k]], writes=[bhT[g % 2]])

        nob = 0

        def _dummy():
            pass
        load(0)
        prep(0)
        for g in range(ng):
            h = hT[g % 2]
            bh = bhT[g % 2]
            t0 = g * G
            def fcinfo(fc):
                isq = fc < 8
                ch = fc % 8
                if ch < 2:
                    col0 = (0 if isq else 256) + ch * 128
                else:
                    col0 = (768 if isq else 1536) + (ch - 2) * 128
                return isq, ch, col0

            def main(fc):
                isq, ch, col0 = fcinfo(fc)
                pq = p_q[fc % 3]
                for c in range(8):
                    S.op("pe", lambda e: e.matmul(pq[:], lhsT=Win[:, c, col0:col0 + 128], rhs=h[:, c, :],
                                                  start=(c == 0), stop=(c == 7)),
                         reads=[bWin, bh], writes=[bp_q[fc % 3]], sig=(c == 7))

            def tail(fc):
                nonlocal nob
                isq, ch, col0 = fcinfo(fc)
                pq = p_q[fc % 3]
                bpq = bp_q[fc % 3]
                o = ob[nob % 3]
                bo = bob[nob % 3]
                nob += 1
                if ch < 2:
                    S.op("act", lambda e: e.activation(out=o[:], in_=pq[:], func=AF.Copy, scale=(0.125 if isq else 1.0)),
                         reads=[bpq], writes=[bo])
                else:
                    k = fc % 2
                    S.op("act", lambda e: e.activation(out=sq[k][:], in_=pq[:], func=AF.Square),
                         reads=[bpq], writes=[bsq[k]])
                    S.op("pe", lambda e: e.matmul(p_s[0][:], lhsT=bones[:], rhs=sq[k][:], start=True, stop=True),
                         reads=[bbones, bsq[k]], writes=[bp_s[0]])
                    S.op("act", lambda e: e.activation(out=lt[k][:], in_=p_s[0][:], func=AF.Ln, scale=1.0 / 64, bias=eps_ap[:]),
                         reads=[bp_s[0], bnh], writes=[blt[k]])
                    S.op("act", lambda e: e.activation(out=rr[k][:], in_=lt[k][:], func=AF.Exp, scale=-0.5),
                         reads=[blt[k]], writes=[brr[k]])
                    gcol = gq if isq else gk
                    S.op("dve", lambda e: e.scalar_tensor_tensor(out=o[:], in0=pq[:], scalar=gcol[:], in1=rr[k][:],
                                                                 op0=ALU.mult, op1=ALU.mult),
                         reads=[bpq, brr[k], bgq, bgk], writes=[bo])
                dst, bdst = (QT, bQT) if isq else (KT, bKT)
                S.dma("sp", dst[ch * 128:(ch + 1) * 128, t0:t0 + G], o[:], bo, reads=[bo], writes=[bdst])

            main(0)
            for fc in range(16):
                if fc + 1 < 16:
                    main(fc + 1)
                tail(fc)
                if fc == 1 and g + 1 < ng:
                    load(g + 1)
                if fc == 6 and g + 1 < ng:
                    prep(g + 1)
            for s in range(4):
                k = s % 2
                for (pv, cols, off) in ((p_v[0], (512, 768), 0), (p_v[0], (2304, 2560), 256), (p_v[1], (2560, 3072), 0)):
                    n = cols[1] - cols[0]
                    for c in range(8):
                        S.op("pe", lambda e: e.matmul(pv[:, off:off + n], lhsT=h[:, c, s * 128:(s + 1) * 128],
                                                      rhs=Win[:, c, cols[0]:cols[1]], start=(c == 0), stop=(c == 7)),
                             reads=[bWin, bh], writes=[bp_v[0], bp_v[1]], sig=(c == 7))
                S.op("act", lambda e: e.copy(out=vb[k][:, 0:512], in_=p_v[0][:]), reads=[bp_v[0]], writes=[bvb[k]])
                S.op("dve", lambda e: e.tensor_copy(out=vb[k][:, 512:1024], in_=p_v[1][:]), reads=[bp_v[1]], writes=[bvb[k]])
                S.dma("sp", V[t0 + s * 128:t0 + (s + 1) * 128, :], vb[k][:], bvb[k], reads=[bvb[k]], writes=[bV])
        S.barrier()
        for b in [bWin, bgB, bgq, bgk] + bxt + bob + bvb:
            S.release(b)


def sb_attn_phase(nc, S, QT, KT, V, MT, bQT, bKT, bV, bMT, ntok):
    nblk = ntok // 128
    with ExitStack() as es:
        sb = lambda name, shape, dt: es.enter_context(nc.sbuf_tensor(_uniq(name), shape, dt))
        ps = lambda name, shape, dt: es.enter_context(nc.psum_tensor(_uniq(name), shape, dt))
        qT = sb("qT", [128, 2, ntok], BF16)
        kT = sb("kT", [128, 2, ntok], BF16)
        v = sb("v", [128, nblk, 256], BF16)
        ones = sb("ones", [128, 512], F32)
        onec = sb("onec", [128, 1], F32)
        mneg = sb("mneg", [128, 128], BF16)
        ident = sb("ident", [128, 128], BF16)
        mk2 = lambda nm, shape, dt: [[sb("%s%d_%d" % (nm, h, i), shape, dt) for i in range(2)] for h in range(2)]
        e_ = mk2("e", [128, 512], F32)
        sp_ = mk2("sp", [128, 512], F32)
        cs_ = mk2("cs", [128, 512], F32)
        lw_ = mk2("lw", [128, 512], F32)
        w_ = mk2("w", [128, 512], BF16)
        wT_ = mk2("wT", [128, 4, 128], BF16)
        oT = [sb("oT%d" % i, [128, 512], BF16) for i in range(2)]
        p_z = [[ps("p_z%d_%d" % (h, i), [128, 512], F32) for i in range(2)] for h in range(2)]
        p_w = [ps("p_w%d" % i, [128, 4, 128], BF16) for i in range(2)]
        p_o = [ps("p_o%d" % i, [128, 128], F32) for i in range(2)]

        bq, bk, bv = S.dbuf("qT"), S.dbuf("kT"), S.dbuf("v")
        boT = [S.dbuf("oT") for _ in range(2)]
        bones, bmneg, bident = S.buf("ones"), S.buf("mneg"), S.buf("ident")
        bb2 = lambda nm: [[S.buf(nm) for _ in range(2)] for _ in range(2)]
        be, bsp, bcs, blw, bw, bwT = bb2("e"), bb2("sp"), bb2("cs"), bb2("lw"), bb2("w"), bb2("wT")
        bp_z = [[S.pbuf("pz") for _ in range(2)] for _ in range(2)]
        bp_w = [S.pbuf("pw") for _ in range(2)]
        bp_o = [S.pbuf("po") for _ in range(2)]

        S.op("pool", lambda e: e.memset(ones[:], 1.0), writes=[bones])
        S.op("pool", lambda e: e.memset(onec[:], 1.0), writes=[bones])
        make_ident(nc, S, ident, bident)
        S.op("pool", lambda e: e.memset(mneg[:], 0.0), writes=[bmneg])
        S.op("pool", lambda e: e.affine_select(out=mneg[:], in_=mneg[:], pattern=[[-1, 128]], compare_op=ALU.is_gt,
                                               fill=-30000.0, base=0, channel_multiplier=1),
             reads=[bmneg], writes=[bmneg])
        for pr in range(2):
            S.dma("sp", qT[:, pr, :], QT[pr * 128:(pr + 1) * 128, 0:ntok], bq, reads=[bQT], writes=[bq])
            S.dma("sp", kT[:, pr, :], KT[pr * 128:(pr + 1) * 128, 0:ntok], bk, reads=[bKT], writes=[bk])
        S.dma("sp", v[:], V[0:ntok, 0:256].rearrange("(b p) c -> p b c", p=128), bv, reads=[bV], writes=[bv])

        steps = []
        for pr in range(2):
            for qb in range(nblk):
                chunks = [(4 * (qb // 4), qb + 1, True)]
                for c in range(qb // 4 - 1, -1, -1):
                    chunks.append((4 * c, 4 * c + 4, False))
                for ci, (b0, b1, diag) in enumerate(chunks):
                    steps.append((pr, qb, ci, b0, b1, diag, len(chunks)))
        HH = [(0, slice(0, 64)), (1, slice(64, 128))]

        def front(t):
            pr, qb, ci, b0, b1, diag, nchk = steps[t]
            W = (b1 - b0) * 128
            i = t % 2
            for (hh, P) in HH:
                pz, bpz = p_z[hh][i], bp_z[hh][i]
                S.op("pe", lambda e: e.matmul(pz[:, 0:W], lhsT=qT[P, pr, qb * 128:(qb + 1) * 128],
                                              rhs=kT[P, pr, b0 * 128:b1 * 128], start=True, stop=(not diag)),
                     reads=[bq, bk], writes=[bpz], sig=(not diag))
                if diag:
                    S.op("pe", lambda e: e.matmul(pz[:, W - 128:W], lhsT=ident[:], rhs=mneg[:], start=False, stop=True),
                         reads=[bident, bmneg], writes=[bpz])
            for (hh, P) in HH:
                S.op("act", lambda e: e.activation(out=e_[hh][i][:, 0:W], in_=p_z[hh][i][:, 0:W], func=AF.Exp),
                     reads=[bp_z[hh][i]], writes=[be[hh][i]])
            for (hh, P) in HH:
                S.op("act", lambda e: e.activation(out=sp_[hh][i][:, 0:W], in_=e_[hh][i][:, 0:W], func=AF.Ln, bias=onec[:]),
                     reads=[be[hh][i], bones], writes=[bsp[hh][i]])

        def back(t):
            pr, qb, ci, b0, b1, diag, nchk = steps[t]
            W = (b1 - b0) * 128
            nb = b1 - b0
            i = t % 2
            rev = (lambda tt: tt[:, W - 1::-1] if W < 512 else tt[:, ::-1])
            for (hh, P) in HH:
                if ci == 0:
                    init, rd = 0.0, [bsp[hh][i], bones]
                else:
                    init, rd = cs_[hh][1 - i][:, 0:1], [bsp[hh][i], bones, bcs[hh][1 - i]]
                S.op("dve", lambda e: e.tensor_tensor_scan(out=rev(cs_[hh][i]), data0=ones[:, 0:W], data1=rev(sp_[hh][i]),
                                                           initial=init, op0=ALU.mult, op1=ALU.add),
                     reads=rd, writes=[bcs[hh][i]])
            for (hh, P) in HH:
                S.op("dve", lambda e: e.tensor_tensor(out=lw_[hh][i][:, 0:W], in0=p_z[hh][i][:, 0:W], in1=cs_[hh][i][:, 0:W], op=ALU.subtract),
                     reads=[bp_z[hh][i], bcs[hh][i]], writes=[blw[hh][i]])
            for (hh, P) in HH:
                S.op("act", lambda e: e.activation(out=w_[hh][i][:, 0:W], in_=lw_[hh][i][:, 0:W], func=AF.Exp),
                     reads=[blw[hh][i]], writes=[bw[hh][i]])
            for (hh, P) in HH:
                pw, bpw = p_w[hh], bp_w[hh]
                for b in range(nb):
                    S.op("pe", lambda e: e.transpose(pw[:, b, :], w_[hh][i][:, b * 128:(b + 1) * 128], ident[:]),
                         reads=[bw[hh][i], bident], writes=[bpw], sig=(b == nb - 1))
                if hh == 0:
                    S.op("act", lambda e: e.copy(out=wT_[hh][i][:, 0:nb, :], in_=pw[:, 0:nb, :]), reads=[bpw], writes=[bwT[hh][i]])
                else:
                    S.op("dve", lambda e: e.tensor_copy(out=wT_[hh][i][:, 0:nb, :], in_=pw[:, 0:nb, :]), reads=[bpw], writes=[bwT[hh][i]])
            po, bpo = p_o[qb % 2], bp_o[qb % 2]
            for (hh, P) in HH:
                h = 2 * pr + hh
                for b in range(nb):
                    first = (ci == 0 and b == 0)
                    last = (ci == nchk - 1 and b == nb - 1)
                    S.op("pe", lambda e: e.matmul(po[P, :], lhsT=v[:, b0 + b, h * 64:(h + 1) * 64], rhs=wT_[hh][i][:, b, :],
                                                  start=first, stop=last),
                         reads=[bv, bwT[hh][i]], writes=[bpo], sig=(b == nb - 1))
            if ci == nchk - 1:
                k = (qb // 4) % 2
                S.op("dve", lambda e: e.tensor_copy(out=oT[k][:, (qb % 4) * 128:(qb % 4 + 1) * 128], in_=po[:, :]),
                     reads=[bpo], writes=[boT[k]])
                if qb % 4 == 3 or qb == nblk - 1:
                    q0 = 4 * (qb // 4)
                    n = (qb - q0 + 1) * 128
                    S.dma("sp", MT[pr * 128:(pr + 1) * 128, q0 * 128:q0 * 128 + n], oT[k][:, 0:n], boT[k],
                          reads=[boT[k]], writes=[bMT])

        front(0)
        for t in range(len(steps)):
            if t + 1 < len(steps):
                front(t + 1)
            back(t)
        S.barrier()
        for b_ in [bq, bk, bv] + boT:
            S.release(b_)


def dil_bias_host(rel_bias):
    out = np.empty((12, 128, 2, 128), np.float32)
    kj = np.arange(128)[:, None]
    q = np.arange(128)[None, :]
    for g, r in enumerate((1, 4, 16)):
        for part, dist in ((1, q - kj), (0, q + 128 - kj)):
            valid = (dist >= 0) & (dist <= 128)
            dd = np.maximum(dist, 0) * r
            d = np.maximum(dd, 1).astype(np.float32)
            large = 16 + (np.log(d / np.float32(16)) / np.float32(np.log(2048 / 16)) * np.float32(16)).astype(np.int32)
            large = np.minimum(large, 31)
            bucket = np.where(dd < 16, dd, large)
            for j in range(4):
                hd = 4 * g + j
                out[hd, :, part, :] = np.where(valid, rel_bias[bucket, hd], np.float32(-30000.0))
    return out


def dil_attn_phase(nc, S, QT, KT, V, MT, dbias, bQT, bKT, bV, bMT, ntok):
    with ExitStack() as es:
        sb = lambda name, shape, dt: es.enter_context(nc.sbuf_tensor(_uniq(name), shape, dt))
        ps = lambda name, shape, dt: es.enter_context(nc.psum_tensor(_uniq(name), shape, dt))
        qT = sb("qT", [128, 2, ntok], BF16)
        kT = sb("kT", [128, 2, ntok], BF16)
        v = sb("v", [128, ntok // 128, 256], BF16)
        bias = sb("bias", [128, 4, 256], F32)
        onesb = sb("onesb", [128, 64], BF16)
        Nacc = sb("Nacc", [128, 2, ntok], F32)
        Dacc = sb("Dacc", [128, 2, ntok], F32)
        s_ = [sb("s%d" % i, [128, 256], F32) for i in range(3)]
        pT_ = [sb("pT%d" % i, [128, 2, 128], BF16) for i in range(3)]
        ob = [sb("ob%d" % i, [128, 1024], BF16) for i in range(2)]
        p_s = [ps("p_s%d" % i, [128, 2, 128], F32) for i in range(2)]
        p_n = [ps("p_n%d" % i, [128, 128], F32) for i in range(2)]
        p_d = [ps("p_d%d" % i, [128, 128], F32) for i in range(2)]
        p_pad = [ps("p_pad%d" % i, [128, 256], F32) for i in range(0)]

        bq, bk, bv, bbias = S.dbuf("qT"), S.dbuf("kT"), S.dbuf("v"), S.dbuf("bias")
        bob = [S.dbuf("ob") for _ in range(2)]
        bones, bN, bD = S.buf("ones"), S.buf("N"), S.buf("D")
        bs = [S.buf("s") for _ in range(3)]
        bpT = [S.buf("pT") for _ in range(3)]
        bp_s = [S.pbuf("ps") for _ in range(2)]
        bp_n = [S.pbuf("pn") for _ in range(2)]
        bp_d = [S.pbuf("pd") for _ in range(2)]

        S.op("pool", lambda e: e.memset(onesb[:], 1.0), writes=[bones])
        u = 0
        for g, r in enumerate((1, 4, 16)):
            L = ntok // r
            nb = L // 128
            for pr in range(2):
                r0 = 256 + (2 * g + pr) * 128
                S.dma("sp", qT[:, pr, :], QT[r0:r0 + 128, 0:ntok], bq, reads=[bQT], writes=[bq])
                S.dma("sp", kT[:, pr, :], KT[r0:r0 + 128, 0:ntok], bk, reads=[bKT], writes=[bk])
            vsrc = V[0:ntok, 256 + g * 256:256 + (g + 1) * 256].rearrange("(n i c) f -> c i n f", i=128, c=r)
            for c in range(r):
                S.dma("sp", v[:, c * nb:(c + 1) * nb, :], vsrc[c], bv, reads=[bV], writes=[bv])
            for j in range(4):
                S.dma("sp", bias[:, j, :], dbias[4 * g + j].rearrange("k a q -> k (a q)"), bbias, writes=[bbias])
            units = []
            for j in range(4):
                for c in range(r):
                    for n in range(nb):
                        units.append((j, c, n))

            def tokf(c, nn):
                st = c + r * 128 * nn
                return slice(st, st + r * 127 + 1, r)

            def front(uu, u):
                j, c, n = units[uu]
                pr, hh = j // 2, j % 2
                P = slice(64 * hh, 64 * hh + 64)
                i = u % 3
                pss, bpss = p_s[u % 2], bp_s[u % 2]
                a0 = 0 if n > 0 else 1
                if n > 0:
                    S.op("pe", lambda e: e.matmul(pss[:, 0, :], lhsT=kT[P, pr, tokf(c, n - 1)], rhs=qT[P, pr, tokf(c, n)],
                                                  start=True, stop=True), reads=[bq, bk], writes=[bpss], sig=False)
                S.op("pe", lambda e: e.matmul(pss[:, 1, :], lhsT=kT[P, pr, tokf(c, n)], rhs=qT[P, pr, tokf(c, n)],
                                              start=True, stop=True), reads=[bq, bk], writes=[bpss])
                S.op("dve", lambda e: e.tensor_tensor(out=s_[i][:, a0 * 128:256], in0=pss[:, a0:2, :],
                                                      in1=bias[:, j, a0 * 128:256], op=ALU.add),
                     reads=[bpss, bbias], writes=[bs[i]])
                S.op("act", lambda e: e.activation(out=pT_[i][:, a0:2, :], in_=s_[i][:, a0 * 128:256], func=AF.Exp),
                     reads=[bs[i]], writes=[bpT[i]])

            def back(uu, u):
                j, c, n = units[uu]
                pr, hh = j // 2, j % 2
                P = slice(64 * hh, 64 * hh + 64)
                i = u % 3
                pn, bpn = p_n[u % 2], bp_n[u % 2]
                pd, bpd = p_d[u % 2], bp_d[u % 2]
                a0 = 0 if n > 0 else 1
                for a_ in range(a0, 2):
                    S.op("pe", lambda e: e.matmul(pn[P, :], lhsT=v[:, c * nb + n - 1 + a_, j * 64:(j + 1) * 64],
                                                  rhs=pT_[i][:, a_, :], start=(a_ == a0), stop=(a_ == 1)),
                         reads=[bv, bpT[i]], writes=[bpn], sig=(a_ == 1))
                for a_ in range(a0, 2):
                    S.op("pe", lambda e: e.matmul(pd[P, :], lhsT=onesb[:, :], rhs=pT_[i][:, a_, :],
                                                  start=(a_ == a0), stop=(a_ == 1)),
                         reads=[bones, bpT[i]], writes=[bpd], sig=(a_ == 1))
                tk = tokf(c, n)
                if g == 0:
                    S.op("act", lambda e: e.copy(out=Nacc[P, pr, tk], in_=pn[P, :]), reads=[bpn], writes=[bN])
                    S.op("dve", lambda e: e.tensor_copy(out=Dacc[P, pr, tk], in_=pd[P, :]), reads=[bpd], writes=[bD])
                else:
                    S.op("dve", lambda e: e.tensor_tensor(out=Nacc[P, pr, tk], in0=pn[P, :], in1=Nacc[P, pr, tk],
                                                          op=ALU.add), reads=[bpn, bN], writes=[bN])
                    S.op("dve", lambda e: e.tensor_tensor(out=Dacc[P, pr, tk], in0=pd[P, :], in1=Dacc[P, pr, tk],
                                                          op=ALU.add), reads=[bpd, bD], writes=[bD])

            front(0, u)
            for uu in range(len(units)):
                if uu + 1 < len(units):
                    front(uu + 1, u + 1)
                back(uu, u)
                u += 1
        k = 0
        for pr in range(2):
            for c0 in range(0, ntok, 1024):
                n = min(1024, ntok - c0)
                S.op("dve", lambda e: e.reciprocal(out=Dacc[:, pr, c0:c0 + n], in_=Dacc[:, pr, c0:c0 + n]), reads=[bD], writes=[bD])
                S.op("dve", lambda e: e.tensor_tensor(out=ob[k % 2][:, 0:n], in0=Nacc[:, pr, c0:c0 + n], in1=Dacc[:, pr, c0:c0 + n],
                                                      op=ALU.mult), reads=[bN, bD], writes=[bob[k % 2]])
                S.dma("sp", MT[256 + pr * 128:256 + (pr + 1) * 128, c0:c0 + n], ob[k % 2][:, 0:n], bob[k % 2],
                      reads=[bob[k % 2]], writes=[bMT])
                k += 1
        S.barrier()
        for b in [bq, bk, bv, bbias] + bob:
            S.release(b)


def out_proj_phase(nc, S, x_in, x_out, xin_b, xout_b, MT, bMT, w_out, kdim, ntok):
    G = 512
    ng = ntok // G
    kc = kdim // 128
    nt = ntok // 128
    with ExitStack() as es:
        sb = lambda name, shape, dt: es.enter_context(nc.sbuf_tensor(_uniq(name), shape, dt))
        ps = lambda name, shape, dt: es.enter_context(nc.psum_tensor(_uniq(name), shape, dt))
        Wo = sb("Wo", [128, kc, D], BF16)
        mT = [sb("mT%d" % i, [128, kc, G], BF16) for i in range(2)]
        xt = [sb("xt%d" % i, [128, D], F32) for i in range(4)]
        ot = [sb("ot%d" % i, [128, D], F32) for i in range(3)]
        p_y = [ps("p_y%d" % i, [128, 512], F32) for i in range(4)]
        bWo = S.dbuf("Wo")
        bmT = [S.dbuf("mT") for _ in range(2)]
        bxt = [S.dbuf("xt") for _ in range(4)]
        bot = [S.dbuf("ot") for _ in range(3)]
        bp_y = [S.pbuf("py") for _ in range(4)]
        w_v = w_out.rearrange("(c p) f -> p c f", p=128)
        for c in range(kc):
            S.dma("pool", Wo[:, c, :], w_v[:, c, :], bWo, writes=[bWo])

        def load_m(g):
            m, bm = mT[g % 2], bmT[g % 2]
            for c in range(kc):
                S.dma("sp", m[:, c, :], MT[c * 128:(c + 1) * 128, g * G:(g + 1) * G], bm, reads=[bMT], writes=[bm])

        def load_x(k):
            S.dma("sp", xt[k % 4][:], x_in[k * 128:(k + 1) * 128, :], bxt[k % 4], reads=[xin_b], writes=[bxt[k % 4]])

        load_m(0)
        load_x(0)
        load_x(1)
        if ng > 1:
            load_m(1)
        for k in range(nt):
            g, s = k // 4, k % 4
            m, bm = mT[g % 2], bmT[g % 2]
            if k + 2 < nt:
                load_x(k + 2)
            x_, bx = xt[k % 4], bxt[k % 4]
            o_, bo = ot[k % 3], bot[k % 3]
            for hh in range(2):
                py, bpy = p_y[(2 * k + hh) % 4], bp_y[(2 * k + hh) % 4]
                for c in range(kc):
                    S.op("pe", lambda e: e.matmul(py[:], lhsT=m[:, c, s * 128:(s + 1) * 128], rhs=Wo[:, c, hh * 512:(hh + 1) * 512],
                                                  start=(c == 0), stop=(c == kc - 1)),
                         reads=[bm, bWo], writes=[bpy], sig=(c == kc - 1))
                S.op("dve", lambda e: e.tensor_tensor(out=o_[:, hh * 512:(hh + 1) * 512], in0=py[:], in1=x_[:, hh * 512:(hh + 1) * 512],
                                                      op=ALU.add), reads=[bpy, bx], writes=[bo])
            S.dma("act", x_out[k * 128:(k + 1) * 128, :], o_[:], bo, reads=[bo], writes=[xout_b])
            if s == 3 and g + 2 < ng:
                load_m(g + 2)
        S.barrier()
        for b_ in [bWo] + bmT + bxt + bot:
            S.release(b_)


C0 = float(np.exp(-0.5))


class RwPrep:
    def __init__(self, nc, S, es, G, mix_ids, x_in, xin_b, gain_row, mix, eps=1e-6):
        sb = lambda name, shape, dt: es.enter_context(nc.sbuf_tensor(_uniq(name), shape, dt))
        ps = lambda name, shape, dt: es.enter_context(nc.psum_tensor(_uniq(name), shape, dt))
        self.nc, self.S, self.G, self.mix_ids, self.x_in, self.xin_b, self.eps = nc, S, G, mix_ids, x_in, xin_b, eps
        self.gB = sb("gB", [128, D], F32)
        self.identf = sb("identf", [128, 128], F32)
        self.nh = sb("nh", [128, 1], F32)
        self.mixc = sb("mixc", [128, 6, 8], F32)
        self.xt = [sb("xt%d" % i, [128, D], F32) for i in range(2)]
        self.hn = [sb("hn%d" % i, [128, D], F32) for i in range(2)]
        self.junk = sb("junk", [128, D], BF16)
        self.ss = [sb("ss%d" % i, [128, 1], F32) for i in range(2)]
        self.rs = [sb("rs%d" % i, [128, 1], F32) for i in range(2)]
        self.hT = [sb("hT%d" % i, [128, 8, G + 1], F32) for i in range(2)]
        self.xx = [sb("xx%d" % i, [128, G], F32) for i in range(2)]
        self.xm = {i: sb("xm%d" % i, [128, 8, G], BF16) for i in mix_ids}
        self.p_tr = [ps("p_tr%d" % i, [128, 4, 128], F32) for i in range(2)]
        self.bgB, self.bmixc = S.dbuf("gB"), S.dbuf("mixc")
        self.bxt = [S.dbuf("xt") for _ in range(2)]
        self.bhn = [S.buf("hn") for _ in range(2)]
        self.bident, self.bnh, self.bjunk = S.buf("identf"), S.buf("nh"), S.buf("junk")
        self.bss = [S.buf("ss") for _ in range(2)]
        self.brs = [S.buf("rs") for _ in range(2)]
        self.bhT = [S.buf("hT") for _ in range(2)]
        self.bxx = [S.buf("xx") for _ in range(2)]
        self.bxm = {i: S.buf("xm") for i in mix_ids}
        self.bp_tr = [S.pbuf("ptr") for _ in range(2)]
        S.op("pool", lambda e: e.memset(self.nh[:], -0.5), writes=[self.bnh])
        make_ident(nc, S, self.identf, self.bident)
        S.dma("sp", self.gB[:], gain_row.to_broadcast([128, D]), self.bgB, writes=[self.bgB])
        for i in range(6):
            S.dma("sp", self.mixc[:, i, :], mix[i:i + 1, :].rearrange("o (c p) -> p (o c)", p=128), self.bmixc, writes=[self.bmixc], slow=True)
        S.op("dve", lambda e: e.memset(self.hT[1][:, :, G:G + 1], 0.0), writes=[self.bhT[1]])
        self.dsems = [self.bgB, self.bmixc] + self.bxt

    def group(self, g):
        nc, S, G = self.nc, self.S, self.G
        hT, bhT = self.hT[g % 2], self.bhT[g % 2]
        hTp, bhTp = self.hT[(g + 1) % 2], self.bhT[(g + 1) % 2]
        S.op("pool", lambda e: e.tensor_copy(out=hT[:, :, 0:1], in_=hTp[:, :, G:G + 1]), reads=[bhTp], writes=[bhT])
        k = 0
        for s in range(G // 128):
            t0 = g * G + s * 128
            x_, bx = self.xt[s % 2], self.bxt[s % 2]
            h_, bh = self.hn[s % 2], self.bhn[s % 2]
            ss, bss, rs, brs = self.ss[s % 2], self.bss[s % 2], self.rs[s % 2], self.brs[s % 2]
            S.dma("sp", x_[:], self.x_in[t0:t0 + 128, :], bx, reads=[self.xin_b], writes=[bx])
            S.op("dve", lambda e: e.scalar_tensor_tensor(out=self.junk[:], in0=x_[:], scalar=1.0, in1=x_[:], op0=ALU.mult, op1=ALU.mult,
                                                         accum_out=ss[:]), reads=[bx], writes=[self.bjunk, bss])
            S.op("dve", lambda e: e.tensor_scalar(out=ss[:], in0=ss[:], scalar1=1.0 / D, scalar2=self.eps, op0=ALU.mult, op1=ALU.add),
                 reads=[bss], writes=[bss])
            S.op("pool", lambda e: e.tensor_tensor(out=rs[:], in0=ss[:], in1=self.nh[:], op=ALU.pow), reads=[bss, self.bnh], writes=[brs])
            S.op("dve", lambda e: e.scalar_tensor_tensor(out=h_[:], in0=x_[:], scalar=rs[:], in1=self.gB[:], op0=ALU.mult, op1=ALU.mult),
                 reads=[bx, brs, self.bgB], writes=[bh])
            for half in range(2):
                pt, bpt = self.p_tr[k % 2], self.bp_tr[k % 2]
                k += 1
                for c4 in range(4):
                    c = half * 4 + c4
                    S.op("pe", lambda e: e.transpose(pt[:, c4, :], h_[:, c * 128:(c + 1) * 128], self.identf[:]),
                         reads=[bh, self.bident], writes=[bpt], sig=(c4 == 3))
                S.op("act", lambda e: e.copy(out=hT[:, half * 4:half * 4 + 4, 1 + s * 128:1 + (s + 1) * 128], in_=pt[:]),
                     reads=[bpt], writes=[bhT])
        for c in range(8):
            xx, bxx = self.xx[c % 2], self.bxx[c % 2]
            S.op("dve", lambda e: e.tensor_tensor(out=xx[:], in0=hT[:, c, 0:G], in1=hT[:, c, 1:G + 1], op=ALU.subtract),
                 reads=[bhT], writes=[bxx])
            for n, i in enumerate(self.mix_ids):
                S.op("dve", lambda e: e.scalar_tensor_tensor(out=self.xm[i][:, c, :], in0=xx[:], scalar=self.mixc[:, i, c:c + 1],
                                                             in1=hT[:, c, 1:G + 1], op0=ALU.mult, op1=ALU.add),
                     reads=[bxx, bhT, self.bmixc], writes=[self.bxm[i]])

    def release(self):
        for b in self.dsems:
            self.S.release(b)


def col_load(S, dst, src_row, track):
    S.dma("sp", dst, src_row.rearrange("o (c p) -> p (o c)", p=128), track, writes=[track], slow=True)


def rwkv_fm_phase(nc, S, x_in, xin_b, gain_row, mix, w0, w1, w2, a0, a1, a2, kk_, ka_, w_r, w_k,
                  RtT, KtT, AtT, BtT, WC, bouts, ntok):
    G = 512
    ng = ntok // G
    with ExitStack() as es:
        sb = lambda name, shape, dt: es.enter_context(nc.sbuf_tensor(_uniq(name), shape, dt))
        ps = lambda name, shape, dt: es.enter_context(nc.psum_tensor(_uniq(name), shape, dt))
        P = RwPrep(nc, S, es, G, (0, 1, 2, 4), x_in, xin_b, gain_row, mix)
        Wr = sb("Wr", [128, 8, D], BF16)
        Wk = sb("Wk", [128, 8, D], BF16)
        W1 = sb("W1", [128, 8, 64], BF16)
        A1 = sb("A1", [128, 8, 64], BF16)
        W2 = sb("W2", [64, D], BF16)
        A2 = sb("A2", [64, D], BF16)
        cols = sb("cols", [128, 5, 8], F32)
        bones = sb("bones", [128, 128], BF16)
        rmask = sb("rmask", [128, G], F32)
        tiny = sb("tiny", [128, 1], F32)
        tw = sb("tw", [64, G], BF16)
        ta = sb("ta", [64, G], BF16)
        names = ["sgu", "av", "kk0", "lnk", "rk", "kkn", "t1", "kp", "csg", "eW", "eWi", "t2"]
        T2 = {n: [sb(n + "_%d" % i, [128, G], F32) for i in range(2)] for n in names}
        sqk2 = [sb("sqk%d" % i, [128, G], BF16) for i in range(2)]
        wc = [sb("wc%d" % i, [128, G // 64], F32) for i in range(2)]
        ob = [sb("ob%d" % i, [128, G], BF16) for i in range(8)]
        p_r = ps("p_r", [128, G], F32)
        p_k = ps("p_k", [128, G], F32)
        p_u = ps("p_u", [128, G], F32)
        p_a = ps("p_a", [128, G], F32)
        p_ss = ps("p_ss", [128, G], F32)
        p_t = ps("p_t", [128, G], F32)
        bW = S.dbuf("W")
        bcols = S.dbuf("cols")
        bwc = [S.dbuf("wc") for _ in range(2)]
        bob = [S.dbuf("ob") for _ in range(8)]
        B2 = {n: [S.buf(n) for _ in range(2)] for n in names + ["sqk"]}
        B = {n: S.buf(n) for n in ["tw", "ta", "bones", "rmask"]}
        B.update({n: S.pbuf(n) for n in ["p_r", "p_k", "p_u", "p_a", "p_ss", "p_t"]})
        for c in range(8):
            S.dma("pool", Wr[:, c, :], w_r.rearrange("(c p) f -> p c f", p=128)[:, c, :], bW, writes=[bW])
            S.dma("pool", Wk[:, c, :], w_k.rearrange("(c p) f -> p c f", p=128)[:, c, :], bW, writes=[bW])
        S.dma("pool", W1[:], w1.rearrange("(c p) f -> p c f", p=128), bW, writes=[bW])
        S.dma("pool", A1[:], a1.rearrange("(c p) f -> p c f", p=128), bW, writes=[bW])
        S.dma("pool", W2[:], w2, bW, writes=[bW])
        S.dma("pool", A2[:], a2, bW, writes=[bW])
        for n, src in enumerate((w0, a0, kk_, ka_)):
            col_load(S, cols[:, n, :], src, bcols)
        S.op("dve", lambda e: e.tensor_scalar(out=cols[:, 4, :], in0=cols[:, 3, :], scalar1=-1.0, scalar2=None, op0=ALU.mult),
             reads=[bcols], writes=[bcols])
        S.op("pool", lambda e: e.memset(bones[:], 0.0), writes=[B["bones"]])
        S.op("pool", lambda e: e.memset(bones[0:64, 0:64], 1.0), writes=[B["bones"]])
        S.op("pool", lambda e: e.memset(bones[64:128, 64:128], 1.0), writes=[B["bones"]])
        S.op("pool", lambda e: e.memset(rmask[:], 1.0), writes=[B["rmask"]])
        S.op("pool", lambda e: e.memset(rmask[:, 0:G:64], 0.0), writes=[B["rmask"]])
        S.op("pool", lambda e: e.memset(tiny[:], 1e-18), writes=[B["rmask"]])
        RB, KB, AB, BB, WCB = bouts
        no = 0
        for g in range(ng):
            P.group(g)
            t0 = g * G
            xr, xw, xk, xa = P.xm[0], P.xm[1], P.xm[2], P.xm[4]
            bxr, bxw, bxk, bxa = P.bxm[0], P.bxm[1], P.bxm[2], P.bxm[4]
            for c in range(8):
                S.op("pe", lambda e: e.matmul(p_t[0:64, :], lhsT=W1[:, c, :], rhs=xw[:, c, :], start=(c == 0), stop=(c == 7)),
                     reads=[bW, bxw], writes=[B["p_t"]], sig=(c == 7))
            S.op("act", lambda e: e.activation(out=tw[:], in_=p_t[0:64, :], func=AF.Tanh), reads=[B["p_t"]], writes=[B["tw"]])
            for c in range(8):
                S.op("pe", lambda e: e.matmul(p_t[0:64, :], lhsT=A1[:, c, :], rhs=xa[:, c, :], start=(c == 0), stop=(c == 7)),
                     reads=[bW, bxa], writes=[B["p_t"]], sig=(c == 7))
            S.op("act", lambda e: e.copy(out=ta[:], in_=p_t[0:64, :]), reads=[B["p_t"]], writes=[B["ta"]])
            for cc in range(8):
                fs = slice(cc * 128, (cc + 1) * 128)
                par = cc % 2
                T = {n: T2[n][par] for n in names}
                sqk = sqk2[par]
                for n in names + ["sqk"]:
                    B[n] = B2[n][par]
                for c in range(8):
                    S.op("pe", lambda e: e.matmul(p_k[:], lhsT=Wk[:, c, fs], rhs=xk[:, c, :], start=(c == 0), stop=(c == 7)),
                         reads=[bW, bxk], writes=[B["p_k"]], sig=(c == 7))
                S.op("pe", lambda e: e.matmul(p_u[:], lhsT=W2[:, fs], rhs=tw[:], start=True, stop=True), reads=[bW, B["tw"]], writes=[B["p_u"]])
                S.op("pe", lambda e: e.matmul(p_a[:], lhsT=A2[:, fs], rhs=ta[:], start=True, stop=True), reads=[bW, B["ta"]], writes=[B["p_a"]])
                for c in range(8):
                    S.op("pe", lambda e: e.matmul(p_r[:], lhsT=Wr[:, c, fs], rhs=xr[:, c, :], start=(c == 0), stop=(c == 7)),
                         reads=[bW, bxr], writes=[B["p_r"]], sig=(c == 7))
                S.op("act", lambda e: e.activation(out=T["kk0"][:], in_=p_k[:], func=AF.Copy, scale=cols[:, 2, cc:cc + 1]),
                     reads=[B["p_k"], bcols], writes=[B["kk0"]])
                S.op("act", lambda e: e.activation(out=sqk[:], in_=p_k[:], func=AF.Square, scale=cols[:, 2, cc:cc + 1]),
                     reads=[B["p_k"], bcols], writes=[B["sqk"]])
                S.op("pe", lambda e: e.matmul(p_ss[:], lhsT=bones[:], rhs=sqk[:], start=True, stop=True), reads=[B["bones"], B["sqk"]],
                     writes=[B["p_ss"]])
                S.op("act", lambda e: e.activation(out=T["sgu"][:], in_=p_u[:], func=AF.Sigmoid, bias=cols[:, 0, cc:cc + 1]),
                     reads=[B["p_u"], bcols], writes=[B["sgu"]])
                S.op("act", lambda e: e.activation(out=T["av"][:], in_=p_a[:], func=AF.Sigmoid, bias=cols[:, 1, cc:cc + 1]),
                     reads=[B["p_a"], bcols], writes=[B["av"]])
                S.op("act", lambda e: e.activation(out=T["lnk"][:], in_=p_ss[:], func=AF.Ln, bias=tiny[:]), reads=[B["p_ss"], B["rmask"]],
                     writes=[B["lnk"]])
                S.op("act", lambda e: e.activation(out=T["rk"][:], in_=T["lnk"][:], func=AF.Exp, scale=-0.5), reads=[B["lnk"]], writes=[B["rk"]])
                S.op("dve", lambda e: e.tensor_tensor_scan(out=T["csg"][:], data0=rmask[:], data1=T["sgu"][:], initial=0.0,
                                                           op0=ALU.mult, op1=ALU.add), reads=[B["rmask"], B["sgu"]], writes=[B["csg"]])
                S.op("act", lambda e: e.activation(out=T["eW"][:], in_=T["csg"][:], func=AF.Exp, scale=-C0), reads=[B["csg"]], writes=[B["eW"]])
                S.op("act", lambda e: e.activation(out=T["eWi"][:], in_=T["csg"][:], func=AF.Exp, scale=C0), reads=[B["csg"]], writes=[B["eWi"]])
                S.op("act", lambda e: e.activation(out=T["t1"][:], in_=T["av"][:], func=AF.Identity, scale=cols[:, 3, cc:cc + 1],
                                                   bias=cols[:, 4, cc:cc + 1]), reads=[B["av"], bcols], writes=[B["t1"]])
                S.op("dve", lambda e: e.tensor_tensor(out=T["kkn"][:], in0=T["kk0"][:], in1=T["rk"][:], op=ALU.mult),
                     reads=[B["kk0"], B["rk"]], writes=[B["kkn"]])
                S.op("dve", lambda e: e.scalar_tensor_tensor(out=T["kp"][:], in0=T["t1"][:], scalar=1.0, in1=p_k[:], op0=ALU.add, op1=ALU.mult),
                     reads=[B["t1"], B["p_k"]], writes=[B["kp"]])
                S.op("dve", lambda e: e.tensor_tensor(out=T["t2"][:], in0=T["kkn"][:], in1=T["av"][:], op=ALU.mult),
                     reads=[B["kkn"], B["av"]], writes=[B["t2"]])
                outs = []
                o, bo = ob[no % 8], bob[no % 8]; no += 1
                S.op("dve", lambda e: e.tensor_tensor(out=o[:], in0=p_r[:], in1=T["eW"][:], op=ALU.mult), reads=[B["p_r"], B["eW"]], writes=[bo])
                outs.append((o, bo, RtT, RB))
                o, bo = ob[no % 8], bob[no % 8]; no += 1
                S.op("dve", lambda e: e.tensor_tensor(out=o[:], in0=T["kp"][:], in1=T["eWi"][:], op=ALU.mult), reads=[B["kp"], B["eWi"]], writes=[bo])
                outs.append((o, bo, KtT, KB))
                o, bo = ob[no % 8], bob[no % 8]; no += 1
                v3 = lambda t_: t_[:].rearrange("p (c j) -> p c j", j=64)
                S.op("dve", lambda e: e.scalar_tensor_tensor(out=v3(o)[:, :, 1:64], in0=v3(T["kkn"])[:, :, 1:64], scalar=-1.0,
                                                             in1=v3(T["eW"])[:, :, 0:63], op0=ALU.mult, op1=ALU.mult),
                     reads=[B["kkn"], B["eW"]], writes=[bo])
                S.op("dve", lambda e: e.tensor_scalar(out=v3(o)[:, :, 0:1], in0=v3(T["kkn"])[:, :, 0:1], scalar1=-1.0, scalar2=None, op0=ALU.mult),
                     reads=[B["kkn"]], writes=[bo])
                outs.append((o, bo, AtT, AB))
                o, bo = ob[no % 8], bob[no % 8]; no += 1
                S.op("dve", lambda e: e.tensor_tensor(out=o[:], in0=T["t2"][:], in1=T["eWi"][:], op=ALU.mult), reads=[B["t2"], B["eWi"]], writes=[bo])
                outs.append((o, bo, BtT, BB))
                for (o, bo, dst, bdst) in outs:
                    S.dma("sp", dst[fs, t0:t0 + G], o[:], bo, reads=[bo], writes=[bdst])
                w_, bw_ = wc[cc % 2], bwc[cc % 2]
                S.op("dve", lambda e: e.tensor_copy(out=w_[:], in_=T["eW"][:, 63:G:64]), reads=[B["eW"]], writes=[bw_])
                S.dma("sp", WC[fs, g * (G // 64):(g + 1) * (G // 64)], w_[:], bw_, reads=[bw_], writes=[WCB])
        S.barrier()
        P.release()
        for b in [bW, bcols] + bwc + bob:
            S.release(b)


def rwkv_tm_phase(nc, S, x_in, xin_b, gain_row, mix, a0, a1, a2, g1, g2, ka_, rk_, w_r, w_k, w_v,
                  Vtok, BV, Gt, bouts, ntok):
    G = 256
    ng = ntok // G
    with ExitStack() as es:
        sb = lambda name, shape, dt: es.enter_context(nc.sbuf_tensor(_uniq(name), shape, dt))
        ps = lambda name, shape, dt: es.enter_context(nc.psum_tensor(_uniq(name), shape, dt))
        P = RwPrep(nc, S, es, G, (0, 2, 3, 4, 5), x_in, xin_b, gain_row, mix)
        Wr = sb("Wr", [128, 8, D], BF16)
        Wk = sb("Wk", [128, 8, D], BF16)
        Wv = sb("Wv", [128, 8, D], BF16)
        A1 = sb("A1", [128, 8, 64], BF16)
        A2 = sb("A2", [64, D], BF16)
        G1 = sb("G1", [128, 8, 160], BF16)
        G2a = sb("G2a", [128, D], BF16)
        G2b = sb("G2b", [32, D], BF16)
        a0B = sb("a0B", [128, D], F32)
        kaB = sb("kaB", [128, D], F32)
        rkB = sb("rkB", [128, D], F32)
        ta = sb("ta", [64, G], BF16)
        sg1a = sb("sg1a", [128, G], BF16)
        sg1b = sb("sg1b", [32, G], BF16)
        tmp = sb("tmp", [128, D], F32)
        av = sb("av", [128, D], F32)
        t1 = sb("t1", [128, D], F32)
        kp = sb("kp", [128, D], F32)
        tmp2 = sb("tmp2", [128, D], F32)
        tmp3 = sb("tmp3", [128, 16, 64], F32)
        bsum = sb("bsum", [128, 16, 1], F32)
        bvo = [sb("bvo%d" % i, [128, 16, 64], F32) for i in range(2)]
        vto = [sb("vto%d" % i, [128, D], BF16) for i in range(2)]
        gto = [sb("gto%d" % i, [128, D], F32) for i in range(2)]
        p_t = ps("p_t", [128, G], F32)
        pp = [ps("pp%d" % i, [128, 2, 512], F32) for i in range(2)]
        bW, bB = S.dbuf("W"), S.dbuf("B")
        bbvo = [S.dbuf("bvo") for _ in range(2)]
        bvto = [S.dbuf("vto") for _ in range(2)]
        bgto = [S.dbuf("gto") for _ in range(2)]
        B = {n: S.buf(n) for n in ["ta", "sg1a", "sg1b", "tmp", "av", "t1", "kp", "tmp2", "tmp3", "bsum"]}
        B.update({n: S.pbuf(n) for n in ["p_t", "pp0", "pp1"]})
        bpp = [B["pp0"], B["pp1"]]
        for c in range(8):
            for (W_, w_) in ((Wr, w_r), (Wk, w_k), (Wv, w_v)):
                S.dma("pool", W_[:, c, :], w_.rearrange("(c p) f -> p c f", p=128)[:, c, :], bW, writes=[bW])
        S.dma("pool", A1[:], a1.rearrange("(c p) f -> p c f", p=128), bW, writes=[bW])
        S.dma("pool", G1[:], g1.rearrange("(c p) f -> p c f", p=128), bW, writes=[bW])
        S.dma("pool", A2[:], a2, bW, writes=[bW])
        S.dma("pool", G2a[:], g2[0:128, :], bW, writes=[bW])
        S.dma("pool", G2b[:], g2[128:160, :], bW, writes=[bW])
        for (t_, src) in ((a0B, a0), (kaB, ka_), (rkB, rk_)):
            S.dma("sp", t_[:], src.to_broadcast([128, D]), bB, writes=[bB])
        VB, BVB, GB = bouts
        npp = 0
        k = 0
        for g in range(ng):
            P.group(g)
            xr, xk, xv, xa, xg = P.xm[0], P.xm[2], P.xm[3], P.xm[4], P.xm[5]
            bxr, bxk, bxv, bxa, bxg = P.bxm[0], P.bxm[2], P.bxm[3], P.bxm[4], P.bxm[5]
            for c in range(8):
                S.op("pe", lambda e: e.matmul(p_t[0:64, :], lhsT=A1[:, c, :], rhs=xa[:, c, :], start=(c == 0), stop=(c == 7)),
                     reads=[bW, bxa], writes=[B["p_t"]], sig=(c == 7))
            S.op("act", lambda e: e.copy(out=ta[:], in_=p_t[0:64, :]), reads=[B["p_t"]], writes=[B["ta"]])
            for c in range(8):
                S.op("pe", lambda e: e.matmul(p_t[:, :], lhsT=G1[:, c, 0:128], rhs=xg[:, c, :], start=(c == 0), stop=(c == 7)),
                     reads=[bW, bxg], writes=[B["p_t"]], sig=(c == 7))
            S.op("act", lambda e: e.activation(out=sg1a[:], in_=p_t[:, :], func=AF.Sigmoid), reads=[B["p_t"]], writes=[B["sg1a"]])
            for c in range(8):
                S.op("pe", lambda e: e.matmul(p_t[0:32, :], lhsT=G1[:, c, 128:160], rhs=xg[:, c, :], start=(c == 0), stop=(c == 7)),
                     reads=[bW, bxg], writes=[B["p_t"]], sig=(c == 7))
            S.op("act", lambda e: e.activation(out=sg1b[:], in_=p_t[0:32, :], func=AF.Sigmoid), reads=[B["p_t"]], writes=[B["sg1b"]])
            for s in range(G // 128):
                ts = slice(s * 128, (s + 1) * 128)
                t0 = g * G + s * 128

                def big(xm_, bxm_, W_):
                    nonlocal npp
                    p, bp = pp[npp % 2], bpp[npp % 2]
                    npp += 1
                    for hh in range(2):
                        for c in range(8):
                            S.op("pe", lambda e: e.matmul(p[:, hh, :], lhsT=xm_[:, c, ts], rhs=W_[:, c, hh * 512:(hh + 1) * 512],
                                                          start=(c == 0), stop=(c == 7)), reads=[bW, bxm_], writes=[bp], sig=(c == 7))
                    return p, bp
                p, bp = pp[npp % 2], bpp[npp % 2]
                npp += 1
                for hh in range(2):
                    S.op("pe", lambda e: e.matmul(p[:, hh, :], lhsT=ta[:, ts], rhs=A2[:, hh * 512:(hh + 1) * 512], start=True, stop=True),
                         reads=[bW, B["ta"]], writes=[bp])
                S.op("dve", lambda e: e.tensor_tensor(out=tmp[:], in0=p[:].rearrange("p a b -> p (a b)"), in1=a0B[:], op=ALU.add),
                     reads=[bp, bB], writes=[B["tmp"]])
                S.op("act", lambda e: e.activation(out=av[:], in_=tmp[:], func=AF.Sigmoid), reads=[B["tmp"]], writes=[B["av"]])
                S.op("dve", lambda e: e.scalar_tensor_tensor(out=t1[:], in0=av[:], scalar=-1.0, in1=kaB[:], op0=ALU.add, op1=ALU.mult),
                     reads=[B["av"], bB], writes=[B["t1"]])
                p, bp = big(xk, bxk, Wk)
                S.op("dve", lambda e: e.scalar_tensor_tensor(out=kp[:], in0=t1[:], scalar=1.0, in1=p[:].rearrange("p a b -> p (a b)"),
                                                             op0=ALU.add, op1=ALU.mult), reads=[B["t1"], bp], writes=[B["kp"]])
                p, bp = big(xr, bxr, Wr)
                S.op("dve", lambda e: e.tensor_tensor(out=tmp2[:], in0=p[:].rearrange("p a b -> p (a b)"), in1=rkB[:], op=ALU.mult),
                     reads=[bp, bB], writes=[B["tmp2"]])
                S.op("dve", lambda e: e.tensor_tensor(out=tmp3[:].rearrange("p a b -> p (a b)"), in0=tmp2[:], in1=kp[:], op=ALU.mult),
                     reads=[B["tmp2"], B["kp"]], writes=[B["tmp3"]])
                S.op("dve", lambda e: e.tensor_reduce(out=bsum[:], in_=tmp3[:], axis=AX.X, op=ALU.add), reads=[B["tmp3"]], writes=[B["bsum"]])
                p, bp = big(xv, bxv, Wv)
                o, bo = bvo[k % 2], bbvo[k % 2]
                S.op("dve", lambda e: e.tensor_tensor(out=o[:], in0=p[:].rearrange("p a (h d) -> p (a h) d", d=64),
                                                      in1=bsum[:].to_broadcast([128, 16, 64]), op=ALU.mult), reads=[bp, B["bsum"]], writes=[bo])
                S.dma("sp", BV[t0:t0 + 128, :], o[:].rearrange("p a b -> p (a b)"), bo, reads=[bo], writes=[BVB])
                o, bo = vto[k % 2], bvto[k % 2]
                S.op("act", lambda e: e.copy(out=o[:], in_=p[:].rearrange("p a b -> p (a b)")), reads=[bp], writes=[bo])
                S.dma("sp", Vtok[t0:t0 + 128, :], o[:], bo, reads=[bo], writes=[VB])
                p, bp = pp[npp % 2], bpp[npp % 2]
                npp += 1
                for hh in range(2):
                    S.op("pe", lambda e: e.matmul(p[:, hh, :], lhsT=sg1a[:, ts], rhs=G2a[:, hh * 512:(hh + 1) * 512], start=True, stop=False),
                         reads=[bW, B["sg1a"]], writes=[bp], sig=False)
                    S.op("pe", lambda e: e.matmul(p[:, hh, :], lhsT=sg1b[:, ts], rhs=G2b[:, hh * 512:(hh + 1) * 512], start=False, stop=True),
                         reads=[bW, B["sg1b"]], writes=[bp])
                o, bo = gto[k % 2], bgto[k % 2]
                S.op("act", lambda e: e.copy(out=o[:], in_=p[:].rearrange("p a b -> p (a b)")), reads=[bp], writes=[bo])
                S.dma("sp", Gt[t0:t0 + 128, :], o[:], bo, reads=[bo], writes=[GB])
                k += 1
        S.barrier()
        P.release()
        for b in [bW, bB] + bbvo + bvto + bgto:
            S.release(b)


def rwkv_scan_phase(nc, S, RtT, KtT, AtT, BtT, WC, Vtok, Ysc, bins, bY, ntok, NI=4):
    nch = ntok // 64
    ngr = nch // 8
    with ExitStack() as es:
        sb = lambda name, shape, dt: es.enter_context(nc.sbuf_tensor(_uniq(name), shape, dt))
        ps = lambda name, shape, dt: es.enter_context(nc.psum_tensor(_uniq(name), shape, dt))
        MU = sb("MU", [128, 128], F32)
        MUI = sb("MUI", [128, 128], F32)
        ML = sb("ML", [128, 128], F32)
        I32 = sb("I32", [128, 128], F32)
        identb = sb("identb", [128, 128], BF16)
        bconst = S.buf("const")
        for (m, chm, pat, op) in ((MU, -1, 1, ALU.is_gt), (MUI, -1, 1, ALU.is_ge), (ML, 1, -1, ALU.is_gt)):
            S.op("pool", lambda e: e.memset(m[:], 1.0), writes=[bconst])
            S.op("pool", lambda e: e.affine_select(out=m[:], in_=m[:], pattern=[[pat, 128]], compare_op=op, fill=0.0, base=0,
                                                   channel_multiplier=chm), reads=[bconst], writes=[bconst])
        make_ident(nc, S, I32, bconst)
        make_ident(nc, S, identb, bconst)

        class Slot:
            pass
        slots = []
        for si in range(NI):
            s = Slot()
            n_ = lambda x: "%s_%d" % (x, si)
            s.AR = [sb(n_("AR%d" % j), [128, 8, 2, 128], BF16) for j in range(2)]
            s.Bd = [sb(n_("Bd%d" % j), [128, 8, 128], BF16) for j in range(2)]
            s.Kd = [sb(n_("Kd%d" % j), [128, 8, 128], BF16) for j in range(2)]
            s.bbd = [S.dbuf(n_("bd0")), S.dbuf(n_("bd1"))]
            s.Vs = [sb(n_("Vs%d" % j), [128, 8, 64], BF16) for j in range(2)]
            s.Yo = [sb(n_("Yo%d" % j), [128, 8, 64], F32) for j in range(2)]
            s.bYo = [S.dbuf(n_("Yo0")), S.dbuf(n_("Yo1"))]
            s.wcs = sb(n_("wcs"), [128, nch], F32)
            s.bwcs = S.dbuf(n_("wcs"))
            s.QX = [sb(n_("QX%d" % j), [128, 2, 128], BF16) for j in range(2)]
            s.P = [sb(n_("P%d" % j), [128, 128], BF16) for j in range(2)]
            s.bQX = [S.buf("QX") for _ in range(2)]
            s.bP = [S.buf("P") for _ in range(2)]
            for k in ("Mak", "Mrb", "Mrk", "BtT", "KtT"):
                setattr(s, k, [sb(n_(k) + "_%d" % q_, [128, 128], BF16) for q_ in range(2)])
                setattr(s, "b" + k, [S.buf(k) for _ in range(2)])
            s.TT = [sb(n_("TT%d" % q_), [128, 128], BF16) for q_ in range(2)]
            s.bTT = [S.buf("TT") for _ in range(2)]
            s.Xs = sb(n_("Xs"), [128, 64], BF16)
            s.Ub = sb(n_("Ub"), [128, 64], BF16)
            s.Sw = sb(n_("Sw"), [128, 64], F32)
            s.St = sb(n_("St"), [128, 64], F32)
            s.Sb = sb(n_("Sb"), [128, 64], BF16)
            s.bXs, s.bUb, s.bSw, s.bSt, s.bSb = [S.buf(k) for k in ("Xs", "Ub", "Sw", "St", "Sb")]
            s.psA = ps(n_("psA"), [128, 512], F32)
            s.psB = ps(n_("psB"), [128, 4, 128], F32)
            s.bA, s.bB = S.pbuf("bankA"), S.pbuf("bankB")
            s.ptr = s.psB[:, 3, :].bitcast(BF16)
            for j in range(2):
                S.op("pool", lambda e: e.memset(s.AR[j][:], 0.0), writes=[s.bbd[j]])
                S.op("pool", lambda e: e.memset(s.Bd[j][:], 0.0), writes=[s.bbd[j]])
                S.op("pool", lambda e: e.memset(s.Kd[j][:], 0.0), writes=[s.bbd[j]])
            slots.append(s)

        bRs, bKs, bAs, bBs, bWC, bV = bins

        def load_group(s, hp, gg):
            j = gg % 2
            t0 = gg * 512
            for h in range(2):
                r0 = hp * 128 + h * 64
                hs, cs = slice(h * 64, (h + 1) * 64), slice(h * 64, (h + 1) * 64)
                for (dst, src, bsrc) in ((s.AR[j][hs, :, 0, cs], AtT, bAs), (s.AR[j][hs, :, 1, cs], RtT, bRs),
                                         (s.Bd[j][hs, :, cs], BtT, bBs), (s.Kd[j][hs, :, cs], KtT, bKs)):
                    S.dma("sp", dst, src[r0:r0 + 64, t0:t0 + 512].rearrange("p (c j) -> p c j", j=64), s.bbd[j],
                          reads=[bsrc], writes=[s.bbd[j]])
                S.dma("sp", s.Vs[j][hs, :, :], Vtok[t0:t0 + 512, r0:r0 + 64].rearrange("(c j) v -> j c v", j=64),
                      s.bbd[j], reads=[bV], writes=[s.bbd[j]])

        def T_steps(gg, c):
            j = gg % 2
            q = (gg * 8 + c) % 2
            steps = []

            def st1():
                for s in slots:
                    AR, A, Bd, K = s.AR[j][:, c, :, :], s.AR[j][:, c, 0, :], s.Bd[j][:, c, :], s.Kd[j][:, c, :]
                    S.op("pe", lambda e: e.matmul(s.psA[:, 0:256], lhsT=Bd, rhs=AR, start=True, stop=True), reads=[s.bbd[j]], writes=[s.bA], sig=False)
                    S.op("pe", lambda e: e.matmul(s.psA[:, 256:512], lhsT=K, rhs=AR, start=True, stop=True), reads=[s.bbd[j]], writes=[s.bA])
                    S.op("pe", lambda e: e.matmul(s.psB[:, 1, :], lhsT=A, rhs=Bd, start=True, stop=True), reads=[s.bbd[j]], writes=[s.bB], sig=False)
                    S.op("pe", lambda e: e.transpose(s.ptr[:, 0:128], Bd, identb[:]), reads=[s.bbd[j], bconst], writes=[s.bB], sig=False)
                    S.op("pe", lambda e: e.transpose(s.ptr[:, 128:256], K, identb[:]), reads=[s.bbd[j], bconst], writes=[s.bB])
            steps.append(st1)

            def st2():
                for s in slots:
                    S.op("dve", lambda e: e.tensor_tensor(out=s.QX[0][:, 0, :], in0=s.psA[:, 0:128], in1=MU[:], op=ALU.mult),
                         reads=[s.bA, bconst], writes=[s.bQX[0]])
                    S.op("dve", lambda e: e.tensor_tensor(out=s.QX[1][:, 1, :], in0=s.QX[0][:, 0, :], in1=I32[:], op=ALU.add),
                         reads=[s.bQX[0], bconst], writes=[s.bQX[1]])
                    S.op("dve", lambda e: e.tensor_tensor(out=s.P[0][:], in0=s.psB[:, 1, :], in1=ML[:], op=ALU.mult),
                         reads=[s.bB, bconst], writes=[s.bP[0]])
            steps.append(st2)

            def st3():
                for s in slots:
                    for (nm, lo, msk) in (("Mrb", 128, MUI), ("Mak", 256, MU), ("Mrk", 384, MUI)):
                        S.op("dve", lambda e: e.tensor_tensor(out=getattr(s, nm)[q][:], in0=s.psA[:, lo:lo + 128], in1=msk[:], op=ALU.mult),
                             reads=[s.bA, bconst], writes=[getattr(s, "b" + nm)[q]])
                    S.op("act", lambda e: e.copy(out=s.BtT[q][:], in_=s.ptr[:, 0:128]), reads=[s.bB], writes=[s.bBtT[q]])
                    S.op("act", lambda e: e.copy(out=s.KtT[q][:], in_=s.ptr[:, 128:256]), reads=[s.bB], writes=[s.bKtT[q]])
            steps.append(st3)

            for lvl in range(6):
                cur = 0 if lvl == 0 else lvl % 2
                nxt = 1 - cur

                def sa(lvl=lvl, cur=cur):
                    for s in slots:
                        Q, X, Pm = s.QX[cur][:, 0, :], s.QX[cur][:, 1, :], s.P[cur][:]
                        rd = [s.bP[cur], s.bQX[cur]]
                        if lvl == 0:
                            S.op("pe", lambda e: e.matmul(s.psA[:, 0:128], lhsT=Pm, rhs=Q, start=True, stop=True), reads=rd, writes=[s.bA], sig=False)
                        elif lvl <= 3:
                            S.op("pe", lambda e: e.matmul(s.psA[:, 0:256], lhsT=Pm, rhs=s.QX[cur][:, :, :], start=True, stop=True),
                                 reads=rd, writes=[s.bA], sig=False)
                        else:
                            S.op("pe", lambda e: e.matmul(s.psA[:, 128:256], lhsT=Pm, rhs=X, start=True, stop=True), reads=rd, writes=[s.bA],
                                 sig=(lvl == 5))
                        if lvl <= 4:
                            S.op("pe", lambda e: e.matmul(s.psA[:, 256:384], lhsT=Q, rhs=Pm, start=True, stop=True), reads=rd, writes=[s.bA])

                def sb_(lvl=lvl, cur=cur, nxt=nxt):
                    for s in slots:
                        if lvl <= 3:
                            S.op("act", lambda e: e.copy(out=s.QX[nxt][:, 0, :], in_=s.psA[:, 0:128]), reads=[s.bA], writes=[s.bQX[nxt]])
                        if lvl <= 4:
                            S.op("act", lambda e: e.copy(out=s.P[nxt][:], in_=s.psA[:, 256:384]), reads=[s.bA], writes=[s.bP[nxt]])
                        if 1 <= lvl <= 4:
                            S.op("dve", lambda e: e.tensor_tensor(out=s.QX[nxt][:, 1, :], in0=s.psA[:, 128:256], in1=s.QX[cur][:, 1, :], op=ALU.add),
                                 reads=[s.bA, s.bQX[cur]], writes=[s.bQX[nxt]])
                        if lvl == 5:
                            S.op("dve", lambda e: e.tensor_tensor(out=s.TT[q][:], in0=s.psA[:, 128:256], in1=s.QX[cur][:, 1, :], op=ALU.add),
                                 reads=[s.bA, s.bQX[cur]], writes=[s.bTT[q]])
                steps += [sa, sb_]
            return steps

        def S_steps(gg, c):
            j = gg % 2
            ch = gg * 8 + c
            q = ch % 2

            def s1():
                for s in slots:
                    A = s.AR[j][:, c, 0, :]
                    S.op("pe", lambda e: e.matmul(s.psB[:, 0, 0:64], lhsT=A, rhs=s.Sb[:], start=True, stop=False),
                         reads=[s.bbd[j], s.bSb], writes=[s.bB], sig=False)
                    S.op("pe", lambda e: e.matmul(s.psB[:, 0, 0:64], lhsT=s.Mak[q][:], rhs=s.Vs[j][:, c, :], start=False, stop=True),
                         reads=[s.bMak[q], s.bbd[j]], writes=[s.bB])
                    S.op("pool", lambda e: e.tensor_scalar(out=s.Sw[:], in0=s.St[:], scalar1=s.wcs[:, ch:ch + 1], scalar2=None, op0=ALU.mult),
                         reads=[s.bSt, s.bwcs], writes=[s.bSw])

            def s2():
                for s in slots:
                    S.op("act", lambda e: e.copy(out=s.Xs[:], in_=s.psB[:, 0, 0:64]), reads=[s.bB], writes=[s.bXs])

            def s3():
                for s in slots:
                    S.op("pe", lambda e: e.matmul(s.psB[:, 1, 0:64], lhsT=s.TT[q][:], rhs=s.Xs[:], start=True, stop=True),
                         reads=[s.bTT[q], s.bXs], writes=[s.bB])

            def s4():
                for s in slots:
                    S.op("act", lambda e: e.copy(out=s.Ub[:], in_=s.psB[:, 1, 0:64]), reads=[s.bB], writes=[s.bUb])

            def s5():
                for s in slots:
                    R = s.AR[j][:, c, 1, :]
                    pY, pS = s.psB[:, 2, 0:64], s.psB[:, 0, 0:64]
                    S.op("pe", lambda e: e.matmul(pY, lhsT=R, rhs=s.Sb[:], start=True, stop=False),
                         reads=[s.bbd[j], s.bSb], writes=[s.bB], sig=False)
                    S.op("pe", lambda e: e.matmul(pY, lhsT=s.Mrb[q][:], rhs=s.Ub[:], start=False, stop=False),
                         reads=[s.bMrb[q], s.bUb], writes=[s.bB], sig=False)
                    S.op("pe", lambda e: e.matmul(pY, lhsT=s.Mrk[q][:], rhs=s.Vs[j][:, c, :], start=False, stop=True),
                         reads=[s.bMrk[q], s.bbd[j]], writes=[s.bB], sig=False)
                    S.op("pe", lambda e: e.matmul(pS, lhsT=s.BtT[q][:], rhs=s.Ub[:], start=True, stop=False),
                         reads=[s.bBtT[q], s.bUb], writes=[s.bB], sig=False)
                    S.op("pe", lambda e: e.matmul(pS, lhsT=s.KtT[q][:], rhs=s.Vs[j][:, c, :], start=False, stop=True),
                         reads=[s.bKtT[q], s.bbd[j]], writes=[s.bB])

            def s6():
                for s in slots:
                    S.op("dve", lambda e: e.scalar_tensor_tensor(out=s.St[:], in0=s.psB[:, 0, 0:64], scalar=s.wcs[:, ch:ch + 1], in1=s.Sw[:],
                                                                 op0=ALU.mult, op1=ALU.add), reads=[s.bB, s.bwcs, s.bSw], writes=[s.bSt])
                    S.op("dve", lambda e: e.tensor_copy(out=s.Yo[j][:, c, :], in_=s.psB[:, 2, 0:64]), reads=[s.bB], writes=[s.bYo[j]])
                    S.op("act", lambda e: e.copy(out=s.Sb[:], in_=s.St[:]), reads=[s.bSt], writes=[s.bSb])
            return [s1, s2, s3, s4, s5, s6]

        for rnd in range(8 // NI):
            hps = [rnd * NI + i for i in range(NI)]
            for s, hp in zip(slots, hps):
                S.dma("sp", s.wcs[:], WC[hp * 128:(hp + 1) * 128, 0:nch], s.bwcs, reads=[bWC], writes=[s.bwcs])
                S.op("pool", lambda e: e.memset(s.St[:], 0.0), writes=[s.bSt])
                S.op("pool", lambda e: e.memset(s.Sb[:], 0.0), writes=[s.bSb])
                load_group(s, hp, 0)
            if ngr > 1:
                for s, hp in zip(slots, hps):
                    load_group(s, hp, 1)
            for st in T_steps(0, 0):
                st()
            for gg in range(ngr):
                j = gg % 2
                for c in range(8):
                    ch = gg * 8 + c
                    if ch + 1 < nch:
                        ng_, nc_ = (gg, c + 1) if c < 7 else (gg + 1, 0)
                        tsteps = T_steps(ng_, nc_)
                    else:
                        tsteps = []
                    ssteps = S_steps(gg, c)
                    ti = 0
                    for ss_ in ssteps:
                        for _ in range(3):
                            if ti < len(tsteps):
                                tsteps[ti]()
                                ti += 1
                        ss_()
                    while ti < len(tsteps):
                        tsteps[ti]()
                        ti += 1
                for s, hp in zip(slots, hps):
                    for h in range(2):
                        c0 = hp * 128 + h * 64
                        S.dma("sp", Ysc[gg * 512:(gg + 1) * 512, c0:c0 + 64].rearrange("(c j) v -> j c v", j=64),
                              s.Yo[j][h * 64:(h + 1) * 64, :, :], s.bYo[j], reads=[s.bYo[j]], writes=[bY])
                if gg + 2 < ngr:
                    for s, hp in zip(slots, hps):
                        load_group(s, hp, gg + 2)
        S.barrier()
        for s in slots:
            for b_ in s.bbd + s.bYo + [s.bwcs]:
                S.release(b_)


def rwkv_post_phase(nc, S, Ysc, BV, Gt, lg_row, lb_row, ZT, bins, bZT, ntok, gn_eps=64e-5):
    with ExitStack() as es:
        sb = lambda name, shape, dt: es.enter_context(nc.sbuf_tensor(_uniq(name), shape, dt))
        ps = lambda name, shape, dt: es.enter_context(nc.psum_tensor(_uniq(name), shape, dt))
        lgB = sb("lgB", [128, D], F32)
        lbB = sb("lbB", [128, D], F32)
        ident = sb("ident", [128, 128], BF16)
        nh = sb("nh", [128, 16, 1], F32)
        yt = [sb("yt%d" % i, [128, 16, 64], F32) for i in range(2)]
        bvt = [sb("bvt%d" % i, [128, D], F32) for i in range(2)]
        gt = [sb("gt%d" % i, [128, D], F32) for i in range(2)]
        sm_ = [sb("sm%d" % i, [128, 16, 1], F32) for i in range(2)]
        vr_ = [sb("vr%d" % i, [128, 16, 1], F32) for i in range(2)]
        rstd_ = [sb("rstd%d" % i, [128, 16, 1], F32) for i in range(2)]
        yc_ = [sb("yc%d" % i, [128, 16, 64], F32) for i in range(2)]
        sq_ = [sb("sq%d" % i, [128, 16, 64], F32) for i in range(2)]
        yn_ = [sb("yn%d" % i, [128, 16, 64], F32) for i in range(2)]
        y2_ = [sb("y2%d" % i, [128, D], F32) for i in range(2)]
        zb = [sb("zb%d" % i, [128, D], BF16) for i in range(2)]
        zT = [sb("zT%d" % i, [128, 8, 512], BF16) for i in range(2)]
        p_tr = [ps("p_tr%d" % i, [128, 8, 128], BF16) for i in range(2)]
        bC = S.dbuf("C")
        byt = [S.dbuf("yt") for _ in range(2)]
        bbvt = [S.dbuf("bvt") for _ in range(2)]
        bgt = [S.dbuf("gt") for _ in range(2)]
        bzT = [S.dbuf("zT") for _ in range(2)]
        B = {n: S.buf(n) for n in ["ident", "nh", "zb0", "zb1"]}
        B2 = {n: [S.buf(n) for _ in range(2)] for n in ["sm", "vr", "rstd", "yc", "sq", "yn", "y2"]}
        bp_tr = [S.pbuf("ptr") for _ in range(2)]
        bYs, bBV, bG = bins
        make_ident(nc, S, ident, B["ident"])
        S.op("pool", lambda e: e.memset(nh[:], -0.5), writes=[B["nh"]])
        S.dma("sp", lgB[:], lg_row.to_broadcast([128, D]), bC, writes=[bC])
        S.dma("sp", lbB[:], lb_row.to_broadcast([128, D]), bC, writes=[bC])
        nt = ntok // 128

        def bind(t):
            i = t % 2
            for n_ in ["sm", "vr", "rstd", "yc", "sq", "yn", "y2"]:
                B[n_] = B2[n_][i]
            return i, sm_[i], vr_[i], rstd_[i], yc_[i], sq_[i], yn_[i], y2_[i]

        def front(t):
            i, sm, vr, rstd, yc, sq, yn, y2 = bind(t)
            t0 = t * 128
            S.dma("sp", yt[i][:].rearrange("p a b -> p (a b)"), Ysc[t0:t0 + 128, :], byt[i], reads=[bYs], writes=[byt[i]])
            S.dma("sp", bvt[i][:], BV[t0:t0 + 128, :], bbvt[i], reads=[bBV], writes=[bbvt[i]])
            S.dma("sp", gt[i][:], Gt[t0:t0 + 128, :], bgt[i], reads=[bG], writes=[bgt[i]])
            y3 = yt[i]
            S.op("dve", lambda e: e.tensor_reduce(out=sm[:], in_=y3[:], axis=AX.X, op=ALU.add), reads=[byt[i]], writes=[B["sm"]])
            S.op("dve", lambda e: e.tensor_scalar(out=sm[:], in0=sm[:], scalar1=1.0 / 64, scalar2=None, op0=ALU.mult), reads=[B["sm"]], writes=[B["sm"]])
            S.op("dve", lambda e: e.tensor_tensor(out=yc[:], in0=y3[:], in1=sm[:].to_broadcast([128, 16, 64]), op=ALU.subtract),
                 reads=[byt[i], B["sm"]], writes=[B["yc"]])
            S.op("act", lambda e: e.activation(out=sq[:], in_=yc[:], func=AF.Square), reads=[B["yc"]], writes=[B["sq"]])

        def mid(t):
            i, sm, vr, rstd, yc, sq, yn, y2 = bind(t)
            S.op("dve", lambda e: e.tensor_reduce(out=vr[:], in_=sq[:], axis=AX.X, op=ALU.add), reads=[B["sq"]], writes=[B["vr"]])
            S.op("dve", lambda e: e.tensor_scalar(out=vr[:], in0=vr[:], scalar1=1.0 / 64, scalar2=gn_eps, op0=ALU.mult, op1=ALU.add),
                 reads=[B["vr"]], writes=[B["vr"]])
            S.op("pool", lambda e: e.tensor_tensor(out=rstd[:], in0=vr[:], in1=nh[:], op=ALU.pow), reads=[B["vr"], B["nh"]], writes=[B["rstd"]])

        def back(t):
            i, sm, vr, rstd, yc, sq, yn, y2 = bind(t)
            S.op("dve", lambda e: e.tensor_tensor(out=yn[:], in0=yc[:], in1=rstd[:].to_broadcast([128, 16, 64]), op=ALU.mult),
                 reads=[B["yc"], B["rstd"]], writes=[B["yn"]])
            ynf = yn[:].rearrange("p a b -> p (a b)")
            S.op("dve", lambda e: e.tensor_tensor(out=y2[:], in0=ynf, in1=lgB[:], op=ALU.mult), reads=[B["yn"], bC], writes=[B["y2"]])
            S.op("dve", lambda e: e.tensor_tensor(out=y2[:], in0=y2[:], in1=lbB[:], op=ALU.add), reads=[B["y2"], bC], writes=[B["y2"]])
            S.op("dve", lambda e: e.tensor_tensor(out=y2[:], in0=y2[:], in1=bvt[i][:], op=ALU.add), reads=[B["y2"], bbvt[i]], writes=[B["y2"]])
            z, bz = zb[i], B["zb%d" % i]
            S.op("dve", lambda e: e.tensor_tensor(out=z[:], in0=y2[:], in1=gt[i][:], op=ALU.mult), reads=[B["y2"], bgt[i]], writes=[bz])
            pt, bpt = p_tr[i], bp_tr[i]
            for c in range(8):
                S.op("pe", lambda e: e.transpose(pt[:, c, :], z[:, c * 128:(c + 1) * 128], ident[:]), reads=[bz, B["ident"]], writes=[bpt], sig=(c == 7))
            gi = (t // 4) % 2
            S.op("act", lambda e: e.copy(out=zT[gi][:, :, (t % 4) * 128:(t % 4 + 1) * 128], in_=pt[:]), reads=[bpt], writes=[bzT[gi]])
            if t % 4 == 3:
                g0 = (t // 4) * 512
                for c in range(8):
                    S.dma("sp", ZT[c * 128:(c + 1) * 128, g0:g0 + 512], zT[gi][:, c, :], bzT[gi], reads=[bzT[gi]], writes=[bZT])

        front(0)
        mid(0)
        for t in range(nt):
            if t + 1 < nt:
                front(t + 1)
            back(t)
            if t + 1 < nt:
                mid(t + 1)
        S.barrier()
        for b in [bC] + byt + bbvt + bgt + bzT:
            S.release(b)


def build_program(ntok=SEQ):
    nc = bass.Bass("TRN2", target_bir_lowering=False)
    di = lambda n, s: nc.dram_tensor(n, list(s), F32, kind="ExternalInput").ap()
    x = di("x", [ntok, D])
    ffn_norm = di("ffn_norm", [4, D])
    wg = di("ffn_w_gate", [2, 2, D, DFF])
    wu = di("ffn_w_up", [2, 2, D, DFF])
    wd = di("ffn_w_down", [2, 2, DFF, D])
    mix_norm = di("mix_norm", [2, D])
    dbias = di("dbias", [12, 128, 2, 128])
    w_in = di("attn_w_in", [D, 3072])
    qn = di("attn_q_norm", [64, 1])
    kn = di("attn_k_norm", [64, 1])
    w_out = di("attn_w_out", [512, D])
    rw_mix = di("rw_mix", [6, D])
    rows = {n: di(n, [1, D]) for n in ("rw_w0", "rw_a0", "rw_kk", "rw_ka", "rw_rk", "rw_lnx_g", "rw_lnx_b")}
    rw_w1 = di("rw_w1", [D, 64]); rw_w2 = di("rw_w2", [64, D]); rw_a1 = di("rw_a1", [D, 64]); rw_a2 = di("rw_a2", [64, D])
    rw_g1 = di("rw_g1", [D, 160]); rw_g2 = di("rw_g2", [160, D])
    rw_wr = di("rw_wr", [D, D]); rw_wk = di("rw_wk", [D, D]); rw_wv = di("rw_wv", [D, D]); rw_wo = di("rw_wo", [D, D])
    out = nc.dram_tensor("out", [ntok, D], F32, kind="ExternalOutput").ap()
    scr = lambda n, s, dt: nc.dram_tensor(n, list(s), dt, kind="Internal").ap()
    xa = scr("xa", [ntok, D], F32); xb = scr("xb", [ntok, D], F32)
    QT = scr("QT", [D, ntok], BF16); KT = scr("KT", [D, ntok], BF16); V = scr("V", [ntok, D], BF16); MT = scr("MT", [512, ntok], BF16)
    RtT, KtT, AtT, BtT = [scr(n, [D, ntok], BF16) for n in ("RtT", "KtT", "AtT", "BtT")]
    WC = scr("WC", [D, ntok // 64], F32)
    Vtok = scr("Vtok", [ntok, D], BF16); BV = scr("BV", [ntok, D], F32); Gt = scr("Gt", [ntok, D], F32); Ysc = scr("Ysc", [ntok, D], F32)
    ZT = scr("ZT", [D, ntok], BF16)

    S = Sched(nc, n_dma_sems=24)
    nb = lambda n: S.buf(n, acc=True)
    bx, bxa, bxb, bout = nb("x"), nb("xa"), nb("xb"), nb("out")
    bQT, bKT, bV, bMT = nb("QT"), nb("KT"), nb("V"), nb("MT")
    bR, bK, bA, bB, bWC, bVt, bBV, bG, bY, bZ = [nb(n) for n in ("R", "K", "A", "B", "WC", "Vt", "BV", "G", "Y", "Z")]

    ffn_phase(nc, S, x, xa, bx, bxa, wg[0, 0], wu[0, 0], wd[0, 0], ffn_norm[0:1, :], ntok)
    attn_in_phase(nc, S, xa, bxa, w_in, mix_norm[0:1, :], qn, kn, QT, KT, V, bQT, bKT, bV, ntok)
    sb_attn_phase(nc, S, QT, KT, V, MT, bQT, bKT, bV, bMT, ntok)
    dil_attn_phase(nc, S, QT, KT, V, MT, dbias, bQT, bKT, bV, bMT, ntok)
    out_proj_phase(nc, S, xa, xb, bxa, bxb, MT, bMT, w_out, 512, ntok)
    ffn_phase(nc, S, xb, xa, bxb, bxa, wg[0, 1], wu[0, 1], wd[0, 1], ffn_norm[1:2, :], ntok)
    ffn_phase(nc, S, xa, xb, bxa, bxb, wg[1, 0], wu[1, 0], wd[1, 0], ffn_norm[2:3, :], ntok)
    rwkv_fm_phase(nc, S, xb, bxb, mix_norm[1:2, :], rw_mix, rows["rw_w0"], rw_w1, rw_w2, rows["rw_a0"], rw_a1, rw_a2,
                  rows["rw_kk"], rows["rw_ka"], rw_wr, rw_wk, RtT, KtT, AtT, BtT, WC, [bR, bK, bA, bB, bWC], ntok)
    rwkv_tm_phase(nc, S, xb, bxb, mix_norm[1:2, :], rw_mix, rows["rw_a0"], rw_a1, rw_a2, rw_g1, rw_g2, rows["rw_ka"], rows["rw_rk"],
                  rw_wr, rw_wk, rw_wv, Vtok, BV, Gt, [bVt, bBV, bG], ntok)
    rwkv_scan_phase(nc, S, RtT, KtT, AtT, BtT, WC, Vtok, Ysc, [bR, bK, bA, bB, bWC, bVt], bY, ntok)
    rwkv_post_phase(nc, S, Ysc, BV, Gt, rows["rw_lnx_g"], rows["rw_lnx_b"], ZT, [bY, bBV, bG], bZ, ntok)
    out_proj_phase(nc, S, xb, xa, bxb, bxa, ZT, bZ, rw_wo, 1024, ntok)
    ffn_phase(nc, S, xa, out, bxa, bout, wg[1, 1], wu[1, 1], wd[1, 1], ffn_norm[3:4, :], ntok)
    S.wait_for("sp", [bout])
    return nc


def kernel(x, ffn_norm, ffn_w_gate, ffn_w_up, ffn_w_down, mix_norm, rel_bias,
           attn_w_in, attn_q_norm, attn_k_norm, attn_w_out,
           rw_mix, rw_w0, rw_w1, rw_w2, rw_a0, rw_a1, rw_a2, rw_g1, rw_g2,
           rw_kk, rw_ka, rw_rk, rw_wr, rw_wk, rw_wv, rw_wo, rw_lnx_g, rw_lnx_b):
    f = lambda a: np.ascontiguousarray(np.asarray(a, dtype=np.float32))
    x = f(x)
    n = x.shape[0]
    shared = {
        "ffn_norm": f(ffn_norm).reshape(4, D), "ffn_w_gate": f(ffn_w_gate), "ffn_w_up": f(ffn_w_up), "ffn_w_down": f(ffn_w_down),
        "mix_norm": f(mix_norm), "dbias": dil_bias_host(f(rel_bias)),
        "attn_w_in": f(attn_w_in)[0], "attn_q_norm": f(attn_q_norm).reshape(64, 1), "attn_k_norm": f(attn_k_norm).reshape(64, 1),
        "attn_w_out": f(attn_w_out)[0], "rw_mix": f(rw_mix)[0],
        "rw_w0": f(rw_w0).reshape(1, D), "rw_a0": f(rw_a0).reshape(1, D), "rw_kk": f(rw_kk).reshape(1, D), "rw_ka": f(rw_ka).reshape(1, D),
        "rw_rk": f(rw_rk).reshape(1, D), "rw_lnx_g": f(rw_lnx_g).reshape(1, D), "rw_lnx_b": f(rw_lnx_b).reshape(1, D),
        "rw_w1": f(rw_w1)[0], "rw_w2": f(rw_w2)[0], "rw_a1": f(rw_a1)[0], "rw_a2": f(rw_a2)[0], "rw_g1": f(rw_g1)[0], "rw_g2": f(rw_g2)[0],
        "rw_wr": f(rw_wr)[0], "rw_wk": f(rw_wk)[0], "rw_wv": f(rw_wv)[0], "rw_wo": f(rw_wo)[0],
    }
    nc = build_program(x.shape[1])
    in_maps = [dict(shared, x=x[i]) for i in range(n)]
    res = run_bass_kernel_spmd(nc, in_maps, core_ids=list(range(n)))
    return np.stack([np.asarray(r["out"]) for r in res.results], axis=0).astype(np.float32)
```

```python
import numpy as np
from contextlib import ExitStack
import concourse.bass as bass
import concourse.mybir as mybir
from concourse.bass_utils import run_bass_kernel_spmd

F32 = mybir.dt.float32
BF16 = mybir.dt.bfloat16
AF = mybir.ActivationFunctionType
ALU = mybir.AluOpType
AX = mybir.AxisListType

D = 1024
DFF = 2816
NF = DFF // 128
SEQ = 4096


_UID = [0]


def _uniq(name):
    _UID[0] += 1
    return "%s_u%d" % (name, _UID[0])


def _merge(d, s):
    for k, v in s.items():
        if d.get(k, 0) < v:
            d[k] = v


class Buf:
    __slots__ = ("name", "wr", "rd", "acc", "dkey", "excl")

    def __init__(self, name, acc=False, excl=False):
        self.name = name
        self.wr = {}
        self.rd = {}
        self.acc = acc
        self.dkey = None
        self.excl = excl


class Sched:
    ENG = ("pe", "act", "dve", "pool", "sp")

    def __init__(self, nc, n_dma_sems=40):
        self.nc = nc
        self.eng = {"pe": nc.tensor, "act": nc.scalar, "dve": nc.vector, "pool": nc.gpsimd, "sp": nc.sync}
        self.sems = {}
        self.val = {}
        self.seen = {e: {} for e in self.ENG}
        self.epoch = 0
        self.ekey = {}
        self._new_engine_sems()
        self.dma_pool = []
        for i in range(n_dma_sems):
            k = "dma%d" % i
            self.sems[k] = nc.semaphore(k).__enter__()
            self.val[k] = 0
            self.dma_pool.append(k)
        self.nwait = 0

    def _new_engine_sems(self):
        for e in self.ENG:
            k = "%s_e%d" % (e, self.epoch)
            self.sems[k] = self.nc.semaphore(k).__enter__()
            self.val[k] = 0
            self.ekey[e] = k

    def buf(self, name, acc=False):
        return Buf(name, acc)

    def pbuf(self, name):
        return Buf(name, False, True)

    def dbuf(self, name, acc=False):
        b = Buf(name, acc)
        b.dkey = self.dma_pool.pop()
        return b

    def release(self, b):
        self.dma_pool.append(b.dkey)
        b.dkey = None

    def _wait(self, e, deps):
        for k, v in deps.items():
            if v <= 0:
                continue
            if e == "pe" and k == self.ekey["pe"]:
                continue
            if self.seen[e].get(k, 0) < v:
                self.eng[e].wait_ge(self.sems[k], v)
                self.seen[e][k] = v
                self.nwait += 1

    def _deps(self, reads, writes, e=None):
        deps = {}
        for b in reads:
            _merge(deps, b.wr)
            if b.excl:
                own = self.ekey.get(e)
                _merge(deps, {k: v for k, v in b.rd.items() if k != own})
        for b in writes:
            _merge(deps, b.wr)
            _merge(deps, b.rd)
        return deps

    def _record(self, ev, reads, writes):
        for b in reads:
            _merge(b.rd, ev)
        for b in writes:
            if b.acc:
                _merge(b.wr, ev)
            else:
                b.wr = dict(ev)
                b.rd = {}

    def op(self, e, fn, reads=(), writes=(), sig=True):
        self._wait(e, self._deps(reads, writes, e))
        ins = fn(self.eng[e])
        k = self.ekey[e]
        if sig:
            ins.then_inc(self.sems[k], 1)
            self.val[k] += 1
            v = self.val[k]
        else:
            v = self.val[k] + 1
        self._record({k: v}, reads, writes)
        return ins

    def dma(self, q, out, in_, track, reads=(), writes=(), slow=False):
        self._wait(q, self._deps(reads, writes))
        if slow:
            ins = self.eng[q].dma_start(out=out, in_=in_, allow_slow_non_contiguous=True)
        else:
            ins = self.eng[q].dma_start(out=out, in_=in_)
        k = track.dkey
        ins.then_inc(self.sems[k], 16)
        self.val[k] += 16
        self._record({k: self.val[k]}, reads, writes)
        return ins

    def barrier(self, new_epoch=True):
        allv = {k: v for k, v in self.val.items() if v > 0}
        for e in self.ENG:
            self._wait(e, allv)
        if new_epoch:
            self.epoch += 1
            self._new_engine_sems()

    def wait_for(self, e, bufs):
        deps = {}
        for b in bufs:
            _merge(deps, b.wr)
            _merge(deps, b.rd)
        self._wait(e, deps)


def ffn_phase(nc, S, x_in, x_out, xin_b, xout_b, wg, wu, wd, gain_row, ntok, eps=1e-6):
    G = 256
    ng = ntok // G
    with ExitStack() as es:
        sb = lambda name, shape, dt: es.enter_context(nc.sbuf_tensor(_uniq(name), shape, dt))
        ps = lambda name, shape, dt: es.enter_context(nc.psum_tensor(_uniq(name), shape, dt))
        Wg = sb("Wg", [128, 8, DFF], BF16)
        Wu = sb("Wu", [128, 8, DFF], BF16)
        Wd = sb("Wd", [128, NF, D], BF16)
        gB = sb("gB", [128, D], F32)
        ident = sb("ident", [128, 128], BF16)
        xt = [sb("xt%d" % i, [128, D], F32) for i in range(4)]
        ot = [sb("ot%d" % i, [128, D], F32) for i in range(2)]
        hb = [sb("hb%d" % i, [128, D], BF16) for i in range(2)]
        hT = [sb("hT%d" % i, [128, 8, G], BF16) for i in range(2)]
        aT = [sb("aT%d" % i, [128, G], BF16) for i in range(3)]
        sg = [sb("sg%d" % i, [128, G], F32) for i in range(2)]
        junk = sb("junk", [128, D], BF16)
        ss = [sb("ss%d" % i, [128, 1], F32) for i in range(2)]
        rs = [sb("rs%d" % i, [128, 1], F32) for i in range(2)]
        nh = sb("nh", [128, 1], F32)
        p_gu = [ps("p_gu%d" % i, [128, 2, G], F32) for i in range(2)]
        p_dn = [ps("p_dn%d" % i, [128, 512], F32) for i in range(4)]
        p_tr = [ps("p_tr%d" % i, [128, 8, 128], BF16) for i in range(2)]

        FB = [(0, 6), (6, 12), (12, 17), (17, 22)]
        blk_of = {}
        for bi, (j0, j1) in enumerate(FB):
            for j in range(j0, j1):
                blk_of[j] = bi
        bWgL = [S.dbuf("Wg") for _ in FB]
        bWuL = [S.dbuf("Wu") for _ in FB]
        bWdL = [S.dbuf("Wd") for _ in FB]
        bgB = S.dbuf("gB")
        bxt = [S.dbuf("xt%d" % i) for i in range(4)]
        bot = [S.dbuf("ot%d" % i) for i in range(2)]
        bhb = [S.buf("hb") for _ in range(2)]
        bhT = [S.buf("hT") for _ in range(2)]
        baT = [S.buf("aT") for _ in range(3)]
        bsg = [S.buf("sg") for _ in range(2)]
        bjunk = S.buf("junk")
        bss = [S.buf("ss") for _ in range(2)]
        brs = [S.buf("rs") for _ in range(2)]
        bnh, bident = S.buf("nh"), S.buf("ident")
        bp_gu = [S.pbuf("pgu") for _ in range(2)]
        bp_dn = [S.pbuf("pdn") for _ in range(4)]
        bp_tr = [S.pbuf("ptr") for _ in range(2)]

        S.op("pool", lambda e: e.memset(nh[:], -0.5), writes=[bnh])
        S.op("pool", lambda e: e.memset(ident[:], 0.0), writes=[bident])
        S.op("pool", lambda e: e.affine_select(out=ident[:], in_=ident[:], pattern=[[-1, 128]],
                                               compare_op=ALU.not_equal, fill=1.0, base=0,
                                               channel_multiplier=1), reads=[bident], writes=[bident])
        S.dma("sp", gB[:], gain_row.to_broadcast([128, D]), bgB, writes=[bgB])
        wg_v = wg.rearrange("(c p) f -> p c f", p=128)
        wu_v = wu.rearrange("(c p) f -> p c f", p=128)
        wd_v = wd.rearrange("(c p) f -> p c f", p=128)
        for bi, (j0, j1) in enumerate(FB):
            f0, f1 = j0 * 128, j1 * 128
            S.dma("pool", Wg[:, :, f0:f1], wg_v[:, :, f0:f1], bWgL[bi], writes=[bWgL[bi]])
            S.dma("pool", Wu[:, :, f0:f1], wu_v[:, :, f0:f1], bWuL[bi], writes=[bWuL[bi]])
            S.dma("pool", Wd[:, j0:j1, :], wd_v[:, j0:j1, :], bWdL[bi], writes=[bWdL[bi]])

        def load(g):
            for s in range(2):
                i = (g % 2) * 2 + s
                t0 = g * G + s * 128
                S.dma("sp", xt[i][:], x_in[t0:t0 + 128, :], bxt[i], reads=[xin_b], writes=[bxt[i]])

        def prep_dve(g):
            for s in range(2):
                i = (g % 2) * 2 + s
                S.op("dve", lambda e: e.scalar_tensor_tensor(out=junk[:], in0=xt[i][:], scalar=1.0, in1=xt[i][:],
                                                             op0=ALU.mult, op1=ALU.mult, accum_out=ss[s][:]),
                     reads=[bxt[i]], writes=[bjunk, bss[s]])
                S.op("dve", lambda e: e.tensor_scalar(out=ss[s][:], in0=ss[s][:], scalar1=1.0 / D, scalar2=eps,
                                                      op0=ALU.mult, op1=ALU.add), reads=[bss[s]], writes=[bss[s]])
                S.op("pool", lambda e: e.tensor_tensor(out=rs[s][:], in0=ss[s][:], in1=nh[:], op=ALU.pow),
                     reads=[bss[s], bnh], writes=[brs[s]])
                S.op("dve", lambda e: e.scalar_tensor_tensor(out=hb[s][:], in0=xt[i][:], scalar=rs[s][:], in1=gB[:],
                                                             op0=ALU.mult, op1=ALU.mult),
                     reads=[bxt[i], brs[s], bgB], writes=[bhb[s]])

        def prep_pe(g):
            for s in range(2):
                for c in range(8):
                    S.op("pe", lambda e: e.transpose(p_tr[s][:, c, :], hb[s][:, c * 128:(c + 1) * 128], ident[:]),
                         reads=[bhb[s], bident], writes=[bp_tr[s]], sig=(c == 7))
                S.op("act", lambda e: e.copy(out=hT[g % 2][:, :, s * 128:(s + 1) * 128], in_=p_tr[s][:]),
                     reads=[bp_tr[s]], writes=[bhT[g % 2]])

        def gate_up(g, j):
            h = hT[g % 2]
            pg = p_gu[j % 2]
            for c in range(8):
                S.op("pe", lambda e: e.matmul(pg[:, 0, :], lhsT=Wg[:, c, j * 128:(j + 1) * 128], rhs=h[:, c, :],
                                              start=(c == 0), stop=(c == 7)),
                     reads=[bWgL[blk_of[j]], bhT[g % 2]], writes=[bp_gu[j % 2]], sig=False)
            for c in range(8):
                S.op("pe", lambda e: e.matmul(pg[:, 1, :], lhsT=Wu[:, c, j * 128:(j + 1) * 128], rhs=h[:, c, :],
                                              start=(c == 0), stop=(c == 7)),
                     reads=[bWuL[blk_of[j]], bhT[g % 2]], writes=[bp_gu[j % 2]], sig=(c == 7))
            S.op("act", lambda e: e.activation(out=sg[j % 2][:], in_=pg[:, 0, :], func=AF.Silu),
                 reads=[bp_gu[j % 2]], writes=[bsg[j % 2]])
            S.op("dve", lambda e: e.tensor_tensor(out=aT[j % 3][:], in0=sg[j % 2][:], in1=pg[:, 1, :], op=ALU.mult),
                 reads=[bsg[j % 2], bp_gu[j % 2]], writes=[baT[j % 3]])

        def down(g, j):
            for s in range(2):
                for hh in range(2):
                    S.op("pe", lambda e: e.matmul(p_dn[s * 2 + hh][:], lhsT=aT[j % 3][:, s * 128:(s + 1) * 128],
                                                  rhs=Wd[:, j, hh * 512:(hh + 1) * 512],
                                                  start=(j == 0), stop=(j == NF - 1)),
                         reads=[baT[j % 3], bWdL[blk_of[j]]], writes=[bp_dn[s * 2 + hh]], sig=(j == NF - 1 or (s == 1 and hh == 1)))

        def epilogue(g):
            for s in range(2):
                i = (g % 2) * 2 + s
                for hh in range(2):
                    S.op("dve", lambda e: e.scalar_tensor_tensor(out=ot[s][:, hh * 512:(hh + 1) * 512], in0=p_dn[s * 2 + hh][:],
                                                                 scalar=0.5, in1=xt[i][:, hh * 512:(hh + 1) * 512],
                                                                 op0=ALU.mult, op1=ALU.add),
                         reads=[bp_dn[s * 2 + hh], bxt[i]], writes=[bot[s]])
                t0 = g * G + s * 128
                S.dma("sp", x_out[t0:t0 + 128, :], ot[s][:], bot[s], reads=[bot[s]], writes=[xout_b])

        load(0)
        if ng > 1:
            load(1)
        prep_dve(0)
        prep_pe(0)
        for g in range(ng):
            gate_up(g, 0)
            for j in range(NF):
                if j + 1 < NF:
                    gate_up(g, j + 1)
                elif g + 1 < ng:
                    pass
                down(g, j)
                if j == 3 and g + 1 < ng:
                    prep_dve(g + 1)
                if j == 14 and g + 1 < ng:
                    prep_pe(g + 1)
            epilogue(g)
            if g + 2 < ng:
                load(g + 2)
        S.barrier()
        for b in bWgL + bWuL + bWdL + [bgB] + bxt + bot:
            S.release(b)


def make_ident(nc, S, ident, bident, dt_is_bf16=True):
    S.op("pool", lambda e: e.memset(ident[:], 0.0), writes=[bident])
    S.op("pool", lambda e: e.affine_select(out=ident[:], in_=ident[:], pattern=[[-1, 128]],
                                           compare_op=ALU.not_equal, fill=1.0, base=0,
                                           channel_multiplier=1), reads=[bident], writes=[bident])


def attn_in_phase(nc, S, x_in, xin_b, w_in, gain_row, qn, kn, QT, KT, V, bQT, bKT, bV, ntok, eps=1e-6):
    G = 512
    ng = ntok // G
    with ExitStack() as es:
        sb = lambda name, shape, dt: es.enter_context(nc.sbuf_tensor(_uniq(name), shape, dt))
        ps = lambda name, shape, dt: es.enter_context(nc.psum_tensor(_uniq(name), shape, dt))
        Win = sb("Win", [128, 8, 3072], BF16)
        gB = sb("gB", [128, D], F32)
        ident = sb("ident", [128, 128], BF16)
        bones = sb("bones", [128, 128], BF16)
        gq = sb("gq", [128, 1], F32)
        gk = sb("gk", [128, 1], F32)
        nh = sb("nh", [128, 1], F32)
        eps_ap = sb("eps_ap", [128, 1], F32)
        xt = [sb("xt%d" % i, [128, D], F32) for i in range(4)]
        hb = [sb("hb%d" % i, [128, D], BF16) for i in range(2)]
        hT = [sb("hT%d" % i, [128, 8, G], BF16) for i in range(2)]
        junk = sb("junk", [128, D], BF16)
        ss = [sb("ss%d" % i, [128, 1], F32) for i in range(2)]
        rs = [sb("rs%d" % i, [128, 1], F32) for i in range(2)]
        ob = [sb("ob%d" % i, [128, G], BF16) for i in range(3)]
        sq = [sb("sq%d" % i, [128, G], BF16) for i in range(2)]
        lt = [sb("lt%d" % i, [128, G], F32) for i in range(2)]
        rr = [sb("rr%d" % i, [128, G], F32) for i in range(2)]
        vb = [sb("vb%d" % i, [128, D], BF16) for i in range(2)]
        p_q = [ps("p_q%d" % i, [128, G], F32) for i in range(3)]
        p_s = [ps("p_s%d" % i, [128, G], F32) for i in range(1)]
        p_v = [ps("p_v%d" % i, [128, 512], F32) for i in range(2)]
        p_tr = [ps("p_tr%d" % i, [128, 8, 128], BF16) for i in range(2)]

        bWin, bgB, bgq, bgk = S.dbuf("Win"), S.dbuf("gB"), S.dbuf("gq"), S.dbuf("gk")
        bxt = [S.dbuf("xt") for _ in range(4)]
        bob = [S.dbuf("ob") for _ in range(3)]
        bvb = [S.dbuf("vb") for _ in range(2)]
        bhb = [S.buf("hb") for _ in range(2)]
        bhT = [S.buf("hT") for _ in range(2)]
        bjunk, bnh, bident, bbones = S.buf("junk"), S.buf("nh"), S.buf("ident"), S.buf("bones")
        bss = [S.buf("ss") for _ in range(2)]
        brs = [S.buf("rs") for _ in range(2)]
        bsq = [S.buf("sq") for _ in range(2)]
        blt = [S.buf("lt") for _ in range(2)]
        brr = [S.buf("rr") for _ in range(2)]
        bp_q = [S.pbuf("pq") for _ in range(3)]
        bp_s = [S.pbuf("ps") for _ in range(1)]
        bp_v = [S.pbuf("pv") for _ in range(2)]
        bp_tr = [S.pbuf("ptr") for _ in range(2)]

        S.op("pool", lambda e: e.memset(nh[:], -0.5), writes=[bnh])
        S.op("pool", lambda e: e.memset(eps_ap[:], eps), writes=[bnh])
        make_ident(nc, S, ident, bident)
        S.op("pool", lambda e: e.memset(bones[:], 0.0), writes=[bbones])
        S.op("pool", lambda e: e.memset(bones[0:64, 0:64], 1.0), writes=[bbones])
        S.op("pool", lambda e: e.memset(bones[64:128, 64:128], 1.0), writes=[bbones])
        S.dma("sp", gB[:], gain_row.to_broadcast([128, D]), bgB, writes=[bgB])
        for hh in range(2):
            S.dma("sp", gq[hh * 64:(hh + 1) * 64, :], qn, bgq, writes=[bgq])
            S.dma("sp", gk[hh * 64:(hh + 1) * 64, :], kn, bgk, writes=[bgk])
        S.op("dve", lambda e: e.tensor_scalar(out=gq[:], in0=gq[:], scalar1=0.125, scalar2=None, op0=ALU.mult),
             reads=[bgq], writes=[bgq])
        w_v = w_in.rearrange("(c p) f -> p c f", p=128)
        for c in range(8):
            S.dma("pool", Win[:, c, :], w_v[:, c, :], bWin, writes=[bWin])

        def load(g):
            for s in range(4):
                t0 = g * G + s * 128
                S.dma("sp", xt[s][:], x_in[t0:t0 + 128, :], bxt[s], reads=[xin_b], writes=[bxt[s]])

        def prep(g):
            for s in range(4):
                k = s % 2
                S.op("dve", lambda e: e.scalar_tensor_tensor(out=junk[:], in0=xt[s][:], scalar=1.0, in1=xt[s][:],
                                                             op0=ALU.mult, op1=ALU.mult, accum_out=ss[k][:]),
                     reads=[bxt[s]], writes=[bjunk, bss[k]])
                S.op("dve", lambda e: e.tensor_scalar(out=ss[k][:], in0=ss[k][:], scalar1=1.0 / D, scalar2=eps,
                                                      op0=ALU.mult, op1=ALU.add), reads=[bss[k]], writes=[bss[k]])
                S.op("pool", lambda e: e.tensor_tensor(out=rs[k][:], in0=ss[k][:], in1=nh[:], op=ALU.pow),
                     reads=[bss[k], bnh], writes=[brs[k]])
                S.op("dve", lambda e: e.scalar_tensor_tensor(out=hb[k][:], in0=xt[s][:], scalar=rs[k][:], in1=gB[:],
                                                             op0=ALU.mult, op1=ALU.mult),
                     reads=[bxt[s], brs[k], bgB], writes=[bhb[k]])
                for c in range(8):
                    S.op("pe", lambda e: e.transpose(p_tr[k][:, c, :], hb[k][:, c * 128:(c + 1) * 128], ident[:]),
                         reads=[bhb[k], bident], writes=[bp_tr[k]], sig=(c == 7))
                S.op("act", lambda e: e.copy(out=hT[g % 2][:, :, s * 128:(s + 1) * 128], in_=p_tr[k][:]),
                     reads=[bp_tr[k]], writes=[bhT[g % 2]])

        nob = 0

        def _dummy():
            pass
        load(0)
        prep(0)
        for g in range(ng):
            h = hT[g % 2]
            bh = bhT[g % 2]
            t0 = g * G
            def fcinfo(fc):
                isq = fc < 8
                ch = fc % 8
                if ch < 2:
                    col0 = (0 if isq else 256) + ch * 128
                else:
                    col0 = (768 if isq else 1536) + (ch - 2) * 128
                return isq, ch, col0

            def main(fc):
                isq, ch, col0 = fcinfo(fc)
                pq = p_q[fc % 3]
                for c in range(8):
                    S.op("pe", lambda e: e.matmul(pq[:], lhsT=Win[:, c, col0:col0 + 128], rhs=h[:, c, :],
                                                  start=(c == 0), stop=(c == 7)),
                         reads=[bWin, bh], writes=[bp_q[fc % 3]], sig=(c == 7))

            def tail(fc):
                nonlocal nob
                isq, ch, col0 = fcinfo(fc)
                pq = p_q[fc % 3]
                bpq = bp_q[fc % 3]
                o = ob[nob % 3]
                bo = bob[nob % 3]
                nob += 1
                if ch < 2:
                    S.op("act", lambda e: e.activation(out=o[:], in_=pq[:], func=AF.Copy, scale=(0.125 if isq else 1.0)),
                         reads=[bpq], writes=[bo])
                else:
                    k = fc % 2
                    S.op("act", lambda e: e.activation(out=sq[k][:], in_=pq[:], func=AF.Square),
                         reads=[bpq], writes=[bsq[k]])
                    S.op("pe", lambda e: e.matmul(p_s[0][:], lhsT=bones[:], rhs=sq[k][:], start=True, stop=True),
                         reads=[bbones, bsq[k]], writes=[bp_s[0]])
                    S.op("act", lambda e: e.activation(out=lt[k][:], in_=p_s[0][:], func=AF.Ln, scale=1.0 / 64, bias=eps_ap[:]),
                         reads=[bp_s[0], bnh], writes=[blt[k]])
                    S.op("act", lambda e: e.activation(out=rr[k][:], in_=lt[k][:], func=AF.Exp, scale=-0.5),
                         reads=[blt[k]], writes=[brr[k]])
                    gcol = gq if isq else gk
                    S.op("dve", lambda e: e.scalar_tensor_tensor(out=o[:], in0=pq[:], scalar=gcol[:], in1=rr[k][:],
                                                                 op0=ALU.mult, op1=ALU.mult),
                         reads=[bpq, brr[k], bgq, bgk], writes=[bo])
                dst, bdst = (QT, bQT) if isq else (KT, bKT)
                S.dma("sp", dst[ch * 128:(ch + 1) * 128, t0:t0 + G], o[:], bo, reads=[bo], writes=[bdst])

            main(0)
            for fc in range(16):
                if fc + 1 < 16:
                    main(fc + 1)
                tail(fc)
                if fc == 1 and g + 1 < ng:
                    load(g + 1)
                if fc == 6 and g + 1 < ng:
                    prep(g + 1)
            for s in range(4):
                k = s % 2
                for (pv, cols, off) in ((p_v[0], (512, 768), 0), (p_v[0], (2304, 2560), 256), (p_v[1], (2560, 3072), 0)):
                    n = cols[1] - cols[0]
                    for c in range(8):
                        S.op("pe", lambda e: e.matmul(pv[:, off:off + n], lhsT=h[:, c, s * 128:(s + 1) * 128],
                                                      rhs=Win[:, c, cols[0]:cols[1]], start=(c == 0), stop=(c == 7)),
                             reads=[bWin, bh], writes=[bp_v[0], bp_v[1]], sig=(c == 7))
                S.op("act", lambda e: e.copy(out=vb[k][:, 0:512], in_=p_v[0][:]), reads=[bp_v[0]], writes=[bvb[k]])
                S.op("dve", lambda e: e.tensor_copy(out=vb[k][:, 512:1024], in_=p_v[1][:]), reads=[bp_v[1]], writes=[bvb[k]])
                S.dma("sp", V[t0 + s * 128:t0 + (s + 1) * 128, :], vb[k][:], bvb[k], reads=[bvb[k]], writes=[bV])
        S.barrier()
        for b in [bWin, bgB, bgq, bgk] + bxt + bob + bvb:
            S.release(b)


def sb_attn_phase(nc, S, QT, KT, V, MT, bQT, bKT, bV, bMT, ntok):
    nblk = ntok // 128
    with ExitStack() as es:
        sb = lambda name, shape, dt: es.enter_context(nc.sbuf_tensor(_uniq(name), shape, dt))
        ps = lambda name, shape, dt: es.enter_context(nc.psum_tensor(_uniq(name), shape, dt))
        qT = sb("qT", [128, 2, ntok], BF16)
        kT = sb("kT", [128, 2, ntok], BF16)
        v = sb("v", [128, nblk, 256], BF16)
        ones = sb("ones", [128, 512], F32)
        onec = sb("onec", [128, 1], F32)
        mneg = sb("mneg", [128, 128], BF16)
        ident = sb("ident", [128, 128], BF16)
        mk2 = lambda nm, shape, dt: [[sb("%s%d_%d" % (nm, h, i), shape, dt) for i in range(2)] for h in range(2)]
        e_ = mk2("e", [128, 512], F32)
        sp_ = mk2("sp", [128, 512], F32)
        cs_ = mk2("cs", [128, 512], F32)
        lw_ = mk2("lw", [128, 512], F32)
        w_ = mk2("w", [128, 512], BF16)
        wT_ = mk2("wT", [128, 4, 128], BF16)
        oT = [sb("oT%d" % i, [128, 512], BF16) for i in range(2)]
        p_z = [[ps("p_z%d_%d" % (h, i), [128, 512], F32) for i in range(2)] for h in range(2)]
        p_w = [ps("p_w%d" % i, [128, 4, 128], BF16) for i in range(2)]
        p_o = [ps("p_o%d" % i, [128, 128], F32) for i in range(2)]

        bq, bk, bv = S.dbuf("qT"), S.dbuf("kT"), S.dbuf("v")
        boT = [S.dbuf("oT") for _ in range(2)]
        bones, bmneg, bident = S.buf("ones"), S.buf("mneg"), S.buf("ident")
        bb2 = lambda nm: [[S.buf(nm) for _ in range(2)] for _ in range(2)]
        be, bsp, bcs, blw, bw, bwT = bb2("e"), bb2("sp"), bb2("cs"), bb2("lw"), bb2("w"), bb2("wT")
        bp_z = [[S.pbuf("pz") for _ in range(2)] for _ in range(2)]
        bp_w = [S.pbuf("pw") for _ in range(2)]
        bp_o = [S.pbuf("po") for _ in range(2)]

        S.op("pool", lambda e: e.memset(ones[:], 0.0), writes=[bones])
        S.op("pool", lambda e: e.memset(onec[:], 1.0), writes=[bones])
        make_ident(nc, S, ident, bident)
        S.op("pool", lambda e: e.memset(mneg[:], 0.0), writes=[bmneg])
        S.op("pool", lambda e: e.affine_select(out=mneg[:], in_=mneg[:], pattern=[[-1, 128]], compare_op=ALU.is_gt,
                                               fill=-30000.0, base=0, channel_multiplier=1),
             reads=[bmneg], writes=[bmneg])
        for pr in range(2):
            S.dma("sp", qT[:, pr, :], QT[pr * 128:(pr + 1) * 128, 0:ntok], bq, reads=[bQT], writes=[bq])
            S.dma("sp", kT[:, pr, :], KT[pr * 128:(pr + 1) * 128, 0:ntok], bk, reads=[bKT], writes=[bk])
        S.dma("sp", v[:], V[0:ntok, 0:256].rearrange("(b p) c -> p b c", p=128), bv, reads=[bV], writes=[bv])

        steps = []
        for pr in range(2):
            for qb in range(nblk):
                chunks = [(4 * (qb // 4), qb + 1, True)]
                for c in range(qb // 4 - 1, -1, -1):
                    chunks.append((4 * c, 4 * c + 4, False))
                for ci, (b0, b1, diag) in enumerate(chunks):
                    steps.append((pr, qb, ci, b0, b1, diag, len(chunks)))
        HH = [(0, slice(0, 64)), (1, slice(64, 128))]

        def front(t):
            pr, qb, ci, b0, b1, diag, nchk = steps[t]
            W = (b1 - b0) * 128
            i = t % 2
            for (hh, P) in HH:
                pz, bpz = p_z[hh][i], bp_z[hh][i]
                S.op("pe", lambda e: e.matmul(pz[:, 0:W], lhsT=qT[P, pr, qb * 128:(qb + 1) * 128],
                                              rhs=kT[P, pr, b0 * 128:b1 * 128], start=True, stop=(not diag)),
                     reads=[bq, bk], writes=[bpz], sig=(not diag))
                if diag:
                    S.op("pe", lambda e: e.matmul(pz[:, W - 128:W], lhsT=ident[:], rhs=mneg[:], start=False, stop=True),
                         reads=[bident, bmneg], writes=[bpz])
            for (hh, P) in HH:
                S.op("act", lambda e: e.activation(out=e_[hh][i][:, 0:W], in_=p_z[hh][i][:, 0:W], func=AF.Sigmoid),
                     reads=[bp_z[hh][i]], writes=[be[hh][i]])
            for (hh, P) in HH:
                S.op("act", lambda e: e.activation(out=sp_[hh][i][:, 0:W], in_=p_z[hh][i][:, 0:W], func=AF.Sigmoid, scale=-1.0),
                     reads=[bp_z[hh][i]], writes=[bsp[hh][i]])

        def back(t):
            pr, qb, ci, b0, b1, diag, nchk = steps[t]
            W = (b1 - b0) * 128
            nb = b1 - b0
            i = t % 2
            rev = (lambda tt: tt[:, W - 1::-1] if W < 512 else tt[:, ::-1])
            for (hh, P) in HH:
                if ci == 0:
                    init, rd = 1.0, [bsp[hh][i], bones]
                else:
                    init, rd = cs_[hh][1 - i][:, 0:1], [bsp[hh][i], bones, bcs[hh][1 - i]]
                S.op("dve", lambda e: e.tensor_tensor_scan(out=rev(cs_[hh][i]), data0=rev(sp_[hh][i]), data1=ones[:, 0:W],
                                                           initial=init, op0=ALU.mult, op1=ALU.add),
                     reads=rd, writes=[bcs[hh][i]])
            for (hh, P) in HH:
                if W > 1:
                    S.op("dve", lambda e: e.tensor_tensor(out=w_[hh][i][:, 0:W - 1], in0=e_[hh][i][:, 0:W - 1], in1=cs_[hh][i][:, 1:W], op=ALU.mult),
                         reads=[be[hh][i], bcs[hh][i]], writes=[bw[hh][i]])
                if ci == 0:
                    S.op("dve", lambda e: e.tensor_copy(out=w_[hh][i][:, W - 1:W], in_=e_[hh][i][:, W - 1:W]),
                         reads=[be[hh][i]], writes=[bw[hh][i]])
                else:
                    S.op("dve", lambda e: e.tensor_scalar(out=w_[hh][i][:, W - 1:W], in0=e_[hh][i][:, W - 1:W], scalar1=cs_[hh][1 - i][:, 0:1],
                                                          scalar2=None, op0=ALU.mult),
                         reads=[be[hh][i], bcs[hh][1 - i]], writes=[bw[hh][i]])
            for (hh, P) in HH:
                pw, bpw = p_w[hh], bp_w[hh]
                for b in range(nb):
                    S.op("pe", lambda e: e.transpose(pw[:, b, :], w_[hh][i][:, b * 128:(b + 1) * 128], ident[:]),
                         reads=[bw[hh][i], bident], writes=[bpw], sig=(b == nb - 1))
                if hh == 0:
                    S.op("act", lambda e: e.copy(out=wT_[hh][i][:, 0:nb, :], in_=pw[:, 0:nb, :]), reads=[bpw], writes=[bwT[hh][i]])
                else:
                    S.op("dve", lambda e: e.tensor_copy(out=wT_[hh][i][:, 0:nb, :], in_=pw[:, 0:nb, :]), reads=[bpw], writes=[bwT[hh][i]])
            po, bpo = p_o[qb % 2], bp_o[qb % 2]
            for (hh, P) in HH:
                h = 2 * pr + hh
                for b in range(nb):
                    first = (ci == 0 and b == 0)
                    last = (ci == nchk - 1 and b == nb - 1)
                    S.op("pe", lambda e: e.matmul(po[P, :], lhsT=v[:, b0 + b, h * 64:(h + 1) * 64], rhs=wT_[hh][i][:, b, :],
                                                  start=first, stop=last),
                         reads=[bv, bwT[hh][i]], writes=[bpo], sig=(b == nb - 1))
            if ci == nchk - 1:
                k = (qb // 4) % 2
                S.op("dve", lambda e: e.tensor_copy(out=oT[k][:, (qb % 4) * 128:(qb % 4 + 1) * 128], in_=po[:, :]),
                     reads=[bpo], writes=[boT[k]])
                if qb % 4 == 3 or qb == nblk - 1:
                    q0 = 4 * (qb // 4)
                    n = (qb - q0 + 1) * 128
                    S.dma("sp", MT[pr * 128:(pr + 1) * 128, q0 * 128:q0 * 128 + n], oT[k][:, 0:n], boT[k],
                          reads=[boT[k]], writes=[bMT])

        front(0)
        for t in range(len(steps)):
            if t + 1 < len(steps):
                front(t + 1)
            back(t)
        S.barrier()
        for b_ in [bq, bk, bv] + boT:
            S.release(b_)


def dil_bias_host(rel_bias):
    out = np.empty((12, 128, 2, 128), np.float32)
    kj = np.arange(128)[:, None]
    q = np.arange(128)[None, :]
    for g, r in enumerate((1, 4, 16)):
        for part, dist in ((1, q - kj), (0, q + 128 - kj)):
            valid = (dist >= 0) & (dist <= 128)
            dd = np.maximum(dist, 0) * r
            d = np.maximum(dd, 1).astype(np.float32)
            large = 16 + (np.log(d / np.float32(16)) / np.float32(np.log(2048 / 16)) * np.float32(16)).astype(np.int32)
            large = np.minimum(large, 31)
            bucket = np.where(dd < 16, dd, large)
            for j in range(4):
                hd = 4 * g + j
                out[hd, :, part, :] = np.where(valid, rel_bias[bucket, hd], np.float32(-30000.0))
    return out


def dil_attn_phase(nc, S, QT, KT, V, MT, dbias, bQT, bKT, bV, bMT, ntok):
    with ExitStack() as es:
        sb = lambda name, shape, dt: es.enter_context(nc.sbuf_tensor(_uniq(name), shape, dt))
        ps = lambda name, shape, dt: es.enter_context(nc.psum_tensor(_uniq(name), shape, dt))
        qT = sb("qT", [128, 2, ntok], BF16)
        kT = sb("kT", [128, 2, ntok], BF16)
        v = sb("v", [128, ntok // 128, 256], BF16)
        bias = sb("bias", [128, 4, 256], F32)
        onesb = sb("onesb", [128, 64], BF16)
        Nacc = sb("Nacc", [128, 2, ntok], F32)
        Dacc = sb("Dacc", [128, 2, ntok], F32)
        s_ = [sb("s%d" % i, [128, 256], F32) for i in range(3)]
        pT_ = [sb("pT%d" % i, [128, 2, 128], BF16) for i in range(3)]
        ob = [sb("ob%d" % i, [128, 1024], BF16) for i in range(2)]
        p_s = [ps("p_s%d" % i, [128, 2, 128], F32) for i in range(2)]
        p_n = [ps("p_n%d" % i, [128, 128], F32) for i in range(2)]
        p_d = [ps("p_d%d" % i, [128, 128], F32) for i in range(2)]
        p_pad = [ps("p_pad%d" % i, [128, 256], F32) for i in range(0)]

        bq, bk, bv, bbias = S.dbuf("qT"), S.dbuf("kT"), S.dbuf("v"), S.dbuf("bias")
        bob = [S.dbuf("ob") for _ in range(2)]
        bones, bN, bD = S.buf("ones"), S.buf("N"), S.buf("D")
        bs = [S.buf("s") for _ in range(3)]
        bpT = [S.buf("pT") for _ in range(3)]
        bp_s = [S.pbuf("ps") for _ in range(2)]
        bp_n = [S.pbuf("pn") for _ in range(2)]
        bp_d = [S.pbuf("pd") for _ in range(2)]

        S.op("pool", lambda e: e.memset(onesb[:], 1.0), writes=[bones])
        u = 0
        for g, r in enumerate((1, 4, 16)):
            L = ntok // r
            nb = L // 128
            for pr in range(2):
                r0 = 256 + (2 * g + pr) * 128
                S.dma("sp", qT[:, pr, :], QT[r0:r0 + 128, 0:ntok], bq, reads=[bQT], writes=[bq])
                S.dma("sp", kT[:, pr, :], KT[r0:r0 + 128, 0:ntok], bk, reads=[bKT], writes=[bk])
            vsrc = V[0:ntok, 256 + g * 256:256 + (g + 1) * 256].rearrange("(n i c) f -> c i n f", i=128, c=r)
            for c in range(r):
                S.dma("sp", v[:, c * nb:(c + 1) * nb, :], vsrc[c], bv, reads=[bV], writes=[bv])
            for j in range(4):
                S.dma("sp", bias[:, j, :], dbias[4 * g + j].rearrange("k a q -> k (a q)"), bbias, writes=[bbias])
            units = []
            for j in range(4):
                for c in range(r):
                    for n in range(nb):
                        units.append((j, c, n))

            def tokf(c, nn):
                st = c + r * 128 * nn
                return slice(st, st + r * 127 + 1, r)

            def front(uu, u):
                j, c, n = units[uu]
                pr, hh = j // 2, j % 2
                P = slice(64 * hh, 64 * hh + 64)
                i = u % 3
                pss, bpss = p_s[u % 2], bp_s[u % 2]
                a0 = 0 if n > 0 else 1
                if n > 0:
                    S.op("pe", lambda e: e.matmul(pss[:, 0, :], lhsT=kT[P, pr, tokf(c, n - 1)], rhs=qT[P, pr, tokf(c, n)],
                                                  start=True, stop=True), reads=[bq, bk], writes=[bpss], sig=False)
                S.op("pe", lambda e: e.matmul(pss[:, 1, :], lhsT=kT[P, pr, tokf(c, n)], rhs=qT[P, pr, tokf(c, n)],
                                              start=True, stop=True), reads=[bq, bk], writes=[bpss])
                S.op("dve", lambda e: e.tensor_tensor(out=s_[i][:, a0 * 128:256], in0=pss[:, a0:2, :],
                                                      in1=bias[:, j, a0 * 128:256], op=ALU.add),
                     reads=[bpss, bbias], writes=[bs[i]])
                S.op("act", lambda e: e.activation(out=pT_[i][:, a0:2, :], in_=s_[i][:, a0 * 128:256], func=AF.Exp),
                     reads=[bs[i]], writes=[bpT[i]])

            def back(uu, u):
                j, c, n = units[uu]
                pr, hh = j // 2, j % 2
                P = slice(64 * hh, 64 * hh + 64)
                i = u % 3
                pn, bpn = p_n[u % 2], bp_n[u % 2]
                pd, bpd = p_d[u % 2], bp_d[u % 2]
                a0 = 0 if n > 0 else 1
                for a_ in range(a0, 2):
                    S.op("pe", lambda e: e.matmul(pn[P, :], lhsT=v[:, c * nb + n - 1 + a_, j * 64:(j + 1) * 64],
                                                  rhs=pT_[i][:, a_, :], start=(a_ == a0), stop=(a_ == 1)),
                         reads=[bv, bpT[i]], writes=[bpn], sig=(a_ == 1))
                for a_ in range(a0, 2):
                    S.op("pe", lambda e: e.matmul(pd[P, :], lhsT=onesb[:, :], rhs=pT_[i][:, a_, :],
                                                  start=(a_ == a0), stop=(a_ == 1)),
                         reads=[bones, bpT[i]], writes=[bpd], sig=(a_ == 1))
                tk = tokf(c, n)
                if g == 0:
                    S.op("act", lambda e: e.copy(out=Nacc[P, pr, tk], in_=pn[P, :]), reads=[bpn], writes=[bN])
                    S.op("dve", lambda e: e.tensor_copy(out=Dacc[P, pr, tk], in_=pd[P, :]), reads=[bpd], writes=[bD])
                else:
                    S.op("dve", lambda e: e.tensor_tensor(out=Nacc[P, pr, tk], in0=pn[P, :], in1=Nacc[P, pr, tk],
                                                          op=ALU.add), reads=[bpn, bN], writes=[bN])
                    S.op("dve", lambda e: e.tensor_tensor(out=Dacc[P, pr, tk], in0=pd[P, :], in1=Dacc[P, pr, tk],
                                                          op=ALU.add), reads=[bpd, bD], writes=[bD])

            front(0, u)
            for uu in range(len(units)):
                if uu + 1 < len(units):
                    front(uu + 1, u + 1)
                back(uu, u)
                u += 1
        k = 0
        for pr in range(2):
            for c0 in range(0, ntok, 1024):
                n = min(1024, ntok - c0)
                S.op("dve", lambda e: e.reciprocal(out=Dacc[:, pr, c0:c0 + n], in_=Dacc[:, pr, c0:c0 + n]), reads=[bD], writes=[bD])
                S.op("dve", lambda e: e.tensor_tensor(out=ob[k % 2][:, 0:n], in0=Nacc[:, pr, c0:c0 + n], in1=Dacc[:, pr, c0:c0 + n],
                                                      op=ALU.mult), reads=[bN, bD], writes=[bob[k % 2]])
                S.dma("sp", MT[256 + pr * 128:256 + (pr + 1) * 128, c0:c0 + n], ob[k % 2][:, 0:n], bob[k % 2],
                      reads=[bob[k % 2]], writes=[bMT])
                k += 1
        S.barrier()
        for b in [bq, bk, bv, bbias] + bob:
            S.release(b)


def out_proj_phase(nc, S, x_in, x_out, xin_b, xout_b, MT, bMT, w_out, kdim, ntok):
    G = 512
    ng = ntok // G
    kc = kdim // 128
    nt = ntok // 128
    with ExitStack() as es:
        sb = lambda name, shape, dt: es.enter_context(nc.sbuf_tensor(_uniq(name), shape, dt))
        ps = lambda name, shape, dt: es.enter_context(nc.psum_tensor(_uniq(name), shape, dt))
        Wo = sb("Wo", [128, kc, D], BF16)
        mT = [sb("mT%d" % i, [128, kc, G], BF16) for i in range(2)]
        xt = [sb("xt%d" % i, [128, D], F32) for i in range(4)]
        ot = [sb("ot%d" % i, [128, D], F32) for i in range(3)]
        p_y = [ps("p_y%d" % i, [128, 512], F32) for i in range(4)]
        bWo = S.dbuf("Wo")
        bmT = [S.dbuf("mT") for _ in range(2)]
        bxt = [S.dbuf("xt") for _ in range(4)]
        bot = [S.dbuf("ot") for _ in range(3)]
        bp_y = [S.pbuf("py") for _ in range(4)]
        w_v = w_out.rearrange("(c p) f -> p c f", p=128)
        for c in range(kc):
            S.dma("pool", Wo[:, c, :], w_v[:, c, :], bWo, writes=[bWo])

        def load_m(g):
            m, bm = mT[g % 2], bmT[g % 2]
            for c in range(kc):
                S.dma("sp", m[:, c, :], MT[c * 128:(c + 1) * 128, g * G:(g + 1) * G], bm, reads=[bMT], writes=[bm])

        def load_x(k):
            S.dma("sp", xt[k % 4][:], x_in[k * 128:(k + 1) * 128, :], bxt[k % 4], reads=[xin_b], writes=[bxt[k % 4]])

        load_m(0)
        load_x(0)
        load_x(1)
        if ng > 1:
            load_m(1)
        for k in range(nt):
            g, s = k // 4, k % 4
            m, bm = mT[g % 2], bmT[g % 2]
            if k + 2 < nt:
                load_x(k + 2)
            x_, bx = xt[k % 4], bxt[k % 4]
            o_, bo = ot[k % 3], bot[k % 3]
            for hh in range(2):
                py, bpy = p_y[(2 * k + hh) % 4], bp_y[(2 * k + hh) % 4]
                for c in range(kc):
                    S.op("pe", lambda e: e.matmul(py[:], lhsT=m[:, c, s * 128:(s + 1) * 128], rhs=Wo[:, c, hh * 512:(hh + 1) * 512],
                                                  start=(c == 0), stop=(c == kc - 1)),
                         reads=[bm, bWo], writes=[bpy], sig=(c == kc - 1))
                S.op("dve", lambda e: e.tensor_tensor(out=o_[:, hh * 512:(hh + 1) * 512], in0=py[:], in1=x_[:, hh * 512:(hh + 1) * 512],
                                                      op=ALU.add), reads=[bpy, bx], writes=[bo])
            S.dma("act", x_out[k * 128:(k + 1) * 128, :], o_[:], bo, reads=[bo], writes=[xout_b])
            if s == 3 and g + 2 < ng:
                load_m(g + 2)
        S.barrier()
        for b_ in [bWo] + bmT + bxt + bot:
            S.release(b_)


C0 = float(np.exp(-0.5))


class RwPrep:
    def __init__(self, nc, S, es, G, mix_ids, x_in, xin_b, gain_row, mix, eps=1e-6):
        sb = lambda name, shape, dt: es.enter_context(nc.sbuf_tensor(_uniq(name), shape, dt))
        ps = lambda name, shape, dt: es.enter_context(nc.psum_tensor(_uniq(name), shape, dt))
        self.nc, self.S, self.G, self.mix_ids, self.x_in, self.xin_b, self.eps = nc, S, G, mix_ids, x_in, xin_b, eps
        self.gB = sb("gB", [128, D], F32)
        self.identf = sb("identf", [128, 128], F32)
        self.nh = sb("nh", [128, 1], F32)
        self.mixc = sb("mixc", [128, 6, 8], F32)
        self.xt = [sb("xt%d" % i, [128, D], F32) for i in range(2)]
        self.hn = [sb("hn%d" % i, [128, D], F32) for i in range(2)]
        self.junk = sb("junk", [128, D], BF16)
        self.ss = [sb("ss%d" % i, [128, 1], F32) for i in range(2)]
        self.rs = [sb("rs%d" % i, [128, 1], F32) for i in range(2)]
        self.hT = [sb("hT%d" % i, [128, 8, G + 1], F32) for i in range(2)]
        self.xx = [sb("xx%d" % i, [128, G], F32) for i in range(2)]
        self.xm = {i: sb("xm%d" % i, [128, 8, G], BF16) for i in mix_ids}
        self.p_tr = [ps("p_tr%d" % i, [128, 4, 128], F32) for i in range(2)]
        self.bgB, self.bmixc = S.dbuf("gB"), S.dbuf("mixc")
        self.bxt = [S.dbuf("xt") for _ in range(2)]
        self.bhn = [S.buf("hn") for _ in range(2)]
        self.bident, self.bnh, self.bjunk = S.buf("identf"), S.buf("nh"), S.buf("junk")
        self.bss = [S.buf("ss") for _ in range(2)]
        self.brs = [S.buf("rs") for _ in range(2)]
        self.bhT = [S.buf("hT") for _ in range(2)]
        self.bxx = [S.buf("xx") for _ in range(2)]
        self.bxm = {i: S.buf("xm") for i in mix_ids}
        self.bp_tr = [S.pbuf("ptr") for _ in range(2)]
        S.op("pool", lambda e: e.memset(self.nh[:], -0.5), writes=[self.bnh])
        make_ident(nc, S, self.identf, self.bident)
        S.dma("sp", self.gB[:], gain_row.to_broadcast([128, D]), self.bgB, writes=[self.bgB])
        for i in range(6):
            S.dma("sp", self.mixc[:, i, :], mix[i:i + 1, :].rearrange("o (c p) -> p (o c)", p=128), self.bmixc, writes=[self.bmixc], slow=True)
        S.op("dve", lambda e: e.memset(self.hT[1][:, :, G:G + 1], 0.0), writes=[self.bhT[1]])
        self.dsems = [self.bgB, self.bmixc] + self.bxt

    def group(self, g):
        nc, S, G = self.nc, self.S, self.G
        hT, bhT = self.hT[g % 2], self.bhT[g % 2]
        hTp, bhTp = self.hT[(g + 1) % 2], self.bhT[(g + 1) % 2]
        S.op("pool", lambda e: e.tensor_copy(out=hT[:, :, 0:1], in_=hTp[:, :, G:G + 1]), reads=[bhTp], writes=[bhT])
        k = 0
        for s in range(G // 128):
            t0 = g * G + s * 128
            x_, bx = self.xt[s % 2], self.bxt[s % 2]
            h_, bh = self.hn[s % 2], self.bhn[s % 2]
            ss, bss, rs, brs = self.ss[s % 2], self.bss[s % 2], self.rs[s % 2], self.brs[s % 2]
            S.dma("sp", x_[:], self.x_in[t0:t0 + 128, :], bx, reads=[self.xin_b], writes=[bx])
            S.op("dve", lambda e: e.scalar_tensor_tensor(out=self.junk[:], in0=x_[:], scalar=1.0, in1=x_[:], op0=ALU.mult, op1=ALU.mult,
                                                         accum_out=ss[:]), reads=[bx], writes=[self.bjunk, bss])
            S.op("dve", lambda e: e.tensor_scalar(out=ss[:], in0=ss[:], scalar1=1.0 / D, scalar2=self.eps, op0=ALU.mult, op1=ALU.add),
                 reads=[bss], writes=[bss])
            S.op("pool", lambda e: e.tensor_tensor(out=rs[:], in0=ss[:], in1=self.nh[:], op=ALU.pow), reads=[bss, self.bnh], writes=[brs])
            S.op("dve", lambda e: e.scalar_tensor_tensor(out=h_[:], in0=x_[:], scalar=rs[:], in1=self.gB[:], op0=ALU.mult, op1=ALU.mult),
                 reads=[bx, brs, self.bgB], writes=[bh])
            for half in range(2):
                pt, bpt = self.p_tr[k % 2], self.bp_tr[k % 2]
                k += 1
                for c4 in range(4):
                    c = half * 4 + c4
                    S.op("pe", lambda e: e.transpose(pt[:, c4, :], h_[:, c * 128:(c + 1) * 128], self.identf[:]),
                         reads=[bh, self.bident], writes=[bpt], sig=(c4 == 3))
                S.op("act", lambda e: e.copy(out=hT[:, half * 4:half * 4 + 4, 1 + s * 128:1 + (s + 1) * 128], in_=pt[:]),
                     reads=[bpt], writes=[bhT])
        for c in range(8):
            xx, bxx = self.xx[c % 2], self.bxx[c % 2]
            S.op("dve", lambda e: e.tensor_tensor(out=xx[:], in0=hT[:, c, 0:G], in1=hT[:, c, 1:G + 1], op=ALU.subtract),
                 reads=[bhT], writes=[bxx])
            for n, i in enumerate(self.mix_ids):
                S.op("dve", lambda e: e.scalar_tensor_tensor(out=self.xm[i][:, c, :], in0=xx[:], scalar=self.mixc[:, i, c:c + 1],
                                                             in1=hT[:, c, 1:G + 1], op0=ALU.mult, op1=ALU.add),
                     reads=[bxx, bhT, self.bmixc], writes=[self.bxm[i]])

    def release(self):
        for b in self.dsems:
            self.S.release(b)


def col_load(S, dst, src_row, track):
    S.dma("sp", dst, src_row.rearrange("o (c p) -> p (o c)", p=128), track, writes=[track], slow=True)


def rwkv_fm_phase(nc, S, x_in, xin_b, gain_row, mix, w0, w1, w2, a0, a1, a2, kk_, ka_, w_r, w_k,
                  RtT, KtT, AtT, BtT, WC, bouts, ntok):
    G = 512
    ng = ntok // G
    with ExitStack() as es:
        sb = lambda name, shape, dt: es.enter_context(nc.sbuf_tensor(_uniq(name), shape, dt))
        ps = lambda name, shape, dt: es.enter_context(nc.psum_tensor(_uniq(name), shape, dt))
        P = RwPrep(nc, S, es, G, (0, 1, 2, 4), x_in, xin_b, gain_row, mix)
        Wr = sb("Wr", [128, 8, D], BF16)
        Wk = sb("Wk", [128, 8, D], BF16)
        W1 = sb("W1", [128, 8, 64], BF16)
        A1 = sb("A1", [128, 8, 64], BF16)
        W2 = sb("W2", [64, D], BF16)
        A2 = sb("A2", [64, D], BF16)
        cols = sb("cols", [128, 5, 8], F32)
        bones = sb("bones", [128, 128], BF16)
        rmask = sb("rmask", [128, G], F32)
        tiny = sb("tiny", [128, 1], F32)
        tw = sb("tw", [64, G], BF16)
        ta = sb("ta", [64, G], BF16)
        names = ["sgu", "av", "kk0", "lnk", "rk", "kkn", "t1", "kp", "csg", "eW", "eWi", "t2"]
        T2 = {n: [sb(n + "_%d" % i, [128, G], F32) for i in range(2)] for n in names}
        sqk2 = [sb("sqk%d" % i, [128, G], BF16) for i in range(2)]
        wc = [sb("wc%d" % i, [128, G // 64], F32) for i in range(2)]
        ob = [sb("ob%d" % i, [128, G], BF16) for i in range(8)]
        p_r = ps("p_r", [128, G], F32)
        p_k = ps("p_k", [128, G], F32)
        p_u = ps("p_u", [128, G], F32)
        p_a = ps("p_a", [128, G], F32)
        p_ss = ps("p_ss", [128, G], F32)
        p_t = ps("p_t", [128, G], F32)
        bW = S.dbuf("W")
        bcols = S.dbuf("cols")
        bwc = [S.dbuf("wc") for _ in range(2)]
        bob = [S.dbuf("ob") for _ in range(8)]
        B2 = {n: [S.buf(n) for _ in range(2)] for n in names + ["sqk"]}
        B = {n: S.buf(n) for n in ["tw", "ta", "bones", "rmask"]}
        B.update({n: S.pbuf(n) for n in ["p_r", "p_k", "p_u", "p_a", "p_ss", "p_t"]})
        for c in range(8):
            S.dma("pool", Wr[:, c, :], w_r.rearrange("(c p) f -> p c f", p=128)[:, c, :], bW, writes=[bW])
            S.dma("pool", Wk[:, c, :], w_k.rearrange("(c p) f -> p c f", p=128)[:, c, :], bW, writes=[bW])
        S.dma("pool", W1[:], w1.rearrange("(c p) f -> p c f", p=128), bW, writes=[bW])
        S.dma("pool", A1[:], a1.rearrange("(c p) f -> p c f", p=128), bW, writes=[bW])
        S.dma("pool", W2[:], w2, bW, writes=[bW])
        S.dma("pool", A2[:], a2, bW, writes=[bW])
        for n, src in enumerate((w0, a0, kk_, ka_)):
            col_load(S, cols[:, n, :], src, bcols)
        S.op("dve", lambda e: e.tensor_scalar(out=cols[:, 4, :], in0=cols[:, 3, :], scalar1=-1.0, scalar2=None, op0=ALU.mult),
             reads=[bcols], writes=[bcols])
        S.op("pool", lambda e: e.memset(bones[:], 0.0), writes=[B["bones"]])
        S.op("pool", lambda e: e.memset(bones[0:64, 0:64], 1.0), writes=[B["bones"]])
        S.op("pool", lambda e: e.memset(bones[64:128, 64:128], 1.0), writes=[B["bones"]])
        S.op("pool", lambda e: e.memset(rmask[:], 1.0), writes=[B["rmask"]])
        S.op("pool", lambda e: e.memset(rmask[:, 0:G:64], 0.0), writes=[B["rmask"]])
        S.op("pool", lambda e: e.memset(tiny[:], 1e-18), writes=[B["rmask"]])
        RB, KB, AB, BB, WCB = bouts
        no = 0
        for g in range(ng):
            P.group(g)
            t0 = g * G
            xr, xw, xk, xa = P.xm[0], P.xm[1], P.xm[2], P.xm[4]
            bxr, bxw, bxk, bxa = P.bxm[0], P.bxm[1], P.bxm[2], P.bxm[4]
            for c in range(8):
                S.op("pe", lambda e: e.matmul(p_t[0:64, :], lhsT=W1[:, c, :], rhs=xw[:, c, :], start=(c == 0), stop=(c == 7)),
                     reads=[bW, bxw], writes=[B["p_t"]], sig=(c == 7))
            S.op("act", lambda e: e.activation(out=tw[:], in_=p_t[0:64, :], func=AF.Tanh), reads=[B["p_t"]], writes=[B["tw"]])
            for c in range(8):
                S.op("pe", lambda e: e.matmul(p_t[0:64, :], lhsT=A1[:, c, :], rhs=xa[:, c, :], start=(c == 0), stop=(c == 7)),
                     reads=[bW, bxa], writes=[B["p_t"]], sig=(c == 7))
            S.op("act", lambda e: e.copy(out=ta[:], in_=p_t[0:64, :]), reads=[B["p_t"]], writes=[B["ta"]])
            for cc in range(8):
                fs = slice(cc * 128, (cc + 1) * 128)
                par = cc % 2
                T = {n: T2[n][par] for n in names}
                sqk = sqk2[par]
                for n in names + ["sqk"]:
                    B[n] = B2[n][par]
                for c in range(8):
                    S.op("pe", lambda e: e.matmul(p_k[:], lhsT=Wk[:, c, fs], rhs=xk[:, c, :], start=(c == 0), stop=(c == 7)),
                         reads=[bW, bxk], writes=[B["p_k"]], sig=(c == 7))
                S.op("pe", lambda e: e.matmul(p_u[:], lhsT=W2[:, fs], rhs=tw[:], start=True, stop=True), reads=[bW, B["tw"]], writes=[B["p_u"]])
                S.op("pe", lambda e: e.matmul(p_a[:], lhsT=A2[:, fs], rhs=ta[:], start=True, stop=True), reads=[bW, B["ta"]], writes=[B["p_a"]])
                for c in range(8):
                    S.op("pe", lambda e: e.matmul(p_r[:], lhsT=Wr[:, c, fs], rhs=xr[:, c, :], start=(c == 0), stop=(c == 7)),
                         reads=[bW, bxr], writes=[B["p_r"]], sig=(c == 7))
                S.op("act", lambda e: e.activation(out=T["kk0"][:], in_=p_k[:], func=AF.Copy, scale=cols[:, 2, cc:cc + 1]),
                     reads=[B["p_k"], bcols], writes=[B["kk0"]])
                S.op("act", lambda e: e.activation(out=sqk[:], in_=p_k[:], func=AF.Square, scale=cols[:, 2, cc:cc + 1]),
                     reads=[B["p_k"], bcols], writes=[B["sqk"]])
                S.op("pe", lambda e: e.matmul(p_ss[:], lhsT=bones[:], rhs=sqk[:], start=True, stop=True), reads=[B["bones"], B["sqk"]],
                     writes=[B["p_ss"]])
                S.op("act", lambda e: e.activation(out=T["sgu"][:], in_=p_u[:], func=AF.Sigmoid, bias=cols[:, 0, cc:cc + 1]),
                     reads=[B["p_u"], bcols], writes=[B["sgu"]])
                S.op("act", lambda e: e.activation(out=T["av"][:], in_=p_a[:], func=AF.Sigmoid, bias=cols[:, 1, cc:cc + 1]),
                     reads=[B["p_a"], bcols], writes=[B["av"]])
                S.op("act", lambda e: e.activation(out=T["lnk"][:], in_=p_ss[:], func=AF.Ln, bias=tiny[:]), reads=[B["p_ss"], B["rmask"]],
                     writes=[B["lnk"]])
                S.op("act", lambda e: e.activation(out=T["rk"][:], in_=T["lnk"][:], func=AF.Exp, scale=-0.5), reads=[B["lnk"]], writes=[B["rk"]])
                S.op("dve", lambda e: e.tensor_tensor_scan(out=T["csg"][:], data0=rmask[:], data1=T["sgu"][:], initial=0.0,
                                                           op0=ALU.mult, op1=ALU.add), reads=[B["rmask"], B["sgu"]], writes=[B["csg"]])
                S.op("act", lambda e: e.activation(out=T["eW"][:], in_=T["csg"][:], func=AF.Exp, scale=-C0), reads=[B["csg"]], writes=[B["eW"]])
                S.op("act", lambda e: e.activation(out=T["eWi"][:], in_=T["csg"][:], func=AF.Exp, scale=C0), reads=[B["csg"]], writes=[B["eWi"]])
                S.op("act", lambda e: e.activation(out=T["t1"][:], in_=T["av"][:], func=AF.Identity, scale=cols[:, 3, cc:cc + 1],
                                                   bias=cols[:, 4, cc:cc + 1]), reads=[B["av"], bcols], writes=[B["t1"]])
                S.op("dve", lambda e: e.tensor_tensor(out=T["kkn"][:], in0=T["kk0"][:], in1=T["rk"][:], op=ALU.mult),
                     reads=[B["kk0"], B["rk"]], writes=[B["kkn"]])
                S.op("dve", lambda e: e.scalar_tensor_tensor(out=T["kp"][:], in0=T["t1"][:], scalar=1.0, in1=p_k[:], op0=ALU.add, op1=ALU.mult),
                     reads=[B["t1"], B["p_k"]], writes=[B["kp"]])
                S.op("dve", lambda e: e.tensor_tensor(out=T["t2"][:], in0=T["kkn"][:], in1=T["av"][:], op=ALU.mult),
                     reads=[B["kkn"], B["av"]], writes=[B["t2"]])
                outs = []
                o, bo = ob[no % 8], bob[no % 8]; no += 1
                S.op("dve", lambda e: e.tensor_tensor(out=o[:], in0=p_r[:], in1=T["eW"][:], op=ALU.mult), reads=[B["p_r"], B["eW"]], writes=[bo])
                outs.append((o, bo, RtT, RB))
                o, bo = ob[no % 8], bob[no % 8]; no += 1
                S.op("dve", lambda e: e.tensor_tensor(out=o[:], in0=T["kp"][:], in1=T["eWi"][:], op=ALU.mult), reads=[B["kp"], B["eWi"]], writes=[bo])
                outs.append((o, bo, KtT, KB))
                o, bo = ob[no % 8], bob[no % 8]; no += 1
                v3 = lambda t_: t_[:].rearrange("p (c j) -> p c j", j=64)
                S.op("dve", lambda e: e.scalar_tensor_tensor(out=v3(o)[:, :, 1:64], in0=v3(T["kkn"])[:, :, 1:64], scalar=-1.0,
                                                             in1=v3(T["eW"])[:, :, 0:63], op0=ALU.mult, op1=ALU.mult),
                     reads=[B["kkn"], B["eW"]], writes=[bo])
                S.op("dve", lambda e: e.tensor_scalar(out=v3(o)[:, :, 0:1], in0=v3(T["kkn"])[:, :, 0:1], scalar1=-1.0, scalar2=None, op0=ALU.mult),
                     reads=[B["kkn"]], writes=[bo])
                outs.append((o, bo, AtT, AB))
                o, bo = ob[no % 8], bob[no % 8]; no += 1
                S.op("dve", lambda e: e.tensor_tensor(out=o[:], in0=T["t2"][:], in1=T["eWi"][:], op=ALU.mult), reads=[B["t2"], B["eWi"]], writes=[bo])
                outs.append((o, bo, BtT, BB))
                for (o, bo, dst, bdst) in outs:
                    S.dma("sp", dst[fs, t0:t0 + G], o[:], bo, reads=[bo], writes=[bdst])
                w_, bw_ = wc[cc % 2], bwc[cc % 2]
                S.op("dve", lambda e: e.tensor_copy(out=w_[:], in_=T["eW"][:, 63:G:64]), reads=[B["eW"]], writes=[bw_])
                S.dma("sp", WC[fs, g * (G // 64):(g + 1) * (G // 64)], w_[:], bw_, reads=[bw_], writes=[WCB])
        S.barrier()
        P.release()
        for b in [bW, bcols] + bwc + bob:
            S.release(b)


def rwkv_tm_phase(nc, S, x_in, xin_b, gain_row, mix, a0, a1, a2, g1, g2, ka_, rk_, w_r, w_k, w_v,
                  Vtok, BV, Gt, bouts, ntok):
    G = 256
    ng = ntok // G
    with ExitStack() as es:
        sb = lambda name, shape, dt: es.enter_context(nc.sbuf_tensor(_uniq(name), shape, dt))
        ps = lambda name, shape, dt: es.enter_context(nc.psum_tensor(_uniq(name), shape, dt))
        P = RwPrep(nc, S, es, G, (0, 2, 3, 4, 5), x_in, xin_b, gain_row, mix)
        Wr = sb("Wr", [128, 8, D], BF16)
        Wk = sb("Wk", [128, 8, D], BF16)
        Wv = sb("Wv", [128, 8, D], BF16)
        A1 = sb("A1", [128, 8, 64], BF16)
        A2 = sb("A2", [64, D], BF16)
        G1 = sb("G1", [128, 8, 160], BF16)
        G2a = sb("G2a", [128, D], BF16)
        G2b = sb("G2b", [32, D], BF16)
        a0B = sb("a0B", [128, D], F32)
        kaB = sb("kaB", [128, D], F32)
        rkB = sb("rkB", [128, D], F32)
        ta = sb("ta", [64, G], BF16)
        sg1a = sb("sg1a", [128, G], BF16)
        sg1b = sb("sg1b", [32, G], BF16)
        tmp = sb("tmp", [128, D], F32)
        av = sb("av", [128, D], F32)
        t1 = sb("t1", [128, D], F32)
        kp = sb("kp", [128, D], F32)
        tmp2 = sb("tmp2", [128, D], F32)
        tmp3 = sb("tmp3", [128, 16, 64], F32)
        bsum = sb("bsum", [128, 16, 1], F32)
        bvo = [sb("bvo%d" % i, [128, 16, 64], F32) for i in range(2)]
        vto = [sb("vto%d" % i, [128, D], BF16) for i in range(2)]
        gto = [sb("gto%d" % i, [128, D], F32) for i in range(2)]
        p_t = ps("p_t", [128, G], F32)
        pp = [ps("pp%d" % i, [128, 2, 512], F32) for i in range(2)]
        bW, bB = S.dbuf("W"), S.dbuf("B")
        bbvo = [S.dbuf("bvo") for _ in range(2)]
        bvto = [S.dbuf("vto") for _ in range(2)]
        bgto = [S.dbuf("gto") for _ in range(2)]
        B = {n: S.buf(n) for n in ["ta", "sg1a", "sg1b", "tmp", "av", "t1", "kp", "tmp2", "tmp3", "bsum"]}
        B.update({n: S.pbuf(n) for n in ["p_t", "pp0", "pp1"]})
        bpp = [B["pp0"], B["pp1"]]
        for c in range(8):
            for (W_, w_) in ((Wr, w_r), (Wk, w_k), (Wv, w_v)):
                S.dma("pool", W_[:, c, :], w_.rearrange("(c p) f -> p c f", p=128)[:, c, :], bW, writes=[bW])
        S.dma("pool", A1[:], a1.rearrange("(c p) f -> p c f", p=128), bW, writes=[bW])
        S.dma("pool", G1[:], g1.rearrange("(c p) f -> p c f", p=128), bW, writes=[bW])
        S.dma("pool", A2[:], a2, bW, writes=[bW])
        S.dma("pool", G2a[:], g2[0:128, :], bW, writes=[bW])
        S.dma("pool", G2b[:], g2[128:160, :], bW, writes=[bW])
        for (t_, src) in ((a0B, a0), (kaB, ka_), (rkB, rk_)):
            S.dma("sp", t_[:], src.to_broadcast([128, D]), bB, writes=[bB])
        VB, BVB, GB = bouts
        npp = 0
        k = 0
        for g in range(ng):
            P.group(g)
            xr, xk, xv, xa, xg = P.xm[0], P.xm[2], P.xm[3], P.xm[4], P.xm[5]
            bxr, bxk, bxv, bxa, bxg = P.bxm[0], P.bxm[2], P.bxm[3], P.bxm[4], P.bxm[5]
            for c in range(8):
                S.op("pe", lambda e: e.matmul(p_t[0:64, :], lhsT=A1[:, c, :], rhs=xa[:, c, :], start=(c == 0), stop=(c == 7)),
                     reads=[bW, bxa], writes=[B["p_t"]], sig=(c == 7))
            S.op("act", lambda e: e.copy(out=ta[:], in_=p_t[0:64, :]), reads=[B["p_t"]], writes=[B["ta"]])
            for c in range(8):
                S.op("pe", lambda e: e.matmul(p_t[:, :], lhsT=G1[:, c, 0:128], rhs=xg[:, c, :], start=(c == 0), stop=(c == 7)),
                     reads=[bW, bxg], writes=[B["p_t"]], sig=(c == 7))
            S.op("act", lambda e: e.activation(out=sg1a[:], in_=p_t[:, :], func=AF.Sigmoid), reads=[B["p_t"]], writes=[B["sg1a"]])
            for c in range(8):
                S.op("pe", lambda e: e.matmul(p_t[0:32, :], lhsT=G1[:, c, 128:160], rhs=xg[:, c, :], start=(c == 0), stop=(c == 7)),
                     reads=[bW, bxg], writes=[B["p_t"]], sig=(c == 7))
            S.op("act", lambda e: e.activation(out=sg1b[:], in_=p_t[0:32, :], func=AF.Sigmoid), reads=[B["p_t"]], writes=[B["sg1b"]])
            for s in range(G // 128):
                ts = slice(s * 128, (s + 1) * 128)
                t0 = g * G + s * 128

                def big(xm_, bxm_, W_):
                    nonlocal npp
                    p, bp = pp[npp % 2], bpp[npp % 2]
                    npp += 1
                    for hh in range(2):
                        for c in range(8):
                            S.op("pe", lambda e: e.matmul(p[:, hh, :], lhsT=xm_[:, c, ts], rhs=W_[:, c, hh * 512:(hh + 1) * 512],
                                                          start=(c == 0), stop=(c == 7)), reads=[bW, bxm_], writes=[bp], sig=(c == 7))
                    return p, bp
                p, bp = pp[npp % 2], bpp[npp % 2]
                npp += 1
                for hh in range(2):
                    S.op("pe", lambda e: e.matmul(p[:, hh, :], lhsT=ta[:, ts], rhs=A2[:, hh * 512:(hh + 1) * 512], start=True, stop=True),
                         reads=[bW, B["ta"]], writes=[bp])
                S.op("dve", lambda e: e.tensor_tensor(out=tmp[:], in0=p[:].rearrange("p a b -> p (a b)"), in1=a0B[:], op=ALU.add),
                     reads=[bp, bB], writes=[B["tmp"]])
                S.op("act", lambda e: e.activation(out=av[:], in_=tmp[:], func=AF.Sigmoid), reads=[B["tmp"]], writes=[B["av"]])
                S.op("dve", lambda e: e.scalar_tensor_tensor(out=t1[:], in0=av[:], scalar=-1.0, in1=kaB[:], op0=ALU.add, op1=ALU.mult),
                     reads=[B["av"], bB], writes=[B["t1"]])
                p, bp = big(xk, bxk, Wk)
                S.op("dve", lambda e: e.scalar_tensor_tensor(out=kp[:], in0=t1[:], scalar=1.0, in1=p[:].rearrange("p a b -> p (a b)"),
                                                             op0=ALU.add, op1=ALU.mult), reads=[B["t1"], bp], writes=[B["kp"]])
                p, bp = big(xr, bxr, Wr)
                S.op("dve", lambda e: e.tensor_tensor(out=tmp2[:], in0=p[:].rearrange("p a b -> p (a b)"), in1=rkB[:], op=ALU.mult),
                     reads=[bp, bB], writes=[B["tmp2"]])
                S.op("dve", lambda e: e.tensor_tensor(out=tmp3[:].rearrange("p a b -> p (a b)"), in0=tmp2[:], in1=kp[:], op=ALU.mult),
                     reads=[B["tmp2"], B["kp"]], writes=[B["tmp3"]])
                S.op("dve", lambda e: e.tensor_reduce(out=bsum[:], in_=tmp3[:], axis=AX.X, op=ALU.add), reads=[B["tmp3"]], writes=[B["bsum"]])
                p, bp = big(xv, bxv, Wv)
                o, bo = bvo[k % 2], bbvo[k % 2]
                S.op("dve", lambda e: e.tensor_tensor(out=o[:], in0=p[:].rearrange("p a (h d) -> p (a h) d", d=64),
                                                      in1=bsum[:].to_broadcast([128, 16, 64]), op=ALU.mult), reads=[bp, B["bsum"]], writes=[bo])
                S.dma("sp", BV[t0:t0 + 128, :], o[:].rearrange("p a b -> p (a b)"), bo, reads=[bo], writes=[BVB])
                o, bo = vto[k % 2], bvto[k % 2]
                S.op("act", lambda e: e.copy(out=o[:], in_=p[:].rearrange("p a b -> p (a b)")), reads=[bp], writes=[bo])
                S.dma("sp", Vtok[t0:t0 + 128, :], o[:], bo, reads=[bo], writes=[VB])
                p, bp = pp[npp % 2], bpp[npp % 2]
                npp += 1
                for hh in range(2):
                    S.op("pe", lambda e: e.matmul(p[:, hh, :], lhsT=sg1a[:, ts], rhs=G2a[:, hh * 512:(hh + 1) * 512], start=True, stop=False),
                         reads=[bW, B["sg1a"]], writes=[bp], sig=False)
                    S.op("pe", lambda e: e.matmul(p[:, hh, :], lhsT=sg1b[:, ts], rhs=G2b[:, hh * 512:(hh + 1) * 512], start=False, stop=True),
                         reads=[bW, B["sg1b"]], writes=[bp])
                o, bo = gto[k % 2], bgto[k % 2]
                S.op("act", lambda e: e.copy(out=o[:], in_=p[:].rearrange("p a b -> p (a b)")), reads=[bp], writes=[bo])
                S.dma("sp", Gt[t0:t0 + 128, :], o[:], bo, reads=[bo], writes=[GB])
                k += 1
        S.barrier()
        P.release()
        for b in [bW, bB] + bbvo + bvto + bgto:
            S.release(b)


def rwkv_scan_phase(nc, S, RtT, KtT, AtT, BtT, WC, Vtok, Ysc, bins, bY, ntok, NI=4):
    nch = ntok // 64
    ngr = nch // 8
    with ExitStack() as es:
        sb = lambda name, shape, dt: es.enter_context(nc.sbuf_tensor(_uniq(name), shape, dt))
        ps = lambda name, shape, dt: es.enter_context(nc.psum_tensor(_uniq(name), shape, dt))
        MU = sb("MU", [128, 128], F32)
        MUI = sb("MUI", [128, 128], F32)
        ML = sb("ML", [128, 128], F32)
        I32 = sb("I32", [128, 128], F32)
        identb = sb("identb", [128, 128], BF16)
        bconst = S.buf("const")
        for (m, chm, pat, op) in ((MU, -1, 1, ALU.is_gt), (MUI, -1, 1, ALU.is_ge), (ML, 1, -1, ALU.is_gt)):
            S.op("pool", lambda e: e.memset(m[:], 1.0), writes=[bconst])
            S.op("pool", lambda e: e.affine_select(out=m[:], in_=m[:], pattern=[[pat, 128]], compare_op=op, fill=0.0, base=0,
                                                   channel_multiplier=chm), reads=[bconst], writes=[bconst])
        make_ident(nc, S, I32, bconst)
        make_ident(nc, S, identb, bconst)

        class Slot:
            pass
        slots = []
        for si in range(NI):
            s = Slot()
            n_ = lambda x: "%s_%d" % (x, si)
            s.AR = [sb(n_("AR%d" % j), [128, 8, 2, 128], BF16) for j in range(2)]
            s.Bd = [sb(n_("Bd%d" % j), [128, 8, 128], BF16) for j in range(2)]
            s.Kd = [sb(n_("Kd%d" % j), [128, 8, 128], BF16) for j in range(2)]
            s.bbd = [S.dbuf(n_("bd0")), S.dbuf(n_("bd1"))]
            s.Vs = [sb(n_("Vs%d" % j), [128, 8, 64], BF16) for j in range(2)]
            s.Yo = [sb(n_("Yo%d" % j), [128, 8, 64], F32) for j in range(2)]
            s.bYo = [S.dbuf(n_("Yo0")), S.dbuf(n_("Yo1"))]
            s.wcs = sb(n_("wcs"), [128, nch], F32)
            s.bwcs = S.dbuf(n_("wcs"))
            s.QX = [sb(n_("QX%d" % j), [128, 2, 128], BF16) for j in range(2)]
            s.P = [sb(n_("P%d" % j), [128, 128], BF16) for j in range(2)]
            s.bQX = [S.buf("QX") for _ in range(2)]
            s.bP = [S.buf("P") for _ in range(2)]
            for k in ("Mak", "Mrb", "Mrk", "BtT", "KtT"):
                setattr(s, k, [sb(n_(k) + "_%d" % q_, [128, 128], BF16) for q_ in range(2)])
                setattr(s, "b" + k, [S.buf(k) for _ in range(2)])
            s.TT = [sb(n_("TT%d" % q_), [128, 128], BF16) for q_ in range(2)]
            s.bTT = [S.buf("TT") for _ in range(2)]
            s.Xs = sb(n_("Xs"), [128, 64], BF16)
            s.Ub = sb(n_("Ub"), [128, 64], BF16)
            s.Sw = sb(n_("Sw"), [128, 64], F32)
            s.St = sb(n_("St"), [128, 64], F32)
            s.Sb = sb(n_("Sb"), [128, 64], BF16)
            s.bXs, s.bUb, s.bSw, s.bSt, s.bSb = [S.buf(k) for k in ("Xs", "Ub", "Sw", "St", "Sb")]
            s.psA = ps(n_("psA"), [128, 512], F32)
            s.psB = ps(n_("psB"), [128, 4, 128], F32)
            s.bA, s.bB = S.pbuf("bankA"), S.pbuf("bankB")
            s.ptr = s.psB[:, 3, :].bitcast(BF16)
            for j in range(2):
                S.op("pool", lambda e: e.memset(s.AR[j][:], 0.0), writes=[s.bbd[j]])
                S.op("pool", lambda e: e.memset(s.Bd[j][:], 0.0), writes=[s.bbd[j]])
                S.op("pool", lambda e: e.memset(s.Kd[j][:], 0.0), writes=[s.bbd[j]])
            slots.append(s)

        bRs, bKs, bAs, bBs, bWC, bV = bins

        def load_group(s, hp, gg):
            j = gg % 2
            t0 = gg * 512
            for h in range(2):
                r0 = hp * 128 + h * 64
                hs, cs = slice(h * 64, (h + 1) * 64), slice(h * 64, (h + 1) * 64)
                for (dst, src, bsrc) in ((s.AR[j][hs, :, 0, cs], AtT, bAs), (s.AR[j][hs, :, 1, cs], RtT, bRs),
                                         (s.Bd[j][hs, :, cs], BtT, bBs), (s.Kd[j][hs, :, cs], KtT, bKs)):
                    S.dma("sp", dst, src[r0:r0 + 64, t0:t0 + 512].rearrange("p (c j) -> p c j", j=64), s.bbd[j],
                          reads=[bsrc], writes=[s.bbd[j]])
                S.dma("sp", s.Vs[j][hs, :, :], Vtok[t0:t0 + 512, r0:r0 + 64].rearrange("(c j) v -> j c v", j=64),
                      s.bbd[j], reads=[bV], writes=[s.bbd[j]])

        def T_steps(gg, c):
            j = gg % 2
            q = (gg * 8 + c) % 2
            steps = []

            def st1():
                for s in slots:
                    AR, A, Bd, K = s.AR[j][:, c, :, :], s.AR[j][:, c, 0, :], s.Bd[j][:, c, :], s.Kd[j][:, c, :]
                    S.op("pe", lambda e: e.matmul(s.psA[:, 0:256], lhsT=Bd, rhs=AR, start=True, stop=True), reads=[s.bbd[j]], writes=[s.bA], sig=False)
                    S.op("pe", lambda e: e.matmul(s.psA[:, 256:512], lhsT=K, rhs=AR, start=True, stop=True), reads=[s.bbd[j]], writes=[s.bA])
                    S.op("pe", lambda e: e.matmul(s.psB[:, 1, :], lhsT=A, rhs=Bd, start=True, stop=True), reads=[s.bbd[j]], writes=[s.bB], sig=False)
                    S.op("pe", lambda e: e.transpose(s.ptr[:, 0:128], Bd, identb[:]), reads=[s.bbd[j], bconst], writes=[s.bB], sig=False)
                    S.op("pe", lambda e: e.transpose(s.ptr[:, 128:256], K, identb[:]), reads=[s.bbd[j], bconst], writes=[s.bB])
            steps.append(st1)

            def st2():
                for s in slots:
                    S.op("dve", lambda e: e.tensor_tensor(out=s.QX[0][:, 0, :], in0=s.psA[:, 0:128], in1=MU[:], op=ALU.mult),
                         reads=[s.bA, bconst], writes=[s.bQX[0]])
                    S.op("dve", lambda e: e.tensor_tensor(out=s.QX[1][:, 1, :], in0=s.QX[0][:, 0, :], in1=I32[:], op=ALU.add),
                         reads=[s.bQX[0], bconst], writes=[s.bQX[1]])
                    S.op("dve", lambda e: e.tensor_tensor(out=s.P[0][:], in0=s.psB[:, 1, :], in1=ML[:], op=ALU.mult),
                         reads=[s.bB, bconst], writes=[s.bP[0]])
            steps.append(st2)

            def st3():
                for s in slots:
                    for (nm, lo, msk) in (("Mrb", 128, MUI), ("Mak", 256, MU), ("Mrk", 384, MUI)):
                        S.op("dve", lambda e: e.tensor_tensor(out=getattr(s, nm)[q][:], in0=s.psA[:, lo:lo + 128], in1=msk[:], op=ALU.mult),
                             reads=[s.bA, bconst], writes=[getattr(s, "b" + nm)[q]])
                    S.op("act", lambda e: e.copy(out=s.BtT[q][:], in_=s.ptr[:, 0:128]), reads=[s.bB], writes=[s.bBtT[q]])
                    S.op("act", lambda e: e.copy(out=s.KtT[q][:], in_=s.ptr[:, 128:256]), reads=[s.bB], writes=[s.bKtT[q]])
            steps.append(st3)

            for lvl in range(6):
                cur = 0 if lvl == 0 else lvl % 2
                nxt = 1 - cur

                def sa(lvl=lvl, cur=cur):
                    for s in slots:
                        Q, X, Pm = s.QX[cur][:, 0, :], s.QX[cur][:, 1, :], s.P[cur][:]
                        rd = [s.bP[cur], s.bQX[cur]]
                        if lvl == 0:
                            S.op("pe", lambda e: e.matmul(s.psA[:, 0:128], lhsT=Pm, rhs=Q, start=True, stop=True), reads=rd, writes=[s.bA], sig=False)
                        elif lvl <= 3:
                            S.op("pe", lambda e: e.matmul(s.psA[:, 0:256], lhsT=Pm, rhs=s.QX[cur][:, :, :], start=True, stop=True),
                                 reads=rd, writes=[s.bA], sig=False)
                        else:
                            S.op("pe", lambda e: e.matmul(s.psA[:, 128:256], lhsT=Pm, rhs=X, start=True, stop=True), reads=rd, writes=[s.bA],
                                 sig=(lvl == 5))
                        if lvl <= 4:
                            S.op("pe", lambda e: e.matmul(s.psA[:, 256:384], lhsT=Q, rhs=Pm, start=True, stop=True), reads=rd, writes=[s.bA])

                def sb_(lvl=lvl, cur=cur, nxt=nxt):
                    for s in slots:
                        if lvl <= 3:
                            S.op("act", lambda e: e.copy(out=s.QX[nxt][:, 0, :], in_=s.psA[:, 0:128]), reads=[s.bA], writes=[s.bQX[nxt]])
                        if lvl <= 4:
                            S.op("act", lambda e: e.copy(out=s.P[nxt][:], in_=s.psA[:, 256:384]), reads=[s.bA], writes=[s.bP[nxt]])
                        if 1 <= lvl <= 4:
                            S.op("dve", lambda e: e.tensor_tensor(out=s.QX[nxt][:, 1, :], in0=s.psA[:, 128:256], in1=s.QX[cur][:, 1, :], op=ALU.add),
                                 reads=[s.bA, s.bQX[cur]], writes=[s.bQX[nxt]])
                        if lvl == 5:
                            S.op("dve", lambda e: e.tensor_tensor(out=s.TT[q][:], in0=s.psA[:, 128:256], in1=s.QX[cur][:, 1, :], op=ALU.add),
                                 reads=[s.bA, s.bQX[cur]], writes=[s.bTT[q]])
                steps += [sa, sb_]
            return steps

        def S_steps(gg, c):
            j = gg % 2
            ch = gg * 8 + c
            q = ch % 2

            def s1():
                for s in slots:
                    A = s.AR[j][:, c, 0, :]
                    S.op("pe", lambda e: e.matmul(s.psB[:, 0, 0:64], lhsT=A, rhs=s.Sb[:], start=True, stop=False),
                         reads=[s.bbd[j], s.bSb], writes=[s.bB], sig=False)
                    S.op("pe", lambda e: e.matmul(s.psB[:, 0, 0:64], lhsT=s.Mak[q][:], rhs=s.Vs[j][:, c, :], start=False, stop=True),
                         reads=[s.bMak[q], s.bbd[j]], writes=[s.bB])
                    S.op("pool", lambda e: e.tensor_scalar(out=s.Sw[:], in0=s.St[:], scalar1=s.wcs[:, ch:ch + 1], scalar2=None, op0=ALU.mult),
                         reads=[s.bSt, s.bwcs], writes=[s.bSw])

            def s2():
                for s in slots:
                    S.op("act", lambda e: e.copy(out=s.Xs[:], in_=s.psB[:, 0, 0:64]), reads=[s.bB], writes=[s.bXs])

            def s3():
                for s in slots:
                    S.op("pe", lambda e: e.matmul(s.psB[:, 1, 0:64], lhsT=s.TT[q][:], rhs=s.Xs[:], start=True, stop=True),
                         reads=[s.bTT[q], s.bXs], writes=[s.bB])

            def s4():
                for s in slots:
                    S.op("act", lambda e: e.copy(out=s.Ub[:], in_=s.psB[:, 1, 0:64]), reads=[s.bB], writes=[s.bUb])

            def s5():
                for s in slots:
                    R = s.AR[j][:, c, 1, :]
                    pY, pS = s.psB[:, 2, 0:64], s.psB[:, 0, 0:64]
                    S.op("pe", lambda e: e.matmul(pY, lhsT=R, rhs=s.Sb[:], start=True, stop=False),
                         reads=[s.bbd[j], s.bSb], writes=[s.bB], sig=False)
                    S.op("pe", lambda e: e.matmul(pY, lhsT=s.Mrb[q][:], rhs=s.Ub[:], start=False, stop=False),
                         reads=[s.bMrb[q], s.bUb], writes=[s.bB], sig=False)
                    S.op("pe", lambda e: e.matmul(pY, lhsT=s.Mrk[q][:], rhs=s.Vs[j][:, c, :], start=False, stop=True),
                         reads=[s.bMrk[q], s.bbd[j]], writes=[s.bB], sig=False)
                    S.op("pe", lambda e: e.matmul(pS, lhsT=s.BtT[q][:], rhs=s.Ub[:], start=True, stop=False),
                         reads=[s.bBtT[q], s.bUb], writes=[s.bB], sig=False)
                    S.op("pe", lambda e: e.matmul(pS, lhsT=s.KtT[q][:], rhs=s.Vs[j][:, c, :], start=False, stop=True),
                         reads=[s.bKtT[q], s.bbd[j]], writes=[s.bB])

            def s6():
                for s in slots:
                    S.op("dve", lambda e: e.scalar_tensor_tensor(out=s.St[:], in0=s.psB[:, 0, 0:64], scalar=s.wcs[:, ch:ch + 1], in1=s.Sw[:],
                                                                 op0=ALU.mult, op1=ALU.add), reads=[s.bB, s.bwcs, s.bSw], writes=[s.bSt])
                    S.op("dve", lambda e: e.tensor_copy(out=s.Yo[j][:, c, :], in_=s.psB[:, 2, 0:64]), reads=[s.bB], writes=[s.bYo[j]])
                    S.op("act", lambda e: e.copy(out=s.Sb[:], in_=s.St[:]), reads=[s.bSt], writes=[s.bSb])
            return [s1, s2, s3, s4, s5, s6]

        for rnd in range(8 // NI):
            hps = [rnd * NI + i for i in range(NI)]
            for s, hp in zip(slots, hps):
                S.dma("sp", s.wcs[:], WC[hp * 128:(hp + 1) * 128, 0:nch], s.bwcs, reads=[bWC], writes=[s.bwcs])
                S.op("pool", lambda e: e.memset(s.St[:], 0.0), writes=[s.bSt])
                S.op("pool", lambda e: e.memset(s.Sb[:], 0.0), writes=[s.bSb])
                load_group(s, hp, 0)
            if ngr > 1:
                for s, hp in zip(slots, hps):
                    load_group(s, hp, 1)
            for st in T_steps(0, 0):
                st()
            for gg in range(ngr):
                j = gg % 2
                for c in range(8):
                    ch = gg * 8 + c
                    if ch + 1 < nch:
                        ng_, nc_ = (gg, c + 1) if c < 7 else (gg + 1, 0)
                        tsteps = T_steps(ng_, nc_)
                    else:
                        tsteps = []
                    ssteps = S_steps(gg, c)
                    ti = 0
                    for ss_ in ssteps:
                        for _ in range(3):
                            if ti < len(tsteps):
                                tsteps[ti]()
                                ti += 1
                        ss_()
                    while ti < len(tsteps):
                        tsteps[ti]()
                        ti += 1
                for s, hp in zip(slots, hps):
                    for h in range(2):
                        c0 = hp * 128 + h * 64
                        S.dma("sp", Ysc[gg * 512:(gg + 1) * 512, c0:c0 + 64].rearrange("(c j) v -> j c v", j=64),
                              s.Yo[j][h * 64:(h + 1) * 64, :, :], s.bYo[j], reads=[s.bYo[j]], writes=[bY])
                if gg + 2 < ngr:
                    for s, hp in zip(slots, hps):
                        load_group(s, hp, gg + 2)
        S.barrier()
        for s in slots:
            for b_ in s.bbd + s.bYo + [s.bwcs]:
                S.release(b_)


def rwkv_post_phase(nc, S, Ysc, BV, Gt, lg_row, lb_row, ZT, bins, bZT, ntok, gn_eps=64e-5):
    with ExitStack() as es:
        sb = lambda name, shape, dt: es.enter_context(nc.sbuf_tensor(_uniq(name), shape, dt))
        ps = lambda name, shape, dt: es.enter_context(nc.psum_tensor(_uniq(name), shape, dt))
        lgB = sb("lgB", [128, D], F32)
        lbB = sb("lbB", [128, D], F32)
        ident = sb("ident", [128, 128], BF16)
        nh = sb("nh", [128, 16, 1], F32)
        yt = [sb("yt%d" % i, [128, 16, 64], F32) for i in range(2)]
        bvt = [sb("bvt%d" % i, [128, D], F32) for i in range(2)]
        gt = [sb("gt%d" % i, [128, D], F32) for i in range(2)]
        sm_ = [sb("sm%d" % i, [128, 16, 1], F32) for i in range(2)]
        vr_ = [sb("vr%d" % i, [128, 16, 1], F32) for i in range(2)]
        rstd_ = [sb("rstd%d" % i, [128, 16, 1], F32) for i in range(2)]
        yc_ = [sb("yc%d" % i, [128, 16, 64], F32) for i in range(2)]
        sq_ = [sb("sq%d" % i, [128, 16, 64], F32) for i in range(2)]
        yn_ = [sb("yn%d" % i, [128, 16, 64], F32) for i in range(2)]
        y2_ = [sb("y2%d" % i, [128, D], F32) for i in range(2)]
        zb = [sb("zb%d" % i, [128, D], BF16) for i in range(2)]
        zT = [sb("zT%d" % i, [128, 8, 512], BF16) for i in range(2)]
        p_tr = [ps("p_tr%d" % i, [128, 8, 128], BF16) for i in range(2)]
        bC = S.dbuf("C")
        byt = [S.dbuf("yt") for _ in range(2)]
        bbvt = [S.dbuf("bvt") for _ in range(2)]
        bgt = [S.dbuf("gt") for _ in range(2)]
        bzT = [S.dbuf("zT") for _ in range(2)]
        B = {n: S.buf(n) for n in ["ident", "nh", "zb0", "zb1"]}
        B2 = {n: [S.buf(n) for _ in range(2)] for n in ["sm", "vr", "rstd", "yc", "sq", "yn", "y2"]}
        bp_tr = [S.pbuf("ptr") for _ in range(2)]
        bYs, bBV, bG = bins
        make_ident(nc, S, ident, B["ident"])
        S.op("pool", lambda e: e.memset(nh[:], -0.5), writes=[B["nh"]])
        S.dma("sp", lgB[:], lg_row.to_broadcast([128, D]), bC, writes=[bC])
        S.dma("sp", lbB[:], lb_row.to_broadcast([128, D]), bC, writes=[bC])
        nt = ntok // 128

        def bind(t):
            i = t % 2
            for n_ in ["sm", "vr", "rstd", "yc", "sq", "yn", "y2"]:
                B[n_] = B2[n_][i]
            return i, sm_[i], vr_[i], rstd_[i], yc_[i], sq_[i], yn_[i], y2_[i]

        def front(t):
            i, sm, vr, rstd, yc, sq, yn, y2 = bind(t)
            t0 = t * 128
            S.dma("sp", yt[i][:].rearrange("p a b -> p (a b)"), Ysc[t0:t0 + 128, :], byt[i], reads=[bYs], writes=[byt[i]])
            S.dma("sp", bvt[i][:], BV[t0:t0 + 128, :], bbvt[i], reads=[bBV], writes=[bbvt[i]])
            S.dma("sp", gt[i][:], Gt[t0:t0 + 128, :], bgt[i], reads=[bG], writes=[bgt[i]])
            y3 = yt[i]
            S.op("dve", lambda e: e.tensor_reduce(out=sm[:], in_=y3[:], axis=AX.X, op=ALU.add), reads=[byt[i]], writes=[B["sm"]])
            S.op("dve", lambda e: e.tensor_scalar(out=sm[:], in0=sm[:], scalar1=1.0 / 64, scalar2=None, op0=ALU.mult), reads=[B["sm"]], writes=[B["sm"]])
            S.op("dve", lambda e: e.tensor_tensor(out=yc[:], in0=y3[:], in1=sm[:].to_broadcast([128, 16, 64]), op=ALU.subtract),
                 reads=[byt[i], B["sm"]], writes=[B["yc"]])
            S.op("act", lambda e: e.activation(out=sq[:], in_=yc[:], func=AF.Square), reads=[B["yc"]], writes=[B["sq"]])

        def mid(t):
            i, sm, vr, rstd, yc, sq, yn, y2 = bind(t)
            S.op("dve", lambda e: e.tensor_reduce(out=vr[:], in_=sq[:], axis=AX.X, op=ALU.add), reads=[B["sq"]], writes=[B["vr"]])
            S.op("dve", lambda e: e.tensor_scalar(out=vr[:], in0=vr[:], scalar1=1.0 / 64, scalar2=gn_eps, op0=ALU.mult, op1=ALU.add),
                 reads=[B["vr"]], writes=[B["vr"]])
            S.op("pool", lambda e: e.tensor_tensor(out=rstd[:], in0=vr[:], in1=nh[:], op=ALU.pow), reads=[B["vr"], B["nh"]], writes=[B["rstd"]])

        def back(t):
            i, sm, vr, rstd, yc, sq, yn, y2 = bind(t)
            S.op("dve", lambda e: e.tensor_tensor(out=yn[:], in0=yc[:], in1=rstd[:].to_broadcast([128, 16, 64]), op=ALU.mult),
                 reads=[B["yc"], B["rstd"]], writes=[B["yn"]])
            ynf = yn[:].rearrange("p a b -> p (a b)")
            S.op("dve", lambda e: e.tensor_tensor(out=y2[:], in0=ynf, in1=lgB[:], op=ALU.mult), reads=[B["yn"], bC], writes=[B["y2"]])
            S.op("dve", lambda e: e.tensor_tensor(out=y2[:], in0=y2[:], in1=lbB[:], op=ALU.add), reads=[B["y2"], bC], writes=[B["y2"]])
            S.op("dve", lambda e: e.tensor_tensor(out=y2[:], in0=y2[:], in1=bvt[i][:], op=ALU.add), reads=[B["y2"], bbvt[i]], writes=[B["y2"]])
            z, bz = zb[i], B["zb%d" % i]
            S.op("dve", lambda e: e.tensor_tensor(out=z[:], in0=y2[:], in1=gt[i][:], op=ALU.mult), reads=[B["y2"], bgt[i]], writes=[bz])
            pt, bpt = p_tr[i], bp_tr[i]
            for c in range(8):
                S.op("pe", lambda e: e.transpose(pt[:, c, :], z[:, c * 128:(c + 1) * 128], ident[:]), reads=[bz, B["ident"]], writes=[bpt], sig=(c == 7))
            gi = (t // 4) % 2
            S.op("act", lambda e: e.copy(out=zT[gi][:, :, (t % 4) * 128:(t % 4 + 1) * 128], in_=pt[:]), reads=[bpt], writes=[bzT[gi]])
            if t % 4 == 3:
                g0 = (t // 4) * 512
                for c in range(8):
                    S.dma("sp", ZT[c * 128:(c + 1) * 128, g0:g0 + 512], zT[gi][:, c, :], bzT[gi], reads=[bzT[gi]], writes=[bZT])

        front(0)
        mid(0)
        for t in range(nt):
            if t + 1 < nt:
                front(t + 1)
            back(t)
            if t + 1 < nt:
                mid(t + 1)
        S.barrier()
        for b in [bC] + byt + bbvt + bgt + bzT:
            S.release(b)


def build_program(ntok=SEQ):
    nc = bass.Bass("TRN2", target_bir_lowering=False)
    di = lambda n, s: nc.dram_tensor(n, list(s), F32, kind="ExternalInput").ap()
    x = di("x", [ntok, D])
    ffn_norm = di("ffn_norm", [4, D])
    wg = di("ffn_w_gate", [2, 2, D, DFF])
    wu = di("ffn_w_up", [2, 2, D, DFF])
    wd = di("ffn_w_down", [2, 2, DFF, D])
    mix_norm = di("mix_norm", [2, D])
    dbias = di("dbias", [12, 128, 2, 128])
    w_in = di("attn_w_in", [D, 3072])
    qn = di("attn_q_norm", [64, 1])
    kn = di("attn_k_norm", [64, 1])
    w_out = di("attn_w_out", [512, D])
    rw_mix = di("rw_mix", [6, D])
    rows = {n: di(n, [1, D]) for n in ("rw_w0", "rw_a0", "rw_kk", "rw_ka", "rw_rk", "rw_lnx_g", "rw_lnx_b")}
    rw_w1 = di("rw_w1", [D, 64]); rw_w2 = di("rw_w2", [64, D]); rw_a1 = di("rw_a1", [D, 64]); rw_a2 = di("rw_a2", [64, D])
    rw_g1 = di("rw_g1", [D, 160]); rw_g2 = di("rw_g2", [160, D])
    rw_wr = di("rw_wr", [D, D]); rw_wk = di("rw_wk", [D, D]); rw_wv = di("rw_wv", [D, D]); rw_wo = di("rw_wo", [D, D])
    out = nc.dram_tensor("out", [ntok, D], F32, kind="ExternalOutput").ap()
    scr = lambda n, s, dt: nc.dram_tensor(n, list(s), dt, kind="Internal").ap()
    xa = scr("xa", [ntok, D], F32); xb = scr("xb", [ntok, D], F32)
    QT = scr("QT", [D, ntok], BF16); KT = scr("KT", [D, ntok], BF16); V = scr("V", [ntok, D], BF16); MT = scr("MT", [512, ntok], BF16)
    RtT, KtT, AtT, BtT = [scr(n, [D, ntok], BF16) for n in ("RtT", "KtT", "AtT", "BtT")]
    WC = scr("WC", [D, ntok // 64], F32)
    Vtok = scr("Vtok", [ntok, D], BF16); BV = scr("BV", [ntok, D], F32); Gt = scr("Gt", [ntok, D], F32); Ysc = scr("Ysc", [ntok, D], F32)
    ZT = scr("ZT", [D, ntok], BF16)

    S = Sched(nc, n_dma_sems=24)
    nb = lambda n: S.buf(n, acc=True)
    bx, bxa, bxb, bout = nb("x"), nb("xa"), nb("xb"), nb("out")
    bQT, bKT, bV, bMT = nb("QT"), nb("KT"), nb("V"), nb("MT")
    bR, bK, bA, bB, bWC, bVt, bBV, bG, bY, bZ = [nb(n) for n in ("R", "K", "A", "B", "WC", "Vt", "BV", "G", "Y", "Z")]

    ffn_phase(nc, S, x, xa, bx, bxa, wg[0, 0], wu[0, 0], wd[0, 0], ffn_norm[0:1, :], ntok)
    attn_in_phase(nc, S, xa, bxa, w_in, mix_norm[0:1, :], qn, kn, QT, KT, V, bQT, bKT, bV, ntok)
    sb_attn_phase(nc, S, QT, KT, V, MT, bQT, bKT, bV, bMT, ntok)
    dil_attn_phase(nc, S, QT, KT, V, MT, dbias, bQT, bKT, bV, bMT, ntok)
    out_proj_phase(nc, S, xa, xb, bxa, bxb, MT, bMT, w_out, 512, ntok)
    ffn_phase(nc, S, xb, xa, bxb, bxa, wg[0, 1], wu[0, 1], wd[0, 1], ffn_norm[1:2, :], ntok)
    ffn_phase(nc, S, xa, xb, bxa, bxb, wg[1, 0], wu[1, 0], wd[1, 0], ffn_norm[2:3, :], ntok)
    rwkv_fm_phase(nc, S, xb, bxb, mix_norm[1:2, :], rw_mix, rows["rw_w0"], rw_w1, rw_w2, rows["rw_a0"], rw_a1, rw_a2,
                  rows["rw_kk"], rows["rw_ka"], rw_wr, rw_wk, RtT, KtT, AtT, BtT, WC, [bR, bK, bA, bB, bWC], ntok)
    rwkv_tm_phase(nc, S, xb, bxb, mix_norm[1:2, :], rw_mix, rows["rw_a0"], rw_a1, rw_a2, rw_g1, rw_g2, rows["rw_ka"], rows["rw_rk"],
                  rw_wr, rw_wk, rw_wv, Vtok, BV, Gt, [bVt, bBV, bG], ntok)
    rwkv_scan_phase(nc, S, RtT, KtT, AtT, BtT, WC, Vtok, Ysc, [bR, bK, bA, bB, bWC, bVt], bY, ntok)
    rwkv_post_phase(nc, S, Ysc, BV, Gt, rows["rw_lnx_g"], rows["rw_lnx_b"], ZT, [bY, bBV, bG], bZ, ntok)
    out_proj_phase(nc, S, xb, xa, bxb, bxa, ZT, bZ, rw_wo, 1024, ntok)
    ffn_phase(nc, S, xa, out, bxa, bout, wg[1, 1], wu[1, 1], wd[1, 1], ffn_norm[3:4, :], ntok)
    S.wait_for("sp", [bout])
    return nc


def kernel(x, ffn_norm, ffn_w_gate, ffn_w_up, ffn_w_down, mix_norm, rel_bias,
           attn_w_in, attn_q_norm, attn_k_norm, attn_w_out,
           rw_mix, rw_w0, rw_w1, rw_w2, rw_a0, rw_a1, rw_a2, rw_g1, rw_g2,
           rw_kk, rw_ka, rw_rk, rw_wr, rw_wk, rw_wv, rw_wo, rw_lnx_g, rw_lnx_b):
    f = lambda a: np.ascontiguousarray(np.asarray(a, dtype=np.float32))
    x = f(x)
    n = x.shape[0]
    shared = {
        "ffn_norm": f(ffn_norm).reshape(4, D), "ffn_w_gate": f(ffn_w_gate), "ffn_w_up": f(ffn_w_up), "ffn_w_down": f(ffn_w_down),
        "mix_norm": f(mix_norm), "dbias": dil_bias_host(f(rel_bias)),
        "attn_w_in": f(attn_w_in)[0], "attn_q_norm": f(attn_q_norm).reshape(64, 1), "attn_k_norm": f(attn_k_norm).reshape(64, 1),
        "attn_w_out": f(attn_w_out)[0], "rw_mix": f(rw_mix)[0],
        "rw_w0": f(rw_w0).reshape(1, D), "rw_a0": f(rw_a0).reshape(1, D), "rw_kk": f(rw_kk).reshape(1, D), "rw_ka": f(rw_ka).reshape(1, D),
        "rw_rk": f(rw_rk).reshape(1, D), "rw_lnx_g": f(rw_lnx_g).reshape(1, D), "rw_lnx_b": f(rw_lnx_b).reshape(1, D),
        "rw_w1": f(rw_w1)[0], "rw_w2": f(rw_w2)[0], "rw_a1": f(rw_a1)[0], "rw_a2": f(rw_a2)[0], "rw_g1": f(rw_g1)[0], "rw_g2": f(rw_g2)[0],
        "rw_wr": f(rw_wr)[0], "rw_wk": f(rw_wk)[0], "rw_wv": f(rw_wv)[0], "rw_wo": f(rw_wo)[0],
    }
    nc = build_program(x.shape[1])
    in_maps = [dict(shared, x=x[i]) for i in range(n)]
    res = run_bass_kernel_spmd(nc, in_maps, core_ids=list(range(n)))
    return np.stack([np.asarray(r["out"]) for r in res.results], axis=0).astype(np.float32)
```

```python
import numpy as np
from contextlib import ExitStack
import concourse.bass as bass
import concourse.mybir as mybir
from concourse.bass_utils import run_bass_kernel_spmd

F32 = mybir.dt.float32
BF16 = mybir.dt.bfloat16
AF = mybir.ActivationFunctionType
ALU = mybir.AluOpType
AX = mybir.AxisListType

D = 1024
DFF = 2816
NF = DFF // 128
SEQ = 4096


_UID = [0]


def _uniq(name):
    _UID[0] += 1
    return "%s_u%d" % (name, _UID[0])


def _merge(d, s):
    for k, v in s.items():
        if d.get(k, 0) < v:
            d[k] = v


class Buf:
    __slots__ = ("name", "wr", "rd", "acc", "dkey", "excl")

    def __init__(self, name, acc=False, excl=False):
        self.name = name
        self.wr = {}
        self.rd = {}
        self.acc = acc
        self.dkey = None
        self.excl = excl


class Sched:
    ENG = ("pe", "act", "dve", "pool", "sp")

    def __init__(self, nc, n_dma_sems=40):
        self.nc = nc
        self.eng = {"pe": nc.tensor, "act": nc.scalar, "dve": nc.vector, "pool": nc.gpsimd, "sp": nc.sync}
        self.sems = {}
        self.val = {}
        self.seen = {e: {} for e in self.ENG}
        self.epoch = 0
        self.ekey = {}
        self._new_engine_sems()
        self.dma_pool = []
        for i in range(n_dma_sems):
            k = "dma%d" % i
            self.sems[k] = nc.semaphore(k).__enter__()
            self.val[k] = 0
            self.dma_pool.append(k)
        self.nwait = 0

    def _new_engine_sems(self):
        for e in self.ENG:
            k = "%s_e%d" % (e, self.epoch)
            self.sems[k] = self.nc.semaphore(k).__enter__()
            self.val[k] = 0
            self.ekey[e] = k

    def buf(self, name, acc=False):
        return Buf(name, acc)

    def pbuf(self, name):
        return Buf(name, False, True)

    def dbuf(self, name, acc=False):
        b = Buf(name, acc)
        b.dkey = self.dma_pool.pop()
        return b

    def release(self, b):
        self.dma_pool.append(b.dkey)
        b.dkey = None

    def _wait(self, e, deps):
        for k, v in deps.items():
            if v <= 0:
                continue
            if e == "pe" and k == self.ekey["pe"]:
                continue
            if self.seen[e].get(k, 0) < v:
                self.eng[e].wait_ge(self.sems[k], v)
                self.seen[e][k] = v
                self.nwait += 1

    def _deps(self, reads, writes, e=None):
        deps = {}
        for b in reads:
            _merge(deps, b.wr)
            if b.excl:
                own = self.ekey.get(e)
                _merge(deps, {k: v for k, v in b.rd.items() if k != own})
        for b in writes:
            _merge(deps, b.wr)
            _merge(deps, b.rd)
        return deps

    def _record(self, ev, reads, writes):
        for b in reads:
            _merge(b.rd, ev)
        for b in writes:
            if b.acc:
                _merge(b.wr, ev)
            else:
                b.wr = dict(ev)
                b.rd = {}

    def op(self, e, fn, reads=(), writes=(), sig=True):
        self._wait(e, self._deps(reads, writes, e))
        ins = fn(self.eng[e])
        k = self.ekey[e]
        if sig:
            ins.then_inc(self.sems[k], 1)
            self.val[k] += 1
            v = self.val[k]
        else:
            v = self.val[k] + 1
        self._record({k: v}, reads, writes)
        return ins

    def dma(self, q, out, in_, track, reads=(), writes=(), slow=False):
        self._wait(q, self._deps(reads, writes))
        if slow:
            ins = self.eng[q].dma_start(out=out, in_=in_, allow_slow_non_contiguous=True)
        else:
            ins = self.eng[q].dma_start(out=out, in_=in_)
        k = track.dkey
        ins.then_inc(self.sems[k], 16)
        self.val[k] += 16
        self._record({k: self.val[k]}, reads, writes)
        return ins

    def barrier(self, new_epoch=True):
        allv = {k: v for k, v in self.val.items() if v > 0}
        for e in self.ENG:
            self._wait(e, allv)
        if new_epoch:
            self.epoch += 1
            self._new_engine_sems()

    def wait_for(self, e, bufs):
        deps = {}
        for b in bufs:
            _merge(deps, b.wr)
            _merge(deps, b.rd)
        self._wait(e, deps)


def ffn_phase(nc, S, x_in, x_out, xin_b, xout_b, wg, wu, wd, gain_row, ntok, eps=1e-6):
    G = 256
    ng = ntok // G
    with ExitStack() as es:
        sb = lambda name, shape, dt: es.enter_context(nc.sbuf_tensor(_uniq(name), shape, dt))
        ps = lambda name, shape, dt: es.enter_context(nc.psum_tensor(_uniq(name), shape, dt))
        Wg = sb("Wg", [128, 8, DFF], BF16)
        Wu = sb("Wu", [128, 8, DFF], BF16)
        Wd = sb("Wd", [128, NF, D], BF16)
        gB = sb("gB", [128, D], F32)
        ident = sb("ident", [128, 128], BF16)
        xt = [sb("xt%d" % i, [128, D], F32) for i in range(4)]
        ot = [sb("ot%d" % i, [128, D], F32) for i in range(2)]
        hb = [sb("hb%d" % i, [128, D], BF16) for i in range(2)]
        hT = [sb("hT%d" % i, [128, 8, G], BF16) for i in range(2)]
        aT = [sb("aT%d" % i, [128, G], BF16) for i in range(3)]
        sg = [sb("sg%d" % i, [128, G], F32) for i in range(2)]
        junk = sb("junk", [128, D], BF16)
        ss = [sb("ss%d" % i, [128, 1], F32) for i in range(2)]
        rs = [sb("rs%d" % i, [128, 1], F32) for i in range(2)]
        nh = sb("nh", [128, 1], F32)
        p_gu = [ps("p_gu%d" % i, [128, 2, G], F32) for i in range(2)]
        p_dn = [ps("p_dn%d" % i, [128, 512], F32) for i in range(4)]
        p_tr = [ps("p_tr%d" % i, [128, 8, 128], BF16) for i in range(2)]

        FB = [(0, 6), (6, 12), (12, 17), (17, 22)]
        blk_of = {}
        for bi, (j0, j1) in enumerate(FB):
            for j in range(j0, j1):
                blk_of[j] = bi
        bWgL = [S.dbuf("Wg") for _ in FB]
        bWuL = [S.dbuf("Wu") for _ in FB]
        bWdL = [S.dbuf("Wd") for _ in FB]
        bgB = S.dbuf("gB")
        bxt = [S.dbuf("xt%d" % i) for i in range(4)]
        bot = [S.dbuf("ot%d" % i) for i in range(2)]
        bhb = [S.buf("hb") for _ in range(2)]
        bhT = [S.buf("hT") for _ in range(2)]
        baT = [S.buf("aT") for _ in range(3)]
        bsg = [S.buf("sg") for _ in range(2)]
        bjunk = S.buf("junk")
        bss = [S.buf("ss") for _ in range(2)]
        brs = [S.buf("rs") for _ in range(2)]
        bnh, bident = S.buf("nh"), S.buf("ident")
        bp_gu = [S.pbuf("pgu") for _ in range(2)]
        bp_dn = [S.pbuf("pdn") for _ in range(4)]
        bp_tr = [S.pbuf("ptr") for _ in range(2)]

        S.op("pool", lambda e: e.memset(nh[:], -0.5), writes=[bnh])
        S.op("pool", lambda e: e.memset(ident[:], 0.0), writes=[bident])
        S.op("pool", lambda e: e.affine_select(out=ident[:], in_=ident[:], pattern=[[-1, 128]],
                                               compare_op=ALU.not_equal, fill=1.0, base=0,
                                               channel_multiplier=1), reads=[bident], writes=[bident])
        S.dma("sp", gB[:], gain_row.to_broadcast([128, D]), bgB, writes=[bgB])
        wg_v = wg.rearrange("(c p) f -> p c f", p=128)
        wu_v = wu.rearrange("(c p) f -> p c f", p=128)
        wd_v = wd.rearrange("(c p) f -> p c f", p=128)
        for bi, (j0, j1) in enumerate(FB):
            f0, f1 = j0 * 128, j1 * 128
            S.dma("pool", Wg[:, :, f0:f1], wg_v[:, :, f0:f1], bWgL[bi], writes=[bWgL[bi]])
            S.dma("pool", Wu[:, :, f0:f1], wu_v[:, :, f0:f1], bWuL[bi], writes=[bWuL[bi]])
            S.dma("pool", Wd[:, j0:j1, :], wd_v[:, j0:j1, :], bWdL[bi], writes=[bWdL[bi]])

        def load(g):
            for s in range(2):
                i = (g % 2) * 2 + s
                t0 = g * G + s * 128
                S.dma("sp", xt[i][:], x_in[t0:t0 + 128, :], bxt[i], reads=[xin_b], writes=[bxt[i]])

        def prep_dve(g):
            for s in range(2):
                i = (g % 2) * 2 + s
                S.op("dve", lambda e: e.scalar_tensor_tensor(out=junk[:], in0=xt[i][:], scalar=1.0, in1=xt[i][:],
                                                             op0=ALU.mult, op1=ALU.mult, accum_out=ss[s][:]),
                     reads=[bxt[i]], writes=[bjunk, bss[s]])
                S.op("dve", lambda e: e.tensor_scalar(out=ss[s][:], in0=ss[s][:], scalar1=1.0 / D, scalar2=eps,
                                                      op0=ALU.mult, op1=ALU.add), reads=[bss[s]], writes=[bss[s]])
                S.op("pool", lambda e: e.tensor_tensor(out=rs[s][:], in0=ss[s][:], in1=nh[:], op=ALU.pow),
                     reads=[bss[s], bnh], writes=[brs[s]])
                S.op("dve", lambda e: e.scalar_tensor_tensor(out=hb[s][:], in0=xt[i][:], scalar=rs[s][:], in1=gB[:],
                                                             op0=ALU.mult, op1=ALU.mult),
                     reads=[bxt[i], brs[s], bgB], writes=[bhb[s]])

        def prep_pe(g):
            for s in range(2):
                for c in range(8):
                    S.op("pe", lambda e: e.transpose(p_tr[s][:, c, :], hb[s][:, c * 128:(c + 1) * 128], ident[:]),
                         reads=[bhb[s], bident], writes=[bp_tr[s]], sig=(c == 7))
                S.op("act", lambda e: e.copy(out=hT[g % 2][:, :, s * 128:(s + 1) * 128], in_=p_tr[s][:]),
                     reads=[bp_tr[s]], writes=[bhT[g % 2]])

        def gate_up(g, j):
            h = hT[g % 2]
            pg = p_gu[j % 2]
            for c in range(8):
                S.op("pe", lambda e: e.matmul(pg[:, 0, :], lhsT=Wg[:, c, j * 128:(j + 1) * 128], rhs=h[:, c, :],
                                              start=(c == 0), stop=(c == 7)),
                     reads=[bWgL[blk_of[j]], bhT[g % 2]], writes=[bp_gu[j % 2]], sig=False)
            for c in range(8):
                S.op("pe", lambda e: e.matmul(pg[:, 1, :], lhsT=Wu[:, c, j * 128:(j + 1) * 128], rhs=h[:, c, :],
                                              start=(c == 0), stop=(c == 7)),
                     reads=[bWuL[blk_of[j]], bhT[g % 2]], writes=[bp_gu[j % 2]], sig=(c == 7))
            S.op("act", lambda e: e.activation(out=sg[j % 2][:], in_=pg[:, 0, :], func=AF.Silu),
                 reads=[bp_gu[j % 2]], writes=[bsg[j % 2]])
            S.op("dve", lambda e: e.tensor_tensor(out=aT[j % 3][:], in0=sg[j % 2][:], in1=pg[:, 1, :], op=ALU.mult),
                 reads=[bsg[j % 2], bp_gu[j % 2]], writes=[baT[j % 3]])

        def down(g, j):
            for s in range(2):
                for hh in range(2):
                    S.op("pe", lambda e: e.matmul(p_dn[s * 2 + hh][:], lhsT=aT[j % 3][:, s * 128:(s + 1) * 128],
                                                  rhs=Wd[:, j, hh * 512:(hh + 1) * 512],
                                                  start=(j == 0), stop=(j == NF - 1)),
                         reads=[baT[j % 3], bWdL[blk_of[j]]], writes=[bp_dn[s * 2 + hh]], sig=(j == NF - 1 or (s == 1 and hh == 1)))

        def epilogue(g):
            for s in range(2):
                i = (g % 2) * 2 + s
                for hh in range(2):
                    S.op("dve", lambda e: e.scalar_tensor_tensor(out=ot[s][:, hh * 512:(hh + 1) * 512], in0=p_dn[s * 2 + hh][:],
                                                                 scalar=0.5, in1=xt[i][:, hh * 512:(hh + 1) * 512],
                                                                 op0=ALU.mult, op1=ALU.add),
                         reads=[bp_dn[s * 2 + hh], bxt[i]], writes=[bot[s]])
                t0 = g * G + s * 128
                S.dma("sp", x_out[t0:t0 + 128, :], ot[s][:], bot[s], reads=[bot[s]], writes=[xout_b])

        load(0)
        if ng > 1:
            load(1)
        prep_dve(0)
        prep_pe(0)
        for g in range(ng):
            gate_up(g, 0)
            for j in range(NF):
                if j + 1 < NF:
                    gate_up(g, j + 1)
                elif g + 1 < ng:
                    pass
                down(g, j)
                if j == 3 and g + 1 < ng:
                    prep_dve(g + 1)
                if j == 14 and g + 1 < ng:
                    prep_pe(g + 1)
            epilogue(g)
            if g + 2 < ng:
                load(g + 2)
        S.barrier()
        for b in bWgL + bWuL + bWdL + [bgB] + bxt + bot:
            S.release(b)


def make_ident(nc, S, ident, bident, dt_is_bf16=True):
    S.op("pool", lambda e: e.memset(ident[:], 0.0), writes=[bident])
    S.op("pool", lambda e: e.affine_select(out=ident[:], in_=ident[:], pattern=[[-1, 128]],
                                           compare_op=ALU.not_equal, fill=1.0, base=0,
                                           channel_multiplier=1), reads=[bident], writes=[bident])


def attn_in_phase(nc, S, x_in, xin_b, w_in, gain_row, qn, kn, QT, KT, V, bQT, bKT, bV, ntok, eps=1e-6):
    G = 512
    ng = ntok // G
    with ExitStack() as es:
        sb = lambda name, shape, dt: es.enter_context(nc.sbuf_tensor(_uniq(name), shape, dt))
        ps = lambda name, shape, dt: es.enter_context(nc.psum_tensor(_uniq(name), shape, dt))
        Win = sb("Win", [128, 8, 3072], BF16)
        gB = sb("gB", [128, D], F32)
        ident = sb("ident", [128, 128], BF16)
        bones = sb("bones", [128, 128], BF16)
        gq = sb("gq", [128, 1], F32)
        gk = sb("gk", [128, 1], F32)
        nh = sb("nh", [128, 1], F32)
        eps_ap = sb("eps_ap", [128, 1], F32)
        xt = [sb("xt%d" % i, [128, D], F32) for i in range(4)]
        hb = [sb("hb%d" % i, [128, D], BF16) for i in range(2)]
        hT = [sb("hT%d" % i, [128, 8, G], BF16) for i in range(2)]
        junk = sb("junk", [128, D], BF16)
        ss = [sb("ss%d" % i, [128, 1], F32) for i in range(2)]
        rs = [sb("rs%d" % i, [128, 1], F32) for i in range(2)]
        ob = [sb("ob%d" % i, [128, G], BF16) for i in range(3)]
        sq = [sb("sq%d" % i, [128, G], BF16) for i in range(2)]
        lt = [sb("lt%d" % i, [128, G], F32) for i in range(2)]
        rr = [sb("rr%d" % i, [128, G], F32) for i in range(2)]
        vb = [sb("vb%d" % i, [128, D], BF16) for i in range(2)]
        p_q = [ps("p_q%d" % i, [128, G], F32) for i in range(3)]
        p_s = [ps("p_s%d" % i, [128, G], F32) for i in range(1)]
        p_v = [ps("p_v%d" % i, [128, 512], F32) for i in range(2)]
        p_tr = [ps("p_tr%d" % i, [128, 8, 128], BF16) for i in range(2)]

        bWin, bgB, bgq, bgk = S.dbuf("Win"), S.dbuf("gB"), S.dbuf("gq"), S.dbuf("gk")
        bxt = [S.dbuf("xt") for _ in range(4)]
        bob = [S.dbuf("ob") for _ in range(3)]
        bvb = [S.dbuf("vb") for _ in range(2)]
        bhb = [S.buf("hb") for _ in range(2)]
        bhT = [S.buf("hT") for _ in range(2)]
        bjunk, bnh, bident, bbones = S.buf("junk"), S.buf("nh"), S.buf("ident"), S.buf("bones")
        bss = [S.buf("ss") for _ in range(2)]
        brs = [S.buf("rs") for _ in range(2)]
        bsq = [S.buf("sq") for _ in range(2)]
        blt = [S.buf("lt") for _ in range(2)]
        brr = [S.buf("rr") for _ in range(2)]
        bp_q = [S.pbuf("pq") for _ in range(3)]
        bp_s = [S.pbuf("ps") for _ in range(1)]
        bp_v = [S.pbuf("pv") for _ in range(2)]
        bp_tr = [S.pbuf("ptr") for _ in range(2)]

        S.op("pool", lambda e: e.memset(nh[:], -0.5), writes=[bnh])
        S.op("pool", lambda e: e.memset(eps_ap[:], eps), writes=[bnh])
        make_ident(nc, S, ident, bident)
        S.op("pool", lambda e: e.memset(bones[:], 0.0), writes=[bbones])
        S.op("pool", lambda e: e.memset(bones[0:64, 0:64], 1.0), writes=[bbones])
        S.op("pool", lambda e: e.memset(bones[64:128, 64:128], 1.0), writes=[bbones])
        S.dma("sp", gB[:], gain_row.to_broadcast([128, D]), bgB, writes=[bgB])
        for hh in range(2):
            S.dma("sp", gq[hh * 64:(hh + 1) * 64, :], qn, bgq, writes=[bgq])
            S.dma("sp", gk[hh * 64:(hh + 1) * 64, :], kn, bgk, writes=[bgk])
        S.op("dve", lambda e: e.tensor_scalar(out=gq[:], in0=gq[:], scalar1=0.125, scalar2=None, op0=ALU.mult),
             reads=[bgq], writes=[bgq])
        w_v = w_in.rearrange("(c p) f -> p c f", p=128)
        for c in range(8):
            S.dma("pool", Win[:, c, :], w_v[:, c, :], bWin, writes=[bWin])

        def load(g):
            for s in range(4):
                t0 = g * G + s * 128
                S.dma("sp", xt[s][:], x_in[t0:t0 + 128, :], bxt[s], reads=[xin_b], writes=[bxt[s]])

        def prep(g):
            for s in range(4):
                k = s % 2
                S.op("dve", lambda e: e.scalar_tensor_tensor(out=junk[:], in0=xt[s][:], scalar=1.0, in1=xt[s][:],
                                                             op0=ALU.mult, op1=ALU.mult, accum_out=ss[k][:]),
                     reads=[bxt[s]], writes=[bjunk, bss[k]])
                S.op("dve", lambda e: e.tensor_scalar(out=ss[k][:], in0=ss[k][:], scalar1=1.0 / D, scalar2=eps,
                                                      op0=ALU.mult, op1=ALU.add), reads=[bss[k]], writes=[bss[k]])
                S.op("pool", lambda e: e.tensor_tensor(out=rs[k][:], in0=ss[k][:], in1=nh[:], op=ALU.pow),
                     reads=[bss[k], bnh], writes=[brs[k]])
                S.op("dve", lambda e: e.scalar_tensor_tensor(out=hb[k][:], in0=xt[s][:], scalar=rs[k][:], in1=gB[:],
                                                             op0=ALU.mult, op1=ALU.mult),
                     reads=[bxt[s], brs[k], bgB], writes=[bhb[k]])
                for c in range(8):
                    S.op("pe", lambda e: e.transpose(p_tr[k][:, c, :], hb[k][:, c * 128:(c + 1) * 128], ident[:]),
                         reads=[bhb[k], bident], writes=[bp_tr[k]], sig=(c == 7))
                S.op("act", lambda e: e.copy(out=hT[g % 2][:, :, s * 128:(s + 1) * 128], in_=p_tr[k][:]),
                     reads=[bp_tr[k]], writes=[bhT[g % 2]])

        nob = 0

        def _dummy():
            pass
        load(0)
        prep(0)
        for g in range(ng):
            h = hT[g % 2]
            bh = bhT[g % 2]
            t0 = g * G
            def fcinfo(fc):
                isq = fc < 8
                ch = fc % 8
                if ch < 2:
                    col0 = (0 if isq else 256) + ch * 128
                else:
                    col0 = (768 if isq else 1536) + (ch - 2) * 128
                return isq, ch, col0

            def main(fc):
                isq, ch, col0 = fcinfo(fc)
                pq = p_q[fc % 3]
                for c in range(8):
                    S.op("pe", lambda e: e.matmul(pq[:], lhsT=Win[:, c, col0:col0 + 128], rhs=h[:, c, :],
                                                  start=(c == 0), stop=(c == 7)),
                         reads=[bWin, bh], writes=[bp_q[fc % 3]], sig=(c == 7))

            def tail(fc):
                nonlocal nob
                isq, ch, col0 = fcinfo(fc)
                pq = p_q[fc % 3]
                bpq = bp_q[fc % 3]
                o = ob[nob % 3]
                bo = bob[nob % 3]
                nob += 1
                if ch < 2:
                    S.op("act", lambda e: e.activation(out=o[:], in_=pq[:], func=AF.Copy, scale=(0.125 if isq else 1.0)),
                         reads=[bpq], writes=[bo])
                else:
                    k = fc % 2
                    S.op("act", lambda e: e.activation(out=sq[k][:], in_=pq[:], func=AF.Square),
                         reads=[bpq], writes=[bsq[k]])
                    S.op("pe", lambda e: e.matmul(p_s[0][:], lhsT=bones[:], rhs=sq[k][:], start=True, stop=True),
                         reads=[bbones, bsq[k]], writes=[bp_s[0]])
                    S.op("act", lambda e: e.activation(out=lt[k][:], in_=p_s[0][:], func=AF.Ln, scale=1.0 / 64, bias=eps_ap[:]),
                         reads=[bp_s[0], bnh], writes=[blt[k]])
                    S.op("act", lambda e: e.activation(out=rr[k][:], in_=lt[k][:], func=AF.Exp, scale=-0.5),
                         reads=[blt[k]], writes=[brr[k]])
                    gcol = gq if isq else gk
                    S.op("dve", lambda e: e.scalar_tensor_tensor(out=o[:], in0=pq[:], scalar=gcol[:], in1=rr[k][:],
                                                                 op0=ALU.mult, op1=ALU.mult),
                         reads=[bpq, brr[k], bgq, bgk], writes=[bo])
                dst, bdst = (QT, bQT) if isq else (KT, bKT)
                S.dma("sp", dst[ch * 128:(ch + 1) * 128, t0:t0 + G], o[:], bo, reads=[bo], writes=[bdst])

            main(0)
            for fc in range(16):
                if fc + 1 < 16:
                    main(fc + 1)
                tail(fc)
                if fc == 1 and g + 1 < ng:
                    load(g + 1)
                if fc == 6 and g + 1 < ng:
                    prep(g + 1)
            for s in range(4):
                k = s % 2
                for (pv, cols, off) in ((p_v[0], (512, 768), 0), (p_v[0], (2304, 2560), 256), (p_v[1], (2560, 3072), 0)):
                    n = cols[1] - cols[0]
                    for c in range(8):
                        S.op("pe", lambda e: e.matmul(pv[:, off:off + n], lhsT=h[:, c, s * 128:(s + 1) * 128],
                                                      rhs=Win[:, c, cols[0]:cols[1]], start=(c == 0), stop=(c == 7)),
                             reads=[bWin, bh], writes=[bp_v[0], bp_v[1]], sig=(c == 7))
                S.op("act", lambda e: e.copy(out=vb[k][:, 0:512], in_=p_v[0][:]), reads=[bp_v[0]], writes=[bvb[k]])
                S.op("dve", lambda e: e.tensor_copy(out=vb[k][:, 512:1024], in_=p_v[1][:]), reads=[bp_v[1]], writes=[bvb[k]])
                S.dma("sp", V[t0 + s * 128:t0 + (s + 1) * 128, :], vb[k][:], bvb[k], reads=[bvb[k]], writes=[bV])
        S.barrier()
        for b in [bWin, bgB, bgq, bgk] + bxt + bob + bvb:
            S.release(b)


def sb_attn_phase(nc, S, QT, KT, V, MT, bQT, bKT, bV, bMT, ntok):
    nblk = ntok // 128
    with ExitStack() as es:
        sb = lambda name, shape, dt: es.enter_context(nc.sbuf_tensor(_uniq(name), shape, dt))
        ps = lambda name, shape, dt: es.enter_context(nc.psum_tensor(_uniq(name), shape, dt))
        qT = sb("qT", [128, 2, ntok], BF16)
        kT = sb("kT", [128, 2, ntok], BF16)
        v = sb("v", [128, nblk, 256], BF16)
        ones = sb("ones", [128, 512], F32)
        onec = sb("onec", [128, 1], F32)
        mneg = sb("mneg", [128, 128], BF16)
        ident = sb("ident", [128, 128], BF16)
        mk2 = lambda nm, shape, dt: [[sb("%s%d_%d" % (nm, h, i), shape, dt) for i in range(2)] for h in range(2)]
        e_ = mk2("e", [128, 512], F32)
        sp_ = mk2("sp", [128, 512], F32)
        cs_ = mk2("cs", [128, 512], F32)
        lw_ = mk2("lw", [128, 512], F32)
        w_ = mk2("w", [128, 512], BF16)
        wT_ = mk2("wT", [128, 4, 128], BF16)
        oT = [sb("oT%d" % i, [128, 512], BF16) for i in range(2)]
        p_z = [[ps("p_z%d_%d" % (h, i), [128, 512], F32) for i in range(2)] for h in range(2)]
        p_w = [ps("p_w%d" % i, [128, 4, 128], BF16) for i in range(2)]
        p_o = [ps("p_o%d" % i, [128, 128], F32) for i in range(2)]

        bq, bk, bv = S.dbuf("qT"), S.dbuf("kT"), S.dbuf("v")
        boT = [S.dbuf("oT") for _ in range(2)]
        bones, bmneg, bident = S.buf("ones"), S.buf("mneg"), S.buf("ident")
        bb2 = lambda nm: [[S.buf(nm) for _ in range(2)] for _ in range(2)]
        be, bsp, bcs, blw, bw, bwT = bb2("e"), bb2("sp"), bb2("cs"), bb2("lw"), bb2("w"), bb2("wT")
        bp_z = [[S.pbuf("pz") for _ in range(2)] for _ in range(2)]
        bp_w = [S.pbuf("pw") for _ in range(2)]
        bp_o = [S.pbuf("po") for _ in range(2)]

        S.op("pool", lambda e: e.memset(ones[:], 0.0), writes=[bones])
        S.op("pool", lambda e: e.memset(onec[:], 1.0), writes=[bones])
        make_ident(nc, S, ident, bident)
        S.op("pool", lambda e: e.memset(mneg[:], 0.0), writes=[bmneg])
        S.op("pool", lambda e: e.affine_select(out=mneg[:], in_=mneg[:], pattern=[[-1, 128]], compare_op=ALU.is_gt,
                                               fill=-30000.0, base=0, channel_multiplier=1),
             reads=[bmneg], writes=[bmneg])
        for pr in range(2):
            S.dma("sp", qT[:, pr, :], QT[pr * 128:(pr + 1) * 128, 0:ntok], bq, reads=[bQT], writes=[bq])
            S.dma("sp", kT[:, pr, :], KT[pr * 128:(pr + 1) * 128, 0:ntok], bk, reads=[bKT], writes=[bk])
        S.dma("sp", v[:], V[0:ntok, 0:256].rearrange("(b p) c -> p b c", p=128), bv, reads=[bV], writes=[bv])

        steps = []
        for pr in range(2):
            for qb in range(nblk):
                chunks = [(4 * (qb // 4), qb + 1, True)]
                for c in range(qb // 4 - 1, -1, -1):
                    chunks.append((4 * c, 4 * c + 4, False))
                for ci, (b0, b1, diag) in enumerate(chunks):
                    steps.append((pr, qb, ci, b0, b1, diag, len(chunks)))
        HH = [(0, slice(0, 64)), (1, slice(64, 128))]

        def front(t):
            pr, qb, ci, b0, b1, diag, nchk = steps[t]
            W = (b1 - b0) * 128
            i = t % 2
            for (hh, P) in HH:
                pz, bpz = p_z[hh][i], bp_z[hh][i]
                S.op("pe", lambda e: e.matmul(pz[:, 0:W], lhsT=qT[P, pr, qb * 128:(qb + 1) * 128],
                                              rhs=kT[P, pr, b0 * 128:b1 * 128], start=True, stop=(not diag)),
                     reads=[bq, bk], writes=[bpz], sig=(not diag))
                if diag:
                    S.op("pe", lambda e: e.matmul(pz[:, W - 128:W], lhsT=ident[:], rhs=mneg[:], start=False, stop=True),
                         reads=[bident, bmneg], writes=[bpz])
            for (hh, P) in HH:
                S.op("act", lambda e: e.activation(out=e_[hh][i][:, 0:W], in_=p_z[hh][i][:, 0:W], func=AF.Sigmoid),
                     reads=[bp_z[hh][i]], writes=[be[hh][i]])
            for (hh, P) in HH:
                S.op("act", lambda e: e.activation(out=sp_[hh][i][:, 0:W], in_=p_z[hh][i][:, 0:W], func=AF.Sigmoid, scale=-1.0),
                     reads=[bp_z[hh][i]], writes=[bsp[hh][i]])

        def back(t):
            pr, qb, ci, b0, b1, diag, nchk = steps[t]
            W = (b1 - b0) * 128
            nb = b1 - b0
            i = t % 2
            rev = (lambda tt: tt[:, W - 1::-1] if W < 512 else tt[:, ::-1])
            for (hh, P) in HH:
                if ci == 0:
                    init, rd = 1.0, [bsp[hh][i], bones]
                else:
                    init, rd = cs_[hh][1 - i][:, 0:1], [bsp[hh][i], bones, bcs[hh][1 - i]]
                S.op("dve", lambda e: e.tensor_tensor_scan(out=rev(cs_[hh][i]), data0=rev(sp_[hh][i]), data1=ones[:, 0:W],
                                                           initial=init, op0=ALU.mult, op1=ALU.add),
                     reads=rd, writes=[bcs[hh][i]])
            for (hh, P) in HH:
                if W > 1:
                    S.op("dve", lambda e: e.tensor_tensor(out=w_[hh][i][:, 0:W - 1], in0=e_[hh][i][:, 0:W - 1], in1=cs_[hh][i][:, 1:W], op=ALU.mult),
                         reads=[be[hh][i], bcs[hh][i]], writes=[bw[hh][i]])
                if ci == 0:
                    S.op("dve", lambda e: e.tensor_copy(out=w_[hh][i][:, W - 1:W], in_=e_[hh][i][:, W - 1:W]),
                         reads=[be[hh][i]], writes=[bw[hh][i]])
                else:
                    S.op("dve", lambda e: e.tensor_scalar(out=w_[hh][i][:, W - 1:W], in0=e_[hh][i][:, W - 1:W], scalar1=cs_[hh][1 - i][:, 0:1],
                                                          scalar2=None, op0=ALU.mult),
                         reads=[be[hh][i], bcs[hh][1 - i]], writes=[bw[hh][i]])
            for (hh, P) in HH:
                pw, bpw = p_w[hh], bp_w[hh]
                for b in range(nb):
                    S.op("pe", lambda e: e.transpose(pw[:, b, :], w_[hh][i][:, b * 128:(b + 1) * 128], ident[:]),
                         reads=[bw[hh][i], bident], writes=[bpw], sig=(b == nb - 1))
                if hh == 0:
                    S.op("act", lambda e: e.copy(out=wT_[hh][i][:, 0:nb, :], in_=pw[:, 0:nb, :]), reads=[bpw], writes=[bwT[hh][i]])
                else:
                    S.op("dve", lambda e: e.tensor_copy(out=wT_[hh][i][:, 0:nb, :], in_=pw[:, 0:nb, :]), reads=[bpw], writes=[bwT[hh][i]])
            po, bpo = p_o[qb % 2], bp_o[qb % 2]
            for (hh, P) in HH:
                h = 2 * pr + hh
                for b in range(nb):
                    first = (ci == 0 and b == 0)
                    last = (ci == nchk - 1 and b == nb - 1)
                    S.op("pe", lambda e: e.matmul(po[P, :], lhsT=v[:, b0 + b, h * 64:(h + 1) * 64], rhs=wT_[hh][i][:, b, :],
                                                  start=first, stop=last),
                         reads=[bv, bwT[hh][i]], writes=[bpo], sig=(b == nb - 1))
            if ci == nchk - 1:
                k = (qb // 4) % 2
                S.op("dve", lambda e: e.tensor_copy(out=oT[k][:, (qb % 4) * 128:(qb % 4 + 1) * 128], in_=po[:, :]),
                     reads=[bpo], writes=[boT[k]])
                if qb % 4 == 3 or qb == nblk - 1:
                    q0 = 4 * (qb // 4)
                    n = (qb - q0 + 1) * 128
                    S.dma("sp", MT[pr * 128:(pr + 1) * 128, q0 * 128:q0 * 128 + n], oT[k][:, 0:n], boT[k],
                          reads=[boT[k]], writes=[bMT])

        front(0)
        for t in range(len(steps)):
            if t + 1 < len(steps):
                front(t + 1)
            back(t)
        S.barrier()
        for b_ in [bq, bk, bv] + boT:
            S.release(b_)


def dil_bias_host(rel_bias):
    out = np.empty((12, 128, 2, 128), np.float32)
    kj = np.arange(128)[:, None]
    q = np.arange(128)[None, :]
    for g, r in enumerate((1, 4, 16)):
        for part, dist in ((1, q - kj), (0, q + 128 - kj)):
            valid = (dist >= 0) & (dist <= 128)
            dd = np.maximum(dist, 0) * r
            d = np.maximum(dd, 1).astype(np.float32)
            large = 16 + (np.log(d / np.float32(16)) / np.float32(np.log(2048 / 16)) * np.float32(16)).astype(np.int32)
            large = np.minimum(large, 31)
            bucket = np.where(dd < 16, dd, large)
            for j in range(4):
                hd = 4 * g + j
                out[hd, :, part, :] = np.where(valid, rel_bias[bucket, hd], np.float32(-30000.0))
    return out


def dil_attn_phase(nc, S, QT, KT, V, MT, dbias, bQT, bKT, bV, bMT, ntok):
    with ExitStack() as es:
        sb = lambda name, shape, dt: es.enter_context(nc.sbuf_tensor(_uniq(name), shape, dt))
        ps = lambda name, shape, dt: es.enter_context(nc.psum_tensor(_uniq(name), shape, dt))
        qT = sb("qT", [128, 2, ntok], BF16)
        kT = sb("kT", [128, 2, ntok], BF16)
        v = sb("v", [128, ntok // 128, 256], BF16)
        bias = sb("bias", [128, 4, 256], F32)
        onesb = sb("onesb", [128, 64], BF16)
        Nacc = sb("Nacc", [128, 2, ntok], F32)
        Dacc = sb("Dacc", [128, 2, ntok], F32)
        s_ = [sb("s%d" % i, [128, 256], F32) for i in range(3)]
        pT_ = [sb("pT%d" % i, [128, 2, 128], BF16) for i in range(3)]
        ob = [sb("ob%d" % i, [128, 1024], BF16) for i in range(2)]
        p_s = [ps("p_s%d" % i, [128, 2, 128], F32) for i in range(2)]
        p_n = [ps("p_n%d" % i, [128, 128], F32) for i in range(2)]
        p_d = [ps("p_d%d" % i, [128, 128], F32) for i in range(2)]
        p_pad = [ps("p_pad%d" % i, [128, 256], F32) for i in range(0)]

        bq, bk, bv, bbias = S.dbuf("qT"), S.dbuf("kT"), S.dbuf("v"), S.dbuf("bias")
        bob = [S.dbuf("ob") for _ in range(2)]
        bones, bN, bD = S.buf("ones"), S.buf("N"), S.buf("D")
        bs = [S.buf("s") for _ in range(3)]
        bpT = [S.buf("pT") for _ in range(3)]
        bp_s = [S.pbuf("ps") for _ in range(2)]
        bp_n = [S.pbuf("pn") for _ in range(2)]
        bp_d = [S.pbuf("pd") for _ in range(2)]

        S.op("pool", lambda e: e.memset(onesb[:], 1.0), writes=[bones])
        u = 0
        for g, r in enumerate((1, 4, 16)):
            L = ntok // r
            nb = L // 128
            for pr in range(2):
                r0 = 256 + (2 * g + pr) * 128
                S.dma("sp", qT[:, pr, :], QT[r0:r0 + 128, 0:ntok], bq, reads=[bQT], writes=[bq])
                S.dma("sp", kT[:, pr, :], KT[r0:r0 + 128, 0:ntok], bk, reads=[bKT], writes=[bk])
            vsrc = V[0:ntok, 256 + g * 256:256 + (g + 1) * 256].rearrange("(n i c) f -> c i n f", i=128, c=r)
            for c in range(r):
                S.dma("sp", v[:, c * nb:(c + 1) * nb, :], vsrc[c], bv, reads=[bV], writes=[bv])
            for j in range(4):
                S.dma("sp", bias[:, j, :], dbias[4 * g + j].rearrange("k a q -> k (a q)"), bbias, writes=[bbias])
            units = []
            for j in range(4):
                for c in range(r):
                    for n in range(nb):
                        units.append((j, c, n))

            def tokf(c, nn):
                st = c + r * 128 * nn
                return slice(st, st + r * 127 + 1, r)

            def front(uu, u):
                j, c, n = units[uu]
                pr, hh = j // 2, j % 2
                P = slice(64 * hh, 64 * hh + 64)
                i = u % 3
                pss, bpss = p_s[u % 2], bp_s[u % 2]
                a0 = 0 if n > 0 else 1
                if n > 0:
                    S.op("pe", lambda e: e.matmul(pss[:, 0, :], lhsT=kT[P, pr, tokf(c, n - 1)], rhs=qT[P, pr, tokf(c, n)],
                                                  start=True, stop=True), reads=[bq, bk], writes=[bpss], sig=False)
                S.op("pe", lambda e: e.matmul(pss[:, 1, :], lhsT=kT[P, pr, tokf(c, n)], rhs=qT[P, pr, tokf(c, n)],
                                              start=True, stop=True), reads=[bq, bk], writes=[bpss])
                S.op("dve", lambda e: e.tensor_tensor(out=s_[i][:, a0 * 128:256], in0=pss[:, a0:2, :],
                                                      in1=bias[:, j, a0 * 128:256], op=ALU.add),
                     reads=[bpss, bbias], writes=[bs[i]])
                S.op("act", lambda e: e.activation(out=pT_[i][:, a0:2, :], in_=s_[i][:, a0 * 128:256], func=AF.Exp),
                     reads=[bs[i]], writes=[bpT[i]])

            def back(uu, u):
                j, c, n = units[uu]
                pr, hh = j // 2, j % 2
                P = slice(64 * hh, 64 * hh + 64)
                i = u % 3
                pn, bpn = p_n[u % 2], bp_n[u % 2]
                pd, bpd = p_d[u % 2], bp_d[u % 2]
                a0 = 0 if n > 0 else 1
                for a_ in range(a0, 2):
                    S.op("pe", lambda e: e.matmul(pn[P, :], lhsT=v[:, c * nb + n - 1 + a_, j * 64:(j + 1) * 64],
                                                  rhs=pT_[i][:, a_, :], start=(a_ == a0), stop=(a_ == 1)),
                         reads=[bv, bpT[i]], writes=[bpn], sig=(a_ == 1))
                for a_ in range(a0, 2):
                    S.op("pe", lambda e: e.matmul(pd[P, :], lhsT=onesb[:, :], rhs=pT_[i][:, a_, :],
                                                  start=(a_ == a0), stop=(a_ == 1)),
                         reads=[bones, bpT[i]], writes=[bpd], sig=(a_ == 1))
                tk = tokf(c, n)
                if g == 0:
                    S.op("act", lambda e: e.copy(out=Nacc[P, pr, tk], in_=pn[P, :]), reads=[bpn], writes=[bN])
                    S.op("dve", lambda e: e.tensor_copy(out=Dacc[P, pr, tk], in_=pd[P, :]), reads=[bpd], writes=[bD])
                else:
                    S.op("dve", lambda e: e.tensor_tensor(out=Nacc[P, pr, tk], in0=pn[P, :], in1=Nacc[P, pr, tk],
                                                          op=ALU.add), reads=[bpn, bN], writes=[bN])
                    S.op("dve", lambda e: e.tensor_tensor(out=Dacc[P, pr, tk], in0=pd[P, :], in1=Dacc[P, pr, tk],
                                                          op=ALU.add), reads=[bpd, bD], writes=[bD])

            front(0, u)
            for uu in range(len(units)):
                if uu + 1 < len(units):
                    front(uu + 1, u + 1)
                back(uu, u)
                u += 1
        k = 0
        for pr in range(2):
            for c0 in range(0, ntok, 1024):
                n = min(1024, ntok - c0)
                S.op("dve", lambda e: e.reciprocal(out=Dacc[:, pr, c0:c0 + n], in_=Dacc[:, pr, c0:c0 + n]), reads=[bD], writes=[bD])
                S.op("dve", lambda e: e.tensor_tensor(out=ob[k % 2][:, 0:n], in0=Nacc[:, pr, c0:c0 + n], in1=Dacc[:, pr, c0:c0 + n],
                                                      op=ALU.mult), reads=[bN, bD], writes=[bob[k % 2]])
                S.dma("sp", MT[256 + pr * 128:256 + (pr + 1) * 128, c0:c0 + n], ob[k % 2][:, 0:n], bob[k % 2],
                      reads=[bob[k % 2]], writes=[bMT])
                k += 1
        S.barrier()
        for b in [bq, bk, bv, bbias] + bob:
            S.release(b)


def out_proj_phase(nc, S, x_in, x_out, xin_b, xout_b, MT, bMT, w_out, kdim, ntok):
    G = 512
    ng = ntok // G
    kc = kdim // 128
    nt = ntok // 128
    with ExitStack() as es:
        sb = lambda name, shape, dt: es.enter_context(nc.sbuf_tensor(_uniq(name), shape, dt))
        ps = lambda name, shape, dt: es.enter_context(nc.psum_tensor(_uniq(name), shape, dt))
        Wo = sb("Wo", [128, kc, D], BF16)
        mT = [sb("mT%d" % i, [128, kc, G], BF16) for i in range(2)]
        xt = [sb("xt%d" % i, [128, D], F32) for i in range(4)]
        ot = [sb("ot%d" % i, [128, D], F32) for i in range(3)]
        p_y = [ps("p_y%d" % i, [128, 512], F32) for i in range(4)]
        bWo = S.dbuf("Wo")
        bmT = [S.dbuf("mT") for _ in range(2)]
        bxt = [S.dbuf("xt") for _ in range(4)]
        bot = [S.dbuf("ot") for _ in range(3)]
        bp_y = [S.pbuf("py") for _ in range(4)]
        w_v = w_out.rearrange("(c p) f -> p c f", p=128)
        for c in range(kc):
            S.dma("pool", Wo[:, c, :], w_v[:, c, :], bWo, writes=[bWo])

        def load_m(g):
            m, bm = mT[g % 2], bmT[g % 2]
            for c in range(kc):
                S.dma("sp", m[:, c, :], MT[c * 128:(c + 1) * 128, g * G:(g + 1) * G], bm, reads=[bMT], writes=[bm])

        def load_x(k):
            S.dma("sp", xt[k % 4][:], x_in[k * 128:(k + 1) * 128, :], bxt[k % 4], reads=[xin_b], writes=[bxt[k % 4]])

        load_m(0)
        load_x(0)
        load_x(1)
        if ng > 1:
            load_m(1)
        for k in range(nt):
            g, s = k // 4, k % 4
            m, bm = mT[g % 2], bmT[g % 2]
            if k + 2 < nt:
                load_x(k + 2)
            x_, bx = xt[k % 4], bxt[k % 4]
            o_, bo = ot[k % 3], bot[k % 3]
            for hh in range(2):
                py, bpy = p_y[(2 * k + hh) % 4], bp_y[(2 * k + hh) % 4]
                for c in range(kc):
                    S.op("pe", lambda e: e.matmul(py[:], lhsT=m[:, c, s * 128:(s + 1) * 128], rhs=Wo[:, c, hh * 512:(hh + 1) * 512],
                                                  start=(c == 0), stop=(c == kc - 1)),
                         reads=[bm, bWo], writes=[bpy], sig=(c == kc - 1))
                S.op("dve", lambda e: e.tensor_tensor(out=o_[:, hh * 512:(hh + 1) * 512], in0=py[:], in1=x_[:, hh * 512:(hh + 1) * 512],
                                                      op=ALU.add), reads=[bpy, bx], writes=[bo])
            S.dma("act", x_out[k * 128:(k + 1) * 128, :], o_[:], bo, reads=[bo], writes=[xout_b])
            if s == 3 and g + 2 < ng:
                load_m(g + 2)
        S.barrier()
        for b_ in [bWo] + bmT + bxt + bot:
            S.release(b_)


C0 = float(np.exp(-0.5))


class RwPrep:
    def __init__(self, nc, S, es, G, mix_ids, x_in, xin_b, gain_row, mix, eps=1e-6):
        sb = lambda name, shape, dt: es.enter_context(nc.sbuf_tensor(_uniq(name), shape, dt))
        ps = lambda name, shape, dt: es.enter_context(nc.psum_tensor(_uniq(name), shape, dt))
        self.nc, self.S, self.G, self.mix_ids, self.x_in, self.xin_b, self.eps = nc, S, G, mix_ids, x_in, xin_b, eps
        self.gB = sb("gB", [128, D], F32)
        self.identf = sb("identf", [128, 128], F32)
        self.nh = sb("nh", [128, 1], F32)
        self.mixc = sb("mixc", [128, 6, 8], F32)
        self.xt = [sb("xt%d" % i, [128, D], F32) for i in range(2)]
        self.hn = [sb("hn%d" % i, [128, D], F32) for i in range(2)]
        self.junk = sb("junk", [128, D], BF16)
        self.ss = [sb("ss%d" % i, [128, 1], F32) for i in range(2)]
        self.rs = [sb("rs%d" % i, [128, 1], F32) for i in range(2)]
        self.hT = [sb("hT%d" % i, [128, 8, G + 1], F32) for i in range(2)]
        self.xx = [sb("xx%d" % i, [128, G], F32) for i in range(2)]
        self.xm = {i: sb("xm%d" % i, [128, 8, G], BF16) for i in mix_ids}
        self.p_tr = [ps("p_tr%d" % i, [128, 4, 128], F32) for i in range(2)]
        self.bgB, self.bmixc = S.dbuf("gB"), S.dbuf("mixc")
        self.bxt = [S.dbuf("xt") for _ in range(2)]
        self.bhn = [S.buf("hn") for _ in range(2)]
        self.bident, self.bnh, self.bjunk = S.buf("identf"), S.buf("nh"), S.buf("junk")
        self.bss = [S.buf("ss") for _ in range(2)]
        self.brs = [S.buf("rs") for _ in range(2)]
        self.bhT = [S.buf("hT") for _ in range(2)]
        self.bxx = [S.buf("xx") for _ in range(2)]
        self.bxm = {i: S.buf("xm") for i in mix_ids}
        self.bp_tr = [S.pbuf("ptr") for _ in range(2)]
        S.op("pool", lambda e: e.memset(self.nh[:], -0.5), writes=[self.bnh])
        make_ident(nc, S, self.identf, self.bident)
        S.dma("sp", self.gB[:], gain_row.to_broadcast([128, D]), self.bgB, writes=[self.bgB])
        for i in range(6):
            S.dma("sp", self.mixc[:, i, :], mix[i:i + 1, :].rearrange("o (c p) -> p (o c)", p=128), self.bmixc, writes=[self.bmixc], slow=True)
        S.op("dve", lambda e: e.memset(self.hT[1][:, :, G:G + 1], 0.0), writes=[self.bhT[1]])
        self.dsems = [self.bgB, self.bmixc] + self.bxt

    def group(self, g):
        nc, S, G = self.nc, self.S, self.G
        hT, bhT = self.hT[g % 2], self.bhT[g % 2]
        hTp, bhTp = self.hT[(g + 1) % 2], self.bhT[(g + 1) % 2]
        S.op("pool", lambda e: e.tensor_copy(out=hT[:, :, 0:1], in_=hTp[:, :, G:G + 1]), reads=[bhTp], writes=[bhT])
        k = 0
        for s in range(G // 128):
            t0 = g * G + s * 128
            x_, bx = self.xt[s % 2], self.bxt[s % 2]
            h_, bh = self.hn[s % 2], self.bhn[s % 2]
            ss, bss, rs, brs = self.ss[s % 2], self.bss[s % 2], self.rs[s % 2], self.brs[s % 2]
            S.dma("sp", x_[:], self.x_in[t0:t0 + 128, :], bx, reads=[self.xin_b], writes=[bx])
            S.op("dve", lambda e: e.scalar_tensor_tensor(out=self.junk[:], in0=x_[:], scalar=1.0, in1=x_[:], op0=ALU.mult, op1=ALU.mult,
                                                         accum_out=ss[:]), reads=[bx], writes=[self.bjunk, bss])
            S.op("dve", lambda e: e.tensor_scalar(out=ss[:], in0=ss[:], scalar1=1.0 / D, scalar2=self.eps, op0=ALU.mult, op1=ALU.add),
                 reads=[bss], writes=[bss])
            S.op("pool", lambda e: e.tensor_tensor(out=rs[:], in0=ss[:], in1=self.nh[:], op=ALU.pow), reads=[bss, self.bnh], writes=[brs])
            S.op("dve", lambda e: e.scalar_tensor_tensor(out=h_[:], in0=x_[:], scalar=rs[:], in1=self.gB[:], op0=ALU.mult, op1=ALU.mult),
                 reads=[bx, brs, self.bgB], writes=[bh])
            for half in range(2):
                pt, bpt = self.p_tr[k % 2], self.bp_tr[k % 2]
                k += 1
                for c4 in range(4):
                    c = half * 4 + c4
                    S.op("pe", lambda e: e.transpose(pt[:, c4, :], h_[:, c * 128:(c + 1) * 128], self.identf[:]),
                         reads=[bh, self.bident], writes=[bpt], sig=(c4 == 3))
                S.op("act", lambda e: e.copy(out=hT[:, half * 4:half * 4 + 4, 1 + s * 128:1 + (s + 1) * 128], in_=pt[:]),
                     reads=[bpt], writes=[bhT])
        for c in range(8):
            xx, bxx = self.xx[c % 2], self.bxx[c % 2]
            S.op("dve", lambda e: e.tensor_tensor(out=xx[:], in0=hT[:, c, 0:G], in1=hT[:, c, 1:G + 1], op=ALU.subtract),
                 reads=[bhT], writes=[bxx])
            for n, i in enumerate(self.mix_ids):
                S.op("dve", lambda e: e.scalar_tensor_tensor(out=self.xm[i][:, c, :], in0=xx[:], scalar=self.mixc[:, i, c:c + 1],
                                                             in1=hT[:, c, 1:G + 1], op0=ALU.mult, op1=ALU.add),
                     reads=[bxx, bhT, self.bmixc], writes=[self.bxm[i]])

    def release(self):
        for b in self.dsems:
            self.S.release(b)


def col_load(S, dst, src_row, track):
    S.dma("sp", dst, src_row.rearrange("o (c p) -> p (o c)", p=128), track, writes=[track], slow=True)


def rwkv_fm_phase(nc, S, x_in, xin_b, gain_row, mix, w0, w1, w2, a0, a1, a2, kk_, ka_, w_r, w_k,
                  RtT, KtT, AtT, BtT, WC, bouts, ntok):
    G = 512
    ng = ntok // G
    with ExitStack() as es:
        sb = lambda name, shape, dt: es.enter_context(nc.sbuf_tensor(_uniq(name), shape, dt))
        ps = lambda name, shape, dt: es.enter_context(nc.psum_tensor(_uniq(name), shape, dt))
        P = RwPrep(nc, S, es, G, (0, 1, 2, 4), x_in, xin_b, gain_row, mix)
        Wr = sb("Wr", [128, 8, D], BF16)
        Wk = sb("Wk", [128, 8, D], BF16)
        W1 = sb("W1", [128, 8, 64], BF16)
        A1 = sb("A1", [128, 8, 64], BF16)
        W2 = sb("W2", [64, D], BF16)
        A2 = sb("A2", [64, D], BF16)
        cols = sb("cols", [128, 5, 8], F32)
        bones = sb("bones", [128, 128], BF16)
        rmask = sb("rmask", [128, G], F32)
        tiny = sb("tiny", [128, 1], F32)
        tw = sb("tw", [64, G], BF16)
        ta = sb("ta", [64, G], BF16)
        names = ["sgu", "av", "kk0", "lnk", "rk", "kkn", "t1", "kp", "csg", "eW", "eWi", "t2"]
        T2 = {n: [sb(n + "_%d" % i, [128, G], F32) for i in range(2)] for n in names}
        sqk2 = [sb("sqk%d" % i, [128, G], BF16) for i in range(2)]
        wc = [sb("wc%d" % i, [128, G // 64], F32) for i in range(2)]
        ob = [sb("ob%d" % i, [128, G], BF16) for i in range(8)]
        p_r = ps("p_r", [128, G], F32)
        p_k = ps("p_k", [128, G], F32)
        p_u = ps("p_u", [128, G], F32)
        p_a = ps("p_a", [128, G], F32)
        p_ss = ps("p_ss", [128, G], F32)
        p_t = ps("p_t", [128, G], F32)
        bW = S.dbuf("W")
        bcols = S.dbuf("cols")
        bwc = [S.dbuf("wc") for _ in range(2)]
        bob = [S.dbuf("ob") for _ in range(8)]
        B2 = {n: [S.buf(n) for _ in range(2)] for n in names + ["sqk"]}
        B = {n: S.buf(n) for n in ["tw", "ta", "bones", "rmask"]}
        B.update({n: S.pbuf(n) for n in ["p_r", "p_k", "p_u", "p_a", "p_ss", "p_t"]})
        for c in range(8):
            S.dma("pool", Wr[:, c, :], w_r.rearrange("(c p) f -> p c f", p=128)[:, c, :], bW, writes=[bW])
            S.dma("pool", Wk[:, c, :], w_k.rearrange("(c p) f -> p c f", p=128)[:, c, :], bW, writes=[bW])
        S.dma("pool", W1[:], w1.rearrange("(c p) f -> p c f", p=128), bW, writes=[bW])
        S.dma("pool", A1[:], a1.rearrange("(c p) f -> p c f", p=128), bW, writes=[bW])
        S.dma("pool", W2[:], w2, bW, writes=[bW])
        S.dma("pool", A2[:], a2, bW, writes=[bW])
        for n, src in enumerate((w0, a0, kk_, ka_)):
            col_load(S, cols[:, n, :], src, bcols)
        S.op("dve", lambda e: e.tensor_scalar(out=cols[:, 4, :], in0=cols[:, 3, :], scalar1=-1.0, scalar2=None, op0=ALU.mult),
             reads=[bcols], writes=[bcols])
        S.op("pool", lambda e: e.memset(bones[:], 0.0), writes=[B["bones"]])
        S.op("pool", lambda e: e.memset(bones[0:64, 0:64], 1.0), writes=[B["bones"]])
        S.op("pool", lambda e: e.memset(bones[64:128, 64:128], 1.0), writes=[B["bones"]])
        S.op("pool", lambda e: e.memset(rmask[:], 1.0), writes=[B["rmask"]])
        S.op("pool", lambda e: e.memset(rmask[:, 0:G:64], 0.0), writes=[B["rmask"]])
        S.op("pool", lambda e: e.memset(tiny[:], 1e-18), writes=[B["rmask"]])
        RB, KB, AB, BB, WCB = bouts
        no = 0
        for g in range(ng):
            P.group(g)
            t0 = g * G
            xr, xw, xk, xa = P.xm[0], P.xm[1], P.xm[2], P.xm[4]
            bxr, bxw, bxk, bxa = P.bxm[0], P.bxm[1], P.bxm[2], P.bxm[4]
            for c in range(8):
                S.op("pe", lambda e: e.matmul(p_t[0:64, :], lhsT=W1[:, c, :], rhs=xw[:, c, :], start=(c == 0), stop=(c == 7)),
                     reads=[bW, bxw], writes=[B["p_t"]], sig=(c == 7))
            S.op("act", lambda e: e.activation(out=tw[:], in_=p_t[0:64, :], func=AF.Tanh), reads=[B["p_t"]], writes=[B["tw"]])
            for c in range(8):
                S.op("pe", lambda e: e.matmul(p_t[0:64, :], lhsT=A1[:, c, :], rhs=xa[:, c, :], start=(c == 0), stop=(c == 7)),
                     reads=[bW, bxa], writes=[B["p_t"]], sig=(c == 7))
            S.op("act", lambda e: e.copy(out=ta[:], in_=p_t[0:64, :]), reads=[B["p_t"]], writes=[B["ta"]])
            for cc in range(8):
                fs = slice(cc * 128, (cc + 1) * 128)
                par = cc % 2
                T = {n: T2[n][par] for n in names}
                sqk = sqk2[par]
                for n in names + ["sqk"]:
                    B[n] = B2[n][par]
                for c in range(8):
                    S.op("pe", lambda e: e.matmul(p_k[:], lhsT=Wk[:, c, fs], rhs=xk[:, c, :], start=(c == 0), stop=(c == 7)),
                         reads=[bW, bxk], writes=[B["p_k"]], sig=(c == 7))
                S.op("pe", lambda e: e.matmul(p_u[:], lhsT=W2[:, fs], rhs=tw[:], start=True, stop=True), reads=[bW, B["tw"]], writes=[B["p_u"]])
                S.op("pe", lambda e: e.matmul(p_a[:], lhsT=A2[:, fs], rhs=ta[:], start=True, stop=True), reads=[bW, B["ta"]], writes=[B["p_a"]])
                for c in range(8):
                    S.op("pe", lambda e: e.matmul(p_r[:], lhsT=Wr[:, c, fs], rhs=xr[:, c, :], start=(c == 0), stop=(c == 7)),
                         reads=[bW, bxr], writes=[B["p_r"]], sig=(c == 7))
                S.op("act", lambda e: e.activation(out=T["kk0"][:], in_=p_k[:], func=AF.Copy, scale=cols[:, 2, cc:cc + 1]),
                     reads=[B["p_k"], bcols], writes=[B["kk0"]])
                S.op("act", lambda e: e.activation(out=sqk[:], in_=p_k[:], func=AF.Square, scale=cols[:, 2, cc:cc + 1]),
                     reads=[B["p_k"], bcols], writes=[B["sqk"]])
                S.op("pe", lambda e: e.matmul(p_ss[:], lhsT=bones[:], rhs=sqk[:], start=True, stop=True), reads=[B["bones"], B["sqk"]],
                     writes=[B["p_ss"]])
                S.op("act", lambda e: e.activation(out=T["sgu"][:], in_=p_u[:], func=AF.Sigmoid, bias=cols[:, 0, cc:cc + 1]),
                     reads=[B["p_u"], bcols], writes=[B["sgu"]])
                S.op("act", lambda e: e.activation(out=T["av"][:], in_=p_a[:], func=AF.Sigmoid, bias=cols[:, 1, cc:cc + 1]),
                     reads=[B["p_a"], bcols], writes=[B["av"]])
                S.op("act", lambda e: e.activation(out=T["lnk"][:], in_=p_ss[:], func=AF.Ln, bias=tiny[:]), reads=[B["p_ss"], B["rmask"]],
                     writes=[B["lnk"]])
                S.op("act", lambda e: e.activation(out=T["rk"][:], in_=T["lnk"][:], func=AF.Exp, scale=-0.5), reads=[B["lnk"]], writes=[B["rk"]])
                S.op("dve", lambda e: e.tensor_tensor_scan(out=T["csg"][:], data0=rmask[:], data1=T["sgu"][:], initial=0.0,
                                                           op0=ALU.mult, op1=ALU.add), reads=[B["rmask"], B["sgu"]], writes=[B["csg"]])
                S.op("act", lambda e: e.activation(out=T["eW"][:], in_=T["csg"][:], func=AF.Exp, scale=-C0), reads=[B["csg"]], writes=[B["eW"]])
                S.op("act", lambda e: e.activation(out=T["eWi"][:], in_=T["csg"][:], func=AF.Exp, scale=C0), reads=[B["csg"]], writes=[B["eWi"]])
                S.op("act", lambda e: e.activation(out=T["t1"][:], in_=T["av"][:], func=AF.Identity, scale=cols[:, 3, cc:cc + 1],
                                                   bias=cols[:, 4, cc:cc + 1]), reads=[B["av"], bcols], writes=[B["t1"]])
                S.op("dve", lambda e: e.tensor_tensor(out=T["kkn"][:], in0=T["kk0"][:], in1=T["rk"][:], op=ALU.mult),
                     reads=[B["kk0"], B["rk"]], writes=[B["kkn"]])
                S.op("dve", lambda e: e.scalar_tensor_tensor(out=T["kp"][:], in0=T["t1"][:], scalar=1.0, in1=p_k[:], op0=ALU.add, op1=ALU.mult),
                     reads=[B["t1"], B["p_k"]], writes=[B["kp"]])
                S.op("dve", lambda e: e.tensor_tensor(out=T["t2"][:], in0=T["kkn"][:], in1=T["av"][:], op=ALU.mult),
                     reads=[B["kkn"], B["av"]], writes=[B["t2"]])
                outs = []
                o, bo = ob[no % 8], bob[no % 8]; no += 1
                S.op("dve", lambda e: e.tensor_tensor(out=o[:], in0=p_r[:], in1=T["eW"][:], op=ALU.mult), reads=[B["p_r"], B["eW"]], writes=[bo])
                outs.append((o, bo, RtT, RB))
                o, bo = ob[no % 8], bob[no % 8]; no += 1
                S.op("dve", lambda e: e.tensor_tensor(out=o[:], in0=T["kp"][:], in1=T["eWi"][:], op=ALU.mult), reads=[B["kp"], B["eWi"]], writes=[bo])
                outs.append((o, bo, KtT, KB))
                o, bo = ob[no % 8], bob[no % 8]; no += 1
                v3 = lambda t_: t_[:].rearrange("p (c j) -> p c j", j=64)
                S.op("dve", lambda e: e.scalar_tensor_tensor(out=v3(o)[:, :, 1:64], in0=v3(T["kkn"])[:, :, 1:64], scalar=-1.0,
                                                             in1=v3(T["eW"])[:, :, 0:63], op0=ALU.mult, op1=ALU.mult),
                     reads=[B["kkn"], B["eW"]], writes=[bo])
                S.op("dve", lambda e: e.tensor_scalar(out=v3(o)[:, :, 0:1], in0=v3(T["kkn"])[:, :, 0:1], scalar1=-1.0, scalar2=None, op0=ALU.mult),
                     reads=[B["kkn"]], writes=[bo])
                outs.append((o, bo, AtT, AB))
                o, bo = ob[no % 8], bob[no % 8]; no += 1
                S.op("dve", lambda e: e.tensor_tensor(out=o[:], in0=T["t2"][:], in1=T["eWi"][:], op=ALU.mult), reads=[B["t2"], B["eWi"]], writes=[bo])
                outs.append((o, bo, BtT, BB))
                for (o, bo, dst, bdst) in outs:
                    S.dma("sp", dst[fs, t0:t0 + G], o[:], bo, reads=[bo], writes=[bdst])
                w_, bw_ = wc[cc % 2], bwc[cc % 2]
                S.op("dve", lambda e: e.tensor_copy(out=w_[:], in_=T["eW"][:, 63:G:64]), reads=[B["eW"]], writes=[bw_])
                S.dma("sp", WC[fs, g * (G // 64):(g + 1) * (G // 64)], w_[:], bw_, reads=[bw_], writes=[WCB])
        S.barrier()
        P.release()
        for b in [bW, bcols] + bwc + bob:
            S.release(b)


def rwkv_tm_phase(nc, S, x_in, xin_b, gain_row, mix, a0, a1, a2, g1, g2, ka_, rk_, w_r, w_k, w_v,
                  Vtok, BV, Gt, bouts, ntok):
    G = 256
    ng = ntok // G
    with ExitStack() as es:
        sb = lambda name, shape, dt: es.enter_context(nc.sbuf_tensor(_uniq(name), shape, dt))
        ps = lambda name, shape, dt: es.enter_context(nc.psum_tensor(_uniq(name), shape, dt))
        P = RwPrep(nc, S, es, G, (0, 2, 3, 4, 5), x_in, xin_b, gain_row, mix)
        Wr = sb("Wr", [128, 8, D], BF16)
        Wk = sb("Wk", [128, 8, D], BF16)
        Wv = sb("Wv", [128, 8, D], BF16)
        A1 = sb("A1", [128, 8, 64], BF16)
        A2 = sb("A2", [64, D], BF16)
        G1 = sb("G1", [128, 8, 160], BF16)
        G2a = sb("G2a", [128, D], BF16)
        G2b = sb("G2b", [32, D], BF16)
        a0B = sb("a0B", [128, D], F32)
        kaB = sb("kaB", [128, D], F32)
        rkB = sb("rkB", [128, D], F32)
        ta = sb("ta", [64, G], BF16)
        sg1a = sb("sg1a", [128, G], BF16)
        sg1b = sb("sg1b", [32, G], BF16)
        tmp = sb("tmp", [128, D], F32)
        av = sb("av", [128, D], F32)
        t1 = sb("t1", [128, D], F32)
        kp = sb("kp", [128, D], F32)
        tmp2 = sb("tmp2", [128, D], F32)
        tmp3 = sb("tmp3", [128, 16, 64], F32)
        bsum = sb("bsum", [128, 16, 1], F32)
        bvo = [sb("bvo%d" % i, [128, 16, 64], F32) for i in range(2)]
        vto = [sb("vto%d" % i, [128, D], BF16) for i in range(2)]
        gto = [sb("gto%d" % i, [128, D], F32) for i in range(2)]
        p_t = ps("p_t", [128, G], F32)
        pp = [ps("pp%d" % i, [128, 2, 512], F32) for i in range(2)]
        bW, bB = S.dbuf("W"), S.dbuf("B")
        bbvo = [S.dbuf("bvo") for _ in range(2)]
        bvto = [S.dbuf("vto") for _ in range(2)]
        bgto = [S.dbuf("gto") for _ in range(2)]
        B = {n: S.buf(n) for n in ["ta", "sg1a", "sg1b", "tmp", "av", "t1", "kp", "tmp2", "tmp3", "bsum"]}
        B.update({n: S.pbuf(n) for n in ["p_t", "pp0", "pp1"]})
        bpp = [B["pp0"], B["pp1"]]
        for c in range(8):
            for (W_, w_) in ((Wr, w_r), (Wk, w_k), (Wv, w_v)):
                S.dma("pool", W_[:, c, :], w_.rearrange("(c p) f -> p c f", p=128)[:, c, :], bW, writes=[bW])
        S.dma("pool", A1[:], a1.rearrange("(c p) f -> p c f", p=128), bW, writes=[bW])
        S.dma("pool", G1[:], g1.rearrange("(c p) f -> p c f", p=128), bW, writes=[bW])
        S.dma("pool", A2[:], a2, bW, writes=[bW])
        S.dma("pool", G2a[:], g2[0:128, :], bW, writes=[bW])
        S.dma("pool", G2b[:], g2[128:160, :], bW, writes=[bW])
        for (t_, src) in ((a0B, a0), (kaB, ka_), (rkB, rk_)):
            S.dma("sp", t_[:], src.to_broadcast([128, D]), bB, writes=[bB])
        VB, BVB, GB = bouts
        npp = 0
        k = 0
        for g in range(ng):
            P.group(g)
            xr, xk, xv, xa, xg = P.xm[0], P.xm[2], P.xm[3], P.xm[4], P.xm[5]
            bxr, bxk, bxv, bxa, bxg = P.bxm[0], P.bxm[2], P.bxm[3], P.bxm[4], P.bxm[5]
            for c in range(8):
                S.op("pe", lambda e: e.matmul(p_t[0:64, :], lhsT=A1[:, c, :], rhs=xa[:, c, :], start=(c == 0), stop=(c == 7)),
                     reads=[bW, bxa], writes=[B["p_t"]], sig=(c == 7))
            S.op("act", lambda e: e.copy(out=ta[:], in_=p_t[0:64, :]), reads=[B["p_t"]], writes=[B["ta"]])
            for c in range(8):
                S.op("pe", lambda e: e.matmul(p_t[:, :], lhsT=G1[:, c, 0:128], rhs=xg[:, c, :], start=(c == 0), stop=(c == 7)),
                     reads=[bW, bxg], writes=[B["p_t"]], sig=(c == 7))
            S.op("act", lambda e: e.activation(out=sg1a[:], in_=p_t[:, :], func=AF.Sigmoid), reads=[B["p_t"]], writes=[B["sg1a"]])
            for c in range(8):
                S.op("pe", lambda e: e.matmul(p_t[0:32, :], lhsT=G1[:, c, 128:160], rhs=xg[:, c, :], start=(c == 0), stop=(c == 7)),
                     reads=[bW, bxg], writes=[B["p_t"]], sig=(c == 7))
            S.op("act", lambda e: e.activation(out=sg1b[:], in_=p_t[0:32, :], func=AF.Sigmoid), reads=[B["p_t"]], writes=[B["sg1b"]])
            for s in range(G // 128):
                ts = slice(s * 128, (s + 1) * 128)
                t0 = g * G + s * 128

                def big(xm_, bxm_, W_):
                    nonlocal npp
                    p, bp = pp[npp % 2], bpp[npp % 2]
                    npp += 1
                    for hh in range(2):
                        for c in range(8):
                            S.op("pe", lambda e: e.matmul(p[:, hh, :], lhsT=xm_[:, c, ts], rhs=W_[:, c, hh * 512:(hh + 1) * 512],
                                                          start=(c == 0), stop=(c == 7)), reads=[bW, bxm_], writes=[bp], sig=(c == 7))
                    return p, bp
                p, bp = pp[npp % 2], bpp[npp % 2]
                npp += 1
                for hh in range(2):
                    S.op("pe", lambda e: e.matmul(p[:, hh, :], lhsT=ta[:, ts], rhs=A2[:, hh * 512:(hh + 1) * 512], start=True, stop=True),
                         reads=[bW, B["ta"]], writes=[bp])
                S.op("dve", lambda e: e.tensor_tensor(out=tmp[:], in0=p[:].rearrange("p a b -> p (a b)"), in1=a0B[:], op=ALU.add),
                     reads=[bp, bB], writes=[B["tmp"]])
                S.op("act", lambda e: e.activation(out=av[:], in_=tmp[:], func=AF.Sigmoid), reads=[B["tmp"]], writes=[B["av"]])
                S.op("dve", lambda e: e.scalar_tensor_tensor(out=t1[:], in0=av[:], scalar=-1.0, in1=kaB[:], op0=ALU.add, op1=ALU.mult),
                     reads=[B["av"], bB], writes=[B["t1"]])
                p, bp = big(xk, bxk, Wk)
                S.op("dve", lambda e: e.scalar_tensor_tensor(out=kp[:], in0=t1[:], scalar=1.0, in1=p[:].rearrange("p a b -> p (a b)"),
                                                             op0=ALU.add, op1=ALU.mult), reads=[B["t1"], bp], writes=[B["kp"]])
                p, bp = big(xr, bxr, Wr)
                S.op("dve", lambda e: e.tensor_tensor(out=tmp2[:], in0=p[:].rearrange("p a b -> p (a b)"), in1=rkB[:], op=ALU.mult),
                     reads=[bp, bB], writes=[B["tmp2"]])
                S.op("dve", lambda e: e.tensor_tensor(out=tmp3[:].rearrange("p a b -> p (a b)"), in0=tmp2[:], in1=kp[:], op=ALU.mult),
                     reads=[B["tmp2"], B["kp"]], writes=[B["tmp3"]])
                S.op("dve", lambda e: e.tensor_reduce(out=bsum[:], in_=tmp3[:], axis=AX.X, op=ALU.add), reads=[B["tmp3"]], writes=[B["bsum"]])
                p, bp = big(xv, bxv, Wv)
                o, bo = bvo[k % 2], bbvo[k % 2]
                S.op("dve", lambda e: e.tensor_tensor(out=o[:], in0=p[:].rearrange("p a (h d) -> p (a h) d", d=64),
                                                      in1=bsum[:].to_broadcast([128, 16, 64]), op=ALU.mult), reads=[bp, B["bsum"]], writes=[bo])
                S.dma("sp", BV[t0:t0 + 128, :], o[:].rearrange("p a b -> p (a b)"), bo, reads=[bo], writes=[BVB])
                o, bo = vto[k % 2], bvto[k % 2]
                S.op("act", lambda e: e.copy(out=o[:], in_=p[:].rearrange("p a b -> p (a b)")), reads=[bp], writes=[bo])
                S.dma("sp", Vtok[t0:t0 + 128, :], o[:], bo, reads=[bo], writes=[VB])
                p, bp = pp[npp % 2], bpp[npp % 2]
                npp += 1
                for hh in range(2):
                    S.op("pe", lambda e: e.matmul(p[:, hh, :], lhsT=sg1a[:, ts], rhs=G2a[:, hh * 512:(hh + 1) * 512], start=True, stop=False),
                         reads=[bW, B["sg1a"]], writes=[bp], sig=False)
                    S.op("pe", lambda e: e.matmul(p[:, hh, :], lhsT=sg1b[:, ts], rhs=G2b[:, hh * 512:(hh + 1) * 512], start=False, stop=True),
                         reads=[bW, B["sg1b"]], writes=[bp])
                o, bo = gto[k % 2], bgto[k % 2]
                S.op("act", lambda e: e.copy(out=o[:], in_=p[:].rearrange("p a b -> p (a b)")), reads=[bp], writes=[bo])
                S.dma("sp", Gt[t0:t0 + 128, :], o[:], bo, reads=[bo], writes=[GB])
                k += 1
        S.barrier()
        P.release()
        for b in [bW, bB] + bbvo + bvto + bgto:
            S.release(b)


def rwkv_scan_phase(nc, S, RtT, KtT, AtT, BtT, WC, Vtok, Ysc, bins, bY, ntok, NI=4):
    nch = ntok // 64
    ngr = nch // 8
    with ExitStack() as es:
        sb = lambda name, shape, dt: es.enter_context(nc.sbuf_tensor(_uniq(name), shape, dt))
        ps = lambda name, shape, dt: es.enter_context(nc.psum_tensor(_uniq(name), shape, dt))
        MU = sb("MU", [128, 128], F32)
        MUI = sb("MUI", [128, 128], F32)
        ML = sb("ML", [128, 128], F32)
        I32 = sb("I32", [128, 128], F32)
        identb = sb("identb", [128, 128], BF16)
        bconst = S.buf("const")
        for (m, chm, pat, op) in ((MU, -1, 1, ALU.is_gt), (MUI, -1, 1, ALU.is_ge), (ML, 1, -1, ALU.is_gt)):
            S.op("pool", lambda e: e.memset(m[:], 1.0), writes=[bconst])
            S.op("pool", lambda e: e.affine_select(out=m[:], in_=m[:], pattern=[[pat, 128]], compare_op=op, fill=0.0, base=0,
                                                   channel_multiplier=chm), reads=[bconst], writes=[bconst])
        make_ident(nc, S, I32, bconst)
        make_ident(nc, S, identb, bconst)
        M4 = sb("M4", [128, 4, 128], F32)
        for k_, m_ in enumerate((MU, MUI, MU, MUI)):
            S.op("dve", lambda e: e.tensor_copy(out=M4[:, k_, :], in_=m_[:]), reads=[bconst], writes=[bconst])

        class Slot:
            pass
        slots = []
        for si in range(NI):
            s = Slot()
            n_ = lambda x: "%s_%d" % (x, si)
            s.AR = [sb(n_("AR%d" % j), [128, 8, 2, 128], BF16) for j in range(2)]
            s.Bd = [sb(n_("Bd%d" % j), [128, 8, 128], BF16) for j in range(2)]
            s.Kd = [sb(n_("Kd%d" % j), [128, 8, 128], BF16) for j in range(2)]
            s.bbd = [S.dbuf(n_("bd0")), S.dbuf(n_("bd1"))]
            s.Vs = [sb(n_("Vs%d" % j), [128, 8, 64], BF16) for j in range(2)]
            s.Yo = [sb(n_("Yo%d" % j), [128, 8, 64], F32) for j in range(2)]
            s.bYo = [S.dbuf(n_("Yo0")), S.dbuf(n_("Yo1"))]
            s.wcs = sb(n_("wcs"), [128, nch], F32)
            s.bwcs = S.dbuf(n_("wcs"))
            s.XQP = [sb(n_("XQP%d" % j), [128, 3, 128], BF16) for j in range(2)]
            s.bXQP = [S.buf("XQP") for _ in range(2)]
            s.G4 = [sb(n_("G4%d" % j), [128, 4, 128], BF16) for j in range(2)]
            s.bG4 = [S.buf("G4") for _ in range(2)]
            for k in ("BtT", "KtT"):
                setattr(s, k, [sb(n_(k) + "_%d" % q_, [128, 128], BF16) for q_ in range(2)])
                setattr(s, "b" + k, [S.buf(k) for _ in range(2)])
            s.TT = [sb(n_("TT%d" % q_), [128, 128], BF16) for q_ in range(2)]
            s.bTT = [S.buf("TT") for _ in range(2)]
            s.Xs = sb(n_("Xs"), [128, 64], BF16)
            s.Ub = sb(n_("Ub"), [128, 64], BF16)
            s.Sw = sb(n_("Sw"), [128, 64], F32)
            s.St = sb(n_("St"), [128, 64], F32)
            s.Sb = sb(n_("Sb"), [128, 64], BF16)
            s.bXs, s.bUb, s.bSw, s.bSt, s.bSb = [S.buf(k) for k in ("Xs", "Ub", "Sw", "St", "Sb")]
            s.psA = ps(n_("psA"), [128, 512], F32)
            s.psB = ps(n_("psB"), [128, 4, 128], F32)
            s.bA, s.bB = S.pbuf("bankA"), S.pbuf("bankB")
            s.ptr = s.psB[:, 3, :].bitcast(BF16)
            for j in range(2):
                S.op("pool", lambda e: e.memset(s.AR[j][:], 0.0), writes=[s.bbd[j]])
                S.op("pool", lambda e: e.memset(s.Bd[j][:], 0.0), writes=[s.bbd[j]])
                S.op("pool", lambda e: e.memset(s.Kd[j][:], 0.0), writes=[s.bbd[j]])
            slots.append(s)

        bRs, bKs, bAs, bBs, bWC, bV = bins

        def load_group(s, hp, gg):
            j = gg % 2
            t0 = gg * 512
            for h in range(2):
                r0 = hp * 128 + h * 64
                hs, cs = slice(h * 64, (h + 1) * 64), slice(h * 64, (h + 1) * 64)
                for (dst, src, bsrc) in ((s.AR[j][hs, :, 0, cs], AtT, bAs), (s.AR[j][hs, :, 1, cs], RtT, bRs),
                                         (s.Bd[j][hs, :, cs], BtT, bBs), (s.Kd[j][hs, :, cs], KtT, bKs)):
                    S.dma("sp", dst, src[r0:r0 + 64, t0:t0 + 512].rearrange("p (c j) -> p c j", j=64), s.bbd[j],
                          reads=[bsrc], writes=[s.bbd[j]])
                S.dma("sp", s.Vs[j][hs, :, :], Vtok[t0:t0 + 512, r0:r0 + 64].rearrange("(c j) v -> j c v", j=64),
                      s.bbd[j], reads=[bV], writes=[s.bbd[j]])

        def T_steps(gg, c):
            j = gg % 2
            q = (gg * 8 + c) % 2
            steps = []

            def st1():
                for s in slots:
                    AR, A, Bd, K = s.AR[j][:, c, :, :], s.AR[j][:, c, 0, :], s.Bd[j][:, c, :], s.Kd[j][:, c, :]
                    S.op("pe", lambda e: e.matmul(s.psA[:, 0:256], lhsT=Bd, rhs=AR, start=True, stop=True), reads=[s.bbd[j]], writes=[s.bA], sig=False)
                    S.op("pe", lambda e: e.matmul(s.psA[:, 256:512], lhsT=K, rhs=AR, start=True, stop=True), reads=[s.bbd[j]], writes=[s.bA])
                    S.op("pe", lambda e: e.matmul(s.psB[:, 1, :], lhsT=A, rhs=Bd, start=True, stop=True), reads=[s.bbd[j]], writes=[s.bB], sig=False)
                    S.op("pe", lambda e: e.transpose(s.ptr[:, 0:128], Bd, identb[:]), reads=[s.bbd[j], bconst], writes=[s.bB], sig=False)
                    S.op("pe", lambda e: e.transpose(s.ptr[:, 128:256], K, identb[:]), reads=[s.bbd[j], bconst], writes=[s.bB])
            steps.append(st1)

            def st2():
                for s in slots:
                    S.op("dve", lambda e: e.tensor_tensor(out=s.G4[q][:].rearrange("p a b -> p (a b)"), in0=s.psA[:, 0:512],
                                                          in1=M4[:].rearrange("p a b -> p (a b)"), op=ALU.mult),
                         reads=[s.bA, bconst], writes=[s.bG4[q]])
                    S.op("dve", lambda e: e.tensor_tensor(out=s.XQP[0][:, 2, :], in0=s.psB[:, 1, :], in1=ML[:], op=ALU.mult),
                         reads=[s.bB, bconst], writes=[s.bXQP[0]])
                    S.op("dve", lambda e: e.tensor_tensor(out=s.XQP[1][:, 0, :], in0=s.G4[q][:, 0, :], in1=I32[:], op=ALU.add),
                         reads=[s.bG4[q], bconst], writes=[s.bXQP[1]])
                    S.op("act", lambda e: e.copy(out=s.BtT[q][:], in_=s.ptr[:, 0:128]), reads=[s.bB], writes=[s.bBtT[q]])
                    S.op("act", lambda e: e.copy(out=s.KtT[q][:], in_=s.ptr[:, 128:256]), reads=[s.bB], writes=[s.bKtT[q]])
            steps.append(st2)

            for lvl in range(6):
                cur = 0 if lvl == 0 else lvl % 2
                nxt = 1 - cur

                def sa(lvl=lvl, cur=cur):
                    for s in slots:
                        X, Pm = s.XQP[cur][:, 0, :], s.XQP[cur][:, 2, :]
                        if lvl == 0:
                            Q = s.G4[q][:, 0, :]
                            rd = [s.bXQP[cur], s.bG4[q]]
                            S.op("pe", lambda e: e.matmul(s.psA[:, 128:256], lhsT=Pm, rhs=Q, start=True, stop=True), reads=rd, writes=[s.bA], sig=False)
                        else:
                            Q = s.XQP[cur][:, 1, :]
                            rd = [s.bXQP[cur]]
                            if lvl <= 3:
                                S.op("pe", lambda e: e.matmul(s.psA[:, 0:256], lhsT=Pm, rhs=s.XQP[cur][:, 0:2, :], start=True, stop=True),
                                     reads=rd, writes=[s.bA], sig=False)
                            else:
                                S.op("pe", lambda e: e.matmul(s.psA[:, 0:128], lhsT=Pm, rhs=X, start=True, stop=True), reads=rd, writes=[s.bA],
                                     sig=(lvl == 5))
                        if lvl <= 4:
                            S.op("pe", lambda e: e.matmul(s.psA[:, 256:384], lhsT=Q, rhs=Pm, start=True, stop=True), reads=rd, writes=[s.bA])

                def sb_(lvl=lvl, cur=cur, nxt=nxt):
                    for s in slots:
                        if lvl <= 3:
                            S.op("act", lambda e: e.copy(out=s.XQP[nxt][:, 1:3, :], in_=s.psA[:, 128:384]), reads=[s.bA], writes=[s.bXQP[nxt]])
                        elif lvl == 4:
                            S.op("act", lambda e: e.copy(out=s.XQP[nxt][:, 2, :], in_=s.psA[:, 256:384]), reads=[s.bA], writes=[s.bXQP[nxt]])
                        if 1 <= lvl <= 4:
                            S.op("dve", lambda e: e.tensor_tensor(out=s.XQP[nxt][:, 0, :], in0=s.psA[:, 0:128], in1=s.XQP[cur][:, 0, :], op=ALU.add),
                                 reads=[s.bA, s.bXQP[cur]], writes=[s.bXQP[nxt]])
                        if lvl == 5:
                            S.op("dve", lambda e: e.tensor_tensor(out=s.TT[q][:], in0=s.psA[:, 0:128], in1=s.XQP[cur][:, 0, :], op=ALU.add),
                                 reads=[s.bA, s.bXQP[cur]], writes=[s.bTT[q]])
                steps += [sa, sb_]
            return steps

        def S_steps(gg, c):
            j = gg % 2
            ch = gg * 8 + c
            q = ch % 2

            def s1():
                for s in slots:
                    A = s.AR[j][:, c, 0, :]
                    S.op("pe", lambda e: e.matmul(s.psB[:, 0, 0:64], lhsT=A, rhs=s.Sb[:], start=True, stop=False),
                         reads=[s.bbd[j], s.bSb], writes=[s.bB], sig=False)
                    S.op("pe", lambda e: e.matmul(s.psB[:, 0, 0:64], lhsT=s.G4[q][:, 2, :], rhs=s.Vs[j][:, c, :], start=False, stop=True),
                         reads=[s.bG4[q], s.bbd[j]], writes=[s.bB])
                    S.op("pool", lambda e: e.tensor_scalar(out=s.Sw[:], in0=s.St[:], scalar1=s.wcs[:, ch:ch + 1], scalar2=None, op0=ALU.mult),
                         reads=[s.bSt, s.bwcs], writes=[s.bSw])

            def s2():
                for s in slots:
                    S.op("act", lambda e: e.copy(out=s.Xs[:], in_=s.psB[:, 0, 0:64]), reads=[s.bB], writes=[s.bXs])

            def s3():
                for s in slots:
                    S.op("pe", lambda e: e.matmul(s.psB[:, 1, 0:64], lhsT=s.TT[q][:], rhs=s.Xs[:], start=True, stop=True),
                         reads=[s.bTT[q], s.bXs], writes=[s.bB])

            def s4():
                for s in slots:
                    S.op("act", lambda e: e.copy(out=s.Ub[:], in_=s.psB[:, 1, 0:64]), reads=[s.bB], writes=[s.bUb])

            def s5():
                for s in slots:
                    R = s.AR[j][:, c, 1, :]
                    pY, pS = s.psB[:, 2, 0:64], s.psB[:, 0, 0:64]
                    S.op("pe", lambda e: e.matmul(pY, lhsT=R, rhs=s.Sb[:], start=True, stop=False),
                         reads=[s.bbd[j], s.bSb], writes=[s.bB], sig=False)
                    S.op("pe", lambda e: e.matmul(pY, lhsT=s.G4[q][:, 1, :], rhs=s.Ub[:], start=False, stop=False),
                         reads=[s.bG4[q], s.bUb], writes=[s.bB], sig=False)
                    S.op("pe", lambda e: e.matmul(pY, lhsT=s.G4[q][:, 3, :], rhs=s.Vs[j][:, c, :], start=False, stop=True),
                         reads=[s.bG4[q], s.bbd[j]], writes=[s.bB], sig=False)
                    S.op("pe", lambda e: e.matmul(pS, lhsT=s.BtT[q][:], rhs=s.Ub[:], start=True, stop=False),
                         reads=[s.bBtT[q], s.bUb], writes=[s.bB], sig=False)
                    S.op("pe", lambda e: e.matmul(pS, lhsT=s.KtT[q][:], rhs=s.Vs[j][:, c, :], start=False, stop=True),
                         reads=[s.bKtT[q], s.bbd[j]], writes=[s.bB])

            def s6():
                for s in slots:
                    S.op("dve", lambda e: e.scalar_tensor_tensor(out=s.St[:], in0=s.psB[:, 0, 0:64], scalar=s.wcs[:, ch:ch + 1], in1=s.Sw[:],
                                                                 op0=ALU.mult, op1=ALU.add), reads=[s.bB, s.bwcs, s.bSw], writes=[s.bSt])
                    S.op("dve", lambda e: e.tensor_copy(out=s.Yo[j][:, c, :], in_=s.psB[:, 2, 0:64]), reads=[s.bB], writes=[s.bYo[j]])
                    S.op("act", lambda e: e.copy(out=s.Sb[:], in_=s.St[:]), reads=[s.bSt], writes=[s.bSb])
            return [s1, s2, s3, s4, s5, s6]

        for rnd in range(8 // NI):
            hps = [rnd * NI + i for i in range(NI)]
            for s, hp in zip(slots, hps):
                S.dma("sp", s.wcs[:], WC[hp * 128:(hp + 1) * 128, 0:nch], s.bwcs, reads=[bWC], writes=[s.bwcs])
                S.op("pool", lambda e: e.memset(s.St[:], 0.0), writes=[s.bSt])
                S.op("pool", lambda e: e.memset(s.Sb[:], 0.0), writes=[s.bSb])
                load_group(s, hp, 0)
            if ngr > 1:
                for s, hp in zip(slots, hps):
                    load_group(s, hp, 1)
            for st in T_steps(0, 0):
                st()
            for gg in range(ngr):
                j = gg % 2
                for c in range(8):
                    ch = gg * 8 + c
                    if ch + 1 < nch:
                        ng_, nc_ = (gg, c + 1) if c < 7 else (gg + 1, 0)
                        tsteps = T_steps(ng_, nc_)
                    else:
                        tsteps = []
                    ssteps = S_steps(gg, c)
                    ti = 0
                    for ss_ in ssteps:
                        for _ in range(3):
                            if ti < len(tsteps):
                                tsteps[ti]()
                                ti += 1
                        ss_()
                    while ti < len(tsteps):
                        tsteps[ti]()
                        ti += 1
                for s, hp in zip(slots, hps):
                    for h in range(2):
                        c0 = hp * 128 + h * 64
                        S.dma("sp", Ysc[gg * 512:(gg + 1) * 512, c0:c0 + 64].rearrange("(c j) v -> j c v", j=64),
                              s.Yo[j][h * 64:(h + 1) * 64, :, :], s.bYo[j], reads=[s.bYo[j]], writes=[bY])
                if gg + 2 < ngr:
                    for s, hp in zip(slots, hps):
                        load_group(s, hp, gg + 2)
        S.barrier()
        for s in slots:
            for b_ in s.bbd + s.bYo + [s.bwcs]:
                S.release(b_)


def rwkv_post_phase(nc, S, Ysc, BV, Gt, lg_row, lb_row, ZT, bins, bZT, ntok, gn_eps=64e-5):
    with ExitStack() as es:
        sb = lambda name, shape, dt: es.enter_context(nc.sbuf_tensor(_uniq(name), shape, dt))
        ps = lambda name, shape, dt: es.enter_context(nc.psum_tensor(_uniq(name), shape, dt))
        lgB = sb("lgB", [128, D], F32)
        lbB = sb("lbB", [128, D], F32)
        ident = sb("ident", [128, 128], BF16)
        nh = sb("nh", [128, 16, 1], F32)
        yt = [sb("yt%d" % i, [128, 16, 64], F32) for i in range(2)]
        bvt = [sb("bvt%d" % i, [128, D], F32) for i in range(2)]
        gt = [sb("gt%d" % i, [128, D], F32) for i in range(2)]
        sm_ = [sb("sm%d" % i, [128, 16, 1], F32) for i in range(2)]
        vr_ = [sb("vr%d" % i, [128, 16, 1], F32) for i in range(2)]
        rstd_ = [sb("rstd%d" % i, [128, 16, 1], F32) for i in range(2)]
        yc_ = [sb("yc%d" % i, [128, 16, 64], F32) for i in range(2)]
        sq_ = [sb("sq%d" % i, [128, 16, 64], F32) for i in range(2)]
        yn_ = [sb("yn%d" % i, [128, 16, 64], F32) for i in range(2)]
        y2_ = [sb("y2%d" % i, [128, D], F32) for i in range(2)]
        zb = [sb("zb%d" % i, [128, D], BF16) for i in range(2)]
        zT = [sb("zT%d" % i, [128, 8, 512], BF16) for i in range(2)]
        p_tr = [ps("p_tr%d" % i, [128, 8, 128], BF16) for i in range(2)]
        bC = S.dbuf("C")
        byt = [S.dbuf("yt") for _ in range(2)]
        bbvt = [S.dbuf("bvt") for _ in range(2)]
        bgt = [S.dbuf("gt") for _ in range(2)]
        bzT = [S.dbuf("zT") for _ in range(2)]
        B = {n: S.buf(n) for n in ["ident", "nh", "zb0", "zb1"]}
        B2 = {n: [S.buf(n) for _ in range(2)] for n in ["sm", "vr", "rstd", "yc", "sq", "yn", "y2"]}
        bp_tr = [S.pbuf("ptr") for _ in range(2)]
        bYs, bBV, bG = bins
        make_ident(nc, S, ident, B["ident"])
        S.op("pool", lambda e: e.memset(nh[:], -0.5), writes=[B["nh"]])
        S.dma("sp", lgB[:], lg_row.to_broadcast([128, D]), bC, writes=[bC])
        S.dma("sp", lbB[:], lb_row.to_broadcast([128, D]), bC, writes=[bC])
        nt = ntok // 128

        def bind(t):
            i = t % 2
            for n_ in ["sm", "vr", "rstd", "yc", "sq", "yn", "y2"]:
                B[n_] = B2[n_][i]
            return i, sm_[i], vr_[i], rstd_[i], yc_[i], sq_[i], yn_[i], y2_[i]

        def front(t):
            i, sm, vr, rstd, yc, sq, yn, y2 = bind(t)
            t0 = t * 128
            S.dma("sp", yt[i][:].rearrange("p a b -> p (a b)"), Ysc[t0:t0 + 128, :], byt[i], reads=[bYs], writes=[byt[i]])
            S.dma("sp", bvt[i][:], BV[t0:t0 + 128, :], bbvt[i], reads=[bBV], writes=[bbvt[i]])
            S.dma("sp", gt[i][:], Gt[t0:t0 + 128, :], bgt[i], reads=[bG], writes=[bgt[i]])
            y3 = yt[i]
            S.op("dve", lambda e: e.tensor_reduce(out=sm[:], in_=y3[:], axis=AX.X, op=ALU.add), reads=[byt[i]], writes=[B["sm"]])
            S.op("dve", lambda e: e.tensor_scalar(out=sm[:], in0=sm[:], scalar1=1.0 / 64, scalar2=None, op0=ALU.mult), reads=[B["sm"]], writes=[B["sm"]])
            S.op("dve", lambda e: e.tensor_tensor(out=yc[:], in0=y3[:], in1=sm[:].to_broadcast([128, 16, 64]), op=ALU.subtract),
                 reads=[byt[i], B["sm"]], writes=[B["yc"]])
            S.op("act", lambda e: e.activation(out=sq[:], in_=yc[:], func=AF.Square), reads=[B["yc"]], writes=[B["sq"]])

        def mid(t):
            i, sm, vr, rstd, yc, sq, yn, y2 = bind(t)
            S.op("dve", lambda e: e.tensor_reduce(out=vr[:], in_=sq[:], axis=AX.X, op=ALU.add), reads=[B["sq"]], writes=[B["vr"]])
            S.op("dve", lambda e: e.tensor_scalar(out=vr[:], in0=vr[:], scalar1=1.0 / 64, scalar2=gn_eps, op0=ALU.mult, op1=ALU.add),
                 reads=[B["vr"]], writes=[B["vr"]])
            S.op("pool", lambda e: e.tensor_tensor(out=rstd[:], in0=vr[:], in1=nh[:], op=ALU.pow), reads=[B["vr"], B["nh"]], writes=[B["rstd"]])

        def back(t):
            i, sm, vr, rstd, yc, sq, yn, y2 = bind(t)
            S.op("dve", lambda e: e.tensor_tensor(out=yn[:], in0=yc[:], in1=rstd[:].to_broadcast([128, 16, 64]), op=ALU.mult),
                 reads=[B["yc"], B["rstd"]], writes=[B["yn"]])
            ynf = yn[:].rearrange("p a b -> p (a b)")
            S.op("dve", lambda e: e.tensor_tensor(out=y2[:], in0=ynf, in1=lgB[:], op=ALU.mult), reads=[B["yn"], bC], writes=[B["y2"]])
            S.op("dve", lambda e: e.tensor_tensor(out=y2[:], in0=y2[:], in1=lbB[:], op=ALU.add), reads=[B["y2"], bC], writes=[B["y2"]])
            S.op("dve", lambda e: e.tensor_tensor(out=y2[:], in0=y2[:], in1=bvt[i][:], op=ALU.add), reads=[B["y2"], bbvt[i]], writes=[B["y2"]])
            z, bz = zb[i], B["zb%d" % i]
            S.op("dve", lambda e: e.tensor_tensor(out=z[:], in0=y2[:], in1=gt[i][:], op=ALU.mult), reads=[B["y2"], bgt[i]], writes=[bz])
            pt, bpt = p_tr[i], bp_tr[i]
            for c in range(8):
                S.op("pe", lambda e: e.transpose(pt[:, c, :], z[:, c * 128:(c + 1) * 128], ident[:]), reads=[bz, B["ident"]], writes=[bpt], sig=(c == 7))
            gi = (t // 4) % 2
            S.op("act", lambda e: e.copy(out=zT[gi][:, :, (t % 4) * 128:(t % 4 + 1) * 128], in_=pt[:]), reads=[bpt], writes=[bzT[gi]])
            if t % 4 == 3:
                g0 = (t // 4) * 512
                for c in range(8):
                    S.dma("sp", ZT[c * 128:(c + 1) * 128, g0:g0 + 512], zT[gi][:, c, :], bzT[gi], reads=[bzT[gi]], writes=[bZT])

        front(0)
        mid(0)
        for t in range(nt):
            if t + 1 < nt:
                front(t + 1)
            back(t)
            if t + 1 < nt:
                mid(t + 1)
        S.barrier()
        for b in [bC] + byt + bbvt + bgt + bzT:
            S.release(b)


def build_program(ntok=SEQ):
    nc = bass.Bass("TRN2", target_bir_lowering=False)
    di = lambda n, s: nc.dram_tensor(n, list(s), F32, kind="ExternalInput").ap()
    x = di("x", [ntok, D])
    ffn_norm = di("ffn_norm", [4, D])
    wg = di("ffn_w_gate", [2, 2, D, DFF])
    wu = di("ffn_w_up", [2, 2, D, DFF])
    wd = di("ffn_w_down", [2, 2, DFF, D])
    mix_norm = di("mix_norm", [2, D])
    dbias = di("dbias", [12, 128, 2, 128])
    w_in = di("attn_w_in", [D, 3072])
    qn = di("attn_q_norm", [64, 1])
    kn = di("attn_k_norm", [64, 1])
    w_out = di("attn_w_out", [512, D])
    rw_mix = di("rw_mix", [6, D])
    rows = {n: di(n, [1, D]) for n in ("rw_w0", "rw_a0", "rw_kk", "rw_ka", "rw_rk", "rw_lnx_g", "rw_lnx_b")}
    rw_w1 = di("rw_w1", [D, 64]); rw_w2 = di("rw_w2", [64, D]); rw_a1 = di("rw_a1", [D, 64]); rw_a2 = di("rw_a2", [64, D])
    rw_g1 = di("rw_g1", [D, 160]); rw_g2 = di("rw_g2", [160, D])
    rw_wr = di("rw_wr", [D, D]); rw_wk = di("rw_wk", [D, D]); rw_wv = di("rw_wv", [D, D]); rw_wo = di("rw_wo", [D, D])
    out = nc.dram_tensor("out", [ntok, D], F32, kind="ExternalOutput").ap()
    scr = lambda n, s, dt: nc.dram_tensor(n, list(s), dt, kind="Internal").ap()
    xa = scr("xa", [ntok, D], F32); xb = scr("xb", [ntok, D], F32)
    QT = scr("QT", [D, ntok], BF16); KT = scr("KT", [D, ntok], BF16); V = scr("V", [ntok, D], BF16); MT = scr("MT", [512, ntok], BF16)
    RtT, KtT, AtT, BtT = [scr(n, [D, ntok], BF16) for n in ("RtT", "KtT", "AtT", "BtT")]
    WC = scr("WC", [D, ntok // 64], F32)
    Vtok = scr("Vtok", [ntok, D], BF16); BV = scr("BV", [ntok, D], F32); Gt = scr("Gt", [ntok, D], F32); Ysc = scr("Ysc", [ntok, D], F32)
    ZT = scr("ZT", [D, ntok], BF16)

    S = Sched(nc, n_dma_sems=24)
    nb = lambda n: S.buf(n, acc=True)
    bx, bxa, bxb, bout = nb("x"), nb("xa"), nb("xb"), nb("out")
    bQT, bKT, bV, bMT = nb("QT"), nb("KT"), nb("V"), nb("MT")
    bR, bK, bA, bB, bWC, bVt, bBV, bG, bY, bZ = [nb(n) for n in ("R", "K", "A", "B", "WC", "Vt", "BV", "G", "Y", "Z")]

    ffn_phase(nc, S, x, xa, bx, bxa, wg[0, 0], wu[0, 0], wd[0, 0], ffn_norm[0:1, :], ntok)
    attn_in_phase(nc, S, xa, bxa, w_in, mix_norm[0:1, :], qn, kn, QT, KT, V, bQT, bKT, bV, ntok)
    sb_attn_phase(nc, S, QT, KT, V, MT, bQT, bKT, bV, bMT, ntok)
    dil_attn_phase(nc, S, QT, KT, V, MT, dbias, bQT, bKT, bV, bMT, ntok)
    out_proj_phase(nc, S, xa, xb, bxa, bxb, MT, bMT, w_out, 512, ntok)
    ffn_phase(nc, S, xb, xa, bxb, bxa, wg[0, 1], wu[0, 1], wd[0, 1], ffn_norm[1:2, :], ntok)
    ffn_phase(nc, S, xa, xb, bxa, bxb, wg[1, 0], wu[1, 0], wd[1, 0], ffn_norm[2:3, :], ntok)
    rwkv_fm_phase(nc, S, xb, bxb, mix_norm[1:2, :], rw_mix, rows["rw_w0"], rw_w1, rw_w2, rows["rw_a0"], rw_a1, rw_a2,
                  rows["rw_kk"], rows["rw_ka"], rw_wr, rw_wk, RtT, KtT, AtT, BtT, WC, [bR, bK, bA, bB, bWC], ntok)
    rwkv_tm_phase(nc, S, xb, bxb, mix_norm[1:2, :], rw_mix, rows["rw_a0"], rw_a1, rw_a2, rw_g1, rw_g2, rows["rw_ka"], rows["rw_rk"],
                  rw_wr, rw_wk, rw_wv, Vtok, BV, Gt, [bVt, bBV, bG], ntok)
    rwkv_scan_phase(nc, S, RtT, KtT, AtT, BtT, WC, Vtok, Ysc, [bR, bK, bA, bB, bWC, bVt], bY, ntok)
    rwkv_post_phase(nc, S, Ysc, BV, Gt, rows["rw_lnx_g"], rows["rw_lnx_b"], ZT, [bY, bBV, bG], bZ, ntok)
    out_proj_phase(nc, S, xb, xa, bxb, bxa, ZT, bZ, rw_wo, 1024, ntok)
    ffn_phase(nc, S, xa, out, bxa, bout, wg[1, 1], wu[1, 1], wd[1, 1], ffn_norm[3:4, :], ntok)
    S.wait_for("sp", [bout])
    return nc


def kernel(x, ffn_norm, ffn_w_gate, ffn_w_up, ffn_w_down, mix_norm, rel_bias,
           attn_w_in, attn_q_norm, attn_k_norm, attn_w_out,
           rw_mix, rw_w0, rw_w1, rw_w2, rw_a0, rw_a1, rw_a2, rw_g1, rw_g2,
           rw_kk, rw_ka, rw_rk, rw_wr, rw_wk, rw_wv, rw_wo, rw_lnx_g, rw_lnx_b):
    f = lambda a: np.ascontiguousarray(np.asarray(a, dtype=np.float32))
    x = f(x)
    n = x.shape[0]
    shared = {
        "ffn_norm": f(ffn_norm).reshape(4, D), "ffn_w_gate": f(ffn_w_gate), "ffn_w_up": f(ffn_w_up), "ffn_w_down": f(ffn_w_down),
        "mix_norm": f(mix_norm), "dbias": dil_bias_host(f(rel_bias)),
        "attn_w_in": f(attn_w_in)[0], "attn_q_norm": f(attn_q_norm).reshape(64, 1), "attn_k_norm": f(attn_k_norm).reshape(64, 1),
        "attn_w_out": f(attn_w_out)[0], "rw_mix": f(rw_mix)[0],
        "rw_w0": f(rw_w0).reshape(1, D), "rw_a0": f(rw_a0).reshape(1, D), "rw_kk": f(rw_kk).reshape(1, D), "rw_ka": f(rw_ka).reshape(1, D),
        "rw_rk": f(rw_rk).reshape(1, D), "rw_lnx_g": f(rw_lnx_g).reshape(1, D), "rw_lnx_b": f(rw_lnx_b).reshape(1, D),
        "rw_w1": f(rw_w1)[0], "rw_w2": f(rw_w2)[0], "rw_a1": f(rw_a1)[0], "rw_a2": f(rw_a2)[0], "rw_g1": f(rw_g1)[0], "rw_g2": f(rw_g2)[0],
        "rw_wr": f(rw_wr)[0], "rw_wk": f(rw_wk)[0], "rw_wv": f(rw_wv)[0], "rw_wo": f(rw_wo)[0],
    }
    nc = build_program(x.shape[1])
    in_maps = [dict(shared, x=x[i]) for i in range(n)]
    res = run_bass_kernel_spmd(nc, in_maps, core_ids=list(range(n)))
    return np.stack([np.asarray(r["out"]) for r in res.results], axis=0).astype(np.float32)
```
